# Optimizing a Trainium2 kernel written in Bass

```python
import math
import jax, jax.numpy as jnp
from jax import lax
import numpy as np

D_MODEL = 2048
BATCH = 8
SEQ = 2048
DEPTH = 1

GRID_W = 64
CTX_LEN = 256

ATT_HEADS = 8
ATT_DIM = 64
ATT_WIDTH = ATT_HEADS * 2 * ATT_DIM
SSM_GROUP = 16
SSM_GROUPS = 32
SSM_WIDTH = SSM_GROUP * SSM_GROUPS
SSM_STATE = 64

Q_BLOCK = 128
ROPE_BASE = 10000.0
NORM_EPS = 1e-6

SPLITS = [ATT_WIDTH, 2 * ATT_WIDTH, 3 * ATT_WIDTH, 4 * ATT_WIDTH,
          4 * ATT_WIDTH + SSM_WIDTH, 4 * ATT_WIDTH + 2 * SSM_WIDTH]
IN_WIDTH = 4 * ATT_WIDTH + 2 * SSM_WIDTH + 2 * D_MODEL

kernel_name = "hybrid_diffattn_s5_prefix_dit"


def rmsnorm(x, g):
    xf = x.astype(jnp.float32)
    y = xf * lax.rsqrt(jnp.mean(xf * xf, axis=-1, keepdims=True) + NORM_EPS)
    return (y * g.astype(jnp.float32)).astype(x.dtype)


def rope_axis(xp, pos):
    nf = xp.shape[-1] // 2
    inv = ROPE_BASE ** (-jnp.arange(nf, dtype=jnp.float32) / nf)
    ang = pos[:, None] * inv[None, :]
    cos = jnp.cos(ang)[None, :, None, None, :].astype(xp.dtype)
    sin = jnp.sin(ang)[None, :, None, None, :].astype(xp.dtype)
    x1, x2 = xp[..., :nf], xp[..., nf:]
    return jnp.concatenate([x1 * cos - x2 * sin, x2 * cos + x1 * sin], axis=-1)


def rope2d(x, row, col):
    half = x.shape[-1] // 2
    return jnp.concatenate([rope_axis(x[..., :half], row), rope_axis(x[..., half:], col)], axis=-1)


def diff_attend(qb, k, v, lam):
    s = jnp.einsum('bqhsd,bkhsd->bhsqk', qb, k).astype(jnp.float32) * (ATT_DIM ** -0.5)
    p = jax.nn.softmax(s, axis=-1)
    a = p[:, :, 0] - lam * p[:, :, 1]
    return jnp.einsum('bhqk,bkhe->bqhe', a.astype(v.dtype), v)


def attn_post(o, subln_g, lam_init, gate):
    B, L = o.shape[0], o.shape[1]
    o = rmsnorm(o, subln_g) * (1.0 - lam_init)
    return o.reshape(B, L, ATT_WIDTH) * jax.nn.silu(gate)


def discretize(lre, lim, log_step, b_re, b_im):
    lre = lre.astype(jnp.float32); lim = lim.astype(jnp.float32)
    dt = jnp.exp(log_step.astype(jnp.float32))[:, None]
    mag = jnp.exp(lre * dt)
    ar, ai = mag * jnp.cos(lim * dt), mag * jnp.sin(lim * dt)
    den = lre * lre + lim * lim
    fr = ((ar - 1.0) * lre + ai * lim) / den
    fi = (ai * lre - (ar - 1.0) * lim) / den
    br, bi = b_re.astype(jnp.float32), b_im.astype(jnp.float32)
    bbr = fr[..., None] * br - fi[..., None] * bi
    bbi = fr[..., None] * bi + fi[..., None] * br
    return ar, ai, bbr, bbi


def _lin_combine(e1, e2):
    a1r, a1i, b1r, b1i = e1
    a2r, a2i, b2r, b2i = e2
    return (a1r * a2r - a1i * a2i,
            a1r * a2i + a1i * a2r,
            a2r * b1r - a2i * b1i + b2r,
            a2r * b1i + a2i * b1r + b2i)


def ssm_scan(u, disc, h0, reverse):
    ar, ai, bbr, bbi = disc
    bu_r = jnp.einsum('blgc,gpc->blgp', u, bbr)
    bu_i = jnp.einsum('blgc,gpc->blgp', u, bbi)
    if reverse:
        bu_r, bu_i = jnp.flip(bu_r, axis=1), jnp.flip(bu_i, axis=1)
    if h0 is not None:
        h0r, h0i = h0
        bu_r = bu_r.at[:, 0].add(ar * h0r - ai * h0i)
        bu_i = bu_i.at[:, 0].add(ar * h0i + ai * h0r)
    L = u.shape[1]
    a_r = jnp.broadcast_to(ar, (1, L) + ar.shape)
    a_i = jnp.broadcast_to(ai, (1, L) + ai.shape)
    _, _, hr, hi = lax.associative_scan(_lin_combine, (a_r, a_i, bu_r, bu_i), axis=1)
    if reverse:
        hr, hi = jnp.flip(hr, axis=1), jnp.flip(hi, axis=1)
    return hr, hi


def ssm_readout(u, st_f, st_b, c_re, c_im, d, w_glu, b_glu, gate, out_dtype):
    B, L = u.shape[0], u.shape[1]
    cr, ci = c_re.astype(jnp.float32), c_im.astype(jnp.float32)
    y = d.astype(jnp.float32) * u
    for di, (hr, hi) in enumerate((st_f, st_b)):
        y = y + jnp.einsum('gcp,blgp->blgc', cr[di], hr) - jnp.einsum('gcp,blgp->blgc', ci[di], hi)
    y = jax.nn.gelu(y.reshape(B, L, SSM_WIDTH).astype(out_dtype))
    y = y * jax.nn.sigmoid(y @ w_glu + b_glu)
    return y * jax.nn.silu(gate)


def merge_out(a_br, s_br, gm, w_pa, w_ps, w_out):
    g_a, g_s = jnp.split(jax.nn.sigmoid(gm), 2, axis=-1)
    return (g_a * (a_br @ w_pa) + g_s * (s_br @ w_ps)) @ w_out


def _layer(x, xc, c, c_ctx, p, lam_init, update_ctx):
    B, L, _ = x.shape
    Lc = xc.shape[1]
    shift, scale, gate = jnp.split(jax.nn.silu(c) @ p['w_ada'] + p['b_ada'], 3, axis=-1)
    shift_c, scale_c, gate_c = jnp.split(jax.nn.silu(c_ctx) @ p['w_ada'] + p['b_ada'], 3, axis=-1)
    h = rmsnorm(x, p['norm_g']) * (1.0 + scale[:, None]) + shift[:, None]
    hc = rmsnorm(xc, p['norm_g']) * (1.0 + scale_c) + shift_c

    q, k, v, ga, u, gs, gm = jnp.split(h @ p['w_in'], SPLITS, axis=-1)
    qc, kc, vc, gac, uc, gsc, gmc = jnp.split(hc @ p['w_in'], SPLITS, axis=-1)

    rows = L // GRID_W
    row = jnp.repeat(jnp.arange(rows), GRID_W).astype(jnp.float32)
    col = jnp.tile(jnp.arange(GRID_W), rows).astype(jnp.float32)
    q = rope2d(q.reshape(B, L, ATT_HEADS, 2, ATT_DIM), row, col)
    k = rope2d(k.reshape(B, L, ATT_HEADS, 2, ATT_DIM), row, col)
    v = v.reshape(B, L, ATT_HEADS, 2 * ATT_DIM)
    kc = kc.reshape(B, Lc, ATT_HEADS, 2, ATT_DIM)
    vc = vc.reshape(B, Lc, ATT_HEADS, 2 * ATT_DIM)
    lam = (jnp.exp(jnp.sum(p['lambda_q1'].astype(jnp.float32) * p['lambda_k1'].astype(jnp.float32)))
           - jnp.exp(jnp.sum(p['lambda_q2'].astype(jnp.float32) * p['lambda_k2'].astype(jnp.float32)))
           + lam_init)
    k_all = jnp.concatenate([k, kc], axis=1)
    v_all = jnp.concatenate([v, vc], axis=1)
    nblk = L // Q_BLOCK
    q_blocks = jnp.moveaxis(q.reshape(B, nblk, Q_BLOCK, ATT_HEADS, 2, ATT_DIM), 1, 0)
    o = lax.map(lambda qb: diff_attend(qb, k_all, v_all, lam), q_blocks)
    o = jnp.moveaxis(o, 0, 1).reshape(B, L, ATT_HEADS, 2 * ATT_DIM)
    a_br = attn_post(o, p['subln_g'], lam_init, ga)

    u = u.reshape(B, L, SSM_GROUPS, SSM_GROUP).astype(jnp.float32)
    uc = uc.reshape(B, Lc, SSM_GROUPS, SSM_GROUP).astype(jnp.float32)
    disc_f = discretize(p['ssm_lambda_re'][0], p['ssm_lambda_im'][0], p['ssm_log_step'][0],
                        p['ssm_b_re'][0], p['ssm_b_im'][0])
    disc_b = discretize(p['ssm_lambda_re'][1], p['ssm_lambda_im'][1], p['ssm_log_step'][1],
                        p['ssm_b_re'][1], p['ssm_b_im'][1])
    hcf = ssm_scan(uc, disc_f, None, reverse=False)
    hcb = ssm_scan(uc, disc_b, None, reverse=True)
    hf = ssm_scan(u, disc_f, (hcf[0][:, -1], hcf[1][:, -1]), reverse=False)
    hb = ssm_scan(u, disc_b, (hcb[0][:, 0], hcb[1][:, 0]), reverse=True)
    s_br = ssm_readout(u, hf, hb, p['ssm_c_re'], p['ssm_c_im'], p['ssm_d'],
                       p['w_glu'], p['b_glu'], gs, x.dtype)

    out = merge_out(a_br, s_br, gm, p['w_pa'], p['w_ps'], p['w_out'])
    x_new = x + gate[:, None] * out

    if update_ctx:
        qc = qc.reshape(B, Lc, ATT_HEADS, 2, ATT_DIM)
        a_c = attn_post(diff_attend(qc, kc, vc, lam), p['subln_g'], lam_init, gac)
        s_c = ssm_readout(uc, hcf, hcb, p['ssm_c_re'], p['ssm_c_im'], p['ssm_d'],
                          p['w_glu'], p['b_glu'], gsc, xc.dtype)
        xc = xc + gate_c * merge_out(a_c, s_c, gmc, p['w_pa'], p['w_ps'], p['w_out'])
    return x_new, xc


def setup_inputs(seed: int = 0) -> dict:
    key = jax.random.key(seed)
    ks = jax.random.split(key, 26)
    f32 = jnp.float32

    def nrm(k, shape, s):
        return jax.random.normal(k, shape, f32) * s

    G, P = SSM_GROUPS, SSM_STATE
    return {
        "x": nrm(ks[0], (BATCH, SEQ, D_MODEL), 1.0),
        "c": nrm(ks[1], (BATCH, D_MODEL), 1.0),
        "ctx": nrm(ks[2], (BATCH, CTX_LEN, D_MODEL), 1.0),
        "c_ctx": nrm(ks[3], (D_MODEL,), 1.0),
        "w_ada": nrm(ks[4], (DEPTH, D_MODEL, 3 * D_MODEL), 0.5 * D_MODEL ** -0.5),
        "b_ada": nrm(ks[5], (DEPTH, 3 * D_MODEL), 0.01),
        "norm_g": 1.0 + nrm(ks[6], (DEPTH, D_MODEL), 0.02),
        "w_in": nrm(ks[7], (DEPTH, D_MODEL, IN_WIDTH), D_MODEL ** -0.5),
        "lambda_q1": nrm(ks[8], (DEPTH, ATT_DIM), 0.1),
        "lambda_k1": nrm(ks[9], (DEPTH, ATT_DIM), 0.1),
        "lambda_q2": nrm(ks[10], (DEPTH, ATT_DIM), 0.1),
        "lambda_k2": nrm(ks[11], (DEPTH, ATT_DIM), 0.1),
        "subln_g": 1.0 + nrm(ks[12], (DEPTH, 2 * ATT_DIM), 0.02),
        "ssm_lambda_re": -0.5 + nrm(ks[13], (DEPTH, 2, G, P), 0.01),
        "ssm_lambda_im": jnp.pi * jnp.arange(P, dtype=f32) + nrm(ks[14], (DEPTH, 2, G, P), 0.01),
        "ssm_log_step": jax.random.uniform(ks[15], (DEPTH, 2, G), f32, math.log(1e-3), math.log(1e-1)),
        "ssm_b_re": nrm(ks[16], (DEPTH, 2, G, P, SSM_GROUP), (2 * SSM_GROUP) ** -0.5),
        "ssm_b_im": nrm(ks[17], (DEPTH, 2, G, P, SSM_GROUP), (2 * SSM_GROUP) ** -0.5),
        "ssm_c_re": nrm(ks[18], (DEPTH, 2, G, SSM_GROUP, P), P ** -0.5),
        "ssm_c_im": nrm(ks[19], (DEPTH, 2, G, SSM_GROUP, P), P ** -0.5),
        "ssm_d": nrm(ks[20], (DEPTH, G, SSM_GROUP), 1.0),
        "w_glu": nrm(ks[21], (DEPTH, SSM_WIDTH, SSM_WIDTH), SSM_WIDTH ** -0.5),
        "b_glu": nrm(ks[22], (DEPTH, SSM_WIDTH), 0.01),
        "w_pa": nrm(ks[23], (DEPTH, ATT_WIDTH, D_MODEL), ATT_WIDTH ** -0.5),
        "w_ps": nrm(ks[24], (DEPTH, SSM_WIDTH, D_MODEL), SSM_WIDTH ** -0.5),
        "w_out": nrm(ks[25], (DEPTH, D_MODEL, D_MODEL), D_MODEL ** -0.5),
        "final_g": 1.0 + nrm(jax.random.fold_in(key, 99), (D_MODEL,), 0.02),
    }


def reference(x, c, ctx, c_ctx, w_ada, b_ada, norm_g, w_in, lambda_q1, lambda_k1, lambda_q2,
              lambda_k2, subln_g, ssm_lambda_re, ssm_lambda_im, ssm_log_step, ssm_b_re, ssm_b_im,
              ssm_c_re, ssm_c_im, ssm_d, w_glu, b_glu, w_pa, w_ps, w_out, final_g):
    xc = ctx
    for i in range(DEPTH):
        p = dict(w_ada=w_ada[i], b_ada=b_ada[i], norm_g=norm_g[i], w_in=w_in[i],
                 lambda_q1=lambda_q1[i], lambda_k1=lambda_k1[i],
                 lambda_q2=lambda_q2[i], lambda_k2=lambda_k2[i], subln_g=subln_g[i],
                 ssm_lambda_re=ssm_lambda_re[i], ssm_lambda_im=ssm_lambda_im[i],
                 ssm_log_step=ssm_log_step[i], ssm_b_re=ssm_b_re[i], ssm_b_im=ssm_b_im[i],
                 ssm_c_re=ssm_c_re[i], ssm_c_im=ssm_c_im[i], ssm_d=ssm_d[i],
                 w_glu=w_glu[i], b_glu=b_glu[i], w_pa=w_pa[i], w_ps=w_ps[i], w_out=w_out[i])
        lam_init = 0.8 - 0.6 * math.exp(-0.3 * i)
        x, xc = _layer(x, xc, c, c_ctx, p, lam_init, update_ctx=(i < DEPTH - 1))
    return rmsnorm(x, final_g)
```

```python
import math
import numpy as np
import ml_dtypes
from contextlib import ExitStack
import concourse.bass as bass
import concourse.mybir as mybir
from concourse.bass_utils import run_bass_kernel_spmd

F32 = mybir.dt.float32
BF16 = mybir.dt.bfloat16
I32 = mybir.dt.int32
AF = mybir.ActivationFunctionType
ALU = mybir.AluOpType

D = 2048
L = 2048
LC = 256
LT = L + LC
NKC = D // 128
INW = 9216
HEADS = 8
EPS = 1e-6
LAM_INIT = 0.8 - 0.6 * math.exp(-0.3 * 0)
TWO_PI = 2.0 * math.pi


class Buf:
    __slots__ = ("name", "w", "r")

    def __init__(self, name):
        self.name = name
        self.w = None
        self.r = {}


class Prog:
    ENG = ["pe", "act", "dve", "pool", "sp"]

    def __init__(self, nc, st):
        self.nc = nc
        self.st = st
        self.q = {e: [] for e in self.ENG}
        self.seen = {e: {} for e in self.ENG}
        self.psem = {e: st.enter_context(nc.semaphore("p_" + e)) for e in ["pe", "act", "dve", "pool"]}
        self.dsems = []

    def new_dsem(self, name):
        h = self.st.enter_context(self.nc.semaphore(name))
        d = {"h": h, "n": 0, "name": name}
        self.dsems.append(d)
        return d

    def _deps(self, eng, reads, writes):
        need = {}

        def add(t):
            if t[0] == "c":
                if t[1] == "pe" and eng == "pe":
                    return
                key = ("c", t[1])
                if need.get(key, (None, -1))[1] < t[2]:
                    need[key] = (t[1], t[2])
            else:
                key = ("d", id(t[1]))
                if need.get(key, (None, -1))[1] < t[2]:
                    need[key] = (t[1], t[2])

        for b in reads:
            if b.w is not None:
                add(b.w)
        for b in writes:
            if b.w is not None:
                add(b.w)
            for t in b.r.values():
                add(t)
        waits = []
        for key, (obj, v) in need.items():
            if self.seen[eng].get(key, -1) >= v:
                continue
            self.seen[eng][key] = v
            waits.append((key[0], obj, v))
        return waits

    def _record(self, tok, reads, writes):
        for b in reads:
            key = (tok[0], tok[1] if tok[0] == "c" else id(tok[1]))
            b.r[key] = tok
        for b in writes:
            b.w = tok
            b.r = {}

    def op(self, eng, fn, reads=(), writes=()):
        waits = self._deps(eng, reads, writes)
        idx = len(self.q[eng])
        self.q[eng].append({"fn": fn, "waits": waits, "awaited": False, "dma": None})
        tok = ("c", eng, idx)
        self._record(tok, reads, writes)
        return tok

    def dma(self, eng, out, in_, reads=(), writes=(), sem=None, **kw):
        waits = self._deps(eng, reads, writes)
        sem["n"] += 16
        tok = ("d", sem, sem["n"])
        self.q[eng].append({"fn": (lambda e, o=out, i=in_, k=kw: e.dma_start(out=o, in_=i, **k)),
                            "waits": waits, "awaited": False, "dma": sem})
        self._record(tok, reads, writes)
        return tok

    def barrier(self):
        for e in self.ENG:
            waits = []
            for e2 in ["pe", "act", "dve", "pool"]:
                n = len(self.q[e2])
                if e2 == e:
                    n -= 0
                idx = None
                for i in range(len(self.q[e2]) - 1, -1, -1):
                    if self.q[e2][i]["fn"] is not None and self.q[e2][i]["dma"] is None:
                        idx = i
                        break
                if idx is None:
                    continue
                key = ("c", e2)
                if self.seen[e].get(key, -1) >= idx:
                    continue
                self.seen[e][key] = idx
                waits.append(("c", e2, idx))
            for d in self.dsems:
                if d["n"] == 0:
                    continue
                key = ("d", id(d))
                if self.seen[e].get(key, -1) >= d["n"]:
                    continue
                self.seen[e][key] = d["n"]
                waits.append(("d", d, d["n"]))
            if waits:
                self.q[e].append({"fn": None, "waits": waits, "awaited": False, "dma": None})

    def emit(self):
        for e in self.ENG:
            for ent in self.q[e]:
                for w in ent["waits"]:
                    if w[0] == "c":
                        self.q[w[1]][w[2]]["awaited"] = True
        cnt = {}
        for e in ["pe", "act", "dve", "pool"]:
            c = 0
            arr = []
            for ent in self.q[e]:
                if ent["awaited"]:
                    c += 1
                arr.append(c)
            cnt[e] = arr
        psem = self.psem
        q = self.q

        def run(name, e):
            for ent in q[name]:
                for w in ent["waits"]:
                    if w[0] == "c":
                        e.wait_ge(psem[w[1]], cnt[w[1]][w[2]])
                    else:
                        e.wait_ge(w[1]["h"], w[2])
                if ent["fn"] is None:
                    continue
                inst = ent["fn"](e)
                if ent["dma"] is not None:
                    inst.then_inc(ent["dma"]["h"], 16)
                elif ent["awaited"]:
                    inst.then_inc(psem[name], 1)

        with self.nc.Block() as block:
            @block.sync
            def _(e):
                run("sp", e)

            @block.scalar
            def _(e):
                run("act", e)

            @block.vector
            def _(e):
                run("dve", e)

            @block.gpsimd
            def _(e):
                run("pool", e)

            @block.tensor
            def _(e):
                run("pe", e)


def _consts():
    ident = np.eye(128, dtype=np.float32)
    m = np.arange(128)
    partner = np.where((m % 32) < 16, m + 16, m - 16)
    perm = np.zeros((128, 128), np.float32)
    perm[partner, m] = 1.0
    sgn = np.where((m % 32) < 16, -1.0, 1.0).astype(np.float32)
    tok = np.arange(L)
    pos = np.where(((m % 64) < 32)[:, None], (tok // 64)[None, :], (tok % 64)[None, :]).astype(np.float32)
    fexp = ((m % 16) / 16.0).astype(np.float32)
    colc = np.zeros((128, 4), np.float32)
    colc[:, 0] = sgn
    colc[:, 1] = fexp
    colc[:, 2] = np.where(m < 64, 1.0, -1.0)
    sel = np.zeros((2, 128), np.float32)
    sel[0, :] = 1.0
    tauA = np.zeros((128, 32, 9), np.float32)
    tauA[:, 0:16, :] = np.arange(9)[None, None, :]
    tauA[:, 16:32, :] = (8 - np.arange(9))[None, None, :]
    tauB = np.zeros((128, 32, 8), np.float32)
    tauB[:, 0:16, :] = (7 - np.arange(8))[None, None, :]
    tauB[:, 16:32, :] = np.arange(8)[None, None, :]
    return {"c_ident": ident, "c_perm": perm, "c_pos": pos, "c_col": colc, "c_sel": sel, "c_tauA": tauA, "c_tauB": tauB}


class K:
    pass


def build(debug=None):
    nc = bass.Bass("TRN2", target_bir_lowering=False)
    st = ExitStack()
    P = Prog(nc, st)
    k = K()
    k.nc, k.P, k.st = nc, P, st

    def dram_in(name, shape, dt=F32):
        return nc.dram_tensor(name, list(shape), dt, kind="ExternalInput").ap()

    dbg_outs = []

    def dram_scr(name, shape, dt):
        kind = "Internal"
        if debug is not None and name in debug.get("_inject", ()):
            kind = "ExternalInput"
        elif debug is not None and name in debug:
            kind = "ExternalOutput"
            dbg_outs.append(name)
        return nc.dram_tensor(name, list(shape), dt, kind=kind).ap()

    I = {}
    I["x"] = dram_in("x", [L, D])
    I["ctx"] = dram_in("ctx", [LC, D])
    I["cc"] = dram_in("cc", [2, D])
    I["w_ada"] = dram_in("w_ada", [D, 3 * D])
    I["b_ada"] = dram_in("b_ada", [1, 3 * D])
    I["norm_g"] = dram_in("norm_g", [D])
    I["w_in"] = dram_in("w_in", [D, INW])
    I["lam"] = dram_in("lam", [4, 64])
    I["subln_g"] = dram_in("subln_g", [1, 128])
    I["ssm_lre"] = dram_in("ssm_lre", [2, 32, 64])
    I["ssm_lim"] = dram_in("ssm_lim", [2, 32, 64])
    I["ssm_ls"] = dram_in("ssm_ls", [2, 32])
    I["ssm_bre"] = dram_in("ssm_bre", [2, 32, 64, 16])
    I["ssm_bim"] = dram_in("ssm_bim", [2, 32, 64, 16])
    I["ssm_cre"] = dram_in("ssm_cre", [2, 32, 16, 64])
    I["ssm_cim"] = dram_in("ssm_cim", [2, 32, 16, 64])
    I["ssm_d"] = dram_in("ssm_d", [1, 512])
    I["w_glu"] = dram_in("w_glu", [512, 512])
    I["b_glu"] = dram_in("b_glu", [512])
    I["w_pa"] = dram_in("w_pa", [1024, D])
    I["w_ps"] = dram_in("w_ps", [512, D])
    I["w_out"] = dram_in("w_out", [D, D])
    I["final_g"] = dram_in("final_g", [1, D])
    for cn, arr in _consts().items():
        I[cn] = dram_in(cn, arr.shape)
    out = nc.dram_tensor("out", [L, D], F32, kind="ExternalOutput").ap()

    S = {}
    S["modrow"] = dram_scr("modrow", [2, 3 * D], F32)
    S["qT"] = dram_scr("qT", [HEADS, 128, L], BF16)
    S["kT"] = dram_scr("kT", [HEADS, 128, LT], BF16)
    S["v"] = dram_scr("v", [LT, 1024], BF16)
    S["sga"] = dram_scr("sga", [L, 1024], BF16)
    S["u"] = dram_scr("u", [LT, 512], F32)
    S["sgsT"] = dram_scr("sgsT", [512, L], BF16)
    S["sgmT"] = dram_scr("sgmT", [2 * D, L], BF16)
    S["abrT"] = dram_scr("abrT", [1024, L], BF16)
    S["sbrT"] = dram_scr("sbrT", [512, L], BF16)
    S["hT"] = dram_scr("hT_dbg", [128, NKC, LT], BF16) if (debug is not None and "hT_dbg" in debug) else None
    k.I, k.S, k.out = I, S, out
    k.dbg_sem = None

    def dbg(name, shape, dt, ap_fn, bufs):
        if debug is None or name not in debug:
            return
        if name not in S:
            S[name] = nc.dram_tensor(name, list(shape), dt, kind="ExternalOutput").ap()
            dbg_outs.append(name)
        if k.dbg_sem is None:
            k.dbg_sem = P.new_dsem("dbgsem")
        o, i = ap_fn(S[name])
        P.dma("sp", o, i, reads=bufs, sem=k.dbg_sem)
    k.dbg = dbg

    def sb(name, shape, dt, stack=st):
        return stack.enter_context(nc.sbuf_tensor(name, list(shape), dt))

    def ps(name, shape, dt, stack=st):
        return stack.enter_context(nc.psum_tensor(name, list(shape), dt))

    k.sb, k.ps = sb, ps
    ident = sb("ident", [128, 128], F32)
    colc = sb("colc", [128, 4], F32)
    b_ident, b_colc = Buf("ident"), Buf("colc")
    csem = P.new_dsem("csem")
    P.dma("sp", ident[:], I["c_ident"], writes=[b_ident], sem=csem)
    P.dma("sp", colc[:], I["c_col"], writes=[b_colc], sem=csem)
    k.ident, k.b_ident, k.colc, k.b_colc, k.csem = ident, b_ident, colc, b_colc, csem

    phase_adaln(k)
    P.barrier()
    if debug is None or debug.get("_upto", 99) >= 1:
        phase_norm_inproj(k, debug)
        P.barrier()
    if (debug is None or debug.get("_upto", 99) >= 2) and not (debug or {}).get("_skip_ssm"):
        phase_ssm(k)
        P.barrier()
    if debug is None or debug.get("_upto", 99) >= 3:
        phase_attn(k)
        P.barrier()
    if debug is None or debug.get("_upto", 99) >= 4:
        phase_merge(k)
        P.barrier()
    P.emit()
    st.close()
    return nc, dbg_outs


def range_sin(k, stack, out_ap, y_ap, shape, tag, rbufs, wbufs, eng="dve"):
    nc, P = k.nc, k.P
    ki = k.sb(tag + "_ki", shape, I32, stack)
    kf = k.sb(tag + "_kf", shape, F32, stack)
    g = k.sb(tag + "_g", shape, F32, stack)
    bki, bkf, bg = Buf(tag + "ki"), Buf(tag + "kf"), Buf(tag + "g")
    sl = tuple([slice(None)] * len(shape))
    P.op(eng, lambda e: e.tensor_copy(out=ki[sl], in_=y_ap), reads=rbufs, writes=[bki])
    P.op(eng, lambda e: e.tensor_copy(out=kf[sl], in_=ki[sl]), reads=[bki], writes=[bkf])
    P.op(eng, lambda e: e.tensor_tensor(out=kf[sl], in0=y_ap, in1=kf[sl], op=ALU.subtract), reads=rbufs + [bkf], writes=[bkf])
    P.op(eng, lambda e: e.tensor_single_scalar(out=g[sl], in_=kf[sl], scalar=0.5, op=ALU.is_gt), reads=[bkf], writes=[bg])
    P.op(eng, lambda e: e.tensor_tensor(out=kf[sl], in0=kf[sl], in1=g[sl], op=ALU.subtract), reads=[bkf, bg], writes=[bkf])
    P.op(eng, lambda e: e.tensor_single_scalar(out=g[sl], in_=kf[sl], scalar=-0.5, op=ALU.is_lt), reads=[bkf], writes=[bg])
    P.op(eng, lambda e: e.tensor_tensor(out=kf[sl], in0=kf[sl], in1=g[sl], op=ALU.add), reads=[bkf, bg], writes=[bkf])
    P.op("act", lambda e: e.activation(out=out_ap, in_=kf[sl], func=AF.Sin, scale=TWO_PI * (1.0 - 2e-7)), reads=[bkf], writes=wbufs)


def phase_adaln(k):
    nc, P, I, S = k.nc, k.P, k.I, k.S
    with ExitStack() as ls:
        sT = k.sb("ad_sT", [128, NKC, 2], F32, ls)
        b_sT = Buf("sT")
        sem_c = P.new_dsem("ad_c")
        for v in range(2):
            P.dma("sp", sT[:, :, v], I["cc"][v].rearrange("(kc p) -> p kc", p=128), writes=[b_sT], sem=sem_c,
                  allow_slow_non_contiguous=True)
        P.op("act", lambda e: e.activation(out=sT[:], in_=sT[:], func=AF.Silu), reads=[b_sT], writes=[b_sT])
        brow = k.sb("ad_brow", [2, 3 * D], F32, ls)
        b_brow = Buf("brow")
        for v in range(2):
            P.dma("sp", brow[v:v + 1, :], I["b_ada"], writes=[b_brow], sem=sem_c)
        modrow = k.sb("ad_modrow", [2, 3 * D], F32, ls)
        b_modrow = Buf("modrow")
        NS = 2
        wst = [k.sb(f"ad_w{i}", [128, NKC, 512], F32, ls) for i in range(NS)]
        b_w = [Buf(f"adw{i}") for i in range(NS)]
        wsem = [P.new_dsem(f"ad_ws{i}") for i in range(NS)]
        pst = [k.ps(f"ad_ps{i}", [128, 512], F32, ls) for i in range(2)]
        b_ps = [Buf(f"adps{i}") for i in range(2)]
        wv = I["w_ada"].rearrange("(kc p) c -> p kc c", p=128)
        for cb in range(12):
            s = cb % NS
            P.dma("sp", wst[s][:, 0:8, :], wv[:, 0:8, cb * 512:(cb + 1) * 512], writes=[b_w[s]], sem=wsem[s])
            P.dma("act", wst[s][:, 8:16, :], wv[:, 8:16, cb * 512:(cb + 1) * 512], writes=[b_w[s]], sem=wsem[s])
            pt, bp = pst[cb % 2], b_ps[cb % 2]
            for kc in range(NKC):
                P.op("pe", lambda e, kc=kc, s=s, pt=pt: e.matmul(pt[0:2, :], lhsT=sT[:, kc, :], rhs=wst[s][:, kc, :],
                                                              start=(kc == 0), stop=(kc == NKC - 1)),
                     reads=[b_sT, b_w[s]], writes=[bp])
            P.op("dve", lambda e, cb=cb, pt=pt: e.tensor_tensor(out=modrow[:, cb * 512:(cb + 1) * 512], in0=pt[0:2, :],
                                                             in1=brow[:, cb * 512:(cb + 1) * 512], op=ALU.add),
                 reads=[bp, b_brow], writes=[b_modrow])
        b_mr = Buf("modrow_d")
        k.b_modrow_d = b_mr
        P.dma("sp", S["modrow"], modrow[:], reads=[b_modrow], writes=[b_mr], sem=sem_c)


def phase_norm_inproj(k, debug):
    nc, P, I, S = k.nc, k.P, k.I, k.S
    with ExitStack() as ls:
        hT = k.sb("hT", [128, NKC, LT], BF16, ls)
        b_hT = [Buf(f"hT{t}") for t in range(18)]
        Amod = k.sb("Amod", [128, NKC, 2], F32, ls)
        Smod = k.sb("Smod", [128, NKC, 2], F32, ls)
        gcol = k.sb("gcol", [128, NKC], F32, ls)
        b_A, b_S, b_g = Buf("Amod"), Buf("Smod"), Buf("gcol")
        msem = P.new_dsem("n_m")
        for v in range(2):
            P.dma("sp", Smod[:, :, v], S["modrow"][v, 0:D].rearrange("(kc p) -> p kc", p=128),
                  reads=[k.b_modrow_d], writes=[b_S], sem=msem, allow_slow_non_contiguous=True)
            P.dma("sp", Amod[:, :, v], S["modrow"][v, D:2 * D].rearrange("(kc p) -> p kc", p=128),
                  reads=[k.b_modrow_d], writes=[b_A], sem=msem, allow_slow_non_contiguous=True)
        P.dma("sp", gcol[:], I["norm_g"].rearrange("(kc p) -> p kc", p=128), writes=[b_g], sem=msem,
              allow_slow_non_contiguous=True)
        for v in range(2):
            P.op("dve", lambda e, v=v: e.scalar_tensor_tensor(out=Amod[:, :, v], in0=Amod[:, :, v], scalar=1.0, in1=gcol[:],
                                                             op0=ALU.add, op1=ALU.mult),
                 reads=[b_A, b_g], writes=[b_A])
        with ExitStack() as l1:
            NX = 2
            xt = [k.sb(f"n_x{i}", [128, D], F32, l1) for i in range(NX)]
            b_x = [Buf(f"nx{i}") for i in range(NX)]
            xsem = [P.new_dsem(f"n_xs{i}") for i in range(NX)]
            junk = k.sb("n_junk", [128, D], BF16, l1)
            b_junk = Buf("junk")
            stat = [k.sb(f"n_st{i}", [128, 4], F32, l1) for i in range(NX)]
            b_stat = [Buf(f"nst{i}") for i in range(NX)]
            pt = [k.ps(f"n_ps{i}", [128, 512], F32, l1) for i in range(4)]
            b_pt = [Buf(f"nps{i}") for i in range(4)]
            pi = 0
            for t in range(18):
                s = t % NX
                v = 0 if t < 16 else 1
                src = I["x"][t * 128:(t + 1) * 128, :] if t < 16 else I["ctx"][(t - 16) * 128:(t - 15) * 128, :]
                P.dma("sp", xt[s][:, 0:1024], src[:, 0:1024], writes=[b_x[s]], sem=xsem[s])
                P.dma("act", xt[s][:, 1024:2048], src[:, 1024:2048], writes=[b_x[s]], sem=xsem[s])
                P.op("act", lambda e, s=s: e.activation(out=junk[:], in_=xt[s][:], func=AF.Square, accum_out=stat[s][:, 0:1]),
                     reads=[b_x[s]], writes=[b_junk, b_stat[s]])
                P.op("dve", lambda e, s=s: e.tensor_scalar(out=stat[s][:, 1:2], in0=stat[s][:, 0:1], scalar1=1.0 / D, scalar2=EPS,
                                                         op0=ALU.mult, op1=ALU.add),
                     reads=[b_stat[s]], writes=[b_stat[s]])
                P.op("act", lambda e, s=s: e.activation(out=stat[s][:, 2:3], in_=stat[s][:, 1:2], func=AF.Sqrt),
                     reads=[b_stat[s]], writes=[b_stat[s]])
                P.op("dve", lambda e, s=s: e.reciprocal(out=stat[s][:, 3:4], in_=stat[s][:, 2:3]),
                     reads=[b_stat[s]], writes=[b_stat[s]])
                P.op("dve", lambda e, s=s: e.tensor_scalar(out=xt[s][:], in0=xt[s][:], scalar1=stat[s][:, 3:4], scalar2=None,
                                                         op0=ALU.mult),
                     reads=[b_x[s], b_stat[s]], writes=[b_x[s]])
                for g4 in range(4):
                    p_, bp = pt[pi % 4], b_pt[pi % 4]
                    pi += 1
                    for j in range(4):
                        kc = g4 * 4 + j
                        P.op("pe", lambda e, s=s, kc=kc, j=j, p_=p_: e.transpose(out=p_[:, j * 128:(j + 1) * 128],
                                                                             in_=xt[s][:, kc * 128:(kc + 1) * 128],
                                                                             identity=k.ident[:]),
                             reads=[b_x[s], k.b_ident], writes=[bp])
                    for j in range(4):
                        kc = g4 * 4 + j
                        eng = "dve" if (j % 2 == 0) else "act"
                        if eng == "dve":
                            P.op("dve", lambda e, kc=kc, j=j, p_=p_, t=t, v=v: e.tensor_scalar(
                                out=hT[:, kc, t * 128:(t + 1) * 128], in0=p_[:, j * 128:(j + 1) * 128],
                                scalar1=Amod[:, kc, v:v + 1], scalar2=Smod[:, kc, v:v + 1], op0=ALU.mult, op1=ALU.add),
                                reads=[bp, b_A, b_S], writes=[b_hT[t]])
                        else:
                            P.op("act", lambda e, kc=kc, j=j, p_=p_, t=t, v=v: e.activation(
                                out=hT[:, kc, t * 128:(t + 1) * 128], in_=p_[:, j * 128:(j + 1) * 128],
                                func=AF.Identity, scale=Amod[:, kc, v:v + 1], bias=Smod[:, kc, v:v + 1]),
                                reads=[bp, b_A, b_S], writes=[b_hT[t]])
        if S["hT"] is not None:
            dsem = P.new_dsem("dbg")
            P.dma("sp", S["hT"], hT[:], reads=b_hT, writes=[Buf("x")], sem=dsem)
        P.barrier()
        if debug is not None and debug.get("_upto", 99) < 1.5:
            return
        inproj(k, ls, hT, b_hT)


def inproj(k, ls, hT, b_hT):
    nc, P, I, S = k.nc, k.P, k.I, k.S
    cosT = k.sb("cosT", [128, L], F32, ls)
    sinS = k.sb("sinS", [128, L], F32, ls)
    perm = k.sb("perm", [128, 128], F32, ls)
    b_cos, b_sin, b_perm = Buf("cos"), Buf("sin"), Buf("perm")
    tsem = P.new_dsem("ip_t")
    P.dma("sp", perm[:], I["c_perm"], writes=[b_perm], sem=tsem)
    with ExitStack() as l0:
        pos = k.sb("pos", [128, L], F32, l0)
        yv = k.sb("yv", [128, L], F32, l0)
        inv = k.sb("inv", [128, 1], F32, l0)
        b_pos, b_y, b_inv = Buf("pos"), Buf("yv"), Buf("inv")
        P.dma("sp", pos[:], I["c_pos"], writes=[b_pos], sem=tsem)
        P.op("act", lambda e: e.activation(out=inv[:], in_=k.colc[:, 1:2], func=AF.Exp, scale=-math.log(10000.0)),
             reads=[k.b_colc], writes=[b_inv])
        P.op("dve", lambda e: e.tensor_scalar(out=yv[:], in0=pos[:], scalar1=inv[:, 0:1], scalar2=1.0 / TWO_PI,
                                              op0=ALU.mult, op1=ALU.mult), reads=[b_pos, b_inv], writes=[b_y])
        range_sin(k, l0, sinS[:], yv[:], [128, L], "rs1", [b_y], [b_sin])
        P.op("dve", lambda e: e.tensor_scalar(out=sinS[:], in0=sinS[:], scalar1=k.colc[:, 0:1], scalar2=None, op0=ALU.mult),
             reads=[b_sin, k.b_colc], writes=[b_sin])
        P.op("dve", lambda e: e.tensor_scalar(out=yv[:], in0=yv[:], scalar1=0.25, scalar2=None, op0=ALU.add),
             reads=[b_y], writes=[b_y])
        range_sin(k, l0, cosT[:], yv[:], [128, L], "rs2", [b_y], [b_cos])
        P.barrier()
    NW = 2
    wst = [k.sb(f"ip_wst{i}", [128, 8, 512], F32, ls) for i in range(NW)]
    b_wst = [Buf(f"wst{i}") for i in range(NW)]
    wsem = [P.new_dsem(f"ip_ws{i}") for i in range(NW)]
    wb = [k.sb(f"ip_wb{i}", [128, NKC, 512], BF16, ls) for i in range(2)]
    b_wb = [Buf(f"wb{i}") for i in range(2)]
    NOB = 4
    ob = [k.sb(f"ip_ob{i}", [128, 512], BF16, ls) for i in range(NOB)]
    b_ob = [Buf(f"ob{i}") for i in range(NOB)]
    osem = [P.new_dsem(f"ip_os{i}") for i in range(NOB)]
    NOF = 2
    of = [k.sb(f"ip_of{i}", [128, 512], F32, ls) for i in range(NOF)]
    b_of = [Buf(f"of{i}") for i in range(NOF)]
    fsem = [P.new_dsem(f"ip_fs{i}") for i in range(NOF)]
    t1 = [k.sb(f"ip_t1{i}", [128, 512], F32, ls) for i in range(2)]
    b_t1 = [Buf(f"t1{i}") for i in range(2)]
    t2 = [k.sb(f"ip_t2{i}", [128, 512], F32, ls) for i in range(2)]
    b_t2 = [Buf(f"t2{i}") for i in range(2)]
    pb = [k.ps(f"ip_ps{i}", [128, 512], F32, ls) for i in range(4)]
    b_pb = [Buf(f"ipps{i}") for i in range(4)]
    pr = [k.ps(f"ip_pr{i}", [128, 512], F32, ls) for i in range(2)]
    b_pr = [Buf(f"ippr{i}") for i in range(2)]
    wv = I["w_in"].rearrange("(kc p) c -> p kc c", p=128)
    cnt = {"pb": 0, "ob": 0, "of": 0, "r": 0, "ld": 0, "ev": 0}

    def load_block(cb):
        s2 = cb % 2
        for half in range(2):
            sl = cnt["ld"] % NW
            cnt["ld"] += 1
            P.dma("sp", wst[sl][:], wv[:, half * 8:(half + 1) * 8, cb * 512:(cb + 1) * 512], writes=[b_wst[sl]], sem=wsem[sl])
            P.op("dve", lambda e, sl=sl, s2=s2, half=half: e.tensor_copy(out=wb[s2][:, half * 8:(half + 1) * 8, :], in_=wst[sl][:]),
                 reads=[b_wst[sl]], writes=[b_wb[s2]])

    def next_ob():
        i = cnt["ob"] % NOB
        cnt["ob"] += 1
        return i

    def evac_eng():
        cnt["ev"] += 1
        return "act" if cnt["ev"] % 2 else "dve"

    def tiles_of(tok0, n):
        return [b_hT[t] for t in range(tok0 // 128, (tok0 + n) // 128)]

    def fm_unit(cb, fc, tok0, n, kind, row0, dst):
        s2 = cb % 2
        pi = cnt["pb"] % 4
        cnt["pb"] += 1
        pt, bp = pb[pi], b_pb[pi]
        for kc in range(NKC):
            P.op("pe", lambda e, kc=kc: e.matmul(pt[:, 0:n], lhsT=wb[s2][:, kc, fc * 128:(fc + 1) * 128],
                                                 rhs=hT[:, kc, tok0:tok0 + n], start=(kc == 0), stop=(kc == NKC - 1)),
                 reads=[b_wb[s2]] + tiles_of(tok0, n), writes=[bp])
        oi = next_ob()
        if kind == "rope":
            ri = cnt["r"] % 2
            cnt["r"] += 1
            fi = cnt["of"] % NOF
            cnt["of"] += 1
            P.op("act", lambda e: e.activation(out=of[fi][:, 0:n], in_=pt[:, 0:n], func=AF.Copy), reads=[bp], writes=[b_of[fi]])
            P.op("pe", lambda e: e.matmul(pr[ri][:, 0:n], lhsT=perm[:], rhs=of[fi][:, 0:n], start=True, stop=True),
                 reads=[b_perm, b_of[fi]], writes=[b_pr[ri]])
            P.op("dve", lambda e: e.tensor_tensor(out=t1[ri][:, 0:n], in0=of[fi][:, 0:n], in1=cosT[:, tok0:tok0 + n], op=ALU.mult),
                 reads=[b_of[fi], b_cos], writes=[b_t1[ri]])
            P.op("dve", lambda e: e.tensor_tensor(out=t2[ri][:, 0:n], in0=pr[ri][:, 0:n], in1=sinS[:, tok0:tok0 + n], op=ALU.mult),
                 reads=[b_pr[ri], b_sin], writes=[b_t2[ri]])
            P.op("pool", lambda e: e.tensor_tensor(out=ob[oi][:, 0:n], in0=t1[ri][:, 0:n], in1=t2[ri][:, 0:n], op=ALU.add),
                 reads=[b_t1[ri], b_t2[ri]], writes=[b_ob[oi]])
        elif kind == "copy":
            eg = evac_eng()
            if eg == "act":
                P.op("act", lambda e: e.activation(out=ob[oi][:, 0:n], in_=pt[:, 0:n], func=AF.Copy), reads=[bp], writes=[b_ob[oi]])
            else:
                P.op("dve", lambda e: e.tensor_copy(out=ob[oi][:, 0:n], in_=pt[:, 0:n]), reads=[bp], writes=[b_ob[oi]])
        else:
            fn = AF.Silu if kind == "silu" else AF.Sigmoid
            P.op("act", lambda e: e.activation(out=ob[oi][:, 0:n], in_=pt[:, 0:n], func=fn), reads=[bp], writes=[b_ob[oi]])
        P.dma("pool", dst, ob[oi][:, 0:n], reads=[b_ob[oi]], sem=osem[oi])

    def tm_unit(cb, t, kind, dst):
        s2 = cb % 2
        pi = cnt["pb"] % 4
        cnt["pb"] += 1
        pt, bp = pb[pi], b_pb[pi]
        for kc in range(NKC):
            P.op("pe", lambda e, kc=kc: e.matmul(pt[:], lhsT=hT[:, kc, t * 128:(t + 1) * 128], rhs=wb[s2][:, kc, :],
                                                 start=(kc == 0), stop=(kc == NKC - 1)),
                 reads=[b_wb[s2], b_hT[t]], writes=[bp])
        if kind == "f32":
            fi = cnt["of"] % NOF
            cnt["of"] += 1
            P.op("dve", lambda e: e.tensor_copy(out=of[fi][:], in_=pt[:]), reads=[bp], writes=[b_of[fi]])
            P.dma("pool", dst, of[fi][:], reads=[b_of[fi]], sem=fsem[fi])
            return
        oi = next_ob()
        if kind == "copy":
            eg = evac_eng()
            if eg == "act":
                P.op("act", lambda e: e.activation(out=ob[oi][:], in_=pt[:], func=AF.Copy), reads=[bp], writes=[b_ob[oi]])
            else:
                P.op("dve", lambda e: e.tensor_copy(out=ob[oi][:], in_=pt[:]), reads=[bp], writes=[b_ob[oi]])
        else:
            P.op("act", lambda e: e.activation(out=ob[oi][:], in_=pt[:], func=AF.Silu), reads=[bp], writes=[b_ob[oi]])
        P.dma("pool", dst, ob[oi][:], reads=[b_ob[oi]], sem=osem[oi])

    NCB = INW // 512
    load_block(0)
    for cb in range(NCB):
        if cb + 1 < NCB:
            load_block(cb + 1)
        c0 = cb * 512
        if cb < 2:
            for fc in range(4):
                h = cb * 4 + fc
                for tb in range(4):
                    fm_unit(cb, fc, tb * 512, 512, "rope", 0, S["qT"][h, :, tb * 512:(tb + 1) * 512])
        elif cb < 4:
            for fc in range(4):
                h = (cb - 2) * 4 + fc
                for tb in range(4):
                    fm_unit(cb, fc, tb * 512, 512, "rope", 0, S["kT"][h, :, tb * 512:(tb + 1) * 512])
                fm_unit(cb, fc, L, LC, "copy", 0, S["kT"][h, :, L:LT])
        elif cb < 6:
            for t in range(18):
                tm_unit(cb, t, "copy", S["v"][t * 128:(t + 1) * 128, (cb - 4) * 512:(cb - 3) * 512])
        elif cb < 8:
            for t in range(16):
                tm_unit(cb, t, "silu", S["sga"][t * 128:(t + 1) * 128, (cb - 6) * 512:(cb - 5) * 512])
        elif cb == 8:
            for t in range(18):
                tm_unit(cb, t, "f32", S["u"][t * 128:(t + 1) * 128, :])
        elif cb == 9:
            for fc in range(4):
                for tb in range(4):
                    fm_unit(cb, fc, tb * 512, 512, "silu", 0, S["sgsT"][fc * 128:(fc + 1) * 128, tb * 512:(tb + 1) * 512])
        else:
            for fc in range(4):
                r0 = (cb - 10) * 512 + fc * 128
                for tb in range(4):
                    fm_unit(cb, fc, tb * 512, 512, "sigm", 0, S["sgmT"][r0:r0 + 128, tb * 512:(tb + 1) * 512])


def phase_ssm(k):
    nc, P, I, S = k.nc, k.P, k.I, k.S
    MUL, ADD, SUB = ALU.mult, ALU.add, ALU.subtract
    with ExitStack() as ls:
        ToepT = k.sb("ss_toep", [128, 32, 128], BF16, ls)
        RCp = k.sb("ss_rcp", [128, 2, 2, 16, 256], BF16, ls)
        WT = k.sb("ss_wt", [128, 2, 16, 2, 128], BF16, ls)
        A8c = k.sb("ss_a8c", [128, 2, 16, 2], F32, ls)
        A8s = k.sb("ss_a8s", [128, 2, 16, 2], F32, ls)
        b_toep = [Buf(f"toep{g}") for g in range(32)]
        b_rcp, b_wt = Buf("rcp"), Buf("wt")
        b_U = [Buf(f"U{g}") for g in range(32)]
        b_zbf = [Buf("zbf0"), Buf("zbf1")]
        b_ygT = Buf("ygT")
        b_a8 = Buf("a8")
        pbk = [k.ps(f"ss_ps{i}", [128, 512], F32, ls) for i in range(8)]
        b_pbk = [Buf(f"ssps{i}") for i in range(8)]
        pc = {"i": 0}

        def nb():
            i = pc["i"] % 8
            pc["i"] += 1
            return pbk[i], b_pbk[i]

        csem = P.new_dsem("ss_c")
        with ExitStack() as l0:
            lre = k.sb("ss_lre", [128, 32], F32, l0)
            lim = k.sb("ss_lim", [128, 32], F32, l0)
            dtt = k.sb("ss_dt", [128, 32], F32, l0)
            alog = k.sb("ss_alog", [128, 32], F32, l0)
            th = k.sb("ss_th", [128, 32], F32, l0)
            b_l, b_dt, b_al = Buf("lrelim"), Buf("dtt"), Buf("alogth")
            for gp in range(2):
                for d in range(2):
                    P.dma("sp", lre[gp * 64:(gp + 1) * 64, d * 16:(d + 1) * 16], I["ssm_lre"][d, gp * 16:(gp + 1) * 16, :].rearrange("g p -> p g"),
                          writes=[b_l], sem=csem, allow_slow_non_contiguous=True)
                    P.dma("sp", lim[gp * 64:(gp + 1) * 64, d * 16:(d + 1) * 16], I["ssm_lim"][d, gp * 16:(gp + 1) * 16, :].rearrange("g p -> p g"),
                          writes=[b_l], sem=csem, allow_slow_non_contiguous=True)
                    P.dma("sp", dtt[gp * 64:(gp + 1) * 64, d * 16:(d + 1) * 16], I["ssm_ls"][d:d + 1, gp * 16:(gp + 1) * 16].broadcast_to([64, 16]),
                          writes=[b_dt], sem=csem)
            P.op("act", lambda e: e.activation(out=dtt[:], in_=dtt[:], func=AF.Exp), reads=[b_dt], writes=[b_dt])
            P.op("dve", lambda e: e.tensor_tensor(out=alog[:], in0=lre[:], in1=dtt[:], op=MUL), reads=[b_l, b_dt], writes=[b_al])
            P.op("dve", lambda e: e.scalar_tensor_tensor(out=th[:], in0=lim[:], scalar=1.0 / TWO_PI, in1=dtt[:], op0=MUL, op1=MUL),
                 reads=[b_l, b_dt], writes=[b_al])
            tabs = {}
            for nm, n in (("A", 9), ("B", 8)):
                tau = k.sb(f"ss_tau{nm}", [128, 32, n], F32, l0)
                ex = k.sb(f"ss_ex{nm}", [128, 32, n], F32, l0)
                yv = k.sb(f"ss_yv{nm}", [128, 32, n], F32, l0)
                sn = k.sb(f"ss_sn{nm}", [128, 32, n], F32, l0)
                cs = k.sb(f"ss_cs{nm}", [128, 32, n], F32, l0)
                b_tau, b_ex, b_yv, b_sn, b_cs = Buf("tau" + nm), Buf("ex" + nm), Buf("yv" + nm), Buf("sn" + nm), Buf("cs" + nm)
                P.dma("sp", tau[:], I["c_tau" + nm], writes=[b_tau], sem=csem)
                P.op("dve", lambda e, ex=ex, tau=tau, n=n: e.tensor_tensor(out=ex[:], in0=tau[:], in1=alog[:, :, None].broadcast_to([128, 32, n]), op=MUL),
                     reads=[b_tau, b_al], writes=[b_ex])
                P.op("act", lambda e, ex=ex: e.activation(out=ex[:], in_=ex[:], func=AF.Exp), reads=[b_ex], writes=[b_ex])
                P.op("dve", lambda e, yv=yv, tau=tau, n=n: e.tensor_tensor(out=yv[:], in0=tau[:], in1=th[:, :, None].broadcast_to([128, 32, n]), op=MUL),
                     reads=[b_tau, b_al], writes=[b_yv])
                fl = lambda t: t[:].rearrange("p a b -> p (a b)")
                range_sin(k, l0, fl(sn), fl(yv), [128, 32 * n], "ssr1" + nm, [b_yv], [b_sn])
                P.op("dve", lambda e, yv=yv: e.tensor_scalar(out=yv[:], in0=yv[:], scalar1=0.25, scalar2=None, op0=ADD), reads=[b_yv], writes=[b_yv])
                range_sin(k, l0, fl(cs), fl(yv), [128, 32 * n], "ssr2" + nm, [b_yv], [b_cs])
                P.op("dve", lambda e, cs=cs, ex=ex: e.tensor_tensor(out=cs[:], in0=cs[:], in1=ex[:], op=MUL), reads=[b_cs, b_ex], writes=[b_cs])
                P.op("dve", lambda e, sn=sn, ex=ex: e.tensor_tensor(out=sn[:], in0=sn[:], in1=ex[:], op=MUL), reads=[b_sn, b_ex], writes=[b_sn])
                tabs[nm] = (cs, sn, b_cs, b_sn)
            ARA, AIA, b_ARA, b_AIA = tabs["A"]
            ARB, AIB, b_ARB, b_AIB = tabs["B"]
            a1 = k.sb("ss_a1", [128, 2, 32], F32, l0)
            b_a1 = Buf("a1")
            for d in range(2):
                i8 = 8 if d == 0 else 0
                i1 = 1 if d == 0 else 7
                dsl = slice(d * 16, (d + 1) * 16)
                for ri in range(2):
                    P.op("dve", lambda e, d=d, ri=ri, i8=i8, dsl=dsl: e.tensor_copy(out=A8c[:, d, :, ri], in_=ARA[:, dsl, i8]), reads=[b_ARA], writes=[b_a8])
                P.op("dve", lambda e, d=d, i8=i8, dsl=dsl: e.tensor_scalar(out=A8s[:, d, :, 0], in0=AIA[:, dsl, i8], scalar1=-1.0, scalar2=None, op0=MUL),
                     reads=[b_AIA], writes=[b_a8])
                P.op("dve", lambda e, d=d, i8=i8, dsl=dsl: e.tensor_copy(out=A8s[:, d, :, 1], in_=AIA[:, dsl, i8]), reads=[b_AIA], writes=[b_a8])
                P.op("dve", lambda e, d=d, i1=i1, dsl=dsl: e.tensor_copy(out=a1[:, 0, dsl], in_=ARA[:, dsl, i1]), reads=[b_ARA], writes=[b_a1])
                P.op("dve", lambda e, d=d, i1=i1, dsl=dsl: e.tensor_copy(out=a1[:, 1, dsl], in_=AIA[:, dsl, i1]), reads=[b_AIA], writes=[b_a1])
            fz = k.sb("ss_fz", [128, 6, 32], F32, l0)
            b_fz = Buf("fz")
            P.op("dve", lambda e: e.tensor_tensor(out=fz[:, 0, :], in0=lre[:], in1=lre[:], op=MUL), reads=[b_l], writes=[b_fz])
            P.op("dve", lambda e: e.tensor_tensor(out=fz[:, 1, :], in0=lim[:], in1=lim[:], op=MUL), reads=[b_l], writes=[b_fz])
            P.op("dve", lambda e: e.tensor_tensor(out=fz[:, 0, :], in0=fz[:, 0, :], in1=fz[:, 1, :], op=ADD), reads=[b_fz], writes=[b_fz])
            P.op("dve", lambda e: e.reciprocal(out=fz[:, 1, :], in_=fz[:, 0, :]), reads=[b_fz], writes=[b_fz])
            P.op("dve", lambda e: e.tensor_scalar(out=fz[:, 0, :], in0=a1[:, 0, :], scalar1=-1.0, scalar2=None, op0=ADD), reads=[b_a1], writes=[b_fz])
            P.op("dve", lambda e: e.tensor_tensor(out=fz[:, 2, :], in0=fz[:, 0, :], in1=lre[:], op=MUL), reads=[b_fz, b_l], writes=[b_fz])
            P.op("dve", lambda e: e.tensor_tensor(out=fz[:, 3, :], in0=a1[:, 1, :], in1=lim[:], op=MUL), reads=[b_a1, b_l], writes=[b_fz])
            P.op("dve", lambda e: e.tensor_tensor(out=fz[:, 2, :], in0=fz[:, 2, :], in1=fz[:, 3, :], op=ADD), reads=[b_fz], writes=[b_fz])
            P.op("dve", lambda e: e.tensor_tensor(out=fz[:, 2, :], in0=fz[:, 2, :], in1=fz[:, 1, :], op=MUL), reads=[b_fz], writes=[b_fz])
            P.op("dve", lambda e: e.tensor_tensor(out=fz[:, 4, :], in0=a1[:, 1, :], in1=lre[:], op=MUL), reads=[b_a1, b_l], writes=[b_fz])
            P.op("dve", lambda e: e.tensor_tensor(out=fz[:, 5, :], in0=fz[:, 0, :], in1=lim[:], op=MUL), reads=[b_fz, b_l], writes=[b_fz])
            P.op("dve", lambda e: e.tensor_tensor(out=fz[:, 4, :], in0=fz[:, 4, :], in1=fz[:, 5, :], op=SUB), reads=[b_fz], writes=[b_fz])
            P.op("dve", lambda e: e.tensor_tensor(out=fz[:, 4, :], in0=fz[:, 4, :], in1=fz[:, 1, :], op=MUL), reads=[b_fz], writes=[b_fz])
            BT = k.sb("ss_BT", [128, 2, 2, 16, 16], F32, l0)
            BB = k.sb("ss_BB", [128, 2, 2, 16, 16], F32, l0)
            CN = k.sb("ss_CN", [128, 2, 2, 2, 128], F32, l0)
            CT = k.sb("ss_CT", [128, 2, 2, 16, 16], F32, l0)
            tA = k.sb("ss_tA", [128, 16, 9, 16], F32, l0)
            tB = k.sb("ss_tB", [128, 16, 9, 16], F32, l0)
            b_BT, b_BB, b_CN, b_CT, b_tA, b_tB = Buf("BT"), Buf("BB"), Buf("CN"), Buf("CT"), Buf("tA"), Buf("tB")
            for d in range(2):
                for ri in range(2):
                    bsrc = I["ssm_bre"] if ri == 0 else I["ssm_bim"]
                    csrc = I["ssm_cre"] if ri == 0 else I["ssm_cim"]
                    for gp in range(2):
                        P.dma("sp", BT[gp * 64:(gp + 1) * 64, d, ri, :, :], bsrc[d, gp * 16:(gp + 1) * 16].rearrange("g p c -> p g c"),
                              writes=[b_BT], sem=csem)
                        for blk in range(2):
                            g0 = gp * 16 + blk * 8
                            P.dma("sp", CN[:, d, ri, blk, gp * 64:(gp + 1) * 64], csrc[d, g0:g0 + 8].rearrange("g c p -> (g c) p"),
                                  writes=[b_CN], sem=csem)
            for d in range(2):
                for ri in range(2):
                    for blk in range(2):
                        pt, bp = nb()
                        P.op("pe", lambda e, d=d, ri=ri, blk=blk, pt=pt: e.transpose(out=pt[:, 0:128], in_=CN[:, d, ri, blk, :], identity=k.ident[:]),
                             reads=[b_CN, k.b_ident], writes=[bp])
                        P.op("dve", lambda e, d=d, ri=ri, blk=blk, pt=pt: e.tensor_copy(
                            out=CT[:, d, ri, blk * 8:(blk + 1) * 8, :].rearrange("p a b -> p (a b)"), in_=pt[:, 0:128]), reads=[bp], writes=[b_CT])
            for d in range(2):
                dsl = slice(d * 16, (d + 1) * 16)
                frb = lambda d=d, dsl=dsl: fz[:, 2, dsl][:, :, None].broadcast_to([128, 16, 16])
                fib = lambda d=d, dsl=dsl: fz[:, 4, dsl][:, :, None].broadcast_to([128, 16, 16])
                t16a = tA[:, :, 0, :]
                t16b = tB[:, :, 0, :]
                P.op("dve", lambda e, d=d, frb=frb: e.tensor_tensor(out=t16a, in0=BT[:, d, 0], in1=frb(), op=MUL), reads=[b_BT, b_fz], writes=[b_tA])
                P.op("dve", lambda e, d=d, fib=fib: e.tensor_tensor(out=t16b, in0=BT[:, d, 1], in1=fib(), op=MUL), reads=[b_BT, b_fz], writes=[b_tB])
                P.op("dve", lambda e, d=d: e.tensor_tensor(out=BB[:, d, 0], in0=t16a, in1=t16b, op=SUB), reads=[b_tA, b_tB], writes=[b_BB])
                P.op("dve", lambda e, d=d, frb=frb: e.tensor_tensor(out=t16a, in0=BT[:, d, 1], in1=frb(), op=MUL), reads=[b_BT, b_fz], writes=[b_tA])
                P.op("dve", lambda e, d=d, fib=fib: e.tensor_tensor(out=t16b, in0=BT[:, d, 0], in1=fib(), op=MUL), reads=[b_BT, b_fz], writes=[b_tB])
                P.op("dve", lambda e, d=d: e.tensor_tensor(out=BB[:, d, 1], in0=t16a, in1=t16b, op=ADD), reads=[b_tA, b_tB], writes=[b_BB])
            P.op("pool", lambda e: e.memset(RCp[:].rearrange("p a b c d -> p (a b c d)"), 0.0), writes=[b_rcp])
            for d in range(2):
                dsl = slice(d * 16, (d + 1) * 16)
                off = 112 if d == 0 else 0
                bc_c = lambda ri, d=d: CT[:, d, ri][:, :, None, :].broadcast_to([128, 16, 9, 16])
                bc_ar = lambda dsl=dsl: ARA[:, dsl, :][:, :, :, None].broadcast_to([128, 16, 9, 16])
                bc_ai = lambda dsl=dsl: AIA[:, dsl, :][:, :, :, None].broadcast_to([128, 16, 9, 16])
                dst = lambda ri, d=d, off=off: RCp[:, d, ri, :, off:off + 144].rearrange("p g (t c) -> p g t c", c=16)
                P.op("dve", lambda e, bc_c=bc_c, bc_ar=bc_ar: e.tensor_tensor(out=tA[:], in0=bc_c(0), in1=bc_ar(), op=MUL), reads=[b_CT, b_ARA], writes=[b_tA])
                P.op("dve", lambda e, bc_c=bc_c, bc_ai=bc_ai: e.tensor_tensor(out=tB[:], in0=bc_c(1), in1=bc_ai(), op=MUL), reads=[b_CT, b_AIA], writes=[b_tB])
                P.op("dve", lambda e, dst=dst: e.tensor_tensor(out=dst(0), in0=tA[:], in1=tB[:], op=SUB), reads=[b_tA, b_tB], writes=[b_rcp])
                P.op("dve", lambda e, bc_c=bc_c, bc_ai=bc_ai: e.tensor_tensor(out=tA[:], in0=bc_c(0), in1=bc_ai(), op=MUL), reads=[b_CT, b_AIA], writes=[b_tA])
                P.op("dve", lambda e, bc_c=bc_c, bc_ar=bc_ar: e.tensor_tensor(out=tB[:], in0=bc_c(1), in1=bc_ar(), op=MUL), reads=[b_CT, b_ARA], writes=[b_tB])
                P.op("dve", lambda e: e.tensor_tensor(out=tA[:], in0=tA[:], in1=tB[:], op=ADD), reads=[b_tA, b_tB], writes=[b_tA])
                P.op("dve", lambda e, dst=dst: e.tensor_scalar(out=dst(1), in0=tA[:], scalar1=-1.0, scalar2=None, op0=MUL), reads=[b_tA], writes=[b_rcp])
            Lp = k.sb("ss_Lp", [128, 64, 240], BF16, l0)
            b_Lp = Buf("Lp")
            P.op("pool", lambda e: e.memset(Lp[:].rearrange("p a b -> p (a b)"), 0.0), writes=[b_Lp])
            P.op("pool", lambda e: e.tensor_copy(out=Lp[:, :, 112:128], in_=BB[:].rearrange("p a b c d -> p (a b c) d")), reads=[b_BB], writes=[b_Lp])
            BW = k.sb("ss_BW", [128, 2, 2, 16, 128], F32, l0)
            b_BW = Buf("BW")
            for d in range(2):
                dsl = slice(d * 16, (d + 1) * 16)
                bc_b = lambda ri, d=d: BB[:, d, ri][:, :, None, :].broadcast_to([128, 16, 8, 16])
                bc_ar = lambda dsl=dsl: ARB[:, dsl, :][:, :, :, None].broadcast_to([128, 16, 8, 16])
                bc_ai = lambda dsl=dsl: AIB[:, dsl, :][:, :, :, None].broadcast_to([128, 16, 8, 16])
                dst = lambda ri, d=d: BW[:, d, ri].rearrange("p g (t c) -> p g t c", c=16)
                ta8 = tA[:, :, 0:8, :]
                tb8 = tB[:, :, 0:8, :]
                P.op("dve", lambda e, bc_b=bc_b, bc_ar=bc_ar: e.tensor_tensor(out=ta8, in0=bc_b(0), in1=bc_ar(), op=MUL), reads=[b_BB, b_ARB], writes=[b_tA])
                P.op("dve", lambda e, bc_b=bc_b, bc_ai=bc_ai: e.tensor_tensor(out=tb8, in0=bc_b(1), in1=bc_ai(), op=MUL), reads=[b_BB, b_AIB], writes=[b_tB])
                P.op("dve", lambda e, dst=dst: e.tensor_tensor(out=dst(0), in0=ta8, in1=tb8, op=SUB), reads=[b_tA, b_tB], writes=[b_BW])
                P.op("dve", lambda e, bc_b=bc_b, bc_ai=bc_ai: e.tensor_tensor(out=ta8, in0=bc_b(0), in1=bc_ai(), op=MUL), reads=[b_BB, b_AIB], writes=[b_tA])
                P.op("dve", lambda e, bc_b=bc_b, bc_ar=bc_ar: e.tensor_tensor(out=tb8, in0=bc_b(1), in1=bc_ar(), op=MUL), reads=[b_BB, b_ARB], writes=[b_tB])
                P.op("dve", lambda e, dst=dst: e.tensor_tensor(out=dst(1), in0=ta8, in1=tb8, op=ADD), reads=[b_tA, b_tB], writes=[b_BW])
            for d in range(2):
                for g2 in range(16):
                    for ri in range(2):
                        pt, bp = nb()
                        P.op("pe", lambda e, d=d, g2=g2, ri=ri, pt=pt: e.transpose(out=pt[:, 0:128], in_=BW[:, d, ri, g2, :], identity=k.ident[:]),
                             reads=[b_BW, k.b_ident], writes=[bp])
                        eng = "act" if (g2 + ri) % 2 else "dve"
                        if eng == "act":
                            P.op("act", lambda e, d=d, g2=g2, ri=ri, pt=pt: e.activation(out=WT[:, d, g2, ri, :], in_=pt[:, 0:128], func=AF.Copy), reads=[bp], writes=[b_wt])
                        else:
                            P.op("dve", lambda e, d=d, g2=g2, ri=ri, pt=pt: e.tensor_copy(out=WT[:, d, g2, ri, :], in_=pt[:, 0:128]), reads=[bp], writes=[b_wt])
            for g2 in range(16):
                for gp in range(2):
                    g = gp * 16 + g2
                    pt, bp = nb()
                    psl = slice(gp * 64, (gp + 1) * 64)
                    n = 0
                    for d in range(2):
                        for ri in range(2):
                            for s_ in range(8):
                                w0 = (7 - s_) * 16 if d == 0 else (8 - s_) * 16
                                l0_ = (7 - s_) * 16
                                P.op("pe", lambda e, d=d, ri=ri, g2=g2, w0=w0, l0_=l0_, psl=psl, pt=pt, n=n: e.matmul(
                                    pt[:, 0:128], lhsT=Lp[psl, (d * 2 + ri) * 16 + g2, l0_:l0_ + 128], rhs=RCp[psl, d, ri, g2, w0:w0 + 128],
                                    start=(n == 0), stop=(n == 31)), reads=[b_Lp, b_rcp], writes=[bp])
                                n += 1
                    P.op("dve" if g % 2 else "act",
                         (lambda e, g=g, pt=pt: e.tensor_copy(out=ToepT[:, g, :], in_=pt[:, 0:128])) if g % 2 else
                         (lambda e, g=g, pt=pt: e.activation(out=ToepT[:, g, :], in_=pt[:, 0:128], func=AF.Copy)),
                         reads=[bp], writes=[b_toep[g]])
            P.barrier()
        Ubuf = k.sb("ss_ubuf", [128, 32, 320], BF16, ls)
        Zbf = k.sb("ss_zbf", [128, 2, 16, 2, 288], BF16, ls)
        ygT = k.sb("ss_ygT", [128, 4, L], BF16, ls)
        with ExitStack() as l1:
            ucm = [k.sb(f"ss_ucm{i}", [128, 8, 512], F32, l1) for i in range(2)]
            b_ucm = [Buf(f"ucm{i}") for i in range(2)]
            usem = [P.new_dsem(f"ss_us{i}") for i in range(2)]
            ucg = k.sb("ss_ucg", [128, 32, 128], F32, l1)
            b_ucg = Buf("ucg")
            for jt in range(3):
                si = jt % 2
                nj = 128 if jt < 2 else 32
                r0 = jt * 1024
                P.dma("sp", ucm[si][0:nj], S["u"][r0:r0 + nj * 8, :].rearrange("(j s) c -> j s c", s=8), writes=[b_ucm[si]], sem=usem[si])
                P.op("dve", lambda e, si=si, nj=nj: e.tensor_copy(out=ucg[0:nj].rearrange("p g (s c) -> p g s c", c=16),
                                                                 in_=ucm[si][0:nj].rearrange("p s (g c) -> p g s c", c=16)),
                     reads=[b_ucm[si]], writes=[b_ucg])
                for g0 in range(0, 32, 4):
                    pt, bp = nb()
                    for gg in range(4):
                        g = g0 + gg
                        P.op("pe", lambda e, si=si, nj=nj, g=g, gg=gg, pt=pt: e.transpose(
                            out=pt[:, gg * 128:gg * 128 + nj], in_=ucg[0:nj, g, :], identity=k.ident[0:nj, 0:nj]),
                            reads=[b_ucg, k.b_ident], writes=[bp])
                    src = lambda pt=pt, nj=nj: pt[:].rearrange("p (a b) -> p a b", b=128)[:, :, 0:nj]
                    cols = [32 + jt * 128] if jt < 2 else [0, 288]
                    for ci, c0 in enumerate(cols):
                        eng = "act" if (g0 // 4 + ci) % 2 else "dve"
                        if eng == "act":
                            P.op("act", lambda e, g0=g0, c0=c0, nj=nj, src=src: e.activation(out=Ubuf[:, g0:g0 + 4, c0:c0 + nj], in_=src(), func=AF.Copy),
                                 reads=[bp], writes=[b_U[g0 + i] for i in range(4)])
                        else:
                            P.op("dve", lambda e, g0=g0, c0=c0, nj=nj, src=src: e.tensor_copy(out=Ubuf[:, g0:g0 + 4, c0:c0 + nj], in_=src()),
                                 reads=[bp], writes=[b_U[g0 + i] for i in range(4)])
            P.barrier()
        with ExitStack() as l2:
            Z = [k.sb(f"ss_Z{d}", [128, 16, 2, 288], F32, l2) for d in range(2)]
            b_Z = [Buf("Z0"), Buf("Z1")]
            for d in range(2):
                j0 = 0 if d == 0 else 32
                for g2 in range(16):
                    for ri in range(2):
                        pt, bp = nb()
                        for gp in range(2):
                            P.op("pe", lambda e, d=d, g2=g2, ri=ri, gp=gp, pt=pt, j0=j0: e.matmul(
                                pt[gp * 64:(gp + 1) * 64, 0:288], lhsT=WT[:, d, g2, ri, gp * 64:(gp + 1) * 64], rhs=Ubuf[:, gp * 16 + g2, j0:j0 + 288],
                                start=True, stop=True), reads=[b_wt, b_U[gp * 16 + g2]], writes=[bp])
                        if (g2 + ri) % 2:
                            P.op("act", lambda e, d=d, g2=g2, ri=ri, pt=pt: e.activation(out=Z[d][:, g2, ri, :], in_=pt[:, 0:288], func=AF.Copy), reads=[bp], writes=[b_Z[d]])
                        else:
                            P.op("dve", lambda e, d=d, g2=g2, ri=ri, pt=pt: e.tensor_copy(out=Z[d][:, g2, ri, :], in_=pt[:, 0:288]), reads=[bp], writes=[b_Z[d]])
            k.dbg("V_dbg", [2, 128, 16 * 2 * 288], F32, lambda dd: (dd[0], Z[0][:].rearrange("p a b c -> p (a b c)")), [b_Z[0]])
            k.dbg("V_dbg", [2, 128, 16 * 2 * 288], F32, lambda dd: (dd[1], Z[1][:].rearrange("p a b c -> p (a b c)")), [b_Z[1]])
            m1 = [k.sb(f"ss_m1{d}", [128, 16, 2], F32, l2) for d in range(2)]
            m2 = [k.sb(f"ss_m2{d}", [128, 16, 2], F32, l2) for d in range(2)]
            b_m1 = [Buf("m10"), Buf("m11")]
            b_m2 = [Buf("m20"), Buf("m21")]

            def scan_step(d, J, Jp):
                eng = "dve" if d == 0 else "pool"
                P.op(eng, lambda e: e.tensor_tensor(out=m1[d][:], in0=Z[d][:, :, :, Jp], in1=A8c[:, d], op=MUL), reads=[b_Z[d], b_a8], writes=[b_m1[d]])
                P.op(eng, lambda e: e.tensor_tensor(out=m2[d][:], in0=Z[d][:, :, ::-1, Jp], in1=A8s[:, d], op=MUL), reads=[b_Z[d], b_a8], writes=[b_m2[d]])
                P.op(eng, lambda e: e.tensor_tensor(out=m1[d][:], in0=m1[d][:], in1=m2[d][:], op=ADD), reads=[b_m1[d], b_m2[d]], writes=[b_m1[d]])
                P.op(eng, lambda e: e.tensor_tensor(out=Z[d][:, :, :, J], in0=Z[d][:, :, :, J], in1=m1[d][:], op=ADD), reads=[b_Z[d], b_m1[d]], writes=[b_Z[d]])

            for st_ in range(1, 288):
                scan_step(0, st_, st_ - 1)
                scan_step(1, 287 - st_, 288 - st_)
            for d in range(2):
                eng = "dve" if d == 0 else "pool"
                P.op(eng, lambda e, d=d: e.tensor_copy(out=Zbf[:, d].rearrange("p a b c -> p (a b c)"), in_=Z[d][:].rearrange("p a b c -> p (a b c)")),
                     reads=[b_Z[d]], writes=[b_zbf[d]])
            k.dbg("Z_dbg", [2, 128, 16 * 2 * 288], F32, lambda dd: (dd[0], Z[0][:].rearrange("p a b c -> p (a b c)")), [b_Z[0]])
            k.dbg("Z_dbg", [2, 128, 16 * 2 * 288], F32, lambda dd: (dd[1], Z[1][:].rearrange("p a b c -> p (a b c)")), [b_Z[1]])
            P.barrier()
        with ExitStack() as l3:
            ycm = k.sb("ss_ycm", [128, 8, 512], F32, l3)
            b_ycm = [Buf(f"ycm{g}") for g in range(32)]
            ut = k.sb("ss_ut", [128, 8, 512], F32, l3)
            b_ut = Buf("ut")
            utsem = P.new_dsem("ss_uts")
            Dfull = k.sb("ss_D", [128, 512], F32, l3)
            b_D = Buf("Dfull")
            P.dma("sp", Dfull[:], I["ssm_d"][0:1, :].broadcast_to([128, 512]), writes=[b_D], sem=csem)
            sq = [k.sb(f"ss_sq{i}", [128, 512], F32, l3) for i in range(2)]
            b_sq = [Buf("sq0"), Buf("sq1")]
            GC = math.sqrt(2.0 / math.pi)
            for jt in range(2):
                P.dma("sp", ut[:], S["u"][jt * 1024:(jt + 1) * 1024, :].rearrange("(j s) c -> j s c", s=8), writes=[b_ut], sem=utsem)
                for g in range(32):
                    gp, g2 = g // 16, g % 16
                    psl = slice(gp * 64, (gp + 1) * 64)
                    pt, bp = nb()
                    c0 = 32 + jt * 128
                    P.op("pe", lambda e, g=g, c0=c0, pt=pt: e.matmul(pt[:, 0:128], lhsT=Ubuf[:, g, c0:c0 + 128], rhs=ToepT[:, g, :], start=True, stop=False),
                         reads=[b_U[g], b_toep[g]], writes=[bp])
                    for d in range(2):
                        jz = (31 + jt * 128) if d == 0 else (1 + jt * 128)
                        w0 = 128 if d == 0 else 0
                        for ri in range(2):
                            last = (d == 1 and ri == 1)
                            P.op("pe", lambda e, d=d, ri=ri, g2=g2, psl=psl, jz=jz, w0=w0, pt=pt, last=last: e.matmul(
                                pt[:, 0:128], lhsT=Zbf[psl, d, g2, ri, jz:jz + 128], rhs=RCp[psl, d, ri, g2, w0:w0 + 128], start=False, stop=last),
                                reads=[b_zbf[d], b_rcp], writes=[bp])
                    src = lambda pt=pt: pt[:, 0:128].rearrange("p (t c) -> p t c", c=16)
                    P.op("dve", lambda e, g=g, src=src: e.tensor_tensor(out=ycm[:, :, g * 16:(g + 1) * 16], in0=ut[:, :, g * 16:(g + 1) * 16],
                                                                       in1=Dfull[:, g * 16:(g + 1) * 16][:, None, :].broadcast_to([128, 8, 16]), op=MUL),
                         reads=[b_ut, b_D], writes=[b_ycm[g]])
                    P.op("dve", lambda e, g=g, src=src: e.tensor_tensor(out=ycm[:, :, g * 16:(g + 1) * 16], in0=ycm[:, :, g * 16:(g + 1) * 16], in1=src(), op=ADD),
                         reads=[bp, b_ycm[g]], writes=[b_ycm[g]])
                k.dbg("y_dbg", [L, 512], F32, lambda dd, jt=jt: (dd[jt * 1024:(jt + 1) * 1024, :].rearrange("(j s) c -> j s c", s=8), ycm[:]), b_ycm)
                for t in range(8):
                    i = t % 2
                    P.op("dve", lambda e, t=t, i=i: e.tensor_tensor(out=sq[i][:], in0=ycm[:, t, :], in1=ycm[:, t, :], op=MUL), reads=b_ycm, writes=[b_sq[i]])
                    P.op("dve", lambda e, t=t, i=i: e.tensor_scalar(out=sq[i][:], in0=sq[i][:], scalar1=0.044715, scalar2=1.0, op0=MUL, op1=ADD), reads=[b_sq[i]], writes=[b_sq[i]])
                    P.op("dve", lambda e, t=t, i=i: e.tensor_tensor(out=sq[i][:], in0=sq[i][:], in1=ycm[:, t, :], op=MUL), reads=[b_sq[i]] + b_ycm, writes=[b_sq[i]])
                    P.op("act", lambda e, t=t, i=i: e.activation(out=sq[i][:], in_=sq[i][:], func=AF.Sigmoid, scale=2.0 * GC), reads=[b_sq[i]], writes=[b_sq[i]])
                    P.op("dve", lambda e, t=t, i=i: e.tensor_tensor(out=sq[i][:], in0=sq[i][:], in1=ycm[:, t, :], op=MUL), reads=[b_sq[i]] + b_ycm, writes=[b_sq[i]])
                    pt, bp = nb()
                    for chb in range(4):
                        P.op("pe", lambda e, i=i, chb=chb, pt=pt: e.transpose(out=pt[:, chb * 128:(chb + 1) * 128], in_=sq[i][:, chb * 128:(chb + 1) * 128], identity=k.ident[:]),
                             reads=[b_sq[i], k.b_ident], writes=[bp])
                    tsl = slice(jt * 1024 + t, (jt + 1) * 1024, 8)
                    P.op("act", lambda e, pt=pt, tsl=tsl: e.activation(out=ygT[:, :, tsl], in_=pt[:].rearrange("p (a b) -> p a b", b=128), func=AF.Copy),
                         reads=[bp], writes=[b_ygT])
            P.barrier()
        with ExitStack() as l4:
            wg32 = k.sb("ss_wg32", [128, 4, 512], F32, l4)
            wg = k.sb("ss_wg", [128, 4, 512], BF16, l4)
            bg = k.sb("ss_bg", [128, 4], F32, l4)
            b_wg32, b_wg, b_bg = Buf("wg32"), Buf("wg"), Buf("bg")
            P.dma("sp", wg32[:], I["w_glu"].rearrange("(fc p) c -> p fc c", p=128), writes=[b_wg32], sem=csem)
            P.dma("sp", bg[:], I["b_glu"].rearrange("(fc p) -> p fc", p=128), writes=[b_bg], sem=csem, allow_slow_non_contiguous=True)
            P.op("dve", lambda e: e.tensor_copy(out=wg[:], in_=wg32[:]), reads=[b_wg32], writes=[b_wg])
            gst = [k.sb(f"ss_gst{i}", [128, 512], BF16, l4) for i in range(2)]
            b_gst = [Buf("gst0"), Buf("gst1")]
            gsem = [P.new_dsem(f"ss_gs{i}") for i in range(2)]
            sg = [k.sb(f"ss_sg{i}", [128, 512], F32, l4) for i in range(2)]
            b_sg = [Buf("sg0"), Buf("sg1")]
            so = [k.sb(f"ss_so{i}", [128, 512], BF16, l4) for i in range(2)]
            b_so = [Buf("so0"), Buf("so1")]
            sosem = [P.new_dsem(f"ss_sos{i}") for i in range(2)]
            ui = 0
            for fo in range(4):
                for tb in range(4):
                    i = ui % 2
                    ui += 1
                    tsl = slice(tb * 512, (tb + 1) * 512)
                    P.dma("sp", gst[i][:], S["sgsT"][fo * 128:(fo + 1) * 128, tsl], writes=[b_gst[i]], sem=gsem[i])
                    pt, bp = nb()
                    for fc in range(4):
                        P.op("pe", lambda e, fc=fc, fo=fo, tsl=tsl, pt=pt: e.matmul(pt[:], lhsT=wg[:, fc, fo * 128:(fo + 1) * 128], rhs=ygT[:, fc, tsl],
                                                                               start=(fc == 0), stop=(fc == 3)), reads=[b_wg, b_ygT], writes=[bp])
                    P.op("act", lambda e, i=i, fo=fo, pt=pt: e.activation(out=sg[i][:], in_=pt[:], func=AF.Sigmoid, bias=bg[:, fo:fo + 1]),
                         reads=[bp, b_bg], writes=[b_sg[i]])
                    P.op("dve", lambda e, i=i, fo=fo, tsl=tsl: e.tensor_tensor(out=sg[i][:], in0=sg[i][:], in1=ygT[:, fo, tsl], op=MUL),
                         reads=[b_sg[i], b_ygT], writes=[b_sg[i]])
                    P.op("dve", lambda e, i=i: e.tensor_tensor(out=so[i][:], in0=sg[i][:], in1=gst[i][:], op=MUL),
                         reads=[b_sg[i], b_gst[i]], writes=[b_so[i]])
                    P.dma("sp", S["sbrT"][fo * 128:(fo + 1) * 128, tsl], so[i][:], reads=[b_so[i]], sem=sosem[i])


def phase_attn(k):
    nc, P, I, S = k.nc, k.P, k.I, k.S
    with ExitStack() as ls:
        lamv = k.sb("at_lamv", [128, 4, 64], F32, ls)
        lw = k.sb("at_lw", [128, 8], F32, ls)
        G = k.sb("at_G", [128, 128], F32, ls)
        b_lamv, b_lw, b_G = Buf("lamv"), Buf("lw"), Buf("G")
        csem = P.new_dsem("at_c")
        P.dma("sp", lamv[:].rearrange("p a b -> p (a b)"), I["lam"].rearrange("a b -> (a b)").partition_broadcast(128),
              writes=[b_lamv], sem=csem)
        P.dma("sp", G[:], I["subln_g"][0:1, :].broadcast_to([128, 128]), writes=[b_G], sem=csem)
        P.op("dve", lambda e: e.tensor_scalar(out=G[:], in0=G[:], scalar1=(1.0 - LAM_INIT), scalar2=None, op0=ALU.mult),
             reads=[b_G], writes=[b_G])
        for i in range(2):
            P.op("dve", lambda e, i=i: e.tensor_tensor(out=lamv[:, 2 * i, :], in0=lamv[:, 2 * i, :], in1=lamv[:, 2 * i + 1, :], op=ALU.mult),
                 reads=[b_lamv], writes=[b_lamv])
            P.op("dve", lambda e, i=i: e.tensor_reduce(out=lw[:, i:i + 1], in_=lamv[:, 2 * i, :], axis=mybir.AxisListType.X, op=ALU.add),
                 reads=[b_lamv], writes=[b_lw])
        P.op("act", lambda e: e.activation(out=lw[:, 2:4], in_=lw[:, 0:2], func=AF.Exp), reads=[b_lw], writes=[b_lw])
        P.op("dve", lambda e: e.tensor_tensor(out=lw[:, 4:5], in0=lw[:, 3:4], in1=lw[:, 2:3], op=ALU.subtract), reads=[b_lw], writes=[b_lw])
        P.op("dve", lambda e: e.tensor_scalar(out=lw[:, 5:6], in0=lw[:, 4:5], scalar1=-LAM_INIT, scalar2=None, op0=ALU.add),
             reads=[b_lw], writes=[b_lw])
        neglam = lw[:, 5:6]
        qTs = [k.sb(f"at_q{i}", [128, L], BF16, ls) for i in range(2)]
        kTs = [k.sb(f"at_k{i}", [128, LT], BF16, ls) for i in range(2)]
        Vs = [k.sb(f"at_v{i}", [128, 18, 130], BF16, ls) for i in range(2)]
        gas = [k.sb(f"at_ga{i}", [128, 16, 128], BF16, ls) for i in range(2)]
        aTs = [k.sb(f"at_aT{i}", [128, L], BF16, ls) for i in range(2)]
        b_q = [Buf(f"atq{i}") for i in range(2)]
        b_k = [Buf(f"atk{i}") for i in range(2)]
        b_v = [Buf(f"atv{i}") for i in range(2)]
        b_ga = [Buf(f"atga{i}") for i in range(2)]
        b_aT = [Buf(f"ataT{i}") for i in range(2)]
        hsem = [P.new_dsem(f"at_h{i}") for i in range(2)]
        asem = [P.new_dsem(f"at_a{i}") for i in range(2)]
        for i in range(2):
            P.op("pool", lambda e, i=i: e.memset(Vs[i][:, :, 128:130], 1.0), writes=[b_v[i]])
        PT = [k.sb(f"at_pt{i}", [128, 2, 18, 256], BF16, ls) for i in range(2)]
        b_PT = [[[Buf(f"pt{i}_{c}_{kp}") for kp in range(9)] for c in range(2)] for i in range(2)]
        sbk = [k.ps(f"at_s{i}", [128, 512], F32, ls) for i in range(3)]
        b_sbk = [Buf(f"ats{i}") for i in range(3)]
        obk = [k.ps(f"at_o{i}", [128, 512], F32, ls) for i in range(4)]
        b_obk = [Buf(f"ato{i}") for i in range(4)]
        tbk = k.ps("at_t", [128, 512], F32, ls)
        b_tbk = Buf("att")
        sm = [k.sb(f"at_sm{i}", [128, 8], F32, ls) for i in range(2)]
        b_sm = [Buf(f"atsm{i}") for i in range(2)]
        tmp = [k.sb(f"at_tmp{i}", [128, 128], F32, ls) for i in range(2)]
        b_tmp = [Buf(f"attmp{i}") for i in range(2)]
        ov = [k.sb(f"at_ov{i}", [128, 128], F32, ls) for i in range(2)]
        b_ov = [Buf(f"atov{i}") for i in range(2)]
        junk = k.sb("at_junk", [128, 128], F32, ls)
        b_junk = Buf("atjunk")
        cnt = {"s": 0, "u": 0}

        def load_head(h):
            s = h % 2
            P.dma("sp", qTs[s][:], S["qT"][h], writes=[b_q[s]], sem=hsem[s])
            P.dma("sp", kTs[s][:], S["kT"][h], writes=[b_k[s]], sem=hsem[s])
            P.dma("sp", Vs[s][:, :, 0:128], S["v"][:, h * 128:(h + 1) * 128].rearrange("(t p) e -> p t e", p=128),
                  writes=[b_v[s]], sem=hsem[s])
            P.dma("sp", gas[s][:], S["sga"][:, h * 128:(h + 1) * 128].rearrange("(t p) e -> p t e", p=128),
                  writes=[b_ga[s]], sem=hsem[s])

        def phaseA(h, qb):
            s = h % 2
            ps_ = qb % 2
            for kp in range(9):
                for c in range(2):
                    si = cnt["s"] % 3
                    cnt["s"] += 1
                    for j in range(2):
                        kt = 2 * kp + j
                        P.op("pe", lambda e, kt=kt, j=j, c=c, si=si: e.matmul(
                            sbk[si][:, j * 256:(j + 1) * 256], lhsT=kTs[s][c * 64:(c + 1) * 64, kt * 128:(kt + 1) * 128],
                            rhs=qTs[s][c * 64:(c + 1) * 64, qb * 256:(qb + 1) * 256], start=True, stop=True),
                            reads=[b_k[s], b_q[s]], writes=[b_sbk[si]])
                    P.op("act", lambda e, c=c, kp=kp, si=si: e.activation(
                        out=PT[ps_][:, c, 2 * kp:2 * kp + 2, :].rearrange("p a b -> p (a b)"), in_=sbk[si][:], func=AF.Exp, scale=0.125),
                        reads=[b_sbk[si]], writes=[b_PT[ps_][c][kp]])

        def phaseB(h, qb):
            s = h % 2
            ps_ = qb % 2
            for qi_ in range(2):
                unitB(h, qb, qi_, s, ps_)

        def unitB(h, qb, qi, s, ps_):
            if True:
                qt = qb * 2 + qi
                u = cnt["u"] % 2
                cnt["u"] += 1
                banks = [obk[u * 2], obk[u * 2 + 1]]
                bb = [b_obk[u * 2], b_obk[u * 2 + 1]]
                for c in range(2):
                    for kt in range(18):
                        P.op("pe", lambda e, c=c, kt=kt: e.matmul(
                            banks[c][:, 0:129], lhsT=PT[ps_][:, c, kt, qi * 128:(qi + 1) * 128], rhs=Vs[s][:, kt, 0:129],
                            start=(kt == 0), stop=(kt == 17)),
                            reads=[b_PT[ps_][c][kt // 2], b_v[s]], writes=[bb[c]])
                smt, bsm = sm[u], b_sm[u]
                for c in range(2):
                    P.op("dve", lambda e, c=c: e.reciprocal(out=smt[:, c:c + 1], in_=banks[c][:, 128:129]), reads=[bb[c]], writes=[bsm])
                P.op("dve", lambda e: e.tensor_tensor(out=smt[:, 2:3], in0=smt[:, 1:2], in1=neglam, op=ALU.mult), reads=[bsm, b_lw], writes=[bsm])
                P.op("dve", lambda e: e.tensor_scalar(out=tmp[u][:], in0=banks[1][:, 0:128], scalar1=smt[:, 2:3], scalar2=None, op0=ALU.mult),
                     reads=[bb[1], bsm], writes=[b_tmp[u]])
                P.op("dve", lambda e: e.scalar_tensor_tensor(out=ov[u][:], in0=banks[0][:, 0:128], scalar=smt[:, 0:1], in1=tmp[u][:],
                                                            op0=ALU.mult, op1=ALU.add),
                     reads=[bb[0], bsm, b_tmp[u]], writes=[b_ov[u]])
                if h == 0:
                    k.dbg("o_dbg", [L, 128], F32, lambda d, qt=qt, u=u: (d[qt * 128:(qt + 1) * 128, :], ov[u][:]), [b_ov[u]])
                    k.dbg("sm_dbg", [L, 8], F32, lambda d, qt=qt, u=u: (d[qt * 128:(qt + 1) * 128, :], sm[u][:]), [b_sm[u]])
                    if qt == 0:
                        k.dbg("pt_dbg", [128, 2 * 18 * 256], BF16, lambda d: (d, PT[ps_][:].rearrange("p a b c -> p (a b c)")),
                              [b for c_ in range(2) for b in b_PT[ps_][c_]])
                        k.dbg("lw_dbg", [128, 8], F32, lambda d: (d, lw[:]), [b_lw])
                P.op("act", lambda e: e.activation(out=junk[:], in_=ov[u][:], func=AF.Square, accum_out=smt[:, 3:4]),
                     reads=[b_ov[u]], writes=[b_junk, bsm])
                P.op("dve", lambda e: e.tensor_scalar(out=smt[:, 4:5], in0=smt[:, 3:4], scalar1=1.0 / 128, scalar2=EPS, op0=ALU.mult, op1=ALU.add),
                     reads=[bsm], writes=[bsm])
                P.op("act", lambda e: e.activation(out=smt[:, 5:6], in_=smt[:, 4:5], func=AF.Sqrt), reads=[bsm], writes=[bsm])
                P.op("dve", lambda e: e.reciprocal(out=smt[:, 6:7], in_=smt[:, 5:6]), reads=[bsm], writes=[bsm])
                P.op("dve", lambda e: e.scalar_tensor_tensor(out=ov[u][:], in0=ov[u][:], scalar=smt[:, 6:7], in1=G[:], op0=ALU.mult, op1=ALU.mult),
                     reads=[b_ov[u], bsm, b_G], writes=[b_ov[u]])
                P.op("pool", lambda e: e.tensor_tensor(out=ov[u][:], in0=ov[u][:], in1=gas[s][:, qt, :], op=ALU.mult),
                     reads=[b_ov[u], b_ga[s]], writes=[b_ov[u]])
                P.op("pe", lambda e: e.transpose(out=tbk[:, 0:128], in_=ov[u][:], identity=k.ident[:]), reads=[b_ov[u], k.b_ident], writes=[b_tbk])
                P.op("act", lambda e: e.activation(out=aTs[s][:, qt * 128:(qt + 1) * 128], in_=tbk[:, 0:128], func=AF.Copy),
                     reads=[b_tbk], writes=[b_aT[s]])

        load_head(0)
        for h in range(HEADS):
            if h + 1 < HEADS:
                load_head(h + 1)
            phaseA(h, 0)
            for qb in range(8):
                if qb + 1 < 8:
                    phaseA(h, qb + 1)
                phaseB(h, qb)
            P.dma("pool", S["abrT"][h * 128:(h + 1) * 128, :], aTs[h % 2][:], reads=[b_aT[h % 2]], sem=asem[h % 2])


def phase_merge(k):
    nc, P, I, S = k.nc, k.P, k.I, k.S
    with ExitStack() as ls:
        mT = k.sb("mg_mT", [128, NKC, L], BF16, ls)
        b_mT = [Buf(f"mT{tb}") for tb in range(4)]
        with ExitStack() as l1:
            abrT = k.sb("mg_abrT", [128, 8, L], BF16, l1)
            sbrT = k.sb("mg_sbrT", [128, 4, L], BF16, l1)
            b_abrT, b_sbrT = Buf("abrT"), Buf("sbrT")
            lsem = P.new_dsem("mg_l")
            P.dma("sp", abrT[:], S["abrT"].rearrange("(fc p) t -> p fc t", p=128), writes=[b_abrT], sem=lsem)
            P.dma("sp", sbrT[:], S["sbrT"].rearrange("(fc p) t -> p fc t", p=128), writes=[b_sbrT], sem=lsem)
            NWS = 2
            wstg = [k.sb(f"mg_wstg{i}", [128, 12, 128], F32, l1) for i in range(NWS)]
            b_wstg = [Buf(f"mgwstg{i}") for i in range(NWS)]
            wsem = [P.new_dsem(f"mg_ws{i}") for i in range(NWS)]
            wbf = [k.sb(f"mg_wbf{i}", [128, 12, 128], BF16, l1) for i in range(NWS)]
            b_wbf = [Buf(f"mgwbf{i}") for i in range(NWS)]
            NG = 4
            gt = [k.sb(f"mg_gt{i}", [128, 2, 512], BF16, l1) for i in range(NG)]
            b_gt = [Buf(f"mggt{i}") for i in range(NG)]
            gsem = [P.new_dsem(f"mg_gs{i}") for i in range(NG)]
            t1 = [k.sb(f"mg_t1{i}", [128, 512], F32, l1) for i in range(2)]
            t2 = [k.sb(f"mg_t2{i}", [128, 512], F32, l1) for i in range(2)]
            b_t1 = [Buf(f"mgt1{i}") for i in range(2)]
            b_t2 = [Buf(f"mgt2{i}") for i in range(2)]
            pa = [k.ps(f"mg_pa{i}", [128, 512], F32, l1) for i in range(2)]
            pp = [k.ps(f"mg_pp{i}", [128, 512], F32, l1) for i in range(2)]
            b_pa = [Buf(f"mgpa{i}") for i in range(2)]
            b_pp = [Buf(f"mgpp{i}") for i in range(2)]
            wpa_v = I["w_pa"].rearrange("(fc p) c -> p fc c", p=128)
            wps_v = I["w_ps"].rearrange("(fc p) c -> p fc c", p=128)
            ui = 0

            def load_w(fo):
                s = fo % NWS
                P.dma("sp", wstg[s][:, 0:8, :], wpa_v[:, :, fo * 128:(fo + 1) * 128], writes=[b_wstg[s]], sem=wsem[s])
                P.dma("sp", wstg[s][:, 8:12, :], wps_v[:, :, fo * 128:(fo + 1) * 128], writes=[b_wstg[s]], sem=wsem[s])
                P.op("pool", lambda e, s=s: e.tensor_copy(out=wbf[s][:], in_=wstg[s][:]), reads=[b_wstg[s]], writes=[b_wbf[s]])

            load_w(0)
            for fo in range(NKC):
                if fo + 1 < NKC:
                    load_w(fo + 1)
                s = fo % NWS
                for tb in range(4):
                    gi = ui % NG
                    u2 = ui % 2
                    ui += 1
                    P.dma("sp", gt[gi][:, 0, :], S["sgmT"][fo * 128:(fo + 1) * 128, tb * 512:(tb + 1) * 512], writes=[b_gt[gi]], sem=gsem[gi])
                    P.dma("sp", gt[gi][:, 1, :], S["sgmT"][D + fo * 128:D + (fo + 1) * 128, tb * 512:(tb + 1) * 512], writes=[b_gt[gi]], sem=gsem[gi])
                    for fc in range(8):
                        P.op("pe", lambda e, fc=fc, s=s, tb=tb, u2=u2: e.matmul(pa[u2][:], lhsT=wbf[s][:, fc, :], rhs=abrT[:, fc, tb * 512:(tb + 1) * 512],
                                                                       start=(fc == 0), stop=(fc == 7)),
                             reads=[b_wbf[s], b_abrT], writes=[b_pa[u2]])
                    for fc in range(4):
                        P.op("pe", lambda e, fc=fc, s=s, tb=tb, u2=u2: e.matmul(pp[u2][:], lhsT=wbf[s][:, 8 + fc, :], rhs=sbrT[:, fc, tb * 512:(tb + 1) * 512],
                                                                       start=(fc == 0), stop=(fc == 3)),
                             reads=[b_wbf[s], b_sbrT], writes=[b_pp[u2]])
                    P.op("dve", lambda e, gi=gi, u2=u2: e.tensor_tensor(out=t1[u2][:], in0=pa[u2][:], in1=gt[gi][:, 0, :], op=ALU.mult),
                         reads=[b_pa[u2], b_gt[gi]], writes=[b_t1[u2]])
                    P.op("dve", lambda e, gi=gi, u2=u2: e.tensor_tensor(out=t2[u2][:], in0=pp[u2][:], in1=gt[gi][:, 1, :], op=ALU.mult),
                         reads=[b_pp[u2], b_gt[gi]], writes=[b_t2[u2]])
                    P.op("pool", lambda e, fo=fo, tb=tb, u2=u2: e.tensor_tensor(out=mT[:, fo, tb * 512:(tb + 1) * 512], in0=t1[u2][:], in1=t2[u2][:], op=ALU.add),
                         reads=[b_t1[u2], b_t2[u2]], writes=[b_mT[tb]])
            P.barrier()
        wout = k.sb("mg_wout", [128, NKC, D], BF16, ls)
        b_wout = Buf("wout")
        gateB = k.sb("mg_gateB", [128, D], F32, ls)
        fgB = k.sb("mg_fgB", [128, D], F32, ls)
        b_gateB, b_fgB = Buf("gateB"), Buf("fgB")
        c2 = P.new_dsem("mg_c2")
        P.dma("sp", gateB[:], S["modrow"][0:1, 2 * D:3 * D].broadcast_to([128, D]), writes=[b_gateB], sem=c2)
        P.dma("sp", fgB[:], I["final_g"][0:1, :].broadcast_to([128, D]), writes=[b_fgB], sem=c2)
        NXB = 2
        xb = [k.sb(f"mg_x{i}", [128, D], F32, ls) for i in range(NXB)]
        xn = [k.sb(f"mg_xn{i}", [128, D], F32, ls) for i in range(NXB)]
        b_xb = [Buf(f"mgx{i}") for i in range(NXB)]
        b_xn = [Buf(f"mgxn{i}") for i in range(NXB)]
        xsem = [P.new_dsem(f"mg_xs{i}") for i in range(NXB)]
        osem = [P.new_dsem(f"mg_os{i}") for i in range(NXB)]
        st2 = [k.sb(f"mg_st{i}", [128, 4], F32, ls) for i in range(NXB)]
        b_st2 = [Buf(f"mgst{i}") for i in range(NXB)]
        wov = I["w_out"].rearrange("(kc p) c -> p kc c", p=128)
        for kc in range(NKC):
            s = kc % NXB
            P.dma("sp", xb[s][:], wov[:, kc, :], writes=[b_xb[s]], sem=xsem[s])
            eng = "dve" if kc % 2 == 0 else "pool"
            P.op(eng, lambda e, kc=kc, s=s: e.tensor_copy(out=wout[:, kc, :], in_=xb[s][:]), reads=[b_xb[s]], writes=[b_wout])
        po = [k.ps(f"mg_po{i}", [128, 512], F32, ls) for i in range(3)]
        b_po = [Buf(f"mgpo{i}") for i in range(3)]
        pi = 0
        for t in range(16):
            s = t % NXB
            tb = t // 4
            P.dma("sp", xb[s][:], I["x"][t * 128:(t + 1) * 128, :], writes=[b_xb[s]], sem=xsem[s])
            for cbk in range(4):
                p_ = pi % 3
                pi += 1
                for kc in range(NKC):
                    P.op("pe", lambda e, kc=kc, cbk=cbk, p_=p_, t=t: e.matmul(po[p_][:], lhsT=mT[:, kc, t * 128:(t + 1) * 128],
                                                                          rhs=wout[:, kc, cbk * 512:(cbk + 1) * 512],
                                                                          start=(kc == 0), stop=(kc == NKC - 1)),
                         reads=[b_mT[tb], b_wout], writes=[b_po[p_]])
                P.op("dve", lambda e, cbk=cbk, p_=p_, s=s: e.tensor_tensor(out=xn[s][:, cbk * 512:(cbk + 1) * 512], in0=po[p_][:],
                                                                       in1=gateB[:, cbk * 512:(cbk + 1) * 512], op=ALU.mult),
                     reads=[b_po[p_], b_gateB], writes=[b_xn[s]])
            P.op("pool", lambda e, s=s: e.tensor_tensor(out=xn[s][:], in0=xn[s][:], in1=xb[s][:], op=ALU.add),
                 reads=[b_xn[s], b_xb[s]], writes=[b_xn[s]])
            P.op("act", lambda e, s=s: e.activation(out=xb[s][:], in_=xn[s][:], func=AF.Square, accum_out=st2[s][:, 0:1]),
                 reads=[b_xn[s]], writes=[b_xb[s], b_st2[s]])
            P.op("dve", lambda e, s=s: e.tensor_scalar(out=st2[s][:, 1:2], in0=st2[s][:, 0:1], scalar1=1.0 / D, scalar2=EPS, op0=ALU.mult, op1=ALU.add),
                 reads=[b_st2[s]], writes=[b_st2[s]])
            P.op("act", lambda e, s=s: e.activation(out=st2[s][:, 2:3], in_=st2[s][:, 1:2], func=AF.Sqrt), reads=[b_st2[s]], writes=[b_st2[s]])
            P.op("dve", lambda e, s=s: e.reciprocal(out=st2[s][:, 3:4], in_=st2[s][:, 2:3]), reads=[b_st2[s]], writes=[b_st2[s]])
            P.op("dve", lambda e, s=s: e.scalar_tensor_tensor(out=xn[s][:], in0=xn[s][:], scalar=st2[s][:, 3:4], in1=fgB[:], op0=ALU.mult, op1=ALU.mult),
                 reads=[b_xn[s], b_st2[s], b_fgB], writes=[b_xn[s]])
            P.dma("sp", k.out[t * 128:(t + 1) * 128, :], xn[s][:], reads=[b_xn[s]], sem=osem[s])


_CACHE = {}


def _prep_inputs(inputs, b):
    f = lambda a: np.ascontiguousarray(np.asarray(a, dtype=np.float32))
    m = {}
    m["x"] = f(inputs["x"][b])
    m["ctx"] = f(inputs["ctx"][b])
    m["cc"] = f(np.stack([np.asarray(inputs["c"])[b], np.asarray(inputs["c_ctx"])], axis=0))
    m["w_ada"] = f(inputs["w_ada"][0])
    m["b_ada"] = f(inputs["b_ada"][0]).reshape(1, -1)
    m["norm_g"] = f(inputs["norm_g"][0])
    m["w_in"] = f(inputs["w_in"][0])
    m["lam"] = f(np.stack([np.asarray(inputs["lambda_q1"])[0], np.asarray(inputs["lambda_k1"])[0],
                           np.asarray(inputs["lambda_q2"])[0], np.asarray(inputs["lambda_k2"])[0]], axis=0))
    m["subln_g"] = f(inputs["subln_g"][0]).reshape(1, 128)
    m["ssm_lre"] = f(inputs["ssm_lambda_re"][0])
    m["ssm_lim"] = f(inputs["ssm_lambda_im"][0])
    m["ssm_ls"] = f(inputs["ssm_log_step"][0])
    m["ssm_bre"] = f(inputs["ssm_b_re"][0])
    m["ssm_bim"] = f(inputs["ssm_b_im"][0])
    m["ssm_cre"] = f(inputs["ssm_c_re"][0])
    m["ssm_cim"] = f(inputs["ssm_c_im"][0])
    m["ssm_d"] = f(inputs["ssm_d"][0]).reshape(1, 512)
    m["w_glu"] = f(inputs["w_glu"][0])
    m["b_glu"] = f(inputs["b_glu"][0])
    m["w_pa"] = f(inputs["w_pa"][0])
    m["w_ps"] = f(inputs["w_ps"][0])
    m["w_out"] = f(inputs["w_out"][0])
    m["final_g"] = f(inputs["final_g"]).reshape(1, D)
    m.update(_consts())
    return m


def kernel(**inputs):
    if "nc" not in _CACHE:
        _CACHE["nc"] = build()[0]
    nc = _CACHE["nc"]
    shared = None
    in_maps = []
    for b in range(8):
        m = _prep_inputs(inputs, b)
        if shared is None:
            shared = m
        else:
            for key in m:
                if key not in ("x", "ctx", "cc"):
                    m[key] = shared[key]
        in_maps.append(m)
    res = run_bass_kernel_spmd(nc, in_maps, core_ids=list(range(8)))
    return np.stack([np.asarray(r["out"], dtype=np.float32) for r in res.results], axis=0)
```

```python
import math
import numpy as np
import ml_dtypes
from contextlib import ExitStack
import concourse.bass as bass
import concourse.mybir as mybir
from concourse.bass_utils import run_bass_kernel_spmd

F32 = mybir.dt.float32
BF16 = mybir.dt.bfloat16
I32 = mybir.dt.int32
AF = mybir.ActivationFunctionType
ALU = mybir.AluOpType

D = 2048
L = 2048
LC = 256
LT = L + LC
NKC = D // 128
INW = 9216
HEADS = 8
EPS = 1e-6
LAM_INIT = 0.8 - 0.6 * math.exp(-0.3 * 0)
TWO_PI = 2.0 * math.pi


class Buf:
    __slots__ = ("name", "w", "r")

    def __init__(self, name):
        self.name = name
        self.w = None
        self.r = {}


class Prog:
    ENG = ["pe", "act", "dve", "pool", "sp"]

    def __init__(self, nc, st):
        self.nc = nc
        self.st = st
        self.q = {e: [] for e in self.ENG}
        self.seen = {e: {} for e in self.ENG}
        self.psem = {e: st.enter_context(nc.semaphore("p_" + e)) for e in ["pe", "act", "dve", "pool"]}
        self.dsems = []
        self.bufsem = {}
        self.bufsem_keep = []
        self.free_dsems = []

    def new_dsem(self, name):
        return None

    def _auto_dsem(self, reads, writes):
        b = writes[0] if len(writes) else reads[0]
        key = id(b)
        d = self.bufsem.get(key)
        if d is None:
            if self.free_dsems:
                d = self.free_dsems.pop()
            else:
                h = self.st.enter_context(self.nc.semaphore(f"d{len(self.dsems)}"))
                d = {"h": h, "n": 0, "name": f"d{len(self.dsems)}"}
                self.dsems.append(d)
            self.bufsem[key] = d
            self.bufsem_keep.append(b)
        return d

    def _deps(self, eng, reads, writes):
        need = {}

        def add(t):
            if t[0] == "c":
                if t[1] == "pe" and eng == "pe":
                    return
                key = ("c", t[1])
                if need.get(key, (None, -1))[1] < t[2]:
                    need[key] = (t[1], t[2])
            else:
                key = ("d", id(t[1]))
                if need.get(key, (None, -1))[1] < t[2]:
                    need[key] = (t[1], t[2])

        for b in reads:
            if b.w is not None:
                add(b.w)
        for b in writes:
            if b.w is not None:
                add(b.w)
            for t in b.r.values():
                add(t)
        waits = []
        for key, (obj, v) in need.items():
            if self.seen[eng].get(key, -1) >= v:
                continue
            self.seen[eng][key] = v
            waits.append((key[0], obj, v))
        return waits

    def _record(self, tok, reads, writes):
        for b in reads:
            key = (tok[0], tok[1] if tok[0] == "c" else id(tok[1]))
            b.r[key] = tok
        for b in writes:
            b.w = tok
            b.r = {}

    def op(self, eng, fn, reads=(), writes=()):
        waits = self._deps(eng, reads, writes)
        idx = len(self.q[eng])
        self.q[eng].append({"fn": fn, "waits": waits, "awaited": False, "dma": None})
        tok = ("c", eng, idx)
        self._record(tok, reads, writes)
        return tok

    def dma(self, eng, out, in_, reads=(), writes=(), sem=None, **kw):
        reads, writes = list(reads), list(writes)
        sem = self._auto_dsem(reads, writes)
        waits = self._deps(eng, reads, writes)
        sem["n"] += 16
        tok = ("d", sem, sem["n"])
        self.q[eng].append({"fn": (lambda e, o=out, i=in_, k=kw: e.dma_start(out=o, in_=i, **k)),
                            "waits": waits, "awaited": False, "dma": sem})
        self._record(tok, reads, writes)
        return tok

    def barrier(self):
        for e in self.ENG:
            waits = []
            for e2 in ["pe", "act", "dve", "pool"]:
                n = len(self.q[e2])
                if e2 == e:
                    n -= 0
                idx = None
                for i in range(len(self.q[e2]) - 1, -1, -1):
                    if self.q[e2][i]["fn"] is not None and self.q[e2][i]["dma"] is None:
                        idx = i
                        break
                if idx is None:
                    continue
                key = ("c", e2)
                if self.seen[e].get(key, -1) >= idx:
                    continue
                self.seen[e][key] = idx
                waits.append(("c", e2, idx))
            for d in self.dsems:
                if d["n"] == 0:
                    continue
                key = ("d", id(d))
                if self.seen[e].get(key, -1) >= d["n"]:
                    continue
                self.seen[e][key] = d["n"]
                waits.append(("d", d, d["n"]))
            if waits:
                self.q[e].append({"fn": None, "waits": waits, "awaited": False, "dma": None})
        for d in self.bufsem.values():
            self.free_dsems.append(d)
        self.bufsem = {}
        self.bufsem_keep = []

    def emit(self):
        for e in self.ENG:
            for ent in self.q[e]:
                for w in ent["waits"]:
                    if w[0] == "c":
                        self.q[w[1]][w[2]]["awaited"] = True
        cnt = {}
        for e in ["pe", "act", "dve", "pool"]:
            c = 0
            arr = []
            for ent in self.q[e]:
                if ent["awaited"]:
                    c += 1
                arr.append(c)
            cnt[e] = arr
        psem = self.psem
        q = self.q

        def run(name, e):
            for ent in q[name]:
                for w in ent["waits"]:
                    if w[0] == "c":
                        e.wait_ge(psem[w[1]], cnt[w[1]][w[2]])
                    else:
                        e.wait_ge(w[1]["h"], w[2])
                if ent["fn"] is None:
                    continue
                inst = ent["fn"](e)
                if ent["dma"] is not None:
                    inst.then_inc(ent["dma"]["h"], 16)
                elif ent["awaited"]:
                    inst.then_inc(psem[name], 1)

        with self.nc.Block() as block:
            @block.sync
            def _(e):
                run("sp", e)

            @block.scalar
            def _(e):
                run("act", e)

            @block.vector
            def _(e):
                run("dve", e)

            @block.gpsimd
            def _(e):
                run("pool", e)

            @block.tensor
            def _(e):
                run("pe", e)


def _consts():
    ident = np.eye(128, dtype=np.float32)
    m = np.arange(128)
    partner = np.where((m % 32) < 16, m + 16, m - 16)
    perm = np.zeros((128, 128), np.float32)
    perm[partner, m] = 1.0
    sgn = np.where((m % 32) < 16, -1.0, 1.0).astype(np.float32)
    tok = np.arange(L)
    pos = np.where(((m % 64) < 32)[:, None], (tok // 64)[None, :], (tok % 64)[None, :]).astype(np.float32)
    fexp = ((m % 16) / 16.0).astype(np.float32)
    colc = np.zeros((128, 4), np.float32)
    colc[:, 0] = sgn
    colc[:, 1] = fexp
    colc[:, 2] = np.where(m < 64, 1.0, -1.0)
    sel = np.zeros((2, 128), np.float32)
    sel[0, :] = 1.0
    tauA = np.zeros((128, 32, 9), np.float32)
    tauA[:, 0:16, :] = np.arange(9)[None, None, :]
    tauA[:, 16:32, :] = (8 - np.arange(9))[None, None, :]
    tauB = np.zeros((128, 32, 8), np.float32)
    tauB[:, 0:16, :] = (7 - np.arange(8))[None, None, :]
    tauB[:, 16:32, :] = np.arange(8)[None, None, :]
    return {"c_ident": ident, "c_perm": perm, "c_pos": pos, "c_col": colc, "c_sel": sel, "c_tauA": tauA, "c_tauB": tauB}


class K:
    pass


def build(debug=None):
    nc = bass.Bass("TRN2", target_bir_lowering=False)
    st = ExitStack()
    P = Prog(nc, st)
    k = K()
    k.nc, k.P, k.st = nc, P, st
    k.debug = debug or {}

    def dram_in(name, shape, dt=F32):
        return nc.dram_tensor(name, list(shape), dt, kind="ExternalInput").ap()

    dbg_outs = []

    def dram_scr(name, shape, dt):
        kind = "Internal"
        if debug is not None and name in debug.get("_inject", ()):
            kind = "ExternalInput"
        elif debug is not None and name in debug:
            kind = "ExternalOutput"
            dbg_outs.append(name)
        return nc.dram_tensor(name, list(shape), dt, kind=kind).ap()

    I = {}
    I["x"] = dram_in("x", [L, D])
    I["ctx"] = dram_in("ctx", [LC, D])
    I["cc"] = dram_in("cc", [2, D])
    I["w_ada"] = dram_in("w_ada", [D, 3 * D])
    I["b_ada"] = dram_in("b_ada", [1, 3 * D])
    I["norm_g"] = dram_in("norm_g", [D])
    I["w_in"] = dram_in("w_in", [D, INW])
    I["lam"] = dram_in("lam", [4, 64])
    I["subln_g"] = dram_in("subln_g", [1, 128])
    I["ssm_lre"] = dram_in("ssm_lre", [2, 32, 64])
    I["ssm_lim"] = dram_in("ssm_lim", [2, 32, 64])
    I["ssm_ls"] = dram_in("ssm_ls", [2, 32])
    I["ssm_bre"] = dram_in("ssm_bre", [2, 32, 64, 16])
    I["ssm_bim"] = dram_in("ssm_bim", [2, 32, 64, 16])
    I["ssm_cre"] = dram_in("ssm_cre", [2, 32, 16, 64])
    I["ssm_cim"] = dram_in("ssm_cim", [2, 32, 16, 64])
    I["ssm_d"] = dram_in("ssm_d", [1, 512])
    I["w_glu"] = dram_in("w_glu", [512, 512])
    I["b_glu"] = dram_in("b_glu", [512])
    I["w_pa"] = dram_in("w_pa", [1024, D])
    I["w_ps"] = dram_in("w_ps", [512, D])
    I["w_out"] = dram_in("w_out", [D, D])
    I["final_g"] = dram_in("final_g", [1, D])
    for cn, arr in _consts().items():
        I[cn] = dram_in(cn, arr.shape)
    out = nc.dram_tensor("out", [L, D], F32, kind="ExternalOutput").ap()

    S = {}
    S["modrow"] = dram_scr("modrow", [2, 3 * D], F32)
    S["qT"] = dram_scr("qT", [HEADS, 128, L], BF16)
    S["kT"] = dram_scr("kT", [HEADS, 128, LT], BF16)
    S["v"] = dram_scr("v", [LT, 1024], BF16)
    S["sga"] = dram_scr("sga", [L, 1024], BF16)
    S["u"] = dram_scr("u", [LT, 512], F32)
    S["sgsT"] = dram_scr("sgsT", [512, L], BF16)
    S["sgmT"] = dram_scr("sgmT", [2 * D, L], BF16)
    S["abrT"] = dram_scr("abrT", [1024, L], BF16)
    S["sbrT"] = dram_scr("sbrT", [512, L], BF16)
    S["hT"] = dram_scr("hT_dbg", [128, NKC, LT], BF16) if (debug is not None and "hT_dbg" in debug) else None
    k.I, k.S, k.out = I, S, out
    k.dbg_sem = None

    def dbg(name, shape, dt, ap_fn, bufs):
        if debug is None or name not in debug:
            return
        if name not in S:
            S[name] = nc.dram_tensor(name, list(shape), dt, kind="ExternalOutput").ap()
            dbg_outs.append(name)
        if k.dbg_sem is None:
            k.dbg_sem = P.new_dsem("dbgsem")
        o, i = ap_fn(S[name])
        P.dma("sp", o, i, reads=bufs, sem=k.dbg_sem)
    k.dbg = dbg

    def sb(name, shape, dt, stack=st):
        return stack.enter_context(nc.sbuf_tensor(name, list(shape), dt))

    def ps(name, shape, dt, stack=st):
        return stack.enter_context(nc.psum_tensor(name, list(shape), dt))

    k.sb, k.ps = sb, ps
    ident = sb("ident", [128, 128], F32)
    colc = sb("colc", [128, 4], F32)
    b_ident, b_colc = Buf("ident"), Buf("colc")
    csem = P.new_dsem("csem")
    P.dma("sp", ident[:], I["c_ident"], writes=[b_ident], sem=csem)
    P.dma("sp", colc[:], I["c_col"], writes=[b_colc], sem=csem)
    k.ident, k.b_ident, k.colc, k.b_colc, k.csem = ident, b_ident, colc, b_colc, csem
    k.ssq = sb("ssq", [128, 40], F32)
    k.b_ssq = Buf("ssq")

    phase_adaln(k)
    P.barrier()
    if debug is None or debug.get("_upto", 99) >= 1:
        phase_norm_inproj(k, debug)
        P.barrier()
    if (debug is None or debug.get("_upto", 99) >= 2) and not (debug or {}).get("_skip_ssm"):
        phase_ssm(k)
        P.barrier()
    if debug is None or debug.get("_upto", 99) >= 3:
        phase_attn(k)
        P.barrier()
    if debug is None or debug.get("_upto", 99) >= 4:
        phase_merge(k)
        P.barrier()
    P.emit()
    st.close()
    return nc, dbg_outs


def range_sin(k, stack, out_ap, y_ap, shape, tag, rbufs, wbufs, eng="dve"):
    nc, P = k.nc, k.P
    ki = k.sb(tag + "_ki", shape, I32, stack)
    kf = k.sb(tag + "_kf", shape, F32, stack)
    g = k.sb(tag + "_g", shape, F32, stack)
    bki, bkf, bg = Buf(tag + "ki"), Buf(tag + "kf"), Buf(tag + "g")
    sl = tuple([slice(None)] * len(shape))
    P.op(eng, lambda e: e.tensor_copy(out=ki[sl], in_=y_ap), reads=rbufs, writes=[bki])
    P.op(eng, lambda e: e.tensor_copy(out=kf[sl], in_=ki[sl]), reads=[bki], writes=[bkf])
    P.op(eng, lambda e: e.tensor_tensor(out=kf[sl], in0=y_ap, in1=kf[sl], op=ALU.subtract), reads=rbufs + [bkf], writes=[bkf])
    P.op(eng, lambda e: e.tensor_single_scalar(out=g[sl], in_=kf[sl], scalar=0.5, op=ALU.is_gt), reads=[bkf], writes=[bg])
    P.op(eng, lambda e: e.tensor_tensor(out=kf[sl], in0=kf[sl], in1=g[sl], op=ALU.subtract), reads=[bkf, bg], writes=[bkf])
    P.op(eng, lambda e: e.tensor_single_scalar(out=g[sl], in_=kf[sl], scalar=-0.5, op=ALU.is_lt), reads=[bkf], writes=[bg])
    P.op(eng, lambda e: e.tensor_tensor(out=kf[sl], in0=kf[sl], in1=g[sl], op=ALU.add), reads=[bkf, bg], writes=[bkf])
    P.op("act", lambda e: e.activation(out=out_ap, in_=kf[sl], func=AF.Sin, scale=TWO_PI * (1.0 - 2e-7)), reads=[bkf], writes=wbufs)


def phase_adaln(k):
    nc, P, I, S = k.nc, k.P, k.I, k.S
    with ExitStack() as ls:
        sT = k.sb("ad_sT", [128, NKC, 2], F32, ls)
        b_sT = Buf("sT")
        sem_c = P.new_dsem("ad_c")
        for v in range(2):
            P.dma("sp", sT[:, :, v], I["cc"][v].rearrange("(kc p) -> p kc", p=128), writes=[b_sT], sem=sem_c,
                  allow_slow_non_contiguous=True)
        P.op("act", lambda e: e.activation(out=sT[:], in_=sT[:], func=AF.Silu), reads=[b_sT], writes=[b_sT])
        brow = k.sb("ad_brow", [2, 3 * D], F32, ls)
        b_brow = Buf("brow")
        for v in range(2):
            P.dma("sp", brow[v:v + 1, :], I["b_ada"], writes=[b_brow], sem=sem_c)
        modrow = k.sb("ad_modrow", [2, 3 * D], F32, ls)
        b_modrow = Buf("modrow")
        NS = 2
        wst = [k.sb(f"ad_w{i}", [128, NKC, 512], F32, ls) for i in range(NS)]
        b_w = [Buf(f"adw{i}") for i in range(NS)]
        wsem = [P.new_dsem(f"ad_ws{i}") for i in range(NS)]
        pst = [k.ps(f"ad_ps{i}", [128, 512], F32, ls) for i in range(2)]
        b_ps = [Buf(f"adps{i}") for i in range(2)]
        wv = I["w_ada"].rearrange("(kc p) c -> p kc c", p=128)
        xs1 = [k.sb(f"ad_x{i}", [128, D], F32, ls) for i in range(2)]
        b_xs1 = [Buf(f"adx{i}") for i in range(2)]
        junk1 = k.sb("ad_junk", [128, D], BF16, ls)
        b_junk1 = Buf("adjunk")
        tiles_done = 0

        def ss_tile(t):
            s1 = t % 2
            src = I["x"][t * 128:(t + 1) * 128, :] if t < 16 else I["ctx"][(t - 16) * 128:(t - 15) * 128, :]
            P.dma("pool", xs1[s1][:], src, writes=[b_xs1[s1]])
            P.op("act", lambda e, s1=s1, t=t: e.activation(out=junk1[:], in_=xs1[s1][:], func=AF.Square, accum_out=k.ssq[:, t:t + 1]),
                 reads=[b_xs1[s1]], writes=[b_junk1, k.b_ssq])

        for cb in range(12):
            for _ in range(2 if cb < 6 else 1):
                if tiles_done < 18:
                    ss_tile(tiles_done)
                    tiles_done += 1
            s = cb % NS
            P.dma("sp", wst[s][:, 0:8, :], wv[:, 0:8, cb * 512:(cb + 1) * 512], writes=[b_w[s]], sem=wsem[s])
            P.dma("act", wst[s][:, 8:16, :], wv[:, 8:16, cb * 512:(cb + 1) * 512], writes=[b_w[s]], sem=wsem[s])
            pt, bp = pst[cb % 2], b_ps[cb % 2]
            for kc in range(NKC):
                P.op("pe", lambda e, kc=kc, s=s, pt=pt: e.matmul(pt[0:2, :], lhsT=sT[:, kc, :], rhs=wst[s][:, kc, :],
                                                              start=(kc == 0), stop=(kc == NKC - 1)),
                     reads=[b_sT, b_w[s]], writes=[bp])
            P.op("dve", lambda e, cb=cb, pt=pt: e.tensor_tensor(out=modrow[:, cb * 512:(cb + 1) * 512], in0=pt[0:2, :],
                                                             in1=brow[:, cb * 512:(cb + 1) * 512], op=ALU.add),
                 reads=[bp, b_brow], writes=[b_modrow])
        b_mr = Buf("modrow_d")
        k.b_modrow_d = b_mr
        P.dma("sp", S["modrow"], modrow[:], reads=[b_modrow], writes=[b_mr], sem=sem_c)


def phase_norm_inproj(k, debug):
    nc, P, I, S = k.nc, k.P, k.I, k.S
    with ExitStack() as ls:
        hT = k.sb("hT", [128, NKC, LT], BF16, ls)
        b_hT = [Buf(f"hT{t}") for t in range(18)]
        Amod = k.sb("Amod", [128, NKC, 2], F32, ls)
        Smod = k.sb("Smod", [128, NKC, 2], F32, ls)
        gcol = k.sb("gcol", [128, NKC], F32, ls)
        b_A, b_S, b_g = Buf("Amod"), Buf("Smod"), Buf("gcol")
        msem = P.new_dsem("n_m")
        for v in range(2):
            P.dma("sp", Smod[:, :, v], S["modrow"][v, 0:D].rearrange("(kc p) -> p kc", p=128),
                  reads=[k.b_modrow_d], writes=[b_S], sem=msem, allow_slow_non_contiguous=True)
            P.dma("sp", Amod[:, :, v], S["modrow"][v, D:2 * D].rearrange("(kc p) -> p kc", p=128),
                  reads=[k.b_modrow_d], writes=[b_A], sem=msem, allow_slow_non_contiguous=True)
        P.dma("sp", gcol[:], I["norm_g"].rearrange("(kc p) -> p kc", p=128), writes=[b_g], sem=msem,
              allow_slow_non_contiguous=True)
        for v in range(2):
            P.op("dve", lambda e, v=v: e.scalar_tensor_tensor(out=Amod[:, :, v], in0=Amod[:, :, v], scalar=1.0, in1=gcol[:],
                                                             op0=ALU.add, op1=ALU.mult),
                 reads=[b_A, b_g], writes=[b_A])
        with ExitStack() as l1:
            NX = 2
            xt = [k.sb(f"n_x{i}", [128, D], F32, l1) for i in range(NX)]
            b_x = [Buf(f"nx{i}") for i in range(NX)]
            xsem = [P.new_dsem(f"n_xs{i}") for i in range(NX)]
            junk = k.sb("n_junk", [128, D], BF16, l1)
            b_junk = Buf("junk")
            stat = [k.sb(f"n_st{i}", [128, 4], F32, l1) for i in range(NX)]
            b_stat = [Buf(f"nst{i}") for i in range(NX)]
            pt = [k.ps(f"n_ps{i}", [128, 512], F32, l1) for i in range(4)]
            b_pt = [Buf(f"nps{i}") for i in range(4)]
            pi = 0
            P.op("dve", lambda e: e.tensor_scalar(out=k.ssq[:, 0:18], in0=k.ssq[:, 0:18], scalar1=1.0 / D, scalar2=EPS, op0=ALU.mult, op1=ALU.add),
                 reads=[k.b_ssq], writes=[k.b_ssq])
            P.op("act", lambda e: e.activation(out=k.ssq[:, 0:18], in_=k.ssq[:, 0:18], func=AF.Ln), reads=[k.b_ssq], writes=[k.b_ssq])
            P.op("act", lambda e: e.activation(out=k.ssq[:, 20:38], in_=k.ssq[:, 0:18], func=AF.Exp, scale=-0.5), reads=[k.b_ssq], writes=[k.b_ssq])
            for t in range(18):
                s = t % NX
                v = 0 if t < 16 else 1
                src = I["x"][t * 128:(t + 1) * 128, :] if t < 16 else I["ctx"][(t - 16) * 128:(t - 15) * 128, :]
                P.dma("sp", xt[s][:, 0:1024], src[:, 0:1024], writes=[b_x[s]], sem=xsem[s])
                P.dma("act", xt[s][:, 1024:2048], src[:, 1024:2048], writes=[b_x[s]], sem=xsem[s])
                P.op("dve", lambda e, s=s, t=t: e.tensor_scalar(out=xt[s][:], in0=xt[s][:], scalar1=k.ssq[:, 20 + t:21 + t], scalar2=None,
                                                              op0=ALU.mult),
                     reads=[b_x[s], k.b_ssq], writes=[b_x[s]])
                for g4 in range(4):
                    p_, bp = pt[pi % 4], b_pt[pi % 4]
                    pi += 1
                    for j in range(4):
                        kc = g4 * 4 + j
                        P.op("pe", lambda e, s=s, kc=kc, j=j, p_=p_: e.transpose(out=p_[:, j * 128:(j + 1) * 128],
                                                                             in_=xt[s][:, kc * 128:(kc + 1) * 128],
                                                                             identity=k.ident[:]),
                             reads=[b_x[s], k.b_ident], writes=[bp])
                    for j in range(4):
                        kc = g4 * 4 + j
                        eng = "dve" if (j % 2 == 0) else "act"
                        if eng == "dve":
                            P.op("dve", lambda e, kc=kc, j=j, p_=p_, t=t, v=v: e.tensor_scalar(
                                out=hT[:, kc, t * 128:(t + 1) * 128], in0=p_[:, j * 128:(j + 1) * 128],
                                scalar1=Amod[:, kc, v:v + 1], scalar2=Smod[:, kc, v:v + 1], op0=ALU.mult, op1=ALU.add),
                                reads=[bp, b_A, b_S], writes=[b_hT[t]])
                        else:
                            P.op("act", lambda e, kc=kc, j=j, p_=p_, t=t, v=v: e.activation(
                                out=hT[:, kc, t * 128:(t + 1) * 128], in_=p_[:, j * 128:(j + 1) * 128],
                                func=AF.Identity, scale=Amod[:, kc, v:v + 1], bias=Smod[:, kc, v:v + 1]),
                                reads=[bp, b_A, b_S], writes=[b_hT[t]])
        if S["hT"] is not None:
            dsem = P.new_dsem("dbg")
            P.dma("sp", S["hT"], hT[:], reads=b_hT, writes=[Buf("x")], sem=dsem)
        P.barrier()
        if debug is not None and debug.get("_upto", 99) < 1.5:
            return
        inproj(k, ls, hT, b_hT)


def inproj(k, ls, hT, b_hT):
    nc, P, I, S = k.nc, k.P, k.I, k.S
    cosT = k.sb("cosT", [128, L], F32, ls)
    sinS = k.sb("sinS", [128, L], F32, ls)
    perm = k.sb("perm", [128, 128], F32, ls)
    b_cos, b_sin, b_perm = Buf("cos"), Buf("sin"), Buf("perm")
    tsem = P.new_dsem("ip_t")
    P.dma("sp", perm[:], I["c_perm"], writes=[b_perm], sem=tsem)
    with ExitStack() as l0:
        pos = k.sb("pos", [128, L], F32, l0)
        yv = k.sb("yv", [128, L], F32, l0)
        inv = k.sb("inv", [128, 1], F32, l0)
        b_pos, b_y, b_inv = Buf("pos"), Buf("yv"), Buf("inv")
        P.dma("sp", pos[:], I["c_pos"], writes=[b_pos], sem=tsem)
        P.op("act", lambda e: e.activation(out=inv[:], in_=k.colc[:, 1:2], func=AF.Exp, scale=-math.log(10000.0)),
             reads=[k.b_colc], writes=[b_inv])
        P.op("dve", lambda e: e.tensor_scalar(out=yv[:], in0=pos[:], scalar1=inv[:, 0:1], scalar2=1.0 / TWO_PI,
                                              op0=ALU.mult, op1=ALU.mult), reads=[b_pos, b_inv], writes=[b_y])
        range_sin(k, l0, sinS[:], yv[:], [128, L], "rs1", [b_y], [b_sin])
        P.op("dve", lambda e: e.tensor_scalar(out=sinS[:], in0=sinS[:], scalar1=k.colc[:, 0:1], scalar2=None, op0=ALU.mult),
             reads=[b_sin, k.b_colc], writes=[b_sin])
        P.op("dve", lambda e: e.tensor_scalar(out=yv[:], in0=yv[:], scalar1=0.25, scalar2=None, op0=ALU.add),
             reads=[b_y], writes=[b_y])
        range_sin(k, l0, cosT[:], yv[:], [128, L], "rs2", [b_y], [b_cos])
        P.barrier()
    NW = 2
    wst = [k.sb(f"ip_wst{i}", [128, 8, 512], F32, ls) for i in range(NW)]
    b_wst = [Buf(f"wst{i}") for i in range(NW)]
    wsem = [P.new_dsem(f"ip_ws{i}") for i in range(NW)]
    wb = [k.sb(f"ip_wb{i}", [128, NKC, 512], BF16, ls) for i in range(2)]
    b_wb = [Buf(f"wb{i}") for i in range(2)]
    NOB = 4
    ob = [k.sb(f"ip_ob{i}", [128, 512], BF16, ls) for i in range(NOB)]
    b_ob = [Buf(f"ob{i}") for i in range(NOB)]
    osem = [P.new_dsem(f"ip_os{i}") for i in range(NOB)]
    NOF = 2
    of = [k.sb(f"ip_of{i}", [128, 512], F32, ls) for i in range(NOF)]
    b_of = [Buf(f"of{i}") for i in range(NOF)]
    fsem = [P.new_dsem(f"ip_fs{i}") for i in range(NOF)]
    t1 = [k.sb(f"ip_t1{i}", [128, 512], F32, ls) for i in range(2)]
    b_t1 = [Buf(f"t1{i}") for i in range(2)]
    t2 = [k.sb(f"ip_t2{i}", [128, 512], F32, ls) for i in range(2)]
    b_t2 = [Buf(f"t2{i}") for i in range(2)]
    pb = [k.ps(f"ip_ps{i}", [128, 512], F32, ls) for i in range(4)]
    b_pb = [Buf(f"ipps{i}") for i in range(4)]
    pr = [k.ps(f"ip_pr{i}", [128, 512], F32, ls) for i in range(2)]
    b_pr = [Buf(f"ippr{i}") for i in range(2)]
    wv = I["w_in"].rearrange("(kc p) c -> p kc c", p=128)
    cnt = {"pb": 0, "ob": 0, "of": 0, "r": 0, "ld": 0, "ev": 0}

    def load_block(cb):
        s2 = cb % 2
        for half in range(2):
            sl = cnt["ld"] % NW
            cnt["ld"] += 1
            P.dma("sp", wst[sl][:], wv[:, half * 8:(half + 1) * 8, cb * 512:(cb + 1) * 512], writes=[b_wst[sl]], sem=wsem[sl])
            P.op("dve", lambda e, sl=sl, s2=s2, half=half: e.tensor_copy(out=wb[s2][:, half * 8:(half + 1) * 8, :], in_=wst[sl][:]),
                 reads=[b_wst[sl]], writes=[b_wb[s2]])

    def next_ob():
        i = cnt["ob"] % NOB
        cnt["ob"] += 1
        return i

    def evac_eng():
        cnt["ev"] += 1
        return "act" if cnt["ev"] % 2 else "dve"

    def tiles_of(tok0, n):
        return [b_hT[t] for t in range(tok0 // 128, (tok0 + n) // 128)]

    def fm_unit(cb, fc, tok0, n, kind, row0, dst):
        s2 = cb % 2
        pi = cnt["pb"] % 4
        cnt["pb"] += 1
        pt, bp = pb[pi], b_pb[pi]
        for kc in range(NKC):
            P.op("pe", lambda e, kc=kc: e.matmul(pt[:, 0:n], lhsT=wb[s2][:, kc, fc * 128:(fc + 1) * 128],
                                                 rhs=hT[:, kc, tok0:tok0 + n], start=(kc == 0), stop=(kc == NKC - 1)),
                 reads=[b_wb[s2]] + tiles_of(tok0, n), writes=[bp])
        oi = next_ob()
        if kind == "rope":
            ri = cnt["r"] % 2
            cnt["r"] += 1
            fi = cnt["of"] % NOF
            cnt["of"] += 1
            P.op("act", lambda e: e.activation(out=of[fi][:, 0:n], in_=pt[:, 0:n], func=AF.Copy), reads=[bp], writes=[b_of[fi]])
            P.op("pe", lambda e: e.matmul(pr[ri][:, 0:n], lhsT=perm[:], rhs=of[fi][:, 0:n], start=True, stop=True),
                 reads=[b_perm, b_of[fi]], writes=[b_pr[ri]])
            P.op("dve", lambda e: e.tensor_tensor(out=t1[ri][:, 0:n], in0=of[fi][:, 0:n], in1=cosT[:, tok0:tok0 + n], op=ALU.mult),
                 reads=[b_of[fi], b_cos], writes=[b_t1[ri]])
            P.op("dve", lambda e: e.tensor_tensor(out=t2[ri][:, 0:n], in0=pr[ri][:, 0:n], in1=sinS[:, tok0:tok0 + n], op=ALU.mult),
                 reads=[b_pr[ri], b_sin], writes=[b_t2[ri]])
            P.op("pool", lambda e: e.tensor_tensor(out=ob[oi][:, 0:n], in0=t1[ri][:, 0:n], in1=t2[ri][:, 0:n], op=ALU.add),
                 reads=[b_t1[ri], b_t2[ri]], writes=[b_ob[oi]])
        elif kind == "copy":
            eg = evac_eng()
            if eg == "act":
                P.op("act", lambda e: e.activation(out=ob[oi][:, 0:n], in_=pt[:, 0:n], func=AF.Copy), reads=[bp], writes=[b_ob[oi]])
            else:
                P.op("dve", lambda e: e.tensor_copy(out=ob[oi][:, 0:n], in_=pt[:, 0:n]), reads=[bp], writes=[b_ob[oi]])
        else:
            fn = AF.Silu if kind == "silu" else AF.Sigmoid
            P.op("act", lambda e: e.activation(out=ob[oi][:, 0:n], in_=pt[:, 0:n], func=fn), reads=[bp], writes=[b_ob[oi]])
        P.dma("pool", dst, ob[oi][:, 0:n], reads=[b_ob[oi]], sem=osem[oi])

    def tm_unit(cb, t, kind, dst):
        s2 = cb % 2
        pi = cnt["pb"] % 4
        cnt["pb"] += 1
        pt, bp = pb[pi], b_pb[pi]
        for kc in range(NKC):
            P.op("pe", lambda e, kc=kc: e.matmul(pt[:], lhsT=hT[:, kc, t * 128:(t + 1) * 128], rhs=wb[s2][:, kc, :],
                                                 start=(kc == 0), stop=(kc == NKC - 1)),
                 reads=[b_wb[s2], b_hT[t]], writes=[bp])
        if kind == "f32":
            fi = cnt["of"] % NOF
            cnt["of"] += 1
            P.op("dve", lambda e: e.tensor_copy(out=of[fi][:], in_=pt[:]), reads=[bp], writes=[b_of[fi]])
            P.dma("pool", dst, of[fi][:], reads=[b_of[fi]], sem=fsem[fi])
            return
        oi = next_ob()
        if kind == "copy":
            eg = evac_eng()
            if eg == "act":
                P.op("act", lambda e: e.activation(out=ob[oi][:], in_=pt[:], func=AF.Copy), reads=[bp], writes=[b_ob[oi]])
            else:
                P.op("dve", lambda e: e.tensor_copy(out=ob[oi][:], in_=pt[:]), reads=[bp], writes=[b_ob[oi]])
        else:
            P.op("act", lambda e: e.activation(out=ob[oi][:], in_=pt[:], func=AF.Silu), reads=[bp], writes=[b_ob[oi]])
        P.dma("pool", dst, ob[oi][:], reads=[b_ob[oi]], sem=osem[oi])

    NCB = INW // 512
    load_block(0)
    for cb in range(NCB):
        if cb + 1 < NCB:
            load_block(cb + 1)
        c0 = cb * 512
        if cb < 2:
            for fc in range(4):
                h = cb * 4 + fc
                for tb in range(4):
                    fm_unit(cb, fc, tb * 512, 512, "rope", 0, S["qT"][h, :, tb * 512:(tb + 1) * 512])
        elif cb < 4:
            for fc in range(4):
                h = (cb - 2) * 4 + fc
                for tb in range(4):
                    fm_unit(cb, fc, tb * 512, 512, "rope", 0, S["kT"][h, :, tb * 512:(tb + 1) * 512])
                fm_unit(cb, fc, L, LC, "copy", 0, S["kT"][h, :, L:LT])
        elif cb < 6:
            for t in range(18):
                tm_unit(cb, t, "copy", S["v"][t * 128:(t + 1) * 128, (cb - 4) * 512:(cb - 3) * 512])
        elif cb < 8:
            for t in range(16):
                tm_unit(cb, t, "silu", S["sga"][t * 128:(t + 1) * 128, (cb - 6) * 512:(cb - 5) * 512])
        elif cb == 8:
            for t in range(18):
                tm_unit(cb, t, "f32", S["u"][t * 128:(t + 1) * 128, :])
        elif cb == 9:
            for fc in range(4):
                for tb in range(4):
                    fm_unit(cb, fc, tb * 512, 512, "silu", 0, S["sgsT"][fc * 128:(fc + 1) * 128, tb * 512:(tb + 1) * 512])
        else:
            for fc in range(4):
                r0 = (cb - 10) * 512 + fc * 128
                for tb in range(4):
                    fm_unit(cb, fc, tb * 512, 512, "sigm", 0, S["sgmT"][r0:r0 + 128, tb * 512:(tb + 1) * 512])


def phase_ssm(k):
    nc, P, I, S = k.nc, k.P, k.I, k.S
    MUL, ADD, SUB = ALU.mult, ALU.add, ALU.subtract
    with ExitStack() as ls:
        ToepT = k.sb("ss_toep", [128, 32, 128], BF16, ls)
        RCp = k.sb("ss_rcp", [128, 2, 2, 16, 256], BF16, ls)
        WT = k.sb("ss_wt", [128, 2, 16, 2, 128], BF16, ls)
        A8c = k.sb("ss_a8c", [128, 2, 16, 2], F32, ls)
        A8s = k.sb("ss_a8s", [128, 2, 16, 2], F32, ls)
        b_toep = [Buf(f"toep{g}") for g in range(32)]
        b_rcp, b_wt = Buf("rcp"), Buf("wt")
        b_U = [Buf(f"U{g}") for g in range(32)]
        b_zbf = [Buf("zbf0"), Buf("zbf1")]
        b_ygT = Buf("ygT")
        b_a8 = Buf("a8")
        pbk = [k.ps(f"ss_ps{i}", [128, 512], F32, ls) for i in range(8)]
        b_pbk = [Buf(f"ssps{i}") for i in range(8)]
        pc = {"i": 0}

        def nb():
            i = pc["i"] % 8
            pc["i"] += 1
            return pbk[i], b_pbk[i]

        csem = P.new_dsem("ss_c")
        with ExitStack() as l0:
            lre = k.sb("ss_lre", [128, 32], F32, l0)
            lim = k.sb("ss_lim", [128, 32], F32, l0)
            dtt = k.sb("ss_dt", [128, 32], F32, l0)
            alog = k.sb("ss_alog", [128, 32], F32, l0)
            th = k.sb("ss_th", [128, 32], F32, l0)
            b_l, b_dt, b_al = Buf("lrelim"), Buf("dtt"), Buf("alogth")
            for gp in range(2):
                for d in range(2):
                    P.dma("sp", lre[gp * 64:(gp + 1) * 64, d * 16:(d + 1) * 16], I["ssm_lre"][d, gp * 16:(gp + 1) * 16, :].rearrange("g p -> p g"),
                          writes=[b_l], sem=csem, allow_slow_non_contiguous=True)
                    P.dma("sp", lim[gp * 64:(gp + 1) * 64, d * 16:(d + 1) * 16], I["ssm_lim"][d, gp * 16:(gp + 1) * 16, :].rearrange("g p -> p g"),
                          writes=[b_l], sem=csem, allow_slow_non_contiguous=True)
                    P.dma("sp", dtt[gp * 64:(gp + 1) * 64, d * 16:(d + 1) * 16], I["ssm_ls"][d:d + 1, gp * 16:(gp + 1) * 16].broadcast_to([64, 16]),
                          writes=[b_dt], sem=csem)
            P.op("act", lambda e: e.activation(out=dtt[:], in_=dtt[:], func=AF.Exp), reads=[b_dt], writes=[b_dt])
            P.op("dve", lambda e: e.tensor_tensor(out=alog[:], in0=lre[:], in1=dtt[:], op=MUL), reads=[b_l, b_dt], writes=[b_al])
            P.op("dve", lambda e: e.scalar_tensor_tensor(out=th[:], in0=lim[:], scalar=1.0 / TWO_PI, in1=dtt[:], op0=MUL, op1=MUL),
                 reads=[b_l, b_dt], writes=[b_al])
            tabs = {}
            for nm, n in (("A", 9), ("B", 8)):
                tau = k.sb(f"ss_tau{nm}", [128, 32, n], F32, l0)
                ex = k.sb(f"ss_ex{nm}", [128, 32, n], F32, l0)
                yv = k.sb(f"ss_yv{nm}", [128, 32, n], F32, l0)
                sn = k.sb(f"ss_sn{nm}", [128, 32, n], F32, l0)
                cs = k.sb(f"ss_cs{nm}", [128, 32, n], F32, l0)
                b_tau, b_ex, b_yv, b_sn, b_cs = Buf("tau" + nm), Buf("ex" + nm), Buf("yv" + nm), Buf("sn" + nm), Buf("cs" + nm)
                P.dma("sp", tau[:], I["c_tau" + nm], writes=[b_tau], sem=csem)
                P.op("dve", lambda e, ex=ex, tau=tau, n=n: e.tensor_tensor(out=ex[:], in0=tau[:], in1=alog[:, :, None].broadcast_to([128, 32, n]), op=MUL),
                     reads=[b_tau, b_al], writes=[b_ex])
                P.op("act", lambda e, ex=ex: e.activation(out=ex[:], in_=ex[:], func=AF.Exp), reads=[b_ex], writes=[b_ex])
                P.op("dve", lambda e, yv=yv, tau=tau, n=n: e.tensor_tensor(out=yv[:], in0=tau[:], in1=th[:, :, None].broadcast_to([128, 32, n]), op=MUL),
                     reads=[b_tau, b_al], writes=[b_yv])
                fl = lambda t: t[:].rearrange("p a b -> p (a b)")
                range_sin(k, l0, fl(sn), fl(yv), [128, 32 * n], "ssr1" + nm, [b_yv], [b_sn])
                P.op("dve", lambda e, yv=yv: e.tensor_scalar(out=yv[:], in0=yv[:], scalar1=0.25, scalar2=None, op0=ADD), reads=[b_yv], writes=[b_yv])
                range_sin(k, l0, fl(cs), fl(yv), [128, 32 * n], "ssr2" + nm, [b_yv], [b_cs])
                P.op("dve", lambda e, cs=cs, ex=ex: e.tensor_tensor(out=cs[:], in0=cs[:], in1=ex[:], op=MUL), reads=[b_cs, b_ex], writes=[b_cs])
                P.op("dve", lambda e, sn=sn, ex=ex: e.tensor_tensor(out=sn[:], in0=sn[:], in1=ex[:], op=MUL), reads=[b_sn, b_ex], writes=[b_sn])
                tabs[nm] = (cs, sn, b_cs, b_sn)
            ARA, AIA, b_ARA, b_AIA = tabs["A"]
            ARB, AIB, b_ARB, b_AIB = tabs["B"]
            a1 = k.sb("ss_a1", [128, 2, 32], F32, l0)
            b_a1 = Buf("a1")
            for d in range(2):
                i8 = 8 if d == 0 else 0
                i1 = 1 if d == 0 else 7
                dsl = slice(d * 16, (d + 1) * 16)
                for ri in range(2):
                    P.op("dve", lambda e, d=d, ri=ri, i8=i8, dsl=dsl: e.tensor_copy(out=A8c[:, d, :, ri], in_=ARA[:, dsl, i8]), reads=[b_ARA], writes=[b_a8])
                P.op("dve", lambda e, d=d, i8=i8, dsl=dsl: e.tensor_scalar(out=A8s[:, d, :, 0], in0=AIA[:, dsl, i8], scalar1=-1.0, scalar2=None, op0=MUL),
                     reads=[b_AIA], writes=[b_a8])
                P.op("dve", lambda e, d=d, i8=i8, dsl=dsl: e.tensor_copy(out=A8s[:, d, :, 1], in_=AIA[:, dsl, i8]), reads=[b_AIA], writes=[b_a8])
                P.op("dve", lambda e, d=d, i1=i1, dsl=dsl: e.tensor_copy(out=a1[:, 0, dsl], in_=ARA[:, dsl, i1]), reads=[b_ARA], writes=[b_a1])
                P.op("dve", lambda e, d=d, i1=i1, dsl=dsl: e.tensor_copy(out=a1[:, 1, dsl], in_=AIA[:, dsl, i1]), reads=[b_AIA], writes=[b_a1])
            fz = k.sb("ss_fz", [128, 6, 32], F32, l0)
            b_fz = Buf("fz")
            P.op("dve", lambda e: e.tensor_tensor(out=fz[:, 0, :], in0=lre[:], in1=lre[:], op=MUL), reads=[b_l], writes=[b_fz])
            P.op("dve", lambda e: e.tensor_tensor(out=fz[:, 1, :], in0=lim[:], in1=lim[:], op=MUL), reads=[b_l], writes=[b_fz])
            P.op("dve", lambda e: e.tensor_tensor(out=fz[:, 0, :], in0=fz[:, 0, :], in1=fz[:, 1, :], op=ADD), reads=[b_fz], writes=[b_fz])
            P.op("dve", lambda e: e.reciprocal(out=fz[:, 1, :], in_=fz[:, 0, :]), reads=[b_fz], writes=[b_fz])
            P.op("dve", lambda e: e.tensor_scalar(out=fz[:, 0, :], in0=a1[:, 0, :], scalar1=-1.0, scalar2=None, op0=ADD), reads=[b_a1], writes=[b_fz])
            P.op("dve", lambda e: e.tensor_tensor(out=fz[:, 2, :], in0=fz[:, 0, :], in1=lre[:], op=MUL), reads=[b_fz, b_l], writes=[b_fz])
            P.op("dve", lambda e: e.tensor_tensor(out=fz[:, 3, :], in0=a1[:, 1, :], in1=lim[:], op=MUL), reads=[b_a1, b_l], writes=[b_fz])
            P.op("dve", lambda e: e.tensor_tensor(out=fz[:, 2, :], in0=fz[:, 2, :], in1=fz[:, 3, :], op=ADD), reads=[b_fz], writes=[b_fz])
            P.op("dve", lambda e: e.tensor_tensor(out=fz[:, 2, :], in0=fz[:, 2, :], in1=fz[:, 1, :], op=MUL), reads=[b_fz], writes=[b_fz])
            P.op("dve", lambda e: e.tensor_tensor(out=fz[:, 4, :], in0=a1[:, 1, :], in1=lre[:], op=MUL), reads=[b_a1, b_l], writes=[b_fz])
            P.op("dve", lambda e: e.tensor_tensor(out=fz[:, 5, :], in0=fz[:, 0, :], in1=lim[:], op=MUL), reads=[b_fz, b_l], writes=[b_fz])
            P.op("dve", lambda e: e.tensor_tensor(out=fz[:, 4, :], in0=fz[:, 4, :], in1=fz[:, 5, :], op=SUB), reads=[b_fz], writes=[b_fz])
            P.op("dve", lambda e: e.tensor_tensor(out=fz[:, 4, :], in0=fz[:, 4, :], in1=fz[:, 1, :], op=MUL), reads=[b_fz], writes=[b_fz])
            BT = k.sb("ss_BT", [128, 2, 2, 16, 16], F32, l0)
            BB = k.sb("ss_BB", [128, 2, 2, 16, 16], F32, l0)
            CN = k.sb("ss_CN", [128, 2, 2, 2, 128], F32, l0)
            CT = k.sb("ss_CT", [128, 2, 2, 16, 16], F32, l0)
            tA = k.sb("ss_tA", [128, 16, 9, 16], F32, l0)
            tB = k.sb("ss_tB", [128, 16, 9, 16], F32, l0)
            b_BT, b_BB, b_CN, b_CT, b_tA, b_tB = Buf("BT"), Buf("BB"), Buf("CN"), Buf("CT"), Buf("tA"), Buf("tB")
            for d in range(2):
                for ri in range(2):
                    bsrc = I["ssm_bre"] if ri == 0 else I["ssm_bim"]
                    csrc = I["ssm_cre"] if ri == 0 else I["ssm_cim"]
                    for gp in range(2):
                        P.dma("sp", BT[gp * 64:(gp + 1) * 64, d, ri, :, :], bsrc[d, gp * 16:(gp + 1) * 16].rearrange("g p c -> p g c"),
                              writes=[b_BT], sem=csem)
                        for blk in range(2):
                            g0 = gp * 16 + blk * 8
                            P.dma("sp", CN[:, d, ri, blk, gp * 64:(gp + 1) * 64], csrc[d, g0:g0 + 8].rearrange("g c p -> (g c) p"),
                                  writes=[b_CN], sem=csem)
            for d in range(2):
                for ri in range(2):
                    for blk in range(2):
                        pt, bp = nb()
                        P.op("pe", lambda e, d=d, ri=ri, blk=blk, pt=pt: e.transpose(out=pt[:, 0:128], in_=CN[:, d, ri, blk, :], identity=k.ident[:]),
                             reads=[b_CN, k.b_ident], writes=[bp])
                        P.op("dve", lambda e, d=d, ri=ri, blk=blk, pt=pt: e.tensor_copy(
                            out=CT[:, d, ri, blk * 8:(blk + 1) * 8, :].rearrange("p a b -> p (a b)"), in_=pt[:, 0:128]), reads=[bp], writes=[b_CT])
            for d in range(2):
                dsl = slice(d * 16, (d + 1) * 16)
                frb = lambda d=d, dsl=dsl: fz[:, 2, dsl][:, :, None].broadcast_to([128, 16, 16])
                fib = lambda d=d, dsl=dsl: fz[:, 4, dsl][:, :, None].broadcast_to([128, 16, 16])
                t16a = tA[:, :, 0, :]
                t16b = tB[:, :, 0, :]
                P.op("dve", lambda e, d=d, frb=frb: e.tensor_tensor(out=t16a, in0=BT[:, d, 0], in1=frb(), op=MUL), reads=[b_BT, b_fz], writes=[b_tA])
                P.op("dve", lambda e, d=d, fib=fib: e.tensor_tensor(out=t16b, in0=BT[:, d, 1], in1=fib(), op=MUL), reads=[b_BT, b_fz], writes=[b_tB])
                P.op("dve", lambda e, d=d: e.tensor_tensor(out=BB[:, d, 0], in0=t16a, in1=t16b, op=SUB), reads=[b_tA, b_tB], writes=[b_BB])
                P.op("dve", lambda e, d=d, frb=frb: e.tensor_tensor(out=t16a, in0=BT[:, d, 1], in1=frb(), op=MUL), reads=[b_BT, b_fz], writes=[b_tA])
                P.op("dve", lambda e, d=d, fib=fib: e.tensor_tensor(out=t16b, in0=BT[:, d, 0], in1=fib(), op=MUL), reads=[b_BT, b_fz], writes=[b_tB])
                P.op("dve", lambda e, d=d: e.tensor_tensor(out=BB[:, d, 1], in0=t16a, in1=t16b, op=ADD), reads=[b_tA, b_tB], writes=[b_BB])
            P.op("pool", lambda e: e.memset(RCp[:].rearrange("p a b c d -> p (a b c d)"), 0.0), writes=[b_rcp])
            for d in range(2):
                dsl = slice(d * 16, (d + 1) * 16)
                off = 112 if d == 0 else 0
                bc_c = lambda ri, d=d: CT[:, d, ri][:, :, None, :].broadcast_to([128, 16, 9, 16])
                bc_ar = lambda dsl=dsl: ARA[:, dsl, :][:, :, :, None].broadcast_to([128, 16, 9, 16])
                bc_ai = lambda dsl=dsl: AIA[:, dsl, :][:, :, :, None].broadcast_to([128, 16, 9, 16])
                dst = lambda ri, d=d, off=off: RCp[:, d, ri, :, off:off + 144].rearrange("p g (t c) -> p g t c", c=16)
                P.op("dve", lambda e, bc_c=bc_c, bc_ar=bc_ar: e.tensor_tensor(out=tA[:], in0=bc_c(0), in1=bc_ar(), op=MUL), reads=[b_CT, b_ARA], writes=[b_tA])
                P.op("dve", lambda e, bc_c=bc_c, bc_ai=bc_ai: e.tensor_tensor(out=tB[:], in0=bc_c(1), in1=bc_ai(), op=MUL), reads=[b_CT, b_AIA], writes=[b_tB])
                P.op("dve", lambda e, dst=dst: e.tensor_tensor(out=dst(0), in0=tA[:], in1=tB[:], op=SUB), reads=[b_tA, b_tB], writes=[b_rcp])
                P.op("dve", lambda e, bc_c=bc_c, bc_ai=bc_ai: e.tensor_tensor(out=tA[:], in0=bc_c(0), in1=bc_ai(), op=MUL), reads=[b_CT, b_AIA], writes=[b_tA])
                P.op("dve", lambda e, bc_c=bc_c, bc_ar=bc_ar: e.tensor_tensor(out=tB[:], in0=bc_c(1), in1=bc_ar(), op=MUL), reads=[b_CT, b_ARA], writes=[b_tB])
                P.op("dve", lambda e: e.tensor_tensor(out=tA[:], in0=tA[:], in1=tB[:], op=ADD), reads=[b_tA, b_tB], writes=[b_tA])
                P.op("dve", lambda e, dst=dst: e.tensor_scalar(out=dst(1), in0=tA[:], scalar1=-1.0, scalar2=None, op0=MUL), reads=[b_tA], writes=[b_rcp])
            Lp = k.sb("ss_Lp", [128, 64, 240], BF16, l0)
            b_Lp = Buf("Lp")
            P.op("pool", lambda e: e.memset(Lp[:].rearrange("p a b -> p (a b)"), 0.0), writes=[b_Lp])
            P.op("pool", lambda e: e.tensor_copy(out=Lp[:, :, 112:128], in_=BB[:].rearrange("p a b c d -> p (a b c) d")), reads=[b_BB], writes=[b_Lp])
            BW = k.sb("ss_BW", [128, 2, 2, 16, 128], F32, l0)
            b_BW = Buf("BW")
            for d in range(2):
                dsl = slice(d * 16, (d + 1) * 16)
                bc_b = lambda ri, d=d: BB[:, d, ri][:, :, None, :].broadcast_to([128, 16, 8, 16])
                bc_ar = lambda dsl=dsl: ARB[:, dsl, :][:, :, :, None].broadcast_to([128, 16, 8, 16])
                bc_ai = lambda dsl=dsl: AIB[:, dsl, :][:, :, :, None].broadcast_to([128, 16, 8, 16])
                dst = lambda ri, d=d: BW[:, d, ri].rearrange("p g (t c) -> p g t c", c=16)
                ta8 = tA[:, :, 0:8, :]
                tb8 = tB[:, :, 0:8, :]
                P.op("dve", lambda e, bc_b=bc_b, bc_ar=bc_ar: e.tensor_tensor(out=ta8, in0=bc_b(0), in1=bc_ar(), op=MUL), reads=[b_BB, b_ARB], writes=[b_tA])
                P.op("dve", lambda e, bc_b=bc_b, bc_ai=bc_ai: e.tensor_tensor(out=tb8, in0=bc_b(1), in1=bc_ai(), op=MUL), reads=[b_BB, b_AIB], writes=[b_tB])
                P.op("dve", lambda e, dst=dst: e.tensor_tensor(out=dst(0), in0=ta8, in1=tb8, op=SUB), reads=[b_tA, b_tB], writes=[b_BW])
                P.op("dve", lambda e, bc_b=bc_b, bc_ai=bc_ai: e.tensor_tensor(out=ta8, in0=bc_b(0), in1=bc_ai(), op=MUL), reads=[b_BB, b_AIB], writes=[b_tA])
                P.op("dve", lambda e, bc_b=bc_b, bc_ar=bc_ar: e.tensor_tensor(out=tb8, in0=bc_b(1), in1=bc_ar(), op=MUL), reads=[b_BB, b_ARB], writes=[b_tB])
                P.op("dve", lambda e, dst=dst: e.tensor_tensor(out=dst(1), in0=ta8, in1=tb8, op=ADD), reads=[b_tA, b_tB], writes=[b_BW])
            for d in range(2):
                for g2 in range(16):
                    for ri in range(2):
                        pt, bp = nb()
                        P.op("pe", lambda e, d=d, g2=g2, ri=ri, pt=pt: e.transpose(out=pt[:, 0:128], in_=BW[:, d, ri, g2, :], identity=k.ident[:]),
                             reads=[b_BW, k.b_ident], writes=[bp])
                        eng = "act" if (g2 + ri) % 2 else "dve"
                        if eng == "act":
                            P.op("act", lambda e, d=d, g2=g2, ri=ri, pt=pt: e.activation(out=WT[:, d, g2, ri, :], in_=pt[:, 0:128], func=AF.Copy), reads=[bp], writes=[b_wt])
                        else:
                            P.op("dve", lambda e, d=d, g2=g2, ri=ri, pt=pt: e.tensor_copy(out=WT[:, d, g2, ri, :], in_=pt[:, 0:128]), reads=[bp], writes=[b_wt])
            for g2 in range(16):
                for gp in range(2):
                    g = gp * 16 + g2
                    pt, bp = nb()
                    psl = slice(gp * 64, (gp + 1) * 64)
                    n = 0
                    for d in range(2):
                        for ri in range(2):
                            for s_ in range(8):
                                w0 = (7 - s_) * 16 if d == 0 else (8 - s_) * 16
                                l0_ = (7 - s_) * 16
                                P.op("pe", lambda e, d=d, ri=ri, g2=g2, w0=w0, l0_=l0_, psl=psl, pt=pt, n=n: e.matmul(
                                    pt[:, 0:128], lhsT=Lp[psl, (d * 2 + ri) * 16 + g2, l0_:l0_ + 128], rhs=RCp[psl, d, ri, g2, w0:w0 + 128],
                                    start=(n == 0), stop=(n == 31)), reads=[b_Lp, b_rcp], writes=[bp])
                                n += 1
                    P.op("dve" if g % 2 else "act",
                         (lambda e, g=g, pt=pt: e.tensor_copy(out=ToepT[:, g, :], in_=pt[:, 0:128])) if g % 2 else
                         (lambda e, g=g, pt=pt: e.activation(out=ToepT[:, g, :], in_=pt[:, 0:128], func=AF.Copy)),
                         reads=[bp], writes=[b_toep[g]])
            P.barrier()
        Ubuf = k.sb("ss_ubuf", [128, 32, 320], BF16, ls)
        Zbf = k.sb("ss_zbf", [128, 2, 16, 2, 288], BF16, ls)
        ygT = k.sb("ss_ygT", [128, 4, L], BF16, ls)
        if k.debug.get("_ssm_upto", 99) < 1:
            return
        with ExitStack() as l1:
            ucm = [k.sb(f"ss_ucm{i}", [128, 8, 512], F32, l1) for i in range(2)]
            b_ucm = [Buf(f"ucm{i}") for i in range(2)]
            usem = [P.new_dsem(f"ss_us{i}") for i in range(2)]
            ucg = k.sb("ss_ucg", [128, 32, 128], F32, l1)
            b_ucg = Buf("ucg")
            for jt in range(3):
                si = jt % 2
                nj = 128 if jt < 2 else 32
                r0 = jt * 1024
                P.dma("sp", ucm[si][0:nj], S["u"][r0:r0 + nj * 8, :].rearrange("(j s) c -> j s c", s=8), writes=[b_ucm[si]], sem=usem[si])
                P.op("dve", lambda e, si=si, nj=nj: e.tensor_copy(out=ucg[0:nj].rearrange("p g (s c) -> p g s c", c=16),
                                                                 in_=ucm[si][0:nj].rearrange("p s (g c) -> p g s c", c=16)),
                     reads=[b_ucm[si]], writes=[b_ucg])
                for g0 in range(0, 32, 4):
                    pt, bp = nb()
                    for gg in range(4):
                        g = g0 + gg
                        P.op("pe", lambda e, si=si, nj=nj, g=g, gg=gg, pt=pt: e.transpose(
                            out=pt[:, gg * 128:gg * 128 + nj], in_=ucg[0:nj, g, :], identity=k.ident[0:nj, 0:nj]),
                            reads=[b_ucg, k.b_ident], writes=[bp])
                    src = lambda pt=pt, nj=nj: pt[:].rearrange("p (a b) -> p a b", b=128)[:, :, 0:nj]
                    cols = [32 + jt * 128] if jt < 2 else [0, 288]
                    for ci, c0 in enumerate(cols):
                        eng = "act" if (g0 // 4 + ci) % 2 else "dve"
                        if eng == "act":
                            P.op("act", lambda e, g0=g0, c0=c0, nj=nj, src=src: e.activation(out=Ubuf[:, g0:g0 + 4, c0:c0 + nj], in_=src(), func=AF.Copy),
                                 reads=[bp], writes=[b_U[g0 + i] for i in range(4)])
                        else:
                            P.op("dve", lambda e, g0=g0, c0=c0, nj=nj, src=src: e.tensor_copy(out=Ubuf[:, g0:g0 + 4, c0:c0 + nj], in_=src()),
                                 reads=[bp], writes=[b_U[g0 + i] for i in range(4)])
            P.barrier()
        if k.debug.get("_ssm_upto", 99) < 2:
            return
        with ExitStack() as l2:
            Z = [k.sb(f"ss_Z{d}", [128, 16, 2, 288], F32, l2) for d in range(2)]
            b_Z = [Buf("Z0"), Buf("Z1")]
            for d in range(2):
                j0 = 0 if d == 0 else 32
                for g2 in range(16):
                    for ri in range(2):
                        pt, bp = nb()
                        for gp in range(2):
                            P.op("pe", lambda e, d=d, g2=g2, ri=ri, gp=gp, pt=pt, j0=j0: e.matmul(
                                pt[gp * 64:(gp + 1) * 64, 0:288], lhsT=WT[:, d, g2, ri, gp * 64:(gp + 1) * 64], rhs=Ubuf[:, gp * 16 + g2, j0:j0 + 288],
                                start=True, stop=True), reads=[b_wt, b_U[gp * 16 + g2]], writes=[bp])
                        if (g2 + ri) % 2:
                            P.op("act", lambda e, d=d, g2=g2, ri=ri, pt=pt: e.activation(out=Z[d][:, g2, ri, :], in_=pt[:, 0:288], func=AF.Copy), reads=[bp], writes=[b_Z[d]])
                        else:
                            P.op("dve", lambda e, d=d, g2=g2, ri=ri, pt=pt: e.tensor_copy(out=Z[d][:, g2, ri, :], in_=pt[:, 0:288]), reads=[bp], writes=[b_Z[d]])
            k.dbg("V_dbg", [2, 128, 16 * 2 * 288], F32, lambda dd: (dd[0], Z[0][:].rearrange("p a b c -> p (a b c)")), [b_Z[0]])
            k.dbg("V_dbg", [2, 128, 16 * 2 * 288], F32, lambda dd: (dd[1], Z[1][:].rearrange("p a b c -> p (a b c)")), [b_Z[1]])
            m1 = [k.sb(f"ss_m1{d}", [128, 16, 2], F32, l2) for d in range(2)]
            m2 = [k.sb(f"ss_m2{d}", [128, 16, 2], F32, l2) for d in range(2)]
            b_m1 = [Buf("m10"), Buf("m11")]
            b_m2 = [Buf("m20"), Buf("m21")]

            def scan_step(d, J, Jp):
                eng = "dve" if d == 0 else "pool"
                P.op(eng, lambda e: e.tensor_tensor(out=m1[d][:], in0=Z[d][:, :, :, Jp], in1=A8c[:, d], op=MUL), reads=[b_Z[d], b_a8], writes=[b_m1[d]])
                P.op(eng, lambda e: e.tensor_tensor(out=m2[d][:], in0=Z[d][:, :, ::-1, Jp], in1=A8s[:, d], op=MUL), reads=[b_Z[d], b_a8], writes=[b_m2[d]])
                P.op(eng, lambda e: e.tensor_tensor(out=m1[d][:], in0=m1[d][:], in1=m2[d][:], op=ADD), reads=[b_m1[d], b_m2[d]], writes=[b_m1[d]])
                P.op(eng, lambda e: e.tensor_tensor(out=Z[d][:, :, :, J], in0=Z[d][:, :, :, J], in1=m1[d][:], op=ADD), reads=[b_Z[d], b_m1[d]], writes=[b_Z[d]])

            for st_ in range(1, 288):
                scan_step(0, st_, st_ - 1)
                scan_step(1, 287 - st_, 288 - st_)
            for d in range(2):
                eng = "dve" if d == 0 else "pool"
                P.op(eng, lambda e, d=d: e.tensor_copy(out=Zbf[:, d].rearrange("p a b c -> p (a b c)"), in_=Z[d][:].rearrange("p a b c -> p (a b c)")),
                     reads=[b_Z[d]], writes=[b_zbf[d]])
            k.dbg("Z_dbg", [2, 128, 16 * 2 * 288], F32, lambda dd: (dd[0], Z[0][:].rearrange("p a b c -> p (a b c)")), [b_Z[0]])
            k.dbg("Z_dbg", [2, 128, 16 * 2 * 288], F32, lambda dd: (dd[1], Z[1][:].rearrange("p a b c -> p (a b c)")), [b_Z[1]])
            P.barrier()
        if k.debug.get("_ssm_upto", 99) < 3:
            return
        with ExitStack() as l3:
            ycm = k.sb("ss_ycm", [128, 8, 512], F32, l3)
            b_ycm = [Buf(f"ycm{g}") for g in range(32)]
            ut = k.sb("ss_ut", [128, 8, 512], F32, l3)
            b_ut = Buf("ut")
            utsem = P.new_dsem("ss_uts")
            Dfull = k.sb("ss_D", [128, 512], F32, l3)
            b_D = Buf("Dfull")
            P.dma("sp", Dfull[:], I["ssm_d"][0:1, :].broadcast_to([128, 512]), writes=[b_D], sem=csem)
            sq = [k.sb(f"ss_sq{i}", [128, 512], F32, l3) for i in range(2)]
            b_sq = [Buf("sq0"), Buf("sq1")]
            GC = math.sqrt(2.0 / math.pi)
            for jt in range(2):
                P.dma("sp", ut[:], S["u"][jt * 1024:(jt + 1) * 1024, :].rearrange("(j s) c -> j s c", s=8), writes=[b_ut], sem=utsem)
                for g in range(32):
                    gp, g2 = g // 16, g % 16
                    psl = slice(gp * 64, (gp + 1) * 64)
                    pt, bp = nb()
                    c0 = 32 + jt * 128
                    P.op("pe", lambda e, g=g, c0=c0, pt=pt: e.matmul(pt[:, 0:128], lhsT=Ubuf[:, g, c0:c0 + 128], rhs=ToepT[:, g, :], start=True, stop=False),
                         reads=[b_U[g], b_toep[g]], writes=[bp])
                    for d in range(2):
                        jz = (31 + jt * 128) if d == 0 else (1 + jt * 128)
                        w0 = 128 if d == 0 else 0
                        for ri in range(2):
                            last = (d == 1 and ri == 1)
                            P.op("pe", lambda e, d=d, ri=ri, g2=g2, psl=psl, jz=jz, w0=w0, pt=pt, last=last: e.matmul(
                                pt[:, 0:128], lhsT=Zbf[psl, d, g2, ri, jz:jz + 128], rhs=RCp[psl, d, ri, g2, w0:w0 + 128], start=False, stop=last),
                                reads=[b_zbf[d], b_rcp], writes=[bp])
                    src = lambda pt=pt: pt[:, 0:128].rearrange("p (t c) -> p t c", c=16)
                    P.op("dve", lambda e, g=g, src=src: e.tensor_tensor(out=ycm[:, :, g * 16:(g + 1) * 16], in0=ut[:, :, g * 16:(g + 1) * 16],
                                                                       in1=Dfull[:, g * 16:(g + 1) * 16][:, None, :].broadcast_to([128, 8, 16]), op=MUL),
                         reads=[b_ut, b_D], writes=[b_ycm[g]])
                    P.op("dve", lambda e, g=g, src=src: e.tensor_tensor(out=ycm[:, :, g * 16:(g + 1) * 16], in0=ycm[:, :, g * 16:(g + 1) * 16], in1=src(), op=ADD),
                         reads=[bp, b_ycm[g]], writes=[b_ycm[g]])
                k.dbg("y_dbg", [L, 512], F32, lambda dd, jt=jt: (dd[jt * 1024:(jt + 1) * 1024, :].rearrange("(j s) c -> j s c", s=8), ycm[:]), b_ycm)
                for t in range(8):
                    i = t % 2
                    P.op("dve", lambda e, t=t, i=i: e.tensor_tensor(out=sq[i][:], in0=ycm[:, t, :], in1=ycm[:, t, :], op=MUL), reads=b_ycm, writes=[b_sq[i]])
                    P.op("dve", lambda e, t=t, i=i: e.tensor_scalar(out=sq[i][:], in0=sq[i][:], scalar1=0.044715, scalar2=1.0, op0=MUL, op1=ADD), reads=[b_sq[i]], writes=[b_sq[i]])
                    P.op("dve", lambda e, t=t, i=i: e.tensor_tensor(out=sq[i][:], in0=sq[i][:], in1=ycm[:, t, :], op=MUL), reads=[b_sq[i]] + b_ycm, writes=[b_sq[i]])
                    P.op("act", lambda e, t=t, i=i: e.activation(out=sq[i][:], in_=sq[i][:], func=AF.Sigmoid, scale=2.0 * GC), reads=[b_sq[i]], writes=[b_sq[i]])
                    P.op("dve", lambda e, t=t, i=i: e.tensor_tensor(out=sq[i][:], in0=sq[i][:], in1=ycm[:, t, :], op=MUL), reads=[b_sq[i]] + b_ycm, writes=[b_sq[i]])
                    pt, bp = nb()
                    for chb in range(4):
                        P.op("pe", lambda e, i=i, chb=chb, pt=pt: e.transpose(out=pt[:, chb * 128:(chb + 1) * 128], in_=sq[i][:, chb * 128:(chb + 1) * 128], identity=k.ident[:]),
                             reads=[b_sq[i], k.b_ident], writes=[bp])
                    tsl = slice(jt * 1024 + t, (jt + 1) * 1024, 8)
                    P.op("act", lambda e, pt=pt, tsl=tsl: e.activation(out=ygT[:, :, tsl], in_=pt[:].rearrange("p (a b) -> p a b", b=128), func=AF.Copy),
                         reads=[bp], writes=[b_ygT])
            P.barrier()
        if k.debug.get("_ssm_upto", 99) < 4:
            return
        with ExitStack() as l4:
            wg32 = k.sb("ss_wg32", [128, 4, 512], F32, l4)
            wg = k.sb("ss_wg", [128, 4, 512], BF16, l4)
            bg = k.sb("ss_bg", [128, 4], F32, l4)
            b_wg32, b_wg, b_bg = Buf("wg32"), Buf("wg"), Buf("bg")
            P.dma("sp", wg32[:], I["w_glu"].rearrange("(fc p) c -> p fc c", p=128), writes=[b_wg32], sem=csem)
            P.dma("sp", bg[:], I["b_glu"].rearrange("(fc p) -> p fc", p=128), writes=[b_bg], sem=csem, allow_slow_non_contiguous=True)
            P.op("dve", lambda e: e.tensor_copy(out=wg[:], in_=wg32[:]), reads=[b_wg32], writes=[b_wg])
            gst = [k.sb(f"ss_gst{i}", [128, 512], BF16, l4) for i in range(2)]
            b_gst = [Buf("gst0"), Buf("gst1")]
            gsem = [P.new_dsem(f"ss_gs{i}") for i in range(2)]
            sg = [k.sb(f"ss_sg{i}", [128, 512], F32, l4) for i in range(2)]
            b_sg = [Buf("sg0"), Buf("sg1")]
            so = [k.sb(f"ss_so{i}", [128, 512], BF16, l4) for i in range(2)]
            b_so = [Buf("so0"), Buf("so1")]
            sosem = [P.new_dsem(f"ss_sos{i}") for i in range(2)]
            ui = 0
            for fo in range(4):
                for tb in range(4):
                    i = ui % 2
                    ui += 1
                    tsl = slice(tb * 512, (tb + 1) * 512)
                    P.dma("sp", gst[i][:], S["sgsT"][fo * 128:(fo + 1) * 128, tsl], writes=[b_gst[i]], sem=gsem[i])
                    pt, bp = nb()
                    for fc in range(4):
                        P.op("pe", lambda e, fc=fc, fo=fo, tsl=tsl, pt=pt: e.matmul(pt[:], lhsT=wg[:, fc, fo * 128:(fo + 1) * 128], rhs=ygT[:, fc, tsl],
                                                                               start=(fc == 0), stop=(fc == 3)), reads=[b_wg, b_ygT], writes=[bp])
                    P.op("act", lambda e, i=i, fo=fo, pt=pt: e.activation(out=sg[i][:], in_=pt[:], func=AF.Sigmoid, bias=bg[:, fo:fo + 1]),
                         reads=[bp, b_bg], writes=[b_sg[i]])
                    P.op("dve", lambda e, i=i, fo=fo, tsl=tsl: e.tensor_tensor(out=sg[i][:], in0=sg[i][:], in1=ygT[:, fo, tsl], op=MUL),
                         reads=[b_sg[i], b_ygT], writes=[b_sg[i]])
                    P.op("dve", lambda e, i=i: e.tensor_tensor(out=so[i][:], in0=sg[i][:], in1=gst[i][:], op=MUL),
                         reads=[b_sg[i], b_gst[i]], writes=[b_so[i]])
                    P.dma("sp", S["sbrT"][fo * 128:(fo + 1) * 128, tsl], so[i][:], reads=[b_so[i]], sem=sosem[i])


def phase_attn(k):
    nc, P, I, S = k.nc, k.P, k.I, k.S
    with ExitStack() as ls:
        lamv = k.sb("at_lamv", [128, 4, 64], F32, ls)
        lw = k.sb("at_lw", [128, 8], F32, ls)
        G = k.sb("at_G", [128, 128], F32, ls)
        b_lamv, b_lw, b_G = Buf("lamv"), Buf("lw"), Buf("G")
        csem = P.new_dsem("at_c")
        P.dma("sp", lamv[:].rearrange("p a b -> p (a b)"), I["lam"].rearrange("a b -> (a b)").partition_broadcast(128),
              writes=[b_lamv], sem=csem)
        P.dma("sp", G[:], I["subln_g"][0:1, :].broadcast_to([128, 128]), writes=[b_G], sem=csem)
        P.op("dve", lambda e: e.tensor_scalar(out=G[:], in0=G[:], scalar1=(1.0 - LAM_INIT), scalar2=None, op0=ALU.mult),
             reads=[b_G], writes=[b_G])
        for i in range(2):
            P.op("dve", lambda e, i=i: e.tensor_tensor(out=lamv[:, 2 * i, :], in0=lamv[:, 2 * i, :], in1=lamv[:, 2 * i + 1, :], op=ALU.mult),
                 reads=[b_lamv], writes=[b_lamv])
            P.op("dve", lambda e, i=i: e.tensor_reduce(out=lw[:, i:i + 1], in_=lamv[:, 2 * i, :], axis=mybir.AxisListType.X, op=ALU.add),
                 reads=[b_lamv], writes=[b_lw])
        P.op("act", lambda e: e.activation(out=lw[:, 2:4], in_=lw[:, 0:2], func=AF.Exp), reads=[b_lw], writes=[b_lw])
        P.op("dve", lambda e: e.tensor_tensor(out=lw[:, 4:5], in0=lw[:, 3:4], in1=lw[:, 2:3], op=ALU.subtract), reads=[b_lw], writes=[b_lw])
        P.op("dve", lambda e: e.tensor_scalar(out=lw[:, 5:6], in0=lw[:, 4:5], scalar1=-LAM_INIT, scalar2=None, op0=ALU.add),
             reads=[b_lw], writes=[b_lw])
        neglam = lw[:, 5:6]
        qTs = [k.sb(f"at_q{i}", [128, L], BF16, ls) for i in range(2)]
        kTs = [k.sb(f"at_k{i}", [128, LT], BF16, ls) for i in range(2)]
        Vs = [k.sb(f"at_v{i}", [128, 18, 130], BF16, ls) for i in range(2)]
        gas = [k.sb(f"at_ga{i}", [128, 16, 128], BF16, ls) for i in range(2)]
        aTs = [k.sb(f"at_aT{i}", [128, L], BF16, ls) for i in range(2)]
        b_q = [Buf(f"atq{i}") for i in range(2)]
        b_k = [Buf(f"atk{i}") for i in range(2)]
        b_v = [Buf(f"atv{i}") for i in range(2)]
        b_ga = [Buf(f"atga{i}") for i in range(2)]
        b_aT = [Buf(f"ataT{i}") for i in range(2)]
        hsem = [P.new_dsem(f"at_h{i}") for i in range(2)]
        asem = [P.new_dsem(f"at_a{i}") for i in range(2)]
        for i in range(2):
            P.op("pool", lambda e, i=i: e.memset(Vs[i][:, :, 128:130], 1.0), writes=[b_v[i]])
        PT = [k.sb(f"at_pt{i}", [128, 2, 18, 256], BF16, ls) for i in range(2)]
        b_PT = [[[Buf(f"pt{i}_{c}_{kp}") for kp in range(9)] for c in range(2)] for i in range(2)]
        sbk = [k.ps(f"at_s{i}", [128, 512], F32, ls) for i in range(3)]
        b_sbk = [Buf(f"ats{i}") for i in range(3)]
        obk = [k.ps(f"at_o{i}", [128, 512], F32, ls) for i in range(4)]
        b_obk = [Buf(f"ato{i}") for i in range(4)]
        tbk = k.ps("at_t", [128, 512], F32, ls)
        b_tbk = Buf("att")
        sm = [k.sb(f"at_sm{i}", [128, 8], F32, ls) for i in range(2)]
        b_sm = [Buf(f"atsm{i}") for i in range(2)]
        tmp = [k.sb(f"at_tmp{i}", [128, 128], F32, ls) for i in range(2)]
        b_tmp = [Buf(f"attmp{i}") for i in range(2)]
        ov = [k.sb(f"at_ov{i}", [128, 128], F32, ls) for i in range(2)]
        b_ov = [Buf(f"atov{i}") for i in range(2)]
        junk = k.sb("at_junk", [128, 128], F32, ls)
        b_junk = Buf("atjunk")
        cnt = {"s": 0, "u": 0}

        def load_head(h):
            s = h % 2
            P.dma("sp", qTs[s][:], S["qT"][h], writes=[b_q[s]], sem=hsem[s])
            P.dma("sp", kTs[s][:], S["kT"][h], writes=[b_k[s]], sem=hsem[s])
            P.dma("sp", Vs[s][:, :, 0:128], S["v"][:, h * 128:(h + 1) * 128].rearrange("(t p) e -> p t e", p=128),
                  writes=[b_v[s]], sem=hsem[s])
            P.dma("sp", gas[s][:], S["sga"][:, h * 128:(h + 1) * 128].rearrange("(t p) e -> p t e", p=128),
                  writes=[b_ga[s]], sem=hsem[s])

        def phaseA(h, qb):
            s = h % 2
            ps_ = qb % 2
            for kp in range(9):
                for c in range(2):
                    si = cnt["s"] % 3
                    cnt["s"] += 1
                    for j in range(2):
                        kt = 2 * kp + j
                        P.op("pe", lambda e, kt=kt, j=j, c=c, si=si: e.matmul(
                            sbk[si][:, j * 256:(j + 1) * 256], lhsT=kTs[s][c * 64:(c + 1) * 64, kt * 128:(kt + 1) * 128],
                            rhs=qTs[s][c * 64:(c + 1) * 64, qb * 256:(qb + 1) * 256], start=True, stop=True),
                            reads=[b_k[s], b_q[s]], writes=[b_sbk[si]])
                    P.op("act", lambda e, c=c, kp=kp, si=si: e.activation(
                        out=PT[ps_][:, c, 2 * kp:2 * kp + 2, :].rearrange("p a b -> p (a b)"), in_=sbk[si][:], func=AF.Exp, scale=0.125),
                        reads=[b_sbk[si]], writes=[b_PT[ps_][c][kp]])

        def phaseB(h, qb):
            s = h % 2
            ps_ = qb % 2
            for qi_ in range(2):
                unitB(h, qb, qi_, s, ps_)

        def unitB(h, qb, qi, s, ps_):
            if True:
                qt = qb * 2 + qi
                u = cnt["u"] % 2
                cnt["u"] += 1
                banks = [obk[u * 2], obk[u * 2 + 1]]
                bb = [b_obk[u * 2], b_obk[u * 2 + 1]]
                for c in range(2):
                    for kt in range(18):
                        P.op("pe", lambda e, c=c, kt=kt: e.matmul(
                            banks[c][:, 0:129], lhsT=PT[ps_][:, c, kt, qi * 128:(qi + 1) * 128], rhs=Vs[s][:, kt, 0:129],
                            start=(kt == 0), stop=(kt == 17)),
                            reads=[b_PT[ps_][c][kt // 2], b_v[s]], writes=[bb[c]])
                smt, bsm = sm[u], b_sm[u]
                for c in range(2):
                    P.op("dve", lambda e, c=c: e.reciprocal(out=smt[:, c:c + 1], in_=banks[c][:, 128:129]), reads=[bb[c]], writes=[bsm])
                P.op("dve", lambda e: e.tensor_tensor(out=smt[:, 2:3], in0=smt[:, 1:2], in1=neglam, op=ALU.mult), reads=[bsm, b_lw], writes=[bsm])
                P.op("dve", lambda e: e.tensor_scalar(out=tmp[u][:], in0=banks[1][:, 0:128], scalar1=smt[:, 2:3], scalar2=None, op0=ALU.mult),
                     reads=[bb[1], bsm], writes=[b_tmp[u]])
                P.op("dve", lambda e: e.scalar_tensor_tensor(out=ov[u][:], in0=banks[0][:, 0:128], scalar=smt[:, 0:1], in1=tmp[u][:],
                                                            op0=ALU.mult, op1=ALU.add),
                     reads=[bb[0], bsm, b_tmp[u]], writes=[b_ov[u]])
                if h == 0:
                    k.dbg("o_dbg", [L, 128], F32, lambda d, qt=qt, u=u: (d[qt * 128:(qt + 1) * 128, :], ov[u][:]), [b_ov[u]])
                    k.dbg("sm_dbg", [L, 8], F32, lambda d, qt=qt, u=u: (d[qt * 128:(qt + 1) * 128, :], sm[u][:]), [b_sm[u]])
                    if qt == 0:
                        k.dbg("pt_dbg", [128, 2 * 18 * 256], BF16, lambda d: (d, PT[ps_][:].rearrange("p a b c -> p (a b c)")),
                              [b for c_ in range(2) for b in b_PT[ps_][c_]])
                        k.dbg("lw_dbg", [128, 8], F32, lambda d: (d, lw[:]), [b_lw])
                P.op("dve", lambda e: e.tensor_tensor(out=tmp[u][:], in0=ov[u][:], in1=ov[u][:], op=ALU.mult),
                     reads=[b_ov[u]], writes=[b_tmp[u]])
                P.op("dve", lambda e: e.tensor_reduce(out=smt[:, 3:4], in_=tmp[u][:], axis=mybir.AxisListType.X, op=ALU.add),
                     reads=[b_tmp[u]], writes=[bsm])
                P.op("dve", lambda e: e.tensor_scalar(out=smt[:, 4:5], in0=smt[:, 3:4], scalar1=1.0 / 128, scalar2=EPS, op0=ALU.mult, op1=ALU.add),
                     reads=[bsm], writes=[bsm])
                P.op("act", lambda e: e.activation(out=smt[:, 5:6], in_=smt[:, 4:5], func=AF.Ln), reads=[bsm], writes=[bsm])
                P.op("act", lambda e: e.activation(out=smt[:, 6:7], in_=smt[:, 5:6], func=AF.Exp, scale=-0.5), reads=[bsm], writes=[bsm])
                P.op("dve", lambda e: e.scalar_tensor_tensor(out=ov[u][:], in0=ov[u][:], scalar=smt[:, 6:7], in1=G[:], op0=ALU.mult, op1=ALU.mult),
                     reads=[b_ov[u], bsm, b_G], writes=[b_ov[u]])
                P.op("pool", lambda e: e.tensor_tensor(out=ov[u][:], in0=ov[u][:], in1=gas[s][:, qt, :], op=ALU.mult),
                     reads=[b_ov[u], b_ga[s]], writes=[b_ov[u]])
                P.op("pe", lambda e: e.transpose(out=tbk[:, 0:128], in_=ov[u][:], identity=k.ident[:]), reads=[b_ov[u], k.b_ident], writes=[b_tbk])
                P.op("dve", lambda e: e.tensor_copy(out=aTs[s][:, qt * 128:(qt + 1) * 128], in_=tbk[:, 0:128]),
                     reads=[b_tbk], writes=[b_aT[s]])

        load_head(0)
        for h in range(HEADS):
            if h + 1 < HEADS:
                load_head(h + 1)
            phaseA(h, 0)
            for qb in range(8):
                if qb + 1 < 8:
                    phaseA(h, qb + 1)
                phaseB(h, qb)
            P.dma("pool", S["abrT"][h * 128:(h + 1) * 128, :], aTs[h % 2][:], reads=[b_aT[h % 2]], sem=asem[h % 2])


def phase_merge(k):
    nc, P, I, S = k.nc, k.P, k.I, k.S
    with ExitStack() as ls:
        mT = k.sb("mg_mT", [128, NKC, L], BF16, ls)
        b_mT = [Buf(f"mT{tb}") for tb in range(4)]
        wout = k.sb("mg_wout", [128, NKC, D], BF16, ls)
        b_wout = Buf("wout")
        NXB = 2
        wov = I["w_out"].rearrange("(kc p) c -> p kc c", p=128)
        wo_state = {"kc": 0}
        b_woutc = [Buf(f"woutc{i}") for i in range(NKC)]

        def load_wout_chunk():
            kc = wo_state["kc"]
            if kc >= NKC:
                return
            wo_state["kc"] += 1
            P.dma("pool", wout[:, kc, :], wov[:, kc, :], writes=[b_woutc[kc]])
        with ExitStack() as l1:
            abrT = k.sb("mg_abrT", [128, 8, L], BF16, l1)
            sbrT = k.sb("mg_sbrT", [128, 4, L], BF16, l1)
            b_abrT, b_sbrT = Buf("abrT"), Buf("sbrT")
            lsem = P.new_dsem("mg_l")
            P.dma("sp", abrT[:], S["abrT"].rearrange("(fc p) t -> p fc t", p=128), writes=[b_abrT], sem=lsem)
            P.dma("sp", sbrT[:], S["sbrT"].rearrange("(fc p) t -> p fc t", p=128), writes=[b_sbrT], sem=lsem)
            NWS = 2
            wstg = [k.sb(f"mg_wstg{i}", [128, 12, 128], F32, l1) for i in range(NWS)]
            b_wstg = [Buf(f"mgwstg{i}") for i in range(NWS)]
            wsem = [P.new_dsem(f"mg_ws{i}") for i in range(NWS)]
            wbf = [k.sb(f"mg_wbf{i}", [128, 12, 128], BF16, l1) for i in range(NWS)]
            b_wbf = [Buf(f"mgwbf{i}") for i in range(NWS)]
            NG = 2
            gt = [k.sb(f"mg_gt{i}", [128, 2, 512], BF16, l1) for i in range(NG)]
            b_gt = [Buf(f"mggt{i}") for i in range(NG)]
            gsem = [P.new_dsem(f"mg_gs{i}") for i in range(NG)]
            t1 = [k.sb(f"mg_t1{i}", [128, 512], F32, l1) for i in range(2)]
            t2 = [k.sb(f"mg_t2{i}", [128, 512], F32, l1) for i in range(2)]
            b_t1 = [Buf(f"mgt1{i}") for i in range(2)]
            b_t2 = [Buf(f"mgt2{i}") for i in range(2)]
            pa = [k.ps(f"mg_pa{i}", [128, 512], F32, l1) for i in range(2)]
            pp = [k.ps(f"mg_pp{i}", [128, 512], F32, l1) for i in range(2)]
            b_pa = [Buf(f"mgpa{i}") for i in range(2)]
            b_pp = [Buf(f"mgpp{i}") for i in range(2)]
            wpa_v = I["w_pa"].rearrange("(fc p) c -> p fc c", p=128)
            wps_v = I["w_ps"].rearrange("(fc p) c -> p fc c", p=128)
            ui = 0

            def load_w(fo):
                s = fo % NWS
                P.dma("sp", wstg[s][:, 0:8, :], wpa_v[:, :, fo * 128:(fo + 1) * 128], writes=[b_wstg[s]], sem=wsem[s])
                P.dma("sp", wstg[s][:, 8:12, :], wps_v[:, :, fo * 128:(fo + 1) * 128], writes=[b_wstg[s]], sem=wsem[s])
                P.op("pool", lambda e, s=s: e.tensor_copy(out=wbf[s][:], in_=wstg[s][:]), reads=[b_wstg[s]], writes=[b_wbf[s]])

            load_w(0)
            for fo in range(NKC):
                if fo + 1 < NKC:
                    load_w(fo + 1)
                load_wout_chunk()
                s = fo % NWS
                for tb in range(4):
                    gi = ui % NG
                    u2 = ui % 2
                    ui += 1
                    P.dma("sp", gt[gi][:, 0, :], S["sgmT"][fo * 128:(fo + 1) * 128, tb * 512:(tb + 1) * 512], writes=[b_gt[gi]], sem=gsem[gi])
                    P.dma("sp", gt[gi][:, 1, :], S["sgmT"][D + fo * 128:D + (fo + 1) * 128, tb * 512:(tb + 1) * 512], writes=[b_gt[gi]], sem=gsem[gi])
                    for fc in range(8):
                        P.op("pe", lambda e, fc=fc, s=s, tb=tb, u2=u2: e.matmul(pa[u2][:], lhsT=wbf[s][:, fc, :], rhs=abrT[:, fc, tb * 512:(tb + 1) * 512],
                                                                       start=(fc == 0), stop=(fc == 7)),
                             reads=[b_wbf[s], b_abrT], writes=[b_pa[u2]])
                    for fc in range(4):
                        P.op("pe", lambda e, fc=fc, s=s, tb=tb, u2=u2: e.matmul(pp[u2][:], lhsT=wbf[s][:, 8 + fc, :], rhs=sbrT[:, fc, tb * 512:(tb + 1) * 512],
                                                                       start=(fc == 0), stop=(fc == 3)),
                             reads=[b_wbf[s], b_sbrT], writes=[b_pp[u2]])
                    P.op("dve", lambda e, gi=gi, u2=u2: e.tensor_tensor(out=t1[u2][:], in0=pa[u2][:], in1=gt[gi][:, 0, :], op=ALU.mult),
                         reads=[b_pa[u2], b_gt[gi]], writes=[b_t1[u2]])
                    P.op("dve", lambda e, gi=gi, u2=u2: e.tensor_tensor(out=t2[u2][:], in0=pp[u2][:], in1=gt[gi][:, 1, :], op=ALU.mult),
                         reads=[b_pp[u2], b_gt[gi]], writes=[b_t2[u2]])
                    P.op("pool", lambda e, fo=fo, tb=tb, u2=u2: e.tensor_tensor(out=mT[:, fo, tb * 512:(tb + 1) * 512], in0=t1[u2][:], in1=t2[u2][:], op=ALU.add),
                         reads=[b_t1[u2], b_t2[u2]], writes=[b_mT[tb]])
            P.barrier()
        gateB = k.sb("mg_gateB", [128, D], F32, ls)
        fgB = k.sb("mg_fgB", [128, D], F32, ls)
        b_gateB, b_fgB = Buf("gateB"), Buf("fgB")
        c2 = P.new_dsem("mg_c2")
        P.dma("sp", gateB[:], S["modrow"][0:1, 2 * D:3 * D].broadcast_to([128, D]), writes=[b_gateB], sem=c2)
        P.dma("sp", fgB[:], I["final_g"][0:1, :].broadcast_to([128, D]), writes=[b_fgB], sem=c2)
        xb = [k.sb(f"mg_x{i}", [128, D], F32, ls) for i in range(NXB)]
        b_xb = [Buf(f"mgx{i}") for i in range(NXB)]
        xn = [k.sb(f"mg_xn{i}", [128, D], F32, ls) for i in range(NXB)]
        b_xn = [Buf(f"mgxn{i}") for i in range(NXB)]
        xsem = [P.new_dsem(f"mg_xs{i}") for i in range(NXB)]
        osem = [P.new_dsem(f"mg_os{i}") for i in range(NXB)]
        st2 = [k.sb(f"mg_st{i}", [128, 4], F32, ls) for i in range(NXB)]
        b_st2 = [Buf(f"mgst{i}") for i in range(NXB)]
        while wo_state["kc"] < NKC:
            load_wout_chunk()
        po = [k.ps(f"mg_po{i}", [128, 512], F32, ls) for i in range(3)]
        b_po = [Buf(f"mgpo{i}") for i in range(3)]
        pi = 0
        for t in range(16):
            s = t % NXB
            tb = t // 4
            P.dma("sp", xb[s][:], I["x"][t * 128:(t + 1) * 128, :], writes=[b_xb[s]], sem=xsem[s])
            for cbk in range(4):
                p_ = pi % 3
                pi += 1
                for kc in range(NKC):
                    P.op("pe", lambda e, kc=kc, cbk=cbk, p_=p_, t=t: e.matmul(po[p_][:], lhsT=mT[:, kc, t * 128:(t + 1) * 128],
                                                                          rhs=wout[:, kc, cbk * 512:(cbk + 1) * 512],
                                                                          start=(kc == 0), stop=(kc == NKC - 1)),
                         reads=[b_mT[tb], b_woutc[kc]], writes=[b_po[p_]])
                P.op("dve", lambda e, cbk=cbk, p_=p_, s=s: e.tensor_tensor(out=xn[s][:, cbk * 512:(cbk + 1) * 512], in0=po[p_][:],
                                                                       in1=gateB[:, cbk * 512:(cbk + 1) * 512], op=ALU.mult),
                     reads=[b_po[p_], b_gateB], writes=[b_xn[s]])
            P.op("pool", lambda e, s=s: e.tensor_tensor(out=xn[s][:], in0=xn[s][:], in1=xb[s][:], op=ALU.add),
                 reads=[b_xn[s], b_xb[s]], writes=[b_xn[s]])
            P.op("pool", lambda e, s=s: e.tensor_tensor(out=xb[s][:], in0=xn[s][:], in1=xn[s][:], op=ALU.mult),
                 reads=[b_xn[s]], writes=[b_xb[s]])
            P.op("dve", lambda e, s=s: e.tensor_reduce(out=st2[s][:, 0:1], in_=xb[s][:], axis=mybir.AxisListType.X, op=ALU.add),
                 reads=[b_xb[s]], writes=[b_st2[s]])
            P.op("dve", lambda e, s=s: e.tensor_scalar(out=st2[s][:, 1:2], in0=st2[s][:, 0:1], scalar1=1.0 / D, scalar2=EPS, op0=ALU.mult, op1=ALU.add),
                 reads=[b_st2[s]], writes=[b_st2[s]])
            P.op("act", lambda e, s=s: e.activation(out=st2[s][:, 2:3], in_=st2[s][:, 1:2], func=AF.Ln), reads=[b_st2[s]], writes=[b_st2[s]])
            P.op("act", lambda e, s=s: e.activation(out=st2[s][:, 3:4], in_=st2[s][:, 2:3], func=AF.Exp, scale=-0.5), reads=[b_st2[s]], writes=[b_st2[s]])
            P.op("dve", lambda e, s=s: e.scalar_tensor_tensor(out=xn[s][:], in0=xn[s][:], scalar=st2[s][:, 3:4], in1=fgB[:], op0=ALU.mult, op1=ALU.mult),
                 reads=[b_xn[s], b_st2[s], b_fgB], writes=[b_xn[s]])
            P.dma("sp", k.out[t * 128:(t + 1) * 128, :], xn[s][:], reads=[b_xn[s]], sem=osem[s])


_CACHE = {}


def _prep_inputs(inputs, b):
    f = lambda a: np.ascontiguousarray(np.asarray(a, dtype=np.float32))
    m = {}
    m["x"] = f(inputs["x"][b])
    m["ctx"] = f(inputs["ctx"][b])
    m["cc"] = f(np.stack([np.asarray(inputs["c"])[b], np.asarray(inputs["c_ctx"])], axis=0))
    m["w_ada"] = f(inputs["w_ada"][0])
    m["b_ada"] = f(inputs["b_ada"][0]).reshape(1, -1)
    m["norm_g"] = f(inputs["norm_g"][0])
    m["w_in"] = f(inputs["w_in"][0])
    m["lam"] = f(np.stack([np.asarray(inputs["lambda_q1"])[0], np.asarray(inputs["lambda_k1"])[0],
                           np.asarray(inputs["lambda_q2"])[0], np.asarray(inputs["lambda_k2"])[0]], axis=0))
    m["subln_g"] = f(inputs["subln_g"][0]).reshape(1, 128)
    m["ssm_lre"] = f(inputs["ssm_lambda_re"][0])
    m["ssm_lim"] = f(inputs["ssm_lambda_im"][0])
    m["ssm_ls"] = f(inputs["ssm_log_step"][0])
    m["ssm_bre"] = f(inputs["ssm_b_re"][0])
    m["ssm_bim"] = f(inputs["ssm_b_im"][0])
    m["ssm_cre"] = f(inputs["ssm_c_re"][0])
    m["ssm_cim"] = f(inputs["ssm_c_im"][0])
    m["ssm_d"] = f(inputs["ssm_d"][0]).reshape(1, 512)
    m["w_glu"] = f(inputs["w_glu"][0])
    m["b_glu"] = f(inputs["b_glu"][0])
    m["w_pa"] = f(inputs["w_pa"][0])
    m["w_ps"] = f(inputs["w_ps"][0])
    m["w_out"] = f(inputs["w_out"][0])
    m["final_g"] = f(inputs["final_g"]).reshape(1, D)
    m.update(_consts())
    return m


def kernel(**inputs):
    if "nc" not in _CACHE:
        _CACHE["nc"] = build()[0]
    nc = _CACHE["nc"]
    shared = None
    in_maps = []
    for b in range(8):
        m = _prep_inputs(inputs, b)
        if shared is None:
            shared = m
        else:
            for key in m:
                if key not in ("x", "ctx", "cc"):
                    m[key] = shared[key]
        in_maps.append(m)
    res = run_bass_kernel_spmd(nc, in_maps, core_ids=list(range(8)))
    return np.stack([np.asarray(r["out"], dtype=np.float32) for r in res.results], axis=0)
```

```python
import math
import numpy as np
import ml_dtypes
from contextlib import ExitStack
import concourse.bass as bass
import concourse.mybir as mybir
from concourse.bass_utils import run_bass_kernel_spmd

F32 = mybir.dt.float32
BF16 = mybir.dt.bfloat16
I32 = mybir.dt.int32
AF = mybir.ActivationFunctionType
ALU = mybir.AluOpType

D = 2048
L = 2048
LC = 256
LT = L + LC
NKC = D // 128
INW = 9216
HEADS = 8
EPS = 1e-6
LAM_INIT = 0.8 - 0.6 * math.exp(-0.3 * 0)
TWO_PI = 2.0 * math.pi


class Buf:
    __slots__ = ("name", "w", "r")

    def __init__(self, name):
        self.name = name
        self.w = None
        self.r = {}


class Prog:
    ENG = ["pe", "act", "dve", "pool", "sp"]

    def __init__(self, nc, st):
        self.nc = nc
        self.st = st
        self.q = {e: [] for e in self.ENG}
        self.seen = {e: {} for e in self.ENG}
        self.psem = {e: st.enter_context(nc.semaphore("p_" + e)) for e in ["pe", "act", "dve", "pool"]}
        self.dsems = []
        self.bufsem = {}
        self.bufsem_keep = []
        self.free_dsems = []

    def new_dsem(self, name):
        return None

    def _auto_dsem(self, reads, writes):
        b = writes[0] if len(writes) else reads[0]
        key = id(b)
        d = self.bufsem.get(key)
        if d is None:
            if self.free_dsems:
                d = self.free_dsems.pop()
            else:
                h = self.st.enter_context(self.nc.semaphore(f"d{len(self.dsems)}"))
                d = {"h": h, "n": 0, "name": f"d{len(self.dsems)}"}
                self.dsems.append(d)
            self.bufsem[key] = d
            self.bufsem_keep.append(b)
        return d

    def _deps(self, eng, reads, writes):
        need = {}

        def add(t):
            if t[0] == "c":
                if t[1] == "pe" and eng == "pe":
                    return
                key = ("c", t[1])
                if need.get(key, (None, -1))[1] < t[2]:
                    need[key] = (t[1], t[2])
            else:
                key = ("d", id(t[1]))
                if need.get(key, (None, -1))[1] < t[2]:
                    need[key] = (t[1], t[2])

        for b in reads:
            if b.w is not None:
                add(b.w)
        for b in writes:
            if b.w is not None:
                add(b.w)
            for t in b.r.values():
                add(t)
        waits = []
        for key, (obj, v) in need.items():
            if self.seen[eng].get(key, -1) >= v:
                continue
            self.seen[eng][key] = v
            waits.append((key[0], obj, v))
        return waits

    def _record(self, tok, reads, writes):
        for b in reads:
            key = (tok[0], tok[1] if tok[0] == "c" else id(tok[1]))
            b.r[key] = tok
        for b in writes:
            b.w = tok
            b.r = {}

    def op(self, eng, fn, reads=(), writes=()):
        waits = self._deps(eng, reads, writes)
        idx = len(self.q[eng])
        self.q[eng].append({"fn": fn, "waits": waits, "awaited": False, "dma": None})
        tok = ("c", eng, idx)
        self._record(tok, reads, writes)
        return tok

    def dma(self, eng, out, in_, reads=(), writes=(), sem=None, **kw):
        reads, writes = list(reads), list(writes)
        sem = self._auto_dsem(reads, writes)
        waits = self._deps(eng, reads, writes)
        sem["n"] += 16
        tok = ("d", sem, sem["n"])
        self.q[eng].append({"fn": (lambda e, o=out, i=in_, k=kw: e.dma_start(out=o, in_=i, **k)),
                            "waits": waits, "awaited": False, "dma": sem})
        self._record(tok, reads, writes)
        return tok

    def barrier(self):
        for e in self.ENG:
            waits = []
            for e2 in ["pe", "act", "dve", "pool"]:
                n = len(self.q[e2])
                if e2 == e:
                    n -= 0
                idx = None
                for i in range(len(self.q[e2]) - 1, -1, -1):
                    if self.q[e2][i]["fn"] is not None and self.q[e2][i]["dma"] is None:
                        idx = i
                        break
                if idx is None:
                    continue
                key = ("c", e2)
                if self.seen[e].get(key, -1) >= idx:
                    continue
                self.seen[e][key] = idx
                waits.append(("c", e2, idx))
            for d in self.dsems:
                if d["n"] == 0:
                    continue
                key = ("d", id(d))
                if self.seen[e].get(key, -1) >= d["n"]:
                    continue
                self.seen[e][key] = d["n"]
                waits.append(("d", d, d["n"]))
            if waits:
                self.q[e].append({"fn": None, "waits": waits, "awaited": False, "dma": None})
        for d in self.bufsem.values():
            self.free_dsems.append(d)
        self.bufsem = {}
        self.bufsem_keep = []

    def emit(self):
        for e in self.ENG:
            for ent in self.q[e]:
                for w in ent["waits"]:
                    if w[0] == "c":
                        self.q[w[1]][w[2]]["awaited"] = True
        cnt = {}
        for e in ["pe", "act", "dve", "pool"]:
            c = 0
            arr = []
            for ent in self.q[e]:
                if ent["awaited"]:
                    c += 1
                arr.append(c)
            cnt[e] = arr
        psem = self.psem
        q = self.q

        def run(name, e):
            for ent in q[name]:
                for w in ent["waits"]:
                    if w[0] == "c":
                        e.wait_ge(psem[w[1]], cnt[w[1]][w[2]])
                    else:
                        e.wait_ge(w[1]["h"], w[2])
                if ent["fn"] is None:
                    continue
                inst = ent["fn"](e)
                if ent["dma"] is not None:
                    inst.then_inc(ent["dma"]["h"], 16)
                elif ent["awaited"]:
                    inst.then_inc(psem[name], 1)

        with self.nc.Block() as block:
            @block.sync
            def _(e):
                run("sp", e)

            @block.scalar
            def _(e):
                run("act", e)

            @block.vector
            def _(e):
                run("dve", e)

            @block.gpsimd
            def _(e):
                run("pool", e)

            @block.tensor
            def _(e):
                run("pe", e)


def _consts():
    ident = np.eye(128, dtype=np.float32)
    m = np.arange(128)
    partner = np.where((m % 32) < 16, m + 16, m - 16)
    perm = np.zeros((128, 128), np.float32)
    perm[partner, m] = 1.0
    sgn = np.where((m % 32) < 16, -1.0, 1.0).astype(np.float32)
    tok = np.arange(L)
    pos = np.where(((m % 64) < 32)[:, None], (tok // 64)[None, :], (tok % 64)[None, :]).astype(np.float32)
    fexp = ((m % 16) / 16.0).astype(np.float32)
    colc = np.zeros((128, 4), np.float32)
    colc[:, 0] = sgn
    colc[:, 1] = fexp
    colc[:, 2] = np.where(m < 64, 1.0, -1.0)
    sel = np.zeros((2, 128), np.float32)
    sel[0, :] = 1.0
    tauA = np.zeros((128, 32, 9), np.float32)
    tauA[:, 0:16, :] = np.arange(9)[None, None, :]
    tauA[:, 16:32, :] = (8 - np.arange(9))[None, None, :]
    tauB = np.zeros((128, 32, 8), np.float32)
    tauB[:, 0:16, :] = (7 - np.arange(8))[None, None, :]
    tauB[:, 16:32, :] = np.arange(8)[None, None, :]
    return {"c_ident": ident, "c_perm": perm, "c_pos": pos, "c_col": colc, "c_sel": sel, "c_tauA": tauA, "c_tauB": tauB}


class K:
    pass


def build(debug=None):
    nc = bass.Bass("TRN2", target_bir_lowering=False)
    st = ExitStack()
    P = Prog(nc, st)
    k = K()
    k.nc, k.P, k.st = nc, P, st
    k.debug = debug or {}

    def dram_in(name, shape, dt=F32):
        return nc.dram_tensor(name, list(shape), dt, kind="ExternalInput").ap()

    dbg_outs = []

    def dram_scr(name, shape, dt):
        kind = "Internal"
        if debug is not None and name in debug.get("_inject", ()):
            kind = "ExternalInput"
        elif debug is not None and name in debug:
            kind = "ExternalOutput"
            dbg_outs.append(name)
        return nc.dram_tensor(name, list(shape), dt, kind=kind).ap()

    I = {}
    I["x"] = dram_in("x", [L, D])
    I["ctx"] = dram_in("ctx", [LC, D])
    I["cc"] = dram_in("cc", [2, D])
    I["w_ada"] = dram_in("w_ada", [D, 3 * D])
    I["b_ada"] = dram_in("b_ada", [1, 3 * D])
    I["norm_g"] = dram_in("norm_g", [D])
    I["w_in"] = dram_in("w_in", [D, INW])
    I["lam"] = dram_in("lam", [4, 64])
    I["subln_g"] = dram_in("subln_g", [1, 128])
    I["ssm_lre"] = dram_in("ssm_lre", [2, 32, 64])
    I["ssm_lim"] = dram_in("ssm_lim", [2, 32, 64])
    I["ssm_ls"] = dram_in("ssm_ls", [2, 32])
    I["ssm_bre"] = dram_in("ssm_bre", [2, 32, 64, 16])
    I["ssm_bim"] = dram_in("ssm_bim", [2, 32, 64, 16])
    I["ssm_cre"] = dram_in("ssm_cre", [2, 32, 16, 64])
    I["ssm_cim"] = dram_in("ssm_cim", [2, 32, 16, 64])
    I["ssm_d"] = dram_in("ssm_d", [1, 512])
    I["w_glu"] = dram_in("w_glu", [512, 512])
    I["b_glu"] = dram_in("b_glu", [512])
    I["w_pa"] = dram_in("w_pa", [1024, D])
    I["w_ps"] = dram_in("w_ps", [512, D])
    I["w_out"] = dram_in("w_out", [D, D])
    I["final_g"] = dram_in("final_g", [1, D])
    for cn, arr in _consts().items():
        I[cn] = dram_in(cn, arr.shape)
    out = nc.dram_tensor("out", [L, D], F32, kind="ExternalOutput").ap()

    S = {}
    S["modrow"] = dram_scr("modrow", [2, 3 * D], F32)
    S["qT"] = dram_scr("qT", [HEADS, 128, L], BF16)
    S["kT"] = dram_scr("kT", [HEADS, 128, LT], BF16)
    S["v"] = dram_scr("v", [LT, 1024], BF16)
    S["sga"] = dram_scr("sga", [L, 1024], BF16)
    S["u"] = dram_scr("u", [LT, 512], F32)
    S["sgsT"] = dram_scr("sgsT", [512, L], BF16)
    S["sgmT"] = dram_scr("sgmT", [2 * D, L], BF16)
    S["abrT"] = dram_scr("abrT", [1024, L], BF16)
    S["sbrT"] = dram_scr("sbrT", [512, L], BF16)
    S["hT"] = dram_scr("hT_dbg", [128, NKC, LT], BF16) if (debug is not None and "hT_dbg" in debug) else None
    k.I, k.S, k.out = I, S, out
    k.dbg_sem = None

    def dbg(name, shape, dt, ap_fn, bufs):
        if debug is None or name not in debug:
            return
        if name not in S:
            S[name] = nc.dram_tensor(name, list(shape), dt, kind="ExternalOutput").ap()
            dbg_outs.append(name)
        if k.dbg_sem is None:
            k.dbg_sem = P.new_dsem("dbgsem")
        o, i = ap_fn(S[name])
        P.dma("sp", o, i, reads=bufs, sem=k.dbg_sem)
    k.dbg = dbg

    def sb(name, shape, dt, stack=st):
        return stack.enter_context(nc.sbuf_tensor(name, list(shape), dt))

    def ps(name, shape, dt, stack=st):
        return stack.enter_context(nc.psum_tensor(name, list(shape), dt))

    k.sb, k.ps = sb, ps
    ident = sb("ident", [128, 128], F32)
    colc = sb("colc", [128, 4], F32)
    b_ident, b_colc = Buf("ident"), Buf("colc")
    csem = P.new_dsem("csem")
    P.dma("sp", ident[:], I["c_ident"], writes=[b_ident], sem=csem)
    P.dma("sp", colc[:], I["c_col"], writes=[b_colc], sem=csem)
    k.ident, k.b_ident, k.colc, k.b_colc, k.csem = ident, b_ident, colc, b_colc, csem
    k.ssq = sb("ssq", [128, 40], F32)
    k.b_ssq = Buf("ssq")

    phase_adaln(k)
    P.barrier()
    if debug is None or debug.get("_upto", 99) >= 1:
        phase_norm_inproj(k, debug)
        P.barrier()
    if (debug is None or debug.get("_upto", 99) >= 2) and not (debug or {}).get("_skip_ssm"):
        phase_ssm(k)
        P.barrier()
    if debug is None or debug.get("_upto", 99) >= 3:
        phase_attn(k)
        P.barrier()
    if debug is None or debug.get("_upto", 99) >= 4:
        phase_merge(k)
        P.barrier()
    P.emit()
    st.close()
    return nc, dbg_outs


def range_sin(k, stack, out_ap, y_ap, shape, tag, rbufs, wbufs, eng="dve"):
    nc, P = k.nc, k.P
    ki = k.sb(tag + "_ki", shape, I32, stack)
    kf = k.sb(tag + "_kf", shape, F32, stack)
    g = k.sb(tag + "_g", shape, F32, stack)
    bki, bkf, bg = Buf(tag + "ki"), Buf(tag + "kf"), Buf(tag + "g")
    sl = tuple([slice(None)] * len(shape))
    P.op(eng, lambda e: e.tensor_copy(out=ki[sl], in_=y_ap), reads=rbufs, writes=[bki])
    P.op(eng, lambda e: e.tensor_copy(out=kf[sl], in_=ki[sl]), reads=[bki], writes=[bkf])
    P.op(eng, lambda e: e.tensor_tensor(out=kf[sl], in0=y_ap, in1=kf[sl], op=ALU.subtract), reads=rbufs + [bkf], writes=[bkf])
    P.op(eng, lambda e: e.tensor_single_scalar(out=g[sl], in_=kf[sl], scalar=0.5, op=ALU.is_gt), reads=[bkf], writes=[bg])
    P.op(eng, lambda e: e.tensor_tensor(out=kf[sl], in0=kf[sl], in1=g[sl], op=ALU.subtract), reads=[bkf, bg], writes=[bkf])
    P.op(eng, lambda e: e.tensor_single_scalar(out=g[sl], in_=kf[sl], scalar=-0.5, op=ALU.is_lt), reads=[bkf], writes=[bg])
    P.op(eng, lambda e: e.tensor_tensor(out=kf[sl], in0=kf[sl], in1=g[sl], op=ALU.add), reads=[bkf, bg], writes=[bkf])
    P.op("act", lambda e: e.activation(out=out_ap, in_=kf[sl], func=AF.Sin, scale=TWO_PI * (1.0 - 2e-7)), reads=[bkf], writes=wbufs)


def phase_adaln(k):
    nc, P, I, S = k.nc, k.P, k.I, k.S
    with ExitStack() as ls:
        sT = k.sb("ad_sT", [128, NKC, 2], F32, ls)
        b_sT = Buf("sT")
        sem_c = P.new_dsem("ad_c")
        for v in range(2):
            P.dma("sp", sT[:, :, v], I["cc"][v].rearrange("(kc p) -> p kc", p=128), writes=[b_sT], sem=sem_c,
                  allow_slow_non_contiguous=True)
        P.op("act", lambda e: e.activation(out=sT[:], in_=sT[:], func=AF.Silu), reads=[b_sT], writes=[b_sT])
        brow = k.sb("ad_brow", [2, 3 * D], F32, ls)
        b_brow = Buf("brow")
        for v in range(2):
            P.dma("sp", brow[v:v + 1, :], I["b_ada"], writes=[b_brow], sem=sem_c)
        modrow = k.sb("ad_modrow", [2, 3 * D], F32, ls)
        b_modrow = Buf("modrow")
        NS = 2
        wst = [k.sb(f"ad_w{i}", [128, NKC, 512], F32, ls) for i in range(NS)]
        b_w = [Buf(f"adw{i}") for i in range(NS)]
        wsem = [P.new_dsem(f"ad_ws{i}") for i in range(NS)]
        pst = [k.ps(f"ad_ps{i}", [128, 512], F32, ls) for i in range(2)]
        b_ps = [Buf(f"adps{i}") for i in range(2)]
        wv = I["w_ada"].rearrange("(kc p) c -> p kc c", p=128)
        xs1 = [k.sb(f"ad_x{i}", [128, D], F32, ls) for i in range(2)]
        b_xs1 = [Buf(f"adx{i}") for i in range(2)]
        junk1 = k.sb("ad_junk", [128, D], BF16, ls)
        b_junk1 = Buf("adjunk")
        tiles_done = 0

        def ss_tile(t):
            s1 = t % 2
            src = I["x"][t * 128:(t + 1) * 128, :] if t < 16 else I["ctx"][(t - 16) * 128:(t - 15) * 128, :]
            P.dma("pool", xs1[s1][:], src, writes=[b_xs1[s1]])
            P.op("act", lambda e, s1=s1, t=t: e.activation(out=junk1[:], in_=xs1[s1][:], func=AF.Square, accum_out=k.ssq[:, t:t + 1]),
                 reads=[b_xs1[s1]], writes=[b_junk1, k.b_ssq])

        for cb in range(12):
            for _ in range(2 if cb < 6 else 1):
                if tiles_done < 18:
                    ss_tile(tiles_done)
                    tiles_done += 1
            s = cb % NS
            P.dma("sp", wst[s][:, 0:8, :], wv[:, 0:8, cb * 512:(cb + 1) * 512], writes=[b_w[s]], sem=wsem[s])
            P.dma("act", wst[s][:, 8:16, :], wv[:, 8:16, cb * 512:(cb + 1) * 512], writes=[b_w[s]], sem=wsem[s])
            pt, bp = pst[cb % 2], b_ps[cb % 2]
            for kc in range(NKC):
                P.op("pe", lambda e, kc=kc, s=s, pt=pt: e.matmul(pt[0:2, :], lhsT=sT[:, kc, :], rhs=wst[s][:, kc, :],
                                                              start=(kc == 0), stop=(kc == NKC - 1)),
                     reads=[b_sT, b_w[s]], writes=[bp])
            P.op("dve", lambda e, cb=cb, pt=pt: e.tensor_tensor(out=modrow[:, cb * 512:(cb + 1) * 512], in0=pt[0:2, :],
                                                             in1=brow[:, cb * 512:(cb + 1) * 512], op=ALU.add),
                 reads=[bp, b_brow], writes=[b_modrow])
        b_mr = Buf("modrow_d")
        k.b_modrow_d = b_mr
        P.dma("sp", S["modrow"], modrow[:], reads=[b_modrow], writes=[b_mr], sem=sem_c)


def phase_norm_inproj(k, debug):
    nc, P, I, S = k.nc, k.P, k.I, k.S
    with ExitStack() as ls:
        hT = k.sb("hT", [128, NKC, LT], BF16, ls)
        b_hT = [Buf(f"hT{t}") for t in range(18)]
        Amod = k.sb("Amod", [128, NKC, 2], F32, ls)
        Smod = k.sb("Smod", [128, NKC, 2], F32, ls)
        gcol = k.sb("gcol", [128, NKC], F32, ls)
        b_A, b_S, b_g = Buf("Amod"), Buf("Smod"), Buf("gcol")
        msem = P.new_dsem("n_m")
        for v in range(2):
            P.dma("sp", Smod[:, :, v], S["modrow"][v, 0:D].rearrange("(kc p) -> p kc", p=128),
                  reads=[k.b_modrow_d], writes=[b_S], sem=msem, allow_slow_non_contiguous=True)
            P.dma("sp", Amod[:, :, v], S["modrow"][v, D:2 * D].rearrange("(kc p) -> p kc", p=128),
                  reads=[k.b_modrow_d], writes=[b_A], sem=msem, allow_slow_non_contiguous=True)
        P.dma("sp", gcol[:], I["norm_g"].rearrange("(kc p) -> p kc", p=128), writes=[b_g], sem=msem,
              allow_slow_non_contiguous=True)
        for v in range(2):
            P.op("dve", lambda e, v=v: e.scalar_tensor_tensor(out=Amod[:, :, v], in0=Amod[:, :, v], scalar=1.0, in1=gcol[:],
                                                             op0=ALU.add, op1=ALU.mult),
                 reads=[b_A, b_g], writes=[b_A])
        with ExitStack() as l1:
            NX = 2
            xt = [k.sb(f"n_x{i}", [128, D], F32, l1) for i in range(NX)]
            b_x = [Buf(f"nx{i}") for i in range(NX)]
            xsem = [P.new_dsem(f"n_xs{i}") for i in range(NX)]
            junk = k.sb("n_junk", [128, D], BF16, l1)
            b_junk = Buf("junk")
            stat = [k.sb(f"n_st{i}", [128, 4], F32, l1) for i in range(NX)]
            b_stat = [Buf(f"nst{i}") for i in range(NX)]
            pt = [k.ps(f"n_ps{i}", [128, 512], F32, l1) for i in range(4)]
            b_pt = [Buf(f"nps{i}") for i in range(4)]
            pi = 0
            P.op("dve", lambda e: e.tensor_scalar(out=k.ssq[:, 0:18], in0=k.ssq[:, 0:18], scalar1=1.0 / D, scalar2=EPS, op0=ALU.mult, op1=ALU.add),
                 reads=[k.b_ssq], writes=[k.b_ssq])
            P.op("act", lambda e: e.activation(out=k.ssq[:, 0:18], in_=k.ssq[:, 0:18], func=AF.Ln), reads=[k.b_ssq], writes=[k.b_ssq])
            P.op("act", lambda e: e.activation(out=k.ssq[:, 20:38], in_=k.ssq[:, 0:18], func=AF.Exp, scale=-0.5), reads=[k.b_ssq], writes=[k.b_ssq])
            for t in range(18):
                s = t % NX
                v = 0 if t < 16 else 1
                src = I["x"][t * 128:(t + 1) * 128, :] if t < 16 else I["ctx"][(t - 16) * 128:(t - 15) * 128, :]
                P.dma("sp", xt[s][:, 0:1024], src[:, 0:1024], writes=[b_x[s]], sem=xsem[s])
                P.dma("act", xt[s][:, 1024:2048], src[:, 1024:2048], writes=[b_x[s]], sem=xsem[s])
                P.op("dve", lambda e, s=s, t=t: e.tensor_scalar(out=xt[s][:], in0=xt[s][:], scalar1=k.ssq[:, 20 + t:21 + t], scalar2=None,
                                                              op0=ALU.mult),
                     reads=[b_x[s], k.b_ssq], writes=[b_x[s]])
                for g4 in range(4):
                    p_, bp = pt[pi % 4], b_pt[pi % 4]
                    pi += 1
                    for j in range(4):
                        kc = g4 * 4 + j
                        P.op("pe", lambda e, s=s, kc=kc, j=j, p_=p_: e.transpose(out=p_[:, j * 128:(j + 1) * 128],
                                                                             in_=xt[s][:, kc * 128:(kc + 1) * 128],
                                                                             identity=k.ident[:]),
                             reads=[b_x[s], k.b_ident], writes=[bp])
                    for j in range(4):
                        kc = g4 * 4 + j
                        eng = "dve" if (j % 2 == 0) else "act"
                        if eng == "dve":
                            P.op("dve", lambda e, kc=kc, j=j, p_=p_, t=t, v=v: e.tensor_scalar(
                                out=hT[:, kc, t * 128:(t + 1) * 128], in0=p_[:, j * 128:(j + 1) * 128],
                                scalar1=Amod[:, kc, v:v + 1], scalar2=Smod[:, kc, v:v + 1], op0=ALU.mult, op1=ALU.add),
                                reads=[bp, b_A, b_S], writes=[b_hT[t]])
                        else:
                            P.op("act", lambda e, kc=kc, j=j, p_=p_, t=t, v=v: e.activation(
                                out=hT[:, kc, t * 128:(t + 1) * 128], in_=p_[:, j * 128:(j + 1) * 128],
                                func=AF.Identity, scale=Amod[:, kc, v:v + 1], bias=Smod[:, kc, v:v + 1]),
                                reads=[bp, b_A, b_S], writes=[b_hT[t]])
        if S["hT"] is not None:
            dsem = P.new_dsem("dbg")
            P.dma("sp", S["hT"], hT[:], reads=b_hT, writes=[Buf("x")], sem=dsem)
        P.barrier()
        if debug is not None and debug.get("_upto", 99) < 1.5:
            return
        inproj(k, ls, hT, b_hT)


def inproj(k, ls, hT, b_hT):
    nc, P, I, S = k.nc, k.P, k.I, k.S
    cosT = k.sb("cosT", [128, L], F32, ls)
    sinS = k.sb("sinS", [128, L], F32, ls)
    perm = k.sb("perm", [128, 128], F32, ls)
    b_cos, b_sin, b_perm = Buf("cos"), Buf("sin"), Buf("perm")
    tsem = P.new_dsem("ip_t")
    P.dma("sp", perm[:], I["c_perm"], writes=[b_perm], sem=tsem)
    with ExitStack() as l0:
        pos = k.sb("pos", [128, L], F32, l0)
        yv = k.sb("yv", [128, L], F32, l0)
        inv = k.sb("inv", [128, 1], F32, l0)
        b_pos, b_y, b_inv = Buf("pos"), Buf("yv"), Buf("inv")
        P.dma("sp", pos[:], I["c_pos"], writes=[b_pos], sem=tsem)
        P.op("act", lambda e: e.activation(out=inv[:], in_=k.colc[:, 1:2], func=AF.Exp, scale=-math.log(10000.0)),
             reads=[k.b_colc], writes=[b_inv])
        P.op("dve", lambda e: e.tensor_scalar(out=yv[:], in0=pos[:], scalar1=inv[:, 0:1], scalar2=1.0 / TWO_PI,
                                              op0=ALU.mult, op1=ALU.mult), reads=[b_pos, b_inv], writes=[b_y])
        range_sin(k, l0, sinS[:], yv[:], [128, L], "rs1", [b_y], [b_sin])
        P.op("dve", lambda e: e.tensor_scalar(out=sinS[:], in0=sinS[:], scalar1=k.colc[:, 0:1], scalar2=None, op0=ALU.mult),
             reads=[b_sin, k.b_colc], writes=[b_sin])
        P.op("dve", lambda e: e.tensor_scalar(out=yv[:], in0=yv[:], scalar1=0.25, scalar2=None, op0=ALU.add),
             reads=[b_y], writes=[b_y])
        range_sin(k, l0, cosT[:], yv[:], [128, L], "rs2", [b_y], [b_cos])
        P.barrier()
    NW = 2
    wst = [k.sb(f"ip_wst{i}", [128, 8, 512], F32, ls) for i in range(NW)]
    b_wst = [Buf(f"wst{i}") for i in range(NW)]
    wsem = [P.new_dsem(f"ip_ws{i}") for i in range(NW)]
    wb = [k.sb(f"ip_wb{i}", [128, NKC, 512], BF16, ls) for i in range(2)]
    b_wb = [Buf(f"wb{i}") for i in range(2)]
    NOB = 4
    ob = [k.sb(f"ip_ob{i}", [128, 512], BF16, ls) for i in range(NOB)]
    b_ob = [Buf(f"ob{i}") for i in range(NOB)]
    osem = [P.new_dsem(f"ip_os{i}") for i in range(NOB)]
    NOF = 2
    of = [k.sb(f"ip_of{i}", [128, 512], F32, ls) for i in range(NOF)]
    b_of = [Buf(f"of{i}") for i in range(NOF)]
    fsem = [P.new_dsem(f"ip_fs{i}") for i in range(NOF)]
    t1 = [k.sb(f"ip_t1{i}", [128, 512], F32, ls) for i in range(2)]
    b_t1 = [Buf(f"t1{i}") for i in range(2)]
    t2 = [k.sb(f"ip_t2{i}", [128, 512], F32, ls) for i in range(2)]
    b_t2 = [Buf(f"t2{i}") for i in range(2)]
    pb = [k.ps(f"ip_ps{i}", [128, 512], F32, ls) for i in range(4)]
    b_pb = [Buf(f"ipps{i}") for i in range(4)]
    pr = [k.ps(f"ip_pr{i}", [128, 512], F32, ls) for i in range(2)]
    b_pr = [Buf(f"ippr{i}") for i in range(2)]
    wv = I["w_in"].rearrange("(kc p) c -> p kc c", p=128)
    cnt = {"pb": 0, "ob": 0, "of": 0, "r": 0, "ld": 0, "ev": 0}

    def load_block(cb):
        s2 = cb % 2
        for half in range(2):
            sl = cnt["ld"] % NW
            cnt["ld"] += 1
            P.dma("sp", wst[sl][:], wv[:, half * 8:(half + 1) * 8, cb * 512:(cb + 1) * 512], writes=[b_wst[sl]], sem=wsem[sl])
            P.op("dve", lambda e, sl=sl, s2=s2, half=half: e.tensor_copy(out=wb[s2][:, half * 8:(half + 1) * 8, :], in_=wst[sl][:]),
                 reads=[b_wst[sl]], writes=[b_wb[s2]])

    def next_ob():
        i = cnt["ob"] % NOB
        cnt["ob"] += 1
        return i

    def evac_eng():
        cnt["ev"] += 1
        return "act" if cnt["ev"] % 2 else "dve"

    def tiles_of(tok0, n):
        return [b_hT[t] for t in range(tok0 // 128, (tok0 + n) // 128)]

    def fm_unit(cb, fc, tok0, n, kind, row0, dst):
        s2 = cb % 2
        pi = cnt["pb"] % 4
        cnt["pb"] += 1
        pt, bp = pb[pi], b_pb[pi]
        for kc in range(NKC):
            P.op("pe", lambda e, kc=kc: e.matmul(pt[:, 0:n], lhsT=wb[s2][:, kc, fc * 128:(fc + 1) * 128],
                                                 rhs=hT[:, kc, tok0:tok0 + n], start=(kc == 0), stop=(kc == NKC - 1)),
                 reads=[b_wb[s2]] + tiles_of(tok0, n), writes=[bp])
        oi = next_ob()
        if kind == "rope":
            ri = cnt["r"] % 2
            cnt["r"] += 1
            fi = cnt["of"] % NOF
            cnt["of"] += 1
            P.op("act", lambda e: e.activation(out=of[fi][:, 0:n], in_=pt[:, 0:n], func=AF.Copy), reads=[bp], writes=[b_of[fi]])
            P.op("pe", lambda e: e.matmul(pr[ri][:, 0:n], lhsT=perm[:], rhs=of[fi][:, 0:n], start=True, stop=True),
                 reads=[b_perm, b_of[fi]], writes=[b_pr[ri]])
            P.op("dve", lambda e: e.tensor_tensor(out=t1[ri][:, 0:n], in0=of[fi][:, 0:n], in1=cosT[:, tok0:tok0 + n], op=ALU.mult),
                 reads=[b_of[fi], b_cos], writes=[b_t1[ri]])
            P.op("dve", lambda e: e.tensor_tensor(out=t2[ri][:, 0:n], in0=pr[ri][:, 0:n], in1=sinS[:, tok0:tok0 + n], op=ALU.mult),
                 reads=[b_pr[ri], b_sin], writes=[b_t2[ri]])
            P.op("pool", lambda e: e.tensor_tensor(out=ob[oi][:, 0:n], in0=t1[ri][:, 0:n], in1=t2[ri][:, 0:n], op=ALU.add),
                 reads=[b_t1[ri], b_t2[ri]], writes=[b_ob[oi]])
        elif kind == "copy":
            eg = evac_eng()
            if eg == "act":
                P.op("act", lambda e: e.activation(out=ob[oi][:, 0:n], in_=pt[:, 0:n], func=AF.Copy), reads=[bp], writes=[b_ob[oi]])
            else:
                P.op("dve", lambda e: e.tensor_copy(out=ob[oi][:, 0:n], in_=pt[:, 0:n]), reads=[bp], writes=[b_ob[oi]])
        else:
            fn = AF.Silu if kind == "silu" else AF.Sigmoid
            P.op("act", lambda e: e.activation(out=ob[oi][:, 0:n], in_=pt[:, 0:n], func=fn), reads=[bp], writes=[b_ob[oi]])
        P.dma("pool", dst, ob[oi][:, 0:n], reads=[b_ob[oi]], sem=osem[oi])

    def tm_unit(cb, t, kind, dst):
        s2 = cb % 2
        pi = cnt["pb"] % 4
        cnt["pb"] += 1
        pt, bp = pb[pi], b_pb[pi]
        for kc in range(NKC):
            P.op("pe", lambda e, kc=kc: e.matmul(pt[:], lhsT=hT[:, kc, t * 128:(t + 1) * 128], rhs=wb[s2][:, kc, :],
                                                 start=(kc == 0), stop=(kc == NKC - 1)),
                 reads=[b_wb[s2], b_hT[t]], writes=[bp])
        if kind == "f32":
            fi = cnt["of"] % NOF
            cnt["of"] += 1
            P.op("dve", lambda e: e.tensor_copy(out=of[fi][:], in_=pt[:]), reads=[bp], writes=[b_of[fi]])
            P.dma("pool", dst, of[fi][:], reads=[b_of[fi]], sem=fsem[fi])
            return
        oi = next_ob()
        if kind == "copy":
            eg = evac_eng()
            if eg == "act":
                P.op("act", lambda e: e.activation(out=ob[oi][:], in_=pt[:], func=AF.Copy), reads=[bp], writes=[b_ob[oi]])
            else:
                P.op("dve", lambda e: e.tensor_copy(out=ob[oi][:], in_=pt[:]), reads=[bp], writes=[b_ob[oi]])
        else:
            P.op("act", lambda e: e.activation(out=ob[oi][:], in_=pt[:], func=AF.Silu), reads=[bp], writes=[b_ob[oi]])
        P.dma("pool", dst, ob[oi][:], reads=[b_ob[oi]], sem=osem[oi])

    NCB = INW // 512
    load_block(0)
    for cb in range(NCB):
        if cb + 1 < NCB:
            load_block(cb + 1)
        c0 = cb * 512
        if cb < 2:
            for fc in range(4):
                h = cb * 4 + fc
                for tb in range(4):
                    fm_unit(cb, fc, tb * 512, 512, "rope", 0, S["qT"][h, :, tb * 512:(tb + 1) * 512])
        elif cb < 4:
            for fc in range(4):
                h = (cb - 2) * 4 + fc
                for tb in range(4):
                    fm_unit(cb, fc, tb * 512, 512, "rope", 0, S["kT"][h, :, tb * 512:(tb + 1) * 512])
                fm_unit(cb, fc, L, LC, "copy", 0, S["kT"][h, :, L:LT])
        elif cb < 6:
            for t in range(18):
                tm_unit(cb, t, "copy", S["v"][t * 128:(t + 1) * 128, (cb - 4) * 512:(cb - 3) * 512])
        elif cb < 8:
            for t in range(16):
                tm_unit(cb, t, "silu", S["sga"][t * 128:(t + 1) * 128, (cb - 6) * 512:(cb - 5) * 512])
        elif cb == 8:
            for t in range(18):
                tm_unit(cb, t, "f32", S["u"][t * 128:(t + 1) * 128, :])
        elif cb == 9:
            for fc in range(4):
                for tb in range(4):
                    fm_unit(cb, fc, tb * 512, 512, "silu", 0, S["sgsT"][fc * 128:(fc + 1) * 128, tb * 512:(tb + 1) * 512])
        else:
            for fc in range(4):
                r0 = (cb - 10) * 512 + fc * 128
                for tb in range(4):
                    fm_unit(cb, fc, tb * 512, 512, "sigm", 0, S["sgmT"][r0:r0 + 128, tb * 512:(tb + 1) * 512])


def phase_ssm(k):
    nc, P, I, S = k.nc, k.P, k.I, k.S
    MUL, ADD, SUB = ALU.mult, ALU.add, ALU.subtract
    with ExitStack() as ls:
        ToepT = k.sb("ss_toep", [128, 32, 128], BF16, ls)
        RCp = k.sb("ss_rcp", [128, 2, 2, 16, 256], BF16, ls)
        WT = k.sb("ss_wt", [128, 2, 16, 2, 128], BF16, ls)
        A8c = k.sb("ss_a8c", [128, 2, 16, 2], F32, ls)
        A8s = k.sb("ss_a8s", [128, 2, 16, 2], F32, ls)
        b_toep = [Buf(f"toep{g}") for g in range(32)]
        b_rcp, b_wt = Buf("rcp"), Buf("wt")
        b_U = [Buf(f"U{g}") for g in range(32)]
        b_zbf = [Buf("zbf0"), Buf("zbf1")]
        b_ygT = Buf("ygT")
        b_a8 = Buf("a8")
        pbk = [k.ps(f"ss_ps{i}", [128, 512], F32, ls) for i in range(8)]
        b_pbk = [Buf(f"ssps{i}") for i in range(8)]
        pc = {"i": 0}

        def nb():
            i = pc["i"] % 8
            pc["i"] += 1
            return pbk[i], b_pbk[i]

        csem = P.new_dsem("ss_c")
        with ExitStack() as l0:
            lre = k.sb("ss_lre", [128, 32], F32, l0)
            lim = k.sb("ss_lim", [128, 32], F32, l0)
            dtt = k.sb("ss_dt", [128, 32], F32, l0)
            alog = k.sb("ss_alog", [128, 32], F32, l0)
            th = k.sb("ss_th", [128, 32], F32, l0)
            b_l, b_dt, b_al = Buf("lrelim"), Buf("dtt"), Buf("alogth")
            for gp in range(2):
                for d in range(2):
                    P.dma("sp", lre[gp * 64:(gp + 1) * 64, d * 16:(d + 1) * 16], I["ssm_lre"][d, gp * 16:(gp + 1) * 16, :].rearrange("g p -> p g"),
                          writes=[b_l], sem=csem, allow_slow_non_contiguous=True)
                    P.dma("sp", lim[gp * 64:(gp + 1) * 64, d * 16:(d + 1) * 16], I["ssm_lim"][d, gp * 16:(gp + 1) * 16, :].rearrange("g p -> p g"),
                          writes=[b_l], sem=csem, allow_slow_non_contiguous=True)
                    P.dma("sp", dtt[gp * 64:(gp + 1) * 64, d * 16:(d + 1) * 16], I["ssm_ls"][d:d + 1, gp * 16:(gp + 1) * 16].broadcast_to([64, 16]),
                          writes=[b_dt], sem=csem)
            P.op("act", lambda e: e.activation(out=dtt[:], in_=dtt[:], func=AF.Exp), reads=[b_dt], writes=[b_dt])
            P.op("dve", lambda e: e.tensor_tensor(out=alog[:], in0=lre[:], in1=dtt[:], op=MUL), reads=[b_l, b_dt], writes=[b_al])
            P.op("dve", lambda e: e.scalar_tensor_tensor(out=th[:], in0=lim[:], scalar=1.0 / TWO_PI, in1=dtt[:], op0=MUL, op1=MUL),
                 reads=[b_l, b_dt], writes=[b_al])
            tabs = {}
            for nm, n in (("A", 9), ("B", 8)):
                tau = k.sb(f"ss_tau{nm}", [128, 32, n], F32, l0)
                ex = k.sb(f"ss_ex{nm}", [128, 32, n], F32, l0)
                yv = k.sb(f"ss_yv{nm}", [128, 32, n], F32, l0)
                sn = k.sb(f"ss_sn{nm}", [128, 32, n], F32, l0)
                cs = k.sb(f"ss_cs{nm}", [128, 32, n], F32, l0)
                b_tau, b_ex, b_yv, b_sn, b_cs = Buf("tau" + nm), Buf("ex" + nm), Buf("yv" + nm), Buf("sn" + nm), Buf("cs" + nm)
                P.dma("sp", tau[:], I["c_tau" + nm], writes=[b_tau], sem=csem)
                P.op("dve", lambda e, ex=ex, tau=tau, n=n: e.tensor_tensor(out=ex[:], in0=tau[:], in1=alog[:, :, None].broadcast_to([128, 32, n]), op=MUL),
                     reads=[b_tau, b_al], writes=[b_ex])
                P.op("act", lambda e, ex=ex: e.activation(out=ex[:], in_=ex[:], func=AF.Exp), reads=[b_ex], writes=[b_ex])
                P.op("dve", lambda e, yv=yv, tau=tau, n=n: e.tensor_tensor(out=yv[:], in0=tau[:], in1=th[:, :, None].broadcast_to([128, 32, n]), op=MUL),
                     reads=[b_tau, b_al], writes=[b_yv])
                fl = lambda t: t[:].rearrange("p a b -> p (a b)")
                range_sin(k, l0, fl(sn), fl(yv), [128, 32 * n], "ssr1" + nm, [b_yv], [b_sn])
                P.op("dve", lambda e, yv=yv: e.tensor_scalar(out=yv[:], in0=yv[:], scalar1=0.25, scalar2=None, op0=ADD), reads=[b_yv], writes=[b_yv])
                range_sin(k, l0, fl(cs), fl(yv), [128, 32 * n], "ssr2" + nm, [b_yv], [b_cs])
                P.op("dve", lambda e, cs=cs, ex=ex: e.tensor_tensor(out=cs[:], in0=cs[:], in1=ex[:], op=MUL), reads=[b_cs, b_ex], writes=[b_cs])
                P.op("dve", lambda e, sn=sn, ex=ex: e.tensor_tensor(out=sn[:], in0=sn[:], in1=ex[:], op=MUL), reads=[b_sn, b_ex], writes=[b_sn])
                tabs[nm] = (cs, sn, b_cs, b_sn)
            ARA, AIA, b_ARA, b_AIA = tabs["A"]
            ARB, AIB, b_ARB, b_AIB = tabs["B"]
            a1 = k.sb("ss_a1", [128, 2, 32], F32, l0)
            b_a1 = Buf("a1")
            for d in range(2):
                i8 = 8 if d == 0 else 0
                i1 = 1 if d == 0 else 7
                dsl = slice(d * 16, (d + 1) * 16)
                for ri in range(2):
                    P.op("dve", lambda e, d=d, ri=ri, i8=i8, dsl=dsl: e.tensor_copy(out=A8c[:, d, :, ri], in_=ARA[:, dsl, i8]), reads=[b_ARA], writes=[b_a8])
                P.op("dve", lambda e, d=d, i8=i8, dsl=dsl: e.tensor_scalar(out=A8s[:, d, :, 0], in0=AIA[:, dsl, i8], scalar1=-1.0, scalar2=None, op0=MUL),
                     reads=[b_AIA], writes=[b_a8])
                P.op("dve", lambda e, d=d, i8=i8, dsl=dsl: e.tensor_copy(out=A8s[:, d, :, 1], in_=AIA[:, dsl, i8]), reads=[b_AIA], writes=[b_a8])
                P.op("dve", lambda e, d=d, i1=i1, dsl=dsl: e.tensor_copy(out=a1[:, 0, dsl], in_=ARA[:, dsl, i1]), reads=[b_ARA], writes=[b_a1])
                P.op("dve", lambda e, d=d, i1=i1, dsl=dsl: e.tensor_copy(out=a1[:, 1, dsl], in_=AIA[:, dsl, i1]), reads=[b_AIA], writes=[b_a1])
            fz = k.sb("ss_fz", [128, 6, 32], F32, l0)
            b_fz = Buf("fz")
            P.op("dve", lambda e: e.tensor_tensor(out=fz[:, 0, :], in0=lre[:], in1=lre[:], op=MUL), reads=[b_l], writes=[b_fz])
            P.op("dve", lambda e: e.tensor_tensor(out=fz[:, 1, :], in0=lim[:], in1=lim[:], op=MUL), reads=[b_l], writes=[b_fz])
            P.op("dve", lambda e: e.tensor_tensor(out=fz[:, 0, :], in0=fz[:, 0, :], in1=fz[:, 1, :], op=ADD), reads=[b_fz], writes=[b_fz])
            P.op("dve", lambda e: e.reciprocal(out=fz[:, 1, :], in_=fz[:, 0, :]), reads=[b_fz], writes=[b_fz])
            P.op("dve", lambda e: e.tensor_scalar(out=fz[:, 0, :], in0=a1[:, 0, :], scalar1=-1.0, scalar2=None, op0=ADD), reads=[b_a1], writes=[b_fz])
            P.op("dve", lambda e: e.tensor_tensor(out=fz[:, 2, :], in0=fz[:, 0, :], in1=lre[:], op=MUL), reads=[b_fz, b_l], writes=[b_fz])
            P.op("dve", lambda e: e.tensor_tensor(out=fz[:, 3, :], in0=a1[:, 1, :], in1=lim[:], op=MUL), reads=[b_a1, b_l], writes=[b_fz])
            P.op("dve", lambda e: e.tensor_tensor(out=fz[:, 2, :], in0=fz[:, 2, :], in1=fz[:, 3, :], op=ADD), reads=[b_fz], writes=[b_fz])
            P.op("dve", lambda e: e.tensor_tensor(out=fz[:, 2, :], in0=fz[:, 2, :], in1=fz[:, 1, :], op=MUL), reads=[b_fz], writes=[b_fz])
            P.op("dve", lambda e: e.tensor_tensor(out=fz[:, 4, :], in0=a1[:, 1, :], in1=lre[:], op=MUL), reads=[b_a1, b_l], writes=[b_fz])
            P.op("dve", lambda e: e.tensor_tensor(out=fz[:, 5, :], in0=fz[:, 0, :], in1=lim[:], op=MUL), reads=[b_fz, b_l], writes=[b_fz])
            P.op("dve", lambda e: e.tensor_tensor(out=fz[:, 4, :], in0=fz[:, 4, :], in1=fz[:, 5, :], op=SUB), reads=[b_fz], writes=[b_fz])
            P.op("dve", lambda e: e.tensor_tensor(out=fz[:, 4, :], in0=fz[:, 4, :], in1=fz[:, 1, :], op=MUL), reads=[b_fz], writes=[b_fz])
            BT = k.sb("ss_BT", [128, 2, 2, 16, 16], F32, l0)
            BB = k.sb("ss_BB", [128, 2, 2, 16, 16], F32, l0)
            CN = k.sb("ss_CN", [128, 2, 2, 2, 128], F32, l0)
            CT = k.sb("ss_CT", [128, 2, 2, 16, 16], F32, l0)
            tA = k.sb("ss_tA", [128, 16, 9, 16], F32, l0)
            tB = k.sb("ss_tB", [128, 16, 9, 16], F32, l0)
            b_BT, b_BB, b_CN, b_CT, b_tA, b_tB = Buf("BT"), Buf("BB"), Buf("CN"), Buf("CT"), Buf("tA"), Buf("tB")
            for d in range(2):
                for ri in range(2):
                    bsrc = I["ssm_bre"] if ri == 0 else I["ssm_bim"]
                    csrc = I["ssm_cre"] if ri == 0 else I["ssm_cim"]
                    for gp in range(2):
                        P.dma("sp", BT[gp * 64:(gp + 1) * 64, d, ri, :, :], bsrc[d, gp * 16:(gp + 1) * 16].rearrange("g p c -> p g c"),
                              writes=[b_BT], sem=csem)
                        for blk in range(2):
                            g0 = gp * 16 + blk * 8
                            P.dma("sp", CN[:, d, ri, blk, gp * 64:(gp + 1) * 64], csrc[d, g0:g0 + 8].rearrange("g c p -> (g c) p"),
                                  writes=[b_CN], sem=csem)
            for d in range(2):
                for ri in range(2):
                    for blk in range(2):
                        pt, bp = nb()
                        P.op("pe", lambda e, d=d, ri=ri, blk=blk, pt=pt: e.transpose(out=pt[:, 0:128], in_=CN[:, d, ri, blk, :], identity=k.ident[:]),
                             reads=[b_CN, k.b_ident], writes=[bp])
                        P.op("dve", lambda e, d=d, ri=ri, blk=blk, pt=pt: e.tensor_copy(
                            out=CT[:, d, ri, blk * 8:(blk + 1) * 8, :].rearrange("p a b -> p (a b)"), in_=pt[:, 0:128]), reads=[bp], writes=[b_CT])
            for d in range(2):
                dsl = slice(d * 16, (d + 1) * 16)
                frb = lambda d=d, dsl=dsl: fz[:, 2, dsl][:, :, None].broadcast_to([128, 16, 16])
                fib = lambda d=d, dsl=dsl: fz[:, 4, dsl][:, :, None].broadcast_to([128, 16, 16])
                t16a = tA[:, :, 0, :]
                t16b = tB[:, :, 0, :]
                P.op("dve", lambda e, d=d, frb=frb: e.tensor_tensor(out=t16a, in0=BT[:, d, 0], in1=frb(), op=MUL), reads=[b_BT, b_fz], writes=[b_tA])
                P.op("dve", lambda e, d=d, fib=fib: e.tensor_tensor(out=t16b, in0=BT[:, d, 1], in1=fib(), op=MUL), reads=[b_BT, b_fz], writes=[b_tB])
                P.op("dve", lambda e, d=d: e.tensor_tensor(out=BB[:, d, 0], in0=t16a, in1=t16b, op=SUB), reads=[b_tA, b_tB], writes=[b_BB])
                P.op("dve", lambda e, d=d, frb=frb: e.tensor_tensor(out=t16a, in0=BT[:, d, 1], in1=frb(), op=MUL), reads=[b_BT, b_fz], writes=[b_tA])
                P.op("dve", lambda e, d=d, fib=fib: e.tensor_tensor(out=t16b, in0=BT[:, d, 0], in1=fib(), op=MUL), reads=[b_BT, b_fz], writes=[b_tB])
                P.op("dve", lambda e, d=d: e.tensor_tensor(out=BB[:, d, 1], in0=t16a, in1=t16b, op=ADD), reads=[b_tA, b_tB], writes=[b_BB])
            P.op("pool", lambda e: e.memset(RCp[:].rearrange("p a b c d -> p (a b c d)"), 0.0), writes=[b_rcp])
            for d in range(2):
                dsl = slice(d * 16, (d + 1) * 16)
                off = 112 if d == 0 else 0
                bc_c = lambda ri, d=d: CT[:, d, ri][:, :, None, :].broadcast_to([128, 16, 9, 16])
                bc_ar = lambda dsl=dsl: ARA[:, dsl, :][:, :, :, None].broadcast_to([128, 16, 9, 16])
                bc_ai = lambda dsl=dsl: AIA[:, dsl, :][:, :, :, None].broadcast_to([128, 16, 9, 16])
                dst = lambda ri, d=d, off=off: RCp[:, d, ri, :, off:off + 144].rearrange("p g (t c) -> p g t c", c=16)
                P.op("dve", lambda e, bc_c=bc_c, bc_ar=bc_ar: e.tensor_tensor(out=tA[:], in0=bc_c(0), in1=bc_ar(), op=MUL), reads=[b_CT, b_ARA], writes=[b_tA])
                P.op("dve", lambda e, bc_c=bc_c, bc_ai=bc_ai: e.tensor_tensor(out=tB[:], in0=bc_c(1), in1=bc_ai(), op=MUL), reads=[b_CT, b_AIA], writes=[b_tB])
                P.op("dve", lambda e, dst=dst: e.tensor_tensor(out=dst(0), in0=tA[:], in1=tB[:], op=SUB), reads=[b_tA, b_tB], writes=[b_rcp])
                P.op("dve", lambda e, bc_c=bc_c, bc_ai=bc_ai: e.tensor_tensor(out=tA[:], in0=bc_c(0), in1=bc_ai(), op=MUL), reads=[b_CT, b_AIA], writes=[b_tA])
                P.op("dve", lambda e, bc_c=bc_c, bc_ar=bc_ar: e.tensor_tensor(out=tB[:], in0=bc_c(1), in1=bc_ar(), op=MUL), reads=[b_CT, b_ARA], writes=[b_tB])
                P.op("dve", lambda e: e.tensor_tensor(out=tA[:], in0=tA[:], in1=tB[:], op=ADD), reads=[b_tA, b_tB], writes=[b_tA])
                P.op("dve", lambda e, dst=dst: e.tensor_scalar(out=dst(1), in0=tA[:], scalar1=-1.0, scalar2=None, op0=MUL), reads=[b_tA], writes=[b_rcp])
            Lp = k.sb("ss_Lp", [128, 64, 240], BF16, l0)
            b_Lp = Buf("Lp")
            P.op("pool", lambda e: e.memset(Lp[:].rearrange("p a b -> p (a b)"), 0.0), writes=[b_Lp])
            P.op("pool", lambda e: e.tensor_copy(out=Lp[:, :, 112:128], in_=BB[:].rearrange("p a b c d -> p (a b c) d")), reads=[b_BB], writes=[b_Lp])
            BW = k.sb("ss_BW", [128, 2, 2, 16, 128], F32, l0)
            b_BW = Buf("BW")
            for d in range(2):
                dsl = slice(d * 16, (d + 1) * 16)
                bc_b = lambda ri, d=d: BB[:, d, ri][:, :, None, :].broadcast_to([128, 16, 8, 16])
                bc_ar = lambda dsl=dsl: ARB[:, dsl, :][:, :, :, None].broadcast_to([128, 16, 8, 16])
                bc_ai = lambda dsl=dsl: AIB[:, dsl, :][:, :, :, None].broadcast_to([128, 16, 8, 16])
                dst = lambda ri, d=d: BW[:, d, ri].rearrange("p g (t c) -> p g t c", c=16)
                ta8 = tA[:, :, 0:8, :]
                tb8 = tB[:, :, 0:8, :]
                P.op("dve", lambda e, bc_b=bc_b, bc_ar=bc_ar: e.tensor_tensor(out=ta8, in0=bc_b(0), in1=bc_ar(), op=MUL), reads=[b_BB, b_ARB], writes=[b_tA])
                P.op("dve", lambda e, bc_b=bc_b, bc_ai=bc_ai: e.tensor_tensor(out=tb8, in0=bc_b(1), in1=bc_ai(), op=MUL), reads=[b_BB, b_AIB], writes=[b_tB])
                P.op("dve", lambda e, dst=dst: e.tensor_tensor(out=dst(0), in0=ta8, in1=tb8, op=SUB), reads=[b_tA, b_tB], writes=[b_BW])
                P.op("dve", lambda e, bc_b=bc_b, bc_ai=bc_ai: e.tensor_tensor(out=ta8, in0=bc_b(0), in1=bc_ai(), op=MUL), reads=[b_BB, b_AIB], writes=[b_tA])
                P.op("dve", lambda e, bc_b=bc_b, bc_ar=bc_ar: e.tensor_tensor(out=tb8, in0=bc_b(1), in1=bc_ar(), op=MUL), reads=[b_BB, b_ARB], writes=[b_tB])
                P.op("dve", lambda e, dst=dst: e.tensor_tensor(out=dst(1), in0=ta8, in1=tb8, op=ADD), reads=[b_tA, b_tB], writes=[b_BW])
            for d in range(2):
                for g2 in range(16):
                    for ri in range(2):
                        pt, bp = nb()
                        P.op("pe", lambda e, d=d, g2=g2, ri=ri, pt=pt: e.transpose(out=pt[:, 0:128], in_=BW[:, d, ri, g2, :], identity=k.ident[:]),
                             reads=[b_BW, k.b_ident], writes=[bp])
                        eng = "act" if (g2 + ri) % 2 else "dve"
                        if eng == "act":
                            P.op("act", lambda e, d=d, g2=g2, ri=ri, pt=pt: e.activation(out=WT[:, d, g2, ri, :], in_=pt[:, 0:128], func=AF.Copy), reads=[bp], writes=[b_wt])
                        else:
                            P.op("dve", lambda e, d=d, g2=g2, ri=ri, pt=pt: e.tensor_copy(out=WT[:, d, g2, ri, :], in_=pt[:, 0:128]), reads=[bp], writes=[b_wt])
            for g2 in range(16):
                for gp in range(2):
                    g = gp * 16 + g2
                    pt, bp = nb()
                    psl = slice(gp * 64, (gp + 1) * 64)
                    n = 0
                    for d in range(2):
                        for ri in range(2):
                            for s_ in range(8):
                                w0 = (7 - s_) * 16 if d == 0 else (8 - s_) * 16
                                l0_ = (7 - s_) * 16
                                P.op("pe", lambda e, d=d, ri=ri, g2=g2, w0=w0, l0_=l0_, psl=psl, pt=pt, n=n: e.matmul(
                                    pt[:, 0:128], lhsT=Lp[psl, (d * 2 + ri) * 16 + g2, l0_:l0_ + 128], rhs=RCp[psl, d, ri, g2, w0:w0 + 128],
                                    start=(n == 0), stop=(n == 31)), reads=[b_Lp, b_rcp], writes=[bp])
                                n += 1
                    P.op("dve" if g % 2 else "act",
                         (lambda e, g=g, pt=pt: e.tensor_copy(out=ToepT[:, g, :], in_=pt[:, 0:128])) if g % 2 else
                         (lambda e, g=g, pt=pt: e.activation(out=ToepT[:, g, :], in_=pt[:, 0:128], func=AF.Copy)),
                         reads=[bp], writes=[b_toep[g]])
            P.barrier()
        Ubuf = k.sb("ss_ubuf", [128, 32, 320], BF16, ls)
        Zbf = k.sb("ss_zbf", [128, 2, 16, 2, 288], BF16, ls)
        ygT = k.sb("ss_ygT", [128, 4, L], BF16, ls)
        if k.debug.get("_ssm_upto", 99) < 1:
            return
        with ExitStack() as l1:
            ucm = [k.sb(f"ss_ucm{i}", [128, 8, 512], F32, l1) for i in range(2)]
            b_ucm = [Buf(f"ucm{i}") for i in range(2)]
            usem = [P.new_dsem(f"ss_us{i}") for i in range(2)]
            ucg = k.sb("ss_ucg", [128, 32, 128], F32, l1)
            b_ucg = Buf("ucg")
            for jt in range(3):
                si = jt % 2
                nj = 128 if jt < 2 else 32
                r0 = jt * 1024
                P.dma("sp", ucm[si][0:nj], S["u"][r0:r0 + nj * 8, :].rearrange("(j s) c -> j s c", s=8), writes=[b_ucm[si]], sem=usem[si])
                P.op("dve", lambda e, si=si, nj=nj: e.tensor_copy(out=ucg[0:nj].rearrange("p g (s c) -> p g s c", c=16),
                                                                 in_=ucm[si][0:nj].rearrange("p s (g c) -> p g s c", c=16)),
                     reads=[b_ucm[si]], writes=[b_ucg])
                for g0 in range(0, 32, 4):
                    pt, bp = nb()
                    for gg in range(4):
                        g = g0 + gg
                        P.op("pe", lambda e, si=si, nj=nj, g=g, gg=gg, pt=pt: e.transpose(
                            out=pt[:, gg * 128:gg * 128 + nj], in_=ucg[0:nj, g, :], identity=k.ident[0:nj, 0:nj]),
                            reads=[b_ucg, k.b_ident], writes=[bp])
                    src = lambda pt=pt, nj=nj: pt[:].rearrange("p (a b) -> p a b", b=128)[:, :, 0:nj]
                    cols = [32 + jt * 128] if jt < 2 else [0, 288]
                    for ci, c0 in enumerate(cols):
                        eng = "act" if (g0 // 4 + ci) % 2 else "dve"
                        if eng == "act":
                            P.op("act", lambda e, g0=g0, c0=c0, nj=nj, src=src: e.activation(out=Ubuf[:, g0:g0 + 4, c0:c0 + nj], in_=src(), func=AF.Copy),
                                 reads=[bp], writes=[b_U[g0 + i] for i in range(4)])
                        else:
                            P.op("dve", lambda e, g0=g0, c0=c0, nj=nj, src=src: e.tensor_copy(out=Ubuf[:, g0:g0 + 4, c0:c0 + nj], in_=src()),
                                 reads=[bp], writes=[b_U[g0 + i] for i in range(4)])
            P.barrier()
        if k.debug.get("_ssm_upto", 99) < 2:
            return
        with ExitStack() as l2:
            Z = [k.sb(f"ss_Z{d}", [128, 16, 2, 288], F32, l2) for d in range(2)]
            b_Z = [Buf("Z0"), Buf("Z1")]
            for d in range(2):
                j0 = 0 if d == 0 else 32
                for g2 in range(16):
                    for ri in range(2):
                        pt, bp = nb()
                        for gp in range(2):
                            P.op("pe", lambda e, d=d, g2=g2, ri=ri, gp=gp, pt=pt, j0=j0: e.matmul(
                                pt[gp * 64:(gp + 1) * 64, 0:288], lhsT=WT[:, d, g2, ri, gp * 64:(gp + 1) * 64], rhs=Ubuf[:, gp * 16 + g2, j0:j0 + 288],
                                start=True, stop=True), reads=[b_wt, b_U[gp * 16 + g2]], writes=[bp])
                        if (g2 + ri) % 2:
                            P.op("act", lambda e, d=d, g2=g2, ri=ri, pt=pt: e.activation(out=Z[d][:, g2, ri, :], in_=pt[:, 0:288], func=AF.Copy), reads=[bp], writes=[b_Z[d]])
                        else:
                            P.op("dve", lambda e, d=d, g2=g2, ri=ri, pt=pt: e.tensor_copy(out=Z[d][:, g2, ri, :], in_=pt[:, 0:288]), reads=[bp], writes=[b_Z[d]])
            k.dbg("V_dbg", [2, 128, 16 * 2 * 288], F32, lambda dd: (dd[0], Z[0][:].rearrange("p a b c -> p (a b c)")), [b_Z[0]])
            k.dbg("V_dbg", [2, 128, 16 * 2 * 288], F32, lambda dd: (dd[1], Z[1][:].rearrange("p a b c -> p (a b c)")), [b_Z[1]])
            m1 = [k.sb(f"ss_m1{d}", [128, 16, 2], F32, l2) for d in range(2)]
            m2 = [k.sb(f"ss_m2{d}", [128, 16, 2], F32, l2) for d in range(2)]
            b_m1 = [Buf("m10"), Buf("m11")]
            b_m2 = [Buf("m20"), Buf("m21")]

            def scan_step(d, J, Jp):
                eng = "dve" if d == 0 else "pool"
                P.op(eng, lambda e: e.tensor_tensor(out=m1[d][:], in0=Z[d][:, :, :, Jp], in1=A8c[:, d], op=MUL), reads=[b_Z[d], b_a8], writes=[b_m1[d]])
                P.op(eng, lambda e: e.tensor_tensor(out=m2[d][:], in0=Z[d][:, :, ::-1, Jp], in1=A8s[:, d], op=MUL), reads=[b_Z[d], b_a8], writes=[b_m2[d]])
                P.op(eng, lambda e: e.tensor_tensor(out=m1[d][:], in0=m1[d][:], in1=m2[d][:], op=ADD), reads=[b_m1[d], b_m2[d]], writes=[b_m1[d]])
                P.op(eng, lambda e: e.tensor_tensor(out=Z[d][:, :, :, J], in0=Z[d][:, :, :, J], in1=m1[d][:], op=ADD), reads=[b_Z[d], b_m1[d]], writes=[b_Z[d]])

            for st_ in range(1, 288):
                scan_step(0, st_, st_ - 1)
                scan_step(1, 287 - st_, 288 - st_)
            for d in range(2):
                eng = "dve" if d == 0 else "pool"
                P.op(eng, lambda e, d=d: e.tensor_copy(out=Zbf[:, d].rearrange("p a b c -> p (a b c)"), in_=Z[d][:].rearrange("p a b c -> p (a b c)")),
                     reads=[b_Z[d]], writes=[b_zbf[d]])
            k.dbg("Z_dbg", [2, 128, 16 * 2 * 288], F32, lambda dd: (dd[0], Z[0][:].rearrange("p a b c -> p (a b c)")), [b_Z[0]])
            k.dbg("Z_dbg", [2, 128, 16 * 2 * 288], F32, lambda dd: (dd[1], Z[1][:].rearrange("p a b c -> p (a b c)")), [b_Z[1]])
            P.barrier()
        if k.debug.get("_ssm_upto", 99) < 3:
            return
        with ExitStack() as l3:
            ycm = k.sb("ss_ycm", [128, 8, 512], F32, l3)
            b_ycm = [Buf(f"ycm{g}") for g in range(32)]
            ut = k.sb("ss_ut", [128, 8, 512], F32, l3)
            b_ut = Buf("ut")
            utsem = P.new_dsem("ss_uts")
            Dfull = k.sb("ss_D", [128, 512], F32, l3)
            b_D = Buf("Dfull")
            P.dma("sp", Dfull[:], I["ssm_d"][0:1, :].broadcast_to([128, 512]), writes=[b_D], sem=csem)
            sq = [k.sb(f"ss_sq{i}", [128, 512], F32, l3) for i in range(2)]
            b_sq = [Buf("sq0"), Buf("sq1")]
            GC = math.sqrt(2.0 / math.pi)
            for jt in range(2):
                P.dma("sp", ut[:], S["u"][jt * 1024:(jt + 1) * 1024, :].rearrange("(j s) c -> j s c", s=8), writes=[b_ut], sem=utsem)
                for g in range(32):
                    gp, g2 = g // 16, g % 16
                    psl = slice(gp * 64, (gp + 1) * 64)
                    pt, bp = nb()
                    c0 = 32 + jt * 128
                    P.op("pe", lambda e, g=g, c0=c0, pt=pt: e.matmul(pt[:, 0:128], lhsT=Ubuf[:, g, c0:c0 + 128], rhs=ToepT[:, g, :], start=True, stop=False),
                         reads=[b_U[g], b_toep[g]], writes=[bp])
                    for d in range(2):
                        jz = (31 + jt * 128) if d == 0 else (1 + jt * 128)
                        w0 = 128 if d == 0 else 0
                        for ri in range(2):
                            last = (d == 1 and ri == 1)
                            P.op("pe", lambda e, d=d, ri=ri, g2=g2, psl=psl, jz=jz, w0=w0, pt=pt, last=last: e.matmul(
                                pt[:, 0:128], lhsT=Zbf[psl, d, g2, ri, jz:jz + 128], rhs=RCp[psl, d, ri, g2, w0:w0 + 128], start=False, stop=last),
                                reads=[b_zbf[d], b_rcp], writes=[bp])
                    src = lambda pt=pt: pt[:, 0:128].rearrange("p (t c) -> p t c", c=16)
                    P.op("dve", lambda e, g=g, src=src: e.tensor_tensor(out=ycm[:, :, g * 16:(g + 1) * 16], in0=ut[:, :, g * 16:(g + 1) * 16],
                                                                       in1=Dfull[:, g * 16:(g + 1) * 16][:, None, :].broadcast_to([128, 8, 16]), op=MUL),
                         reads=[b_ut, b_D], writes=[b_ycm[g]])
                    P.op("dve", lambda e, g=g, src=src: e.tensor_tensor(out=ycm[:, :, g * 16:(g + 1) * 16], in0=ycm[:, :, g * 16:(g + 1) * 16], in1=src(), op=ADD),
                         reads=[bp, b_ycm[g]], writes=[b_ycm[g]])
                k.dbg("y_dbg", [L, 512], F32, lambda dd, jt=jt: (dd[jt * 1024:(jt + 1) * 1024, :].rearrange("(j s) c -> j s c", s=8), ycm[:]), b_ycm)
                for t in range(8):
                    i = t % 2
                    P.op("dve", lambda e, t=t, i=i: e.tensor_tensor(out=sq[i][:], in0=ycm[:, t, :], in1=ycm[:, t, :], op=MUL), reads=b_ycm, writes=[b_sq[i]])
                    P.op("dve", lambda e, t=t, i=i: e.tensor_scalar(out=sq[i][:], in0=sq[i][:], scalar1=0.044715, scalar2=1.0, op0=MUL, op1=ADD), reads=[b_sq[i]], writes=[b_sq[i]])
                    P.op("dve", lambda e, t=t, i=i: e.tensor_tensor(out=sq[i][:], in0=sq[i][:], in1=ycm[:, t, :], op=MUL), reads=[b_sq[i]] + b_ycm, writes=[b_sq[i]])
                    P.op("act", lambda e, t=t, i=i: e.activation(out=sq[i][:], in_=sq[i][:], func=AF.Sigmoid, scale=2.0 * GC), reads=[b_sq[i]], writes=[b_sq[i]])
                    P.op("dve", lambda e, t=t, i=i: e.tensor_tensor(out=sq[i][:], in0=sq[i][:], in1=ycm[:, t, :], op=MUL), reads=[b_sq[i]] + b_ycm, writes=[b_sq[i]])
                    pt, bp = nb()
                    for chb in range(4):
                        P.op("pe", lambda e, i=i, chb=chb, pt=pt: e.transpose(out=pt[:, chb * 128:(chb + 1) * 128], in_=sq[i][:, chb * 128:(chb + 1) * 128], identity=k.ident[:]),
                             reads=[b_sq[i], k.b_ident], writes=[bp])
                    tsl = slice(jt * 1024 + t, (jt + 1) * 1024, 8)
                    P.op("act", lambda e, pt=pt, tsl=tsl: e.activation(out=ygT[:, :, tsl], in_=pt[:].rearrange("p (a b) -> p a b", b=128), func=AF.Copy),
                         reads=[bp], writes=[b_ygT])
            P.barrier()
        if k.debug.get("_ssm_upto", 99) < 4:
            return
        with ExitStack() as l4:
            wg32 = k.sb("ss_wg32", [128, 4, 512], F32, l4)
            wg = k.sb("ss_wg", [128, 4, 512], BF16, l4)
            bg = k.sb("ss_bg", [128, 4], F32, l4)
            b_wg32, b_wg, b_bg = Buf("wg32"), Buf("wg"), Buf("bg")
            P.dma("sp", wg32[:], I["w_glu"].rearrange("(fc p) c -> p fc c", p=128), writes=[b_wg32], sem=csem)
            P.dma("sp", bg[:], I["b_glu"].rearrange("(fc p) -> p fc", p=128), writes=[b_bg], sem=csem, allow_slow_non_contiguous=True)
            P.op("dve", lambda e: e.tensor_copy(out=wg[:], in_=wg32[:]), reads=[b_wg32], writes=[b_wg])
            gst = [k.sb(f"ss_gst{i}", [128, 512], BF16, l4) for i in range(2)]
            b_gst = [Buf("gst0"), Buf("gst1")]
            gsem = [P.new_dsem(f"ss_gs{i}") for i in range(2)]
            sg = [k.sb(f"ss_sg{i}", [128, 512], F32, l4) for i in range(2)]
            b_sg = [Buf("sg0"), Buf("sg1")]
            so = [k.sb(f"ss_so{i}", [128, 512], BF16, l4) for i in range(2)]
            b_so = [Buf("so0"), Buf("so1")]
            sosem = [P.new_dsem(f"ss_sos{i}") for i in range(2)]
            ui = 0
            for fo in range(4):
                for tb in range(4):
                    i = ui % 2
                    ui += 1
                    tsl = slice(tb * 512, (tb + 1) * 512)
                    P.dma("sp", gst[i][:], S["sgsT"][fo * 128:(fo + 1) * 128, tsl], writes=[b_gst[i]], sem=gsem[i])
                    pt, bp = nb()
                    for fc in range(4):
                        P.op("pe", lambda e, fc=fc, fo=fo, tsl=tsl, pt=pt: e.matmul(pt[:], lhsT=wg[:, fc, fo * 128:(fo + 1) * 128], rhs=ygT[:, fc, tsl],
                                                                               start=(fc == 0), stop=(fc == 3)), reads=[b_wg, b_ygT], writes=[bp])
                    P.op("act", lambda e, i=i, fo=fo, pt=pt: e.activation(out=sg[i][:], in_=pt[:], func=AF.Sigmoid, bias=bg[:, fo:fo + 1]),
                         reads=[bp, b_bg], writes=[b_sg[i]])
                    P.op("dve", lambda e, i=i, fo=fo, tsl=tsl: e.tensor_tensor(out=sg[i][:], in0=sg[i][:], in1=ygT[:, fo, tsl], op=MUL),
                         reads=[b_sg[i], b_ygT], writes=[b_sg[i]])
                    P.op("dve", lambda e, i=i: e.tensor_tensor(out=so[i][:], in0=sg[i][:], in1=gst[i][:], op=MUL),
                         reads=[b_sg[i], b_gst[i]], writes=[b_so[i]])
                    P.dma("sp", S["sbrT"][fo * 128:(fo + 1) * 128, tsl], so[i][:], reads=[b_so[i]], sem=sosem[i])


def phase_attn(k):
    nc, P, I, S = k.nc, k.P, k.I, k.S
    with ExitStack() as ls:
        lamv = k.sb("at_lamv", [128, 4, 64], F32, ls)
        lw = k.sb("at_lw", [128, 8], F32, ls)
        G = k.sb("at_G", [128, 128], F32, ls)
        b_lamv, b_lw, b_G = Buf("lamv"), Buf("lw"), Buf("G")
        csem = P.new_dsem("at_c")
        P.dma("sp", lamv[:].rearrange("p a b -> p (a b)"), I["lam"].rearrange("a b -> (a b)").partition_broadcast(128),
              writes=[b_lamv], sem=csem)
        P.dma("sp", G[:], I["subln_g"][0:1, :].broadcast_to([128, 128]), writes=[b_G], sem=csem)
        P.op("dve", lambda e: e.tensor_scalar(out=G[:], in0=G[:], scalar1=(1.0 - LAM_INIT), scalar2=None, op0=ALU.mult),
             reads=[b_G], writes=[b_G])
        for i in range(2):
            P.op("dve", lambda e, i=i: e.tensor_tensor(out=lamv[:, 2 * i, :], in0=lamv[:, 2 * i, :], in1=lamv[:, 2 * i + 1, :], op=ALU.mult),
                 reads=[b_lamv], writes=[b_lamv])
            P.op("dve", lambda e, i=i: e.tensor_reduce(out=lw[:, i:i + 1], in_=lamv[:, 2 * i, :], axis=mybir.AxisListType.X, op=ALU.add),
                 reads=[b_lamv], writes=[b_lw])
        P.op("act", lambda e: e.activation(out=lw[:, 2:4], in_=lw[:, 0:2], func=AF.Exp), reads=[b_lw], writes=[b_lw])
        P.op("dve", lambda e: e.tensor_tensor(out=lw[:, 4:5], in0=lw[:, 3:4], in1=lw[:, 2:3], op=ALU.subtract), reads=[b_lw], writes=[b_lw])
        P.op("dve", lambda e: e.tensor_scalar(out=lw[:, 5:6], in0=lw[:, 4:5], scalar1=-LAM_INIT, scalar2=None, op0=ALU.add),
             reads=[b_lw], writes=[b_lw])
        neglam = lw[:, 5:6]
        qTs = [k.sb(f"at_q{i}", [128, L], BF16, ls) for i in range(2)]
        kTs = [k.sb(f"at_k{i}", [128, LT], BF16, ls) for i in range(2)]
        Vs = [k.sb(f"at_v{i}", [128, 18, 130], BF16, ls) for i in range(2)]
        gas = [k.sb(f"at_ga{i}", [128, 16, 128], BF16, ls) for i in range(2)]
        aTs = [k.sb(f"at_aT{i}", [128, L], BF16, ls) for i in range(2)]
        b_q = [Buf(f"atq{i}") for i in range(2)]
        b_k = [Buf(f"atk{i}") for i in range(2)]
        b_v = [Buf(f"atv{i}") for i in range(2)]
        b_ga = [Buf(f"atga{i}") for i in range(2)]
        b_aT = [Buf(f"ataT{i}") for i in range(2)]
        hsem = [P.new_dsem(f"at_h{i}") for i in range(2)]
        asem = [P.new_dsem(f"at_a{i}") for i in range(2)]
        for i in range(2):
            P.op("pool", lambda e, i=i: e.memset(Vs[i][:, :, 128:130], 1.0), writes=[b_v[i]])
        PT = [k.sb(f"at_pt{i}", [128, 2, 18, 256], BF16, ls) for i in range(2)]
        b_PT = [[[Buf(f"pt{i}_{c}_{kp}") for kp in range(9)] for c in range(2)] for i in range(2)]
        sbk = [k.ps(f"at_s{i}", [128, 512], F32, ls) for i in range(3)]
        b_sbk = [Buf(f"ats{i}") for i in range(3)]
        obk = [k.ps(f"at_o{i}", [128, 512], F32, ls) for i in range(4)]
        b_obk = [Buf(f"ato{i}") for i in range(4)]
        tbk = k.ps("at_t", [128, 512], F32, ls)
        b_tbk = Buf("att")
        sm = [k.sb(f"at_sm{i}", [128, 8], F32, ls) for i in range(2)]
        b_sm = [Buf(f"atsm{i}") for i in range(2)]
        tmp = [k.sb(f"at_tmp{i}", [128, 128], F32, ls) for i in range(2)]
        b_tmp = [Buf(f"attmp{i}") for i in range(2)]
        ov = [k.sb(f"at_ov{i}", [128, 128], F32, ls) for i in range(2)]
        b_ov = [Buf(f"atov{i}") for i in range(2)]
        junk = k.sb("at_junk", [128, 128], F32, ls)
        b_junk = Buf("atjunk")
        cnt = {"s": 0, "u": 0}

        def load_head(h):
            s = h % 2
            P.dma("sp", qTs[s][:], S["qT"][h], writes=[b_q[s]], sem=hsem[s])
            P.dma("sp", kTs[s][:], S["kT"][h], writes=[b_k[s]], sem=hsem[s])
            P.dma("sp", Vs[s][:, :, 0:128], S["v"][:, h * 128:(h + 1) * 128].rearrange("(t p) e -> p t e", p=128),
                  writes=[b_v[s]], sem=hsem[s])
            P.dma("sp", gas[s][:], S["sga"][:, h * 128:(h + 1) * 128].rearrange("(t p) e -> p t e", p=128),
                  writes=[b_ga[s]], sem=hsem[s])

        def A_steps(h, qb):
            s = h % 2
            ps_ = qb % 2
            steps = []
            for kp in range(9):
                def step(kp=kp):
                    for c in range(2):
                        si = cnt["s"] % 3
                        cnt["s"] += 1
                        for j in range(2):
                            kt = 2 * kp + j
                            P.op("pe", lambda e, kt=kt, j=j, c=c, si=si: e.matmul(
                                sbk[si][:, j * 256:(j + 1) * 256], lhsT=kTs[s][c * 64:(c + 1) * 64, kt * 128:(kt + 1) * 128],
                                rhs=qTs[s][c * 64:(c + 1) * 64, qb * 256:(qb + 1) * 256], start=True, stop=True),
                                reads=[b_k[s], b_q[s]], writes=[b_sbk[si]])
                        P.op("act", lambda e, c=c, kp=kp, si=si: e.activation(
                            out=PT[ps_][:, c, 2 * kp:2 * kp + 2, :].rearrange("p a b -> p (a b)"), in_=sbk[si][:], func=AF.Exp, scale=0.125),
                            reads=[b_sbk[si]], writes=[b_PT[ps_][c][kp]])
                steps.append(step)
            return steps

        def B_gen(h, qb):
            s = h % 2
            ps_ = qb % 2
            for qi_ in range(2):
                yield from unitB(h, qb, qi_, s, ps_)

        def unitB(h, qb, qi, s, ps_):
            if True:
                qt = qb * 2 + qi
                u = cnt["u"] % 2
                cnt["u"] += 1
                banks = [obk[u * 2], obk[u * 2 + 1]]
                bb = [b_obk[u * 2], b_obk[u * 2 + 1]]
                for c in range(2):
                    for kt in range(18):
                        P.op("pe", lambda e, c=c, kt=kt: e.matmul(
                            banks[c][:, 0:129], lhsT=PT[ps_][:, c, kt, qi * 128:(qi + 1) * 128], rhs=Vs[s][:, kt, 0:129],
                            start=(kt == 0), stop=(kt == 17)),
                            reads=[b_PT[ps_][c][kt // 2], b_v[s]], writes=[bb[c]])
                        yield
                smt, bsm = sm[u], b_sm[u]
                for c in range(2):
                    P.op("dve", lambda e, c=c: e.reciprocal(out=smt[:, c:c + 1], in_=banks[c][:, 128:129]), reads=[bb[c]], writes=[bsm])
                P.op("dve", lambda e: e.tensor_tensor(out=smt[:, 2:3], in0=smt[:, 1:2], in1=neglam, op=ALU.mult), reads=[bsm, b_lw], writes=[bsm])
                P.op("dve", lambda e: e.tensor_scalar(out=tmp[u][:], in0=banks[1][:, 0:128], scalar1=smt[:, 2:3], scalar2=None, op0=ALU.mult),
                     reads=[bb[1], bsm], writes=[b_tmp[u]])
                P.op("dve", lambda e: e.scalar_tensor_tensor(out=ov[u][:], in0=banks[0][:, 0:128], scalar=smt[:, 0:1], in1=tmp[u][:],
                                                            op0=ALU.mult, op1=ALU.add),
                     reads=[bb[0], bsm, b_tmp[u]], writes=[b_ov[u]])
                P.op("dve", lambda e: e.tensor_tensor(out=tmp[u][:], in0=ov[u][:], in1=ov[u][:], op=ALU.mult),
                     reads=[b_ov[u]], writes=[b_tmp[u]])
                P.op("dve", lambda e: e.tensor_reduce(out=smt[:, 3:4], in_=tmp[u][:], axis=mybir.AxisListType.X, op=ALU.add),
                     reads=[b_tmp[u]], writes=[bsm])
                P.op("dve", lambda e: e.tensor_scalar(out=smt[:, 4:5], in0=smt[:, 3:4], scalar1=1.0 / 128, scalar2=EPS, op0=ALU.mult, op1=ALU.add),
                     reads=[bsm], writes=[bsm])
                P.op("act", lambda e: e.activation(out=smt[:, 5:6], in_=smt[:, 4:5], func=AF.Ln), reads=[bsm], writes=[bsm])
                P.op("act", lambda e: e.activation(out=smt[:, 6:7], in_=smt[:, 5:6], func=AF.Exp, scale=-0.5), reads=[bsm], writes=[bsm])
                P.op("dve", lambda e: e.scalar_tensor_tensor(out=ov[u][:], in0=ov[u][:], scalar=smt[:, 6:7], in1=G[:], op0=ALU.mult, op1=ALU.mult),
                     reads=[b_ov[u], bsm, b_G], writes=[b_ov[u]])
                P.op("pool", lambda e: e.tensor_tensor(out=ov[u][:], in0=ov[u][:], in1=gas[s][:, qt, :], op=ALU.mult),
                     reads=[b_ov[u], b_ga[s]], writes=[b_ov[u]])
                P.op("pe", lambda e: e.transpose(out=tbk[:, 0:128], in_=ov[u][:], identity=k.ident[:]), reads=[b_ov[u], k.b_ident], writes=[b_tbk])
                P.op("dve", lambda e: e.tensor_copy(out=aTs[s][:, qt * 128:(qt + 1) * 128], in_=tbk[:, 0:128]),
                     reads=[b_tbk], writes=[b_aT[s]])

        def interleave(a_steps, bgen, per=8):
            for st_ in a_steps:
                st_()
                if bgen is not None:
                    for _ in range(per):
                        try:
                            next(bgen)
                        except StopIteration:
                            bgen = None
                            break
            if bgen is not None:
                for _ in bgen:
                    pass

        load_head(0)
        load_head(1)
        interleave(A_steps(0, 0), None)
        for h in range(HEADS):
            for qb in range(8):
                if qb + 1 < 8:
                    nxt = A_steps(h, qb + 1)
                elif h + 1 < HEADS:
                    nxt = A_steps(h + 1, 0)
                else:
                    nxt = []
                interleave(nxt, B_gen(h, qb))
            P.dma("pool", S["abrT"][h * 128:(h + 1) * 128, :], aTs[h % 2][:], reads=[b_aT[h % 2]], sem=asem[h % 2])
            if h + 2 < HEADS:
                load_head(h + 2)


def phase_merge(k):
    nc, P, I, S = k.nc, k.P, k.I, k.S
    with ExitStack() as ls:
        mT = k.sb("mg_mT", [128, NKC, L], BF16, ls)
        b_mT = [Buf(f"mT{tb}") for tb in range(4)]
        wout = k.sb("mg_wout", [128, NKC, D], BF16, ls)
        b_wout = Buf("wout")
        NXB = 2
        wov = I["w_out"].rearrange("(kc p) c -> p kc c", p=128)
        wo_state = {"kc": 0}
        b_woutc = [Buf(f"woutc{i}") for i in range(NKC)]

        def load_wout_chunk():
            kc = wo_state["kc"]
            if kc >= NKC:
                return
            wo_state["kc"] += 1
            P.dma("pool", wout[:, kc, :], wov[:, kc, :], writes=[b_woutc[kc]])
        with ExitStack() as l1:
            abrT = k.sb("mg_abrT", [128, 8, L], BF16, l1)
            sbrT = k.sb("mg_sbrT", [128, 4, L], BF16, l1)
            b_abrT, b_sbrT = Buf("abrT"), Buf("sbrT")
            lsem = P.new_dsem("mg_l")
            P.dma("sp", abrT[:], S["abrT"].rearrange("(fc p) t -> p fc t", p=128), writes=[b_abrT], sem=lsem)
            P.dma("sp", sbrT[:], S["sbrT"].rearrange("(fc p) t -> p fc t", p=128), writes=[b_sbrT], sem=lsem)
            NWS = 2
            wbf = [k.sb(f"mg_wbf{i}", [128, 12, 128], BF16, l1) for i in range(NWS)]
            b_wbf = [Buf(f"mgwbf{i}") for i in range(NWS)]
            NG = 2
            gt = [k.sb(f"mg_gt{i}", [128, 2, L], BF16, l1) for i in range(NG)]
            b_gt = [Buf(f"mggt{i}") for i in range(NG)]
            t1 = [k.sb(f"mg_t1{i}", [128, 512], F32, l1) for i in range(2)]
            t2 = [k.sb(f"mg_t2{i}", [128, 512], F32, l1) for i in range(2)]
            b_t1 = [Buf(f"mgt1{i}") for i in range(2)]
            b_t2 = [Buf(f"mgt2{i}") for i in range(2)]
            pa = [k.ps(f"mg_pa{i}", [128, 512], F32, l1) for i in range(2)]
            pp = [k.ps(f"mg_pp{i}", [128, 512], F32, l1) for i in range(2)]
            b_pa = [Buf(f"mgpa{i}") for i in range(2)]
            b_pp = [Buf(f"mgpp{i}") for i in range(2)]
            wpa_v = I["w_pa"].rearrange("(fc p) c -> p fc c", p=128)
            wps_v = I["w_ps"].rearrange("(fc p) c -> p fc c", p=128)
            ui = 0

            def load_w(fo):
                s = fo % NWS
                P.dma("pool", wbf[s][:, 0:8, :], wpa_v[:, :, fo * 128:(fo + 1) * 128], writes=[b_wbf[s]])
                P.dma("pool", wbf[s][:, 8:12, :], wps_v[:, :, fo * 128:(fo + 1) * 128], writes=[b_wbf[s]])
                gi = fo % NG
                P.dma("sp", gt[gi][:, 0, :], S["sgmT"][fo * 128:(fo + 1) * 128, :], writes=[b_gt[gi]])
                P.dma("sp", gt[gi][:, 1, :], S["sgmT"][D + fo * 128:D + (fo + 1) * 128, :], writes=[b_gt[gi]])

            load_w(0)
            for fo in range(NKC):
                if fo + 1 < NKC:
                    load_w(fo + 1)
                load_wout_chunk()
                s = fo % NWS
                gi = fo % NG
                for tb in range(4):
                    u2 = ui % 2
                    ui += 1
                    tsl = slice(tb * 512, (tb + 1) * 512)
                    for fc in range(8):
                        P.op("pe", lambda e, fc=fc, s=s, tsl=tsl, u2=u2: e.matmul(pa[u2][:], lhsT=wbf[s][:, fc, :], rhs=abrT[:, fc, tsl],
                                                                        start=(fc == 0), stop=(fc == 7)),
                             reads=[b_wbf[s], b_abrT], writes=[b_pa[u2]])
                    for fc in range(4):
                        P.op("pe", lambda e, fc=fc, s=s, tsl=tsl, u2=u2: e.matmul(pp[u2][:], lhsT=wbf[s][:, 8 + fc, :], rhs=sbrT[:, fc, tsl],
                                                                        start=(fc == 0), stop=(fc == 3)),
                             reads=[b_wbf[s], b_sbrT], writes=[b_pp[u2]])
                    P.op("dve", lambda e, gi=gi, u2=u2, tsl=tsl: e.tensor_tensor(out=t1[u2][:], in0=pa[u2][:], in1=gt[gi][:, 0, tsl], op=ALU.mult),
                         reads=[b_pa[u2], b_gt[gi]], writes=[b_t1[u2]])
                    P.op("dve", lambda e, gi=gi, u2=u2, tsl=tsl: e.tensor_tensor(out=t2[u2][:], in0=pp[u2][:], in1=gt[gi][:, 1, tsl], op=ALU.mult),
                         reads=[b_pp[u2], b_gt[gi]], writes=[b_t2[u2]])
                    P.op("pool", lambda e, fo=fo, tsl=tsl, u2=u2: e.tensor_tensor(out=mT[:, fo, tsl], in0=t1[u2][:], in1=t2[u2][:], op=ALU.add),
                         reads=[b_t1[u2], b_t2[u2]], writes=[b_mT[tb]])
            P.barrier()
        gateB = k.sb("mg_gateB", [128, D], F32, ls)
        fgB = k.sb("mg_fgB", [128, D], F32, ls)
        b_gateB, b_fgB = Buf("gateB"), Buf("fgB")
        c2 = P.new_dsem("mg_c2")
        P.dma("sp", gateB[:], S["modrow"][0:1, 2 * D:3 * D].broadcast_to([128, D]), writes=[b_gateB], sem=c2)
        P.dma("sp", fgB[:], I["final_g"][0:1, :].broadcast_to([128, D]), writes=[b_fgB], sem=c2)
        xb = [k.sb(f"mg_x{i}", [128, D], F32, ls) for i in range(NXB)]
        b_xb = [Buf(f"mgx{i}") for i in range(NXB)]
        xn = [k.sb(f"mg_xn{i}", [128, D], F32, ls) for i in range(NXB)]
        b_xn = [Buf(f"mgxn{i}") for i in range(NXB)]
        xsem = [P.new_dsem(f"mg_xs{i}") for i in range(NXB)]
        osem = [P.new_dsem(f"mg_os{i}") for i in range(NXB)]
        st2 = [k.sb(f"mg_st{i}", [128, 4], F32, ls) for i in range(NXB)]
        b_st2 = [Buf(f"mgst{i}") for i in range(NXB)]
        while wo_state["kc"] < NKC:
            load_wout_chunk()
        po = [k.ps(f"mg_po{i}", [128, 512], F32, ls) for i in range(3)]
        b_po = [Buf(f"mgpo{i}") for i in range(3)]
        pi = 0
        for t in range(16):
            s = t % NXB
            tb = t // 4
            P.dma("sp", xb[s][:], I["x"][t * 128:(t + 1) * 128, :], writes=[b_xb[s]], sem=xsem[s])
            for cbk in range(4):
                p_ = pi % 3
                pi += 1
                for kc in range(NKC):
                    P.op("pe", lambda e, kc=kc, cbk=cbk, p_=p_, t=t: e.matmul(po[p_][:], lhsT=mT[:, kc, t * 128:(t + 1) * 128],
                                                                          rhs=wout[:, kc, cbk * 512:(cbk + 1) * 512],
                                                                          start=(kc == 0), stop=(kc == NKC - 1)),
                         reads=[b_mT[tb], b_woutc[kc]], writes=[b_po[p_]])
                P.op("dve", lambda e, cbk=cbk, p_=p_, s=s: e.tensor_tensor(out=xn[s][:, cbk * 512:(cbk + 1) * 512], in0=po[p_][:],
                                                                       in1=gateB[:, cbk * 512:(cbk + 1) * 512], op=ALU.mult),
                     reads=[b_po[p_], b_gateB], writes=[b_xn[s]])
            P.op("pool", lambda e, s=s: e.tensor_tensor(out=xn[s][:], in0=xn[s][:], in1=xb[s][:], op=ALU.add),
                 reads=[b_xn[s], b_xb[s]], writes=[b_xn[s]])
            P.op("pool", lambda e, s=s: e.tensor_tensor(out=xb[s][:], in0=xn[s][:], in1=xn[s][:], op=ALU.mult),
                 reads=[b_xn[s]], writes=[b_xb[s]])
            P.op("dve", lambda e, s=s: e.tensor_reduce(out=st2[s][:, 0:1], in_=xb[s][:], axis=mybir.AxisListType.X, op=ALU.add),
                 reads=[b_xb[s]], writes=[b_st2[s]])
            P.op("dve", lambda e, s=s: e.tensor_scalar(out=st2[s][:, 1:2], in0=st2[s][:, 0:1], scalar1=1.0 / D, scalar2=EPS, op0=ALU.mult, op1=ALU.add),
                 reads=[b_st2[s]], writes=[b_st2[s]])
            P.op("act", lambda e, s=s: e.activation(out=st2[s][:, 2:3], in_=st2[s][:, 1:2], func=AF.Ln), reads=[b_st2[s]], writes=[b_st2[s]])
            P.op("act", lambda e, s=s: e.activation(out=st2[s][:, 3:4], in_=st2[s][:, 2:3], func=AF.Exp, scale=-0.5), reads=[b_st2[s]], writes=[b_st2[s]])
            P.op("dve", lambda e, s=s: e.scalar_tensor_tensor(out=xn[s][:], in0=xn[s][:], scalar=st2[s][:, 3:4], in1=fgB[:], op0=ALU.mult, op1=ALU.mult),
                 reads=[b_xn[s], b_st2[s], b_fgB], writes=[b_xn[s]])
            P.dma("sp", k.out[t * 128:(t + 1) * 128, :], xn[s][:], reads=[b_xn[s]], sem=osem[s])


_CACHE = {}


def _prep_inputs(inputs, b):
    f = lambda a: np.ascontiguousarray(np.asarray(a, dtype=np.float32))
    m = {}
    m["x"] = f(inputs["x"][b])
    m["ctx"] = f(inputs["ctx"][b])
    m["cc"] = f(np.stack([np.asarray(inputs["c"])[b], np.asarray(inputs["c_ctx"])], axis=0))
    m["w_ada"] = f(inputs["w_ada"][0])
    m["b_ada"] = f(inputs["b_ada"][0]).reshape(1, -1)
    m["norm_g"] = f(inputs["norm_g"][0])
    m["w_in"] = f(inputs["w_in"][0])
    m["lam"] = f(np.stack([np.asarray(inputs["lambda_q1"])[0], np.asarray(inputs["lambda_k1"])[0],
                           np.asarray(inputs["lambda_q2"])[0], np.asarray(inputs["lambda_k2"])[0]], axis=0))
    m["subln_g"] = f(inputs["subln_g"][0]).reshape(1, 128)
    m["ssm_lre"] = f(inputs["ssm_lambda_re"][0])
    m["ssm_lim"] = f(inputs["ssm_lambda_im"][0])
    m["ssm_ls"] = f(inputs["ssm_log_step"][0])
    m["ssm_bre"] = f(inputs["ssm_b_re"][0])
    m["ssm_bim"] = f(inputs["ssm_b_im"][0])
    m["ssm_cre"] = f(inputs["ssm_c_re"][0])
    m["ssm_cim"] = f(inputs["ssm_c_im"][0])
    m["ssm_d"] = f(inputs["ssm_d"][0]).reshape(1, 512)
    m["w_glu"] = f(inputs["w_glu"][0])
    m["b_glu"] = f(inputs["b_glu"][0])
    m["w_pa"] = f(inputs["w_pa"][0])
    m["w_ps"] = f(inputs["w_ps"][0])
    m["w_out"] = f(inputs["w_out"][0])
    m["final_g"] = f(inputs["final_g"]).reshape(1, D)
    m.update(_consts())
    return m


def kernel(**inputs):
    if "nc" not in _CACHE:
        _CACHE["nc"] = build()[0]
    nc = _CACHE["nc"]
    shared = None
    in_maps = []
    for b in range(8):
        m = _prep_inputs(inputs, b)
        if shared is None:
            shared = m
        else:
            for key in m:
                if key not in ("x", "ctx", "cc"):
                    m[key] = shared[key]
        in_maps.append(m)
    res = run_bass_kernel_spmd(nc, in_maps, core_ids=list(range(8)))
    return np.stack([np.asarray(r["out"], dtype=np.float32) for r in res.results], axis=0)
```

```python
import math
import numpy as np
import ml_dtypes
from contextlib import ExitStack
import concourse.bass as bass
import concourse.mybir as mybir
from concourse.bass_utils import run_bass_kernel_spmd

F32 = mybir.dt.float32
BF16 = mybir.dt.bfloat16
I32 = mybir.dt.int32
AF = mybir.ActivationFunctionType
ALU = mybir.AluOpType

D = 2048
L = 2048
LC = 256
LT = L + LC
NKC = D // 128
INW = 9216
HEADS = 8
EPS = 1e-6
LAM_INIT = 0.8 - 0.6 * math.exp(-0.3 * 0)
TWO_PI = 2.0 * math.pi


class Buf:
    __slots__ = ("name", "w", "r")

    def __init__(self, name):
        self.name = name
        self.w = None
        self.r = {}


class Prog:
    ENG = ["pe", "act", "dve", "pool", "sp"]

    def __init__(self, nc, st):
        self.nc = nc
        self.st = st
        self.q = {e: [] for e in self.ENG}
        self.seen = {e: {} for e in self.ENG}
        self.psem = {e: st.enter_context(nc.semaphore("p_" + e)) for e in ["pe", "act", "dve", "pool"]}
        self.dsems = []
        self.bufsem = {}
        self.bufsem_keep = []
        self.free_dsems = []

    def new_dsem(self, name):
        return None

    def _auto_dsem(self, reads, writes):
        b = writes[0] if len(writes) else reads[0]
        key = id(b)
        d = self.bufsem.get(key)
        if d is None:
            if self.free_dsems:
                d = self.free_dsems.pop()
            else:
                h = self.st.enter_context(self.nc.semaphore(f"d{len(self.dsems)}"))
                d = {"h": h, "n": 0, "name": f"d{len(self.dsems)}"}
                self.dsems.append(d)
            self.bufsem[key] = d
            self.bufsem_keep.append(b)
        return d

    def _deps(self, eng, reads, writes):
        need = {}

        def add(t):
            if t[0] == "c":
                if t[1] == "pe" and eng == "pe":
                    return
                key = ("c", t[1])
                if need.get(key, (None, -1))[1] < t[2]:
                    need[key] = (t[1], t[2])
            else:
                key = ("d", id(t[1]))
                if need.get(key, (None, -1))[1] < t[2]:
                    need[key] = (t[1], t[2])

        for b in reads:
            if b.w is not None:
                add(b.w)
        for b in writes:
            if b.w is not None:
                add(b.w)
            for t in b.r.values():
                add(t)
        waits = []
        for key, (obj, v) in need.items():
            if self.seen[eng].get(key, -1) >= v:
                continue
            self.seen[eng][key] = v
            waits.append((key[0], obj, v))
        return waits

    def _record(self, tok, reads, writes):
        for b in reads:
            key = (tok[0], tok[1] if tok[0] == "c" else id(tok[1]))
            b.r[key] = tok
        for b in writes:
            b.w = tok
            b.r = {}

    def op(self, eng, fn, reads=(), writes=()):
        waits = self._deps(eng, reads, writes)
        idx = len(self.q[eng])
        self.q[eng].append({"fn": fn, "waits": waits, "awaited": False, "dma": None})
        tok = ("c", eng, idx)
        self._record(tok, reads, writes)
        return tok

    def dma(self, eng, out, in_, reads=(), writes=(), sem=None, **kw):
        reads, writes = list(reads), list(writes)
        sem = self._auto_dsem(reads, writes)
        waits = self._deps(eng, reads, writes)
        sem["n"] += 16
        tok = ("d", sem, sem["n"])
        self.q[eng].append({"fn": (lambda e, o=out, i=in_, k=kw: e.dma_start(out=o, in_=i, **k)),
                            "waits": waits, "awaited": False, "dma": sem})
        self._record(tok, reads, writes)
        return tok

    def barrier(self):
        for e in self.ENG:
            waits = []
            for e2 in ["pe", "act", "dve", "pool"]:
                n = len(self.q[e2])
                if e2 == e:
                    n -= 0
                idx = None
                for i in range(len(self.q[e2]) - 1, -1, -1):
                    if self.q[e2][i]["fn"] is not None and self.q[e2][i]["dma"] is None:
                        idx = i
                        break
                if idx is None:
                    continue
                key = ("c", e2)
                if self.seen[e].get(key, -1) >= idx:
                    continue
                self.seen[e][key] = idx
                waits.append(("c", e2, idx))
            for d in self.dsems:
                if d["n"] == 0:
                    continue
                key = ("d", id(d))
                if self.seen[e].get(key, -1) >= d["n"]:
                    continue
                self.seen[e][key] = d["n"]
                waits.append(("d", d, d["n"]))
            if waits:
                self.q[e].append({"fn": None, "waits": waits, "awaited": False, "dma": None})
        for d in self.bufsem.values():
            self.free_dsems.append(d)
        self.bufsem = {}
        self.bufsem_keep = []

    def emit(self):
        for e in self.ENG:
            for ent in self.q[e]:
                for w in ent["waits"]:
                    if w[0] == "c":
                        self.q[w[1]][w[2]]["awaited"] = True
        cnt = {}
        for e in ["pe", "act", "dve", "pool"]:
            c = 0
            arr = []
            for ent in self.q[e]:
                if ent["awaited"]:
                    c += 1
                arr.append(c)
            cnt[e] = arr
        psem = self.psem
        q = self.q

        def run(name, e):
            for ent in q[name]:
                for w in ent["waits"]:
                    if w[0] == "c":
                        e.wait_ge(psem[w[1]], cnt[w[1]][w[2]])
                    else:
                        e.wait_ge(w[1]["h"], w[2])
                if ent["fn"] is None:
                    continue
                inst = ent["fn"](e)
                if ent["dma"] is not None:
                    inst.then_inc(ent["dma"]["h"], 16)
                elif ent["awaited"]:
                    inst.then_inc(psem[name], 1)

        with self.nc.Block() as block:
            @block.sync
            def _(e):
                run("sp", e)

            @block.scalar
            def _(e):
                run("act", e)

            @block.vector
            def _(e):
                run("dve", e)

            @block.gpsimd
            def _(e):
                run("pool", e)

            @block.tensor
            def _(e):
                run("pe", e)


def _consts():
    ident = np.eye(128, dtype=np.float32)
    m = np.arange(128)
    partner = np.where((m % 32) < 16, m + 16, m - 16)
    perm = np.zeros((128, 128), np.float32)
    perm[partner, m] = 1.0
    sgn = np.where((m % 32) < 16, -1.0, 1.0).astype(np.float32)
    tok = np.arange(L)
    pos = np.where(((m % 64) < 32)[:, None], (tok // 64)[None, :], (tok % 64)[None, :]).astype(np.float32)
    fexp = ((m % 16) / 16.0).astype(np.float32)
    colc = np.zeros((128, 4), np.float32)
    colc[:, 0] = sgn
    colc[:, 1] = fexp
    colc[:, 2] = np.where(m < 64, 1.0, -1.0)
    sel = np.zeros((2, 128), np.float32)
    sel[0, :] = 1.0
    tauA = np.zeros((128, 32, 9), np.float32)
    tauA[:, 0:16, :] = np.arange(9)[None, None, :]
    tauA[:, 16:32, :] = (8 - np.arange(9))[None, None, :]
    tauB = np.zeros((128, 32, 8), np.float32)
    tauB[:, 0:16, :] = (7 - np.arange(8))[None, None, :]
    tauB[:, 16:32, :] = np.arange(8)[None, None, :]
    return {"c_ident": ident, "c_perm": perm, "c_pos": pos, "c_col": colc, "c_sel": sel, "c_tauA": tauA, "c_tauB": tauB}


class K:
    pass


def build(debug=None):
    nc = bass.Bass("TRN2", target_bir_lowering=False)
    st = ExitStack()
    P = Prog(nc, st)
    k = K()
    k.nc, k.P, k.st = nc, P, st
    k.debug = debug or {}

    def dram_in(name, shape, dt=F32):
        return nc.dram_tensor(name, list(shape), dt, kind="ExternalInput").ap()

    dbg_outs = []

    def dram_scr(name, shape, dt):
        kind = "Internal"
        if debug is not None and name in debug.get("_inject", ()):
            kind = "ExternalInput"
        elif debug is not None and name in debug:
            kind = "ExternalOutput"
            dbg_outs.append(name)
        return nc.dram_tensor(name, list(shape), dt, kind=kind).ap()

    I = {}
    I["x"] = dram_in("x", [L, D])
    I["ctx"] = dram_in("ctx", [LC, D])
    I["cc"] = dram_in("cc", [2, D])
    I["w_ada"] = dram_in("w_ada", [D, 3 * D])
    I["b_ada"] = dram_in("b_ada", [1, 3 * D])
    I["norm_g"] = dram_in("norm_g", [D])
    I["w_in"] = dram_in("w_in", [D, INW])
    I["lam"] = dram_in("lam", [4, 64])
    I["subln_g"] = dram_in("subln_g", [1, 128])
    I["ssm_lre"] = dram_in("ssm_lre", [2, 32, 64])
    I["ssm_lim"] = dram_in("ssm_lim", [2, 32, 64])
    I["ssm_ls"] = dram_in("ssm_ls", [2, 32])
    I["ssm_bre"] = dram_in("ssm_bre", [2, 32, 64, 16])
    I["ssm_bim"] = dram_in("ssm_bim", [2, 32, 64, 16])
    I["ssm_cre"] = dram_in("ssm_cre", [2, 32, 16, 64])
    I["ssm_cim"] = dram_in("ssm_cim", [2, 32, 16, 64])
    I["ssm_d"] = dram_in("ssm_d", [1, 512])
    I["w_glu"] = dram_in("w_glu", [512, 512])
    I["b_glu"] = dram_in("b_glu", [512])
    I["w_pa"] = dram_in("w_pa", [1024, D])
    I["w_ps"] = dram_in("w_ps", [512, D])
    I["w_out"] = dram_in("w_out", [D, D])
    I["final_g"] = dram_in("final_g", [1, D])
    for cn, arr in _consts().items():
        I[cn] = dram_in(cn, arr.shape)
    out = nc.dram_tensor("out", [L, D], F32, kind="ExternalOutput").ap()

    S = {}
    S["modrow"] = dram_scr("modrow", [2, 3 * D], F32)
    S["qT"] = dram_scr("qT", [HEADS, 128, L], BF16)
    S["kT"] = dram_scr("kT", [HEADS, 128, LT], BF16)
    S["v"] = dram_scr("v", [LT, 1024], BF16)
    S["sga"] = dram_scr("sga", [L, 1024], BF16)
    S["u"] = dram_scr("u", [LT, 512], F32)
    S["sgsT"] = dram_scr("sgsT", [512, L], BF16)
    S["sgmT"] = dram_scr("sgmT", [2 * D, L], BF16)
    S["abrT"] = dram_scr("abrT", [1024, L], BF16)
    S["sbrT"] = dram_scr("sbrT", [512, L], BF16)
    S["hT"] = dram_scr("hT_dbg", [128, NKC, LT], BF16) if (debug is not None and "hT_dbg" in debug) else None
    k.I, k.S, k.out = I, S, out
    k.dbg_sem = None

    def dbg(name, shape, dt, ap_fn, bufs):
        if debug is None or name not in debug:
            return
        if name not in S:
            S[name] = nc.dram_tensor(name, list(shape), dt, kind="ExternalOutput").ap()
            dbg_outs.append(name)
        if k.dbg_sem is None:
            k.dbg_sem = P.new_dsem("dbgsem")
        o, i = ap_fn(S[name])
        P.dma("sp", o, i, reads=bufs, sem=k.dbg_sem)
    k.dbg = dbg

    def sb(name, shape, dt, stack=st):
        return stack.enter_context(nc.sbuf_tensor(name, list(shape), dt))

    def ps(name, shape, dt, stack=st):
        return stack.enter_context(nc.psum_tensor(name, list(shape), dt))

    k.sb, k.ps = sb, ps
    ident = sb("ident", [128, 128], F32)
    colc = sb("colc", [128, 4], F32)
    b_ident, b_colc = Buf("ident"), Buf("colc")
    csem = P.new_dsem("csem")
    P.dma("sp", ident[:], I["c_ident"], writes=[b_ident], sem=csem)
    P.dma("sp", colc[:], I["c_col"], writes=[b_colc], sem=csem)
    k.ident, k.b_ident, k.colc, k.b_colc, k.csem = ident, b_ident, colc, b_colc, csem
    k.ssq = sb("ssq", [128, 40], F32)
    k.b_ssq = Buf("ssq")

    phase_adaln(k)
    P.barrier()
    if debug is None or debug.get("_upto", 99) >= 1:
        phase_norm_inproj(k, debug)
        P.barrier()
    if (debug is None or debug.get("_upto", 99) >= 2) and not (debug or {}).get("_skip_ssm"):
        phase_ssm(k)
        P.barrier()
    if debug is None or debug.get("_upto", 99) >= 3:
        phase_attn(k)
        P.barrier()
    if debug is None or debug.get("_upto", 99) >= 4:
        phase_merge(k)
        P.barrier()
    P.emit()
    st.close()
    return nc, dbg_outs


def range_sin(k, stack, out_ap, y_ap, shape, tag, rbufs, wbufs, eng="dve"):
    nc, P = k.nc, k.P
    ki = k.sb(tag + "_ki", shape, I32, stack)
    kf = k.sb(tag + "_kf", shape, F32, stack)
    g = k.sb(tag + "_g", shape, F32, stack)
    bki, bkf, bg = Buf(tag + "ki"), Buf(tag + "kf"), Buf(tag + "g")
    sl = tuple([slice(None)] * len(shape))
    P.op(eng, lambda e: e.tensor_copy(out=ki[sl], in_=y_ap), reads=rbufs, writes=[bki])
    P.op(eng, lambda e: e.tensor_copy(out=kf[sl], in_=ki[sl]), reads=[bki], writes=[bkf])
    P.op(eng, lambda e: e.tensor_tensor(out=kf[sl], in0=y_ap, in1=kf[sl], op=ALU.subtract), reads=rbufs + [bkf], writes=[bkf])
    P.op(eng, lambda e: e.tensor_single_scalar(out=g[sl], in_=kf[sl], scalar=0.5, op=ALU.is_gt), reads=[bkf], writes=[bg])
    P.op(eng, lambda e: e.tensor_tensor(out=kf[sl], in0=kf[sl], in1=g[sl], op=ALU.subtract), reads=[bkf, bg], writes=[bkf])
    P.op(eng, lambda e: e.tensor_single_scalar(out=g[sl], in_=kf[sl], scalar=-0.5, op=ALU.is_lt), reads=[bkf], writes=[bg])
    P.op(eng, lambda e: e.tensor_tensor(out=kf[sl], in0=kf[sl], in1=g[sl], op=ALU.add), reads=[bkf, bg], writes=[bkf])
    P.op("act", lambda e: e.activation(out=out_ap, in_=kf[sl], func=AF.Sin, scale=TWO_PI * (1.0 - 2e-7)), reads=[bkf], writes=wbufs)


def phase_adaln(k):
    nc, P, I, S = k.nc, k.P, k.I, k.S
    with ExitStack() as ls:
        sT = k.sb("ad_sT", [128, NKC, 2], F32, ls)
        b_sT = Buf("sT")
        sem_c = P.new_dsem("ad_c")
        for v in range(2):
            P.dma("sp", sT[:, :, v], I["cc"][v].rearrange("(kc p) -> p kc", p=128), writes=[b_sT], sem=sem_c,
                  allow_slow_non_contiguous=True)
        P.op("act", lambda e: e.activation(out=sT[:], in_=sT[:], func=AF.Silu), reads=[b_sT], writes=[b_sT])
        brow = k.sb("ad_brow", [2, 3 * D], F32, ls)
        b_brow = Buf("brow")
        for v in range(2):
            P.dma("sp", brow[v:v + 1, :], I["b_ada"], writes=[b_brow], sem=sem_c)
        modrow = k.sb("ad_modrow", [2, 3 * D], F32, ls)
        b_modrow = Buf("modrow")
        NS = 2
        wst = [k.sb(f"ad_w{i}", [128, NKC, 512], F32, ls) for i in range(NS)]
        b_w = [Buf(f"adw{i}") for i in range(NS)]
        wsem = [P.new_dsem(f"ad_ws{i}") for i in range(NS)]
        pst = [k.ps(f"ad_ps{i}", [128, 512], F32, ls) for i in range(2)]
        b_ps = [Buf(f"adps{i}") for i in range(2)]
        wv = I["w_ada"].rearrange("(kc p) c -> p kc c", p=128)
        xs1 = [k.sb(f"ad_x{i}", [128, D], F32, ls) for i in range(2)]
        b_xs1 = [Buf(f"adx{i}") for i in range(2)]
        junk1 = k.sb("ad_junk", [128, D], BF16, ls)
        b_junk1 = Buf("adjunk")
        tiles_done = 0

        def ss_tile(t):
            s1 = t % 2
            src = I["x"][t * 128:(t + 1) * 128, :] if t < 16 else I["ctx"][(t - 16) * 128:(t - 15) * 128, :]
            P.dma("pool", xs1[s1][:], src, writes=[b_xs1[s1]])
            P.op("act", lambda e, s1=s1, t=t: e.activation(out=junk1[:], in_=xs1[s1][:], func=AF.Square, accum_out=k.ssq[:, t:t + 1]),
                 reads=[b_xs1[s1]], writes=[b_junk1, k.b_ssq])

        for cb in range(12):
            for _ in range(2 if cb < 6 else 1):
                if tiles_done < 18:
                    ss_tile(tiles_done)
                    tiles_done += 1
            s = cb % NS
            P.dma("sp", wst[s][:, 0:8, :], wv[:, 0:8, cb * 512:(cb + 1) * 512], writes=[b_w[s]], sem=wsem[s])
            P.dma("act", wst[s][:, 8:16, :], wv[:, 8:16, cb * 512:(cb + 1) * 512], writes=[b_w[s]], sem=wsem[s])
            pt, bp = pst[cb % 2], b_ps[cb % 2]
            for kc in range(NKC):
                P.op("pe", lambda e, kc=kc, s=s, pt=pt: e.matmul(pt[0:2, :], lhsT=sT[:, kc, :], rhs=wst[s][:, kc, :],
                                                              start=(kc == 0), stop=(kc == NKC - 1)),
                     reads=[b_sT, b_w[s]], writes=[bp])
            P.op("dve", lambda e, cb=cb, pt=pt: e.tensor_tensor(out=modrow[:, cb * 512:(cb + 1) * 512], in0=pt[0:2, :],
                                                             in1=brow[:, cb * 512:(cb + 1) * 512], op=ALU.add),
                 reads=[bp, b_brow], writes=[b_modrow])
        b_mr = Buf("modrow_d")
        k.b_modrow_d = b_mr
        P.dma("sp", S["modrow"], modrow[:], reads=[b_modrow], writes=[b_mr], sem=sem_c)


def phase_norm_inproj(k, debug):
    nc, P, I, S = k.nc, k.P, k.I, k.S
    with ExitStack() as ls:
        hT = k.sb("hT", [128, NKC, LT], BF16, ls)
        b_hT = [Buf(f"hT{t}") for t in range(18)]
        Amod = k.sb("Amod", [128, NKC, 2], F32, ls)
        Smod = k.sb("Smod", [128, NKC, 2], F32, ls)
        gcol = k.sb("gcol", [128, NKC], F32, ls)
        b_A, b_S, b_g = Buf("Amod"), Buf("Smod"), Buf("gcol")
        msem = P.new_dsem("n_m")
        for v in range(2):
            P.dma("sp", Smod[:, :, v], S["modrow"][v, 0:D].rearrange("(kc p) -> p kc", p=128),
                  reads=[k.b_modrow_d], writes=[b_S], sem=msem, allow_slow_non_contiguous=True)
            P.dma("sp", Amod[:, :, v], S["modrow"][v, D:2 * D].rearrange("(kc p) -> p kc", p=128),
                  reads=[k.b_modrow_d], writes=[b_A], sem=msem, allow_slow_non_contiguous=True)
        P.dma("sp", gcol[:], I["norm_g"].rearrange("(kc p) -> p kc", p=128), writes=[b_g], sem=msem,
              allow_slow_non_contiguous=True)
        for v in range(2):
            P.op("dve", lambda e, v=v: e.scalar_tensor_tensor(out=Amod[:, :, v], in0=Amod[:, :, v], scalar=1.0, in1=gcol[:],
                                                             op0=ALU.add, op1=ALU.mult),
                 reads=[b_A, b_g], writes=[b_A])
        with ExitStack() as l1:
            NX = 2
            xt = [k.sb(f"n_x{i}", [128, D], F32, l1) for i in range(NX)]
            b_x = [Buf(f"nx{i}") for i in range(NX)]
            xsem = [P.new_dsem(f"n_xs{i}") for i in range(NX)]
            junk = k.sb("n_junk", [128, D], BF16, l1)
            b_junk = Buf("junk")
            stat = [k.sb(f"n_st{i}", [128, 4], F32, l1) for i in range(NX)]
            b_stat = [Buf(f"nst{i}") for i in range(NX)]
            pt = [k.ps(f"n_ps{i}", [128, 512], F32, l1) for i in range(4)]
            b_pt = [Buf(f"nps{i}") for i in range(4)]
            pi = 0
            P.op("dve", lambda e: e.tensor_scalar(out=k.ssq[:, 0:18], in0=k.ssq[:, 0:18], scalar1=1.0 / D, scalar2=EPS, op0=ALU.mult, op1=ALU.add),
                 reads=[k.b_ssq], writes=[k.b_ssq])
            P.op("act", lambda e: e.activation(out=k.ssq[:, 0:18], in_=k.ssq[:, 0:18], func=AF.Ln), reads=[k.b_ssq], writes=[k.b_ssq])
            P.op("act", lambda e: e.activation(out=k.ssq[:, 20:38], in_=k.ssq[:, 0:18], func=AF.Exp, scale=-0.5), reads=[k.b_ssq], writes=[k.b_ssq])
            for t in range(18):
                s = t % NX
                v = 0 if t < 16 else 1
                src = I["x"][t * 128:(t + 1) * 128, :] if t < 16 else I["ctx"][(t - 16) * 128:(t - 15) * 128, :]
                P.dma("sp", xt[s][:, 0:1024], src[:, 0:1024], writes=[b_x[s]], sem=xsem[s])
                P.dma("act", xt[s][:, 1024:2048], src[:, 1024:2048], writes=[b_x[s]], sem=xsem[s])
                P.op("dve", lambda e, s=s, t=t: e.tensor_scalar(out=xt[s][:], in0=xt[s][:], scalar1=k.ssq[:, 20 + t:21 + t], scalar2=None,
                                                              op0=ALU.mult),
                     reads=[b_x[s], k.b_ssq], writes=[b_x[s]])
                for g4 in range(4):
                    p_, bp = pt[pi % 4], b_pt[pi % 4]
                    pi += 1
                    for j in range(4):
                        kc = g4 * 4 + j
                        P.op("pe", lambda e, s=s, kc=kc, j=j, p_=p_: e.transpose(out=p_[:, j * 128:(j + 1) * 128],
                                                                             in_=xt[s][:, kc * 128:(kc + 1) * 128],
                                                                             identity=k.ident[:]),
                             reads=[b_x[s], k.b_ident], writes=[bp])
                    for j in range(4):
                        kc = g4 * 4 + j
                        eng = "dve" if (j % 2 == 0) else "act"
                        if eng == "dve":
                            P.op("dve", lambda e, kc=kc, j=j, p_=p_, t=t, v=v: e.tensor_scalar(
                                out=hT[:, kc, t * 128:(t + 1) * 128], in0=p_[:, j * 128:(j + 1) * 128],
                                scalar1=Amod[:, kc, v:v + 1], scalar2=Smod[:, kc, v:v + 1], op0=ALU.mult, op1=ALU.add),
                                reads=[bp, b_A, b_S], writes=[b_hT[t]])
                        else:
                            P.op("act", lambda e, kc=kc, j=j, p_=p_, t=t, v=v: e.activation(
                                out=hT[:, kc, t * 128:(t + 1) * 128], in_=p_[:, j * 128:(j + 1) * 128],
                                func=AF.Identity, scale=Amod[:, kc, v:v + 1], bias=Smod[:, kc, v:v + 1]),
                                reads=[bp, b_A, b_S], writes=[b_hT[t]])
        if S["hT"] is not None:
            dsem = P.new_dsem("dbg")
            P.dma("sp", S["hT"], hT[:], reads=b_hT, writes=[Buf("x")], sem=dsem)
        P.barrier()
        if debug is not None and debug.get("_upto", 99) < 1.5:
            return
        inproj(k, ls, hT, b_hT)


def inproj(k, ls, hT, b_hT):
    nc, P, I, S = k.nc, k.P, k.I, k.S
    cosT = k.sb("cosT", [128, L], F32, ls)
    sinS = k.sb("sinS", [128, L], F32, ls)
    perm = k.sb("perm", [128, 128], F32, ls)
    b_cos, b_sin, b_perm = Buf("cos"), Buf("sin"), Buf("perm")
    tsem = P.new_dsem("ip_t")
    P.dma("sp", perm[:], I["c_perm"], writes=[b_perm], sem=tsem)
    with ExitStack() as l0:
        pos = k.sb("pos", [128, L], F32, l0)
        yv = k.sb("yv", [128, L], F32, l0)
        inv = k.sb("inv", [128, 1], F32, l0)
        b_pos, b_y, b_inv = Buf("pos"), Buf("yv"), Buf("inv")
        P.dma("sp", pos[:], I["c_pos"], writes=[b_pos], sem=tsem)
        P.op("act", lambda e: e.activation(out=inv[:], in_=k.colc[:, 1:2], func=AF.Exp, scale=-math.log(10000.0)),
             reads=[k.b_colc], writes=[b_inv])
        P.op("dve", lambda e: e.tensor_scalar(out=yv[:], in0=pos[:], scalar1=inv[:, 0:1], scalar2=1.0 / TWO_PI,
                                              op0=ALU.mult, op1=ALU.mult), reads=[b_pos, b_inv], writes=[b_y])
        range_sin(k, l0, sinS[:], yv[:], [128, L], "rs1", [b_y], [b_sin])
        P.op("dve", lambda e: e.tensor_scalar(out=sinS[:], in0=sinS[:], scalar1=k.colc[:, 0:1], scalar2=None, op0=ALU.mult),
             reads=[b_sin, k.b_colc], writes=[b_sin])
        P.op("dve", lambda e: e.tensor_scalar(out=yv[:], in0=yv[:], scalar1=0.25, scalar2=None, op0=ALU.add),
             reads=[b_y], writes=[b_y])
        range_sin(k, l0, cosT[:], yv[:], [128, L], "rs2", [b_y], [b_cos])
        P.barrier()
    NW = 2
    wst = [k.sb(f"ip_wst{i}", [128, 8, 512], F32, ls) for i in range(NW)]
    b_wst = [Buf(f"wst{i}") for i in range(NW)]
    wsem = [P.new_dsem(f"ip_ws{i}") for i in range(NW)]
    wb = [k.sb(f"ip_wb{i}", [128, NKC, 512], BF16, ls) for i in range(2)]
    b_wb = [Buf(f"wb{i}") for i in range(2)]
    NOB = 4
    ob = [k.sb(f"ip_ob{i}", [128, 512], BF16, ls) for i in range(NOB)]
    b_ob = [Buf(f"ob{i}") for i in range(NOB)]
    osem = [P.new_dsem(f"ip_os{i}") for i in range(NOB)]
    NOF = 3
    of = [k.sb(f"ip_of{i}", [128, 512], F32, ls) for i in range(NOF)]
    b_of = [Buf(f"of{i}") for i in range(NOF)]
    fsem = [P.new_dsem(f"ip_fs{i}") for i in range(NOF)]
    t1 = [k.sb(f"ip_t1{i}", [128, 512], F32, ls) for i in range(2)]
    b_t1 = [Buf(f"t1{i}") for i in range(2)]
    t2 = [k.sb(f"ip_t2{i}", [128, 512], F32, ls) for i in range(2)]
    b_t2 = [Buf(f"t2{i}") for i in range(2)]
    pb = [k.ps(f"ip_ps{i}", [128, 512], F32, ls) for i in range(4)]
    b_pb = [Buf(f"ipps{i}") for i in range(4)]
    pr = [k.ps(f"ip_pr{i}", [128, 512], F32, ls) for i in range(2)]
    b_pr = [Buf(f"ippr{i}") for i in range(2)]
    wv = I["w_in"].rearrange("(kc p) c -> p kc c", p=128)
    cnt = {"pb": 0, "ob": 0, "of": 0, "r": 0, "ld": 0, "ev": 0}

    rope_pending = []

    def load_block(cb):
        s2 = cb % 2
        for half in range(2):
            sl = cnt["ld"] % NW
            cnt["ld"] += 1
            P.dma("sp", wst[sl][:], wv[:, half * 8:(half + 1) * 8, cb * 512:(cb + 1) * 512], writes=[b_wst[sl]], sem=wsem[sl])
            P.op("dve", lambda e, sl=sl, s2=s2, half=half: e.tensor_copy(out=wb[s2][:, half * 8:(half + 1) * 8, :], in_=wst[sl][:]),
                 reads=[b_wst[sl]], writes=[b_wb[s2]])

    def next_ob():
        i = cnt["ob"] % NOB
        cnt["ob"] += 1
        return i

    def evac_eng():
        cnt["ev"] += 1
        return "act" if cnt["ev"] % 2 else "dve"

    def tiles_of(tok0, n):
        return [b_hT[t] for t in range(tok0 // 128, (tok0 + n) // 128)]

    def fm_unit(cb, fc, tok0, n, kind, row0, dst):
        s2 = cb % 2
        pi = cnt["pb"] % 4
        cnt["pb"] += 1
        pt, bp = pb[pi], b_pb[pi]
        for kc in range(NKC):
            P.op("pe", lambda e, kc=kc: e.matmul(pt[:, 0:n], lhsT=wb[s2][:, kc, fc * 128:(fc + 1) * 128],
                                                 rhs=hT[:, kc, tok0:tok0 + n], start=(kc == 0), stop=(kc == NKC - 1)),
                 reads=[b_wb[s2]] + tiles_of(tok0, n), writes=[bp])
        while rope_pending:
            rope_pending.pop(0)()
        oi = next_ob()
        if kind == "rope":
            ri = cnt["r"] % 2
            cnt["r"] += 1
            fi = cnt["of"] % NOF
            cnt["of"] += 1
            P.op("act", lambda e: e.activation(out=of[fi][:, 0:n], in_=pt[:, 0:n], func=AF.Copy), reads=[bp], writes=[b_of[fi]])
            P.op("dve", lambda e: e.tensor_tensor(out=t1[ri][:, 0:n], in0=of[fi][:, 0:n], in1=cosT[:, tok0:tok0 + n], op=ALU.mult),
                 reads=[b_of[fi], b_cos], writes=[b_t1[ri]])

            def fin():
                P.op("pe", lambda e: e.matmul(pr[ri][:, 0:n], lhsT=perm[:], rhs=of[fi][:, 0:n], start=True, stop=True),
                     reads=[b_perm, b_of[fi]], writes=[b_pr[ri]])
                P.op("dve", lambda e: e.tensor_tensor(out=t2[ri][:, 0:n], in0=pr[ri][:, 0:n], in1=sinS[:, tok0:tok0 + n], op=ALU.mult),
                     reads=[b_pr[ri], b_sin], writes=[b_t2[ri]])
                P.op("pool", lambda e: e.tensor_tensor(out=ob[oi][:, 0:n], in0=t1[ri][:, 0:n], in1=t2[ri][:, 0:n], op=ALU.add),
                     reads=[b_t1[ri], b_t2[ri]], writes=[b_ob[oi]])
                P.dma("pool", dst, ob[oi][:, 0:n], reads=[b_ob[oi]], sem=osem[oi])
            rope_pending.append(fin)
            return
        elif kind == "copy":
            eg = evac_eng()
            if eg == "act":
                P.op("act", lambda e: e.activation(out=ob[oi][:, 0:n], in_=pt[:, 0:n], func=AF.Copy), reads=[bp], writes=[b_ob[oi]])
            else:
                P.op("dve", lambda e: e.tensor_copy(out=ob[oi][:, 0:n], in_=pt[:, 0:n]), reads=[bp], writes=[b_ob[oi]])
        else:
            fn = AF.Silu if kind == "silu" else AF.Sigmoid
            P.op("act", lambda e: e.activation(out=ob[oi][:, 0:n], in_=pt[:, 0:n], func=fn), reads=[bp], writes=[b_ob[oi]])
        P.dma("pool", dst, ob[oi][:, 0:n], reads=[b_ob[oi]], sem=osem[oi])

    def tm_unit(cb, t, kind, dst):
        s2 = cb % 2
        pi = cnt["pb"] % 4
        cnt["pb"] += 1
        pt, bp = pb[pi], b_pb[pi]
        for kc in range(NKC):
            P.op("pe", lambda e, kc=kc: e.matmul(pt[:], lhsT=hT[:, kc, t * 128:(t + 1) * 128], rhs=wb[s2][:, kc, :],
                                                 start=(kc == 0), stop=(kc == NKC - 1)),
                 reads=[b_wb[s2], b_hT[t]], writes=[bp])
        while rope_pending:
            rope_pending.pop(0)()
        if kind == "f32":
            fi = cnt["of"] % NOF
            cnt["of"] += 1
            P.op("dve", lambda e: e.tensor_copy(out=of[fi][:], in_=pt[:]), reads=[bp], writes=[b_of[fi]])
            P.dma("pool", dst, of[fi][:], reads=[b_of[fi]], sem=fsem[fi])
            return
        oi = next_ob()
        if kind == "copy":
            eg = evac_eng()
            if eg == "act":
                P.op("act", lambda e: e.activation(out=ob[oi][:], in_=pt[:], func=AF.Copy), reads=[bp], writes=[b_ob[oi]])
            else:
                P.op("dve", lambda e: e.tensor_copy(out=ob[oi][:], in_=pt[:]), reads=[bp], writes=[b_ob[oi]])
        else:
            P.op("act", lambda e: e.activation(out=ob[oi][:], in_=pt[:], func=AF.Silu), reads=[bp], writes=[b_ob[oi]])
        P.dma("pool", dst, ob[oi][:], reads=[b_ob[oi]], sem=osem[oi])

    NCB = INW // 512
    load_block(0)
    for cb in range(NCB):
        if cb + 1 < NCB:
            load_block(cb + 1)
        c0 = cb * 512
        if cb < 2:
            for fc in range(4):
                h = cb * 4 + fc
                for tb in range(4):
                    fm_unit(cb, fc, tb * 512, 512, "rope", 0, S["qT"][h, :, tb * 512:(tb + 1) * 512])
        elif cb < 4:
            for fc in range(4):
                h = (cb - 2) * 4 + fc
                for tb in range(4):
                    fm_unit(cb, fc, tb * 512, 512, "rope", 0, S["kT"][h, :, tb * 512:(tb + 1) * 512])
                fm_unit(cb, fc, L, LC, "copy", 0, S["kT"][h, :, L:LT])
        elif cb < 6:
            for t in range(18):
                tm_unit(cb, t, "copy", S["v"][t * 128:(t + 1) * 128, (cb - 4) * 512:(cb - 3) * 512])
        elif cb < 8:
            for t in range(16):
                tm_unit(cb, t, "silu", S["sga"][t * 128:(t + 1) * 128, (cb - 6) * 512:(cb - 5) * 512])
        elif cb == 8:
            for t in range(18):
                tm_unit(cb, t, "f32", S["u"][t * 128:(t + 1) * 128, :])
        elif cb == 9:
            for fc in range(4):
                for tb in range(4):
                    fm_unit(cb, fc, tb * 512, 512, "silu", 0, S["sgsT"][fc * 128:(fc + 1) * 128, tb * 512:(tb + 1) * 512])
        else:
            for fc in range(4):
                r0 = (cb - 10) * 512 + fc * 128
                for tb in range(4):
                    fm_unit(cb, fc, tb * 512, 512, "sigm", 0, S["sgmT"][r0:r0 + 128, tb * 512:(tb + 1) * 512])


def phase_ssm(k):
    nc, P, I, S = k.nc, k.P, k.I, k.S
    MUL, ADD, SUB = ALU.mult, ALU.add, ALU.subtract
    with ExitStack() as ls:
        ToepT = k.sb("ss_toep", [128, 32, 128], BF16, ls)
        RCp = k.sb("ss_rcp", [128, 2, 2, 16, 256], BF16, ls)
        WT = k.sb("ss_wt", [128, 2, 16, 2, 128], BF16, ls)
        A8c = k.sb("ss_a8c", [128, 2, 16, 2], F32, ls)
        A8s = k.sb("ss_a8s", [128, 2, 16, 2], F32, ls)
        b_toep = [Buf(f"toep{g}") for g in range(32)]
        b_rcp, b_wt = Buf("rcp"), Buf("wt")
        b_U = [Buf(f"U{g}") for g in range(32)]
        b_zbf = [Buf("zbf0"), Buf("zbf1")]
        b_ygT = Buf("ygT")
        b_a8 = Buf("a8")
        pbk = [k.ps(f"ss_ps{i}", [128, 512], F32, ls) for i in range(8)]
        b_pbk = [Buf(f"ssps{i}") for i in range(8)]
        pc = {"i": 0}

        def nb():
            i = pc["i"] % 8
            pc["i"] += 1
            return pbk[i], b_pbk[i]

        csem = P.new_dsem("ss_c")
        with ExitStack() as l0:
            lre = k.sb("ss_lre", [128, 32], F32, l0)
            lim = k.sb("ss_lim", [128, 32], F32, l0)
            dtt = k.sb("ss_dt", [128, 32], F32, l0)
            alog = k.sb("ss_alog", [128, 32], F32, l0)
            th = k.sb("ss_th", [128, 32], F32, l0)
            b_l, b_dt, b_al = Buf("lrelim"), Buf("dtt"), Buf("alogth")
            for gp in range(2):
                for d in range(2):
                    P.dma("sp", lre[gp * 64:(gp + 1) * 64, d * 16:(d + 1) * 16], I["ssm_lre"][d, gp * 16:(gp + 1) * 16, :].rearrange("g p -> p g"),
                          writes=[b_l], sem=csem, allow_slow_non_contiguous=True)
                    P.dma("sp", lim[gp * 64:(gp + 1) * 64, d * 16:(d + 1) * 16], I["ssm_lim"][d, gp * 16:(gp + 1) * 16, :].rearrange("g p -> p g"),
                          writes=[b_l], sem=csem, allow_slow_non_contiguous=True)
                    P.dma("sp", dtt[gp * 64:(gp + 1) * 64, d * 16:(d + 1) * 16], I["ssm_ls"][d:d + 1, gp * 16:(gp + 1) * 16].broadcast_to([64, 16]),
                          writes=[b_dt], sem=csem)
            P.op("act", lambda e: e.activation(out=dtt[:], in_=dtt[:], func=AF.Exp), reads=[b_dt], writes=[b_dt])
            P.op("dve", lambda e: e.tensor_tensor(out=alog[:], in0=lre[:], in1=dtt[:], op=MUL), reads=[b_l, b_dt], writes=[b_al])
            P.op("dve", lambda e: e.scalar_tensor_tensor(out=th[:], in0=lim[:], scalar=1.0 / TWO_PI, in1=dtt[:], op0=MUL, op1=MUL),
                 reads=[b_l, b_dt], writes=[b_al])
            tabs = {}
            for nm, n in (("A", 9), ("B", 8)):
                tau = k.sb(f"ss_tau{nm}", [128, 32, n], F32, l0)
                ex = k.sb(f"ss_ex{nm}", [128, 32, n], F32, l0)
                yv = k.sb(f"ss_yv{nm}", [128, 32, n], F32, l0)
                sn = k.sb(f"ss_sn{nm}", [128, 32, n], F32, l0)
                cs = k.sb(f"ss_cs{nm}", [128, 32, n], F32, l0)
                b_tau, b_ex, b_yv, b_sn, b_cs = Buf("tau" + nm), Buf("ex" + nm), Buf("yv" + nm), Buf("sn" + nm), Buf("cs" + nm)
                P.dma("sp", tau[:], I["c_tau" + nm], writes=[b_tau], sem=csem)
                P.op("dve", lambda e, ex=ex, tau=tau, n=n: e.tensor_tensor(out=ex[:], in0=tau[:], in1=alog[:, :, None].broadcast_to([128, 32, n]), op=MUL),
                     reads=[b_tau, b_al], writes=[b_ex])
                P.op("act", lambda e, ex=ex: e.activation(out=ex[:], in_=ex[:], func=AF.Exp), reads=[b_ex], writes=[b_ex])
                P.op("dve", lambda e, yv=yv, tau=tau, n=n: e.tensor_tensor(out=yv[:], in0=tau[:], in1=th[:, :, None].broadcast_to([128, 32, n]), op=MUL),
                     reads=[b_tau, b_al], writes=[b_yv])
                fl = lambda t: t[:].rearrange("p a b -> p (a b)")
                range_sin(k, l0, fl(sn), fl(yv), [128, 32 * n], "ssr1" + nm, [b_yv], [b_sn])
                P.op("dve", lambda e, yv=yv: e.tensor_scalar(out=yv[:], in0=yv[:], scalar1=0.25, scalar2=None, op0=ADD), reads=[b_yv], writes=[b_yv])
                range_sin(k, l0, fl(cs), fl(yv), [128, 32 * n], "ssr2" + nm, [b_yv], [b_cs])
                P.op("dve", lambda e, cs=cs, ex=ex: e.tensor_tensor(out=cs[:], in0=cs[:], in1=ex[:], op=MUL), reads=[b_cs, b_ex], writes=[b_cs])
                P.op("dve", lambda e, sn=sn, ex=ex: e.tensor_tensor(out=sn[:], in0=sn[:], in1=ex[:], op=MUL), reads=[b_sn, b_ex], writes=[b_sn])
                tabs[nm] = (cs, sn, b_cs, b_sn)
            ARA, AIA, b_ARA, b_AIA = tabs["A"]
            ARB, AIB, b_ARB, b_AIB = tabs["B"]
            a1 = k.sb("ss_a1", [128, 2, 32], F32, l0)
            b_a1 = Buf("a1")
            for d in range(2):
                i8 = 8 if d == 0 else 0
                i1 = 1 if d == 0 else 7
                dsl = slice(d * 16, (d + 1) * 16)
                for ri in range(2):
                    P.op("dve", lambda e, d=d, ri=ri, i8=i8, dsl=dsl: e.tensor_copy(out=A8c[:, d, :, ri], in_=ARA[:, dsl, i8]), reads=[b_ARA], writes=[b_a8])
                P.op("dve", lambda e, d=d, i8=i8, dsl=dsl: e.tensor_scalar(out=A8s[:, d, :, 0], in0=AIA[:, dsl, i8], scalar1=-1.0, scalar2=None, op0=MUL),
                     reads=[b_AIA], writes=[b_a8])
                P.op("dve", lambda e, d=d, i8=i8, dsl=dsl: e.tensor_copy(out=A8s[:, d, :, 1], in_=AIA[:, dsl, i8]), reads=[b_AIA], writes=[b_a8])
                P.op("dve", lambda e, d=d, i1=i1, dsl=dsl: e.tensor_copy(out=a1[:, 0, dsl], in_=ARA[:, dsl, i1]), reads=[b_ARA], writes=[b_a1])
                P.op("dve", lambda e, d=d, i1=i1, dsl=dsl: e.tensor_copy(out=a1[:, 1, dsl], in_=AIA[:, dsl, i1]), reads=[b_AIA], writes=[b_a1])
            fz = k.sb("ss_fz", [128, 6, 32], F32, l0)
            b_fz = Buf("fz")
            P.op("dve", lambda e: e.tensor_tensor(out=fz[:, 0, :], in0=lre[:], in1=lre[:], op=MUL), reads=[b_l], writes=[b_fz])
            P.op("dve", lambda e: e.tensor_tensor(out=fz[:, 1, :], in0=lim[:], in1=lim[:], op=MUL), reads=[b_l], writes=[b_fz])
            P.op("dve", lambda e: e.tensor_tensor(out=fz[:, 0, :], in0=fz[:, 0, :], in1=fz[:, 1, :], op=ADD), reads=[b_fz], writes=[b_fz])
            P.op("dve", lambda e: e.reciprocal(out=fz[:, 1, :], in_=fz[:, 0, :]), reads=[b_fz], writes=[b_fz])
            P.op("dve", lambda e: e.tensor_scalar(out=fz[:, 0, :], in0=a1[:, 0, :], scalar1=-1.0, scalar2=None, op0=ADD), reads=[b_a1], writes=[b_fz])
            P.op("dve", lambda e: e.tensor_tensor(out=fz[:, 2, :], in0=fz[:, 0, :], in1=lre[:], op=MUL), reads=[b_fz, b_l], writes=[b_fz])
            P.op("dve", lambda e: e.tensor_tensor(out=fz[:, 3, :], in0=a1[:, 1, :], in1=lim[:], op=MUL), reads=[b_a1, b_l], writes=[b_fz])
            P.op("dve", lambda e: e.tensor_tensor(out=fz[:, 2, :], in0=fz[:, 2, :], in1=fz[:, 3, :], op=ADD), reads=[b_fz], writes=[b_fz])
            P.op("dve", lambda e: e.tensor_tensor(out=fz[:, 2, :], in0=fz[:, 2, :], in1=fz[:, 1, :], op=MUL), reads=[b_fz], writes=[b_fz])
            P.op("dve", lambda e: e.tensor_tensor(out=fz[:, 4, :], in0=a1[:, 1, :], in1=lre[:], op=MUL), reads=[b_a1, b_l], writes=[b_fz])
            P.op("dve", lambda e: e.tensor_tensor(out=fz[:, 5, :], in0=fz[:, 0, :], in1=lim[:], op=MUL), reads=[b_fz, b_l], writes=[b_fz])
            P.op("dve", lambda e: e.tensor_tensor(out=fz[:, 4, :], in0=fz[:, 4, :], in1=fz[:, 5, :], op=SUB), reads=[b_fz], writes=[b_fz])
            P.op("dve", lambda e: e.tensor_tensor(out=fz[:, 4, :], in0=fz[:, 4, :], in1=fz[:, 1, :], op=MUL), reads=[b_fz], writes=[b_fz])
            BT = k.sb("ss_BT", [128, 2, 2, 16, 16], F32, l0)
            BB = k.sb("ss_BB", [128, 2, 2, 16, 16], F32, l0)
            CN = k.sb("ss_CN", [128, 2, 2, 2, 128], F32, l0)
            CT = k.sb("ss_CT", [128, 2, 2, 16, 16], F32, l0)
            tA = k.sb("ss_tA", [128, 16, 9, 16], F32, l0)
            tB = k.sb("ss_tB", [128, 16, 9, 16], F32, l0)
            b_BT, b_BB, b_CN, b_CT, b_tA, b_tB = Buf("BT"), Buf("BB"), Buf("CN"), Buf("CT"), Buf("tA"), Buf("tB")
            for d in range(2):
                for ri in range(2):
                    bsrc = I["ssm_bre"] if ri == 0 else I["ssm_bim"]
                    csrc = I["ssm_cre"] if ri == 0 else I["ssm_cim"]
                    for gp in range(2):
                        P.dma("sp", BT[gp * 64:(gp + 1) * 64, d, ri, :, :], bsrc[d, gp * 16:(gp + 1) * 16].rearrange("g p c -> p g c"),
                              writes=[b_BT], sem=csem)
                        for blk in range(2):
                            g0 = gp * 16 + blk * 8
                            P.dma("sp", CN[:, d, ri, blk, gp * 64:(gp + 1) * 64], csrc[d, g0:g0 + 8].rearrange("g c p -> (g c) p"),
                                  writes=[b_CN], sem=csem)
            for d in range(2):
                for ri in range(2):
                    for blk in range(2):
                        pt, bp = nb()
                        P.op("pe", lambda e, d=d, ri=ri, blk=blk, pt=pt: e.transpose(out=pt[:, 0:128], in_=CN[:, d, ri, blk, :], identity=k.ident[:]),
                             reads=[b_CN, k.b_ident], writes=[bp])
                        P.op("dve", lambda e, d=d, ri=ri, blk=blk, pt=pt: e.tensor_copy(
                            out=CT[:, d, ri, blk * 8:(blk + 1) * 8, :].rearrange("p a b -> p (a b)"), in_=pt[:, 0:128]), reads=[bp], writes=[b_CT])
            for d in range(2):
                dsl = slice(d * 16, (d + 1) * 16)
                frb = lambda d=d, dsl=dsl: fz[:, 2, dsl][:, :, None].broadcast_to([128, 16, 16])
                fib = lambda d=d, dsl=dsl: fz[:, 4, dsl][:, :, None].broadcast_to([128, 16, 16])
                t16a = tA[:, :, 0, :]
                t16b = tB[:, :, 0, :]
                P.op("dve", lambda e, d=d, frb=frb: e.tensor_tensor(out=t16a, in0=BT[:, d, 0], in1=frb(), op=MUL), reads=[b_BT, b_fz], writes=[b_tA])
                P.op("dve", lambda e, d=d, fib=fib: e.tensor_tensor(out=t16b, in0=BT[:, d, 1], in1=fib(), op=MUL), reads=[b_BT, b_fz], writes=[b_tB])
                P.op("dve", lambda e, d=d: e.tensor_tensor(out=BB[:, d, 0], in0=t16a, in1=t16b, op=SUB), reads=[b_tA, b_tB], writes=[b_BB])
                P.op("dve", lambda e, d=d, frb=frb: e.tensor_tensor(out=t16a, in0=BT[:, d, 1], in1=frb(), op=MUL), reads=[b_BT, b_fz], writes=[b_tA])
                P.op("dve", lambda e, d=d, fib=fib: e.tensor_tensor(out=t16b, in0=BT[:, d, 0], in1=fib(), op=MUL), reads=[b_BT, b_fz], writes=[b_tB])
                P.op("dve", lambda e, d=d: e.tensor_tensor(out=BB[:, d, 1], in0=t16a, in1=t16b, op=ADD), reads=[b_tA, b_tB], writes=[b_BB])
            P.op("pool", lambda e: e.memset(RCp[:].rearrange("p a b c d -> p (a b c d)"), 0.0), writes=[b_rcp])
            for d in range(2):
                dsl = slice(d * 16, (d + 1) * 16)
                off = 112 if d == 0 else 0
                bc_c = lambda ri, d=d: CT[:, d, ri][:, :, None, :].broadcast_to([128, 16, 9, 16])
                bc_ar = lambda dsl=dsl: ARA[:, dsl, :][:, :, :, None].broadcast_to([128, 16, 9, 16])
                bc_ai = lambda dsl=dsl: AIA[:, dsl, :][:, :, :, None].broadcast_to([128, 16, 9, 16])
                dst = lambda ri, d=d, off=off: RCp[:, d, ri, :, off:off + 144].rearrange("p g (t c) -> p g t c", c=16)
                P.op("dve", lambda e, bc_c=bc_c, bc_ar=bc_ar: e.tensor_tensor(out=tA[:], in0=bc_c(0), in1=bc_ar(), op=MUL), reads=[b_CT, b_ARA], writes=[b_tA])
                P.op("dve", lambda e, bc_c=bc_c, bc_ai=bc_ai: e.tensor_tensor(out=tB[:], in0=bc_c(1), in1=bc_ai(), op=MUL), reads=[b_CT, b_AIA], writes=[b_tB])
                P.op("dve", lambda e, dst=dst: e.tensor_tensor(out=dst(0), in0=tA[:], in1=tB[:], op=SUB), reads=[b_tA, b_tB], writes=[b_rcp])
                P.op("dve", lambda e, bc_c=bc_c, bc_ai=bc_ai: e.tensor_tensor(out=tA[:], in0=bc_c(0), in1=bc_ai(), op=MUL), reads=[b_CT, b_AIA], writes=[b_tA])
                P.op("dve", lambda e, bc_c=bc_c, bc_ar=bc_ar: e.tensor_tensor(out=tB[:], in0=bc_c(1), in1=bc_ar(), op=MUL), reads=[b_CT, b_ARA], writes=[b_tB])
                P.op("dve", lambda e: e.tensor_tensor(out=tA[:], in0=tA[:], in1=tB[:], op=ADD), reads=[b_tA, b_tB], writes=[b_tA])
                P.op("dve", lambda e, dst=dst: e.tensor_scalar(out=dst(1), in0=tA[:], scalar1=-1.0, scalar2=None, op0=MUL), reads=[b_tA], writes=[b_rcp])
            Lp = k.sb("ss_Lp", [128, 64, 240], BF16, l0)
            b_Lp = Buf("Lp")
            P.op("pool", lambda e: e.memset(Lp[:].rearrange("p a b -> p (a b)"), 0.0), writes=[b_Lp])
            P.op("pool", lambda e: e.tensor_copy(out=Lp[:, :, 112:128], in_=BB[:].rearrange("p a b c d -> p (a b c) d")), reads=[b_BB], writes=[b_Lp])
            BW = k.sb("ss_BW", [128, 2, 2, 16, 128], F32, l0)
            b_BW = Buf("BW")
            for d in range(2):
                dsl = slice(d * 16, (d + 1) * 16)
                bc_b = lambda ri, d=d: BB[:, d, ri][:, :, None, :].broadcast_to([128, 16, 8, 16])
                bc_ar = lambda dsl=dsl: ARB[:, dsl, :][:, :, :, None].broadcast_to([128, 16, 8, 16])
                bc_ai = lambda dsl=dsl: AIB[:, dsl, :][:, :, :, None].broadcast_to([128, 16, 8, 16])
                dst = lambda ri, d=d: BW[:, d, ri].rearrange("p g (t c) -> p g t c", c=16)
                ta8 = tA[:, :, 0:8, :]
                tb8 = tB[:, :, 0:8, :]
                P.op("dve", lambda e, bc_b=bc_b, bc_ar=bc_ar: e.tensor_tensor(out=ta8, in0=bc_b(0), in1=bc_ar(), op=MUL), reads=[b_BB, b_ARB], writes=[b_tA])
                P.op("dve", lambda e, bc_b=bc_b, bc_ai=bc_ai: e.tensor_tensor(out=tb8, in0=bc_b(1), in1=bc_ai(), op=MUL), reads=[b_BB, b_AIB], writes=[b_tB])
                P.op("dve", lambda e, dst=dst: e.tensor_tensor(out=dst(0), in0=ta8, in1=tb8, op=SUB), reads=[b_tA, b_tB], writes=[b_BW])
                P.op("dve", lambda e, bc_b=bc_b, bc_ai=bc_ai: e.tensor_tensor(out=ta8, in0=bc_b(0), in1=bc_ai(), op=MUL), reads=[b_BB, b_AIB], writes=[b_tA])
                P.op("dve", lambda e, bc_b=bc_b, bc_ar=bc_ar: e.tensor_tensor(out=tb8, in0=bc_b(1), in1=bc_ar(), op=MUL), reads=[b_BB, b_ARB], writes=[b_tB])
                P.op("dve", lambda e, dst=dst: e.tensor_tensor(out=dst(1), in0=ta8, in1=tb8, op=ADD), reads=[b_tA, b_tB], writes=[b_BW])
            for d in range(2):
                for g2 in range(16):
                    for ri in range(2):
                        pt, bp = nb()
                        P.op("pe", lambda e, d=d, g2=g2, ri=ri, pt=pt: e.transpose(out=pt[:, 0:128], in_=BW[:, d, ri, g2, :], identity=k.ident[:]),
                             reads=[b_BW, k.b_ident], writes=[bp])
                        eng = "act" if (g2 + ri) % 2 else "dve"
                        if eng == "act":
                            P.op("act", lambda e, d=d, g2=g2, ri=ri, pt=pt: e.activation(out=WT[:, d, g2, ri, :], in_=pt[:, 0:128], func=AF.Copy), reads=[bp], writes=[b_wt])
                        else:
                            P.op("dve", lambda e, d=d, g2=g2, ri=ri, pt=pt: e.tensor_copy(out=WT[:, d, g2, ri, :], in_=pt[:, 0:128]), reads=[bp], writes=[b_wt])
            for g2 in range(16):
                for gp in range(2):
                    g = gp * 16 + g2
                    pt, bp = nb()
                    psl = slice(gp * 64, (gp + 1) * 64)
                    n = 0
                    for d in range(2):
                        for ri in range(2):
                            for s_ in range(8):
                                w0 = (7 - s_) * 16 if d == 0 else (8 - s_) * 16
                                l0_ = (7 - s_) * 16
                                P.op("pe", lambda e, d=d, ri=ri, g2=g2, w0=w0, l0_=l0_, psl=psl, pt=pt, n=n: e.matmul(
                                    pt[:, 0:128], lhsT=Lp[psl, (d * 2 + ri) * 16 + g2, l0_:l0_ + 128], rhs=RCp[psl, d, ri, g2, w0:w0 + 128],
                                    start=(n == 0), stop=(n == 31)), reads=[b_Lp, b_rcp], writes=[bp])
                                n += 1
                    P.op("dve" if g % 2 else "act",
                         (lambda e, g=g, pt=pt: e.tensor_copy(out=ToepT[:, g, :], in_=pt[:, 0:128])) if g % 2 else
                         (lambda e, g=g, pt=pt: e.activation(out=ToepT[:, g, :], in_=pt[:, 0:128], func=AF.Copy)),
                         reads=[bp], writes=[b_toep[g]])
            P.barrier()
        Ubuf = k.sb("ss_ubuf", [128, 32, 320], BF16, ls)
        Zbf = k.sb("ss_zbf", [128, 2, 16, 2, 288], BF16, ls)
        ygT = k.sb("ss_ygT", [128, 4, L], BF16, ls)
        if k.debug.get("_ssm_upto", 99) < 1:
            return
        with ExitStack() as l1:
            ucm = [k.sb(f"ss_ucm{i}", [128, 8, 512], F32, l1) for i in range(2)]
            b_ucm = [Buf(f"ucm{i}") for i in range(2)]
            usem = [P.new_dsem(f"ss_us{i}") for i in range(2)]
            ucg = k.sb("ss_ucg", [128, 32, 128], F32, l1)
            b_ucg = Buf("ucg")
            for jt in range(3):
                si = jt % 2
                nj = 128 if jt < 2 else 32
                r0 = jt * 1024
                P.dma("sp", ucm[si][0:nj], S["u"][r0:r0 + nj * 8, :].rearrange("(j s) c -> j s c", s=8), writes=[b_ucm[si]], sem=usem[si])
                P.op("dve", lambda e, si=si, nj=nj: e.tensor_copy(out=ucg[0:nj].rearrange("p g (s c) -> p g s c", c=16),
                                                                 in_=ucm[si][0:nj].rearrange("p s (g c) -> p g s c", c=16)),
                     reads=[b_ucm[si]], writes=[b_ucg])
                for g0 in range(0, 32, 4):
                    pt, bp = nb()
                    for gg in range(4):
                        g = g0 + gg
                        P.op("pe", lambda e, si=si, nj=nj, g=g, gg=gg, pt=pt: e.transpose(
                            out=pt[:, gg * 128:gg * 128 + nj], in_=ucg[0:nj, g, :], identity=k.ident[0:nj, 0:nj]),
                            reads=[b_ucg, k.b_ident], writes=[bp])
                    src = lambda pt=pt, nj=nj: pt[:].rearrange("p (a b) -> p a b", b=128)[:, :, 0:nj]
                    cols = [32 + jt * 128] if jt < 2 else [0, 288]
                    for ci, c0 in enumerate(cols):
                        eng = "act" if (g0 // 4 + ci) % 2 else "dve"
                        if eng == "act":
                            P.op("act", lambda e, g0=g0, c0=c0, nj=nj, src=src: e.activation(out=Ubuf[:, g0:g0 + 4, c0:c0 + nj], in_=src(), func=AF.Copy),
                                 reads=[bp], writes=[b_U[g0 + i] for i in range(4)])
                        else:
                            P.op("dve", lambda e, g0=g0, c0=c0, nj=nj, src=src: e.tensor_copy(out=Ubuf[:, g0:g0 + 4, c0:c0 + nj], in_=src()),
                                 reads=[bp], writes=[b_U[g0 + i] for i in range(4)])
            P.barrier()
        if k.debug.get("_ssm_upto", 99) < 2:
            return
        with ExitStack() as l2:
            Z = [k.sb(f"ss_Z{d}", [128, 16, 2, 288], F32, l2) for d in range(2)]
            b_Z = [Buf("Z0"), Buf("Z1")]
            for d in range(2):
                j0 = 0 if d == 0 else 32
                for g2 in range(16):
                    for ri in range(2):
                        pt, bp = nb()
                        for gp in range(2):
                            P.op("pe", lambda e, d=d, g2=g2, ri=ri, gp=gp, pt=pt, j0=j0: e.matmul(
                                pt[gp * 64:(gp + 1) * 64, 0:288], lhsT=WT[:, d, g2, ri, gp * 64:(gp + 1) * 64], rhs=Ubuf[:, gp * 16 + g2, j0:j0 + 288],
                                start=True, stop=True), reads=[b_wt, b_U[gp * 16 + g2]], writes=[bp])
                        if (g2 + ri) % 2:
                            P.op("act", lambda e, d=d, g2=g2, ri=ri, pt=pt: e.activation(out=Z[d][:, g2, ri, :], in_=pt[:, 0:288], func=AF.Copy), reads=[bp], writes=[b_Z[d]])
                        else:
                            P.op("dve", lambda e, d=d, g2=g2, ri=ri, pt=pt: e.tensor_copy(out=Z[d][:, g2, ri, :], in_=pt[:, 0:288]), reads=[bp], writes=[b_Z[d]])
            k.dbg("V_dbg", [2, 128, 16 * 2 * 288], F32, lambda dd: (dd[0], Z[0][:].rearrange("p a b c -> p (a b c)")), [b_Z[0]])
            k.dbg("V_dbg", [2, 128, 16 * 2 * 288], F32, lambda dd: (dd[1], Z[1][:].rearrange("p a b c -> p (a b c)")), [b_Z[1]])
            m1 = [k.sb(f"ss_m1{d}", [128, 16, 2], F32, l2) for d in range(2)]
            m2 = [k.sb(f"ss_m2{d}", [128, 16, 2], F32, l2) for d in range(2)]
            b_m1 = [Buf("m10"), Buf("m11")]
            b_m2 = [Buf("m20"), Buf("m21")]

            def scan_step(d, J, Jp):
                eng = "dve" if d == 0 else "pool"
                P.op(eng, lambda e: e.tensor_tensor(out=m1[d][:], in0=Z[d][:, :, :, Jp], in1=A8c[:, d], op=MUL), reads=[b_Z[d], b_a8], writes=[b_m1[d]])
                P.op(eng, lambda e: e.tensor_tensor(out=m2[d][:], in0=Z[d][:, :, ::-1, Jp], in1=A8s[:, d], op=MUL), reads=[b_Z[d], b_a8], writes=[b_m2[d]])
                P.op(eng, lambda e: e.tensor_tensor(out=m1[d][:], in0=m1[d][:], in1=m2[d][:], op=ADD), reads=[b_m1[d], b_m2[d]], writes=[b_m1[d]])
                P.op(eng, lambda e: e.tensor_tensor(out=Z[d][:, :, :, J], in0=Z[d][:, :, :, J], in1=m1[d][:], op=ADD), reads=[b_Z[d], b_m1[d]], writes=[b_Z[d]])

            for st_ in range(1, 288):
                scan_step(0, st_, st_ - 1)
                scan_step(1, 287 - st_, 288 - st_)
            for d in range(2):
                eng = "dve" if d == 0 else "pool"
                P.op(eng, lambda e, d=d: e.tensor_copy(out=Zbf[:, d].rearrange("p a b c -> p (a b c)"), in_=Z[d][:].rearrange("p a b c -> p (a b c)")),
                     reads=[b_Z[d]], writes=[b_zbf[d]])
            k.dbg("Z_dbg", [2, 128, 16 * 2 * 288], F32, lambda dd: (dd[0], Z[0][:].rearrange("p a b c -> p (a b c)")), [b_Z[0]])
            k.dbg("Z_dbg", [2, 128, 16 * 2 * 288], F32, lambda dd: (dd[1], Z[1][:].rearrange("p a b c -> p (a b c)")), [b_Z[1]])
            P.barrier()
        if k.debug.get("_ssm_upto", 99) < 3:
            return
        with ExitStack() as l3:
            ycm = k.sb("ss_ycm", [128, 8, 512], F32, l3)
            b_ycm = [Buf(f"ycm{g}") for g in range(32)]
            ut = k.sb("ss_ut", [128, 8, 512], F32, l3)
            b_ut = Buf("ut")
            utsem = P.new_dsem("ss_uts")
            Dfull = k.sb("ss_D", [128, 512], F32, l3)
            b_D = Buf("Dfull")
            P.dma("sp", Dfull[:], I["ssm_d"][0:1, :].broadcast_to([128, 512]), writes=[b_D], sem=csem)
            sq = [k.sb(f"ss_sq{i}", [128, 512], F32, l3) for i in range(2)]
            b_sq = [Buf("sq0"), Buf("sq1")]
            GC = math.sqrt(2.0 / math.pi)
            for jt in range(2):
                P.dma("sp", ut[:], S["u"][jt * 1024:(jt + 1) * 1024, :].rearrange("(j s) c -> j s c", s=8), writes=[b_ut], sem=utsem)
                for g in range(32):
                    gp, g2 = g // 16, g % 16
                    psl = slice(gp * 64, (gp + 1) * 64)
                    pt, bp = nb()
                    c0 = 32 + jt * 128
                    P.op("pe", lambda e, g=g, c0=c0, pt=pt: e.matmul(pt[:, 0:128], lhsT=Ubuf[:, g, c0:c0 + 128], rhs=ToepT[:, g, :], start=True, stop=False),
                         reads=[b_U[g], b_toep[g]], writes=[bp])
                    for d in range(2):
                        jz = (31 + jt * 128) if d == 0 else (1 + jt * 128)
                        w0 = 128 if d == 0 else 0
                        for ri in range(2):
                            last = (d == 1 and ri == 1)
                            P.op("pe", lambda e, d=d, ri=ri, g2=g2, psl=psl, jz=jz, w0=w0, pt=pt, last=last: e.matmul(
                                pt[:, 0:128], lhsT=Zbf[psl, d, g2, ri, jz:jz + 128], rhs=RCp[psl, d, ri, g2, w0:w0 + 128], start=False, stop=last),
                                reads=[b_zbf[d], b_rcp], writes=[bp])
                    src = lambda pt=pt: pt[:, 0:128].rearrange("p (t c) -> p t c", c=16)
                    P.op("dve", lambda e, g=g, src=src: e.tensor_tensor(out=ycm[:, :, g * 16:(g + 1) * 16], in0=ut[:, :, g * 16:(g + 1) * 16],
                                                                       in1=Dfull[:, g * 16:(g + 1) * 16][:, None, :].broadcast_to([128, 8, 16]), op=MUL),
                         reads=[b_ut, b_D], writes=[b_ycm[g]])
                    P.op("dve", lambda e, g=g, src=src: e.tensor_tensor(out=ycm[:, :, g * 16:(g + 1) * 16], in0=ycm[:, :, g * 16:(g + 1) * 16], in1=src(), op=ADD),
                         reads=[bp, b_ycm[g]], writes=[b_ycm[g]])
                k.dbg("y_dbg", [L, 512], F32, lambda dd, jt=jt: (dd[jt * 1024:(jt + 1) * 1024, :].rearrange("(j s) c -> j s c", s=8), ycm[:]), b_ycm)
                for t in range(8):
                    i = t % 2
                    P.op("dve", lambda e, t=t, i=i: e.tensor_tensor(out=sq[i][:], in0=ycm[:, t, :], in1=ycm[:, t, :], op=MUL), reads=b_ycm, writes=[b_sq[i]])
                    P.op("dve", lambda e, t=t, i=i: e.tensor_scalar(out=sq[i][:], in0=sq[i][:], scalar1=0.044715, scalar2=1.0, op0=MUL, op1=ADD), reads=[b_sq[i]], writes=[b_sq[i]])
                    P.op("dve", lambda e, t=t, i=i: e.tensor_tensor(out=sq[i][:], in0=sq[i][:], in1=ycm[:, t, :], op=MUL), reads=[b_sq[i]] + b_ycm, writes=[b_sq[i]])
                    P.op("act", lambda e, t=t, i=i: e.activation(out=sq[i][:], in_=sq[i][:], func=AF.Sigmoid, scale=2.0 * GC), reads=[b_sq[i]], writes=[b_sq[i]])
                    P.op("dve", lambda e, t=t, i=i: e.tensor_tensor(out=sq[i][:], in0=sq[i][:], in1=ycm[:, t, :], op=MUL), reads=[b_sq[i]] + b_ycm, writes=[b_sq[i]])
                    pt, bp = nb()
                    for chb in range(4):
                        P.op("pe", lambda e, i=i, chb=chb, pt=pt: e.transpose(out=pt[:, chb * 128:(chb + 1) * 128], in_=sq[i][:, chb * 128:(chb + 1) * 128], identity=k.ident[:]),
                             reads=[b_sq[i], k.b_ident], writes=[bp])
                    tsl = slice(jt * 1024 + t, (jt + 1) * 1024, 8)
                    P.op("act", lambda e, pt=pt, tsl=tsl: e.activation(out=ygT[:, :, tsl], in_=pt[:].rearrange("p (a b) -> p a b", b=128), func=AF.Copy),
                         reads=[bp], writes=[b_ygT])
            P.barrier()
        if k.debug.get("_ssm_upto", 99) < 4:
            return
        with ExitStack() as l4:
            wg32 = k.sb("ss_wg32", [128, 4, 512], F32, l4)
            wg = k.sb("ss_wg", [128, 4, 512], BF16, l4)
            bg = k.sb("ss_bg", [128, 4], F32, l4)
            b_wg32, b_wg, b_bg = Buf("wg32"), Buf("wg"), Buf("bg")
            P.dma("sp", wg32[:], I["w_glu"].rearrange("(fc p) c -> p fc c", p=128), writes=[b_wg32], sem=csem)
            P.dma("sp", bg[:], I["b_glu"].rearrange("(fc p) -> p fc", p=128), writes=[b_bg], sem=csem, allow_slow_non_contiguous=True)
            P.op("dve", lambda e: e.tensor_copy(out=wg[:], in_=wg32[:]), reads=[b_wg32], writes=[b_wg])
            gst = [k.sb(f"ss_gst{i}", [128, 512], BF16, l4) for i in range(2)]
            b_gst = [Buf("gst0"), Buf("gst1")]
            gsem = [P.new_dsem(f"ss_gs{i}") for i in range(2)]
            sg = [k.sb(f"ss_sg{i}", [128, 512], F32, l4) for i in range(2)]
            b_sg = [Buf("sg0"), Buf("sg1")]
            so = [k.sb(f"ss_so{i}", [128, 512], BF16, l4) for i in range(2)]
            b_so = [Buf("so0"), Buf("so1")]
            sosem = [P.new_dsem(f"ss_sos{i}") for i in range(2)]
            ui = 0
            for fo in range(4):
                for tb in range(4):
                    i = ui % 2
                    ui += 1
                    tsl = slice(tb * 512, (tb + 1) * 512)
                    P.dma("sp", gst[i][:], S["sgsT"][fo * 128:(fo + 1) * 128, tsl], writes=[b_gst[i]], sem=gsem[i])
                    pt, bp = nb()
                    for fc in range(4):
                        P.op("pe", lambda e, fc=fc, fo=fo, tsl=tsl, pt=pt: e.matmul(pt[:], lhsT=wg[:, fc, fo * 128:(fo + 1) * 128], rhs=ygT[:, fc, tsl],
                                                                               start=(fc == 0), stop=(fc == 3)), reads=[b_wg, b_ygT], writes=[bp])
                    P.op("act", lambda e, i=i, fo=fo, pt=pt: e.activation(out=sg[i][:], in_=pt[:], func=AF.Sigmoid, bias=bg[:, fo:fo + 1]),
                         reads=[bp, b_bg], writes=[b_sg[i]])
                    P.op("dve", lambda e, i=i, fo=fo, tsl=tsl: e.tensor_tensor(out=sg[i][:], in0=sg[i][:], in1=ygT[:, fo, tsl], op=MUL),
                         reads=[b_sg[i], b_ygT], writes=[b_sg[i]])
                    P.op("dve", lambda e, i=i: e.tensor_tensor(out=so[i][:], in0=sg[i][:], in1=gst[i][:], op=MUL),
                         reads=[b_sg[i], b_gst[i]], writes=[b_so[i]])
                    P.dma("sp", S["sbrT"][fo * 128:(fo + 1) * 128, tsl], so[i][:], reads=[b_so[i]], sem=sosem[i])


def phase_attn(k):
    nc, P, I, S = k.nc, k.P, k.I, k.S
    with ExitStack() as ls:
        lamv = k.sb("at_lamv", [128, 4, 64], F32, ls)
        lw = k.sb("at_lw", [128, 8], F32, ls)
        G = k.sb("at_G", [128, 128], F32, ls)
        b_lamv, b_lw, b_G = Buf("lamv"), Buf("lw"), Buf("G")
        csem = P.new_dsem("at_c")
        P.dma("sp", lamv[:].rearrange("p a b -> p (a b)"), I["lam"].rearrange("a b -> (a b)").partition_broadcast(128),
              writes=[b_lamv], sem=csem)
        P.dma("sp", G[:], I["subln_g"][0:1, :].broadcast_to([128, 128]), writes=[b_G], sem=csem)
        P.op("dve", lambda e: e.tensor_scalar(out=G[:], in0=G[:], scalar1=(1.0 - LAM_INIT), scalar2=None, op0=ALU.mult),
             reads=[b_G], writes=[b_G])
        for i in range(2):
            P.op("dve", lambda e, i=i: e.tensor_tensor(out=lamv[:, 2 * i, :], in0=lamv[:, 2 * i, :], in1=lamv[:, 2 * i + 1, :], op=ALU.mult),
                 reads=[b_lamv], writes=[b_lamv])
            P.op("dve", lambda e, i=i: e.tensor_reduce(out=lw[:, i:i + 1], in_=lamv[:, 2 * i, :], axis=mybir.AxisListType.X, op=ALU.add),
                 reads=[b_lamv], writes=[b_lw])
        P.op("act", lambda e: e.activation(out=lw[:, 2:4], in_=lw[:, 0:2], func=AF.Exp), reads=[b_lw], writes=[b_lw])
        P.op("dve", lambda e: e.tensor_tensor(out=lw[:, 4:5], in0=lw[:, 3:4], in1=lw[:, 2:3], op=ALU.subtract), reads=[b_lw], writes=[b_lw])
        P.op("dve", lambda e: e.tensor_scalar(out=lw[:, 5:6], in0=lw[:, 4:5], scalar1=-LAM_INIT, scalar2=None, op0=ALU.add),
             reads=[b_lw], writes=[b_lw])
        neglam = lw[:, 5:6]
        qTs = [k.sb(f"at_q{i}", [128, L], BF16, ls) for i in range(2)]
        kTs = [k.sb(f"at_k{i}", [128, LT], BF16, ls) for i in range(2)]
        Vs = [k.sb(f"at_v{i}", [128, 18, 130], BF16, ls) for i in range(2)]
        gas = [k.sb(f"at_ga{i}", [128, 16, 128], BF16, ls) for i in range(2)]
        aTs = [k.sb(f"at_aT{i}", [128, L], BF16, ls) for i in range(2)]
        b_q = [Buf(f"atq{i}") for i in range(2)]
        b_k = [Buf(f"atk{i}") for i in range(2)]
        b_v = [Buf(f"atv{i}") for i in range(2)]
        b_ga = [Buf(f"atga{i}") for i in range(2)]
        b_aT = [Buf(f"ataT{i}") for i in range(2)]
        hsem = [P.new_dsem(f"at_h{i}") for i in range(2)]
        asem = [P.new_dsem(f"at_a{i}") for i in range(2)]
        for i in range(2):
            P.op("pool", lambda e, i=i: e.memset(Vs[i][:, :, 128:130], 1.0), writes=[b_v[i]])
        PT = [k.sb(f"at_pt{i}", [128, 2, 18, 256], BF16, ls) for i in range(2)]
        b_PT = [[[Buf(f"pt{i}_{c}_{kp}") for kp in range(9)] for c in range(2)] for i in range(2)]
        sbk = [k.ps(f"at_s{i}", [128, 512], F32, ls) for i in range(3)]
        b_sbk = [Buf(f"ats{i}") for i in range(3)]
        obk = [k.ps(f"at_o{i}", [128, 512], F32, ls) for i in range(4)]
        b_obk = [Buf(f"ato{i}") for i in range(4)]
        tbk = k.ps("at_t", [128, 512], F32, ls)
        b_tbk = Buf("att")
        sm = [k.sb(f"at_sm{i}", [128, 8], F32, ls) for i in range(2)]
        b_sm = [Buf(f"atsm{i}") for i in range(2)]
        tmp = [k.sb(f"at_tmp{i}", [128, 128], F32, ls) for i in range(2)]
        b_tmp = [Buf(f"attmp{i}") for i in range(2)]
        ov = [k.sb(f"at_ov{i}", [128, 128], F32, ls) for i in range(2)]
        b_ov = [Buf(f"atov{i}") for i in range(2)]
        junk = k.sb("at_junk", [128, 128], F32, ls)
        b_junk = Buf("atjunk")
        cnt = {"s": 0, "u": 0}

        def load_head(h):
            s = h % 2
            P.dma("sp", qTs[s][:], S["qT"][h], writes=[b_q[s]], sem=hsem[s])
            P.dma("sp", kTs[s][:], S["kT"][h], writes=[b_k[s]], sem=hsem[s])
            P.dma("sp", Vs[s][:, :, 0:128], S["v"][:, h * 128:(h + 1) * 128].rearrange("(t p) e -> p t e", p=128),
                  writes=[b_v[s]], sem=hsem[s])
            P.dma("sp", gas[s][:], S["sga"][:, h * 128:(h + 1) * 128].rearrange("(t p) e -> p t e", p=128),
                  writes=[b_ga[s]], sem=hsem[s])

        def A_steps(h, qb):
            s = h % 2
            ps_ = qb % 2
            steps = []
            for kp in range(9):
                def step(kp=kp):
                    for c in range(2):
                        si = cnt["s"] % 3
                        cnt["s"] += 1
                        for j in range(2):
                            kt = 2 * kp + j
                            P.op("pe", lambda e, kt=kt, j=j, c=c, si=si: e.matmul(
                                sbk[si][:, j * 256:(j + 1) * 256], lhsT=kTs[s][c * 64:(c + 1) * 64, kt * 128:(kt + 1) * 128],
                                rhs=qTs[s][c * 64:(c + 1) * 64, qb * 256:(qb + 1) * 256], start=True, stop=True),
                                reads=[b_k[s], b_q[s]], writes=[b_sbk[si]])
                        P.op("act", lambda e, c=c, kp=kp, si=si: e.activation(
                            out=PT[ps_][:, c, 2 * kp:2 * kp + 2, :].rearrange("p a b -> p (a b)"), in_=sbk[si][:], func=AF.Exp, scale=0.125),
                            reads=[b_sbk[si]], writes=[b_PT[ps_][c][kp]])
                steps.append(step)
            return steps

        def B_gen(h, qb):
            s = h % 2
            ps_ = qb % 2
            for qi_ in range(2):
                yield from unitB(h, qb, qi_, s, ps_)

        def unitB(h, qb, qi, s, ps_):
            if True:
                qt = qb * 2 + qi
                u = cnt["u"] % 2
                cnt["u"] += 1
                banks = [obk[u * 2], obk[u * 2 + 1]]
                bb = [b_obk[u * 2], b_obk[u * 2 + 1]]
                for c in range(2):
                    for kt in range(18):
                        P.op("pe", lambda e, c=c, kt=kt: e.matmul(
                            banks[c][:, 0:129], lhsT=PT[ps_][:, c, kt, qi * 128:(qi + 1) * 128], rhs=Vs[s][:, kt, 0:129],
                            start=(kt == 0), stop=(kt == 17)),
                            reads=[b_PT[ps_][c][kt // 2], b_v[s]], writes=[bb[c]])
                        yield
                flush_pending()
                smt, bsm = sm[u], b_sm[u]
                for c in range(2):
                    P.op("dve", lambda e, c=c: e.reciprocal(out=smt[:, c:c + 1], in_=banks[c][:, 128:129]), reads=[bb[c]], writes=[bsm])
                P.op("dve", lambda e: e.tensor_tensor(out=smt[:, 2:3], in0=smt[:, 1:2], in1=neglam, op=ALU.mult), reads=[bsm, b_lw], writes=[bsm])
                P.op("dve", lambda e: e.tensor_scalar(out=tmp[u][:], in0=banks[1][:, 0:128], scalar1=smt[:, 2:3], scalar2=None, op0=ALU.mult),
                     reads=[bb[1], bsm], writes=[b_tmp[u]])
                P.op("dve", lambda e: e.scalar_tensor_tensor(out=ov[u][:], in0=banks[0][:, 0:128], scalar=smt[:, 0:1], in1=tmp[u][:],
                                                            op0=ALU.mult, op1=ALU.add),
                     reads=[bb[0], bsm, b_tmp[u]], writes=[b_ov[u]])
                P.op("dve", lambda e: e.tensor_tensor(out=tmp[u][:], in0=ov[u][:], in1=ov[u][:], op=ALU.mult),
                     reads=[b_ov[u]], writes=[b_tmp[u]])
                P.op("dve", lambda e: e.tensor_reduce(out=smt[:, 3:4], in_=tmp[u][:], axis=mybir.AxisListType.X, op=ALU.add),
                     reads=[b_tmp[u]], writes=[bsm])
                P.op("dve", lambda e: e.tensor_scalar(out=smt[:, 4:5], in0=smt[:, 3:4], scalar1=1.0 / 128, scalar2=EPS, op0=ALU.mult, op1=ALU.add),
                     reads=[bsm], writes=[bsm])
                P.op("act", lambda e: e.activation(out=smt[:, 5:6], in_=smt[:, 4:5], func=AF.Ln), reads=[bsm], writes=[bsm])
                P.op("act", lambda e: e.activation(out=smt[:, 6:7], in_=smt[:, 5:6], func=AF.Exp, scale=-0.5), reads=[bsm], writes=[bsm])
                P.op("dve", lambda e: e.scalar_tensor_tensor(out=ov[u][:], in0=ov[u][:], scalar=smt[:, 6:7], in1=G[:], op0=ALU.mult, op1=ALU.mult),
                     reads=[b_ov[u], bsm, b_G], writes=[b_ov[u]])
                P.op("pool", lambda e: e.tensor_tensor(out=ov[u][:], in0=ov[u][:], in1=gas[s][:, qt, :], op=ALU.mult),
                     reads=[b_ov[u], b_ga[s]], writes=[b_ov[u]])
                def fin():
                    P.op("pe", lambda e: e.transpose(out=tbk[:, 0:128], in_=ov[u][:], identity=k.ident[:]), reads=[b_ov[u], k.b_ident], writes=[b_tbk])
                    P.op("dve", lambda e: e.tensor_copy(out=aTs[s][:, qt * 128:(qt + 1) * 128], in_=tbk[:, 0:128]),
                         reads=[b_tbk], writes=[b_aT[s]])
                pending.append(fin)

        pending = []

        def flush_pending():
            while pending:
                pending.pop(0)()

        def interleave(a_steps, bgen, per=8):
            for st_ in a_steps:
                st_()
                if bgen is not None:
                    for _ in range(per):
                        try:
                            next(bgen)
                        except StopIteration:
                            bgen = None
                            break
            if bgen is not None:
                for _ in bgen:
                    pass

        load_head(0)
        load_head(1)
        interleave(A_steps(0, 0), None)
        for h in range(HEADS):
            for qb in range(8):
                if qb + 1 < 8:
                    nxt = A_steps(h, qb + 1)
                elif h + 1 < HEADS:
                    nxt = A_steps(h + 1, 0)
                else:
                    nxt = []
                interleave(nxt, B_gen(h, qb))
            flush_pending()
            P.dma("pool", S["abrT"][h * 128:(h + 1) * 128, :], aTs[h % 2][:], reads=[b_aT[h % 2]], sem=asem[h % 2])
            if h + 2 < HEADS:
                load_head(h + 2)


def phase_merge(k):
    nc, P, I, S = k.nc, k.P, k.I, k.S
    with ExitStack() as ls:
        mT = k.sb("mg_mT", [128, NKC, L], BF16, ls)
        b_mT = [Buf(f"mT{tb}") for tb in range(4)]
        wout = k.sb("mg_wout", [128, NKC, D], BF16, ls)
        b_wout = Buf("wout")
        NXB = 2
        wov = I["w_out"].rearrange("(kc p) c -> p kc c", p=128)
        wo_state = {"kc": 0}
        b_woutc = [Buf(f"woutc{i}") for i in range(NKC)]

        def load_wout_chunk():
            kc = wo_state["kc"]
            if kc >= NKC:
                return
            wo_state["kc"] += 1
            P.dma("pool", wout[:, kc, :], wov[:, kc, :], writes=[b_woutc[kc]])
        with ExitStack() as l1:
            abrT = k.sb("mg_abrT", [128, 8, L], BF16, l1)
            sbrT = k.sb("mg_sbrT", [128, 4, L], BF16, l1)
            b_abrT, b_sbrT = Buf("abrT"), Buf("sbrT")
            lsem = P.new_dsem("mg_l")
            P.dma("sp", abrT[:], S["abrT"].rearrange("(fc p) t -> p fc t", p=128), writes=[b_abrT], sem=lsem)
            P.dma("sp", sbrT[:], S["sbrT"].rearrange("(fc p) t -> p fc t", p=128), writes=[b_sbrT], sem=lsem)
            NWS = 2
            wbf = [k.sb(f"mg_wbf{i}", [128, 12, 128], BF16, l1) for i in range(NWS)]
            b_wbf = [Buf(f"mgwbf{i}") for i in range(NWS)]
            NG = 2
            gt = [k.sb(f"mg_gt{i}", [128, 2, L], BF16, l1) for i in range(NG)]
            b_gt = [Buf(f"mggt{i}") for i in range(NG)]
            t1 = [k.sb(f"mg_t1{i}", [128, 512], F32, l1) for i in range(2)]
            t2 = [k.sb(f"mg_t2{i}", [128, 512], F32, l1) for i in range(2)]
            b_t1 = [Buf(f"mgt1{i}") for i in range(2)]
            b_t2 = [Buf(f"mgt2{i}") for i in range(2)]
            pa = [k.ps(f"mg_pa{i}", [128, 512], F32, l1) for i in range(2)]
            pp = [k.ps(f"mg_pp{i}", [128, 512], F32, l1) for i in range(2)]
            b_pa = [Buf(f"mgpa{i}") for i in range(2)]
            b_pp = [Buf(f"mgpp{i}") for i in range(2)]
            wpa_v = I["w_pa"].rearrange("(fc p) c -> p fc c", p=128)
            wps_v = I["w_ps"].rearrange("(fc p) c -> p fc c", p=128)
            ui = 0

            def load_w(fo):
                s = fo % NWS
                P.dma("pool", wbf[s][:, 0:8, :], wpa_v[:, :, fo * 128:(fo + 1) * 128], writes=[b_wbf[s]])
                P.dma("pool", wbf[s][:, 8:12, :], wps_v[:, :, fo * 128:(fo + 1) * 128], writes=[b_wbf[s]])
                gi = fo % NG
                P.dma("sp", gt[gi][:, 0, :], S["sgmT"][fo * 128:(fo + 1) * 128, :], writes=[b_gt[gi]])
                P.dma("sp", gt[gi][:, 1, :], S["sgmT"][D + fo * 128:D + (fo + 1) * 128, :], writes=[b_gt[gi]])

            load_w(0)
            for fo in range(NKC):
                if fo + 1 < NKC:
                    load_w(fo + 1)
                load_wout_chunk()
                s = fo % NWS
                gi = fo % NG
                for tb in range(4):
                    u2 = ui % 2
                    ui += 1
                    tsl = slice(tb * 512, (tb + 1) * 512)
                    for fc in range(8):
                        P.op("pe", lambda e, fc=fc, s=s, tsl=tsl, u2=u2: e.matmul(pa[u2][:], lhsT=wbf[s][:, fc, :], rhs=abrT[:, fc, tsl],
                                                                        start=(fc == 0), stop=(fc == 7)),
                             reads=[b_wbf[s], b_abrT], writes=[b_pa[u2]])
                    for fc in range(4):
                        P.op("pe", lambda e, fc=fc, s=s, tsl=tsl, u2=u2: e.matmul(pp[u2][:], lhsT=wbf[s][:, 8 + fc, :], rhs=sbrT[:, fc, tsl],
                                                                        start=(fc == 0), stop=(fc == 3)),
                             reads=[b_wbf[s], b_sbrT], writes=[b_pp[u2]])
                    P.op("dve", lambda e, gi=gi, u2=u2, tsl=tsl: e.tensor_tensor(out=t1[u2][:], in0=pa[u2][:], in1=gt[gi][:, 0, tsl], op=ALU.mult),
                         reads=[b_pa[u2], b_gt[gi]], writes=[b_t1[u2]])
                    P.op("dve", lambda e, gi=gi, u2=u2, tsl=tsl: e.tensor_tensor(out=t2[u2][:], in0=pp[u2][:], in1=gt[gi][:, 1, tsl], op=ALU.mult),
                         reads=[b_pp[u2], b_gt[gi]], writes=[b_t2[u2]])
                    P.op("pool", lambda e, fo=fo, tsl=tsl, u2=u2: e.tensor_tensor(out=mT[:, fo, tsl], in0=t1[u2][:], in1=t2[u2][:], op=ALU.add),
                         reads=[b_t1[u2], b_t2[u2]], writes=[b_mT[tb]])
            P.barrier()
        gateB = k.sb("mg_gateB", [128, D], F32, ls)
        fgB = k.sb("mg_fgB", [128, D], F32, ls)
        b_gateB, b_fgB = Buf("gateB"), Buf("fgB")
        c2 = P.new_dsem("mg_c2")
        P.dma("sp", gateB[:], S["modrow"][0:1, 2 * D:3 * D].broadcast_to([128, D]), writes=[b_gateB], sem=c2)
        P.dma("sp", fgB[:], I["final_g"][0:1, :].broadcast_to([128, D]), writes=[b_fgB], sem=c2)
        xb = [k.sb(f"mg_x{i}", [128, D], F32, ls) for i in range(NXB)]
        b_xb = [Buf(f"mgx{i}") for i in range(NXB)]
        xn = [k.sb(f"mg_xn{i}", [128, D], F32, ls) for i in range(NXB)]
        b_xn = [Buf(f"mgxn{i}") for i in range(NXB)]
        xsem = [P.new_dsem(f"mg_xs{i}") for i in range(NXB)]
        osem = [P.new_dsem(f"mg_os{i}") for i in range(NXB)]
        st2 = [k.sb(f"mg_st{i}", [128, 4], F32, ls) for i in range(NXB)]
        b_st2 = [Buf(f"mgst{i}") for i in range(NXB)]
        while wo_state["kc"] < NKC:
            load_wout_chunk()
        po = [k.ps(f"mg_po{i}", [128, 512], F32, ls) for i in range(3)]
        b_po = [Buf(f"mgpo{i}") for i in range(3)]
        pi = 0
        for t in range(16):
            s = t % NXB
            tb = t // 4
            P.dma("sp", xb[s][:], I["x"][t * 128:(t + 1) * 128, :], writes=[b_xb[s]], sem=xsem[s])
            for cbk in range(4):
                p_ = pi % 3
                pi += 1
                for kc in range(NKC):
                    P.op("pe", lambda e, kc=kc, cbk=cbk, p_=p_, t=t: e.matmul(po[p_][:], lhsT=mT[:, kc, t * 128:(t + 1) * 128],
                                                                          rhs=wout[:, kc, cbk * 512:(cbk + 1) * 512],
                                                                          start=(kc == 0), stop=(kc == NKC - 1)),
                         reads=[b_mT[tb], b_woutc[kc]], writes=[b_po[p_]])
                P.op("dve", lambda e, cbk=cbk, p_=p_, s=s: e.tensor_tensor(out=xn[s][:, cbk * 512:(cbk + 1) * 512], in0=po[p_][:],
                                                                       in1=gateB[:, cbk * 512:(cbk + 1) * 512], op=ALU.mult),
                     reads=[b_po[p_], b_gateB], writes=[b_xn[s]])
            P.op("pool", lambda e, s=s: e.tensor_tensor(out=xn[s][:], in0=xn[s][:], in1=xb[s][:], op=ALU.add),
                 reads=[b_xn[s], b_xb[s]], writes=[b_xn[s]])
            P.op("pool", lambda e, s=s: e.tensor_tensor(out=xb[s][:], in0=xn[s][:], in1=xn[s][:], op=ALU.mult),
                 reads=[b_xn[s]], writes=[b_xb[s]])
            P.op("dve", lambda e, s=s: e.tensor_reduce(out=st2[s][:, 0:1], in_=xb[s][:], axis=mybir.AxisListType.X, op=ALU.add),
                 reads=[b_xb[s]], writes=[b_st2[s]])
            P.op("dve", lambda e, s=s: e.tensor_scalar(out=st2[s][:, 1:2], in0=st2[s][:, 0:1], scalar1=1.0 / D, scalar2=EPS, op0=ALU.mult, op1=ALU.add),
                 reads=[b_st2[s]], writes=[b_st2[s]])
            P.op("act", lambda e, s=s: e.activation(out=st2[s][:, 2:3], in_=st2[s][:, 1:2], func=AF.Ln), reads=[b_st2[s]], writes=[b_st2[s]])
            P.op("act", lambda e, s=s: e.activation(out=st2[s][:, 3:4], in_=st2[s][:, 2:3], func=AF.Exp, scale=-0.5), reads=[b_st2[s]], writes=[b_st2[s]])
            P.op("dve", lambda e, s=s: e.scalar_tensor_tensor(out=xn[s][:], in0=xn[s][:], scalar=st2[s][:, 3:4], in1=fgB[:], op0=ALU.mult, op1=ALU.mult),
                 reads=[b_xn[s], b_st2[s], b_fgB], writes=[b_xn[s]])
            P.dma("sp", k.out[t * 128:(t + 1) * 128, :], xn[s][:], reads=[b_xn[s]], sem=osem[s])


_CACHE = {}


def _prep_inputs(inputs, b):
    f = lambda a: np.ascontiguousarray(np.asarray(a, dtype=np.float32))
    m = {}
    m["x"] = f(inputs["x"][b])
    m["ctx"] = f(inputs["ctx"][b])
    m["cc"] = f(np.stack([np.asarray(inputs["c"])[b], np.asarray(inputs["c_ctx"])], axis=0))
    m["w_ada"] = f(inputs["w_ada"][0])
    m["b_ada"] = f(inputs["b_ada"][0]).reshape(1, -1)
    m["norm_g"] = f(inputs["norm_g"][0])
    m["w_in"] = f(inputs["w_in"][0])
    m["lam"] = f(np.stack([np.asarray(inputs["lambda_q1"])[0], np.asarray(inputs["lambda_k1"])[0],
                           np.asarray(inputs["lambda_q2"])[0], np.asarray(inputs["lambda_k2"])[0]], axis=0))
    m["subln_g"] = f(inputs["subln_g"][0]).reshape(1, 128)
    m["ssm_lre"] = f(inputs["ssm_lambda_re"][0])
    m["ssm_lim"] = f(inputs["ssm_lambda_im"][0])
    m["ssm_ls"] = f(inputs["ssm_log_step"][0])
    m["ssm_bre"] = f(inputs["ssm_b_re"][0])
    m["ssm_bim"] = f(inputs["ssm_b_im"][0])
    m["ssm_cre"] = f(inputs["ssm_c_re"][0])
    m["ssm_cim"] = f(inputs["ssm_c_im"][0])
    m["ssm_d"] = f(inputs["ssm_d"][0]).reshape(1, 512)
    m["w_glu"] = f(inputs["w_glu"][0])
    m["b_glu"] = f(inputs["b_glu"][0])
    m["w_pa"] = f(inputs["w_pa"][0])
    m["w_ps"] = f(inputs["w_ps"][0])
    m["w_out"] = f(inputs["w_out"][0])
    m["final_g"] = f(inputs["final_g"]).reshape(1, D)
    m.update(_consts())
    return m


def kernel(**inputs):
    if "nc" not in _CACHE:
        _CACHE["nc"] = build()[0]
    nc = _CACHE["nc"]
    shared = None
    in_maps = []
    for b in range(8):
        m = _prep_inputs(inputs, b)
        if shared is None:
            shared = m
        else:
            for key in m:
                if key not in ("x", "ctx", "cc"):
                    m[key] = shared[key]
        in_maps.append(m)
    res = run_bass_kernel_spmd(nc, in_maps, core_ids=list(range(8)))
    return np.stack([np.asarray(r["out"], dtype=np.float32) for r in res.results], axis=0)
```

```python
import math
import numpy as np
import ml_dtypes
from contextlib import ExitStack
import concourse.bass as bass
import concourse.mybir as mybir
from concourse.bass_utils import run_bass_kernel_spmd

F32 = mybir.dt.float32
BF16 = mybir.dt.bfloat16
I32 = mybir.dt.int32
AF = mybir.ActivationFunctionType
ALU = mybir.AluOpType

D = 2048
L = 2048
LC = 256
LT = L + LC
NKC = D // 128
INW = 9216
HEADS = 8
EPS = 1e-6
LAM_INIT = 0.8 - 0.6 * math.exp(-0.3 * 0)
TWO_PI = 2.0 * math.pi


class Buf:
    __slots__ = ("name", "w", "r")

    def __init__(self, name):
        self.name = name
        self.w = None
        self.r = {}


class Prog:
    ENG = ["pe", "act", "dve", "pool", "sp"]

    def __init__(self, nc, st):
        self.nc = nc
        self.st = st
        self.q = {e: [] for e in self.ENG}
        self.seen = {e: {} for e in self.ENG}
        self.psem = {e: st.enter_context(nc.semaphore("p_" + e)) for e in ["pe", "act", "dve", "pool"]}
        self.dsems = []
        self.bufsem = {}
        self.bufsem_keep = []
        self.free_dsems = []

    def new_dsem(self, name):
        return None

    def _auto_dsem(self, reads, writes):
        b = writes[0] if len(writes) else reads[0]
        key = id(b)
        d = self.bufsem.get(key)
        if d is None:
            if self.free_dsems:
                d = self.free_dsems.pop()
            else:
                h = self.st.enter_context(self.nc.semaphore(f"d{len(self.dsems)}"))
                d = {"h": h, "n": 0, "name": f"d{len(self.dsems)}"}
                self.dsems.append(d)
            self.bufsem[key] = d
            self.bufsem_keep.append(b)
        return d

    def _deps(self, eng, reads, writes):
        need = {}

        def add(t):
            if t[0] == "c":
                if t[1] == "pe" and eng == "pe":
                    return
                key = ("c", t[1])
                if need.get(key, (None, -1))[1] < t[2]:
                    need[key] = (t[1], t[2])
            else:
                key = ("d", id(t[1]))
                if need.get(key, (None, -1))[1] < t[2]:
                    need[key] = (t[1], t[2])

        for b in reads:
            if b.w is not None:
                add(b.w)
        for b in writes:
            if b.w is not None:
                add(b.w)
            for t in b.r.values():
                add(t)
        waits = []
        for key, (obj, v) in need.items():
            if self.seen[eng].get(key, -1) >= v:
                continue
            self.seen[eng][key] = v
            waits.append((key[0], obj, v))
        return waits

    def _record(self, tok, reads, writes):
        for b in reads:
            key = (tok[0], tok[1] if tok[0] == "c" else id(tok[1]))
            b.r[key] = tok
        for b in writes:
            b.w = tok
            b.r = {}

    def op(self, eng, fn, reads=(), writes=()):
        waits = self._deps(eng, reads, writes)
        idx = len(self.q[eng])
        self.q[eng].append({"fn": fn, "waits": waits, "awaited": False, "dma": None})
        tok = ("c", eng, idx)
        self._record(tok, reads, writes)
        return tok

    def dma(self, eng, out, in_, reads=(), writes=(), sem=None, **kw):
        reads, writes = list(reads), list(writes)
        sem = self._auto_dsem(reads, writes)
        waits = self._deps(eng, reads, writes)
        sem["n"] += 16
        tok = ("d", sem, sem["n"])
        self.q[eng].append({"fn": (lambda e, o=out, i=in_, k=kw: e.dma_start(out=o, in_=i, **k)),
                            "waits": waits, "awaited": False, "dma": sem})
        self._record(tok, reads, writes)
        return tok

    def barrier(self):
        for e in self.ENG:
            waits = []
            for e2 in ["pe", "act", "dve", "pool"]:
                n = len(self.q[e2])
                if e2 == e:
                    n -= 0
                idx = None
                for i in range(len(self.q[e2]) - 1, -1, -1):
                    if self.q[e2][i]["fn"] is not None and self.q[e2][i]["dma"] is None:
                        idx = i
                        break
                if idx is None:
                    continue
                key = ("c", e2)
                if self.seen[e].get(key, -1) >= idx:
                    continue
                self.seen[e][key] = idx
                waits.append(("c", e2, idx))
            for d in self.dsems:
                if d["n"] == 0:
                    continue
                key = ("d", id(d))
                if self.seen[e].get(key, -1) >= d["n"]:
                    continue
                self.seen[e][key] = d["n"]
                waits.append(("d", d, d["n"]))
            if waits:
                self.q[e].append({"fn": None, "waits": waits, "awaited": False, "dma": None})
        for d in self.bufsem.values():
            self.free_dsems.append(d)
        self.bufsem = {}
        self.bufsem_keep = []

    def emit(self):
        for e in self.ENG:
            for ent in self.q[e]:
                for w in ent["waits"]:
                    if w[0] == "c":
                        self.q[w[1]][w[2]]["awaited"] = True
        cnt = {}
        for e in ["pe", "act", "dve", "pool"]:
            c = 0
            arr = []
            for ent in self.q[e]:
                if ent["awaited"]:
                    c += 1
                arr.append(c)
            cnt[e] = arr
        psem = self.psem
        q = self.q

        def run(name, e):
            for ent in q[name]:
                for w in ent["waits"]:
                    if w[0] == "c":
                        e.wait_ge(psem[w[1]], cnt[w[1]][w[2]])
                    else:
                        e.wait_ge(w[1]["h"], w[2])
                if ent["fn"] is None:
                    continue
                inst = ent["fn"](e)
                if ent["dma"] is not None:
                    inst.then_inc(ent["dma"]["h"], 16)
                elif ent["awaited"]:
                    inst.then_inc(psem[name], 1)

        with self.nc.Block() as block:
            @block.sync
            def _(e):
                run("sp", e)

            @block.scalar
            def _(e):
                run("act", e)

            @block.vector
            def _(e):
                run("dve", e)

            @block.gpsimd
            def _(e):
                run("pool", e)

            @block.tensor
            def _(e):
                run("pe", e)


def _consts():
    ident = np.eye(128, dtype=np.float32)
    m = np.arange(128)
    partner = np.where((m % 32) < 16, m + 16, m - 16)
    perm = np.zeros((128, 128), np.float32)
    perm[partner, m] = 1.0
    sgn = np.where((m % 32) < 16, -1.0, 1.0).astype(np.float32)
    tok = np.arange(L)
    pos = np.where(((m % 64) < 32)[:, None], (tok // 64)[None, :], (tok % 64)[None, :]).astype(np.float32)
    fexp = ((m % 16) / 16.0).astype(np.float32)
    colc = np.zeros((128, 4), np.float32)
    colc[:, 0] = sgn
    colc[:, 1] = fexp
    colc[:, 2] = np.where(m < 64, 1.0, -1.0)
    sel = np.zeros((2, 128), np.float32)
    sel[0, :] = 1.0
    tauA = np.zeros((128, 32, 9), np.float32)
    tauA[:, 0:16, :] = np.arange(9)[None, None, :]
    tauA[:, 16:32, :] = (8 - np.arange(9))[None, None, :]
    tauB = np.zeros((128, 32, 8), np.float32)
    tauB[:, 0:16, :] = (7 - np.arange(8))[None, None, :]
    tauB[:, 16:32, :] = np.arange(8)[None, None, :]
    return {"c_ident": ident, "c_perm": perm, "c_pos": pos, "c_col": colc, "c_sel": sel, "c_tauA": tauA, "c_tauB": tauB}


class K:
    pass


def build(debug=None):
    nc = bass.Bass("TRN2", target_bir_lowering=False)
    st = ExitStack()
    P = Prog(nc, st)
    k = K()
    k.nc, k.P, k.st = nc, P, st
    k.debug = debug or {}

    def dram_in(name, shape, dt=F32):
        return nc.dram_tensor(name, list(shape), dt, kind="ExternalInput").ap()

    dbg_outs = []

    def dram_scr(name, shape, dt):
        kind = "Internal"
        if debug is not None and name in debug.get("_inject", ()):
            kind = "ExternalInput"
        elif debug is not None and name in debug:
            kind = "ExternalOutput"
            dbg_outs.append(name)
        return nc.dram_tensor(name, list(shape), dt, kind=kind).ap()

    I = {}
    I["x"] = dram_in("x", [L, D])
    I["ctx"] = dram_in("ctx", [LC, D])
    I["cc"] = dram_in("cc", [2, D])
    I["w_ada"] = dram_in("w_ada", [D, 3 * D])
    I["b_ada"] = dram_in("b_ada", [1, 3 * D])
    I["norm_g"] = dram_in("norm_g", [D])
    I["w_in"] = dram_in("w_in", [D, INW])
    I["lam"] = dram_in("lam", [4, 64])
    I["subln_g"] = dram_in("subln_g", [1, 128])
    I["ssm_lre"] = dram_in("ssm_lre", [2, 32, 64])
    I["ssm_lim"] = dram_in("ssm_lim", [2, 32, 64])
    I["ssm_ls"] = dram_in("ssm_ls", [2, 32])
    I["ssm_bre"] = dram_in("ssm_bre", [2, 32, 64, 16])
    I["ssm_bim"] = dram_in("ssm_bim", [2, 32, 64, 16])
    I["ssm_cre"] = dram_in("ssm_cre", [2, 32, 16, 64])
    I["ssm_cim"] = dram_in("ssm_cim", [2, 32, 16, 64])
    I["ssm_d"] = dram_in("ssm_d", [1, 512])
    I["w_glu"] = dram_in("w_glu", [512, 512])
    I["b_glu"] = dram_in("b_glu", [512])
    I["w_pa"] = dram_in("w_pa", [1024, D])
    I["w_ps"] = dram_in("w_ps", [512, D])
    I["w_out"] = dram_in("w_out", [D, D])
    I["final_g"] = dram_in("final_g", [1, D])
    for cn, arr in _consts().items():
        I[cn] = dram_in(cn, arr.shape)
    out = nc.dram_tensor("out", [L, D], F32, kind="ExternalOutput").ap()

    S = {}
    S["modrow"] = dram_scr("modrow", [2, 3 * D], F32)
    S["qT"] = dram_scr("qT", [HEADS, 128, L], BF16)
    S["kT"] = dram_scr("kT", [HEADS, 128, LT], BF16)
    S["v"] = dram_scr("v", [LT, 1024], BF16)
    S["sga"] = dram_scr("sga", [L, 1024], BF16)
    S["u"] = dram_scr("u", [LT, 512], F32)
    S["sgsT"] = dram_scr("sgsT", [512, L], BF16)
    S["sgmT"] = dram_scr("sgmT", [2 * D, L], BF16)
    S["abrT"] = dram_scr("abrT", [1024, L], BF16)
    S["sbrT"] = dram_scr("sbrT", [512, L], BF16)
    S["hT"] = dram_scr("hT_dbg", [128, NKC, LT], BF16) if (debug is not None and "hT_dbg" in debug) else None
    k.I, k.S, k.out = I, S, out
    k.dbg_sem = None

    def dbg(name, shape, dt, ap_fn, bufs):
        if debug is None or name not in debug:
            return
        if name not in S:
            S[name] = nc.dram_tensor(name, list(shape), dt, kind="ExternalOutput").ap()
            dbg_outs.append(name)
        if k.dbg_sem is None:
            k.dbg_sem = P.new_dsem("dbgsem")
        o, i = ap_fn(S[name])
        P.dma("sp", o, i, reads=bufs, sem=k.dbg_sem)
    k.dbg = dbg

    def sb(name, shape, dt, stack=st):
        return stack.enter_context(nc.sbuf_tensor(name, list(shape), dt))

    def ps(name, shape, dt, stack=st):
        return stack.enter_context(nc.psum_tensor(name, list(shape), dt))

    k.sb, k.ps = sb, ps
    ident = sb("ident", [128, 128], F32)
    colc = sb("colc", [128, 4], F32)
    b_ident, b_colc = Buf("ident"), Buf("colc")
    csem = P.new_dsem("csem")
    P.dma("sp", ident[:], I["c_ident"], writes=[b_ident], sem=csem)
    P.dma("sp", colc[:], I["c_col"], writes=[b_colc], sem=csem)
    k.ident, k.b_ident, k.colc, k.b_colc, k.csem = ident, b_ident, colc, b_colc, csem
    k.ssq = sb("ssq", [128, 40], F32)
    k.b_ssq = Buf("ssq")

    phase_adaln(k)
    P.barrier()
    if debug is None or debug.get("_upto", 99) >= 1:
        phase_norm_inproj(k, debug)
        P.barrier()
    if (debug is None or debug.get("_upto", 99) >= 2) and not (debug or {}).get("_skip_ssm"):
        phase_ssm(k)
        P.barrier()
    if debug is None or debug.get("_upto", 99) >= 3:
        phase_attn(k)
        P.barrier()
    if debug is None or debug.get("_upto", 99) >= 4:
        phase_merge(k)
        P.barrier()
    P.emit()
    st.close()
    return nc, dbg_outs


def range_sin(k, stack, out_ap, y_ap, shape, tag, rbufs, wbufs, eng="dve"):
    nc, P = k.nc, k.P
    ki = k.sb(tag + "_ki", shape, I32, stack)
    kf = k.sb(tag + "_kf", shape, F32, stack)
    g = k.sb(tag + "_g", shape, F32, stack)
    bki, bkf, bg = Buf(tag + "ki"), Buf(tag + "kf"), Buf(tag + "g")
    sl = tuple([slice(None)] * len(shape))
    P.op(eng, lambda e: e.tensor_copy(out=ki[sl], in_=y_ap), reads=rbufs, writes=[bki])
    P.op(eng, lambda e: e.tensor_copy(out=kf[sl], in_=ki[sl]), reads=[bki], writes=[bkf])
    P.op(eng, lambda e: e.tensor_tensor(out=kf[sl], in0=y_ap, in1=kf[sl], op=ALU.subtract), reads=rbufs + [bkf], writes=[bkf])
    P.op(eng, lambda e: e.tensor_single_scalar(out=g[sl], in_=kf[sl], scalar=0.5, op=ALU.is_gt), reads=[bkf], writes=[bg])
    P.op(eng, lambda e: e.tensor_tensor(out=kf[sl], in0=kf[sl], in1=g[sl], op=ALU.subtract), reads=[bkf, bg], writes=[bkf])
    P.op(eng, lambda e: e.tensor_single_scalar(out=g[sl], in_=kf[sl], scalar=-0.5, op=ALU.is_lt), reads=[bkf], writes=[bg])
    P.op(eng, lambda e: e.tensor_tensor(out=kf[sl], in0=kf[sl], in1=g[sl], op=ALU.add), reads=[bkf, bg], writes=[bkf])
    P.op("act", lambda e: e.activation(out=out_ap, in_=kf[sl], func=AF.Sin, scale=TWO_PI * (1.0 - 2e-7)), reads=[bkf], writes=wbufs)


def phase_adaln(k):
    nc, P, I, S = k.nc, k.P, k.I, k.S
    with ExitStack() as ls:
        sT = k.sb("ad_sT", [128, NKC, 2], F32, ls)
        b_sT = Buf("sT")
        sem_c = P.new_dsem("ad_c")
        for v in range(2):
            P.dma("sp", sT[:, :, v], I["cc"][v].rearrange("(kc p) -> p kc", p=128), writes=[b_sT], sem=sem_c,
                  allow_slow_non_contiguous=True)
        P.op("act", lambda e: e.activation(out=sT[:], in_=sT[:], func=AF.Silu), reads=[b_sT], writes=[b_sT])
        brow = k.sb("ad_brow", [2, 3 * D], F32, ls)
        b_brow = Buf("brow")
        for v in range(2):
            P.dma("sp", brow[v:v + 1, :], I["b_ada"], writes=[b_brow], sem=sem_c)
        modrow = k.sb("ad_modrow", [2, 3 * D], F32, ls)
        b_modrow = Buf("modrow")
        NS = 2
        wst = [k.sb(f"ad_w{i}", [128, NKC, 512], F32, ls) for i in range(NS)]
        b_w = [Buf(f"adw{i}") for i in range(NS)]
        wsem = [P.new_dsem(f"ad_ws{i}") for i in range(NS)]
        pst = [k.ps(f"ad_ps{i}", [128, 512], F32, ls) for i in range(2)]
        b_ps = [Buf(f"adps{i}") for i in range(2)]
        wv = I["w_ada"].rearrange("(kc p) c -> p kc c", p=128)
        xs1 = [k.sb(f"ad_x{i}", [128, D], F32, ls) for i in range(2)]
        b_xs1 = [Buf(f"adx{i}") for i in range(2)]
        junk1 = k.sb("ad_junk", [128, D], BF16, ls)
        b_junk1 = Buf("adjunk")
        tiles_done = 0

        def ss_tile(t):
            s1 = t % 2
            src = I["x"][t * 128:(t + 1) * 128, :] if t < 16 else I["ctx"][(t - 16) * 128:(t - 15) * 128, :]
            P.dma("pool", xs1[s1][:], src, writes=[b_xs1[s1]])
            P.op("act", lambda e, s1=s1, t=t: e.activation(out=junk1[:], in_=xs1[s1][:], func=AF.Square, accum_out=k.ssq[:, t:t + 1]),
                 reads=[b_xs1[s1]], writes=[b_junk1, k.b_ssq])

        for cb in range(12):
            for _ in range(2 if cb < 6 else 1):
                if tiles_done < 18:
                    ss_tile(tiles_done)
                    tiles_done += 1
            s = cb % NS
            P.dma("sp", wst[s][:, 0:8, :], wv[:, 0:8, cb * 512:(cb + 1) * 512], writes=[b_w[s]], sem=wsem[s])
            P.dma("act", wst[s][:, 8:16, :], wv[:, 8:16, cb * 512:(cb + 1) * 512], writes=[b_w[s]], sem=wsem[s])
            pt, bp = pst[cb % 2], b_ps[cb % 2]
            for kc in range(NKC):
                P.op("pe", lambda e, kc=kc, s=s, pt=pt: e.matmul(pt[0:2, :], lhsT=sT[:, kc, :], rhs=wst[s][:, kc, :],
                                                              start=(kc == 0), stop=(kc == NKC - 1)),
                     reads=[b_sT, b_w[s]], writes=[bp])
            P.op("dve", lambda e, cb=cb, pt=pt: e.tensor_tensor(out=modrow[:, cb * 512:(cb + 1) * 512], in0=pt[0:2, :],
                                                             in1=brow[:, cb * 512:(cb + 1) * 512], op=ALU.add),
                 reads=[bp, b_brow], writes=[b_modrow])
        b_mr = Buf("modrow_d")
        k.b_modrow_d = b_mr
        P.dma("sp", S["modrow"], modrow[:], reads=[b_modrow], writes=[b_mr], sem=sem_c)


def phase_norm_inproj(k, debug):
    nc, P, I, S = k.nc, k.P, k.I, k.S
    with ExitStack() as ls:
        hT = k.sb("hT", [128, NKC, LT], BF16, ls)
        b_hT = [Buf(f"hT{t}") for t in range(18)]
        Amod = k.sb("Amod", [128, NKC, 2], F32, ls)
        Smod = k.sb("Smod", [128, NKC, 2], F32, ls)
        gcol = k.sb("gcol", [128, NKC], F32, ls)
        b_A, b_S, b_g = Buf("Amod"), Buf("Smod"), Buf("gcol")
        msem = P.new_dsem("n_m")
        for v in range(2):
            P.dma("sp", Smod[:, :, v], S["modrow"][v, 0:D].rearrange("(kc p) -> p kc", p=128),
                  reads=[k.b_modrow_d], writes=[b_S], sem=msem, allow_slow_non_contiguous=True)
            P.dma("sp", Amod[:, :, v], S["modrow"][v, D:2 * D].rearrange("(kc p) -> p kc", p=128),
                  reads=[k.b_modrow_d], writes=[b_A], sem=msem, allow_slow_non_contiguous=True)
        P.dma("sp", gcol[:], I["norm_g"].rearrange("(kc p) -> p kc", p=128), writes=[b_g], sem=msem,
              allow_slow_non_contiguous=True)
        for v in range(2):
            P.op("dve", lambda e, v=v: e.scalar_tensor_tensor(out=Amod[:, :, v], in0=Amod[:, :, v], scalar=1.0, in1=gcol[:],
                                                             op0=ALU.add, op1=ALU.mult),
                 reads=[b_A, b_g], writes=[b_A])
        with ExitStack() as l1:
            NX = 2
            xt = [k.sb(f"n_x{i}", [128, D], F32, l1) for i in range(NX)]
            b_x = [Buf(f"nx{i}") for i in range(NX)]
            xsem = [P.new_dsem(f"n_xs{i}") for i in range(NX)]
            junk = k.sb("n_junk", [128, D], BF16, l1)
            b_junk = Buf("junk")
            stat = [k.sb(f"n_st{i}", [128, 4], F32, l1) for i in range(NX)]
            b_stat = [Buf(f"nst{i}") for i in range(NX)]
            pt = [k.ps(f"n_ps{i}", [128, 512], F32, l1) for i in range(4)]
            b_pt = [Buf(f"nps{i}") for i in range(4)]
            pi = 0
            P.op("dve", lambda e: e.tensor_scalar(out=k.ssq[:, 0:18], in0=k.ssq[:, 0:18], scalar1=1.0 / D, scalar2=EPS, op0=ALU.mult, op1=ALU.add),
                 reads=[k.b_ssq], writes=[k.b_ssq])
            P.op("act", lambda e: e.activation(out=k.ssq[:, 0:18], in_=k.ssq[:, 0:18], func=AF.Ln), reads=[k.b_ssq], writes=[k.b_ssq])
            P.op("act", lambda e: e.activation(out=k.ssq[:, 20:38], in_=k.ssq[:, 0:18], func=AF.Exp, scale=-0.5), reads=[k.b_ssq], writes=[k.b_ssq])
            for t in range(18):
                s = t % NX
                v = 0 if t < 16 else 1
                src = I["x"][t * 128:(t + 1) * 128, :] if t < 16 else I["ctx"][(t - 16) * 128:(t - 15) * 128, :]
                P.dma("sp", xt[s][:, 0:1024], src[:, 0:1024], writes=[b_x[s]], sem=xsem[s])
                P.dma("act", xt[s][:, 1024:2048], src[:, 1024:2048], writes=[b_x[s]], sem=xsem[s])
                P.op("dve", lambda e, s=s, t=t: e.tensor_scalar(out=xt[s][:], in0=xt[s][:], scalar1=k.ssq[:, 20 + t:21 + t], scalar2=None,
                                                              op0=ALU.mult),
                     reads=[b_x[s], k.b_ssq], writes=[b_x[s]])
                for g4 in range(4):
                    p_, bp = pt[pi % 4], b_pt[pi % 4]
                    pi += 1
                    for j in range(4):
                        kc = g4 * 4 + j
                        P.op("pe", lambda e, s=s, kc=kc, j=j, p_=p_: e.transpose(out=p_[:, j * 128:(j + 1) * 128],
                                                                             in_=xt[s][:, kc * 128:(kc + 1) * 128],
                                                                             identity=k.ident[:]),
                             reads=[b_x[s], k.b_ident], writes=[bp])
                    for j in range(4):
                        kc = g4 * 4 + j
                        eng = "dve" if (j % 2 == 0) else "act"
                        if eng == "dve":
                            P.op("dve", lambda e, kc=kc, j=j, p_=p_, t=t, v=v: e.tensor_scalar(
                                out=hT[:, kc, t * 128:(t + 1) * 128], in0=p_[:, j * 128:(j + 1) * 128],
                                scalar1=Amod[:, kc, v:v + 1], scalar2=Smod[:, kc, v:v + 1], op0=ALU.mult, op1=ALU.add),
                                reads=[bp, b_A, b_S], writes=[b_hT[t]])
                        else:
                            P.op("act", lambda e, kc=kc, j=j, p_=p_, t=t, v=v: e.activation(
                                out=hT[:, kc, t * 128:(t + 1) * 128], in_=p_[:, j * 128:(j + 1) * 128],
                                func=AF.Identity, scale=Amod[:, kc, v:v + 1], bias=Smod[:, kc, v:v + 1]),
                                reads=[bp, b_A, b_S], writes=[b_hT[t]])
        if S["hT"] is not None:
            dsem = P.new_dsem("dbg")
            P.dma("sp", S["hT"], hT[:], reads=b_hT, writes=[Buf("x")], sem=dsem)
        P.barrier()
        if debug is not None and debug.get("_upto", 99) < 1.5:
            return
        inproj(k, ls, hT, b_hT)


def inproj(k, ls, hT, b_hT):
    nc, P, I, S = k.nc, k.P, k.I, k.S
    cosT = k.sb("cosT", [128, L], F32, ls)
    sinS = k.sb("sinS", [128, L], F32, ls)
    perm = k.sb("perm", [128, 128], F32, ls)
    b_cos, b_sin, b_perm = Buf("cos"), Buf("sin"), Buf("perm")
    tsem = P.new_dsem("ip_t")
    P.dma("sp", perm[:], I["c_perm"], writes=[b_perm], sem=tsem)
    with ExitStack() as l0:
        pos = k.sb("pos", [128, L], F32, l0)
        yv = k.sb("yv", [128, L], F32, l0)
        inv = k.sb("inv", [128, 1], F32, l0)
        b_pos, b_y, b_inv = Buf("pos"), Buf("yv"), Buf("inv")
        P.dma("sp", pos[:], I["c_pos"], writes=[b_pos], sem=tsem)
        P.op("act", lambda e: e.activation(out=inv[:], in_=k.colc[:, 1:2], func=AF.Exp, scale=-math.log(10000.0)),
             reads=[k.b_colc], writes=[b_inv])
        P.op("dve", lambda e: e.tensor_scalar(out=yv[:], in0=pos[:], scalar1=inv[:, 0:1], scalar2=1.0 / TWO_PI,
                                              op0=ALU.mult, op1=ALU.mult), reads=[b_pos, b_inv], writes=[b_y])
        range_sin(k, l0, sinS[:], yv[:], [128, L], "rs1", [b_y], [b_sin])
        P.op("dve", lambda e: e.tensor_scalar(out=sinS[:], in0=sinS[:], scalar1=k.colc[:, 0:1], scalar2=None, op0=ALU.mult),
             reads=[b_sin, k.b_colc], writes=[b_sin])
        P.op("dve", lambda e: e.tensor_scalar(out=yv[:], in0=yv[:], scalar1=0.25, scalar2=None, op0=ALU.add),
             reads=[b_y], writes=[b_y])
        range_sin(k, l0, cosT[:], yv[:], [128, L], "rs2", [b_y], [b_cos])
        P.barrier()
    NW = 2
    wst = [k.sb(f"ip_wst{i}", [128, 8, 512], F32, ls) for i in range(NW)]
    b_wst = [Buf(f"wst{i}") for i in range(NW)]
    wsem = [P.new_dsem(f"ip_ws{i}") for i in range(NW)]
    wb = [k.sb(f"ip_wb{i}", [128, NKC, 512], BF16, ls) for i in range(2)]
    b_wb = [Buf(f"wb{i}") for i in range(2)]
    NOB = 4
    ob = [k.sb(f"ip_ob{i}", [128, 512], BF16, ls) for i in range(NOB)]
    b_ob = [Buf(f"ob{i}") for i in range(NOB)]
    osem = [P.new_dsem(f"ip_os{i}") for i in range(NOB)]
    NOF = 3
    of = [k.sb(f"ip_of{i}", [128, 512], F32, ls) for i in range(NOF)]
    b_of = [Buf(f"of{i}") for i in range(NOF)]
    fsem = [P.new_dsem(f"ip_fs{i}") for i in range(NOF)]
    t1 = [k.sb(f"ip_t1{i}", [128, 512], F32, ls) for i in range(2)]
    b_t1 = [Buf(f"t1{i}") for i in range(2)]
    t2 = [k.sb(f"ip_t2{i}", [128, 512], F32, ls) for i in range(2)]
    b_t2 = [Buf(f"t2{i}") for i in range(2)]
    pb = [k.ps(f"ip_ps{i}", [128, 512], F32, ls) for i in range(4)]
    b_pb = [Buf(f"ipps{i}") for i in range(4)]
    pr = [k.ps(f"ip_pr{i}", [128, 512], F32, ls) for i in range(2)]
    b_pr = [Buf(f"ippr{i}") for i in range(2)]
    wv = I["w_in"].rearrange("(kc p) c -> p kc c", p=128)
    cnt = {"pb": 0, "ob": 0, "of": 0, "r": 0, "ld": 0, "ev": 0}

    rope_pending = []

    def load_block(cb):
        s2 = cb % 2
        for half in range(2):
            sl = cnt["ld"] % NW
            cnt["ld"] += 1
            P.dma("sp", wst[sl][:], wv[:, half * 8:(half + 1) * 8, cb * 512:(cb + 1) * 512], writes=[b_wst[sl]], sem=wsem[sl])
            P.op("dve", lambda e, sl=sl, s2=s2, half=half: e.tensor_copy(out=wb[s2][:, half * 8:(half + 1) * 8, :], in_=wst[sl][:]),
                 reads=[b_wst[sl]], writes=[b_wb[s2]])

    def next_ob():
        i = cnt["ob"] % NOB
        cnt["ob"] += 1
        return i

    def evac_eng():
        cnt["ev"] += 1
        return "act" if cnt["ev"] % 2 else "dve"

    def tiles_of(tok0, n):
        return [b_hT[t] for t in range(tok0 // 128, (tok0 + n) // 128)]

    def fm_unit(cb, fc, tok0, n, kind, row0, dst):
        s2 = cb % 2
        pi = cnt["pb"] % 4
        cnt["pb"] += 1
        pt, bp = pb[pi], b_pb[pi]
        for kc in range(NKC):
            P.op("pe", lambda e, kc=kc: e.matmul(pt[:, 0:n], lhsT=wb[s2][:, kc, fc * 128:(fc + 1) * 128],
                                                 rhs=hT[:, kc, tok0:tok0 + n], start=(kc == 0), stop=(kc == NKC - 1)),
                 reads=[b_wb[s2]] + tiles_of(tok0, n), writes=[bp])
        while rope_pending:
            rope_pending.pop(0)()
        oi = next_ob()
        if kind == "rope":
            ri = cnt["r"] % 2
            cnt["r"] += 1
            fi = cnt["of"] % NOF
            cnt["of"] += 1
            P.op("act", lambda e: e.activation(out=of[fi][:, 0:n], in_=pt[:, 0:n], func=AF.Copy), reads=[bp], writes=[b_of[fi]])
            P.op("dve", lambda e: e.tensor_tensor(out=t1[ri][:, 0:n], in0=of[fi][:, 0:n], in1=cosT[:, tok0:tok0 + n], op=ALU.mult),
                 reads=[b_of[fi], b_cos], writes=[b_t1[ri]])

            def fin():
                P.op("pe", lambda e: e.matmul(pr[ri][:, 0:n], lhsT=perm[:], rhs=of[fi][:, 0:n], start=True, stop=True),
                     reads=[b_perm, b_of[fi]], writes=[b_pr[ri]])
                P.op("dve", lambda e: e.tensor_tensor(out=t2[ri][:, 0:n], in0=pr[ri][:, 0:n], in1=sinS[:, tok0:tok0 + n], op=ALU.mult),
                     reads=[b_pr[ri], b_sin], writes=[b_t2[ri]])
                P.op("pool", lambda e: e.tensor_tensor(out=ob[oi][:, 0:n], in0=t1[ri][:, 0:n], in1=t2[ri][:, 0:n], op=ALU.add),
                     reads=[b_t1[ri], b_t2[ri]], writes=[b_ob[oi]])
                P.dma("pool", dst, ob[oi][:, 0:n], reads=[b_ob[oi]], sem=osem[oi])
            rope_pending.append(fin)
            return
        elif kind == "copy":
            eg = evac_eng()
            if eg == "act":
                P.op("act", lambda e: e.activation(out=ob[oi][:, 0:n], in_=pt[:, 0:n], func=AF.Copy), reads=[bp], writes=[b_ob[oi]])
            else:
                P.op("dve", lambda e: e.tensor_copy(out=ob[oi][:, 0:n], in_=pt[:, 0:n]), reads=[bp], writes=[b_ob[oi]])
        else:
            fn = AF.Silu if kind == "silu" else AF.Sigmoid
            P.op("act", lambda e: e.activation(out=ob[oi][:, 0:n], in_=pt[:, 0:n], func=fn), reads=[bp], writes=[b_ob[oi]])
        P.dma("pool", dst, ob[oi][:, 0:n], reads=[b_ob[oi]], sem=osem[oi])

    def tm_unit(cb, t, kind, dst):
        s2 = cb % 2
        pi = cnt["pb"] % 4
        cnt["pb"] += 1
        pt, bp = pb[pi], b_pb[pi]
        for kc in range(NKC):
            P.op("pe", lambda e, kc=kc: e.matmul(pt[:], lhsT=hT[:, kc, t * 128:(t + 1) * 128], rhs=wb[s2][:, kc, :],
                                                 start=(kc == 0), stop=(kc == NKC - 1)),
                 reads=[b_wb[s2], b_hT[t]], writes=[bp])
        while rope_pending:
            rope_pending.pop(0)()
        if kind == "f32":
            fi = cnt["of"] % NOF
            cnt["of"] += 1
            P.op("dve", lambda e: e.tensor_copy(out=of[fi][:], in_=pt[:]), reads=[bp], writes=[b_of[fi]])
            P.dma("pool", dst, of[fi][:], reads=[b_of[fi]], sem=fsem[fi])
            return
        oi = next_ob()
        if kind == "copy":
            eg = evac_eng()
            if eg == "act":
                P.op("act", lambda e: e.activation(out=ob[oi][:], in_=pt[:], func=AF.Copy), reads=[bp], writes=[b_ob[oi]])
            else:
                P.op("dve", lambda e: e.tensor_copy(out=ob[oi][:], in_=pt[:]), reads=[bp], writes=[b_ob[oi]])
        else:
            P.op("act", lambda e: e.activation(out=ob[oi][:], in_=pt[:], func=AF.Silu), reads=[bp], writes=[b_ob[oi]])
        P.dma("pool", dst, ob[oi][:], reads=[b_ob[oi]], sem=osem[oi])

    NCB = INW // 512
    load_block(0)
    for cb in range(NCB):
        if cb + 1 < NCB:
            load_block(cb + 1)
        c0 = cb * 512
        if cb < 2:
            for fc in range(4):
                h = cb * 4 + fc
                for tb in range(4):
                    fm_unit(cb, fc, tb * 512, 512, "rope", 0, S["qT"][h, :, tb * 512:(tb + 1) * 512])
        elif cb < 4:
            for fc in range(4):
                h = (cb - 2) * 4 + fc
                for tb in range(4):
                    fm_unit(cb, fc, tb * 512, 512, "rope", 0, S["kT"][h, :, tb * 512:(tb + 1) * 512])
                fm_unit(cb, fc, L, LC, "copy", 0, S["kT"][h, :, L:LT])
        elif cb < 6:
            for t in range(18):
                tm_unit(cb, t, "copy", S["v"][t * 128:(t + 1) * 128, (cb - 4) * 512:(cb - 3) * 512])
        elif cb < 8:
            for t in range(16):
                tm_unit(cb, t, "silu", S["sga"][t * 128:(t + 1) * 128, (cb - 6) * 512:(cb - 5) * 512])
        elif cb == 8:
            for t in range(18):
                tm_unit(cb, t, "f32", S["u"][t * 128:(t + 1) * 128, :])
        elif cb == 9:
            for fc in range(4):
                for tb in range(4):
                    fm_unit(cb, fc, tb * 512, 512, "silu", 0, S["sgsT"][fc * 128:(fc + 1) * 128, tb * 512:(tb + 1) * 512])
        else:
            for fc in range(4):
                r0 = (cb - 10) * 512 + fc * 128
                for tb in range(4):
                    fm_unit(cb, fc, tb * 512, 512, "sigm", 0, S["sgmT"][r0:r0 + 128, tb * 512:(tb + 1) * 512])


def phase_ssm(k):
    nc, P, I, S = k.nc, k.P, k.I, k.S
    MUL, ADD, SUB = ALU.mult, ALU.add, ALU.subtract
    with ExitStack() as ls:
        ToepT = k.sb("ss_toep", [128, 32, 128], BF16, ls)
        RCp = k.sb("ss_rcp", [128, 2, 2, 16, 256], BF16, ls)
        WT = k.sb("ss_wt", [128, 2, 16, 2, 128], BF16, ls)
        A8c = k.sb("ss_a8c", [128, 2, 16, 2], F32, ls)
        A8s = k.sb("ss_a8s", [128, 2, 16, 2], F32, ls)
        b_toep = [Buf(f"toep{g}") for g in range(32)]
        b_rcp, b_wt = Buf("rcp"), Buf("wt")
        b_U = [Buf(f"U{g}") for g in range(32)]
        b_zbf = [Buf("zbf0"), Buf("zbf1")]
        b_ygT = Buf("ygT")
        b_a8 = Buf("a8")
        pbk = [k.ps(f"ss_ps{i}", [128, 512], F32, ls) for i in range(8)]
        b_pbk = [Buf(f"ssps{i}") for i in range(8)]
        pc = {"i": 0}

        def nb():
            i = pc["i"] % 8
            pc["i"] += 1
            return pbk[i], b_pbk[i]

        csem = P.new_dsem("ss_c")
        with ExitStack() as l0:
            lre = k.sb("ss_lre", [128, 32], F32, l0)
            lim = k.sb("ss_lim", [128, 32], F32, l0)
            dtt = k.sb("ss_dt", [128, 32], F32, l0)
            alog = k.sb("ss_alog", [128, 32], F32, l0)
            th = k.sb("ss_th", [128, 32], F32, l0)
            b_l, b_dt, b_al = Buf("lrelim"), Buf("dtt"), Buf("alogth")
            for gp in range(2):
                for d in range(2):
                    P.dma("sp", lre[gp * 64:(gp + 1) * 64, d * 16:(d + 1) * 16], I["ssm_lre"][d, gp * 16:(gp + 1) * 16, :].rearrange("g p -> p g"),
                          writes=[b_l], sem=csem, allow_slow_non_contiguous=True)
                    P.dma("sp", lim[gp * 64:(gp + 1) * 64, d * 16:(d + 1) * 16], I["ssm_lim"][d, gp * 16:(gp + 1) * 16, :].rearrange("g p -> p g"),
                          writes=[b_l], sem=csem, allow_slow_non_contiguous=True)
                    P.dma("sp", dtt[gp * 64:(gp + 1) * 64, d * 16:(d + 1) * 16], I["ssm_ls"][d:d + 1, gp * 16:(gp + 1) * 16].broadcast_to([64, 16]),
                          writes=[b_dt], sem=csem)
            P.op("act", lambda e: e.activation(out=dtt[:], in_=dtt[:], func=AF.Exp), reads=[b_dt], writes=[b_dt])
            P.op("dve", lambda e: e.tensor_tensor(out=alog[:], in0=lre[:], in1=dtt[:], op=MUL), reads=[b_l, b_dt], writes=[b_al])
            P.op("dve", lambda e: e.scalar_tensor_tensor(out=th[:], in0=lim[:], scalar=1.0 / TWO_PI, in1=dtt[:], op0=MUL, op1=MUL),
                 reads=[b_l, b_dt], writes=[b_al])
            tabs = {}
            for nm, n in (("A", 9), ("B", 8)):
                tau = k.sb(f"ss_tau{nm}", [128, 32, n], F32, l0)
                ex = k.sb(f"ss_ex{nm}", [128, 32, n], F32, l0)
                yv = k.sb(f"ss_yv{nm}", [128, 32, n], F32, l0)
                sn = k.sb(f"ss_sn{nm}", [128, 32, n], F32, l0)
                cs = k.sb(f"ss_cs{nm}", [128, 32, n], F32, l0)
                b_tau, b_ex, b_yv, b_sn, b_cs = Buf("tau" + nm), Buf("ex" + nm), Buf("yv" + nm), Buf("sn" + nm), Buf("cs" + nm)
                P.dma("sp", tau[:], I["c_tau" + nm], writes=[b_tau], sem=csem)
                P.op("dve", lambda e, ex=ex, tau=tau, n=n: e.tensor_tensor(out=ex[:], in0=tau[:], in1=alog[:, :, None].broadcast_to([128, 32, n]), op=MUL),
                     reads=[b_tau, b_al], writes=[b_ex])
                P.op("act", lambda e, ex=ex: e.activation(out=ex[:], in_=ex[:], func=AF.Exp), reads=[b_ex], writes=[b_ex])
                P.op("dve", lambda e, yv=yv, tau=tau, n=n: e.tensor_tensor(out=yv[:], in0=tau[:], in1=th[:, :, None].broadcast_to([128, 32, n]), op=MUL),
                     reads=[b_tau, b_al], writes=[b_yv])
                fl = lambda t: t[:].rearrange("p a b -> p (a b)")
                range_sin(k, l0, fl(sn), fl(yv), [128, 32 * n], "ssr1" + nm, [b_yv], [b_sn])
                P.op("dve", lambda e, yv=yv: e.tensor_scalar(out=yv[:], in0=yv[:], scalar1=0.25, scalar2=None, op0=ADD), reads=[b_yv], writes=[b_yv])
                range_sin(k, l0, fl(cs), fl(yv), [128, 32 * n], "ssr2" + nm, [b_yv], [b_cs])
                P.op("dve", lambda e, cs=cs, ex=ex: e.tensor_tensor(out=cs[:], in0=cs[:], in1=ex[:], op=MUL), reads=[b_cs, b_ex], writes=[b_cs])
                P.op("dve", lambda e, sn=sn, ex=ex: e.tensor_tensor(out=sn[:], in0=sn[:], in1=ex[:], op=MUL), reads=[b_sn, b_ex], writes=[b_sn])
                tabs[nm] = (cs, sn, b_cs, b_sn)
            ARA, AIA, b_ARA, b_AIA = tabs["A"]
            ARB, AIB, b_ARB, b_AIB = tabs["B"]
            a1 = k.sb("ss_a1", [128, 2, 32], F32, l0)
            b_a1 = Buf("a1")
            for d in range(2):
                i8 = 8 if d == 0 else 0
                i1 = 1 if d == 0 else 7
                dsl = slice(d * 16, (d + 1) * 16)
                for ri in range(2):
                    P.op("dve", lambda e, d=d, ri=ri, i8=i8, dsl=dsl: e.tensor_copy(out=A8c[:, d, :, ri], in_=ARA[:, dsl, i8]), reads=[b_ARA], writes=[b_a8])
                P.op("dve", lambda e, d=d, i8=i8, dsl=dsl: e.tensor_scalar(out=A8s[:, d, :, 0], in0=AIA[:, dsl, i8], scalar1=-1.0, scalar2=None, op0=MUL),
                     reads=[b_AIA], writes=[b_a8])
                P.op("dve", lambda e, d=d, i8=i8, dsl=dsl: e.tensor_copy(out=A8s[:, d, :, 1], in_=AIA[:, dsl, i8]), reads=[b_AIA], writes=[b_a8])
                P.op("dve", lambda e, d=d, i1=i1, dsl=dsl: e.tensor_copy(out=a1[:, 0, dsl], in_=ARA[:, dsl, i1]), reads=[b_ARA], writes=[b_a1])
                P.op("dve", lambda e, d=d, i1=i1, dsl=dsl: e.tensor_copy(out=a1[:, 1, dsl], in_=AIA[:, dsl, i1]), reads=[b_AIA], writes=[b_a1])
            fz = k.sb("ss_fz", [128, 6, 32], F32, l0)
            b_fz = Buf("fz")
            P.op("dve", lambda e: e.tensor_tensor(out=fz[:, 0, :], in0=lre[:], in1=lre[:], op=MUL), reads=[b_l], writes=[b_fz])
            P.op("dve", lambda e: e.tensor_tensor(out=fz[:, 1, :], in0=lim[:], in1=lim[:], op=MUL), reads=[b_l], writes=[b_fz])
            P.op("dve", lambda e: e.tensor_tensor(out=fz[:, 0, :], in0=fz[:, 0, :], in1=fz[:, 1, :], op=ADD), reads=[b_fz], writes=[b_fz])
            P.op("dve", lambda e: e.reciprocal(out=fz[:, 1, :], in_=fz[:, 0, :]), reads=[b_fz], writes=[b_fz])
            P.op("dve", lambda e: e.tensor_scalar(out=fz[:, 0, :], in0=a1[:, 0, :], scalar1=-1.0, scalar2=None, op0=ADD), reads=[b_a1], writes=[b_fz])
            P.op("dve", lambda e: e.tensor_tensor(out=fz[:, 2, :], in0=fz[:, 0, :], in1=lre[:], op=MUL), reads=[b_fz, b_l], writes=[b_fz])
            P.op("dve", lambda e: e.tensor_tensor(out=fz[:, 3, :], in0=a1[:, 1, :], in1=lim[:], op=MUL), reads=[b_a1, b_l], writes=[b_fz])
            P.op("dve", lambda e: e.tensor_tensor(out=fz[:, 2, :], in0=fz[:, 2, :], in1=fz[:, 3, :], op=ADD), reads=[b_fz], writes=[b_fz])
            P.op("dve", lambda e: e.tensor_tensor(out=fz[:, 2, :], in0=fz[:, 2, :], in1=fz[:, 1, :], op=MUL), reads=[b_fz], writes=[b_fz])
            P.op("dve", lambda e: e.tensor_tensor(out=fz[:, 4, :], in0=a1[:, 1, :], in1=lre[:], op=MUL), reads=[b_a1, b_l], writes=[b_fz])
            P.op("dve", lambda e: e.tensor_tensor(out=fz[:, 5, :], in0=fz[:, 0, :], in1=lim[:], op=MUL), reads=[b_fz, b_l], writes=[b_fz])
            P.op("dve", lambda e: e.tensor_tensor(out=fz[:, 4, :], in0=fz[:, 4, :], in1=fz[:, 5, :], op=SUB), reads=[b_fz], writes=[b_fz])
            P.op("dve", lambda e: e.tensor_tensor(out=fz[:, 4, :], in0=fz[:, 4, :], in1=fz[:, 1, :], op=MUL), reads=[b_fz], writes=[b_fz])
            BT = k.sb("ss_BT", [128, 2, 2, 16, 16], F32, l0)
            BB = k.sb("ss_BB", [128, 2, 2, 16, 16], F32, l0)
            CN = k.sb("ss_CN", [128, 2, 2, 2, 128], F32, l0)
            CT = k.sb("ss_CT", [128, 2, 2, 16, 16], F32, l0)
            tA = k.sb("ss_tA", [128, 16, 9, 16], F32, l0)
            tB = k.sb("ss_tB", [128, 16, 9, 16], F32, l0)
            b_BT, b_BB, b_CN, b_CT, b_tA, b_tB = Buf("BT"), Buf("BB"), Buf("CN"), Buf("CT"), Buf("tA"), Buf("tB")
            for d in range(2):
                for ri in range(2):
                    bsrc = I["ssm_bre"] if ri == 0 else I["ssm_bim"]
                    csrc = I["ssm_cre"] if ri == 0 else I["ssm_cim"]
                    for gp in range(2):
                        P.dma("sp", BT[gp * 64:(gp + 1) * 64, d, ri, :, :], bsrc[d, gp * 16:(gp + 1) * 16].rearrange("g p c -> p g c"),
                              writes=[b_BT], sem=csem)
                        for blk in range(2):
                            g0 = gp * 16 + blk * 8
                            P.dma("sp", CN[:, d, ri, blk, gp * 64:(gp + 1) * 64], csrc[d, g0:g0 + 8].rearrange("g c p -> (g c) p"),
                                  writes=[b_CN], sem=csem)
            for d in range(2):
                for ri in range(2):
                    for blk in range(2):
                        pt, bp = nb()
                        P.op("pe", lambda e, d=d, ri=ri, blk=blk, pt=pt: e.transpose(out=pt[:, 0:128], in_=CN[:, d, ri, blk, :], identity=k.ident[:]),
                             reads=[b_CN, k.b_ident], writes=[bp])
                        P.op("dve", lambda e, d=d, ri=ri, blk=blk, pt=pt: e.tensor_copy(
                            out=CT[:, d, ri, blk * 8:(blk + 1) * 8, :].rearrange("p a b -> p (a b)"), in_=pt[:, 0:128]), reads=[bp], writes=[b_CT])
            for d in range(2):
                dsl = slice(d * 16, (d + 1) * 16)
                frb = lambda d=d, dsl=dsl: fz[:, 2, dsl][:, :, None].broadcast_to([128, 16, 16])
                fib = lambda d=d, dsl=dsl: fz[:, 4, dsl][:, :, None].broadcast_to([128, 16, 16])
                t16a = tA[:, :, 0, :]
                t16b = tB[:, :, 0, :]
                P.op("dve", lambda e, d=d, frb=frb: e.tensor_tensor(out=t16a, in0=BT[:, d, 0], in1=frb(), op=MUL), reads=[b_BT, b_fz], writes=[b_tA])
                P.op("dve", lambda e, d=d, fib=fib: e.tensor_tensor(out=t16b, in0=BT[:, d, 1], in1=fib(), op=MUL), reads=[b_BT, b_fz], writes=[b_tB])
                P.op("dve", lambda e, d=d: e.tensor_tensor(out=BB[:, d, 0], in0=t16a, in1=t16b, op=SUB), reads=[b_tA, b_tB], writes=[b_BB])
                P.op("dve", lambda e, d=d, frb=frb: e.tensor_tensor(out=t16a, in0=BT[:, d, 1], in1=frb(), op=MUL), reads=[b_BT, b_fz], writes=[b_tA])
                P.op("dve", lambda e, d=d, fib=fib: e.tensor_tensor(out=t16b, in0=BT[:, d, 0], in1=fib(), op=MUL), reads=[b_BT, b_fz], writes=[b_tB])
                P.op("dve", lambda e, d=d: e.tensor_tensor(out=BB[:, d, 1], in0=t16a, in1=t16b, op=ADD), reads=[b_tA, b_tB], writes=[b_BB])
            P.op("pool", lambda e: e.memset(RCp[:].rearrange("p a b c d -> p (a b c d)"), 0.0), writes=[b_rcp])
            for d in range(2):
                dsl = slice(d * 16, (d + 1) * 16)
                off = 112 if d == 0 else 0
                bc_c = lambda ri, d=d: CT[:, d, ri][:, :, None, :].broadcast_to([128, 16, 9, 16])
                bc_ar = lambda dsl=dsl: ARA[:, dsl, :][:, :, :, None].broadcast_to([128, 16, 9, 16])
                bc_ai = lambda dsl=dsl: AIA[:, dsl, :][:, :, :, None].broadcast_to([128, 16, 9, 16])
                dst = lambda ri, d=d, off=off: RCp[:, d, ri, :, off:off + 144].rearrange("p g (t c) -> p g t c", c=16)
                P.op("dve", lambda e, bc_c=bc_c, bc_ar=bc_ar: e.tensor_tensor(out=tA[:], in0=bc_c(0), in1=bc_ar(), op=MUL), reads=[b_CT, b_ARA], writes=[b_tA])
                P.op("dve", lambda e, bc_c=bc_c, bc_ai=bc_ai: e.tensor_tensor(out=tB[:], in0=bc_c(1), in1=bc_ai(), op=MUL), reads=[b_CT, b_AIA], writes=[b_tB])
                P.op("dve", lambda e, dst=dst: e.tensor_tensor(out=dst(0), in0=tA[:], in1=tB[:], op=SUB), reads=[b_tA, b_tB], writes=[b_rcp])
                P.op("dve", lambda e, bc_c=bc_c, bc_ai=bc_ai: e.tensor_tensor(out=tA[:], in0=bc_c(0), in1=bc_ai(), op=MUL), reads=[b_CT, b_AIA], writes=[b_tA])
                P.op("dve", lambda e, bc_c=bc_c, bc_ar=bc_ar: e.tensor_tensor(out=tB[:], in0=bc_c(1), in1=bc_ar(), op=MUL), reads=[b_CT, b_ARA], writes=[b_tB])
                P.op("dve", lambda e: e.tensor_tensor(out=tA[:], in0=tA[:], in1=tB[:], op=ADD), reads=[b_tA, b_tB], writes=[b_tA])
                P.op("dve", lambda e, dst=dst: e.tensor_scalar(out=dst(1), in0=tA[:], scalar1=-1.0, scalar2=None, op0=MUL), reads=[b_tA], writes=[b_rcp])
            Lp = k.sb("ss_Lp", [128, 64, 240], BF16, l0)
            b_Lp = Buf("Lp")
            P.op("pool", lambda e: e.memset(Lp[:].rearrange("p a b -> p (a b)"), 0.0), writes=[b_Lp])
            P.op("pool", lambda e: e.tensor_copy(out=Lp[:, :, 112:128], in_=BB[:].rearrange("p a b c d -> p (a b c) d")), reads=[b_BB], writes=[b_Lp])
            BW = k.sb("ss_BW", [128, 2, 2, 16, 128], F32, l0)
            b_BW = Buf("BW")
            for d in range(2):
                dsl = slice(d * 16, (d + 1) * 16)
                bc_b = lambda ri, d=d: BB[:, d, ri][:, :, None, :].broadcast_to([128, 16, 8, 16])
                bc_ar = lambda dsl=dsl: ARB[:, dsl, :][:, :, :, None].broadcast_to([128, 16, 8, 16])
                bc_ai = lambda dsl=dsl: AIB[:, dsl, :][:, :, :, None].broadcast_to([128, 16, 8, 16])
                dst = lambda ri, d=d: BW[:, d, ri].rearrange("p g (t c) -> p g t c", c=16)
                ta8 = tA[:, :, 0:8, :]
                tb8 = tB[:, :, 0:8, :]
                P.op("dve", lambda e, bc_b=bc_b, bc_ar=bc_ar: e.tensor_tensor(out=ta8, in0=bc_b(0), in1=bc_ar(), op=MUL), reads=[b_BB, b_ARB], writes=[b_tA])
                P.op("dve", lambda e, bc_b=bc_b, bc_ai=bc_ai: e.tensor_tensor(out=tb8, in0=bc_b(1), in1=bc_ai(), op=MUL), reads=[b_BB, b_AIB], writes=[b_tB])
                P.op("dve", lambda e, dst=dst: e.tensor_tensor(out=dst(0), in0=ta8, in1=tb8, op=SUB), reads=[b_tA, b_tB], writes=[b_BW])
                P.op("dve", lambda e, bc_b=bc_b, bc_ai=bc_ai: e.tensor_tensor(out=ta8, in0=bc_b(0), in1=bc_ai(), op=MUL), reads=[b_BB, b_AIB], writes=[b_tA])
                P.op("dve", lambda e, bc_b=bc_b, bc_ar=bc_ar: e.tensor_tensor(out=tb8, in0=bc_b(1), in1=bc_ar(), op=MUL), reads=[b_BB, b_ARB], writes=[b_tB])
                P.op("dve", lambda e, dst=dst: e.tensor_tensor(out=dst(1), in0=ta8, in1=tb8, op=ADD), reads=[b_tA, b_tB], writes=[b_BW])
            for d in range(2):
                for g2 in range(16):
                    for ri in range(2):
                        pt, bp = nb()
                        P.op("pe", lambda e, d=d, g2=g2, ri=ri, pt=pt: e.transpose(out=pt[:, 0:128], in_=BW[:, d, ri, g2, :], identity=k.ident[:]),
                             reads=[b_BW, k.b_ident], writes=[bp])
                        eng = "act" if (g2 + ri) % 2 else "dve"
                        if eng == "act":
                            P.op("act", lambda e, d=d, g2=g2, ri=ri, pt=pt: e.activation(out=WT[:, d, g2, ri, :], in_=pt[:, 0:128], func=AF.Copy), reads=[bp], writes=[b_wt])
                        else:
                            P.op("dve", lambda e, d=d, g2=g2, ri=ri, pt=pt: e.tensor_copy(out=WT[:, d, g2, ri, :], in_=pt[:, 0:128]), reads=[bp], writes=[b_wt])
            for g2 in range(16):
                for gp in range(2):
                    g = gp * 16 + g2
                    pt, bp = nb()
                    psl = slice(gp * 64, (gp + 1) * 64)
                    n = 0
                    for d in range(2):
                        for ri in range(2):
                            for s_ in range(8):
                                w0 = (7 - s_) * 16 if d == 0 else (8 - s_) * 16
                                l0_ = (7 - s_) * 16
                                P.op("pe", lambda e, d=d, ri=ri, g2=g2, w0=w0, l0_=l0_, psl=psl, pt=pt, n=n: e.matmul(
                                    pt[:, 0:128], lhsT=Lp[psl, (d * 2 + ri) * 16 + g2, l0_:l0_ + 128], rhs=RCp[psl, d, ri, g2, w0:w0 + 128],
                                    start=(n == 0), stop=(n == 31)), reads=[b_Lp, b_rcp], writes=[bp])
                                n += 1
                    P.op("dve" if g % 2 else "act",
                         (lambda e, g=g, pt=pt: e.tensor_copy(out=ToepT[:, g, :], in_=pt[:, 0:128])) if g % 2 else
                         (lambda e, g=g, pt=pt: e.activation(out=ToepT[:, g, :], in_=pt[:, 0:128], func=AF.Copy)),
                         reads=[bp], writes=[b_toep[g]])
            P.barrier()
        Ubuf = k.sb("ss_ubuf", [128, 32, 320], BF16, ls)
        Zbf = k.sb("ss_zbf", [128, 2, 16, 2, 288], BF16, ls)
        ygT = k.sb("ss_ygT", [128, 4, L], BF16, ls)
        if k.debug.get("_ssm_upto", 99) < 1:
            return
        with ExitStack() as l1:
            ucm = [k.sb(f"ss_ucm{i}", [128, 8, 512], F32, l1) for i in range(2)]
            b_ucm = [Buf(f"ucm{i}") for i in range(2)]
            usem = [P.new_dsem(f"ss_us{i}") for i in range(2)]
            ucg = k.sb("ss_ucg", [128, 32, 128], F32, l1)
            b_ucg = Buf("ucg")
            for jt in range(3):
                si = jt % 2
                nj = 128 if jt < 2 else 32
                r0 = jt * 1024
                P.dma("sp", ucm[si][0:nj], S["u"][r0:r0 + nj * 8, :].rearrange("(j s) c -> j s c", s=8), writes=[b_ucm[si]], sem=usem[si])
                P.op("dve", lambda e, si=si, nj=nj: e.tensor_copy(out=ucg[0:nj].rearrange("p g (s c) -> p g s c", c=16),
                                                                 in_=ucm[si][0:nj].rearrange("p s (g c) -> p g s c", c=16)),
                     reads=[b_ucm[si]], writes=[b_ucg])
                for g0 in range(0, 32, 4):
                    pt, bp = nb()
                    for gg in range(4):
                        g = g0 + gg
                        P.op("pe", lambda e, si=si, nj=nj, g=g, gg=gg, pt=pt: e.transpose(
                            out=pt[:, gg * 128:gg * 128 + nj], in_=ucg[0:nj, g, :], identity=k.ident[0:nj, 0:nj]),
                            reads=[b_ucg, k.b_ident], writes=[bp])
                    src = lambda pt=pt, nj=nj: pt[:].rearrange("p (a b) -> p a b", b=128)[:, :, 0:nj]
                    cols = [32 + jt * 128] if jt < 2 else [0, 288]
                    for ci, c0 in enumerate(cols):
                        eng = "act" if (g0 // 4 + ci) % 2 else "dve"
                        if eng == "act":
                            P.op("act", lambda e, g0=g0, c0=c0, nj=nj, src=src: e.activation(out=Ubuf[:, g0:g0 + 4, c0:c0 + nj], in_=src(), func=AF.Copy),
                                 reads=[bp], writes=[b_U[g0 + i] for i in range(4)])
                        else:
                            P.op("dve", lambda e, g0=g0, c0=c0, nj=nj, src=src: e.tensor_copy(out=Ubuf[:, g0:g0 + 4, c0:c0 + nj], in_=src()),
                                 reads=[bp], writes=[b_U[g0 + i] for i in range(4)])
            P.barrier()
        if k.debug.get("_ssm_upto", 99) < 2:
            return
        with ExitStack() as l2:
            Z = [k.sb(f"ss_Z{d}", [128, 16, 2, 288], F32, l2) for d in range(2)]
            b_Z = [Buf("Z0"), Buf("Z1")]
            for d in range(2):
                j0 = 0 if d == 0 else 32
                for g2 in range(16):
                    for ri in range(2):
                        pt, bp = nb()
                        for gp in range(2):
                            P.op("pe", lambda e, d=d, g2=g2, ri=ri, gp=gp, pt=pt, j0=j0: e.matmul(
                                pt[gp * 64:(gp + 1) * 64, 0:288], lhsT=WT[:, d, g2, ri, gp * 64:(gp + 1) * 64], rhs=Ubuf[:, gp * 16 + g2, j0:j0 + 288],
                                start=True, stop=True), reads=[b_wt, b_U[gp * 16 + g2]], writes=[bp])
                        if (g2 + ri) % 2:
                            P.op("act", lambda e, d=d, g2=g2, ri=ri, pt=pt: e.activation(out=Z[d][:, g2, ri, :], in_=pt[:, 0:288], func=AF.Copy), reads=[bp], writes=[b_Z[d]])
                        else:
                            P.op("dve", lambda e, d=d, g2=g2, ri=ri, pt=pt: e.tensor_copy(out=Z[d][:, g2, ri, :], in_=pt[:, 0:288]), reads=[bp], writes=[b_Z[d]])
            k.dbg("V_dbg", [2, 128, 16 * 2 * 288], F32, lambda dd: (dd[0], Z[0][:].rearrange("p a b c -> p (a b c)")), [b_Z[0]])
            k.dbg("V_dbg", [2, 128, 16 * 2 * 288], F32, lambda dd: (dd[1], Z[1][:].rearrange("p a b c -> p (a b c)")), [b_Z[1]])
            m1 = [k.sb(f"ss_m1{d}", [128, 16, 2], F32, l2) for d in range(2)]
            m2 = [k.sb(f"ss_m2{d}", [128, 16, 2], F32, l2) for d in range(2)]
            b_m1 = [Buf("m10"), Buf("m11")]
            b_m2 = [Buf("m20"), Buf("m21")]

            def scan_step(d, J, Jp):
                eng = "dve" if d == 0 else "pool"
                P.op(eng, lambda e: e.tensor_tensor(out=m1[d][:], in0=Z[d][:, :, :, Jp], in1=A8c[:, d], op=MUL), reads=[b_Z[d], b_a8], writes=[b_m1[d]])
                P.op(eng, lambda e: e.tensor_tensor(out=m2[d][:], in0=Z[d][:, :, ::-1, Jp], in1=A8s[:, d], op=MUL), reads=[b_Z[d], b_a8], writes=[b_m2[d]])
                P.op(eng, lambda e: e.tensor_tensor(out=m1[d][:], in0=m1[d][:], in1=m2[d][:], op=ADD), reads=[b_m1[d], b_m2[d]], writes=[b_m1[d]])
                P.op(eng, lambda e: e.tensor_tensor(out=Z[d][:, :, :, J], in0=Z[d][:, :, :, J], in1=m1[d][:], op=ADD), reads=[b_Z[d], b_m1[d]], writes=[b_Z[d]])

            for st_ in range(1, 288):
                scan_step(0, st_, st_ - 1)
                scan_step(1, 287 - st_, 288 - st_)
            for d in range(2):
                eng = "dve" if d == 0 else "pool"
                P.op(eng, lambda e, d=d: e.tensor_copy(out=Zbf[:, d].rearrange("p a b c -> p (a b c)"), in_=Z[d][:].rearrange("p a b c -> p (a b c)")),
                     reads=[b_Z[d]], writes=[b_zbf[d]])
            k.dbg("Z_dbg", [2, 128, 16 * 2 * 288], F32, lambda dd: (dd[0], Z[0][:].rearrange("p a b c -> p (a b c)")), [b_Z[0]])
            k.dbg("Z_dbg", [2, 128, 16 * 2 * 288], F32, lambda dd: (dd[1], Z[1][:].rearrange("p a b c -> p (a b c)")), [b_Z[1]])
            P.barrier()
        if k.debug.get("_ssm_upto", 99) < 3:
            return
        with ExitStack() as l3:
            ycm = k.sb("ss_ycm", [128, 8, 512], F32, l3)
            b_ycm = [Buf(f"ycm{g}") for g in range(32)]
            ut = k.sb("ss_ut", [128, 8, 512], F32, l3)
            b_ut = Buf("ut")
            utsem = P.new_dsem("ss_uts")
            Dfull = k.sb("ss_D", [128, 512], F32, l3)
            b_D = Buf("Dfull")
            P.dma("sp", Dfull[:], I["ssm_d"][0:1, :].broadcast_to([128, 512]), writes=[b_D], sem=csem)
            sq = [k.sb(f"ss_sq{i}", [128, 512], F32, l3) for i in range(2)]
            b_sq = [Buf("sq0"), Buf("sq1")]
            GC = math.sqrt(2.0 / math.pi)
            for jt in range(2):
                P.dma("sp", ut[:], S["u"][jt * 1024:(jt + 1) * 1024, :].rearrange("(j s) c -> j s c", s=8), writes=[b_ut], sem=utsem)
                for g in range(32):
                    gp, g2 = g // 16, g % 16
                    psl = slice(gp * 64, (gp + 1) * 64)
                    pt, bp = nb()
                    c0 = 32 + jt * 128
                    P.op("pe", lambda e, g=g, c0=c0, pt=pt: e.matmul(pt[:, 0:128], lhsT=Ubuf[:, g, c0:c0 + 128], rhs=ToepT[:, g, :], start=True, stop=False),
                         reads=[b_U[g], b_toep[g]], writes=[bp])
                    for d in range(2):
                        jz = (31 + jt * 128) if d == 0 else (1 + jt * 128)
                        w0 = 128 if d == 0 else 0
                        for ri in range(2):
                            last = (d == 1 and ri == 1)
                            P.op("pe", lambda e, d=d, ri=ri, g2=g2, psl=psl, jz=jz, w0=w0, pt=pt, last=last: e.matmul(
                                pt[:, 0:128], lhsT=Zbf[psl, d, g2, ri, jz:jz + 128], rhs=RCp[psl, d, ri, g2, w0:w0 + 128], start=False, stop=last),
                                reads=[b_zbf[d], b_rcp], writes=[bp])
                    src = lambda pt=pt: pt[:, 0:128].rearrange("p (t c) -> p t c", c=16)
                    P.op("dve", lambda e, g=g, src=src: e.tensor_tensor(out=ycm[:, :, g * 16:(g + 1) * 16], in0=ut[:, :, g * 16:(g + 1) * 16],
                                                                       in1=Dfull[:, g * 16:(g + 1) * 16][:, None, :].broadcast_to([128, 8, 16]), op=MUL),
                         reads=[b_ut, b_D], writes=[b_ycm[g]])
                    P.op("dve", lambda e, g=g, src=src: e.tensor_tensor(out=ycm[:, :, g * 16:(g + 1) * 16], in0=ycm[:, :, g * 16:(g + 1) * 16], in1=src(), op=ADD),
                         reads=[bp, b_ycm[g]], writes=[b_ycm[g]])
                k.dbg("y_dbg", [L, 512], F32, lambda dd, jt=jt: (dd[jt * 1024:(jt + 1) * 1024, :].rearrange("(j s) c -> j s c", s=8), ycm[:]), b_ycm)
                for t in range(8):
                    i = t % 2
                    P.op("dve", lambda e, t=t, i=i: e.tensor_tensor(out=sq[i][:], in0=ycm[:, t, :], in1=ycm[:, t, :], op=MUL), reads=b_ycm, writes=[b_sq[i]])
                    P.op("dve", lambda e, t=t, i=i: e.tensor_scalar(out=sq[i][:], in0=sq[i][:], scalar1=0.044715, scalar2=1.0, op0=MUL, op1=ADD), reads=[b_sq[i]], writes=[b_sq[i]])
                    P.op("dve", lambda e, t=t, i=i: e.tensor_tensor(out=sq[i][:], in0=sq[i][:], in1=ycm[:, t, :], op=MUL), reads=[b_sq[i]] + b_ycm, writes=[b_sq[i]])
                    P.op("act", lambda e, t=t, i=i: e.activation(out=sq[i][:], in_=sq[i][:], func=AF.Sigmoid, scale=2.0 * GC), reads=[b_sq[i]], writes=[b_sq[i]])
                    P.op("dve", lambda e, t=t, i=i: e.tensor_tensor(out=sq[i][:], in0=sq[i][:], in1=ycm[:, t, :], op=MUL), reads=[b_sq[i]] + b_ycm, writes=[b_sq[i]])
                    pt, bp = nb()
                    for chb in range(4):
                        P.op("pe", lambda e, i=i, chb=chb, pt=pt: e.transpose(out=pt[:, chb * 128:(chb + 1) * 128], in_=sq[i][:, chb * 128:(chb + 1) * 128], identity=k.ident[:]),
                             reads=[b_sq[i], k.b_ident], writes=[bp])
                    tsl = slice(jt * 1024 + t, (jt + 1) * 1024, 8)
                    P.op("act", lambda e, pt=pt, tsl=tsl: e.activation(out=ygT[:, :, tsl], in_=pt[:].rearrange("p (a b) -> p a b", b=128), func=AF.Copy),
                         reads=[bp], writes=[b_ygT])
            P.barrier()
        if k.debug.get("_ssm_upto", 99) < 4:
            return
        with ExitStack() as l4:
            wg32 = k.sb("ss_wg32", [128, 4, 512], F32, l4)
            wg = k.sb("ss_wg", [128, 4, 512], BF16, l4)
            bg = k.sb("ss_bg", [128, 4], F32, l4)
            b_wg32, b_wg, b_bg = Buf("wg32"), Buf("wg"), Buf("bg")
            P.dma("sp", wg32[:], I["w_glu"].rearrange("(fc p) c -> p fc c", p=128), writes=[b_wg32], sem=csem)
            P.dma("sp", bg[:], I["b_glu"].rearrange("(fc p) -> p fc", p=128), writes=[b_bg], sem=csem, allow_slow_non_contiguous=True)
            P.op("dve", lambda e: e.tensor_copy(out=wg[:], in_=wg32[:]), reads=[b_wg32], writes=[b_wg])
            gst = [k.sb(f"ss_gst{i}", [128, 512], BF16, l4) for i in range(2)]
            b_gst = [Buf("gst0"), Buf("gst1")]
            gsem = [P.new_dsem(f"ss_gs{i}") for i in range(2)]
            sg = [k.sb(f"ss_sg{i}", [128, 512], F32, l4) for i in range(2)]
            b_sg = [Buf("sg0"), Buf("sg1")]
            so = [k.sb(f"ss_so{i}", [128, 512], BF16, l4) for i in range(2)]
            b_so = [Buf("so0"), Buf("so1")]
            sosem = [P.new_dsem(f"ss_sos{i}") for i in range(2)]
            ui = 0
            for fo in range(4):
                for tb in range(4):
                    i = ui % 2
                    ui += 1
                    tsl = slice(tb * 512, (tb + 1) * 512)
                    P.dma("sp", gst[i][:], S["sgsT"][fo * 128:(fo + 1) * 128, tsl], writes=[b_gst[i]], sem=gsem[i])
                    pt, bp = nb()
                    for fc in range(4):
                        P.op("pe", lambda e, fc=fc, fo=fo, tsl=tsl, pt=pt: e.matmul(pt[:], lhsT=wg[:, fc, fo * 128:(fo + 1) * 128], rhs=ygT[:, fc, tsl],
                                                                               start=(fc == 0), stop=(fc == 3)), reads=[b_wg, b_ygT], writes=[bp])
                    P.op("act", lambda e, i=i, fo=fo, pt=pt: e.activation(out=sg[i][:], in_=pt[:], func=AF.Sigmoid, bias=bg[:, fo:fo + 1]),
                         reads=[bp, b_bg], writes=[b_sg[i]])
                    P.op("dve", lambda e, i=i, fo=fo, tsl=tsl: e.tensor_tensor(out=sg[i][:], in0=sg[i][:], in1=ygT[:, fo, tsl], op=MUL),
                         reads=[b_sg[i], b_ygT], writes=[b_sg[i]])
                    P.op("dve", lambda e, i=i: e.tensor_tensor(out=so[i][:], in0=sg[i][:], in1=gst[i][:], op=MUL),
                         reads=[b_sg[i], b_gst[i]], writes=[b_so[i]])
                    P.dma("sp", S["sbrT"][fo * 128:(fo + 1) * 128, tsl], so[i][:], reads=[b_so[i]], sem=sosem[i])


def phase_attn(k):
    nc, P, I, S = k.nc, k.P, k.I, k.S
    with ExitStack() as ls:
        lamv = k.sb("at_lamv", [128, 4, 64], F32, ls)
        lw = k.sb("at_lw", [128, 8], F32, ls)
        G = k.sb("at_G", [128, 128], F32, ls)
        b_lamv, b_lw, b_G = Buf("lamv"), Buf("lw"), Buf("G")
        csem = P.new_dsem("at_c")
        P.dma("sp", lamv[:].rearrange("p a b -> p (a b)"), I["lam"].rearrange("a b -> (a b)").partition_broadcast(128),
              writes=[b_lamv], sem=csem)
        P.dma("sp", G[:], I["subln_g"][0:1, :].broadcast_to([128, 128]), writes=[b_G], sem=csem)
        P.op("dve", lambda e: e.tensor_scalar(out=G[:], in0=G[:], scalar1=(1.0 - LAM_INIT), scalar2=None, op0=ALU.mult),
             reads=[b_G], writes=[b_G])
        for i in range(2):
            P.op("dve", lambda e, i=i: e.tensor_tensor(out=lamv[:, 2 * i, :], in0=lamv[:, 2 * i, :], in1=lamv[:, 2 * i + 1, :], op=ALU.mult),
                 reads=[b_lamv], writes=[b_lamv])
            P.op("dve", lambda e, i=i: e.tensor_reduce(out=lw[:, i:i + 1], in_=lamv[:, 2 * i, :], axis=mybir.AxisListType.X, op=ALU.add),
                 reads=[b_lamv], writes=[b_lw])
        P.op("act", lambda e: e.activation(out=lw[:, 2:4], in_=lw[:, 0:2], func=AF.Exp), reads=[b_lw], writes=[b_lw])
        P.op("dve", lambda e: e.tensor_tensor(out=lw[:, 4:5], in0=lw[:, 3:4], in1=lw[:, 2:3], op=ALU.subtract), reads=[b_lw], writes=[b_lw])
        P.op("dve", lambda e: e.tensor_scalar(out=lw[:, 5:6], in0=lw[:, 4:5], scalar1=-LAM_INIT, scalar2=None, op0=ALU.add),
             reads=[b_lw], writes=[b_lw])
        neglam = lw[:, 5:6]
        qTs = [k.sb(f"at_q{i}", [128, L], BF16, ls) for i in range(2)]
        kTs = [k.sb(f"at_k{i}", [128, LT], BF16, ls) for i in range(2)]
        Vs = [k.sb(f"at_v{i}", [128, 18, 130], BF16, ls) for i in range(2)]
        gas = [k.sb(f"at_ga{i}", [128, 16, 128], BF16, ls) for i in range(2)]
        aTs = [k.sb(f"at_aT{i}", [128, L], BF16, ls) for i in range(2)]
        b_q = [Buf(f"atq{i}") for i in range(2)]
        b_k = [Buf(f"atk{i}") for i in range(2)]
        b_v = [Buf(f"atv{i}") for i in range(2)]
        b_ga = [Buf(f"atga{i}") for i in range(2)]
        b_aT = [Buf(f"ataT{i}") for i in range(2)]
        hsem = [P.new_dsem(f"at_h{i}") for i in range(2)]
        asem = [P.new_dsem(f"at_a{i}") for i in range(2)]
        for i in range(2):
            P.op("pool", lambda e, i=i: e.memset(Vs[i][:, :, 128:130], 1.0), writes=[b_v[i]])
        PT = [k.sb(f"at_pt{i}", [128, 2, 18, 256], BF16, ls) for i in range(2)]
        b_PT = [[[Buf(f"pt{i}_{c}_{kp}") for kp in range(9)] for c in range(2)] for i in range(2)]
        sbk = [k.ps(f"at_s{i}", [128, 512], F32, ls) for i in range(3)]
        b_sbk = [Buf(f"ats{i}") for i in range(3)]
        obk = [k.ps(f"at_o{i}", [128, 512], F32, ls) for i in range(4)]
        b_obk = [Buf(f"ato{i}") for i in range(4)]
        tbk = k.ps("at_t", [128, 512], F32, ls)
        b_tbk = Buf("att")
        sm = [k.sb(f"at_sm{i}", [128, 8], F32, ls) for i in range(2)]
        b_sm = [Buf(f"atsm{i}") for i in range(2)]
        tmp = [k.sb(f"at_tmp{i}", [128, 128], F32, ls) for i in range(2)]
        b_tmp = [Buf(f"attmp{i}") for i in range(2)]
        ov = [k.sb(f"at_ov{i}", [128, 128], F32, ls) for i in range(2)]
        b_ov = [Buf(f"atov{i}") for i in range(2)]
        junk = k.sb("at_junk", [128, 128], F32, ls)
        b_junk = Buf("atjunk")
        cnt = {"s": 0, "u": 0}

        def load_head(h):
            s = h % 2
            P.dma("sp", qTs[s][:], S["qT"][h], writes=[b_q[s]], sem=hsem[s])
            P.dma("sp", kTs[s][:], S["kT"][h], writes=[b_k[s]], sem=hsem[s])
            P.dma("sp", Vs[s][:, :, 0:128], S["v"][:, h * 128:(h + 1) * 128].rearrange("(t p) e -> p t e", p=128),
                  writes=[b_v[s]], sem=hsem[s])
            P.dma("sp", gas[s][:], S["sga"][:, h * 128:(h + 1) * 128].rearrange("(t p) e -> p t e", p=128),
                  writes=[b_ga[s]], sem=hsem[s])

        def A_steps(h, qb):
            s = h % 2
            ps_ = qb % 2
            steps = []
            for kp in range(9):
                def step(kp=kp):
                    for c in range(2):
                        si = cnt["s"] % 3
                        cnt["s"] += 1
                        for j in range(2):
                            kt = 2 * kp + j
                            P.op("pe", lambda e, kt=kt, j=j, c=c, si=si: e.matmul(
                                sbk[si][:, j * 256:(j + 1) * 256], lhsT=kTs[s][c * 64:(c + 1) * 64, kt * 128:(kt + 1) * 128],
                                rhs=qTs[s][c * 64:(c + 1) * 64, qb * 256:(qb + 1) * 256], start=True, stop=True),
                                reads=[b_k[s], b_q[s]], writes=[b_sbk[si]])
                        P.op("act", lambda e, c=c, kp=kp, si=si: e.activation(
                            out=PT[ps_][:, c, 2 * kp:2 * kp + 2, :].rearrange("p a b -> p (a b)"), in_=sbk[si][:], func=AF.Exp, scale=0.125),
                            reads=[b_sbk[si]], writes=[b_PT[ps_][c][kp]])
                steps.append(step)
            return steps

        def B_gen(h, qb):
            s = h % 2
            ps_ = qb % 2
            for qi_ in range(2):
                yield from unitB(h, qb, qi_, s, ps_)

        def unitB(h, qb, qi, s, ps_):
            if True:
                qt = qb * 2 + qi
                u = cnt["u"] % 2
                cnt["u"] += 1
                banks = [obk[u * 2], obk[u * 2 + 1]]
                bb = [b_obk[u * 2], b_obk[u * 2 + 1]]
                for c in range(2):
                    for kt in range(18):
                        P.op("pe", lambda e, c=c, kt=kt: e.matmul(
                            banks[c][:, 0:129], lhsT=PT[ps_][:, c, kt, qi * 128:(qi + 1) * 128], rhs=Vs[s][:, kt, 0:129],
                            start=(kt == 0), stop=(kt == 17)),
                            reads=[b_PT[ps_][c][kt // 2], b_v[s]], writes=[bb[c]])
                        yield
                flush_pending()
                smt, bsm = sm[u], b_sm[u]
                for c in range(2):
                    P.op("dve", lambda e, c=c: e.reciprocal(out=smt[:, c:c + 1], in_=banks[c][:, 128:129]), reads=[bb[c]], writes=[bsm])
                P.op("dve", lambda e: e.tensor_tensor(out=smt[:, 2:3], in0=smt[:, 1:2], in1=neglam, op=ALU.mult), reads=[bsm, b_lw], writes=[bsm])
                P.op("dve", lambda e: e.tensor_scalar(out=tmp[u][:], in0=banks[1][:, 0:128], scalar1=smt[:, 2:3], scalar2=None, op0=ALU.mult),
                     reads=[bb[1], bsm], writes=[b_tmp[u]])
                P.op("dve", lambda e: e.scalar_tensor_tensor(out=ov[u][:], in0=banks[0][:, 0:128], scalar=smt[:, 0:1], in1=tmp[u][:],
                                                            op0=ALU.mult, op1=ALU.add),
                     reads=[bb[0], bsm, b_tmp[u]], writes=[b_ov[u]])
                P.op("dve", lambda e: e.tensor_tensor(out=tmp[u][:], in0=ov[u][:], in1=ov[u][:], op=ALU.mult),
                     reads=[b_ov[u]], writes=[b_tmp[u]])
                P.op("dve", lambda e: e.tensor_reduce(out=smt[:, 3:4], in_=tmp[u][:], axis=mybir.AxisListType.X, op=ALU.add),
                     reads=[b_tmp[u]], writes=[bsm])
                P.op("dve", lambda e: e.tensor_scalar(out=smt[:, 4:5], in0=smt[:, 3:4], scalar1=1.0 / 128, scalar2=EPS, op0=ALU.mult, op1=ALU.add),
                     reads=[bsm], writes=[bsm])
                def fin():
                    P.op("act", lambda e: e.activation(out=smt[:, 5:6], in_=smt[:, 4:5], func=AF.Ln), reads=[bsm], writes=[bsm])
                    P.op("act", lambda e: e.activation(out=smt[:, 6:7], in_=smt[:, 5:6], func=AF.Exp, scale=-0.5), reads=[bsm], writes=[bsm])
                    P.op("dve", lambda e: e.scalar_tensor_tensor(out=ov[u][:], in0=ov[u][:], scalar=smt[:, 6:7], in1=G[:], op0=ALU.mult, op1=ALU.mult),
                         reads=[b_ov[u], bsm, b_G], writes=[b_ov[u]])
                    P.op("pool", lambda e: e.tensor_tensor(out=ov[u][:], in0=ov[u][:], in1=gas[s][:, qt, :], op=ALU.mult),
                         reads=[b_ov[u], b_ga[s]], writes=[b_ov[u]])

                    def fin2():
                        P.op("pe", lambda e: e.transpose(out=tbk[:, 0:128], in_=ov[u][:], identity=k.ident[:]), reads=[b_ov[u], k.b_ident], writes=[b_tbk])
                        P.op("dve", lambda e: e.tensor_copy(out=aTs[s][:, qt * 128:(qt + 1) * 128], in_=tbk[:, 0:128]),
                             reads=[b_tbk], writes=[b_aT[s]])
                    pending2.append(fin2)
                pending.append(fin)

        pending = []
        pending2 = []

        def flush_pending():
            while pending2:
                pending2.pop(0)()
            while pending:
                pending.pop(0)()

        def interleave(a_steps, bgen, per=8):
            for st_ in a_steps:
                st_()
                if bgen is not None:
                    for _ in range(per):
                        try:
                            next(bgen)
                        except StopIteration:
                            bgen = None
                            break
            if bgen is not None:
                for _ in bgen:
                    pass

        load_head(0)
        load_head(1)
        interleave(A_steps(0, 0), None)
        for h in range(HEADS):
            for qb in range(8):
                if qb + 1 < 8:
                    nxt = A_steps(h, qb + 1)
                elif h + 1 < HEADS:
                    nxt = A_steps(h + 1, 0)
                else:
                    nxt = []
                interleave(nxt, B_gen(h, qb))
            flush_pending()
            flush_pending()
            P.dma("pool", S["abrT"][h * 128:(h + 1) * 128, :], aTs[h % 2][:], reads=[b_aT[h % 2]], sem=asem[h % 2])
            if h + 2 < HEADS:
                load_head(h + 2)


def phase_merge(k):
    nc, P, I, S = k.nc, k.P, k.I, k.S
    with ExitStack() as ls:
        mT = k.sb("mg_mT", [128, NKC, L], BF16, ls)
        b_mT = [Buf(f"mT{tb}") for tb in range(4)]
        wout = k.sb("mg_wout", [128, NKC, D], BF16, ls)
        b_wout = Buf("wout")
        NXB = 2
        wov = I["w_out"].rearrange("(kc p) c -> p kc c", p=128)
        wo_state = {"kc": 0}
        b_woutc = [Buf(f"woutc{i}") for i in range(NKC)]

        def load_wout_chunk():
            kc = wo_state["kc"]
            if kc >= NKC:
                return
            wo_state["kc"] += 1
            P.dma("pool", wout[:, kc, :], wov[:, kc, :], writes=[b_woutc[kc]])
        with ExitStack() as l1:
            abrT = k.sb("mg_abrT", [128, 8, L], BF16, l1)
            sbrT = k.sb("mg_sbrT", [128, 4, L], BF16, l1)
            b_abrT, b_sbrT = Buf("abrT"), Buf("sbrT")
            lsem = P.new_dsem("mg_l")
            P.dma("sp", abrT[:], S["abrT"].rearrange("(fc p) t -> p fc t", p=128), writes=[b_abrT], sem=lsem)
            P.dma("sp", sbrT[:], S["sbrT"].rearrange("(fc p) t -> p fc t", p=128), writes=[b_sbrT], sem=lsem)
            NWS = 2
            wbf = [k.sb(f"mg_wbf{i}", [128, 12, 128], BF16, l1) for i in range(NWS)]
            b_wbf = [Buf(f"mgwbf{i}") for i in range(NWS)]
            NG = 2
            gt = [k.sb(f"mg_gt{i}", [128, 2, L], BF16, l1) for i in range(NG)]
            b_gt = [Buf(f"mggt{i}") for i in range(NG)]
            t1 = [k.sb(f"mg_t1{i}", [128, 512], F32, l1) for i in range(2)]
            t2 = [k.sb(f"mg_t2{i}", [128, 512], F32, l1) for i in range(2)]
            b_t1 = [Buf(f"mgt1{i}") for i in range(2)]
            b_t2 = [Buf(f"mgt2{i}") for i in range(2)]
            pa = [k.ps(f"mg_pa{i}", [128, 512], F32, l1) for i in range(2)]
            pp = [k.ps(f"mg_pp{i}", [128, 512], F32, l1) for i in range(2)]
            b_pa = [Buf(f"mgpa{i}") for i in range(2)]
            b_pp = [Buf(f"mgpp{i}") for i in range(2)]
            wpa_v = I["w_pa"].rearrange("(fc p) c -> p fc c", p=128)
            wps_v = I["w_ps"].rearrange("(fc p) c -> p fc c", p=128)
            ui = 0

            def load_w(fo):
                s = fo % NWS
                P.dma("pool", wbf[s][:, 0:8, :], wpa_v[:, :, fo * 128:(fo + 1) * 128], writes=[b_wbf[s]])
                P.dma("pool", wbf[s][:, 8:12, :], wps_v[:, :, fo * 128:(fo + 1) * 128], writes=[b_wbf[s]])
                gi = fo % NG
                P.dma("sp", gt[gi][:, 0, :], S["sgmT"][fo * 128:(fo + 1) * 128, :], writes=[b_gt[gi]])
                P.dma("sp", gt[gi][:, 1, :], S["sgmT"][D + fo * 128:D + (fo + 1) * 128, :], writes=[b_gt[gi]])

            load_w(0)
            for fo in range(NKC):
                if fo + 1 < NKC:
                    load_w(fo + 1)
                load_wout_chunk()
                s = fo % NWS
                gi = fo % NG
                for tb in range(4):
                    u2 = ui % 2
                    ui += 1
                    tsl = slice(tb * 512, (tb + 1) * 512)
                    for fc in range(8):
                        P.op("pe", lambda e, fc=fc, s=s, tsl=tsl, u2=u2: e.matmul(pa[u2][:], lhsT=wbf[s][:, fc, :], rhs=abrT[:, fc, tsl],
                                                                        start=(fc == 0), stop=(fc == 7)),
                             reads=[b_wbf[s], b_abrT], writes=[b_pa[u2]])
                    for fc in range(4):
                        P.op("pe", lambda e, fc=fc, s=s, tsl=tsl, u2=u2: e.matmul(pp[u2][:], lhsT=wbf[s][:, 8 + fc, :], rhs=sbrT[:, fc, tsl],
                                                                        start=(fc == 0), stop=(fc == 3)),
                             reads=[b_wbf[s], b_sbrT], writes=[b_pp[u2]])
                    P.op("dve", lambda e, gi=gi, u2=u2, tsl=tsl: e.tensor_tensor(out=t1[u2][:], in0=pa[u2][:], in1=gt[gi][:, 0, tsl], op=ALU.mult),
                         reads=[b_pa[u2], b_gt[gi]], writes=[b_t1[u2]])
                    P.op("dve", lambda e, gi=gi, u2=u2, tsl=tsl: e.tensor_tensor(out=t2[u2][:], in0=pp[u2][:], in1=gt[gi][:, 1, tsl], op=ALU.mult),
                         reads=[b_pp[u2], b_gt[gi]], writes=[b_t2[u2]])
                    P.op("pool", lambda e, fo=fo, tsl=tsl, u2=u2: e.tensor_tensor(out=mT[:, fo, tsl], in0=t1[u2][:], in1=t2[u2][:], op=ALU.add),
                         reads=[b_t1[u2], b_t2[u2]], writes=[b_mT[tb]])
            P.barrier()
        gateB = k.sb("mg_gateB", [128, D], F32, ls)
        fgB = k.sb("mg_fgB", [128, D], F32, ls)
        b_gateB, b_fgB = Buf("gateB"), Buf("fgB")
        c2 = P.new_dsem("mg_c2")
        P.dma("sp", gateB[:], S["modrow"][0:1, 2 * D:3 * D].broadcast_to([128, D]), writes=[b_gateB], sem=c2)
        P.dma("sp", fgB[:], I["final_g"][0:1, :].broadcast_to([128, D]), writes=[b_fgB], sem=c2)
        xb = [k.sb(f"mg_x{i}", [128, D], F32, ls) for i in range(NXB)]
        b_xb = [Buf(f"mgx{i}") for i in range(NXB)]
        xn = [k.sb(f"mg_xn{i}", [128, D], F32, ls) for i in range(NXB)]
        b_xn = [Buf(f"mgxn{i}") for i in range(NXB)]
        xsem = [P.new_dsem(f"mg_xs{i}") for i in range(NXB)]
        osem = [P.new_dsem(f"mg_os{i}") for i in range(NXB)]
        st2 = [k.sb(f"mg_st{i}", [128, 4], F32, ls) for i in range(NXB)]
        b_st2 = [Buf(f"mgst{i}") for i in range(NXB)]
        while wo_state["kc"] < NKC:
            load_wout_chunk()
        po = [k.ps(f"mg_po{i}", [128, 512], F32, ls) for i in range(3)]
        b_po = [Buf(f"mgpo{i}") for i in range(3)]
        pi = 0
        for t in range(16):
            s = t % NXB
            tb = t // 4
            P.dma("sp", xb[s][:], I["x"][t * 128:(t + 1) * 128, :], writes=[b_xb[s]], sem=xsem[s])
            for cbk in range(4):
                p_ = pi % 3
                pi += 1
                for kc in range(NKC):
                    P.op("pe", lambda e, kc=kc, cbk=cbk, p_=p_, t=t: e.matmul(po[p_][:], lhsT=mT[:, kc, t * 128:(t + 1) * 128],
                                                                          rhs=wout[:, kc, cbk * 512:(cbk + 1) * 512],
                                                                          start=(kc == 0), stop=(kc == NKC - 1)),
                         reads=[b_mT[tb], b_woutc[kc]], writes=[b_po[p_]])
                P.op("dve", lambda e, cbk=cbk, p_=p_, s=s: e.tensor_tensor(out=xn[s][:, cbk * 512:(cbk + 1) * 512], in0=po[p_][:],
                                                                       in1=gateB[:, cbk * 512:(cbk + 1) * 512], op=ALU.mult),
                     reads=[b_po[p_], b_gateB], writes=[b_xn[s]])
            P.op("pool", lambda e, s=s: e.tensor_tensor(out=xn[s][:], in0=xn[s][:], in1=xb[s][:], op=ALU.add),
                 reads=[b_xn[s], b_xb[s]], writes=[b_xn[s]])
            P.op("pool", lambda e, s=s: e.tensor_tensor(out=xb[s][:], in0=xn[s][:], in1=xn[s][:], op=ALU.mult),
                 reads=[b_xn[s]], writes=[b_xb[s]])
            P.op("dve", lambda e, s=s: e.tensor_reduce(out=st2[s][:, 0:1], in_=xb[s][:], axis=mybir.AxisListType.X, op=ALU.add),
                 reads=[b_xb[s]], writes=[b_st2[s]])
            P.op("dve", lambda e, s=s: e.tensor_scalar(out=st2[s][:, 1:2], in0=st2[s][:, 0:1], scalar1=1.0 / D, scalar2=EPS, op0=ALU.mult, op1=ALU.add),
                 reads=[b_st2[s]], writes=[b_st2[s]])
            P.op("act", lambda e, s=s: e.activation(out=st2[s][:, 2:3], in_=st2[s][:, 1:2], func=AF.Ln), reads=[b_st2[s]], writes=[b_st2[s]])
            P.op("act", lambda e, s=s: e.activation(out=st2[s][:, 3:4], in_=st2[s][:, 2:3], func=AF.Exp, scale=-0.5), reads=[b_st2[s]], writes=[b_st2[s]])
            P.op("dve", lambda e, s=s: e.scalar_tensor_tensor(out=xn[s][:], in0=xn[s][:], scalar=st2[s][:, 3:4], in1=fgB[:], op0=ALU.mult, op1=ALU.mult),
                 reads=[b_xn[s], b_st2[s], b_fgB], writes=[b_xn[s]])
            P.dma("sp", k.out[t * 128:(t + 1) * 128, :], xn[s][:], reads=[b_xn[s]], sem=osem[s])


_CACHE = {}


def _prep_inputs(inputs, b):
    f = lambda a: np.ascontiguousarray(np.asarray(a, dtype=np.float32))
    m = {}
    m["x"] = f(inputs["x"][b])
    m["ctx"] = f(inputs["ctx"][b])
    m["cc"] = f(np.stack([np.asarray(inputs["c"])[b], np.asarray(inputs["c_ctx"])], axis=0))
    m["w_ada"] = f(inputs["w_ada"][0])
    m["b_ada"] = f(inputs["b_ada"][0]).reshape(1, -1)
    m["norm_g"] = f(inputs["norm_g"][0])
    m["w_in"] = f(inputs["w_in"][0])
    m["lam"] = f(np.stack([np.asarray(inputs["lambda_q1"])[0], np.asarray(inputs["lambda_k1"])[0],
                           np.asarray(inputs["lambda_q2"])[0], np.asarray(inputs["lambda_k2"])[0]], axis=0))
    m["subln_g"] = f(inputs["subln_g"][0]).reshape(1, 128)
    m["ssm_lre"] = f(inputs["ssm_lambda_re"][0])
    m["ssm_lim"] = f(inputs["ssm_lambda_im"][0])
    m["ssm_ls"] = f(inputs["ssm_log_step"][0])
    m["ssm_bre"] = f(inputs["ssm_b_re"][0])
    m["ssm_bim"] = f(inputs["ssm_b_im"][0])
    m["ssm_cre"] = f(inputs["ssm_c_re"][0])
    m["ssm_cim"] = f(inputs["ssm_c_im"][0])
    m["ssm_d"] = f(inputs["ssm_d"][0]).reshape(1, 512)
    m["w_glu"] = f(inputs["w_glu"][0])
    m["b_glu"] = f(inputs["b_glu"][0])
    m["w_pa"] = f(inputs["w_pa"][0])
    m["w_ps"] = f(inputs["w_ps"][0])
    m["w_out"] = f(inputs["w_out"][0])
    m["final_g"] = f(inputs["final_g"]).reshape(1, D)
    m.update(_consts())
    return m


def kernel(**inputs):
    if "nc" not in _CACHE:
        _CACHE["nc"] = build()[0]
    nc = _CACHE["nc"]
    shared = None
    in_maps = []
    for b in range(8):
        m = _prep_inputs(inputs, b)
        if shared is None:
            shared = m
        else:
            for key in m:
                if key not in ("x", "ctx", "cc"):
                    m[key] = shared[key]
        in_maps.append(m)
    res = run_bass_kernel_spmd(nc, in_maps, core_ids=list(range(8)))
    return np.stack([np.asarray(r["out"], dtype=np.float32) for r in res.results], axis=0)
```

```python
import math
import numpy as np
import ml_dtypes
from contextlib import ExitStack
import concourse.bass as bass
import concourse.mybir as mybir
from concourse.bass_utils import run_bass_kernel_spmd

F32 = mybir.dt.float32
BF16 = mybir.dt.bfloat16
I32 = mybir.dt.int32
AF = mybir.ActivationFunctionType
ALU = mybir.AluOpType

D = 2048
L = 2048
LC = 256
LT = L + LC
NKC = D // 128
INW = 9216
HEADS = 8
EPS = 1e-6
LAM_INIT = 0.8 - 0.6 * math.exp(-0.3 * 0)
TWO_PI = 2.0 * math.pi


class Buf:
    __slots__ = ("name", "w", "r")

    def __init__(self, name):
        self.name = name
        self.w = None
        self.r = {}


class Prog:
    ENG = ["pe", "act", "dve", "pool", "sp"]

    def __init__(self, nc, st):
        self.nc = nc
        self.st = st
        self.q = {e: [] for e in self.ENG}
        self.seen = {e: {} for e in self.ENG}
        self.psem = {e: st.enter_context(nc.semaphore("p_" + e)) for e in ["pe", "act", "dve", "pool"]}
        self.dsems = []
        self.bufsem = {}
        self.bufsem_keep = []
        self.free_dsems = []

    def new_dsem(self, name):
        return None

    def _auto_dsem(self, reads, writes):
        b = writes[0] if len(writes) else reads[0]
        key = id(b)
        d = self.bufsem.get(key)
        if d is None:
            if self.free_dsems:
                d = self.free_dsems.pop()
            else:
                h = self.st.enter_context(self.nc.semaphore(f"d{len(self.dsems)}"))
                d = {"h": h, "n": 0, "name": f"d{len(self.dsems)}"}
                self.dsems.append(d)
            self.bufsem[key] = d
            self.bufsem_keep.append(b)
        return d

    def _deps(self, eng, reads, writes):
        need = {}

        def add(t):
            if t[0] == "c":
                if t[1] == "pe" and eng == "pe":
                    return
                key = ("c", t[1])
                if need.get(key, (None, -1))[1] < t[2]:
                    need[key] = (t[1], t[2])
            else:
                key = ("d", id(t[1]))
                if need.get(key, (None, -1))[1] < t[2]:
                    need[key] = (t[1], t[2])

        for b in reads:
            if b.w is not None:
                add(b.w)
        for b in writes:
            if b.w is not None:
                add(b.w)
            for t in b.r.values():
                add(t)
        waits = []
        for key, (obj, v) in need.items():
            if self.seen[eng].get(key, -1) >= v:
                continue
            self.seen[eng][key] = v
            waits.append((key[0], obj, v))
        return waits

    def _record(self, tok, reads, writes):
        for b in reads:
            key = (tok[0], tok[1] if tok[0] == "c" else id(tok[1]))
            b.r[key] = tok
        for b in writes:
            b.w = tok
            b.r = {}

    def op(self, eng, fn, reads=(), writes=()):
        waits = self._deps(eng, reads, writes)
        idx = len(self.q[eng])
        self.q[eng].append({"fn": fn, "waits": waits, "awaited": False, "dma": None})
        tok = ("c", eng, idx)
        self._record(tok, reads, writes)
        return tok

    def dma(self, eng, out, in_, reads=(), writes=(), sem=None, **kw):
        reads, writes = list(reads), list(writes)
        sem = self._auto_dsem(reads, writes)
        waits = self._deps(eng, reads, writes)
        sem["n"] += 16
        tok = ("d", sem, sem["n"])
        self.q[eng].append({"fn": (lambda e, o=out, i=in_, k=kw: e.dma_start(out=o, in_=i, **k)),
                            "waits": waits, "awaited": False, "dma": sem})
        self._record(tok, reads, writes)
        return tok

    def barrier(self):
        for e in self.ENG:
            waits = []
            for e2 in ["pe", "act", "dve", "pool"]:
                n = len(self.q[e2])
                if e2 == e:
                    n -= 0
                idx = None
                for i in range(len(self.q[e2]) - 1, -1, -1):
                    if self.q[e2][i]["fn"] is not None and self.q[e2][i]["dma"] is None:
                        idx = i
                        break
                if idx is None:
                    continue
                key = ("c", e2)
                if self.seen[e].get(key, -1) >= idx:
                    continue
                self.seen[e][key] = idx
                waits.append(("c", e2, idx))
            for d in self.dsems:
                if d["n"] == 0:
                    continue
                key = ("d", id(d))
                if self.seen[e].get(key, -1) >= d["n"]:
                    continue
                self.seen[e][key] = d["n"]
                waits.append(("d", d, d["n"]))
            if waits:
                self.q[e].append({"fn": None, "waits": waits, "awaited": False, "dma": None})
        for d in self.bufsem.values():
            self.free_dsems.append(d)
        self.bufsem = {}
        self.bufsem_keep = []

    def emit(self):
        for e in self.ENG:
            for ent in self.q[e]:
                for w in ent["waits"]:
                    if w[0] == "c":
                        self.q[w[1]][w[2]]["awaited"] = True
        cnt = {}
        for e in ["pe", "act", "dve", "pool"]:
            c = 0
            arr = []
            for ent in self.q[e]:
                if ent["awaited"]:
                    c += 1
                arr.append(c)
            cnt[e] = arr
        psem = self.psem
        q = self.q

        def run(name, e):
            for ent in q[name]:
                for w in ent["waits"]:
                    if w[0] == "c":
                        e.wait_ge(psem[w[1]], cnt[w[1]][w[2]])
                    else:
                        e.wait_ge(w[1]["h"], w[2])
                if ent["fn"] is None:
                    continue
                inst = ent["fn"](e)
                if ent["dma"] is not None:
                    inst.then_inc(ent["dma"]["h"], 16)
                elif ent["awaited"]:
                    inst.then_inc(psem[name], 1)

        with self.nc.Block() as block:
            @block.sync
            def _(e):
                run("sp", e)

            @block.scalar
            def _(e):
                run("act", e)

            @block.vector
            def _(e):
                run("dve", e)

            @block.gpsimd
            def _(e):
                run("pool", e)

            @block.tensor
            def _(e):
                run("pe", e)


def _consts():
    ident = np.eye(128, dtype=np.float32)
    m = np.arange(128)
    partner = np.where((m % 32) < 16, m + 16, m - 16)
    perm = np.zeros((128, 128), np.float32)
    perm[partner, m] = 1.0
    sgn = np.where((m % 32) < 16, -1.0, 1.0).astype(np.float32)
    tok = np.arange(L)
    pos = np.where(((m % 64) < 32)[:, None], (tok // 64)[None, :], (tok % 64)[None, :]).astype(np.float32)
    fexp = ((m % 16) / 16.0).astype(np.float32)
    colc = np.zeros((128, 4), np.float32)
    colc[:, 0] = sgn
    colc[:, 1] = fexp
    colc[:, 2] = np.where(m < 64, 1.0, -1.0)
    sel = np.zeros((2, 128), np.float32)
    sel[0, :] = 1.0
    tauA = np.zeros((128, 32, 9), np.float32)
    tauA[:, 0:16, :] = np.arange(9)[None, None, :]
    tauA[:, 16:32, :] = (8 - np.arange(9))[None, None, :]
    tauB = np.zeros((128, 32, 8), np.float32)
    tauB[:, 0:16, :] = (7 - np.arange(8))[None, None, :]
    tauB[:, 16:32, :] = np.arange(8)[None, None, :]
    return {"c_ident": ident, "c_perm": perm, "c_pos": pos, "c_col": colc, "c_sel": sel, "c_tauA": tauA, "c_tauB": tauB}


class K:
    pass


def build(debug=None):
    nc = bass.Bass("TRN2", target_bir_lowering=False)
    st = ExitStack()
    P = Prog(nc, st)
    k = K()
    k.nc, k.P, k.st = nc, P, st
    k.debug = debug or {}

    def dram_in(name, shape, dt=F32):
        return nc.dram_tensor(name, list(shape), dt, kind="ExternalInput").ap()

    dbg_outs = []

    def dram_scr(name, shape, dt):
        kind = "Internal"
        if debug is not None and name in debug.get("_inject", ()):
            kind = "ExternalInput"
        elif debug is not None and name in debug:
            kind = "ExternalOutput"
            dbg_outs.append(name)
        return nc.dram_tensor(name, list(shape), dt, kind=kind).ap()

    I = {}
    I["x"] = dram_in("x", [L, D])
    I["ctx"] = dram_in("ctx", [LC, D])
    I["cc"] = dram_in("cc", [2, D])
    I["w_ada"] = dram_in("w_ada", [D, 3 * D])
    I["b_ada"] = dram_in("b_ada", [1, 3 * D])
    I["norm_g"] = dram_in("norm_g", [D])
    I["w_in"] = dram_in("w_in", [D, INW])
    I["lam"] = dram_in("lam", [4, 64])
    I["subln_g"] = dram_in("subln_g", [1, 128])
    I["ssm_lre"] = dram_in("ssm_lre", [2, 32, 64])
    I["ssm_lim"] = dram_in("ssm_lim", [2, 32, 64])
    I["ssm_ls"] = dram_in("ssm_ls", [2, 32])
    I["ssm_bre"] = dram_in("ssm_bre", [2, 32, 64, 16])
    I["ssm_bim"] = dram_in("ssm_bim", [2, 32, 64, 16])
    I["ssm_cre"] = dram_in("ssm_cre", [2, 32, 16, 64])
    I["ssm_cim"] = dram_in("ssm_cim", [2, 32, 16, 64])
    I["ssm_d"] = dram_in("ssm_d", [1, 512])
    I["w_glu"] = dram_in("w_glu", [512, 512])
    I["b_glu"] = dram_in("b_glu", [512])
    I["w_pa"] = dram_in("w_pa", [1024, D])
    I["w_ps"] = dram_in("w_ps", [512, D])
    I["w_out"] = dram_in("w_out", [D, D])
    I["final_g"] = dram_in("final_g", [1, D])
    for cn, arr in _consts().items():
        I[cn] = dram_in(cn, arr.shape)
    out = nc.dram_tensor("out", [L, D], F32, kind="ExternalOutput").ap()

    S = {}
    S["modrow"] = dram_scr("modrow", [2, 3 * D], F32)
    S["qT"] = dram_scr("qT", [HEADS, 128, L], BF16)
    S["kT"] = dram_scr("kT", [HEADS, 128, LT], BF16)
    S["v"] = dram_scr("v", [LT, 1024], BF16)
    S["sga"] = dram_scr("sga", [L, 1024], BF16)
    S["u"] = dram_scr("u", [LT, 512], F32)
    S["sgsT"] = dram_scr("sgsT", [512, L], BF16)
    S["sgmT"] = dram_scr("sgmT", [2 * D, L], BF16)
    S["abrT"] = dram_scr("abrT", [1024, L], BF16)
    S["sbrT"] = dram_scr("sbrT", [512, L], BF16)
    S["hT"] = dram_scr("hT_dbg", [128, NKC, LT], BF16) if (debug is not None and "hT_dbg" in debug) else None
    k.I, k.S, k.out = I, S, out
    k.dbg_sem = None

    def dbg(name, shape, dt, ap_fn, bufs):
        if debug is None or name not in debug:
            return
        if name not in S:
            S[name] = nc.dram_tensor(name, list(shape), dt, kind="ExternalOutput").ap()
            dbg_outs.append(name)
        if k.dbg_sem is None:
            k.dbg_sem = P.new_dsem("dbgsem")
        o, i = ap_fn(S[name])
        P.dma("sp", o, i, reads=bufs, sem=k.dbg_sem)
    k.dbg = dbg

    def sb(name, shape, dt, stack=st):
        return stack.enter_context(nc.sbuf_tensor(name, list(shape), dt))

    def ps(name, shape, dt, stack=st):
        return stack.enter_context(nc.psum_tensor(name, list(shape), dt))

    k.sb, k.ps = sb, ps
    ident = sb("ident", [128, 128], F32)
    colc = sb("colc", [128, 4], F32)
    b_ident, b_colc = Buf("ident"), Buf("colc")
    csem = P.new_dsem("csem")
    P.dma("sp", ident[:], I["c_ident"], writes=[b_ident], sem=csem)
    P.dma("sp", colc[:], I["c_col"], writes=[b_colc], sem=csem)
    k.ident, k.b_ident, k.colc, k.b_colc, k.csem = ident, b_ident, colc, b_colc, csem
    k.ssq = sb("ssq", [128, 40], F32)
    k.b_ssq = Buf("ssq")

    phase_adaln(k)
    P.barrier()
    if debug is None or debug.get("_upto", 99) >= 1:
        phase_norm_inproj(k, debug)
        P.barrier()
    if (debug is None or debug.get("_upto", 99) >= 2) and not (debug or {}).get("_skip_ssm"):
        phase_ssm(k)
        P.barrier()
    if debug is None or debug.get("_upto", 99) >= 3:
        phase_attn(k)
        P.barrier()
    if debug is None or debug.get("_upto", 99) >= 4:
        phase_merge(k)
        P.barrier()
    P.emit()
    st.close()
    return nc, dbg_outs


def range_sin(k, stack, out_ap, y_ap, shape, tag, rbufs, wbufs, eng="dve"):
    nc, P = k.nc, k.P
    ki = k.sb(tag + "_ki", shape, I32, stack)
    kf = k.sb(tag + "_kf", shape, F32, stack)
    g = k.sb(tag + "_g", shape, F32, stack)
    bki, bkf, bg = Buf(tag + "ki"), Buf(tag + "kf"), Buf(tag + "g")
    sl = tuple([slice(None)] * len(shape))
    P.op(eng, lambda e: e.tensor_copy(out=ki[sl], in_=y_ap), reads=rbufs, writes=[bki])
    P.op(eng, lambda e: e.tensor_copy(out=kf[sl], in_=ki[sl]), reads=[bki], writes=[bkf])
    P.op(eng, lambda e: e.tensor_tensor(out=kf[sl], in0=y_ap, in1=kf[sl], op=ALU.subtract), reads=rbufs + [bkf], writes=[bkf])
    P.op(eng, lambda e: e.tensor_single_scalar(out=g[sl], in_=kf[sl], scalar=0.5, op=ALU.is_gt), reads=[bkf], writes=[bg])
    P.op(eng, lambda e: e.tensor_tensor(out=kf[sl], in0=kf[sl], in1=g[sl], op=ALU.subtract), reads=[bkf, bg], writes=[bkf])
    P.op(eng, lambda e: e.tensor_single_scalar(out=g[sl], in_=kf[sl], scalar=-0.5, op=ALU.is_lt), reads=[bkf], writes=[bg])
    P.op(eng, lambda e: e.tensor_tensor(out=kf[sl], in0=kf[sl], in1=g[sl], op=ALU.add), reads=[bkf, bg], writes=[bkf])
    P.op("act", lambda e: e.activation(out=out_ap, in_=kf[sl], func=AF.Sin, scale=TWO_PI * (1.0 - 2e-7)), reads=[bkf], writes=wbufs)


def phase_adaln(k):
    nc, P, I, S = k.nc, k.P, k.I, k.S
    with ExitStack() as ls:
        sT = k.sb("ad_sT", [128, NKC, 2], F32, ls)
        b_sT = Buf("sT")
        sem_c = P.new_dsem("ad_c")
        for v in range(2):
            P.dma("sp", sT[:, :, v], I["cc"][v].rearrange("(kc p) -> p kc", p=128), writes=[b_sT], sem=sem_c,
                  allow_slow_non_contiguous=True)
        P.op("act", lambda e: e.activation(out=sT[:], in_=sT[:], func=AF.Silu), reads=[b_sT], writes=[b_sT])
        brow = k.sb("ad_brow", [2, 3 * D], F32, ls)
        b_brow = Buf("brow")
        for v in range(2):
            P.dma("sp", brow[v:v + 1, :], I["b_ada"], writes=[b_brow], sem=sem_c)
        modrow = k.sb("ad_modrow", [2, 3 * D], F32, ls)
        b_modrow = Buf("modrow")
        NS = 2
        wst = [k.sb(f"ad_w{i}", [128, NKC, 512], F32, ls) for i in range(NS)]
        b_w = [Buf(f"adw{i}") for i in range(NS)]
        wsem = [P.new_dsem(f"ad_ws{i}") for i in range(NS)]
        pst = [k.ps(f"ad_ps{i}", [128, 512], F32, ls) for i in range(2)]
        b_ps = [Buf(f"adps{i}") for i in range(2)]
        wv = I["w_ada"].rearrange("(kc p) c -> p kc c", p=128)
        xs1 = [k.sb(f"ad_x{i}", [128, D], F32, ls) for i in range(2)]
        b_xs1 = [Buf(f"adx{i}") for i in range(2)]
        junk1 = k.sb("ad_junk", [128, D], BF16, ls)
        b_junk1 = Buf("adjunk")
        tiles_done = 0

        def ss_tile(t):
            s1 = t % 2
            src = I["x"][t * 128:(t + 1) * 128, :] if t < 16 else I["ctx"][(t - 16) * 128:(t - 15) * 128, :]
            P.dma("pool", xs1[s1][:], src, writes=[b_xs1[s1]])
            P.op("act", lambda e, s1=s1, t=t: e.activation(out=junk1[:], in_=xs1[s1][:], func=AF.Square, accum_out=k.ssq[:, t:t + 1]),
                 reads=[b_xs1[s1]], writes=[b_junk1, k.b_ssq])

        for cb in range(12):
            for _ in range(2 if cb < 6 else 1):
                if tiles_done < 18:
                    ss_tile(tiles_done)
                    tiles_done += 1
            s = cb % NS
            P.dma("sp", wst[s][:, 0:8, :], wv[:, 0:8, cb * 512:(cb + 1) * 512], writes=[b_w[s]], sem=wsem[s])
            P.dma("act", wst[s][:, 8:16, :], wv[:, 8:16, cb * 512:(cb + 1) * 512], writes=[b_w[s]], sem=wsem[s])
            pt, bp = pst[cb % 2], b_ps[cb % 2]
            for kc in range(NKC):
                P.op("pe", lambda e, kc=kc, s=s, pt=pt: e.matmul(pt[0:2, :], lhsT=sT[:, kc, :], rhs=wst[s][:, kc, :],
                                                              start=(kc == 0), stop=(kc == NKC - 1)),
                     reads=[b_sT, b_w[s]], writes=[bp])
            P.op("dve", lambda e, cb=cb, pt=pt: e.tensor_tensor(out=modrow[:, cb * 512:(cb + 1) * 512], in0=pt[0:2, :],
                                                             in1=brow[:, cb * 512:(cb + 1) * 512], op=ALU.add),
                 reads=[bp, b_brow], writes=[b_modrow])
        b_mr = Buf("modrow_d")
        k.b_modrow_d = b_mr
        P.dma("sp", S["modrow"], modrow[:], reads=[b_modrow], writes=[b_mr], sem=sem_c)


def phase_norm_inproj(k, debug):
    nc, P, I, S = k.nc, k.P, k.I, k.S
    with ExitStack() as ls:
        hT = k.sb("hT", [128, NKC, LT], BF16, ls)
        b_hT = [Buf(f"hT{t}") for t in range(18)]
        Amod = k.sb("Amod", [128, NKC, 2], F32, ls)
        Smod = k.sb("Smod", [128, NKC, 2], F32, ls)
        gcol = k.sb("gcol", [128, NKC], F32, ls)
        b_A, b_S, b_g = Buf("Amod"), Buf("Smod"), Buf("gcol")
        msem = P.new_dsem("n_m")
        for v in range(2):
            P.dma("sp", Smod[:, :, v], S["modrow"][v, 0:D].rearrange("(kc p) -> p kc", p=128),
                  reads=[k.b_modrow_d], writes=[b_S], sem=msem, allow_slow_non_contiguous=True)
            P.dma("sp", Amod[:, :, v], S["modrow"][v, D:2 * D].rearrange("(kc p) -> p kc", p=128),
                  reads=[k.b_modrow_d], writes=[b_A], sem=msem, allow_slow_non_contiguous=True)
        P.dma("sp", gcol[:], I["norm_g"].rearrange("(kc p) -> p kc", p=128), writes=[b_g], sem=msem,
              allow_slow_non_contiguous=True)
        for v in range(2):
            P.op("dve", lambda e, v=v: e.scalar_tensor_tensor(out=Amod[:, :, v], in0=Amod[:, :, v], scalar=1.0, in1=gcol[:],
                                                             op0=ALU.add, op1=ALU.mult),
                 reads=[b_A, b_g], writes=[b_A])
        with ExitStack() as l1:
            NX = 2
            xt = [k.sb(f"n_x{i}", [128, D], F32, l1) for i in range(NX)]
            b_x = [Buf(f"nx{i}") for i in range(NX)]
            xsem = [P.new_dsem(f"n_xs{i}") for i in range(NX)]
            junk = k.sb("n_junk", [128, D], BF16, l1)
            b_junk = Buf("junk")
            stat = [k.sb(f"n_st{i}", [128, 4], F32, l1) for i in range(NX)]
            b_stat = [Buf(f"nst{i}") for i in range(NX)]
            pt = [k.ps(f"n_ps{i}", [128, 512], F32, l1) for i in range(4)]
            b_pt = [Buf(f"nps{i}") for i in range(4)]
            pi = 0
            P.op("dve", lambda e: e.tensor_scalar(out=k.ssq[:, 0:18], in0=k.ssq[:, 0:18], scalar1=1.0 / D, scalar2=EPS, op0=ALU.mult, op1=ALU.add),
                 reads=[k.b_ssq], writes=[k.b_ssq])
            P.op("act", lambda e: e.activation(out=k.ssq[:, 0:18], in_=k.ssq[:, 0:18], func=AF.Ln), reads=[k.b_ssq], writes=[k.b_ssq])
            P.op("act", lambda e: e.activation(out=k.ssq[:, 20:38], in_=k.ssq[:, 0:18], func=AF.Exp, scale=-0.5), reads=[k.b_ssq], writes=[k.b_ssq])
            for t in range(18):
                s = t % NX
                v = 0 if t < 16 else 1
                src = I["x"][t * 128:(t + 1) * 128, :] if t < 16 else I["ctx"][(t - 16) * 128:(t - 15) * 128, :]
                P.dma("sp", xt[s][:, 0:1024], src[:, 0:1024], writes=[b_x[s]], sem=xsem[s])
                P.dma("act", xt[s][:, 1024:2048], src[:, 1024:2048], writes=[b_x[s]], sem=xsem[s])
                P.op("dve", lambda e, s=s, t=t: e.tensor_scalar(out=xt[s][:], in0=xt[s][:], scalar1=k.ssq[:, 20 + t:21 + t], scalar2=None,
                                                              op0=ALU.mult),
                     reads=[b_x[s], k.b_ssq], writes=[b_x[s]])
                for g4 in range(4):
                    p_, bp = pt[pi % 4], b_pt[pi % 4]
                    pi += 1
                    for j in range(4):
                        kc = g4 * 4 + j
                        P.op("pe", lambda e, s=s, kc=kc, j=j, p_=p_: e.transpose(out=p_[:, j * 128:(j + 1) * 128],
                                                                             in_=xt[s][:, kc * 128:(kc + 1) * 128],
                                                                             identity=k.ident[:]),
                             reads=[b_x[s], k.b_ident], writes=[bp])
                    for j in range(4):
                        kc = g4 * 4 + j
                        eng = "dve" if (j % 2 == 0) else "act"
                        if eng == "dve":
                            P.op("dve", lambda e, kc=kc, j=j, p_=p_, t=t, v=v: e.tensor_scalar(
                                out=hT[:, kc, t * 128:(t + 1) * 128], in0=p_[:, j * 128:(j + 1) * 128],
                                scalar1=Amod[:, kc, v:v + 1], scalar2=Smod[:, kc, v:v + 1], op0=ALU.mult, op1=ALU.add),
                                reads=[bp, b_A, b_S], writes=[b_hT[t]])
                        else:
                            P.op("act", lambda e, kc=kc, j=j, p_=p_, t=t, v=v: e.activation(
                                out=hT[:, kc, t * 128:(t + 1) * 128], in_=p_[:, j * 128:(j + 1) * 128],
                                func=AF.Identity, scale=Amod[:, kc, v:v + 1], bias=Smod[:, kc, v:v + 1]),
                                reads=[bp, b_A, b_S], writes=[b_hT[t]])
        if S["hT"] is not None:
            dsem = P.new_dsem("dbg")
            P.dma("sp", S["hT"], hT[:], reads=b_hT, writes=[Buf("x")], sem=dsem)
        P.barrier()
        if debug is not None and debug.get("_upto", 99) < 1.5:
            return
        inproj(k, ls, hT, b_hT)


def inproj(k, ls, hT, b_hT):
    nc, P, I, S = k.nc, k.P, k.I, k.S
    cosT = k.sb("cosT", [128, L], F32, ls)
    sinS = k.sb("sinS", [128, L], F32, ls)
    perm = k.sb("perm", [128, 128], F32, ls)
    b_cos, b_sin, b_perm = Buf("cos"), Buf("sin"), Buf("perm")
    tsem = P.new_dsem("ip_t")
    P.dma("sp", perm[:], I["c_perm"], writes=[b_perm], sem=tsem)
    with ExitStack() as l0:
        pos = k.sb("pos", [128, L], F32, l0)
        yv = k.sb("yv", [128, L], F32, l0)
        inv = k.sb("inv", [128, 1], F32, l0)
        b_pos, b_y, b_inv = Buf("pos"), Buf("yv"), Buf("inv")
        P.dma("sp", pos[:], I["c_pos"], writes=[b_pos], sem=tsem)
        P.op("act", lambda e: e.activation(out=inv[:], in_=k.colc[:, 1:2], func=AF.Exp, scale=-math.log(10000.0)),
             reads=[k.b_colc], writes=[b_inv])
        P.op("dve", lambda e: e.tensor_scalar(out=yv[:], in0=pos[:], scalar1=inv[:, 0:1], scalar2=1.0 / TWO_PI,
                                              op0=ALU.mult, op1=ALU.mult), reads=[b_pos, b_inv], writes=[b_y])
        range_sin(k, l0, sinS[:], yv[:], [128, L], "rs1", [b_y], [b_sin])
        P.op("dve", lambda e: e.tensor_scalar(out=sinS[:], in0=sinS[:], scalar1=k.colc[:, 0:1], scalar2=None, op0=ALU.mult),
             reads=[b_sin, k.b_colc], writes=[b_sin])
        P.op("dve", lambda e: e.tensor_scalar(out=yv[:], in0=yv[:], scalar1=0.25, scalar2=None, op0=ALU.add),
             reads=[b_y], writes=[b_y])
        range_sin(k, l0, cosT[:], yv[:], [128, L], "rs2", [b_y], [b_cos])
        P.barrier()
    b_wbq = [[Buf(f"wbq{i}_{j}") for j in range(4)] for i in range(2)]
    wb = [k.sb(f"ip_wb{i}", [128, NKC, 512], BF16, ls) for i in range(2)]
    b_wb = [Buf(f"wb{i}") for i in range(2)]
    NOB = 4
    ob = [k.sb(f"ip_ob{i}", [128, 512], BF16, ls) for i in range(NOB)]
    b_ob = [Buf(f"ob{i}") for i in range(NOB)]
    osem = [P.new_dsem(f"ip_os{i}") for i in range(NOB)]
    NOF = 3
    of = [k.sb(f"ip_of{i}", [128, 512], F32, ls) for i in range(NOF)]
    b_of = [Buf(f"of{i}") for i in range(NOF)]
    fsem = [P.new_dsem(f"ip_fs{i}") for i in range(NOF)]
    t1 = [k.sb(f"ip_t1{i}", [128, 512], F32, ls) for i in range(2)]
    b_t1 = [Buf(f"t1{i}") for i in range(2)]
    t2 = [k.sb(f"ip_t2{i}", [128, 512], F32, ls) for i in range(2)]
    b_t2 = [Buf(f"t2{i}") for i in range(2)]
    pb = [k.ps(f"ip_ps{i}", [128, 512], F32, ls) for i in range(4)]
    b_pb = [Buf(f"ipps{i}") for i in range(4)]
    pr = [k.ps(f"ip_pr{i}", [128, 512], F32, ls) for i in range(2)]
    b_pr = [Buf(f"ippr{i}") for i in range(2)]
    wv = I["w_in"].rearrange("(kc p) c -> p kc c", p=128)
    cnt = {"pb": 0, "ob": 0, "of": 0, "r": 0, "ld": 0, "ev": 0}

    rope_pending = []

    def load_block(cb):
        s2 = cb % 2
        for q4 in range(4):
            P.dma("pool", wb[s2][:, q4 * 4:(q4 + 1) * 4, :], wv[:, q4 * 4:(q4 + 1) * 4, cb * 512:(cb + 1) * 512], writes=[b_wbq[s2][q4]])

    def next_ob():
        i = cnt["ob"] % NOB
        cnt["ob"] += 1
        return i

    def evac_eng():
        cnt["ev"] += 1
        return "act" if cnt["ev"] % 2 else "dve"

    def tiles_of(tok0, n):
        return [b_hT[t] for t in range(tok0 // 128, (tok0 + n) // 128)]

    def fm_unit(cb, fc, tok0, n, kind, row0, dst):
        s2 = cb % 2
        pi = cnt["pb"] % 4
        cnt["pb"] += 1
        pt, bp = pb[pi], b_pb[pi]
        for kc in range(NKC):
            P.op("pe", lambda e, kc=kc: e.matmul(pt[:, 0:n], lhsT=wb[s2][:, kc, fc * 128:(fc + 1) * 128],
                                                 rhs=hT[:, kc, tok0:tok0 + n], start=(kc == 0), stop=(kc == NKC - 1)),
                 reads=[b_wbq[s2][kc // 4]] + tiles_of(tok0, n), writes=[bp])
        while rope_pending:
            rope_pending.pop(0)()
        oi = next_ob()
        if kind == "rope":
            ri = cnt["r"] % 2
            cnt["r"] += 1
            fi = cnt["of"] % NOF
            cnt["of"] += 1
            P.op("act", lambda e: e.activation(out=of[fi][:, 0:n], in_=pt[:, 0:n], func=AF.Copy), reads=[bp], writes=[b_of[fi]])
            P.op("dve", lambda e: e.tensor_tensor(out=t1[ri][:, 0:n], in0=of[fi][:, 0:n], in1=cosT[:, tok0:tok0 + n], op=ALU.mult),
                 reads=[b_of[fi], b_cos], writes=[b_t1[ri]])

            def fin():
                P.op("pe", lambda e: e.matmul(pr[ri][:, 0:n], lhsT=perm[:], rhs=of[fi][:, 0:n], start=True, stop=True),
                     reads=[b_perm, b_of[fi]], writes=[b_pr[ri]])
                P.op("dve", lambda e: e.tensor_tensor(out=t2[ri][:, 0:n], in0=pr[ri][:, 0:n], in1=sinS[:, tok0:tok0 + n], op=ALU.mult),
                     reads=[b_pr[ri], b_sin], writes=[b_t2[ri]])
                P.op("pool", lambda e: e.tensor_tensor(out=ob[oi][:, 0:n], in0=t1[ri][:, 0:n], in1=t2[ri][:, 0:n], op=ALU.add),
                     reads=[b_t1[ri], b_t2[ri]], writes=[b_ob[oi]])
                P.dma("sp", dst, ob[oi][:, 0:n], reads=[b_ob[oi]], sem=osem[oi])
            rope_pending.append(fin)
            return
        elif kind == "copy":
            eg = evac_eng()
            if eg == "act":
                P.op("act", lambda e: e.activation(out=ob[oi][:, 0:n], in_=pt[:, 0:n], func=AF.Copy), reads=[bp], writes=[b_ob[oi]])
            else:
                P.op("dve", lambda e: e.tensor_copy(out=ob[oi][:, 0:n], in_=pt[:, 0:n]), reads=[bp], writes=[b_ob[oi]])
        else:
            fn = AF.Silu if kind == "silu" else AF.Sigmoid
            P.op("act", lambda e: e.activation(out=ob[oi][:, 0:n], in_=pt[:, 0:n], func=fn), reads=[bp], writes=[b_ob[oi]])
        P.dma("sp", dst, ob[oi][:, 0:n], reads=[b_ob[oi]], sem=osem[oi])

    def tm_unit(cb, t, kind, dst):
        s2 = cb % 2
        pi = cnt["pb"] % 4
        cnt["pb"] += 1
        pt, bp = pb[pi], b_pb[pi]
        for kc in range(NKC):
            P.op("pe", lambda e, kc=kc: e.matmul(pt[:], lhsT=hT[:, kc, t * 128:(t + 1) * 128], rhs=wb[s2][:, kc, :],
                                                 start=(kc == 0), stop=(kc == NKC - 1)),
                 reads=[b_wbq[s2][kc // 4], b_hT[t]], writes=[bp])
        while rope_pending:
            rope_pending.pop(0)()
        if kind == "f32":
            fi = cnt["of"] % NOF
            cnt["of"] += 1
            P.op("dve", lambda e: e.tensor_copy(out=of[fi][:], in_=pt[:]), reads=[bp], writes=[b_of[fi]])
            P.dma("sp", dst, of[fi][:], reads=[b_of[fi]], sem=fsem[fi])
            return
        oi = next_ob()
        if kind == "copy":
            eg = evac_eng()
            if eg == "act":
                P.op("act", lambda e: e.activation(out=ob[oi][:], in_=pt[:], func=AF.Copy), reads=[bp], writes=[b_ob[oi]])
            else:
                P.op("dve", lambda e: e.tensor_copy(out=ob[oi][:], in_=pt[:]), reads=[bp], writes=[b_ob[oi]])
        else:
            P.op("act", lambda e: e.activation(out=ob[oi][:], in_=pt[:], func=AF.Silu), reads=[bp], writes=[b_ob[oi]])
        P.dma("sp", dst, ob[oi][:], reads=[b_ob[oi]], sem=osem[oi])

    NCB = INW // 512
    load_block(0)
    for cb in range(NCB):
        if cb + 1 < NCB:
            load_block(cb + 1)
        c0 = cb * 512
        if cb < 2:
            for fc in range(4):
                h = cb * 4 + fc
                for tb in range(4):
                    fm_unit(cb, fc, tb * 512, 512, "rope", 0, S["qT"][h, :, tb * 512:(tb + 1) * 512])
        elif cb < 4:
            for fc in range(4):
                h = (cb - 2) * 4 + fc
                for tb in range(4):
                    fm_unit(cb, fc, tb * 512, 512, "rope", 0, S["kT"][h, :, tb * 512:(tb + 1) * 512])
                fm_unit(cb, fc, L, LC, "copy", 0, S["kT"][h, :, L:LT])
        elif cb < 6:
            for t in range(18):
                tm_unit(cb, t, "copy", S["v"][t * 128:(t + 1) * 128, (cb - 4) * 512:(cb - 3) * 512])
        elif cb < 8:
            for t in range(16):
                tm_unit(cb, t, "silu", S["sga"][t * 128:(t + 1) * 128, (cb - 6) * 512:(cb - 5) * 512])
        elif cb == 8:
            for t in range(18):
                tm_unit(cb, t, "f32", S["u"][t * 128:(t + 1) * 128, :])
        elif cb == 9:
            for fc in range(4):
                for tb in range(4):
                    fm_unit(cb, fc, tb * 512, 512, "silu", 0, S["sgsT"][fc * 128:(fc + 1) * 128, tb * 512:(tb + 1) * 512])
        else:
            for fc in range(4):
                r0 = (cb - 10) * 512 + fc * 128
                for tb in range(4):
                    fm_unit(cb, fc, tb * 512, 512, "sigm", 0, S["sgmT"][r0:r0 + 128, tb * 512:(tb + 1) * 512])


def phase_ssm(k):
    nc, P, I, S = k.nc, k.P, k.I, k.S
    MUL, ADD, SUB = ALU.mult, ALU.add, ALU.subtract
    with ExitStack() as ls:
        ToepT = k.sb("ss_toep", [128, 32, 128], BF16, ls)
        RCp = k.sb("ss_rcp", [128, 2, 2, 16, 256], BF16, ls)
        WT = k.sb("ss_wt", [128, 2, 16, 2, 128], BF16, ls)
        A8c = k.sb("ss_a8c", [128, 2, 16, 2], F32, ls)
        A8s = k.sb("ss_a8s", [128, 2, 16, 2], F32, ls)
        b_toep = [Buf(f"toep{g}") for g in range(32)]
        b_rcp, b_wt = Buf("rcp"), Buf("wt")
        b_U = [Buf(f"U{g}") for g in range(32)]
        b_zbf = [Buf("zbf0"), Buf("zbf1")]
        b_ygT = Buf("ygT")
        b_a8 = Buf("a8")
        pbk = [k.ps(f"ss_ps{i}", [128, 512], F32, ls) for i in range(8)]
        b_pbk = [Buf(f"ssps{i}") for i in range(8)]
        pc = {"i": 0}

        def nb():
            i = pc["i"] % 8
            pc["i"] += 1
            return pbk[i], b_pbk[i]

        csem = P.new_dsem("ss_c")
        with ExitStack() as l0:
            lre = k.sb("ss_lre", [128, 32], F32, l0)
            lim = k.sb("ss_lim", [128, 32], F32, l0)
            dtt = k.sb("ss_dt", [128, 32], F32, l0)
            alog = k.sb("ss_alog", [128, 32], F32, l0)
            th = k.sb("ss_th", [128, 32], F32, l0)
            b_l, b_dt, b_al = Buf("lrelim"), Buf("dtt"), Buf("alogth")
            for gp in range(2):
                for d in range(2):
                    P.dma("sp", lre[gp * 64:(gp + 1) * 64, d * 16:(d + 1) * 16], I["ssm_lre"][d, gp * 16:(gp + 1) * 16, :].rearrange("g p -> p g"),
                          writes=[b_l], sem=csem, allow_slow_non_contiguous=True)
                    P.dma("sp", lim[gp * 64:(gp + 1) * 64, d * 16:(d + 1) * 16], I["ssm_lim"][d, gp * 16:(gp + 1) * 16, :].rearrange("g p -> p g"),
                          writes=[b_l], sem=csem, allow_slow_non_contiguous=True)
                    P.dma("sp", dtt[gp * 64:(gp + 1) * 64, d * 16:(d + 1) * 16], I["ssm_ls"][d:d + 1, gp * 16:(gp + 1) * 16].broadcast_to([64, 16]),
                          writes=[b_dt], sem=csem)
            P.op("act", lambda e: e.activation(out=dtt[:], in_=dtt[:], func=AF.Exp), reads=[b_dt], writes=[b_dt])
            P.op("dve", lambda e: e.tensor_tensor(out=alog[:], in0=lre[:], in1=dtt[:], op=MUL), reads=[b_l, b_dt], writes=[b_al])
            P.op("dve", lambda e: e.scalar_tensor_tensor(out=th[:], in0=lim[:], scalar=1.0 / TWO_PI, in1=dtt[:], op0=MUL, op1=MUL),
                 reads=[b_l, b_dt], writes=[b_al])
            tabs = {}
            for nm, n in (("A", 9), ("B", 8)):
                tau = k.sb(f"ss_tau{nm}", [128, 32, n], F32, l0)
                ex = k.sb(f"ss_ex{nm}", [128, 32, n], F32, l0)
                yv = k.sb(f"ss_yv{nm}", [128, 32, n], F32, l0)
                sn = k.sb(f"ss_sn{nm}", [128, 32, n], F32, l0)
                cs = k.sb(f"ss_cs{nm}", [128, 32, n], F32, l0)
                b_tau, b_ex, b_yv, b_sn, b_cs = Buf("tau" + nm), Buf("ex" + nm), Buf("yv" + nm), Buf("sn" + nm), Buf("cs" + nm)
                P.dma("sp", tau[:], I["c_tau" + nm], writes=[b_tau], sem=csem)
                P.op("dve", lambda e, ex=ex, tau=tau, n=n: e.tensor_tensor(out=ex[:], in0=tau[:], in1=alog[:, :, None].broadcast_to([128, 32, n]), op=MUL),
                     reads=[b_tau, b_al], writes=[b_ex])
                P.op("act", lambda e, ex=ex: e.activation(out=ex[:], in_=ex[:], func=AF.Exp), reads=[b_ex], writes=[b_ex])
                P.op("dve", lambda e, yv=yv, tau=tau, n=n: e.tensor_tensor(out=yv[:], in0=tau[:], in1=th[:, :, None].broadcast_to([128, 32, n]), op=MUL),
                     reads=[b_tau, b_al], writes=[b_yv])
                fl = lambda t: t[:].rearrange("p a b -> p (a b)")
                range_sin(k, l0, fl(sn), fl(yv), [128, 32 * n], "ssr1" + nm, [b_yv], [b_sn])
                P.op("dve", lambda e, yv=yv: e.tensor_scalar(out=yv[:], in0=yv[:], scalar1=0.25, scalar2=None, op0=ADD), reads=[b_yv], writes=[b_yv])
                range_sin(k, l0, fl(cs), fl(yv), [128, 32 * n], "ssr2" + nm, [b_yv], [b_cs])
                P.op("dve", lambda e, cs=cs, ex=ex: e.tensor_tensor(out=cs[:], in0=cs[:], in1=ex[:], op=MUL), reads=[b_cs, b_ex], writes=[b_cs])
                P.op("dve", lambda e, sn=sn, ex=ex: e.tensor_tensor(out=sn[:], in0=sn[:], in1=ex[:], op=MUL), reads=[b_sn, b_ex], writes=[b_sn])
                tabs[nm] = (cs, sn, b_cs, b_sn)
            ARA, AIA, b_ARA, b_AIA = tabs["A"]
            ARB, AIB, b_ARB, b_AIB = tabs["B"]
            a1 = k.sb("ss_a1", [128, 2, 32], F32, l0)
            b_a1 = Buf("a1")
            for d in range(2):
                i8 = 8 if d == 0 else 0
                i1 = 1 if d == 0 else 7
                dsl = slice(d * 16, (d + 1) * 16)
                for ri in range(2):
                    P.op("dve", lambda e, d=d, ri=ri, i8=i8, dsl=dsl: e.tensor_copy(out=A8c[:, d, :, ri], in_=ARA[:, dsl, i8]), reads=[b_ARA], writes=[b_a8])
                P.op("dve", lambda e, d=d, i8=i8, dsl=dsl: e.tensor_scalar(out=A8s[:, d, :, 0], in0=AIA[:, dsl, i8], scalar1=-1.0, scalar2=None, op0=MUL),
                     reads=[b_AIA], writes=[b_a8])
                P.op("dve", lambda e, d=d, i8=i8, dsl=dsl: e.tensor_copy(out=A8s[:, d, :, 1], in_=AIA[:, dsl, i8]), reads=[b_AIA], writes=[b_a8])
                P.op("dve", lambda e, d=d, i1=i1, dsl=dsl: e.tensor_copy(out=a1[:, 0, dsl], in_=ARA[:, dsl, i1]), reads=[b_ARA], writes=[b_a1])
                P.op("dve", lambda e, d=d, i1=i1, dsl=dsl: e.tensor_copy(out=a1[:, 1, dsl], in_=AIA[:, dsl, i1]), reads=[b_AIA], writes=[b_a1])
            fz = k.sb("ss_fz", [128, 6, 32], F32, l0)
            b_fz = Buf("fz")
            P.op("dve", lambda e: e.tensor_tensor(out=fz[:, 0, :], in0=lre[:], in1=lre[:], op=MUL), reads=[b_l], writes=[b_fz])
            P.op("dve", lambda e: e.tensor_tensor(out=fz[:, 1, :], in0=lim[:], in1=lim[:], op=MUL), reads=[b_l], writes=[b_fz])
            P.op("dve", lambda e: e.tensor_tensor(out=fz[:, 0, :], in0=fz[:, 0, :], in1=fz[:, 1, :], op=ADD), reads=[b_fz], writes=[b_fz])
            P.op("dve", lambda e: e.reciprocal(out=fz[:, 1, :], in_=fz[:, 0, :]), reads=[b_fz], writes=[b_fz])
            P.op("dve", lambda e: e.tensor_scalar(out=fz[:, 0, :], in0=a1[:, 0, :], scalar1=-1.0, scalar2=None, op0=ADD), reads=[b_a1], writes=[b_fz])
            P.op("dve", lambda e: e.tensor_tensor(out=fz[:, 2, :], in0=fz[:, 0, :], in1=lre[:], op=MUL), reads=[b_fz, b_l], writes=[b_fz])
            P.op("dve", lambda e: e.tensor_tensor(out=fz[:, 3, :], in0=a1[:, 1, :], in1=lim[:], op=MUL), reads=[b_a1, b_l], writes=[b_fz])
            P.op("dve", lambda e: e.tensor_tensor(out=fz[:, 2, :], in0=fz[:, 2, :], in1=fz[:, 3, :], op=ADD), reads=[b_fz], writes=[b_fz])
            P.op("dve", lambda e: e.tensor_tensor(out=fz[:, 2, :], in0=fz[:, 2, :], in1=fz[:, 1, :], op=MUL), reads=[b_fz], writes=[b_fz])
            P.op("dve", lambda e: e.tensor_tensor(out=fz[:, 4, :], in0=a1[:, 1, :], in1=lre[:], op=MUL), reads=[b_a1, b_l], writes=[b_fz])
            P.op("dve", lambda e: e.tensor_tensor(out=fz[:, 5, :], in0=fz[:, 0, :], in1=lim[:], op=MUL), reads=[b_fz, b_l], writes=[b_fz])
            P.op("dve", lambda e: e.tensor_tensor(out=fz[:, 4, :], in0=fz[:, 4, :], in1=fz[:, 5, :], op=SUB), reads=[b_fz], writes=[b_fz])
            P.op("dve", lambda e: e.tensor_tensor(out=fz[:, 4, :], in0=fz[:, 4, :], in1=fz[:, 1, :], op=MUL), reads=[b_fz], writes=[b_fz])
            BT = k.sb("ss_BT", [128, 2, 2, 16, 16], F32, l0)
            BB = k.sb("ss_BB", [128, 2, 2, 16, 16], F32, l0)
            CN = k.sb("ss_CN", [128, 2, 2, 2, 128], F32, l0)
            CT = k.sb("ss_CT", [128, 2, 2, 16, 16], F32, l0)
            tA = k.sb("ss_tA", [128, 16, 9, 16], F32, l0)
            tB = k.sb("ss_tB", [128, 16, 9, 16], F32, l0)
            b_BT, b_BB, b_CN, b_CT, b_tA, b_tB = Buf("BT"), Buf("BB"), Buf("CN"), Buf("CT"), Buf("tA"), Buf("tB")
            for d in range(2):
                for ri in range(2):
                    bsrc = I["ssm_bre"] if ri == 0 else I["ssm_bim"]
                    csrc = I["ssm_cre"] if ri == 0 else I["ssm_cim"]
                    for gp in range(2):
                        P.dma("sp", BT[gp * 64:(gp + 1) * 64, d, ri, :, :], bsrc[d, gp * 16:(gp + 1) * 16].rearrange("g p c -> p g c"),
                              writes=[b_BT], sem=csem)
                        for blk in range(2):
                            g0 = gp * 16 + blk * 8
                            P.dma("sp", CN[:, d, ri, blk, gp * 64:(gp + 1) * 64], csrc[d, g0:g0 + 8].rearrange("g c p -> (g c) p"),
                                  writes=[b_CN], sem=csem)
            for d in range(2):
                for ri in range(2):
                    for blk in range(2):
                        pt, bp = nb()
                        P.op("pe", lambda e, d=d, ri=ri, blk=blk, pt=pt: e.transpose(out=pt[:, 0:128], in_=CN[:, d, ri, blk, :], identity=k.ident[:]),
                             reads=[b_CN, k.b_ident], writes=[bp])
                        P.op("dve", lambda e, d=d, ri=ri, blk=blk, pt=pt: e.tensor_copy(
                            out=CT[:, d, ri, blk * 8:(blk + 1) * 8, :].rearrange("p a b -> p (a b)"), in_=pt[:, 0:128]), reads=[bp], writes=[b_CT])
            for d in range(2):
                dsl = slice(d * 16, (d + 1) * 16)
                frb = lambda d=d, dsl=dsl: fz[:, 2, dsl][:, :, None].broadcast_to([128, 16, 16])
                fib = lambda d=d, dsl=dsl: fz[:, 4, dsl][:, :, None].broadcast_to([128, 16, 16])
                t16a = tA[:, :, 0, :]
                t16b = tB[:, :, 0, :]
                P.op("dve", lambda e, d=d, frb=frb: e.tensor_tensor(out=t16a, in0=BT[:, d, 0], in1=frb(), op=MUL), reads=[b_BT, b_fz], writes=[b_tA])
                P.op("dve", lambda e, d=d, fib=fib: e.tensor_tensor(out=t16b, in0=BT[:, d, 1], in1=fib(), op=MUL), reads=[b_BT, b_fz], writes=[b_tB])
                P.op("dve", lambda e, d=d: e.tensor_tensor(out=BB[:, d, 0], in0=t16a, in1=t16b, op=SUB), reads=[b_tA, b_tB], writes=[b_BB])
                P.op("dve", lambda e, d=d, frb=frb: e.tensor_tensor(out=t16a, in0=BT[:, d, 1], in1=frb(), op=MUL), reads=[b_BT, b_fz], writes=[b_tA])
                P.op("dve", lambda e, d=d, fib=fib: e.tensor_tensor(out=t16b, in0=BT[:, d, 0], in1=fib(), op=MUL), reads=[b_BT, b_fz], writes=[b_tB])
                P.op("dve", lambda e, d=d: e.tensor_tensor(out=BB[:, d, 1], in0=t16a, in1=t16b, op=ADD), reads=[b_tA, b_tB], writes=[b_BB])
            P.op("pool", lambda e: e.memset(RCp[:].rearrange("p a b c d -> p (a b c d)"), 0.0), writes=[b_rcp])
            for d in range(2):
                dsl = slice(d * 16, (d + 1) * 16)
                off = 112 if d == 0 else 0
                bc_c = lambda ri, d=d: CT[:, d, ri][:, :, None, :].broadcast_to([128, 16, 9, 16])
                bc_ar = lambda dsl=dsl: ARA[:, dsl, :][:, :, :, None].broadcast_to([128, 16, 9, 16])
                bc_ai = lambda dsl=dsl: AIA[:, dsl, :][:, :, :, None].broadcast_to([128, 16, 9, 16])
                dst = lambda ri, d=d, off=off: RCp[:, d, ri, :, off:off + 144].rearrange("p g (t c) -> p g t c", c=16)
                P.op("dve", lambda e, bc_c=bc_c, bc_ar=bc_ar: e.tensor_tensor(out=tA[:], in0=bc_c(0), in1=bc_ar(), op=MUL), reads=[b_CT, b_ARA], writes=[b_tA])
                P.op("dve", lambda e, bc_c=bc_c, bc_ai=bc_ai: e.tensor_tensor(out=tB[:], in0=bc_c(1), in1=bc_ai(), op=MUL), reads=[b_CT, b_AIA], writes=[b_tB])
                P.op("dve", lambda e, dst=dst: e.tensor_tensor(out=dst(0), in0=tA[:], in1=tB[:], op=SUB), reads=[b_tA, b_tB], writes=[b_rcp])
                P.op("dve", lambda e, bc_c=bc_c, bc_ai=bc_ai: e.tensor_tensor(out=tA[:], in0=bc_c(0), in1=bc_ai(), op=MUL), reads=[b_CT, b_AIA], writes=[b_tA])
                P.op("dve", lambda e, bc_c=bc_c, bc_ar=bc_ar: e.tensor_tensor(out=tB[:], in0=bc_c(1), in1=bc_ar(), op=MUL), reads=[b_CT, b_ARA], writes=[b_tB])
                P.op("dve", lambda e: e.tensor_tensor(out=tA[:], in0=tA[:], in1=tB[:], op=ADD), reads=[b_tA, b_tB], writes=[b_tA])
                P.op("dve", lambda e, dst=dst: e.tensor_scalar(out=dst(1), in0=tA[:], scalar1=-1.0, scalar2=None, op0=MUL), reads=[b_tA], writes=[b_rcp])
            Lp = k.sb("ss_Lp", [128, 64, 240], BF16, l0)
            b_Lp = Buf("Lp")
            P.op("pool", lambda e: e.memset(Lp[:].rearrange("p a b -> p (a b)"), 0.0), writes=[b_Lp])
            P.op("pool", lambda e: e.tensor_copy(out=Lp[:, :, 112:128], in_=BB[:].rearrange("p a b c d -> p (a b c) d")), reads=[b_BB], writes=[b_Lp])
            BW = k.sb("ss_BW", [128, 2, 2, 16, 128], F32, l0)
            b_BW = Buf("BW")
            for d in range(2):
                dsl = slice(d * 16, (d + 1) * 16)
                bc_b = lambda ri, d=d: BB[:, d, ri][:, :, None, :].broadcast_to([128, 16, 8, 16])
                bc_ar = lambda dsl=dsl: ARB[:, dsl, :][:, :, :, None].broadcast_to([128, 16, 8, 16])
                bc_ai = lambda dsl=dsl: AIB[:, dsl, :][:, :, :, None].broadcast_to([128, 16, 8, 16])
                dst = lambda ri, d=d: BW[:, d, ri].rearrange("p g (t c) -> p g t c", c=16)
                ta8 = tA[:, :, 0:8, :]
                tb8 = tB[:, :, 0:8, :]
                P.op("dve", lambda e, bc_b=bc_b, bc_ar=bc_ar: e.tensor_tensor(out=ta8, in0=bc_b(0), in1=bc_ar(), op=MUL), reads=[b_BB, b_ARB], writes=[b_tA])
                P.op("dve", lambda e, bc_b=bc_b, bc_ai=bc_ai: e.tensor_tensor(out=tb8, in0=bc_b(1), in1=bc_ai(), op=MUL), reads=[b_BB, b_AIB], writes=[b_tB])
                P.op("dve", lambda e, dst=dst: e.tensor_tensor(out=dst(0), in0=ta8, in1=tb8, op=SUB), reads=[b_tA, b_tB], writes=[b_BW])
                P.op("dve", lambda e, bc_b=bc_b, bc_ai=bc_ai: e.tensor_tensor(out=ta8, in0=bc_b(0), in1=bc_ai(), op=MUL), reads=[b_BB, b_AIB], writes=[b_tA])
                P.op("dve", lambda e, bc_b=bc_b, bc_ar=bc_ar: e.tensor_tensor(out=tb8, in0=bc_b(1), in1=bc_ar(), op=MUL), reads=[b_BB, b_ARB], writes=[b_tB])
                P.op("dve", lambda e, dst=dst: e.tensor_tensor(out=dst(1), in0=ta8, in1=tb8, op=ADD), reads=[b_tA, b_tB], writes=[b_BW])
            for d in range(2):
                for g2 in range(16):
                    for ri in range(2):
                        pt, bp = nb()
                        P.op("pe", lambda e, d=d, g2=g2, ri=ri, pt=pt: e.transpose(out=pt[:, 0:128], in_=BW[:, d, ri, g2, :], identity=k.ident[:]),
                             reads=[b_BW, k.b_ident], writes=[bp])
                        eng = "act" if (g2 + ri) % 2 else "dve"
                        if eng == "act":
                            P.op("act", lambda e, d=d, g2=g2, ri=ri, pt=pt: e.activation(out=WT[:, d, g2, ri, :], in_=pt[:, 0:128], func=AF.Copy), reads=[bp], writes=[b_wt])
                        else:
                            P.op("dve", lambda e, d=d, g2=g2, ri=ri, pt=pt: e.tensor_copy(out=WT[:, d, g2, ri, :], in_=pt[:, 0:128]), reads=[bp], writes=[b_wt])
            for g2 in range(16):
                for gp in range(2):
                    g = gp * 16 + g2
                    pt, bp = nb()
                    psl = slice(gp * 64, (gp + 1) * 64)
                    n = 0
                    for d in range(2):
                        for ri in range(2):
                            for s_ in range(8):
                                w0 = (7 - s_) * 16 if d == 0 else (8 - s_) * 16
                                l0_ = (7 - s_) * 16
                                P.op("pe", lambda e, d=d, ri=ri, g2=g2, w0=w0, l0_=l0_, psl=psl, pt=pt, n=n: e.matmul(
                                    pt[:, 0:128], lhsT=Lp[psl, (d * 2 + ri) * 16 + g2, l0_:l0_ + 128], rhs=RCp[psl, d, ri, g2, w0:w0 + 128],
                                    start=(n == 0), stop=(n == 31)), reads=[b_Lp, b_rcp], writes=[bp])
                                n += 1
                    P.op("dve" if g % 2 else "act",
                         (lambda e, g=g, pt=pt: e.tensor_copy(out=ToepT[:, g, :], in_=pt[:, 0:128])) if g % 2 else
                         (lambda e, g=g, pt=pt: e.activation(out=ToepT[:, g, :], in_=pt[:, 0:128], func=AF.Copy)),
                         reads=[bp], writes=[b_toep[g]])
            P.barrier()
        Ubuf = k.sb("ss_ubuf", [128, 32, 320], BF16, ls)
        Zbf = k.sb("ss_zbf", [128, 2, 16, 2, 288], BF16, ls)
        ygT = k.sb("ss_ygT", [128, 4, L], BF16, ls)
        if k.debug.get("_ssm_upto", 99) < 1:
            return
        with ExitStack() as l1:
            ucm = [k.sb(f"ss_ucm{i}", [128, 8, 512], F32, l1) for i in range(2)]
            b_ucm = [Buf(f"ucm{i}") for i in range(2)]
            usem = [P.new_dsem(f"ss_us{i}") for i in range(2)]
            ucg = k.sb("ss_ucg", [128, 32, 128], F32, l1)
            b_ucg = Buf("ucg")
            for jt in range(3):
                si = jt % 2
                nj = 128 if jt < 2 else 32
                r0 = jt * 1024
                P.dma("sp", ucm[si][0:nj], S["u"][r0:r0 + nj * 8, :].rearrange("(j s) c -> j s c", s=8), writes=[b_ucm[si]], sem=usem[si])
                P.op("dve", lambda e, si=si, nj=nj: e.tensor_copy(out=ucg[0:nj].rearrange("p g (s c) -> p g s c", c=16),
                                                                 in_=ucm[si][0:nj].rearrange("p s (g c) -> p g s c", c=16)),
                     reads=[b_ucm[si]], writes=[b_ucg])
                for g0 in range(0, 32, 4):
                    pt, bp = nb()
                    for gg in range(4):
                        g = g0 + gg
                        P.op("pe", lambda e, si=si, nj=nj, g=g, gg=gg, pt=pt: e.transpose(
                            out=pt[:, gg * 128:gg * 128 + nj], in_=ucg[0:nj, g, :], identity=k.ident[0:nj, 0:nj]),
                            reads=[b_ucg, k.b_ident], writes=[bp])
                    src = lambda pt=pt, nj=nj: pt[:].rearrange("p (a b) -> p a b", b=128)[:, :, 0:nj]
                    cols = [32 + jt * 128] if jt < 2 else [0, 288]
                    for ci, c0 in enumerate(cols):
                        eng = "act" if (g0 // 4 + ci) % 2 else "dve"
                        if eng == "act":
                            P.op("act", lambda e, g0=g0, c0=c0, nj=nj, src=src: e.activation(out=Ubuf[:, g0:g0 + 4, c0:c0 + nj], in_=src(), func=AF.Copy),
                                 reads=[bp], writes=[b_U[g0 + i] for i in range(4)])
                        else:
                            P.op("dve", lambda e, g0=g0, c0=c0, nj=nj, src=src: e.tensor_copy(out=Ubuf[:, g0:g0 + 4, c0:c0 + nj], in_=src()),
                                 reads=[bp], writes=[b_U[g0 + i] for i in range(4)])
            P.barrier()
        if k.debug.get("_ssm_upto", 99) < 2:
            return
        with ExitStack() as l2:
            Z = [k.sb(f"ss_Z{d}", [128, 16, 2, 288], F32, l2) for d in range(2)]
            b_Z = [Buf("Z0"), Buf("Z1")]
            for d in range(2):
                j0 = 0 if d == 0 else 32
                for g2 in range(16):
                    for ri in range(2):
                        pt, bp = nb()
                        for gp in range(2):
                            P.op("pe", lambda e, d=d, g2=g2, ri=ri, gp=gp, pt=pt, j0=j0: e.matmul(
                                pt[gp * 64:(gp + 1) * 64, 0:288], lhsT=WT[:, d, g2, ri, gp * 64:(gp + 1) * 64], rhs=Ubuf[:, gp * 16 + g2, j0:j0 + 288],
                                start=True, stop=True), reads=[b_wt, b_U[gp * 16 + g2]], writes=[bp])
                        if (g2 + ri) % 2:
                            P.op("act", lambda e, d=d, g2=g2, ri=ri, pt=pt: e.activation(out=Z[d][:, g2, ri, :], in_=pt[:, 0:288], func=AF.Copy), reads=[bp], writes=[b_Z[d]])
                        else:
                            P.op("dve", lambda e, d=d, g2=g2, ri=ri, pt=pt: e.tensor_copy(out=Z[d][:, g2, ri, :], in_=pt[:, 0:288]), reads=[bp], writes=[b_Z[d]])
            k.dbg("V_dbg", [2, 128, 16 * 2 * 288], F32, lambda dd: (dd[0], Z[0][:].rearrange("p a b c -> p (a b c)")), [b_Z[0]])
            k.dbg("V_dbg", [2, 128, 16 * 2 * 288], F32, lambda dd: (dd[1], Z[1][:].rearrange("p a b c -> p (a b c)")), [b_Z[1]])
            m1 = [k.sb(f"ss_m1{d}", [128, 16, 2], F32, l2) for d in range(2)]
            m2 = [k.sb(f"ss_m2{d}", [128, 16, 2], F32, l2) for d in range(2)]
            b_m1 = [Buf("m10"), Buf("m11")]
            b_m2 = [Buf("m20"), Buf("m21")]

            def scan_step(d, J, Jp):
                eng = "dve" if d == 0 else "pool"
                P.op(eng, lambda e: e.tensor_tensor(out=m1[d][:], in0=Z[d][:, :, :, Jp], in1=A8c[:, d], op=MUL), reads=[b_Z[d], b_a8], writes=[b_m1[d]])
                P.op(eng, lambda e: e.tensor_tensor(out=m2[d][:], in0=Z[d][:, :, ::-1, Jp], in1=A8s[:, d], op=MUL), reads=[b_Z[d], b_a8], writes=[b_m2[d]])
                P.op(eng, lambda e: e.tensor_tensor(out=m1[d][:], in0=m1[d][:], in1=m2[d][:], op=ADD), reads=[b_m1[d], b_m2[d]], writes=[b_m1[d]])
                P.op(eng, lambda e: e.tensor_tensor(out=Z[d][:, :, :, J], in0=Z[d][:, :, :, J], in1=m1[d][:], op=ADD), reads=[b_Z[d], b_m1[d]], writes=[b_Z[d]])

            for st_ in range(1, 288):
                scan_step(0, st_, st_ - 1)
                scan_step(1, 287 - st_, 288 - st_)
            for d in range(2):
                eng = "dve" if d == 0 else "pool"
                P.op(eng, lambda e, d=d: e.tensor_copy(out=Zbf[:, d].rearrange("p a b c -> p (a b c)"), in_=Z[d][:].rearrange("p a b c -> p (a b c)")),
                     reads=[b_Z[d]], writes=[b_zbf[d]])
            k.dbg("Z_dbg", [2, 128, 16 * 2 * 288], F32, lambda dd: (dd[0], Z[0][:].rearrange("p a b c -> p (a b c)")), [b_Z[0]])
            k.dbg("Z_dbg", [2, 128, 16 * 2 * 288], F32, lambda dd: (dd[1], Z[1][:].rearrange("p a b c -> p (a b c)")), [b_Z[1]])
            P.barrier()
        if k.debug.get("_ssm_upto", 99) < 3:
            return
        with ExitStack() as l3:
            ycm = k.sb("ss_ycm", [128, 8, 512], F32, l3)
            b_ycm = [Buf(f"ycm{g}") for g in range(32)]
            ut = k.sb("ss_ut", [128, 8, 512], F32, l3)
            b_ut = Buf("ut")
            utsem = P.new_dsem("ss_uts")
            Dfull = k.sb("ss_D", [128, 512], F32, l3)
            b_D = Buf("Dfull")
            P.dma("sp", Dfull[:], I["ssm_d"][0:1, :].broadcast_to([128, 512]), writes=[b_D], sem=csem)
            sq = [k.sb(f"ss_sq{i}", [128, 512], F32, l3) for i in range(2)]
            b_sq = [Buf("sq0"), Buf("sq1")]
            GC = math.sqrt(2.0 / math.pi)
            for jt in range(2):
                P.dma("sp", ut[:], S["u"][jt * 1024:(jt + 1) * 1024, :].rearrange("(j s) c -> j s c", s=8), writes=[b_ut], sem=utsem)
                for g in range(32):
                    gp, g2 = g // 16, g % 16
                    psl = slice(gp * 64, (gp + 1) * 64)
                    pt, bp = nb()
                    c0 = 32 + jt * 128
                    P.op("pe", lambda e, g=g, c0=c0, pt=pt: e.matmul(pt[:, 0:128], lhsT=Ubuf[:, g, c0:c0 + 128], rhs=ToepT[:, g, :], start=True, stop=False),
                         reads=[b_U[g], b_toep[g]], writes=[bp])
                    for d in range(2):
                        jz = (31 + jt * 128) if d == 0 else (1 + jt * 128)
                        w0 = 128 if d == 0 else 0
                        for ri in range(2):
                            last = (d == 1 and ri == 1)
                            P.op("pe", lambda e, d=d, ri=ri, g2=g2, psl=psl, jz=jz, w0=w0, pt=pt, last=last: e.matmul(
                                pt[:, 0:128], lhsT=Zbf[psl, d, g2, ri, jz:jz + 128], rhs=RCp[psl, d, ri, g2, w0:w0 + 128], start=False, stop=last),
                                reads=[b_zbf[d], b_rcp], writes=[bp])
                    src = lambda pt=pt: pt[:, 0:128].rearrange("p (t c) -> p t c", c=16)
                    P.op("dve", lambda e, g=g, src=src: e.tensor_tensor(out=ycm[:, :, g * 16:(g + 1) * 16], in0=ut[:, :, g * 16:(g + 1) * 16],
                                                                       in1=Dfull[:, g * 16:(g + 1) * 16][:, None, :].broadcast_to([128, 8, 16]), op=MUL),
                         reads=[b_ut, b_D], writes=[b_ycm[g]])
                    P.op("dve", lambda e, g=g, src=src: e.tensor_tensor(out=ycm[:, :, g * 16:(g + 1) * 16], in0=ycm[:, :, g * 16:(g + 1) * 16], in1=src(), op=ADD),
                         reads=[bp, b_ycm[g]], writes=[b_ycm[g]])
                k.dbg("y_dbg", [L, 512], F32, lambda dd, jt=jt: (dd[jt * 1024:(jt + 1) * 1024, :].rearrange("(j s) c -> j s c", s=8), ycm[:]), b_ycm)
                for t in range(8):
                    i = t % 2
                    P.op("dve", lambda e, t=t, i=i: e.tensor_tensor(out=sq[i][:], in0=ycm[:, t, :], in1=ycm[:, t, :], op=MUL), reads=b_ycm, writes=[b_sq[i]])
                    P.op("dve", lambda e, t=t, i=i: e.tensor_scalar(out=sq[i][:], in0=sq[i][:], scalar1=0.044715, scalar2=1.0, op0=MUL, op1=ADD), reads=[b_sq[i]], writes=[b_sq[i]])
                    P.op("dve", lambda e, t=t, i=i: e.tensor_tensor(out=sq[i][:], in0=sq[i][:], in1=ycm[:, t, :], op=MUL), reads=[b_sq[i]] + b_ycm, writes=[b_sq[i]])
                    P.op("act", lambda e, t=t, i=i: e.activation(out=sq[i][:], in_=sq[i][:], func=AF.Sigmoid, scale=2.0 * GC), reads=[b_sq[i]], writes=[b_sq[i]])
                    P.op("dve", lambda e, t=t, i=i: e.tensor_tensor(out=sq[i][:], in0=sq[i][:], in1=ycm[:, t, :], op=MUL), reads=[b_sq[i]] + b_ycm, writes=[b_sq[i]])
                    pt, bp = nb()
                    for chb in range(4):
                        P.op("pe", lambda e, i=i, chb=chb, pt=pt: e.transpose(out=pt[:, chb * 128:(chb + 1) * 128], in_=sq[i][:, chb * 128:(chb + 1) * 128], identity=k.ident[:]),
                             reads=[b_sq[i], k.b_ident], writes=[bp])
                    tsl = slice(jt * 1024 + t, (jt + 1) * 1024, 8)
                    P.op("act", lambda e, pt=pt, tsl=tsl: e.activation(out=ygT[:, :, tsl], in_=pt[:].rearrange("p (a b) -> p a b", b=128), func=AF.Copy),
                         reads=[bp], writes=[b_ygT])
            P.barrier()
        if k.debug.get("_ssm_upto", 99) < 4:
            return
        with ExitStack() as l4:
            wg32 = k.sb("ss_wg32", [128, 4, 512], F32, l4)
            wg = k.sb("ss_wg", [128, 4, 512], BF16, l4)
            bg = k.sb("ss_bg", [128, 4], F32, l4)
            b_wg32, b_wg, b_bg = Buf("wg32"), Buf("wg"), Buf("bg")
            P.dma("sp", wg32[:], I["w_glu"].rearrange("(fc p) c -> p fc c", p=128), writes=[b_wg32], sem=csem)
            P.dma("sp", bg[:], I["b_glu"].rearrange("(fc p) -> p fc", p=128), writes=[b_bg], sem=csem, allow_slow_non_contiguous=True)
            P.op("dve", lambda e: e.tensor_copy(out=wg[:], in_=wg32[:]), reads=[b_wg32], writes=[b_wg])
            gst = [k.sb(f"ss_gst{i}", [128, 512], BF16, l4) for i in range(2)]
            b_gst = [Buf("gst0"), Buf("gst1")]
            gsem = [P.new_dsem(f"ss_gs{i}") for i in range(2)]
            sg = [k.sb(f"ss_sg{i}", [128, 512], F32, l4) for i in range(2)]
            b_sg = [Buf("sg0"), Buf("sg1")]
            so = [k.sb(f"ss_so{i}", [128, 512], BF16, l4) for i in range(2)]
            b_so = [Buf("so0"), Buf("so1")]
            sosem = [P.new_dsem(f"ss_sos{i}") for i in range(2)]
            ui = 0
            for fo in range(4):
                for tb in range(4):
                    i = ui % 2
                    ui += 1
                    tsl = slice(tb * 512, (tb + 1) * 512)
                    P.dma("sp", gst[i][:], S["sgsT"][fo * 128:(fo + 1) * 128, tsl], writes=[b_gst[i]], sem=gsem[i])
                    pt, bp = nb()
                    for fc in range(4):
                        P.op("pe", lambda e, fc=fc, fo=fo, tsl=tsl, pt=pt: e.matmul(pt[:], lhsT=wg[:, fc, fo * 128:(fo + 1) * 128], rhs=ygT[:, fc, tsl],
                                                                               start=(fc == 0), stop=(fc == 3)), reads=[b_wg, b_ygT], writes=[bp])
                    P.op("act", lambda e, i=i, fo=fo, pt=pt: e.activation(out=sg[i][:], in_=pt[:], func=AF.Sigmoid, bias=bg[:, fo:fo + 1]),
                         reads=[bp, b_bg], writes=[b_sg[i]])
                    P.op("dve", lambda e, i=i, fo=fo, tsl=tsl: e.tensor_tensor(out=sg[i][:], in0=sg[i][:], in1=ygT[:, fo, tsl], op=MUL),
                         reads=[b_sg[i], b_ygT], writes=[b_sg[i]])
                    P.op("dve", lambda e, i=i: e.tensor_tensor(out=so[i][:], in0=sg[i][:], in1=gst[i][:], op=MUL),
                         reads=[b_sg[i], b_gst[i]], writes=[b_so[i]])
                    P.dma("sp", S["sbrT"][fo * 128:(fo + 1) * 128, tsl], so[i][:], reads=[b_so[i]], sem=sosem[i])


def phase_attn(k):
    nc, P, I, S = k.nc, k.P, k.I, k.S
    with ExitStack() as ls:
        lamv = k.sb("at_lamv", [128, 4, 64], F32, ls)
        lw = k.sb("at_lw", [128, 8], F32, ls)
        G = k.sb("at_G", [128, 128], F32, ls)
        b_lamv, b_lw, b_G = Buf("lamv"), Buf("lw"), Buf("G")
        csem = P.new_dsem("at_c")
        P.dma("sp", lamv[:].rearrange("p a b -> p (a b)"), I["lam"].rearrange("a b -> (a b)").partition_broadcast(128),
              writes=[b_lamv], sem=csem)
        P.dma("sp", G[:], I["subln_g"][0:1, :].broadcast_to([128, 128]), writes=[b_G], sem=csem)
        P.op("dve", lambda e: e.tensor_scalar(out=G[:], in0=G[:], scalar1=(1.0 - LAM_INIT), scalar2=None, op0=ALU.mult),
             reads=[b_G], writes=[b_G])
        for i in range(2):
            P.op("dve", lambda e, i=i: e.tensor_tensor(out=lamv[:, 2 * i, :], in0=lamv[:, 2 * i, :], in1=lamv[:, 2 * i + 1, :], op=ALU.mult),
                 reads=[b_lamv], writes=[b_lamv])
            P.op("dve", lambda e, i=i: e.tensor_reduce(out=lw[:, i:i + 1], in_=lamv[:, 2 * i, :], axis=mybir.AxisListType.X, op=ALU.add),
                 reads=[b_lamv], writes=[b_lw])
        P.op("act", lambda e: e.activation(out=lw[:, 2:4], in_=lw[:, 0:2], func=AF.Exp), reads=[b_lw], writes=[b_lw])
        P.op("dve", lambda e: e.tensor_tensor(out=lw[:, 4:5], in0=lw[:, 3:4], in1=lw[:, 2:3], op=ALU.subtract), reads=[b_lw], writes=[b_lw])
        P.op("dve", lambda e: e.tensor_scalar(out=lw[:, 5:6], in0=lw[:, 4:5], scalar1=-LAM_INIT, scalar2=None, op0=ALU.add),
             reads=[b_lw], writes=[b_lw])
        neglam = lw[:, 5:6]
        qTs = [k.sb(f"at_q{i}", [128, L], BF16, ls) for i in range(2)]
        kTs = [k.sb(f"at_k{i}", [128, LT], BF16, ls) for i in range(2)]
        Vs = [k.sb(f"at_v{i}", [128, 18, 130], BF16, ls) for i in range(2)]
        gas = [k.sb(f"at_ga{i}", [128, 16, 128], BF16, ls) for i in range(2)]
        aTs = [k.sb(f"at_aT{i}", [128, L], BF16, ls) for i in range(2)]
        b_q = [Buf(f"atq{i}") for i in range(2)]
        b_k = [Buf(f"atk{i}") for i in range(2)]
        b_v = [Buf(f"atv{i}") for i in range(2)]
        b_ga = [Buf(f"atga{i}") for i in range(2)]
        b_aT = [Buf(f"ataT{i}") for i in range(2)]
        hsem = [P.new_dsem(f"at_h{i}") for i in range(2)]
        asem = [P.new_dsem(f"at_a{i}") for i in range(2)]
        for i in range(2):
            P.op("pool", lambda e, i=i: e.memset(Vs[i][:, :, 128:130], 1.0), writes=[b_v[i]])
        PT = [k.sb(f"at_pt{i}", [128, 2, 18, 256], BF16, ls) for i in range(2)]
        b_PT = [[[Buf(f"pt{i}_{c}_{kp}") for kp in range(9)] for c in range(2)] for i in range(2)]
        sbk = [k.ps(f"at_s{i}", [128, 512], F32, ls) for i in range(3)]
        b_sbk = [Buf(f"ats{i}") for i in range(3)]
        obk = [k.ps(f"at_o{i}", [128, 512], F32, ls) for i in range(4)]
        b_obk = [Buf(f"ato{i}") for i in range(4)]
        tbk = k.ps("at_t", [128, 512], F32, ls)
        b_tbk = Buf("att")
        sm = [k.sb(f"at_sm{i}", [128, 8], F32, ls) for i in range(2)]
        b_sm = [Buf(f"atsm{i}") for i in range(2)]
        tmp = [k.sb(f"at_tmp{i}", [128, 128], F32, ls) for i in range(2)]
        b_tmp = [Buf(f"attmp{i}") for i in range(2)]
        ov = [k.sb(f"at_ov{i}", [128, 128], F32, ls) for i in range(2)]
        b_ov = [Buf(f"atov{i}") for i in range(2)]
        junk = k.sb("at_junk", [128, 128], F32, ls)
        b_junk = Buf("atjunk")
        cnt = {"s": 0, "u": 0}

        def load_head(h):
            s = h % 2
            P.dma("sp", qTs[s][:], S["qT"][h], writes=[b_q[s]], sem=hsem[s])
            P.dma("sp", kTs[s][:], S["kT"][h], writes=[b_k[s]], sem=hsem[s])
            P.dma("sp", Vs[s][:, :, 0:128], S["v"][:, h * 128:(h + 1) * 128].rearrange("(t p) e -> p t e", p=128),
                  writes=[b_v[s]], sem=hsem[s])
            P.dma("sp", gas[s][:], S["sga"][:, h * 128:(h + 1) * 128].rearrange("(t p) e -> p t e", p=128),
                  writes=[b_ga[s]], sem=hsem[s])

        def A_steps(h, qb):
            s = h % 2
            ps_ = qb % 2
            steps = []
            for kp in range(9):
                def step(kp=kp):
                    for c in range(2):
                        si = cnt["s"] % 3
                        cnt["s"] += 1
                        for j in range(2):
                            kt = 2 * kp + j
                            P.op("pe", lambda e, kt=kt, j=j, c=c, si=si: e.matmul(
                                sbk[si][:, j * 256:(j + 1) * 256], lhsT=kTs[s][c * 64:(c + 1) * 64, kt * 128:(kt + 1) * 128],
                                rhs=qTs[s][c * 64:(c + 1) * 64, qb * 256:(qb + 1) * 256], start=True, stop=True),
                                reads=[b_k[s], b_q[s]], writes=[b_sbk[si]])
                        P.op("act", lambda e, c=c, kp=kp, si=si: e.activation(
                            out=PT[ps_][:, c, 2 * kp:2 * kp + 2, :].rearrange("p a b -> p (a b)"), in_=sbk[si][:], func=AF.Exp, scale=0.125),
                            reads=[b_sbk[si]], writes=[b_PT[ps_][c][kp]])
                steps.append(step)
            return steps

        def B_gen(h, qb):
            s = h % 2
            ps_ = qb % 2
            for qi_ in range(2):
                yield from unitB(h, qb, qi_, s, ps_)

        def unitB(h, qb, qi, s, ps_):
            if True:
                qt = qb * 2 + qi
                u = cnt["u"] % 2
                cnt["u"] += 1
                banks = [obk[u * 2], obk[u * 2 + 1]]
                bb = [b_obk[u * 2], b_obk[u * 2 + 1]]
                for c in range(2):
                    for kt in range(18):
                        P.op("pe", lambda e, c=c, kt=kt: e.matmul(
                            banks[c][:, 0:129], lhsT=PT[ps_][:, c, kt, qi * 128:(qi + 1) * 128], rhs=Vs[s][:, kt, 0:129],
                            start=(kt == 0), stop=(kt == 17)),
                            reads=[b_PT[ps_][c][kt // 2], b_v[s]], writes=[bb[c]])
                        yield
                flush_pending()
                smt, bsm = sm[u], b_sm[u]
                for c in range(2):
                    P.op("dve", lambda e, c=c: e.reciprocal(out=smt[:, c:c + 1], in_=banks[c][:, 128:129]), reads=[bb[c]], writes=[bsm])
                P.op("dve", lambda e: e.tensor_tensor(out=smt[:, 2:3], in0=smt[:, 1:2], in1=neglam, op=ALU.mult), reads=[bsm, b_lw], writes=[bsm])
                P.op("dve", lambda e: e.tensor_scalar(out=tmp[u][:], in0=banks[1][:, 0:128], scalar1=smt[:, 2:3], scalar2=None, op0=ALU.mult),
                     reads=[bb[1], bsm], writes=[b_tmp[u]])
                P.op("dve", lambda e: e.scalar_tensor_tensor(out=ov[u][:], in0=banks[0][:, 0:128], scalar=smt[:, 0:1], in1=tmp[u][:],
                                                            op0=ALU.mult, op1=ALU.add),
                     reads=[bb[0], bsm, b_tmp[u]], writes=[b_ov[u]])
                P.op("dve", lambda e: e.tensor_tensor(out=tmp[u][:], in0=ov[u][:], in1=ov[u][:], op=ALU.mult),
                     reads=[b_ov[u]], writes=[b_tmp[u]])
                P.op("dve", lambda e: e.tensor_reduce(out=smt[:, 3:4], in_=tmp[u][:], axis=mybir.AxisListType.X, op=ALU.add),
                     reads=[b_tmp[u]], writes=[bsm])
                P.op("dve", lambda e: e.tensor_scalar(out=smt[:, 4:5], in0=smt[:, 3:4], scalar1=1.0 / 128, scalar2=EPS, op0=ALU.mult, op1=ALU.add),
                     reads=[bsm], writes=[bsm])
                def fin():
                    P.op("act", lambda e: e.activation(out=smt[:, 5:6], in_=smt[:, 4:5], func=AF.Ln), reads=[bsm], writes=[bsm])
                    P.op("act", lambda e: e.activation(out=smt[:, 6:7], in_=smt[:, 5:6], func=AF.Exp, scale=-0.5), reads=[bsm], writes=[bsm])
                    P.op("dve", lambda e: e.scalar_tensor_tensor(out=ov[u][:], in0=ov[u][:], scalar=smt[:, 6:7], in1=G[:], op0=ALU.mult, op1=ALU.mult),
                         reads=[b_ov[u], bsm, b_G], writes=[b_ov[u]])
                    P.op("pool", lambda e: e.tensor_tensor(out=ov[u][:], in0=ov[u][:], in1=gas[s][:, qt, :], op=ALU.mult),
                         reads=[b_ov[u], b_ga[s]], writes=[b_ov[u]])

                    def fin2():
                        P.op("pe", lambda e: e.transpose(out=tbk[:, 0:128], in_=ov[u][:], identity=k.ident[:]), reads=[b_ov[u], k.b_ident], writes=[b_tbk])
                        P.op("dve", lambda e: e.tensor_copy(out=aTs[s][:, qt * 128:(qt + 1) * 128], in_=tbk[:, 0:128]),
                             reads=[b_tbk], writes=[b_aT[s]])
                    pending2.append(fin2)
                pending.append(fin)

        pending = []
        pending2 = []

        def flush_pending():
            while pending2:
                pending2.pop(0)()
            while pending:
                pending.pop(0)()

        def interleave(a_steps, bgen, per=8):
            for st_ in a_steps:
                st_()
                if bgen is not None:
                    for _ in range(per):
                        try:
                            next(bgen)
                        except StopIteration:
                            bgen = None
                            break
            if bgen is not None:
                for _ in bgen:
                    pass

        load_head(0)
        load_head(1)
        interleave(A_steps(0, 0), None)
        for h in range(HEADS):
            for qb in range(8):
                if qb + 1 < 8:
                    nxt = A_steps(h, qb + 1)
                elif h + 1 < HEADS:
                    nxt = A_steps(h + 1, 0)
                else:
                    nxt = []
                interleave(nxt, B_gen(h, qb))
            flush_pending()
            flush_pending()
            P.dma("pool", S["abrT"][h * 128:(h + 1) * 128, :], aTs[h % 2][:], reads=[b_aT[h % 2]], sem=asem[h % 2])
            if h + 2 < HEADS:
                load_head(h + 2)


def phase_merge(k):
    nc, P, I, S = k.nc, k.P, k.I, k.S
    with ExitStack() as ls:
        mT = k.sb("mg_mT", [128, NKC, L], BF16, ls)
        b_mT = [Buf(f"mT{tb}") for tb in range(4)]
        wout = k.sb("mg_wout", [128, NKC, D], BF16, ls)
        b_wout = Buf("wout")
        NXB = 2
        wov = I["w_out"].rearrange("(kc p) c -> p kc c", p=128)
        wo_state = {"kc": 0}
        b_woutc = [Buf(f"woutc{i}") for i in range(NKC)]

        def load_wout_chunk():
            kc = wo_state["kc"]
            if kc >= NKC:
                return
            wo_state["kc"] += 1
            P.dma("pool", wout[:, kc, :], wov[:, kc, :], writes=[b_woutc[kc]])
        with ExitStack() as l1:
            abrT = k.sb("mg_abrT", [128, 8, L], BF16, l1)
            sbrT = k.sb("mg_sbrT", [128, 4, L], BF16, l1)
            b_abrT, b_sbrT = Buf("abrT"), Buf("sbrT")
            lsem = P.new_dsem("mg_l")
            P.dma("sp", abrT[:], S["abrT"].rearrange("(fc p) t -> p fc t", p=128), writes=[b_abrT], sem=lsem)
            P.dma("sp", sbrT[:], S["sbrT"].rearrange("(fc p) t -> p fc t", p=128), writes=[b_sbrT], sem=lsem)
            NWS = 2
            wbf = [k.sb(f"mg_wbf{i}", [128, 12, 128], BF16, l1) for i in range(NWS)]
            b_wbf = [Buf(f"mgwbf{i}") for i in range(NWS)]
            NG = 2
            gt = [k.sb(f"mg_gt{i}", [128, 2, L], BF16, l1) for i in range(NG)]
            b_gt = [Buf(f"mggt{i}") for i in range(NG)]
            t1 = [k.sb(f"mg_t1{i}", [128, 512], F32, l1) for i in range(2)]
            t2 = [k.sb(f"mg_t2{i}", [128, 512], F32, l1) for i in range(2)]
            b_t1 = [Buf(f"mgt1{i}") for i in range(2)]
            b_t2 = [Buf(f"mgt2{i}") for i in range(2)]
            pa = [k.ps(f"mg_pa{i}", [128, 512], F32, l1) for i in range(2)]
            pp = [k.ps(f"mg_pp{i}", [128, 512], F32, l1) for i in range(2)]
            b_pa = [Buf(f"mgpa{i}") for i in range(2)]
            b_pp = [Buf(f"mgpp{i}") for i in range(2)]
            wpa_v = I["w_pa"].rearrange("(fc p) c -> p fc c", p=128)
            wps_v = I["w_ps"].rearrange("(fc p) c -> p fc c", p=128)
            ui = 0

            def load_w(fo):
                s = fo % NWS
                P.dma("pool", wbf[s][:, 0:8, :], wpa_v[:, :, fo * 128:(fo + 1) * 128], writes=[b_wbf[s]])
                P.dma("pool", wbf[s][:, 8:12, :], wps_v[:, :, fo * 128:(fo + 1) * 128], writes=[b_wbf[s]])
                gi = fo % NG
                P.dma("sp", gt[gi][:, 0, :], S["sgmT"][fo * 128:(fo + 1) * 128, :], writes=[b_gt[gi]])
                P.dma("sp", gt[gi][:, 1, :], S["sgmT"][D + fo * 128:D + (fo + 1) * 128, :], writes=[b_gt[gi]])

            load_w(0)
            for fo in range(NKC):
                if fo + 1 < NKC:
                    load_w(fo + 1)
                load_wout_chunk()
                s = fo % NWS
                gi = fo % NG
                for tb in range(4):
                    u2 = ui % 2
                    ui += 1
                    tsl = slice(tb * 512, (tb + 1) * 512)
                    for fc in range(8):
                        P.op("pe", lambda e, fc=fc, s=s, tsl=tsl, u2=u2: e.matmul(pa[u2][:], lhsT=wbf[s][:, fc, :], rhs=abrT[:, fc, tsl],
                                                                        start=(fc == 0), stop=(fc == 7)),
                             reads=[b_wbf[s], b_abrT], writes=[b_pa[u2]])
                    for fc in range(4):
                        P.op("pe", lambda e, fc=fc, s=s, tsl=tsl, u2=u2: e.matmul(pp[u2][:], lhsT=wbf[s][:, 8 + fc, :], rhs=sbrT[:, fc, tsl],
                                                                        start=(fc == 0), stop=(fc == 3)),
                             reads=[b_wbf[s], b_sbrT], writes=[b_pp[u2]])
                    P.op("dve", lambda e, gi=gi, u2=u2, tsl=tsl: e.tensor_tensor(out=t1[u2][:], in0=pa[u2][:], in1=gt[gi][:, 0, tsl], op=ALU.mult),
                         reads=[b_pa[u2], b_gt[gi]], writes=[b_t1[u2]])
                    P.op("dve", lambda e, gi=gi, u2=u2, tsl=tsl: e.tensor_tensor(out=t2[u2][:], in0=pp[u2][:], in1=gt[gi][:, 1, tsl], op=ALU.mult),
                         reads=[b_pp[u2], b_gt[gi]], writes=[b_t2[u2]])
                    P.op("pool", lambda e, fo=fo, tsl=tsl, u2=u2: e.tensor_tensor(out=mT[:, fo, tsl], in0=t1[u2][:], in1=t2[u2][:], op=ALU.add),
                         reads=[b_t1[u2], b_t2[u2]], writes=[b_mT[tb]])
            P.barrier()
        gateB = k.sb("mg_gateB", [128, D], F32, ls)
        fgB = k.sb("mg_fgB", [128, D], F32, ls)
        b_gateB, b_fgB = Buf("gateB"), Buf("fgB")
        c2 = P.new_dsem("mg_c2")
        P.dma("sp", gateB[:], S["modrow"][0:1, 2 * D:3 * D].broadcast_to([128, D]), writes=[b_gateB], sem=c2)
        P.dma("sp", fgB[:], I["final_g"][0:1, :].broadcast_to([128, D]), writes=[b_fgB], sem=c2)
        xb = [k.sb(f"mg_x{i}", [128, D], F32, ls) for i in range(NXB)]
        b_xb = [Buf(f"mgx{i}") for i in range(NXB)]
        xn = [k.sb(f"mg_xn{i}", [128, D], F32, ls) for i in range(NXB)]
        b_xn = [Buf(f"mgxn{i}") for i in range(NXB)]
        xsem = [P.new_dsem(f"mg_xs{i}") for i in range(NXB)]
        osem = [P.new_dsem(f"mg_os{i}") for i in range(NXB)]
        st2 = [k.sb(f"mg_st{i}", [128, 4], F32, ls) for i in range(NXB)]
        b_st2 = [Buf(f"mgst{i}") for i in range(NXB)]
        while wo_state["kc"] < NKC:
            load_wout_chunk()
        po = [k.ps(f"mg_po{i}", [128, 512], F32, ls) for i in range(3)]
        b_po = [Buf(f"mgpo{i}") for i in range(3)]
        pi = 0
        for t in range(16):
            s = t % NXB
            tb = t // 4
            P.dma("sp", xb[s][:], I["x"][t * 128:(t + 1) * 128, :], writes=[b_xb[s]], sem=xsem[s])
            for cbk in range(4):
                p_ = pi % 3
                pi += 1
                for kc in range(NKC):
                    P.op("pe", lambda e, kc=kc, cbk=cbk, p_=p_, t=t: e.matmul(po[p_][:], lhsT=mT[:, kc, t * 128:(t + 1) * 128],
                                                                          rhs=wout[:, kc, cbk * 512:(cbk + 1) * 512],
                                                                          start=(kc == 0), stop=(kc == NKC - 1)),
                         reads=[b_mT[tb], b_woutc[kc]], writes=[b_po[p_]])
                P.op("dve", lambda e, cbk=cbk, p_=p_, s=s: e.tensor_tensor(out=xn[s][:, cbk * 512:(cbk + 1) * 512], in0=po[p_][:],
                                                                       in1=gateB[:, cbk * 512:(cbk + 1) * 512], op=ALU.mult),
                     reads=[b_po[p_], b_gateB], writes=[b_xn[s]])
            P.op("pool", lambda e, s=s: e.tensor_tensor(out=xn[s][:], in0=xn[s][:], in1=xb[s][:], op=ALU.add),
                 reads=[b_xn[s], b_xb[s]], writes=[b_xn[s]])
            P.op("pool", lambda e, s=s: e.tensor_tensor(out=xb[s][:], in0=xn[s][:], in1=xn[s][:], op=ALU.mult),
                 reads=[b_xn[s]], writes=[b_xb[s]])
            P.op("dve", lambda e, s=s: e.tensor_reduce(out=st2[s][:, 0:1], in_=xb[s][:], axis=mybir.AxisListType.X, op=ALU.add),
                 reads=[b_xb[s]], writes=[b_st2[s]])
            P.op("dve", lambda e, s=s: e.tensor_scalar(out=st2[s][:, 1:2], in0=st2[s][:, 0:1], scalar1=1.0 / D, scalar2=EPS, op0=ALU.mult, op1=ALU.add),
                 reads=[b_st2[s]], writes=[b_st2[s]])
            P.op("act", lambda e, s=s: e.activation(out=st2[s][:, 2:3], in_=st2[s][:, 1:2], func=AF.Ln), reads=[b_st2[s]], writes=[b_st2[s]])
            P.op("act", lambda e, s=s: e.activation(out=st2[s][:, 3:4], in_=st2[s][:, 2:3], func=AF.Exp, scale=-0.5), reads=[b_st2[s]], writes=[b_st2[s]])
            P.op("dve", lambda e, s=s: e.scalar_tensor_tensor(out=xn[s][:], in0=xn[s][:], scalar=st2[s][:, 3:4], in1=fgB[:], op0=ALU.mult, op1=ALU.mult),
                 reads=[b_xn[s], b_st2[s], b_fgB], writes=[b_xn[s]])
            P.dma("sp", k.out[t * 128:(t + 1) * 128, :], xn[s][:], reads=[b_xn[s]], sem=osem[s])


_CACHE = {}


def _prep_inputs(inputs, b):
    f = lambda a: np.ascontiguousarray(np.asarray(a, dtype=np.float32))
    m = {}
    m["x"] = f(inputs["x"][b])
    m["ctx"] = f(inputs["ctx"][b])
    m["cc"] = f(np.stack([np.asarray(inputs["c"])[b], np.asarray(inputs["c_ctx"])], axis=0))
    m["w_ada"] = f(inputs["w_ada"][0])
    m["b_ada"] = f(inputs["b_ada"][0]).reshape(1, -1)
    m["norm_g"] = f(inputs["norm_g"][0])
    m["w_in"] = f(inputs["w_in"][0])
    m["lam"] = f(np.stack([np.asarray(inputs["lambda_q1"])[0], np.asarray(inputs["lambda_k1"])[0],
                           np.asarray(inputs["lambda_q2"])[0], np.asarray(inputs["lambda_k2"])[0]], axis=0))
    m["subln_g"] = f(inputs["subln_g"][0]).reshape(1, 128)
    m["ssm_lre"] = f(inputs["ssm_lambda_re"][0])
    m["ssm_lim"] = f(inputs["ssm_lambda_im"][0])
    m["ssm_ls"] = f(inputs["ssm_log_step"][0])
    m["ssm_bre"] = f(inputs["ssm_b_re"][0])
    m["ssm_bim"] = f(inputs["ssm_b_im"][0])
    m["ssm_cre"] = f(inputs["ssm_c_re"][0])
    m["ssm_cim"] = f(inputs["ssm_c_im"][0])
    m["ssm_d"] = f(inputs["ssm_d"][0]).reshape(1, 512)
    m["w_glu"] = f(inputs["w_glu"][0])
    m["b_glu"] = f(inputs["b_glu"][0])
    m["w_pa"] = f(inputs["w_pa"][0])
    m["w_ps"] = f(inputs["w_ps"][0])
    m["w_out"] = f(inputs["w_out"][0])
    m["final_g"] = f(inputs["final_g"]).reshape(1, D)
    m.update(_consts())
    return m


def kernel(**inputs):
    if "nc" not in _CACHE:
        _CACHE["nc"] = build()[0]
    nc = _CACHE["nc"]
    shared = None
    in_maps = []
    for b in range(8):
        m = _prep_inputs(inputs, b)
        if shared is None:
            shared = m
        else:
            for key in m:
                if key not in ("x", "ctx", "cc"):
                    m[key] = shared[key]
        in_maps.append(m)
    res = run_bass_kernel_spmd(nc, in_maps, core_ids=list(range(8)))
    return np.stack([np.asarray(r["out"], dtype=np.float32) for r in res.results], axis=0)
```

```python
import math
import numpy as np
import ml_dtypes
from contextlib import ExitStack
import concourse.bass as bass
import concourse.mybir as mybir
from concourse.bass_utils import run_bass_kernel_spmd

F32 = mybir.dt.float32
BF16 = mybir.dt.bfloat16
I32 = mybir.dt.int32
AF = mybir.ActivationFunctionType
ALU = mybir.AluOpType

D = 2048
L = 2048
LC = 256
LT = L + LC
NKC = D // 128
INW = 9216
HEADS = 8
EPS = 1e-6
LAM_INIT = 0.8 - 0.6 * math.exp(-0.3 * 0)
TWO_PI = 2.0 * math.pi


class Buf:
    __slots__ = ("name", "w", "r")

    def __init__(self, name):
        self.name = name
        self.w = None
        self.r = {}


class Prog:
    ENG = ["pe", "act", "dve", "pool", "sp"]

    def __init__(self, nc, st):
        self.nc = nc
        self.st = st
        self.q = {e: [] for e in self.ENG}
        self.seen = {e: {} for e in self.ENG}
        self.psem = {e: st.enter_context(nc.semaphore("p_" + e)) for e in ["pe", "act", "dve", "pool"]}
        self.dsems = []
        self.bufsem = {}
        self.bufsem_keep = []
        self.free_dsems = []

    def new_dsem(self, name):
        return None

    def _auto_dsem(self, reads, writes):
        b = writes[0] if len(writes) else reads[0]
        key = id(b)
        d = self.bufsem.get(key)
        if d is None:
            if self.free_dsems:
                d = self.free_dsems.pop()
            else:
                h = self.st.enter_context(self.nc.semaphore(f"d{len(self.dsems)}"))
                d = {"h": h, "n": 0, "name": f"d{len(self.dsems)}"}
                self.dsems.append(d)
            self.bufsem[key] = d
            self.bufsem_keep.append(b)
        return d

    def _deps(self, eng, reads, writes):
        need = {}

        def add(t):
            if t[0] == "c":
                if t[1] == "pe" and eng == "pe":
                    return
                key = ("c", t[1])
                if need.get(key, (None, -1))[1] < t[2]:
                    need[key] = (t[1], t[2])
            else:
                key = ("d", id(t[1]))
                if need.get(key, (None, -1))[1] < t[2]:
                    need[key] = (t[1], t[2])

        for b in reads:
            if b.w is not None:
                add(b.w)
        for b in writes:
            if b.w is not None:
                add(b.w)
            for t in b.r.values():
                add(t)
        waits = []
        for key, (obj, v) in need.items():
            if self.seen[eng].get(key, -1) >= v:
                continue
            self.seen[eng][key] = v
            waits.append((key[0], obj, v))
        return waits

    def _record(self, tok, reads, writes):
        for b in reads:
            key = (tok[0], tok[1] if tok[0] == "c" else id(tok[1]))
            b.r[key] = tok
        for b in writes:
            b.w = tok
            b.r = {}

    def op(self, eng, fn, reads=(), writes=()):
        waits = self._deps(eng, reads, writes)
        idx = len(self.q[eng])
        self.q[eng].append({"fn": fn, "waits": waits, "awaited": False, "dma": None})
        tok = ("c", eng, idx)
        self._record(tok, reads, writes)
        return tok

    def dma(self, eng, out, in_, reads=(), writes=(), sem=None, **kw):
        reads, writes = list(reads), list(writes)
        sem = self._auto_dsem(reads, writes)
        waits = self._deps(eng, reads, writes)
        sem["n"] += 16
        tok = ("d", sem, sem["n"])
        self.q[eng].append({"fn": (lambda e, o=out, i=in_, k=kw: e.dma_start(out=o, in_=i, **k)),
                            "waits": waits, "awaited": False, "dma": sem})
        self._record(tok, reads, writes)
        return tok

    def barrier(self):
        for e in self.ENG:
            waits = []
            for e2 in ["pe", "act", "dve", "pool"]:
                n = len(self.q[e2])
                if e2 == e:
                    n -= 0
                idx = None
                for i in range(len(self.q[e2]) - 1, -1, -1):
                    if self.q[e2][i]["fn"] is not None and self.q[e2][i]["dma"] is None:
                        idx = i
                        break
                if idx is None:
                    continue
                key = ("c", e2)
                if self.seen[e].get(key, -1) >= idx:
                    continue
                self.seen[e][key] = idx
                waits.append(("c", e2, idx))
            for d in self.dsems:
                if d["n"] == 0:
                    continue
                key = ("d", id(d))
                if self.seen[e].get(key, -1) >= d["n"]:
                    continue
                self.seen[e][key] = d["n"]
                waits.append(("d", d, d["n"]))
            if waits:
                self.q[e].append({"fn": None, "waits": waits, "awaited": False, "dma": None})
        for d in self.bufsem.values():
            self.free_dsems.append(d)
        self.bufsem = {}
        self.bufsem_keep = []

    def emit(self):
        for e in self.ENG:
            for ent in self.q[e]:
                for w in ent["waits"]:
                    if w[0] == "c":
                        self.q[w[1]][w[2]]["awaited"] = True
        cnt = {}
        for e in ["pe", "act", "dve", "pool"]:
            c = 0
            arr = []
            for ent in self.q[e]:
                if ent["awaited"]:
                    c += 1
                arr.append(c)
            cnt[e] = arr
        psem = self.psem
        q = self.q

        def run(name, e):
            for ent in q[name]:
                for w in ent["waits"]:
                    if w[0] == "c":
                        e.wait_ge(psem[w[1]], cnt[w[1]][w[2]])
                    else:
                        e.wait_ge(w[1]["h"], w[2])
                if ent["fn"] is None:
                    continue
                inst = ent["fn"](e)
                if ent["dma"] is not None:
                    inst.then_inc(ent["dma"]["h"], 16)
                elif ent["awaited"]:
                    inst.then_inc(psem[name], 1)

        with self.nc.Block() as block:
            @block.sync
            def _(e):
                run("sp", e)

            @block.scalar
            def _(e):
                run("act", e)

            @block.vector
            def _(e):
                run("dve", e)

            @block.gpsimd
            def _(e):
                run("pool", e)

            @block.tensor
            def _(e):
                run("pe", e)


def _consts():
    ident = np.eye(128, dtype=np.float32)
    m = np.arange(128)
    partner = np.where((m % 32) < 16, m + 16, m - 16)
    perm = np.zeros((128, 128), np.float32)
    perm[partner, m] = 1.0
    sgn = np.where((m % 32) < 16, -1.0, 1.0).astype(np.float32)
    tok = np.arange(L)
    pos = np.where(((m % 64) < 32)[:, None], (tok // 64)[None, :], (tok % 64)[None, :]).astype(np.float32)
    fexp = ((m % 16) / 16.0).astype(np.float32)
    colc = np.zeros((128, 4), np.float32)
    colc[:, 0] = sgn
    colc[:, 1] = fexp
    colc[:, 2] = np.where(m < 64, 1.0, -1.0)
    sel = np.zeros((2, 128), np.float32)
    sel[0, :] = 1.0
    tauA = np.zeros((128, 32, 9), np.float32)
    tauA[:, 0:16, :] = np.arange(9)[None, None, :]
    tauA[:, 16:32, :] = (8 - np.arange(9))[None, None, :]
    tauB = np.zeros((128, 32, 8), np.float32)
    tauB[:, 0:16, :] = (7 - np.arange(8))[None, None, :]
    tauB[:, 16:32, :] = np.arange(8)[None, None, :]
    tauC = np.zeros((128, 32, 8), np.float32)
    tauC[:, :, :] = (8.0 * (np.arange(8) + 1))[None, None, :]
    return {"c_ident": ident, "c_perm": perm, "c_pos": pos, "c_col": colc, "c_sel": sel, "c_tauA": tauA, "c_tauB": tauB,
            "c_tauC": tauC}


class K:
    pass


def build(debug=None):
    nc = bass.Bass("TRN2", target_bir_lowering=False)
    st = ExitStack()
    P = Prog(nc, st)
    k = K()
    k.nc, k.P, k.st = nc, P, st
    k.debug = debug or {}

    def dram_in(name, shape, dt=F32):
        return nc.dram_tensor(name, list(shape), dt, kind="ExternalInput").ap()

    dbg_outs = []

    def dram_scr(name, shape, dt):
        kind = "Internal"
        if debug is not None and name in debug.get("_inject", ()):
            kind = "ExternalInput"
        elif debug is not None and name in debug:
            kind = "ExternalOutput"
            dbg_outs.append(name)
        return nc.dram_tensor(name, list(shape), dt, kind=kind).ap()

    I = {}
    I["x"] = dram_in("x", [L, D])
    I["ctx"] = dram_in("ctx", [LC, D])
    I["cc"] = dram_in("cc", [2, D])
    I["w_ada"] = dram_in("w_ada", [D, 3 * D])
    I["b_ada"] = dram_in("b_ada", [1, 3 * D])
    I["norm_g"] = dram_in("norm_g", [D])
    I["w_in"] = dram_in("w_in", [D, INW])
    I["lam"] = dram_in("lam", [4, 64])
    I["subln_g"] = dram_in("subln_g", [1, 128])
    I["ssm_lre"] = dram_in("ssm_lre", [2, 32, 64])
    I["ssm_lim"] = dram_in("ssm_lim", [2, 32, 64])
    I["ssm_ls"] = dram_in("ssm_ls", [2, 32])
    I["ssm_bre"] = dram_in("ssm_bre", [2, 32, 64, 16])
    I["ssm_bim"] = dram_in("ssm_bim", [2, 32, 64, 16])
    I["ssm_cre"] = dram_in("ssm_cre", [2, 32, 16, 64])
    I["ssm_cim"] = dram_in("ssm_cim", [2, 32, 16, 64])
    I["ssm_d"] = dram_in("ssm_d", [1, 512])
    I["w_glu"] = dram_in("w_glu", [512, 512])
    I["b_glu"] = dram_in("b_glu", [512])
    I["w_pa"] = dram_in("w_pa", [1024, D])
    I["w_ps"] = dram_in("w_ps", [512, D])
    I["w_out"] = dram_in("w_out", [D, D])
    I["final_g"] = dram_in("final_g", [1, D])
    for cn, arr in _consts().items():
        I[cn] = dram_in(cn, arr.shape)
    out = nc.dram_tensor("out", [L, D], F32, kind="ExternalOutput").ap()

    S = {}
    S["modrow"] = dram_scr("modrow", [2, 3 * D], F32)
    S["qT"] = dram_scr("qT", [HEADS, 128, L], BF16)
    S["kT"] = dram_scr("kT", [HEADS, 128, LT], BF16)
    S["v"] = dram_scr("v", [LT, 1024], BF16)
    S["sga"] = dram_scr("sga", [L, 1024], BF16)
    S["u"] = dram_scr("u", [LT, 512], F32)
    S["sgsT"] = dram_scr("sgsT", [512, L], BF16)
    S["sgmT"] = dram_scr("sgmT", [2 * D, L], BF16)
    S["abrT"] = dram_scr("abrT", [1024, L], BF16)
    S["sbrT"] = dram_scr("sbrT", [512, L], BF16)
    S["hT"] = dram_scr("hT_dbg", [128, NKC, LT], BF16) if (debug is not None and "hT_dbg" in debug) else None
    k.I, k.S, k.out = I, S, out
    k.dbg_sem = None

    def dbg(name, shape, dt, ap_fn, bufs):
        if debug is None or name not in debug:
            return
        if name not in S:
            S[name] = nc.dram_tensor(name, list(shape), dt, kind="ExternalOutput").ap()
            dbg_outs.append(name)
        if k.dbg_sem is None:
            k.dbg_sem = P.new_dsem("dbgsem")
        o, i = ap_fn(S[name])
        P.dma("sp", o, i, reads=bufs, sem=k.dbg_sem)
    k.dbg = dbg

    def sb(name, shape, dt, stack=st):
        return stack.enter_context(nc.sbuf_tensor(name, list(shape), dt))

    def ps(name, shape, dt, stack=st):
        return stack.enter_context(nc.psum_tensor(name, list(shape), dt))

    k.sb, k.ps = sb, ps
    ident = sb("ident", [128, 128], F32)
    colc = sb("colc", [128, 4], F32)
    b_ident, b_colc = Buf("ident"), Buf("colc")
    csem = P.new_dsem("csem")
    P.dma("sp", ident[:], I["c_ident"], writes=[b_ident], sem=csem)
    P.dma("sp", colc[:], I["c_col"], writes=[b_colc], sem=csem)
    k.ident, k.b_ident, k.colc, k.b_colc, k.csem = ident, b_ident, colc, b_colc, csem
    k.ssq = sb("ssq", [128, 40], F32)
    k.b_ssq = Buf("ssq")

    phase_adaln(k)
    P.barrier()
    if debug is None or debug.get("_upto", 99) >= 1:
        phase_norm_inproj(k, debug)
        P.barrier()
    if (debug is None or debug.get("_upto", 99) >= 2) and not (debug or {}).get("_skip_ssm"):
        phase_ssm(k)
        P.barrier()
    if debug is None or debug.get("_upto", 99) >= 3:
        phase_attn(k)
        P.barrier()
    if debug is None or debug.get("_upto", 99) >= 4:
        phase_merge(k)
        P.barrier()
    P.emit()
    st.close()
    return nc, dbg_outs


def range_sin(k, stack, out_ap, y_ap, shape, tag, rbufs, wbufs, eng="dve"):
    nc, P = k.nc, k.P
    ki = k.sb(tag + "_ki", shape, I32, stack)
    kf = k.sb(tag + "_kf", shape, F32, stack)
    g = k.sb(tag + "_g", shape, F32, stack)
    bki, bkf, bg = Buf(tag + "ki"), Buf(tag + "kf"), Buf(tag + "g")
    sl = tuple([slice(None)] * len(shape))
    P.op(eng, lambda e: e.tensor_copy(out=ki[sl], in_=y_ap), reads=rbufs, writes=[bki])
    P.op(eng, lambda e: e.tensor_copy(out=kf[sl], in_=ki[sl]), reads=[bki], writes=[bkf])
    P.op(eng, lambda e: e.tensor_tensor(out=kf[sl], in0=y_ap, in1=kf[sl], op=ALU.subtract), reads=rbufs + [bkf], writes=[bkf])
    P.op(eng, lambda e: e.tensor_single_scalar(out=g[sl], in_=kf[sl], scalar=0.5, op=ALU.is_gt), reads=[bkf], writes=[bg])
    P.op(eng, lambda e: e.tensor_tensor(out=kf[sl], in0=kf[sl], in1=g[sl], op=ALU.subtract), reads=[bkf, bg], writes=[bkf])
    P.op(eng, lambda e: e.tensor_single_scalar(out=g[sl], in_=kf[sl], scalar=-0.5, op=ALU.is_lt), reads=[bkf], writes=[bg])
    P.op(eng, lambda e: e.tensor_tensor(out=kf[sl], in0=kf[sl], in1=g[sl], op=ALU.add), reads=[bkf, bg], writes=[bkf])
    P.op("act", lambda e: e.activation(out=out_ap, in_=kf[sl], func=AF.Sin, scale=TWO_PI * (1.0 - 2e-7)), reads=[bkf], writes=wbufs)


def phase_adaln(k):
    nc, P, I, S = k.nc, k.P, k.I, k.S
    with ExitStack() as ls:
        sT = k.sb("ad_sT", [128, NKC, 2], F32, ls)
        b_sT = Buf("sT")
        sem_c = P.new_dsem("ad_c")
        for v in range(2):
            P.dma("sp", sT[:, :, v], I["cc"][v].rearrange("(kc p) -> p kc", p=128), writes=[b_sT], sem=sem_c,
                  allow_slow_non_contiguous=True)
        P.op("act", lambda e: e.activation(out=sT[:], in_=sT[:], func=AF.Silu), reads=[b_sT], writes=[b_sT])
        brow = k.sb("ad_brow", [2, 3 * D], F32, ls)
        b_brow = Buf("brow")
        for v in range(2):
            P.dma("sp", brow[v:v + 1, :], I["b_ada"], writes=[b_brow], sem=sem_c)
        modrow = k.sb("ad_modrow", [2, 3 * D], F32, ls)
        b_modrow = Buf("modrow")
        NS = 2
        wst = [k.sb(f"ad_w{i}", [128, NKC, 512], F32, ls) for i in range(NS)]
        b_w = [Buf(f"adw{i}") for i in range(NS)]
        wsem = [P.new_dsem(f"ad_ws{i}") for i in range(NS)]
        pst = [k.ps(f"ad_ps{i}", [128, 512], F32, ls) for i in range(2)]
        b_ps = [Buf(f"adps{i}") for i in range(2)]
        wv = I["w_ada"].rearrange("(kc p) c -> p kc c", p=128)
        xs1 = [k.sb(f"ad_x{i}", [128, D], F32, ls) for i in range(2)]
        b_xs1 = [Buf(f"adx{i}") for i in range(2)]
        junk1 = k.sb("ad_junk", [128, D], BF16, ls)
        b_junk1 = Buf("adjunk")
        tiles_done = 0

        def ss_tile(t):
            s1 = t % 2
            src = I["x"][t * 128:(t + 1) * 128, :] if t < 16 else I["ctx"][(t - 16) * 128:(t - 15) * 128, :]
            P.dma("pool", xs1[s1][:], src, writes=[b_xs1[s1]])
            P.op("act", lambda e, s1=s1, t=t: e.activation(out=junk1[:], in_=xs1[s1][:], func=AF.Square, accum_out=k.ssq[:, t:t + 1]),
                 reads=[b_xs1[s1]], writes=[b_junk1, k.b_ssq])

        for cb in range(12):
            for _ in range(2 if cb < 6 else 1):
                if tiles_done < 18:
                    ss_tile(tiles_done)
                    tiles_done += 1
            s = cb % NS
            P.dma("sp", wst[s][:, 0:8, :], wv[:, 0:8, cb * 512:(cb + 1) * 512], writes=[b_w[s]], sem=wsem[s])
            P.dma("act", wst[s][:, 8:16, :], wv[:, 8:16, cb * 512:(cb + 1) * 512], writes=[b_w[s]], sem=wsem[s])
            pt, bp = pst[cb % 2], b_ps[cb % 2]
            for kc in range(NKC):
                P.op("pe", lambda e, kc=kc, s=s, pt=pt: e.matmul(pt[0:2, :], lhsT=sT[:, kc, :], rhs=wst[s][:, kc, :],
                                                              start=(kc == 0), stop=(kc == NKC - 1)),
                     reads=[b_sT, b_w[s]], writes=[bp])
            P.op("dve", lambda e, cb=cb, pt=pt: e.tensor_tensor(out=modrow[:, cb * 512:(cb + 1) * 512], in0=pt[0:2, :],
                                                             in1=brow[:, cb * 512:(cb + 1) * 512], op=ALU.add),
                 reads=[bp, b_brow], writes=[b_modrow])
        b_mr = Buf("modrow_d")
        k.b_modrow_d = b_mr
        P.dma("sp", S["modrow"], modrow[:], reads=[b_modrow], writes=[b_mr], sem=sem_c)


def phase_norm_inproj(k, debug):
    nc, P, I, S = k.nc, k.P, k.I, k.S
    with ExitStack() as ls:
        hT = k.sb("hT", [128, NKC, LT], BF16, ls)
        b_hT = [Buf(f"hT{t}") for t in range(18)]
        Amod = k.sb("Amod", [128, NKC, 2], F32, ls)
        Smod = k.sb("Smod", [128, NKC, 2], F32, ls)
        gcol = k.sb("gcol", [128, NKC], F32, ls)
        b_A, b_S, b_g = Buf("Amod"), Buf("Smod"), Buf("gcol")
        msem = P.new_dsem("n_m")
        for v in range(2):
            P.dma("sp", Smod[:, :, v], S["modrow"][v, 0:D].rearrange("(kc p) -> p kc", p=128),
                  reads=[k.b_modrow_d], writes=[b_S], sem=msem, allow_slow_non_contiguous=True)
            P.dma("sp", Amod[:, :, v], S["modrow"][v, D:2 * D].rearrange("(kc p) -> p kc", p=128),
                  reads=[k.b_modrow_d], writes=[b_A], sem=msem, allow_slow_non_contiguous=True)
        P.dma("sp", gcol[:], I["norm_g"].rearrange("(kc p) -> p kc", p=128), writes=[b_g], sem=msem,
              allow_slow_non_contiguous=True)
        for v in range(2):
            P.op("dve", lambda e, v=v: e.scalar_tensor_tensor(out=Amod[:, :, v], in0=Amod[:, :, v], scalar=1.0, in1=gcol[:],
                                                             op0=ALU.add, op1=ALU.mult),
                 reads=[b_A, b_g], writes=[b_A])
        with ExitStack() as l1:
            NX = 2
            xt = [k.sb(f"n_x{i}", [128, D], F32, l1) for i in range(NX)]
            b_x = [Buf(f"nx{i}") for i in range(NX)]
            xsem = [P.new_dsem(f"n_xs{i}") for i in range(NX)]
            junk = k.sb("n_junk", [128, D], BF16, l1)
            b_junk = Buf("junk")
            stat = [k.sb(f"n_st{i}", [128, 4], F32, l1) for i in range(NX)]
            b_stat = [Buf(f"nst{i}") for i in range(NX)]
            pt = [k.ps(f"n_ps{i}", [128, 512], F32, l1) for i in range(4)]
            b_pt = [Buf(f"nps{i}") for i in range(4)]
            pi = 0
            P.op("dve", lambda e: e.tensor_scalar(out=k.ssq[:, 0:18], in0=k.ssq[:, 0:18], scalar1=1.0 / D, scalar2=EPS, op0=ALU.mult, op1=ALU.add),
                 reads=[k.b_ssq], writes=[k.b_ssq])
            P.op("act", lambda e: e.activation(out=k.ssq[:, 0:18], in_=k.ssq[:, 0:18], func=AF.Ln), reads=[k.b_ssq], writes=[k.b_ssq])
            P.op("act", lambda e: e.activation(out=k.ssq[:, 20:38], in_=k.ssq[:, 0:18], func=AF.Exp, scale=-0.5), reads=[k.b_ssq], writes=[k.b_ssq])
            for t in range(18):
                s = t % NX
                v = 0 if t < 16 else 1
                src = I["x"][t * 128:(t + 1) * 128, :] if t < 16 else I["ctx"][(t - 16) * 128:(t - 15) * 128, :]
                P.dma("sp", xt[s][:, 0:1024], src[:, 0:1024], writes=[b_x[s]], sem=xsem[s])
                P.dma("act", xt[s][:, 1024:2048], src[:, 1024:2048], writes=[b_x[s]], sem=xsem[s])
                P.op("dve", lambda e, s=s, t=t: e.tensor_scalar(out=xt[s][:], in0=xt[s][:], scalar1=k.ssq[:, 20 + t:21 + t], scalar2=None,
                                                              op0=ALU.mult),
                     reads=[b_x[s], k.b_ssq], writes=[b_x[s]])
                for g4 in range(4):
                    p_, bp = pt[pi % 4], b_pt[pi % 4]
                    pi += 1
                    for j in range(4):
                        kc = g4 * 4 + j
                        P.op("pe", lambda e, s=s, kc=kc, j=j, p_=p_: e.transpose(out=p_[:, j * 128:(j + 1) * 128],
                                                                             in_=xt[s][:, kc * 128:(kc + 1) * 128],
                                                                             identity=k.ident[:]),
                             reads=[b_x[s], k.b_ident], writes=[bp])
                    for j in range(4):
                        kc = g4 * 4 + j
                        eng = "dve" if (j % 2 == 0) else "act"
                        if eng == "dve":
                            P.op("dve", lambda e, kc=kc, j=j, p_=p_, t=t, v=v: e.tensor_scalar(
                                out=hT[:, kc, t * 128:(t + 1) * 128], in0=p_[:, j * 128:(j + 1) * 128],
                                scalar1=Amod[:, kc, v:v + 1], scalar2=Smod[:, kc, v:v + 1], op0=ALU.mult, op1=ALU.add),
                                reads=[bp, b_A, b_S], writes=[b_hT[t]])
                        else:
                            P.op("act", lambda e, kc=kc, j=j, p_=p_, t=t, v=v: e.activation(
                                out=hT[:, kc, t * 128:(t + 1) * 128], in_=p_[:, j * 128:(j + 1) * 128],
                                func=AF.Identity, scale=Amod[:, kc, v:v + 1], bias=Smod[:, kc, v:v + 1]),
                                reads=[bp, b_A, b_S], writes=[b_hT[t]])
        if S["hT"] is not None:
            dsem = P.new_dsem("dbg")
            P.dma("sp", S["hT"], hT[:], reads=b_hT, writes=[Buf("x")], sem=dsem)
        P.barrier()
        if debug is not None and debug.get("_upto", 99) < 1.5:
            return
        inproj(k, ls, hT, b_hT)


def inproj(k, ls, hT, b_hT):
    nc, P, I, S = k.nc, k.P, k.I, k.S
    cosT = k.sb("cosT", [128, L], F32, ls)
    sinS = k.sb("sinS", [128, L], F32, ls)
    perm = k.sb("perm", [128, 128], F32, ls)
    b_cos, b_sin, b_perm = Buf("cos"), Buf("sin"), Buf("perm")
    tsem = P.new_dsem("ip_t")
    P.dma("sp", perm[:], I["c_perm"], writes=[b_perm], sem=tsem)
    with ExitStack() as l0:
        pos = k.sb("pos", [128, L], F32, l0)
        yv = k.sb("yv", [128, L], F32, l0)
        inv = k.sb("inv", [128, 1], F32, l0)
        b_pos, b_y, b_inv = Buf("pos"), Buf("yv"), Buf("inv")
        P.dma("sp", pos[:], I["c_pos"], writes=[b_pos], sem=tsem)
        P.op("act", lambda e: e.activation(out=inv[:], in_=k.colc[:, 1:2], func=AF.Exp, scale=-math.log(10000.0)),
             reads=[k.b_colc], writes=[b_inv])
        P.op("dve", lambda e: e.tensor_scalar(out=yv[:], in0=pos[:], scalar1=inv[:, 0:1], scalar2=1.0 / TWO_PI,
                                              op0=ALU.mult, op1=ALU.mult), reads=[b_pos, b_inv], writes=[b_y])
        range_sin(k, l0, sinS[:], yv[:], [128, L], "rs1", [b_y], [b_sin])
        P.op("dve", lambda e: e.tensor_scalar(out=sinS[:], in0=sinS[:], scalar1=k.colc[:, 0:1], scalar2=None, op0=ALU.mult),
             reads=[b_sin, k.b_colc], writes=[b_sin])
        P.op("dve", lambda e: e.tensor_scalar(out=yv[:], in0=yv[:], scalar1=0.25, scalar2=None, op0=ALU.add),
             reads=[b_y], writes=[b_y])
        range_sin(k, l0, cosT[:], yv[:], [128, L], "rs2", [b_y], [b_cos])
        P.barrier()
    b_wbq = [[Buf(f"wbq{i}_{j}") for j in range(4)] for i in range(2)]
    wb = [k.sb(f"ip_wb{i}", [128, NKC, 512], BF16, ls) for i in range(2)]
    b_wb = [Buf(f"wb{i}") for i in range(2)]
    NOB = 4
    ob = [k.sb(f"ip_ob{i}", [128, 512], BF16, ls) for i in range(NOB)]
    b_ob = [Buf(f"ob{i}") for i in range(NOB)]
    osem = [P.new_dsem(f"ip_os{i}") for i in range(NOB)]
    NOF = 3
    of = [k.sb(f"ip_of{i}", [128, 512], F32, ls) for i in range(NOF)]
    b_of = [Buf(f"of{i}") for i in range(NOF)]
    fsem = [P.new_dsem(f"ip_fs{i}") for i in range(NOF)]
    t1 = [k.sb(f"ip_t1{i}", [128, 512], F32, ls) for i in range(2)]
    b_t1 = [Buf(f"t1{i}") for i in range(2)]
    t2 = [k.sb(f"ip_t2{i}", [128, 512], F32, ls) for i in range(2)]
    b_t2 = [Buf(f"t2{i}") for i in range(2)]
    pb = [k.ps(f"ip_ps{i}", [128, 512], F32, ls) for i in range(4)]
    b_pb = [Buf(f"ipps{i}") for i in range(4)]
    pr = [k.ps(f"ip_pr{i}", [128, 512], F32, ls) for i in range(2)]
    b_pr = [Buf(f"ippr{i}") for i in range(2)]
    wv = I["w_in"].rearrange("(kc p) c -> p kc c", p=128)
    cnt = {"pb": 0, "ob": 0, "of": 0, "r": 0, "ld": 0, "ev": 0}

    rope_pending = []

    def load_block(cb):
        s2 = cb % 2
        for q4 in range(4):
            P.dma("pool", wb[s2][:, q4 * 4:(q4 + 1) * 4, :], wv[:, q4 * 4:(q4 + 1) * 4, cb * 512:(cb + 1) * 512], writes=[b_wbq[s2][q4]])

    def next_ob():
        i = cnt["ob"] % NOB
        cnt["ob"] += 1
        return i

    def evac_eng():
        cnt["ev"] += 1
        return "act" if cnt["ev"] % 2 else "dve"

    def tiles_of(tok0, n):
        return [b_hT[t] for t in range(tok0 // 128, (tok0 + n) // 128)]

    def fm_unit(cb, fc, tok0, n, kind, row0, dst):
        s2 = cb % 2
        pi = cnt["pb"] % 4
        cnt["pb"] += 1
        pt, bp = pb[pi], b_pb[pi]
        for kc in range(NKC):
            P.op("pe", lambda e, kc=kc: e.matmul(pt[:, 0:n], lhsT=wb[s2][:, kc, fc * 128:(fc + 1) * 128],
                                                 rhs=hT[:, kc, tok0:tok0 + n], start=(kc == 0), stop=(kc == NKC - 1)),
                 reads=[b_wbq[s2][kc // 4]] + tiles_of(tok0, n), writes=[bp])
        while rope_pending:
            rope_pending.pop(0)()
        oi = next_ob()
        if kind == "rope":
            ri = cnt["r"] % 2
            cnt["r"] += 1
            fi = cnt["of"] % NOF
            cnt["of"] += 1
            P.op("act", lambda e: e.activation(out=of[fi][:, 0:n], in_=pt[:, 0:n], func=AF.Copy), reads=[bp], writes=[b_of[fi]])
            P.op("dve", lambda e: e.tensor_tensor(out=t1[ri][:, 0:n], in0=of[fi][:, 0:n], in1=cosT[:, tok0:tok0 + n], op=ALU.mult),
                 reads=[b_of[fi], b_cos], writes=[b_t1[ri]])

            def fin():
                P.op("pe", lambda e: e.matmul(pr[ri][:, 0:n], lhsT=perm[:], rhs=of[fi][:, 0:n], start=True, stop=True),
                     reads=[b_perm, b_of[fi]], writes=[b_pr[ri]])
                P.op("dve", lambda e: e.tensor_tensor(out=t2[ri][:, 0:n], in0=pr[ri][:, 0:n], in1=sinS[:, tok0:tok0 + n], op=ALU.mult),
                     reads=[b_pr[ri], b_sin], writes=[b_t2[ri]])
                P.op("pool", lambda e: e.tensor_tensor(out=ob[oi][:, 0:n], in0=t1[ri][:, 0:n], in1=t2[ri][:, 0:n], op=ALU.add),
                     reads=[b_t1[ri], b_t2[ri]], writes=[b_ob[oi]])
                P.dma("sp", dst, ob[oi][:, 0:n], reads=[b_ob[oi]], sem=osem[oi])
            rope_pending.append(fin)
            return
        elif kind == "copy":
            eg = evac_eng()
            if eg == "act":
                P.op("act", lambda e: e.activation(out=ob[oi][:, 0:n], in_=pt[:, 0:n], func=AF.Copy), reads=[bp], writes=[b_ob[oi]])
            else:
                P.op("dve", lambda e: e.tensor_copy(out=ob[oi][:, 0:n], in_=pt[:, 0:n]), reads=[bp], writes=[b_ob[oi]])
        else:
            fn = AF.Silu if kind == "silu" else AF.Sigmoid
            P.op("act", lambda e: e.activation(out=ob[oi][:, 0:n], in_=pt[:, 0:n], func=fn), reads=[bp], writes=[b_ob[oi]])
        P.dma("sp", dst, ob[oi][:, 0:n], reads=[b_ob[oi]], sem=osem[oi])

    def tm_unit(cb, t, kind, dst):
        s2 = cb % 2
        pi = cnt["pb"] % 4
        cnt["pb"] += 1
        pt, bp = pb[pi], b_pb[pi]
        for kc in range(NKC):
            P.op("pe", lambda e, kc=kc: e.matmul(pt[:], lhsT=hT[:, kc, t * 128:(t + 1) * 128], rhs=wb[s2][:, kc, :],
                                                 start=(kc == 0), stop=(kc == NKC - 1)),
                 reads=[b_wbq[s2][kc // 4], b_hT[t]], writes=[bp])
        while rope_pending:
            rope_pending.pop(0)()
        if kind == "f32":
            fi = cnt["of"] % NOF
            cnt["of"] += 1
            P.op("dve", lambda e: e.tensor_copy(out=of[fi][:], in_=pt[:]), reads=[bp], writes=[b_of[fi]])
            P.dma("sp", dst, of[fi][:], reads=[b_of[fi]], sem=fsem[fi])
            return
        oi = next_ob()
        if kind == "copy":
            eg = evac_eng()
            if eg == "act":
                P.op("act", lambda e: e.activation(out=ob[oi][:], in_=pt[:], func=AF.Copy), reads=[bp], writes=[b_ob[oi]])
            else:
                P.op("dve", lambda e: e.tensor_copy(out=ob[oi][:], in_=pt[:]), reads=[bp], writes=[b_ob[oi]])
        else:
            P.op("act", lambda e: e.activation(out=ob[oi][:], in_=pt[:], func=AF.Silu), reads=[bp], writes=[b_ob[oi]])
        P.dma("sp", dst, ob[oi][:], reads=[b_ob[oi]], sem=osem[oi])

    NCB = INW // 512
    load_block(0)
    for cb in range(NCB):
        if cb + 1 < NCB:
            load_block(cb + 1)
        c0 = cb * 512
        if cb < 2:
            for fc in range(4):
                h = cb * 4 + fc
                for tb in range(4):
                    fm_unit(cb, fc, tb * 512, 512, "rope", 0, S["qT"][h, :, tb * 512:(tb + 1) * 512])
        elif cb < 4:
            for fc in range(4):
                h = (cb - 2) * 4 + fc
                for tb in range(4):
                    fm_unit(cb, fc, tb * 512, 512, "rope", 0, S["kT"][h, :, tb * 512:(tb + 1) * 512])
                fm_unit(cb, fc, L, LC, "copy", 0, S["kT"][h, :, L:LT])
        elif cb < 6:
            for t in range(18):
                tm_unit(cb, t, "copy", S["v"][t * 128:(t + 1) * 128, (cb - 4) * 512:(cb - 3) * 512])
        elif cb < 8:
            for t in range(16):
                tm_unit(cb, t, "silu", S["sga"][t * 128:(t + 1) * 128, (cb - 6) * 512:(cb - 5) * 512])
        elif cb == 8:
            for t in range(18):
                tm_unit(cb, t, "f32", S["u"][t * 128:(t + 1) * 128, :])
        elif cb == 9:
            for fc in range(4):
                for tb in range(4):
                    fm_unit(cb, fc, tb * 512, 512, "silu", 0, S["sgsT"][fc * 128:(fc + 1) * 128, tb * 512:(tb + 1) * 512])
        else:
            for fc in range(4):
                r0 = (cb - 10) * 512 + fc * 128
                for tb in range(4):
                    fm_unit(cb, fc, tb * 512, 512, "sigm", 0, S["sgmT"][r0:r0 + 128, tb * 512:(tb + 1) * 512])


def phase_ssm(k):
    nc, P, I, S = k.nc, k.P, k.I, k.S
    MUL, ADD, SUB = ALU.mult, ALU.add, ALU.subtract
    with ExitStack() as ls:
        ToepT = k.sb("ss_toep", [128, 32, 128], BF16, ls)
        RCp = k.sb("ss_rcp", [128, 2, 2, 16, 256], BF16, ls)
        WT = k.sb("ss_wt", [128, 2, 16, 2, 128], BF16, ls)
        A8c = k.sb("ss_a8c", [128, 2, 16, 2], F32, ls)
        A8s = k.sb("ss_a8s", [128, 2, 16, 2], F32, ls)
        AKc = k.sb("ss_akc", [128, 2, 8, 16, 2], F32, ls)
        AKs = k.sb("ss_aks", [128, 2, 8, 16, 2], F32, ls)
        b_ak = Buf("ak")
        b_toep = [Buf(f"toep{g}") for g in range(32)]
        b_rcp, b_wt = Buf("rcp"), Buf("wt")
        b_U = [Buf(f"U{g}") for g in range(32)]
        b_zbf = [Buf("zbf0"), Buf("zbf1")]
        b_ygT = Buf("ygT")
        b_a8 = Buf("a8")
        pbk = [k.ps(f"ss_ps{i}", [128, 512], F32, ls) for i in range(8)]
        b_pbk = [Buf(f"ssps{i}") for i in range(8)]
        pc = {"i": 0}

        def nb():
            i = pc["i"] % 8
            pc["i"] += 1
            return pbk[i], b_pbk[i]

        csem = P.new_dsem("ss_c")
        with ExitStack() as l0:
            lre = k.sb("ss_lre", [128, 32], F32, l0)
            lim = k.sb("ss_lim", [128, 32], F32, l0)
            dtt = k.sb("ss_dt", [128, 32], F32, l0)
            alog = k.sb("ss_alog", [128, 32], F32, l0)
            th = k.sb("ss_th", [128, 32], F32, l0)
            b_l, b_dt, b_al = Buf("lrelim"), Buf("dtt"), Buf("alogth")
            for gp in range(2):
                for d in range(2):
                    P.dma("sp", lre[gp * 64:(gp + 1) * 64, d * 16:(d + 1) * 16], I["ssm_lre"][d, gp * 16:(gp + 1) * 16, :].rearrange("g p -> p g"),
                          writes=[b_l], sem=csem, allow_slow_non_contiguous=True)
                    P.dma("sp", lim[gp * 64:(gp + 1) * 64, d * 16:(d + 1) * 16], I["ssm_lim"][d, gp * 16:(gp + 1) * 16, :].rearrange("g p -> p g"),
                          writes=[b_l], sem=csem, allow_slow_non_contiguous=True)
                    P.dma("sp", dtt[gp * 64:(gp + 1) * 64, d * 16:(d + 1) * 16], I["ssm_ls"][d:d + 1, gp * 16:(gp + 1) * 16].broadcast_to([64, 16]),
                          writes=[b_dt], sem=csem)
            P.op("act", lambda e: e.activation(out=dtt[:], in_=dtt[:], func=AF.Exp), reads=[b_dt], writes=[b_dt])
            P.op("dve", lambda e: e.tensor_tensor(out=alog[:], in0=lre[:], in1=dtt[:], op=MUL), reads=[b_l, b_dt], writes=[b_al])
            P.op("dve", lambda e: e.scalar_tensor_tensor(out=th[:], in0=lim[:], scalar=1.0 / TWO_PI, in1=dtt[:], op0=MUL, op1=MUL),
                 reads=[b_l, b_dt], writes=[b_al])
            tabs = {}
            for nm, n in (("A", 9), ("B", 8), ("C", 8)):
                tau = k.sb(f"ss_tau{nm}", [128, 32, n], F32, l0)
                ex = k.sb(f"ss_ex{nm}", [128, 32, n], F32, l0)
                yv = k.sb(f"ss_yv{nm}", [128, 32, n], F32, l0)
                sn = k.sb(f"ss_sn{nm}", [128, 32, n], F32, l0)
                cs = k.sb(f"ss_cs{nm}", [128, 32, n], F32, l0)
                b_tau, b_ex, b_yv, b_sn, b_cs = Buf("tau" + nm), Buf("ex" + nm), Buf("yv" + nm), Buf("sn" + nm), Buf("cs" + nm)
                P.dma("sp", tau[:], I["c_tau" + nm], writes=[b_tau], sem=csem)
                P.op("dve", lambda e, ex=ex, tau=tau, n=n: e.tensor_tensor(out=ex[:], in0=tau[:], in1=alog[:, :, None].broadcast_to([128, 32, n]), op=MUL),
                     reads=[b_tau, b_al], writes=[b_ex])
                P.op("act", lambda e, ex=ex: e.activation(out=ex[:], in_=ex[:], func=AF.Exp), reads=[b_ex], writes=[b_ex])
                P.op("dve", lambda e, yv=yv, tau=tau, n=n: e.tensor_tensor(out=yv[:], in0=tau[:], in1=th[:, :, None].broadcast_to([128, 32, n]), op=MUL),
                     reads=[b_tau, b_al], writes=[b_yv])
                fl = lambda t: t[:].rearrange("p a b -> p (a b)")
                range_sin(k, l0, fl(sn), fl(yv), [128, 32 * n], "ssr1" + nm, [b_yv], [b_sn])
                P.op("dve", lambda e, yv=yv: e.tensor_scalar(out=yv[:], in0=yv[:], scalar1=0.25, scalar2=None, op0=ADD), reads=[b_yv], writes=[b_yv])
                range_sin(k, l0, fl(cs), fl(yv), [128, 32 * n], "ssr2" + nm, [b_yv], [b_cs])
                P.op("dve", lambda e, cs=cs, ex=ex: e.tensor_tensor(out=cs[:], in0=cs[:], in1=ex[:], op=MUL), reads=[b_cs, b_ex], writes=[b_cs])
                P.op("dve", lambda e, sn=sn, ex=ex: e.tensor_tensor(out=sn[:], in0=sn[:], in1=ex[:], op=MUL), reads=[b_sn, b_ex], writes=[b_sn])
                tabs[nm] = (cs, sn, b_cs, b_sn)
            ARA, AIA, b_ARA, b_AIA = tabs["A"]
            ARB, AIB, b_ARB, b_AIB = tabs["B"]
            ARC, AIC, b_ARC, b_AIC = tabs["C"]
            for d in range(2):
                dsl = slice(d * 16, (d + 1) * 16)
                for ri in range(2):
                    P.op("dve", lambda e, d=d, ri=ri, dsl=dsl: e.tensor_copy(out=AKc[:, d, :, :, ri], in_=ARC[:, dsl, :].rearrange("p g k -> p k g")),
                         reads=[b_ARC], writes=[b_ak])
                P.op("dve", lambda e, d=d, dsl=dsl: e.tensor_scalar(out=AKs[:, d, :, :, 0], in0=AIC[:, dsl, :].rearrange("p g k -> p k g"),
                                                                   scalar1=-1.0, scalar2=None, op0=MUL), reads=[b_AIC], writes=[b_ak])
                P.op("dve", lambda e, d=d, dsl=dsl: e.tensor_copy(out=AKs[:, d, :, :, 1], in_=AIC[:, dsl, :].rearrange("p g k -> p k g")),
                     reads=[b_AIC], writes=[b_ak])
            a1 = k.sb("ss_a1", [128, 2, 32], F32, l0)
            b_a1 = Buf("a1")
            for d in range(2):
                i8 = 8 if d == 0 else 0
                i1 = 1 if d == 0 else 7
                dsl = slice(d * 16, (d + 1) * 16)
                for ri in range(2):
                    P.op("dve", lambda e, d=d, ri=ri, i8=i8, dsl=dsl: e.tensor_copy(out=A8c[:, d, :, ri], in_=ARA[:, dsl, i8]), reads=[b_ARA], writes=[b_a8])
                P.op("dve", lambda e, d=d, i8=i8, dsl=dsl: e.tensor_scalar(out=A8s[:, d, :, 0], in0=AIA[:, dsl, i8], scalar1=-1.0, scalar2=None, op0=MUL),
                     reads=[b_AIA], writes=[b_a8])
                P.op("dve", lambda e, d=d, i8=i8, dsl=dsl: e.tensor_copy(out=A8s[:, d, :, 1], in_=AIA[:, dsl, i8]), reads=[b_AIA], writes=[b_a8])
                P.op("dve", lambda e, d=d, i1=i1, dsl=dsl: e.tensor_copy(out=a1[:, 0, dsl], in_=ARA[:, dsl, i1]), reads=[b_ARA], writes=[b_a1])
                P.op("dve", lambda e, d=d, i1=i1, dsl=dsl: e.tensor_copy(out=a1[:, 1, dsl], in_=AIA[:, dsl, i1]), reads=[b_AIA], writes=[b_a1])
            fz = k.sb("ss_fz", [128, 6, 32], F32, l0)
            b_fz = Buf("fz")
            P.op("dve", lambda e: e.tensor_tensor(out=fz[:, 0, :], in0=lre[:], in1=lre[:], op=MUL), reads=[b_l], writes=[b_fz])
            P.op("dve", lambda e: e.tensor_tensor(out=fz[:, 1, :], in0=lim[:], in1=lim[:], op=MUL), reads=[b_l], writes=[b_fz])
            P.op("dve", lambda e: e.tensor_tensor(out=fz[:, 0, :], in0=fz[:, 0, :], in1=fz[:, 1, :], op=ADD), reads=[b_fz], writes=[b_fz])
            P.op("dve", lambda e: e.reciprocal(out=fz[:, 1, :], in_=fz[:, 0, :]), reads=[b_fz], writes=[b_fz])
            P.op("dve", lambda e: e.tensor_scalar(out=fz[:, 0, :], in0=a1[:, 0, :], scalar1=-1.0, scalar2=None, op0=ADD), reads=[b_a1], writes=[b_fz])
            P.op("dve", lambda e: e.tensor_tensor(out=fz[:, 2, :], in0=fz[:, 0, :], in1=lre[:], op=MUL), reads=[b_fz, b_l], writes=[b_fz])
            P.op("dve", lambda e: e.tensor_tensor(out=fz[:, 3, :], in0=a1[:, 1, :], in1=lim[:], op=MUL), reads=[b_a1, b_l], writes=[b_fz])
            P.op("dve", lambda e: e.tensor_tensor(out=fz[:, 2, :], in0=fz[:, 2, :], in1=fz[:, 3, :], op=ADD), reads=[b_fz], writes=[b_fz])
            P.op("dve", lambda e: e.tensor_tensor(out=fz[:, 2, :], in0=fz[:, 2, :], in1=fz[:, 1, :], op=MUL), reads=[b_fz], writes=[b_fz])
            P.op("dve", lambda e: e.tensor_tensor(out=fz[:, 4, :], in0=a1[:, 1, :], in1=lre[:], op=MUL), reads=[b_a1, b_l], writes=[b_fz])
            P.op("dve", lambda e: e.tensor_tensor(out=fz[:, 5, :], in0=fz[:, 0, :], in1=lim[:], op=MUL), reads=[b_fz, b_l], writes=[b_fz])
            P.op("dve", lambda e: e.tensor_tensor(out=fz[:, 4, :], in0=fz[:, 4, :], in1=fz[:, 5, :], op=SUB), reads=[b_fz], writes=[b_fz])
            P.op("dve", lambda e: e.tensor_tensor(out=fz[:, 4, :], in0=fz[:, 4, :], in1=fz[:, 1, :], op=MUL), reads=[b_fz], writes=[b_fz])
            BT = k.sb("ss_BT", [128, 2, 2, 16, 16], F32, l0)
            BB = k.sb("ss_BB", [128, 2, 2, 16, 16], F32, l0)
            CN = k.sb("ss_CN", [128, 2, 2, 2, 128], F32, l0)
            CT = k.sb("ss_CT", [128, 2, 2, 16, 16], F32, l0)
            tA = k.sb("ss_tA", [128, 16, 9, 16], F32, l0)
            tB = k.sb("ss_tB", [128, 16, 9, 16], F32, l0)
            b_BT, b_BB, b_CN, b_CT, b_tA, b_tB = Buf("BT"), Buf("BB"), Buf("CN"), Buf("CT"), Buf("tA"), Buf("tB")
            for d in range(2):
                for ri in range(2):
                    bsrc = I["ssm_bre"] if ri == 0 else I["ssm_bim"]
                    csrc = I["ssm_cre"] if ri == 0 else I["ssm_cim"]
                    for gp in range(2):
                        P.dma("sp", BT[gp * 64:(gp + 1) * 64, d, ri, :, :], bsrc[d, gp * 16:(gp + 1) * 16].rearrange("g p c -> p g c"),
                              writes=[b_BT], sem=csem)
                        for blk in range(2):
                            g0 = gp * 16 + blk * 8
                            P.dma("sp", CN[:, d, ri, blk, gp * 64:(gp + 1) * 64], csrc[d, g0:g0 + 8].rearrange("g c p -> (g c) p"),
                                  writes=[b_CN], sem=csem)
            for d in range(2):
                for ri in range(2):
                    for blk in range(2):
                        pt, bp = nb()
                        P.op("pe", lambda e, d=d, ri=ri, blk=blk, pt=pt: e.transpose(out=pt[:, 0:128], in_=CN[:, d, ri, blk, :], identity=k.ident[:]),
                             reads=[b_CN, k.b_ident], writes=[bp])
                        P.op("dve", lambda e, d=d, ri=ri, blk=blk, pt=pt: e.tensor_copy(
                            out=CT[:, d, ri, blk * 8:(blk + 1) * 8, :].rearrange("p a b -> p (a b)"), in_=pt[:, 0:128]), reads=[bp], writes=[b_CT])
            for d in range(2):
                dsl = slice(d * 16, (d + 1) * 16)
                frb = lambda d=d, dsl=dsl: fz[:, 2, dsl][:, :, None].broadcast_to([128, 16, 16])
                fib = lambda d=d, dsl=dsl: fz[:, 4, dsl][:, :, None].broadcast_to([128, 16, 16])
                t16a = tA[:, :, 0, :]
                t16b = tB[:, :, 0, :]
                P.op("dve", lambda e, d=d, frb=frb: e.tensor_tensor(out=t16a, in0=BT[:, d, 0], in1=frb(), op=MUL), reads=[b_BT, b_fz], writes=[b_tA])
                P.op("dve", lambda e, d=d, fib=fib: e.tensor_tensor(out=t16b, in0=BT[:, d, 1], in1=fib(), op=MUL), reads=[b_BT, b_fz], writes=[b_tB])
                P.op("dve", lambda e, d=d: e.tensor_tensor(out=BB[:, d, 0], in0=t16a, in1=t16b, op=SUB), reads=[b_tA, b_tB], writes=[b_BB])
                P.op("dve", lambda e, d=d, frb=frb: e.tensor_tensor(out=t16a, in0=BT[:, d, 1], in1=frb(), op=MUL), reads=[b_BT, b_fz], writes=[b_tA])
                P.op("dve", lambda e, d=d, fib=fib: e.tensor_tensor(out=t16b, in0=BT[:, d, 0], in1=fib(), op=MUL), reads=[b_BT, b_fz], writes=[b_tB])
                P.op("dve", lambda e, d=d: e.tensor_tensor(out=BB[:, d, 1], in0=t16a, in1=t16b, op=ADD), reads=[b_tA, b_tB], writes=[b_BB])
            P.op("pool", lambda e: e.memset(RCp[:].rearrange("p a b c d -> p (a b c d)"), 0.0), writes=[b_rcp])
            for d in range(2):
                dsl = slice(d * 16, (d + 1) * 16)
                off = 112 if d == 0 else 0
                bc_c = lambda ri, d=d: CT[:, d, ri][:, :, None, :].broadcast_to([128, 16, 9, 16])
                bc_ar = lambda dsl=dsl: ARA[:, dsl, :][:, :, :, None].broadcast_to([128, 16, 9, 16])
                bc_ai = lambda dsl=dsl: AIA[:, dsl, :][:, :, :, None].broadcast_to([128, 16, 9, 16])
                dst = lambda ri, d=d, off=off: RCp[:, d, ri, :, off:off + 144].rearrange("p g (t c) -> p g t c", c=16)
                P.op("dve", lambda e, bc_c=bc_c, bc_ar=bc_ar: e.tensor_tensor(out=tA[:], in0=bc_c(0), in1=bc_ar(), op=MUL), reads=[b_CT, b_ARA], writes=[b_tA])
                P.op("dve", lambda e, bc_c=bc_c, bc_ai=bc_ai: e.tensor_tensor(out=tB[:], in0=bc_c(1), in1=bc_ai(), op=MUL), reads=[b_CT, b_AIA], writes=[b_tB])
                P.op("dve", lambda e, dst=dst: e.tensor_tensor(out=dst(0), in0=tA[:], in1=tB[:], op=SUB), reads=[b_tA, b_tB], writes=[b_rcp])
                P.op("dve", lambda e, bc_c=bc_c, bc_ai=bc_ai: e.tensor_tensor(out=tA[:], in0=bc_c(0), in1=bc_ai(), op=MUL), reads=[b_CT, b_AIA], writes=[b_tA])
                P.op("dve", lambda e, bc_c=bc_c, bc_ar=bc_ar: e.tensor_tensor(out=tB[:], in0=bc_c(1), in1=bc_ar(), op=MUL), reads=[b_CT, b_ARA], writes=[b_tB])
                P.op("dve", lambda e: e.tensor_tensor(out=tA[:], in0=tA[:], in1=tB[:], op=ADD), reads=[b_tA, b_tB], writes=[b_tA])
                P.op("dve", lambda e, dst=dst: e.tensor_scalar(out=dst(1), in0=tA[:], scalar1=-1.0, scalar2=None, op0=MUL), reads=[b_tA], writes=[b_rcp])
            Lp = k.sb("ss_Lp", [128, 64, 240], BF16, l0)
            b_Lp = Buf("Lp")
            P.op("pool", lambda e: e.memset(Lp[:].rearrange("p a b -> p (a b)"), 0.0), writes=[b_Lp])
            P.op("pool", lambda e: e.tensor_copy(out=Lp[:, :, 112:128], in_=BB[:].rearrange("p a b c d -> p (a b c) d")), reads=[b_BB], writes=[b_Lp])
            BW = k.sb("ss_BW", [128, 2, 2, 16, 128], F32, l0)
            b_BW = Buf("BW")
            for d in range(2):
                dsl = slice(d * 16, (d + 1) * 16)
                bc_b = lambda ri, d=d: BB[:, d, ri][:, :, None, :].broadcast_to([128, 16, 8, 16])
                bc_ar = lambda dsl=dsl: ARB[:, dsl, :][:, :, :, None].broadcast_to([128, 16, 8, 16])
                bc_ai = lambda dsl=dsl: AIB[:, dsl, :][:, :, :, None].broadcast_to([128, 16, 8, 16])
                dst = lambda ri, d=d: BW[:, d, ri].rearrange("p g (t c) -> p g t c", c=16)
                ta8 = tA[:, :, 0:8, :]
                tb8 = tB[:, :, 0:8, :]
                P.op("dve", lambda e, bc_b=bc_b, bc_ar=bc_ar: e.tensor_tensor(out=ta8, in0=bc_b(0), in1=bc_ar(), op=MUL), reads=[b_BB, b_ARB], writes=[b_tA])
                P.op("dve", lambda e, bc_b=bc_b, bc_ai=bc_ai: e.tensor_tensor(out=tb8, in0=bc_b(1), in1=bc_ai(), op=MUL), reads=[b_BB, b_AIB], writes=[b_tB])
                P.op("dve", lambda e, dst=dst: e.tensor_tensor(out=dst(0), in0=ta8, in1=tb8, op=SUB), reads=[b_tA, b_tB], writes=[b_BW])
                P.op("dve", lambda e, bc_b=bc_b, bc_ai=bc_ai: e.tensor_tensor(out=ta8, in0=bc_b(0), in1=bc_ai(), op=MUL), reads=[b_BB, b_AIB], writes=[b_tA])
                P.op("dve", lambda e, bc_b=bc_b, bc_ar=bc_ar: e.tensor_tensor(out=tb8, in0=bc_b(1), in1=bc_ar(), op=MUL), reads=[b_BB, b_ARB], writes=[b_tB])
                P.op("dve", lambda e, dst=dst: e.tensor_tensor(out=dst(1), in0=ta8, in1=tb8, op=ADD), reads=[b_tA, b_tB], writes=[b_BW])
            for d in range(2):
                for g2 in range(16):
                    for ri in range(2):
                        pt, bp = nb()
                        P.op("pe", lambda e, d=d, g2=g2, ri=ri, pt=pt: e.transpose(out=pt[:, 0:128], in_=BW[:, d, ri, g2, :], identity=k.ident[:]),
                             reads=[b_BW, k.b_ident], writes=[bp])
                        eng = "act" if (g2 + ri) % 2 else "dve"
                        if eng == "act":
                            P.op("act", lambda e, d=d, g2=g2, ri=ri, pt=pt: e.activation(out=WT[:, d, g2, ri, :], in_=pt[:, 0:128], func=AF.Copy), reads=[bp], writes=[b_wt])
                        else:
                            P.op("dve", lambda e, d=d, g2=g2, ri=ri, pt=pt: e.tensor_copy(out=WT[:, d, g2, ri, :], in_=pt[:, 0:128]), reads=[bp], writes=[b_wt])
            for g2 in range(16):
                for gp in range(2):
                    g = gp * 16 + g2
                    pt, bp = nb()
                    psl = slice(gp * 64, (gp + 1) * 64)
                    n = 0
                    for d in range(2):
                        for ri in range(2):
                            for s_ in range(8):
                                w0 = (7 - s_) * 16 if d == 0 else (8 - s_) * 16
                                l0_ = (7 - s_) * 16
                                P.op("pe", lambda e, d=d, ri=ri, g2=g2, w0=w0, l0_=l0_, psl=psl, pt=pt, n=n: e.matmul(
                                    pt[:, 0:128], lhsT=Lp[psl, (d * 2 + ri) * 16 + g2, l0_:l0_ + 128], rhs=RCp[psl, d, ri, g2, w0:w0 + 128],
                                    start=(n == 0), stop=(n == 31)), reads=[b_Lp, b_rcp], writes=[bp])
                                n += 1
                    P.op("dve" if g % 2 else "act",
                         (lambda e, g=g, pt=pt: e.tensor_copy(out=ToepT[:, g, :], in_=pt[:, 0:128])) if g % 2 else
                         (lambda e, g=g, pt=pt: e.activation(out=ToepT[:, g, :], in_=pt[:, 0:128], func=AF.Copy)),
                         reads=[bp], writes=[b_toep[g]])
            P.barrier()
        Ubuf = k.sb("ss_ubuf", [128, 32, 320], BF16, ls)
        Zbf = k.sb("ss_zbf", [128, 2, 16, 2, 288], BF16, ls)
        if k.debug.get("_ssm_upto", 99) < 1:
            return
        with ExitStack() as l1:
            ucm = [k.sb(f"ss_ucm{i}", [128, 8, 512], F32, l1) for i in range(2)]
            b_ucm = [Buf(f"ucm{i}") for i in range(2)]
            usem = [P.new_dsem(f"ss_us{i}") for i in range(2)]
            ucg = k.sb("ss_ucg", [128, 32, 128], F32, l1)
            b_ucg = Buf("ucg")
            for jt in range(3):
                si = jt % 2
                nj = 128 if jt < 2 else 32
                r0 = jt * 1024
                P.dma("sp", ucm[si][0:nj], S["u"][r0:r0 + nj * 8, :].rearrange("(j s) c -> j s c", s=8), writes=[b_ucm[si]], sem=usem[si])
                P.op("dve", lambda e, si=si, nj=nj: e.tensor_copy(out=ucg[0:nj].rearrange("p g (s c) -> p g s c", c=16),
                                                                 in_=ucm[si][0:nj].rearrange("p s (g c) -> p g s c", c=16)),
                     reads=[b_ucm[si]], writes=[b_ucg])
                for g0 in range(0, 32, 4):
                    pt, bp = nb()
                    for gg in range(4):
                        g = g0 + gg
                        P.op("pe", lambda e, si=si, nj=nj, g=g, gg=gg, pt=pt: e.transpose(
                            out=pt[:, gg * 128:gg * 128 + nj], in_=ucg[0:nj, g, :], identity=k.ident[0:nj, 0:nj]),
                            reads=[b_ucg, k.b_ident], writes=[bp])
                    src = lambda pt=pt, nj=nj: pt[:].rearrange("p (a b) -> p a b", b=128)[:, :, 0:nj]
                    cols = [32 + jt * 128] if jt < 2 else [0, 288]
                    for ci, c0 in enumerate(cols):
                        eng = "act" if (g0 // 4 + ci) % 2 else "dve"
                        if eng == "act":
                            P.op("act", lambda e, g0=g0, c0=c0, nj=nj, src=src: e.activation(out=Ubuf[:, g0:g0 + 4, c0:c0 + nj], in_=src(), func=AF.Copy),
                                 reads=[bp], writes=[b_U[g0 + i] for i in range(4)])
                        else:
                            P.op("dve", lambda e, g0=g0, c0=c0, nj=nj, src=src: e.tensor_copy(out=Ubuf[:, g0:g0 + 4, c0:c0 + nj], in_=src()),
                                 reads=[bp], writes=[b_U[g0 + i] for i in range(4)])
            P.barrier()
        if k.debug.get("_ssm_upto", 99) < 2:
            return
        with ExitStack() as l2:
            Z = [k.sb(f"ss_Z{d}", [128, 16, 2, 288], F32, l2) for d in range(2)]
            b_Z = [Buf("Z0"), Buf("Z1")]
            for d in range(2):
                j0 = 0 if d == 0 else 32
                for g2 in range(16):
                    for ri in range(2):
                        pt, bp = nb()
                        for gp in range(2):
                            P.op("pe", lambda e, d=d, g2=g2, ri=ri, gp=gp, pt=pt, j0=j0: e.matmul(
                                pt[gp * 64:(gp + 1) * 64, 0:288], lhsT=WT[:, d, g2, ri, gp * 64:(gp + 1) * 64], rhs=Ubuf[:, gp * 16 + g2, j0:j0 + 288],
                                start=True, stop=True), reads=[b_wt, b_U[gp * 16 + g2]], writes=[bp])
                        if (g2 + ri) % 2:
                            P.op("act", lambda e, d=d, g2=g2, ri=ri, pt=pt: e.activation(out=Z[d][:, g2, ri, :], in_=pt[:, 0:288], func=AF.Copy), reads=[bp], writes=[b_Z[d]])
                        else:
                            P.op("dve", lambda e, d=d, g2=g2, ri=ri, pt=pt: e.tensor_copy(out=Z[d][:, g2, ri, :], in_=pt[:, 0:288]), reads=[bp], writes=[b_Z[d]])
            k.dbg("V_dbg", [2, 128, 16 * 2 * 288], F32, lambda dd: (dd[0], Z[0][:].rearrange("p a b c -> p (a b c)")), [b_Z[0]])
            k.dbg("V_dbg", [2, 128, 16 * 2 * 288], F32, lambda dd: (dd[1], Z[1][:].rearrange("p a b c -> p (a b c)")), [b_Z[1]])
            m1 = [k.sb(f"ss_m1{d}", [128, 16, 2, 36], F32, l2) for d in range(2)]
            m2 = [k.sb(f"ss_m2{d}", [128, 16, 2, 36], F32, l2) for d in range(2)]
            b_m1 = [Buf("m10"), Buf("m11")]
            b_m2 = [Buf("m20"), Buf("m21")]
            Zv = [Z[d][:].rearrange("p g r (K c) -> p g r K c", c=8) for d in range(2)]

            def cmul_add(eng, d, kidx, dst, src, src_sw, nK):
                if nK:
                    ac = AKc[:, d, kidx][:, :, :, None].broadcast_to([128, 16, 2, nK])
                    as_ = AKs[:, d, kidx][:, :, :, None].broadcast_to([128, 16, 2, nK])
                    t1_, t2_ = m1[d][:, :, :, 0:nK], m2[d][:, :, :, 0:nK]
                else:
                    ac, as_ = AKc[:, d, kidx], AKs[:, d, kidx]
                    t1_, t2_ = m1[d][:, :, :, 0], m2[d][:, :, :, 0]
                P.op(eng, lambda e: e.tensor_tensor(out=t1_, in0=src, in1=ac, op=MUL), reads=[b_Z[d], b_ak], writes=[b_m1[d]])
                P.op(eng, lambda e: e.tensor_tensor(out=t2_, in0=src_sw, in1=as_, op=MUL), reads=[b_Z[d], b_ak], writes=[b_m2[d]])
                P.op(eng, lambda e: e.tensor_tensor(out=t1_, in0=t1_, in1=t2_, op=ADD), reads=[b_m1[d], b_m2[d]], writes=[b_m1[d]])
                P.op(eng, lambda e: e.tensor_tensor(out=dst, in0=dst, in1=t1_, op=ADD), reads=[b_Z[d], b_m1[d]], writes=[b_Z[d]])

            for i in range(1, 8):
                r = i
                cmul_add("dve", 0, 0, Zv[0][:, :, :, :, r], Zv[0][:, :, :, :, r - 1], Zv[0][:, :, ::-1, :, r - 1], 36)
                r = 7 - i
                cmul_add("dve", 1, 0, Zv[1][:, :, :, :, r], Zv[1][:, :, :, :, r + 1], Zv[1][:, :, ::-1, :, r + 1], 36)
            for i in range(1, 36):
                K = i
                cmul_add("dve", 0, 7, Zv[0][:, :, :, K, 7], Zv[0][:, :, :, K - 1, 7], Zv[0][:, :, ::-1, K - 1, 7], 0)
                K = 35 - i
                cmul_add("pool", 1, 7, Zv[1][:, :, :, K, 0], Zv[1][:, :, :, K + 1, 0], Zv[1][:, :, ::-1, K + 1, 0], 0)
            for i in range(7):
                r = i
                cmul_add("dve", 0, r, Zv[0][:, :, :, 1:36, r], Zv[0][:, :, :, 0:35, 7], Zv[0][:, :, ::-1, 0:35, 7], 35)
                r = 7 - i
                cmul_add("dve", 1, 7 - r, Zv[1][:, :, :, 0:35, r], Zv[1][:, :, :, 1:36, 0], Zv[1][:, :, ::-1, 1:36, 0], 35)
            for d in range(2):
                eng = "dve" if d == 0 else "pool"
                P.op(eng, lambda e, d=d: e.tensor_copy(out=Zbf[:, d].rearrange("p a b c -> p (a b c)"), in_=Z[d][:].rearrange("p a b c -> p (a b c)")),
                     reads=[b_Z[d]], writes=[b_zbf[d]])
            k.dbg("Z_dbg", [2, 128, 16 * 2 * 288], F32, lambda dd: (dd[0], Z[0][:].rearrange("p a b c -> p (a b c)")), [b_Z[0]])
            k.dbg("Z_dbg", [2, 128, 16 * 2 * 288], F32, lambda dd: (dd[1], Z[1][:].rearrange("p a b c -> p (a b c)")), [b_Z[1]])
            P.barrier()
        if k.debug.get("_ssm_upto", 99) < 3:
            return
        ygT = k.sb("ss_ygT", [128, 4, L], BF16, ls)
        with ExitStack() as l3:
            ycm = k.sb("ss_ycm", [128, 8, 512], F32, l3)
            b_ycm = [Buf(f"ycm{g}") for g in range(32)]
            ut = k.sb("ss_ut", [128, 8, 512], F32, l3)
            b_ut = Buf("ut")
            utsem = P.new_dsem("ss_uts")
            Dfull = k.sb("ss_D", [128, 512], F32, l3)
            b_D = Buf("Dfull")
            P.dma("sp", Dfull[:], I["ssm_d"][0:1, :].broadcast_to([128, 512]), writes=[b_D], sem=csem)
            sq = [k.sb(f"ss_sq{i}", [128, 512], F32, l3) for i in range(2)]
            b_sq = [Buf("sq0"), Buf("sq1")]
            GC = math.sqrt(2.0 / math.pi)
            for jt in range(2):
                P.dma("sp", ut[:], S["u"][jt * 1024:(jt + 1) * 1024, :].rearrange("(j s) c -> j s c", s=8), writes=[b_ut], sem=utsem)
                for g in range(32):
                    gp, g2 = g // 16, g % 16
                    psl = slice(gp * 64, (gp + 1) * 64)
                    pt, bp = nb()
                    c0 = 32 + jt * 128
                    P.op("pe", lambda e, g=g, c0=c0, pt=pt: e.matmul(pt[:, 0:128], lhsT=Ubuf[:, g, c0:c0 + 128], rhs=ToepT[:, g, :], start=True, stop=False),
                         reads=[b_U[g], b_toep[g]], writes=[bp])
                    for d in range(2):
                        jz = (31 + jt * 128) if d == 0 else (1 + jt * 128)
                        w0 = 128 if d == 0 else 0
                        for ri in range(2):
                            last = (d == 1 and ri == 1)
                            P.op("pe", lambda e, d=d, ri=ri, g2=g2, psl=psl, jz=jz, w0=w0, pt=pt, last=last: e.matmul(
                                pt[:, 0:128], lhsT=Zbf[psl, d, g2, ri, jz:jz + 128], rhs=RCp[psl, d, ri, g2, w0:w0 + 128], start=False, stop=last),
                                reads=[b_zbf[d], b_rcp], writes=[bp])
                    src = lambda pt=pt: pt[:, 0:128].rearrange("p (t c) -> p t c", c=16)
                    P.op("dve", lambda e, g=g, src=src: e.tensor_tensor(out=ycm[:, :, g * 16:(g + 1) * 16], in0=ut[:, :, g * 16:(g + 1) * 16],
                                                                       in1=Dfull[:, g * 16:(g + 1) * 16][:, None, :].broadcast_to([128, 8, 16]), op=MUL),
                         reads=[b_ut, b_D], writes=[b_ycm[g]])
                    P.op("dve", lambda e, g=g, src=src: e.tensor_tensor(out=ycm[:, :, g * 16:(g + 1) * 16], in0=ycm[:, :, g * 16:(g + 1) * 16], in1=src(), op=ADD),
                         reads=[bp, b_ycm[g]], writes=[b_ycm[g]])
                k.dbg("y_dbg", [L, 512], F32, lambda dd, jt=jt: (dd[jt * 1024:(jt + 1) * 1024, :].rearrange("(j s) c -> j s c", s=8), ycm[:]), b_ycm)
                for t in range(8):
                    i = t % 2
                    P.op("dve", lambda e, t=t, i=i: e.tensor_tensor(out=sq[i][:], in0=ycm[:, t, :], in1=ycm[:, t, :], op=MUL), reads=b_ycm, writes=[b_sq[i]])
                    P.op("dve", lambda e, t=t, i=i: e.tensor_scalar(out=sq[i][:], in0=sq[i][:], scalar1=0.044715, scalar2=1.0, op0=MUL, op1=ADD), reads=[b_sq[i]], writes=[b_sq[i]])
                    P.op("dve", lambda e, t=t, i=i: e.tensor_tensor(out=sq[i][:], in0=sq[i][:], in1=ycm[:, t, :], op=MUL), reads=[b_sq[i]] + b_ycm, writes=[b_sq[i]])
                    P.op("act", lambda e, t=t, i=i: e.activation(out=sq[i][:], in_=sq[i][:], func=AF.Sigmoid, scale=2.0 * GC), reads=[b_sq[i]], writes=[b_sq[i]])
                    P.op("dve", lambda e, t=t, i=i: e.tensor_tensor(out=sq[i][:], in0=sq[i][:], in1=ycm[:, t, :], op=MUL), reads=[b_sq[i]] + b_ycm, writes=[b_sq[i]])
                    pt, bp = nb()
                    for chb in range(4):
                        P.op("pe", lambda e, i=i, chb=chb, pt=pt: e.transpose(out=pt[:, chb * 128:(chb + 1) * 128], in_=sq[i][:, chb * 128:(chb + 1) * 128], identity=k.ident[:]),
                             reads=[b_sq[i], k.b_ident], writes=[bp])
                    tsl = slice(jt * 1024 + t, (jt + 1) * 1024, 8)
                    P.op("act", lambda e, pt=pt, tsl=tsl: e.activation(out=ygT[:, :, tsl], in_=pt[:].rearrange("p (a b) -> p a b", b=128), func=AF.Copy),
                         reads=[bp], writes=[b_ygT])
            P.barrier()
        if k.debug.get("_ssm_upto", 99) < 4:
            return
        with ExitStack() as l4:
            wg32 = k.sb("ss_wg32", [128, 4, 512], F32, l4)
            wg = k.sb("ss_wg", [128, 4, 512], BF16, l4)
            bg = k.sb("ss_bg", [128, 4], F32, l4)
            b_wg32, b_wg, b_bg = Buf("wg32"), Buf("wg"), Buf("bg")
            P.dma("sp", wg32[:], I["w_glu"].rearrange("(fc p) c -> p fc c", p=128), writes=[b_wg32], sem=csem)
            P.dma("sp", bg[:], I["b_glu"].rearrange("(fc p) -> p fc", p=128), writes=[b_bg], sem=csem, allow_slow_non_contiguous=True)
            P.op("dve", lambda e: e.tensor_copy(out=wg[:], in_=wg32[:]), reads=[b_wg32], writes=[b_wg])
            gst = [k.sb(f"ss_gst{i}", [128, 512], BF16, l4) for i in range(2)]
            b_gst = [Buf("gst0"), Buf("gst1")]
            gsem = [P.new_dsem(f"ss_gs{i}") for i in range(2)]
            sg = [k.sb(f"ss_sg{i}", [128, 512], F32, l4) for i in range(2)]
            b_sg = [Buf("sg0"), Buf("sg1")]
            so = [k.sb(f"ss_so{i}", [128, 512], BF16, l4) for i in range(2)]
            b_so = [Buf("so0"), Buf("so1")]
            sosem = [P.new_dsem(f"ss_sos{i}") for i in range(2)]
            ui = 0
            for fo in range(4):
                for tb in range(4):
                    i = ui % 2
                    ui += 1
                    tsl = slice(tb * 512, (tb + 1) * 512)
                    P.dma("sp", gst[i][:], S["sgsT"][fo * 128:(fo + 1) * 128, tsl], writes=[b_gst[i]], sem=gsem[i])
                    pt, bp = nb()
                    for fc in range(4):
                        P.op("pe", lambda e, fc=fc, fo=fo, tsl=tsl, pt=pt: e.matmul(pt[:], lhsT=wg[:, fc, fo * 128:(fo + 1) * 128], rhs=ygT[:, fc, tsl],
                                                                               start=(fc == 0), stop=(fc == 3)), reads=[b_wg, b_ygT], writes=[bp])
                    P.op("act", lambda e, i=i, fo=fo, pt=pt: e.activation(out=sg[i][:], in_=pt[:], func=AF.Sigmoid, bias=bg[:, fo:fo + 1]),
                         reads=[bp, b_bg], writes=[b_sg[i]])
                    P.op("dve", lambda e, i=i, fo=fo, tsl=tsl: e.tensor_tensor(out=sg[i][:], in0=sg[i][:], in1=ygT[:, fo, tsl], op=MUL),
                         reads=[b_sg[i], b_ygT], writes=[b_sg[i]])
                    P.op("dve", lambda e, i=i: e.tensor_tensor(out=so[i][:], in0=sg[i][:], in1=gst[i][:], op=MUL),
                         reads=[b_sg[i], b_gst[i]], writes=[b_so[i]])
                    P.dma("sp", S["sbrT"][fo * 128:(fo + 1) * 128, tsl], so[i][:], reads=[b_so[i]], sem=sosem[i])


def phase_attn(k):
    nc, P, I, S = k.nc, k.P, k.I, k.S
    with ExitStack() as ls:
        lamv = k.sb("at_lamv", [128, 4, 64], F32, ls)
        lw = k.sb("at_lw", [128, 8], F32, ls)
        G = k.sb("at_G", [128, 128], F32, ls)
        b_lamv, b_lw, b_G = Buf("lamv"), Buf("lw"), Buf("G")
        csem = P.new_dsem("at_c")
        P.dma("sp", lamv[:].rearrange("p a b -> p (a b)"), I["lam"].rearrange("a b -> (a b)").partition_broadcast(128),
              writes=[b_lamv], sem=csem)
        P.dma("sp", G[:], I["subln_g"][0:1, :].broadcast_to([128, 128]), writes=[b_G], sem=csem)
        P.op("dve", lambda e: e.tensor_scalar(out=G[:], in0=G[:], scalar1=(1.0 - LAM_INIT), scalar2=None, op0=ALU.mult),
             reads=[b_G], writes=[b_G])
        for i in range(2):
            P.op("dve", lambda e, i=i: e.tensor_tensor(out=lamv[:, 2 * i, :], in0=lamv[:, 2 * i, :], in1=lamv[:, 2 * i + 1, :], op=ALU.mult),
                 reads=[b_lamv], writes=[b_lamv])
            P.op("dve", lambda e, i=i: e.tensor_reduce(out=lw[:, i:i + 1], in_=lamv[:, 2 * i, :], axis=mybir.AxisListType.X, op=ALU.add),
                 reads=[b_lamv], writes=[b_lw])
        P.op("act", lambda e: e.activation(out=lw[:, 2:4], in_=lw[:, 0:2], func=AF.Exp), reads=[b_lw], writes=[b_lw])
        P.op("dve", lambda e: e.tensor_tensor(out=lw[:, 4:5], in0=lw[:, 3:4], in1=lw[:, 2:3], op=ALU.subtract), reads=[b_lw], writes=[b_lw])
        P.op("dve", lambda e: e.tensor_scalar(out=lw[:, 5:6], in0=lw[:, 4:5], scalar1=-LAM_INIT, scalar2=None, op0=ALU.add),
             reads=[b_lw], writes=[b_lw])
        neglam = lw[:, 5:6]
        qTs = [k.sb(f"at_q{i}", [128, L], BF16, ls) for i in range(2)]
        kTs = [k.sb(f"at_k{i}", [128, LT], BF16, ls) for i in range(2)]
        Vs = [k.sb(f"at_v{i}", [128, 18, 130], BF16, ls) for i in range(2)]
        gas = [k.sb(f"at_ga{i}", [128, 16, 128], BF16, ls) for i in range(2)]
        aTs = [k.sb(f"at_aT{i}", [128, L], BF16, ls) for i in range(2)]
        b_q = [Buf(f"atq{i}") for i in range(2)]
        b_k = [Buf(f"atk{i}") for i in range(2)]
        b_v = [Buf(f"atv{i}") for i in range(2)]
        b_ga = [Buf(f"atga{i}") for i in range(2)]
        b_aT = [Buf(f"ataT{i}") for i in range(2)]
        hsem = [P.new_dsem(f"at_h{i}") for i in range(2)]
        asem = [P.new_dsem(f"at_a{i}") for i in range(2)]
        for i in range(2):
            P.op("pool", lambda e, i=i: e.memset(Vs[i][:, :, 128:130], 1.0), writes=[b_v[i]])
        PT = [k.sb(f"at_pt{i}", [128, 2, 18, 256], BF16, ls) for i in range(2)]
        b_PT = [[[Buf(f"pt{i}_{c}_{kp}") for kp in range(9)] for c in range(2)] for i in range(2)]
        sbk = [k.ps(f"at_s{i}", [128, 512], F32, ls) for i in range(3)]
        b_sbk = [Buf(f"ats{i}") for i in range(3)]
        obk = [k.ps(f"at_o{i}", [128, 512], F32, ls) for i in range(4)]
        b_obk = [Buf(f"ato{i}") for i in range(4)]
        tbk = k.ps("at_t", [128, 512], F32, ls)
        b_tbk = Buf("att")
        sm = [k.sb(f"at_sm{i}", [128, 8], F32, ls) for i in range(2)]
        b_sm = [Buf(f"atsm{i}") for i in range(2)]
        tmp = [k.sb(f"at_tmp{i}", [128, 128], F32, ls) for i in range(2)]
        b_tmp = [Buf(f"attmp{i}") for i in range(2)]
        ov = [k.sb(f"at_ov{i}", [128, 128], F32, ls) for i in range(2)]
        b_ov = [Buf(f"atov{i}") for i in range(2)]
        junk = k.sb("at_junk", [128, 128], F32, ls)
        b_junk = Buf("atjunk")
        cnt = {"s": 0, "u": 0}

        def load_head(h):
            s = h % 2
            P.dma("sp", qTs[s][:], S["qT"][h], writes=[b_q[s]], sem=hsem[s])
            P.dma("sp", kTs[s][:], S["kT"][h], writes=[b_k[s]], sem=hsem[s])
            P.dma("sp", Vs[s][:, :, 0:128], S["v"][:, h * 128:(h + 1) * 128].rearrange("(t p) e -> p t e", p=128),
                  writes=[b_v[s]], sem=hsem[s])
            P.dma("sp", gas[s][:], S["sga"][:, h * 128:(h + 1) * 128].rearrange("(t p) e -> p t e", p=128),
                  writes=[b_ga[s]], sem=hsem[s])

        def A_steps(h, qb):
            s = h % 2
            ps_ = qb % 2
            steps = []
            for kp in range(9):
                def step(kp=kp):
                    for c in range(2):
                        si = cnt["s"] % 3
                        cnt["s"] += 1
                        for j in range(2):
                            kt = 2 * kp + j
                            P.op("pe", lambda e, kt=kt, j=j, c=c, si=si: e.matmul(
                                sbk[si][:, j * 256:(j + 1) * 256], lhsT=kTs[s][c * 64:(c + 1) * 64, kt * 128:(kt + 1) * 128],
                                rhs=qTs[s][c * 64:(c + 1) * 64, qb * 256:(qb + 1) * 256], start=True, stop=True),
                                reads=[b_k[s], b_q[s]], writes=[b_sbk[si]])
                        P.op("act", lambda e, c=c, kp=kp, si=si: e.activation(
                            out=PT[ps_][:, c, 2 * kp:2 * kp + 2, :].rearrange("p a b -> p (a b)"), in_=sbk[si][:], func=AF.Exp, scale=0.125),
                            reads=[b_sbk[si]], writes=[b_PT[ps_][c][kp]])
                steps.append(step)
            return steps

        def B_gen(h, qb):
            s = h % 2
            ps_ = qb % 2
            for qi_ in range(2):
                yield from unitB(h, qb, qi_, s, ps_)

        def unitB(h, qb, qi, s, ps_):
            if True:
                qt = qb * 2 + qi
                u = cnt["u"] % 2
                cnt["u"] += 1
                banks = [obk[u * 2], obk[u * 2 + 1]]
                bb = [b_obk[u * 2], b_obk[u * 2 + 1]]
                for c in range(2):
                    for kt in range(18):
                        P.op("pe", lambda e, c=c, kt=kt: e.matmul(
                            banks[c][:, 0:129], lhsT=PT[ps_][:, c, kt, qi * 128:(qi + 1) * 128], rhs=Vs[s][:, kt, 0:129],
                            start=(kt == 0), stop=(kt == 17)),
                            reads=[b_PT[ps_][c][kt // 2], b_v[s]], writes=[bb[c]])
                        yield
                flush_pending()
                smt, bsm = sm[u], b_sm[u]
                for c in range(2):
                    P.op("dve", lambda e, c=c: e.reciprocal(out=smt[:, c:c + 1], in_=banks[c][:, 128:129]), reads=[bb[c]], writes=[bsm])
                P.op("dve", lambda e: e.tensor_tensor(out=smt[:, 2:3], in0=smt[:, 1:2], in1=neglam, op=ALU.mult), reads=[bsm, b_lw], writes=[bsm])
                P.op("dve", lambda e: e.tensor_scalar(out=tmp[u][:], in0=banks[1][:, 0:128], scalar1=smt[:, 2:3], scalar2=None, op0=ALU.mult),
                     reads=[bb[1], bsm], writes=[b_tmp[u]])
                P.op("dve", lambda e: e.scalar_tensor_tensor(out=ov[u][:], in0=banks[0][:, 0:128], scalar=smt[:, 0:1], in1=tmp[u][:],
                                                            op0=ALU.mult, op1=ALU.add),
                     reads=[bb[0], bsm, b_tmp[u]], writes=[b_ov[u]])
                P.op("dve", lambda e: e.tensor_tensor(out=tmp[u][:], in0=ov[u][:], in1=ov[u][:], op=ALU.mult),
                     reads=[b_ov[u]], writes=[b_tmp[u]])
                P.op("dve", lambda e: e.tensor_reduce(out=smt[:, 3:4], in_=tmp[u][:], axis=mybir.AxisListType.X, op=ALU.add),
                     reads=[b_tmp[u]], writes=[bsm])
                P.op("dve", lambda e: e.tensor_scalar(out=smt[:, 4:5], in0=smt[:, 3:4], scalar1=1.0 / 128, scalar2=EPS, op0=ALU.mult, op1=ALU.add),
                     reads=[bsm], writes=[bsm])
                def fin():
                    P.op("act", lambda e: e.activation(out=smt[:, 5:6], in_=smt[:, 4:5], func=AF.Ln), reads=[bsm], writes=[bsm])
                    P.op("act", lambda e: e.activation(out=smt[:, 6:7], in_=smt[:, 5:6], func=AF.Exp, scale=-0.5), reads=[bsm], writes=[bsm])
                    P.op("dve", lambda e: e.scalar_tensor_tensor(out=ov[u][:], in0=ov[u][:], scalar=smt[:, 6:7], in1=G[:], op0=ALU.mult, op1=ALU.mult),
                         reads=[b_ov[u], bsm, b_G], writes=[b_ov[u]])
                    P.op("pool", lambda e: e.tensor_tensor(out=ov[u][:], in0=ov[u][:], in1=gas[s][:, qt, :], op=ALU.mult),
                         reads=[b_ov[u], b_ga[s]], writes=[b_ov[u]])

                    def fin2():
                        P.op("pe", lambda e: e.transpose(out=tbk[:, 0:128], in_=ov[u][:], identity=k.ident[:]), reads=[b_ov[u], k.b_ident], writes=[b_tbk])
                        P.op("dve", lambda e: e.tensor_copy(out=aTs[s][:, qt * 128:(qt + 1) * 128], in_=tbk[:, 0:128]),
                             reads=[b_tbk], writes=[b_aT[s]])
                    pending2.append(fin2)
                pending.append(fin)

        pending = []
        pending2 = []

        def flush_pending():
            while pending2:
                pending2.pop(0)()
            while pending:
                pending.pop(0)()

        def interleave(a_steps, bgen, per=8):
            for st_ in a_steps:
                st_()
                if bgen is not None:
                    for _ in range(per):
                        try:
                            next(bgen)
                        except StopIteration:
                            bgen = None
                            break
            if bgen is not None:
                for _ in bgen:
                    pass

        load_head(0)
        load_head(1)
        interleave(A_steps(0, 0), None)
        for h in range(HEADS):
            for qb in range(8):
                if qb + 1 < 8:
                    nxt = A_steps(h, qb + 1)
                elif h + 1 < HEADS:
                    nxt = A_steps(h + 1, 0)
                else:
                    nxt = []
                interleave(nxt, B_gen(h, qb))
            flush_pending()
            flush_pending()
            P.dma("pool", S["abrT"][h * 128:(h + 1) * 128, :], aTs[h % 2][:], reads=[b_aT[h % 2]], sem=asem[h % 2])
            if h + 2 < HEADS:
                load_head(h + 2)


def phase_merge(k):
    nc, P, I, S = k.nc, k.P, k.I, k.S
    with ExitStack() as ls:
        mT = k.sb("mg_mT", [128, NKC, L], BF16, ls)
        b_mT = [Buf(f"mT{tb}") for tb in range(4)]
        wout = k.sb("mg_wout", [128, NKC, D], BF16, ls)
        b_wout = Buf("wout")
        NXB = 2
        wov = I["w_out"].rearrange("(kc p) c -> p kc c", p=128)
        wo_state = {"kc": 0}
        b_woutc = [Buf(f"woutc{i}") for i in range(NKC)]

        def load_wout_chunk():
            kc = wo_state["kc"]
            if kc >= NKC:
                return
            wo_state["kc"] += 1
            P.dma("pool", wout[:, kc, :], wov[:, kc, :], writes=[b_woutc[kc]])
        with ExitStack() as l1:
            abrT = k.sb("mg_abrT", [128, 8, L], BF16, l1)
            sbrT = k.sb("mg_sbrT", [128, 4, L], BF16, l1)
            b_abrT, b_sbrT = Buf("abrT"), Buf("sbrT")
            lsem = P.new_dsem("mg_l")
            P.dma("sp", abrT[:], S["abrT"].rearrange("(fc p) t -> p fc t", p=128), writes=[b_abrT], sem=lsem)
            P.dma("sp", sbrT[:], S["sbrT"].rearrange("(fc p) t -> p fc t", p=128), writes=[b_sbrT], sem=lsem)
            NWS = 2
            wbf = [k.sb(f"mg_wbf{i}", [128, 12, 128], BF16, l1) for i in range(NWS)]
            b_wbf = [Buf(f"mgwbf{i}") for i in range(NWS)]
            NG = 2
            gt = [k.sb(f"mg_gt{i}", [128, 2, L], BF16, l1) for i in range(NG)]
            b_gt = [Buf(f"mggt{i}") for i in range(NG)]
            t1 = [k.sb(f"mg_t1{i}", [128, 512], F32, l1) for i in range(2)]
            t2 = [k.sb(f"mg_t2{i}", [128, 512], F32, l1) for i in range(2)]
            b_t1 = [Buf(f"mgt1{i}") for i in range(2)]
            b_t2 = [Buf(f"mgt2{i}") for i in range(2)]
            pa = [k.ps(f"mg_pa{i}", [128, 512], F32, l1) for i in range(2)]
            pp = [k.ps(f"mg_pp{i}", [128, 512], F32, l1) for i in range(2)]
            b_pa = [Buf(f"mgpa{i}") for i in range(2)]
            b_pp = [Buf(f"mgpp{i}") for i in range(2)]
            wpa_v = I["w_pa"].rearrange("(fc p) c -> p fc c", p=128)
            wps_v = I["w_ps"].rearrange("(fc p) c -> p fc c", p=128)
            ui = 0

            def load_w(fo):
                s = fo % NWS
                P.dma("pool", wbf[s][:, 0:8, :], wpa_v[:, :, fo * 128:(fo + 1) * 128], writes=[b_wbf[s]])
                P.dma("pool", wbf[s][:, 8:12, :], wps_v[:, :, fo * 128:(fo + 1) * 128], writes=[b_wbf[s]])
                gi = fo % NG
                P.dma("sp", gt[gi][:, 0, :], S["sgmT"][fo * 128:(fo + 1) * 128, :], writes=[b_gt[gi]])
                P.dma("sp", gt[gi][:, 1, :], S["sgmT"][D + fo * 128:D + (fo + 1) * 128, :], writes=[b_gt[gi]])

            load_w(0)
            for fo in range(NKC):
                if fo + 1 < NKC:
                    load_w(fo + 1)
                load_wout_chunk()
                s = fo % NWS
                gi = fo % NG
                for tb in range(4):
                    u2 = ui % 2
                    ui += 1
                    tsl = slice(tb * 512, (tb + 1) * 512)
                    for fc in range(8):
                        P.op("pe", lambda e, fc=fc, s=s, tsl=tsl, u2=u2: e.matmul(pa[u2][:], lhsT=wbf[s][:, fc, :], rhs=abrT[:, fc, tsl],
                                                                        start=(fc == 0), stop=(fc == 7)),
                             reads=[b_wbf[s], b_abrT], writes=[b_pa[u2]])
                    for fc in range(4):
                        P.op("pe", lambda e, fc=fc, s=s, tsl=tsl, u2=u2: e.matmul(pp[u2][:], lhsT=wbf[s][:, 8 + fc, :], rhs=sbrT[:, fc, tsl],
                                                                        start=(fc == 0), stop=(fc == 3)),
                             reads=[b_wbf[s], b_sbrT], writes=[b_pp[u2]])
                    P.op("dve", lambda e, gi=gi, u2=u2, tsl=tsl: e.tensor_tensor(out=t1[u2][:], in0=pa[u2][:], in1=gt[gi][:, 0, tsl], op=ALU.mult),
                         reads=[b_pa[u2], b_gt[gi]], writes=[b_t1[u2]])
                    P.op("dve", lambda e, gi=gi, u2=u2, tsl=tsl: e.tensor_tensor(out=t2[u2][:], in0=pp[u2][:], in1=gt[gi][:, 1, tsl], op=ALU.mult),
                         reads=[b_pp[u2], b_gt[gi]], writes=[b_t2[u2]])
                    P.op("pool", lambda e, fo=fo, tsl=tsl, u2=u2: e.tensor_tensor(out=mT[:, fo, tsl], in0=t1[u2][:], in1=t2[u2][:], op=ALU.add),
                         reads=[b_t1[u2], b_t2[u2]], writes=[b_mT[tb]])
            P.barrier()
        gateB = k.sb("mg_gateB", [128, D], F32, ls)
        fgB = k.sb("mg_fgB", [128, D], F32, ls)
        b_gateB, b_fgB = Buf("gateB"), Buf("fgB")
        c2 = P.new_dsem("mg_c2")
        P.dma("sp", gateB[:], S["modrow"][0:1, 2 * D:3 * D].broadcast_to([128, D]), writes=[b_gateB], sem=c2)
        P.dma("sp", fgB[:], I["final_g"][0:1, :].broadcast_to([128, D]), writes=[b_fgB], sem=c2)
        xb = [k.sb(f"mg_x{i}", [128, D], F32, ls) for i in range(NXB)]
        b_xb = [Buf(f"mgx{i}") for i in range(NXB)]
        xn = [k.sb(f"mg_xn{i}", [128, D], F32, ls) for i in range(NXB)]
        b_xn = [Buf(f"mgxn{i}") for i in range(NXB)]
        xsem = [P.new_dsem(f"mg_xs{i}") for i in range(NXB)]
        osem = [P.new_dsem(f"mg_os{i}") for i in range(NXB)]
        st2 = [k.sb(f"mg_st{i}", [128, 4], F32, ls) for i in range(NXB)]
        b_st2 = [Buf(f"mgst{i}") for i in range(NXB)]
        while wo_state["kc"] < NKC:
            load_wout_chunk()
        po = [k.ps(f"mg_po{i}", [128, 512], F32, ls) for i in range(3)]
        b_po = [Buf(f"mgpo{i}") for i in range(3)]
        pi = 0
        for t in range(16):
            s = t % NXB
            tb = t // 4
            P.dma("sp", xb[s][:], I["x"][t * 128:(t + 1) * 128, :], writes=[b_xb[s]], sem=xsem[s])
            for cbk in range(4):
                p_ = pi % 3
                pi += 1
                for kc in range(NKC):
                    P.op("pe", lambda e, kc=kc, cbk=cbk, p_=p_, t=t: e.matmul(po[p_][:], lhsT=mT[:, kc, t * 128:(t + 1) * 128],
                                                                          rhs=wout[:, kc, cbk * 512:(cbk + 1) * 512],
                                                                          start=(kc == 0), stop=(kc == NKC - 1)),
                         reads=[b_mT[tb], b_woutc[kc]], writes=[b_po[p_]])
                P.op("dve", lambda e, cbk=cbk, p_=p_, s=s: e.tensor_tensor(out=xn[s][:, cbk * 512:(cbk + 1) * 512], in0=po[p_][:],
                                                                       in1=gateB[:, cbk * 512:(cbk + 1) * 512], op=ALU.mult),
                     reads=[b_po[p_], b_gateB], writes=[b_xn[s]])
            P.op("pool", lambda e, s=s: e.tensor_tensor(out=xn[s][:], in0=xn[s][:], in1=xb[s][:], op=ALU.add),
                 reads=[b_xn[s], b_xb[s]], writes=[b_xn[s]])
            P.op("pool", lambda e, s=s: e.tensor_tensor(out=xb[s][:], in0=xn[s][:], in1=xn[s][:], op=ALU.mult),
                 reads=[b_xn[s]], writes=[b_xb[s]])
            P.op("dve", lambda e, s=s: e.tensor_reduce(out=st2[s][:, 0:1], in_=xb[s][:], axis=mybir.AxisListType.X, op=ALU.add),
                 reads=[b_xb[s]], writes=[b_st2[s]])
            P.op("dve", lambda e, s=s: e.tensor_scalar(out=st2[s][:, 1:2], in0=st2[s][:, 0:1], scalar1=1.0 / D, scalar2=EPS, op0=ALU.mult, op1=ALU.add),
                 reads=[b_st2[s]], writes=[b_st2[s]])
            P.op("act", lambda e, s=s: e.activation(out=st2[s][:, 2:3], in_=st2[s][:, 1:2], func=AF.Ln), reads=[b_st2[s]], writes=[b_st2[s]])
            P.op("act", lambda e, s=s: e.activation(out=st2[s][:, 3:4], in_=st2[s][:, 2:3], func=AF.Exp, scale=-0.5), reads=[b_st2[s]], writes=[b_st2[s]])
            P.op("dve", lambda e, s=s: e.scalar_tensor_tensor(out=xn[s][:], in0=xn[s][:], scalar=st2[s][:, 3:4], in1=fgB[:], op0=ALU.mult, op1=ALU.mult),
                 reads=[b_xn[s], b_st2[s], b_fgB], writes=[b_xn[s]])
            P.dma("sp", k.out[t * 128:(t + 1) * 128, :], xn[s][:], reads=[b_xn[s]], sem=osem[s])


_CACHE = {}


def _prep_inputs(inputs, b):
    f = lambda a: np.ascontiguousarray(np.asarray(a, dtype=np.float32))
    m = {}
    m["x"] = f(inputs["x"][b])
    m["ctx"] = f(inputs["ctx"][b])
    m["cc"] = f(np.stack([np.asarray(inputs["c"])[b], np.asarray(inputs["c_ctx"])], axis=0))
    m["w_ada"] = f(inputs["w_ada"][0])
    m["b_ada"] = f(inputs["b_ada"][0]).reshape(1, -1)
    m["norm_g"] = f(inputs["norm_g"][0])
    m["w_in"] = f(inputs["w_in"][0])
    m["lam"] = f(np.stack([np.asarray(inputs["lambda_q1"])[0], np.asarray(inputs["lambda_k1"])[0],
                           np.asarray(inputs["lambda_q2"])[0], np.asarray(inputs["lambda_k2"])[0]], axis=0))
    m["subln_g"] = f(inputs["subln_g"][0]).reshape(1, 128)
    m["ssm_lre"] = f(inputs["ssm_lambda_re"][0])
    m["ssm_lim"] = f(inputs["ssm_lambda_im"][0])
    m["ssm_ls"] = f(inputs["ssm_log_step"][0])
    m["ssm_bre"] = f(inputs["ssm_b_re"][0])
    m["ssm_bim"] = f(inputs["ssm_b_im"][0])
    m["ssm_cre"] = f(inputs["ssm_c_re"][0])
    m["ssm_cim"] = f(inputs["ssm_c_im"][0])
    m["ssm_d"] = f(inputs["ssm_d"][0]).reshape(1, 512)
    m["w_glu"] = f(inputs["w_glu"][0])
    m["b_glu"] = f(inputs["b_glu"][0])
    m["w_pa"] = f(inputs["w_pa"][0])
    m["w_ps"] = f(inputs["w_ps"][0])
    m["w_out"] = f(inputs["w_out"][0])
    m["final_g"] = f(inputs["final_g"]).reshape(1, D)
    m.update(_consts())
    return m


def kernel(**inputs):
    if "nc" not in _CACHE:
        _CACHE["nc"] = build()[0]
    nc = _CACHE["nc"]
    shared = None
    in_maps = []
    for b in range(8):
        m = _prep_inputs(inputs, b)
        if shared is None:
            shared = m
        else:
            for key in m:
                if key not in ("x", "ctx", "cc"):
                    m[key] = shared[key]
        in_maps.append(m)
    res = run_bass_kernel_spmd(nc, in_maps, core_ids=list(range(8)))
    return np.stack([np.asarray(r["out"], dtype=np.float32) for r in res.results], axis=0)
```

```python
import math
import numpy as np
import ml_dtypes
from contextlib import ExitStack
import concourse.bass as bass
import concourse.mybir as mybir
from concourse.bass_utils import run_bass_kernel_spmd

F32 = mybir.dt.float32
BF16 = mybir.dt.bfloat16
I32 = mybir.dt.int32
AF = mybir.ActivationFunctionType
ALU = mybir.AluOpType

D = 2048
L = 2048
LC = 256
LT = L + LC
NKC = D // 128
INW = 9216
HEADS = 8
EPS = 1e-6
LAM_INIT = 0.8 - 0.6 * math.exp(-0.3 * 0)
TWO_PI = 2.0 * math.pi


class Buf:
    __slots__ = ("name", "w", "r")

    def __init__(self, name):
        self.name = name
        self.w = None
        self.r = {}


class Prog:
    ENG = ["pe", "act", "dve", "pool", "sp"]

    def __init__(self, nc, st):
        self.nc = nc
        self.st = st
        self.q = {e: [] for e in self.ENG}
        self.seen = {e: {} for e in self.ENG}
        self.psem = {e: st.enter_context(nc.semaphore("p_" + e)) for e in ["pe", "act", "dve", "pool"]}
        self.dsems = []
        self.bufsem = {}
        self.bufsem_keep = []
        self.free_dsems = []

    def new_dsem(self, name):
        return None

    def _auto_dsem(self, reads, writes):
        b = writes[0] if len(writes) else reads[0]
        key = id(b)
        d = self.bufsem.get(key)
        if d is None:
            if self.free_dsems:
                d = self.free_dsems.pop()
            else:
                h = self.st.enter_context(self.nc.semaphore(f"d{len(self.dsems)}"))
                d = {"h": h, "n": 0, "name": f"d{len(self.dsems)}"}
                self.dsems.append(d)
            self.bufsem[key] = d
            self.bufsem_keep.append(b)
        return d

    def _deps(self, eng, reads, writes):
        need = {}

        def add(t):
            if t[0] == "c":
                if t[1] == "pe" and eng == "pe":
                    return
                key = ("c", t[1])
                if need.get(key, (None, -1))[1] < t[2]:
                    need[key] = (t[1], t[2])
            else:
                key = ("d", id(t[1]))
                if need.get(key, (None, -1))[1] < t[2]:
                    need[key] = (t[1], t[2])

        for b in reads:
            if b.w is not None:
                add(b.w)
        for b in writes:
            if b.w is not None:
                add(b.w)
            for t in b.r.values():
                add(t)
        waits = []
        for key, (obj, v) in need.items():
            if self.seen[eng].get(key, -1) >= v:
                continue
            self.seen[eng][key] = v
            waits.append((key[0], obj, v))
        return waits

    def _record(self, tok, reads, writes):
        for b in reads:
            key = (tok[0], tok[1] if tok[0] == "c" else id(tok[1]))
            b.r[key] = tok
        for b in writes:
            b.w = tok
            b.r = {}

    def op(self, eng, fn, reads=(), writes=()):
        waits = self._deps(eng, reads, writes)
        idx = len(self.q[eng])
        self.q[eng].append({"fn": fn, "waits": waits, "awaited": False, "dma": None})
        tok = ("c", eng, idx)
        self._record(tok, reads, writes)
        return tok

    def dma(self, eng, out, in_, reads=(), writes=(), sem=None, **kw):
        reads, writes = list(reads), list(writes)
        sem = self._auto_dsem(reads, writes)
        waits = self._deps(eng, reads, writes)
        sem["n"] += 16
        tok = ("d", sem, sem["n"])
        self.q[eng].append({"fn": (lambda e, o=out, i=in_, k=kw: e.dma_start(out=o, in_=i, **k)),
                            "waits": waits, "awaited": False, "dma": sem})
        self._record(tok, reads, writes)
        return tok

    def barrier(self):
        for e in self.ENG:
            waits = []
            for e2 in ["pe", "act", "dve", "pool"]:
                n = len(self.q[e2])
                if e2 == e:
                    n -= 0
                idx = None
                for i in range(len(self.q[e2]) - 1, -1, -1):
                    if self.q[e2][i]["fn"] is not None and self.q[e2][i]["dma"] is None:
                        idx = i
                        break
                if idx is None:
                    continue
                key = ("c", e2)
                if self.seen[e].get(key, -1) >= idx:
                    continue
                self.seen[e][key] = idx
                waits.append(("c", e2, idx))
            for d in self.dsems:
                if d["n"] == 0:
                    continue
                key = ("d", id(d))
                if self.seen[e].get(key, -1) >= d["n"]:
                    continue
                self.seen[e][key] = d["n"]
                waits.append(("d", d, d["n"]))
            if waits:
                self.q[e].append({"fn": None, "waits": waits, "awaited": False, "dma": None})
        for d in self.bufsem.values():
            self.free_dsems.append(d)
        self.bufsem = {}
        self.bufsem_keep = []

    def emit(self):
        for e in self.ENG:
            for ent in self.q[e]:
                for w in ent["waits"]:
                    if w[0] == "c":
                        self.q[w[1]][w[2]]["awaited"] = True
        cnt = {}
        for e in ["pe", "act", "dve", "pool"]:
            c = 0
            arr = []
            for ent in self.q[e]:
                if ent["awaited"]:
                    c += 1
                arr.append(c)
            cnt[e] = arr
        psem = self.psem
        q = self.q

        def run(name, e):
            for ent in q[name]:
                for w in ent["waits"]:
                    if w[0] == "c":
                        e.wait_ge(psem[w[1]], cnt[w[1]][w[2]])
                    else:
                        e.wait_ge(w[1]["h"], w[2])
                if ent["fn"] is None:
                    continue
                inst = ent["fn"](e)
                if ent["dma"] is not None:
                    inst.then_inc(ent["dma"]["h"], 16)
                elif ent["awaited"]:
                    inst.then_inc(psem[name], 1)

        with self.nc.Block() as block:
            @block.sync
            def _(e):
                run("sp", e)

            @block.scalar
            def _(e):
                run("act", e)

            @block.vector
            def _(e):
                run("dve", e)

            @block.gpsimd
            def _(e):
                run("pool", e)

            @block.tensor
            def _(e):
                run("pe", e)


def _consts():
    ident = np.eye(128, dtype=np.float32)
    m = np.arange(128)
    partner = np.where((m % 32) < 16, m + 16, m - 16)
    perm = np.zeros((128, 128), np.float32)
    perm[partner, m] = 1.0
    sgn = np.where((m % 32) < 16, -1.0, 1.0).astype(np.float32)
    tok = np.arange(L)
    pos = np.where(((m % 64) < 32)[:, None], (tok // 64)[None, :], (tok % 64)[None, :]).astype(np.float32)
    fexp = ((m % 16) / 16.0).astype(np.float32)
    colc = np.zeros((128, 4), np.float32)
    colc[:, 0] = sgn
    colc[:, 1] = fexp
    colc[:, 2] = np.where(m < 64, 1.0, -1.0)
    sel = np.zeros((2, 128), np.float32)
    sel[0, :] = 1.0
    tauA = np.zeros((128, 32, 9), np.float32)
    tauA[:, 0:16, :] = np.arange(9)[None, None, :]
    tauA[:, 16:32, :] = (8 - np.arange(9))[None, None, :]
    tauB = np.zeros((128, 32, 8), np.float32)
    tauB[:, 0:16, :] = (7 - np.arange(8))[None, None, :]
    tauB[:, 16:32, :] = np.arange(8)[None, None, :]
    tauC = np.zeros((128, 32, 8), np.float32)
    tauC[:, :, :] = (8.0 * (np.arange(8) + 1))[None, None, :]
    return {"c_ident": ident, "c_perm": perm, "c_pos": pos, "c_col": colc, "c_sel": sel, "c_tauA": tauA, "c_tauB": tauB,
            "c_tauC": tauC}


class K:
    pass


def build(debug=None):
    nc = bass.Bass("TRN2", target_bir_lowering=False)
    st = ExitStack()
    P = Prog(nc, st)
    k = K()
    k.nc, k.P, k.st = nc, P, st
    k.debug = debug or {}

    def dram_in(name, shape, dt=F32):
        return nc.dram_tensor(name, list(shape), dt, kind="ExternalInput").ap()

    dbg_outs = []

    def dram_scr(name, shape, dt):
        kind = "Internal"
        if debug is not None and name in debug.get("_inject", ()):
            kind = "ExternalInput"
        elif debug is not None and name in debug:
            kind = "ExternalOutput"
            dbg_outs.append(name)
        return nc.dram_tensor(name, list(shape), dt, kind=kind).ap()

    I = {}
    I["x"] = dram_in("x", [L, D])
    I["ctx"] = dram_in("ctx", [LC, D])
    I["cc"] = dram_in("cc", [2, D])
    I["w_ada"] = dram_in("w_ada", [D, 3 * D])
    I["b_ada"] = dram_in("b_ada", [1, 3 * D])
    I["norm_g"] = dram_in("norm_g", [D])
    I["w_in"] = dram_in("w_in", [D, INW])
    I["lam"] = dram_in("lam", [4, 64])
    I["subln_g"] = dram_in("subln_g", [1, 128])
    I["ssm_lre"] = dram_in("ssm_lre", [2, 32, 64])
    I["ssm_lim"] = dram_in("ssm_lim", [2, 32, 64])
    I["ssm_ls"] = dram_in("ssm_ls", [2, 32])
    I["ssm_bre"] = dram_in("ssm_bre", [2, 32, 64, 16])
    I["ssm_bim"] = dram_in("ssm_bim", [2, 32, 64, 16])
    I["ssm_cre"] = dram_in("ssm_cre", [2, 32, 16, 64])
    I["ssm_cim"] = dram_in("ssm_cim", [2, 32, 16, 64])
    I["ssm_d"] = dram_in("ssm_d", [1, 512])
    I["w_glu"] = dram_in("w_glu", [512, 512])
    I["b_glu"] = dram_in("b_glu", [512])
    I["w_pa"] = dram_in("w_pa", [1024, D])
    I["w_ps"] = dram_in("w_ps", [512, D])
    I["w_out"] = dram_in("w_out", [D, D])
    I["final_g"] = dram_in("final_g", [1, D])
    for cn, arr in _consts().items():
        I[cn] = dram_in(cn, arr.shape)
    out = nc.dram_tensor("out", [L, D], F32, kind="ExternalOutput").ap()

    S = {}
    S["modrow"] = dram_scr("modrow", [2, 3 * D], F32)
    S["qT"] = dram_scr("qT", [HEADS, 128, L], BF16)
    S["kT"] = dram_scr("kT", [HEADS, 128, LT], BF16)
    S["v"] = dram_scr("v", [LT, 1024], BF16)
    S["sga"] = dram_scr("sga", [L, 1024], BF16)
    S["u"] = dram_scr("u", [LT, 512], F32)
    S["sgsT"] = dram_scr("sgsT", [512, L], BF16)
    S["sgmT"] = dram_scr("sgmT", [2 * D, L], BF16)
    S["abrT"] = dram_scr("abrT", [1024, L], BF16)
    S["sbrT"] = dram_scr("sbrT", [512, L], BF16)
    S["hT"] = dram_scr("hT_dbg", [128, NKC, LT], BF16) if (debug is not None and "hT_dbg" in debug) else None
    k.I, k.S, k.out = I, S, out
    k.dbg_sem = None

    def dbg(name, shape, dt, ap_fn, bufs):
        if debug is None or name not in debug:
            return
        if name not in S:
            S[name] = nc.dram_tensor(name, list(shape), dt, kind="ExternalOutput").ap()
            dbg_outs.append(name)
        if k.dbg_sem is None:
            k.dbg_sem = P.new_dsem("dbgsem")
        o, i = ap_fn(S[name])
        P.dma("sp", o, i, reads=bufs, sem=k.dbg_sem)
    k.dbg = dbg

    def sb(name, shape, dt, stack=st):
        return stack.enter_context(nc.sbuf_tensor(name, list(shape), dt))

    def ps(name, shape, dt, stack=st):
        return stack.enter_context(nc.psum_tensor(name, list(shape), dt))

    k.sb, k.ps = sb, ps
    ident = sb("ident", [128, 128], F32)
    colc = sb("colc", [128, 4], F32)
    b_ident, b_colc = Buf("ident"), Buf("colc")
    csem = P.new_dsem("csem")
    P.dma("sp", ident[:], I["c_ident"], writes=[b_ident], sem=csem)
    P.dma("sp", colc[:], I["c_col"], writes=[b_colc], sem=csem)
    k.ident, k.b_ident, k.colc, k.b_colc, k.csem = ident, b_ident, colc, b_colc, csem
    k.ssq = sb("ssq", [128, 40], F32)
    k.b_ssq = Buf("ssq")

    phase_adaln(k)
    P.barrier()
    if debug is None or debug.get("_upto", 99) >= 1:
        phase_norm_inproj(k, debug)
        P.barrier()
    if (debug is None or debug.get("_upto", 99) >= 2) and not (debug or {}).get("_skip_ssm"):
        phase_ssm(k)
        P.barrier()
    if debug is None or debug.get("_upto", 99) >= 3:
        phase_attn(k)
        P.barrier()
    if debug is None or debug.get("_upto", 99) >= 4:
        phase_merge(k)
        P.barrier()
    P.emit()
    st.close()
    return nc, dbg_outs


def range_sin(k, stack, out_ap, y_ap, shape, tag, rbufs, wbufs, eng="dve"):
    nc, P = k.nc, k.P
    ki = k.sb(tag + "_ki", shape, I32, stack)
    kf = k.sb(tag + "_kf", shape, F32, stack)
    g = k.sb(tag + "_g", shape, F32, stack)
    bki, bkf, bg = Buf(tag + "ki"), Buf(tag + "kf"), Buf(tag + "g")
    sl = tuple([slice(None)] * len(shape))
    P.op(eng, lambda e: e.tensor_copy(out=ki[sl], in_=y_ap), reads=rbufs, writes=[bki])
    P.op(eng, lambda e: e.tensor_copy(out=kf[sl], in_=ki[sl]), reads=[bki], writes=[bkf])
    P.op(eng, lambda e: e.tensor_tensor(out=kf[sl], in0=y_ap, in1=kf[sl], op=ALU.subtract), reads=rbufs + [bkf], writes=[bkf])
    P.op(eng, lambda e: e.tensor_single_scalar(out=g[sl], in_=kf[sl], scalar=0.5, op=ALU.is_gt), reads=[bkf], writes=[bg])
    P.op(eng, lambda e: e.tensor_tensor(out=kf[sl], in0=kf[sl], in1=g[sl], op=ALU.subtract), reads=[bkf, bg], writes=[bkf])
    P.op(eng, lambda e: e.tensor_single_scalar(out=g[sl], in_=kf[sl], scalar=-0.5, op=ALU.is_lt), reads=[bkf], writes=[bg])
    P.op(eng, lambda e: e.tensor_tensor(out=kf[sl], in0=kf[sl], in1=g[sl], op=ALU.add), reads=[bkf, bg], writes=[bkf])
    P.op("act", lambda e: e.activation(out=out_ap, in_=kf[sl], func=AF.Sin, scale=TWO_PI * (1.0 - 2e-7)), reads=[bkf], writes=wbufs)


def phase_adaln(k):
    nc, P, I, S = k.nc, k.P, k.I, k.S
    with ExitStack() as ls:
        sT = k.sb("ad_sT", [128, NKC, 2], F32, ls)
        b_sT = Buf("sT")
        sem_c = P.new_dsem("ad_c")
        for v in range(2):
            P.dma("sp", sT[:, :, v], I["cc"][v].rearrange("(kc p) -> p kc", p=128), writes=[b_sT], sem=sem_c,
                  allow_slow_non_contiguous=True)
        P.op("act", lambda e: e.activation(out=sT[:], in_=sT[:], func=AF.Silu), reads=[b_sT], writes=[b_sT])
        brow = k.sb("ad_brow", [2, 3 * D], F32, ls)
        b_brow = Buf("brow")
        for v in range(2):
            P.dma("sp", brow[v:v + 1, :], I["b_ada"], writes=[b_brow], sem=sem_c)
        modrow = k.sb("ad_modrow", [2, 3 * D], F32, ls)
        b_modrow = Buf("modrow")
        NS = 2
        wst = [k.sb(f"ad_w{i}", [128, NKC, 512], F32, ls) for i in range(NS)]
        b_w = [Buf(f"adw{i}") for i in range(NS)]
        wsem = [P.new_dsem(f"ad_ws{i}") for i in range(NS)]
        pst = [k.ps(f"ad_ps{i}", [128, 512], F32, ls) for i in range(2)]
        b_ps = [Buf(f"adps{i}") for i in range(2)]
        wv = I["w_ada"].rearrange("(kc p) c -> p kc c", p=128)
        xs1 = [k.sb(f"ad_x{i}", [128, D], F32, ls) for i in range(2)]
        b_xs1 = [Buf(f"adx{i}") for i in range(2)]
        junk1 = k.sb("ad_junk", [128, D], BF16, ls)
        b_junk1 = Buf("adjunk")
        tiles_done = 0

        def ss_tile(t):
            s1 = t % 2
            src = I["x"][t * 128:(t + 1) * 128, :] if t < 16 else I["ctx"][(t - 16) * 128:(t - 15) * 128, :]
            P.dma("pool", xs1[s1][:], src, writes=[b_xs1[s1]])
            P.op("act", lambda e, s1=s1, t=t: e.activation(out=junk1[:], in_=xs1[s1][:], func=AF.Square, accum_out=k.ssq[:, t:t + 1]),
                 reads=[b_xs1[s1]], writes=[b_junk1, k.b_ssq])

        for cb in range(12):
            for _ in range(2 if cb < 6 else 1):
                if tiles_done < 18:
                    ss_tile(tiles_done)
                    tiles_done += 1
            s = cb % NS
            P.dma("sp", wst[s][:, 0:8, :], wv[:, 0:8, cb * 512:(cb + 1) * 512], writes=[b_w[s]], sem=wsem[s])
            P.dma("act", wst[s][:, 8:16, :], wv[:, 8:16, cb * 512:(cb + 1) * 512], writes=[b_w[s]], sem=wsem[s])
            pt, bp = pst[cb % 2], b_ps[cb % 2]
            for kc in range(NKC):
                P.op("pe", lambda e, kc=kc, s=s, pt=pt: e.matmul(pt[0:2, :], lhsT=sT[:, kc, :], rhs=wst[s][:, kc, :],
                                                              start=(kc == 0), stop=(kc == NKC - 1)),
                     reads=[b_sT, b_w[s]], writes=[bp])
            P.op("dve", lambda e, cb=cb, pt=pt: e.tensor_tensor(out=modrow[:, cb * 512:(cb + 1) * 512], in0=pt[0:2, :],
                                                             in1=brow[:, cb * 512:(cb + 1) * 512], op=ALU.add),
                 reads=[bp, b_brow], writes=[b_modrow])
        b_mr = Buf("modrow_d")
        k.b_modrow_d = b_mr
        P.dma("sp", S["modrow"], modrow[:], reads=[b_modrow], writes=[b_mr], sem=sem_c)


def phase_norm_inproj(k, debug):
    nc, P, I, S = k.nc, k.P, k.I, k.S
    with ExitStack() as ls:
        hT = k.sb("hT", [128, NKC, LT], BF16, ls)
        b_hT = [[Buf(f"hT{t}_{kc}") for kc in range(NKC)] for t in range(18)]
        Amod = k.sb("Amod", [128, NKC, 2], F32, ls)
        Smod = k.sb("Smod", [128, NKC, 2], F32, ls)
        gcol = k.sb("gcol", [128, NKC], F32, ls)
        b_A, b_S, b_g = Buf("Amod"), Buf("Smod"), Buf("gcol")
        msem = P.new_dsem("n_m")
        for v in range(2):
            P.dma("sp", Smod[:, :, v], S["modrow"][v, 0:D].rearrange("(kc p) -> p kc", p=128),
                  reads=[k.b_modrow_d], writes=[b_S], sem=msem, allow_slow_non_contiguous=True)
            P.dma("sp", Amod[:, :, v], S["modrow"][v, D:2 * D].rearrange("(kc p) -> p kc", p=128),
                  reads=[k.b_modrow_d], writes=[b_A], sem=msem, allow_slow_non_contiguous=True)
        P.dma("sp", gcol[:], I["norm_g"].rearrange("(kc p) -> p kc", p=128), writes=[b_g], sem=msem,
              allow_slow_non_contiguous=True)
        for v in range(2):
            P.op("dve", lambda e, v=v: e.scalar_tensor_tensor(out=Amod[:, :, v], in0=Amod[:, :, v], scalar=1.0, in1=gcol[:],
                                                             op0=ALU.add, op1=ALU.mult),
                 reads=[b_A, b_g], writes=[b_A])
        with ExitStack() as l1:
            NX = 2
            xt = [k.sb(f"n_x{i}", [128, D], F32, l1) for i in range(NX)]
            b_x = [Buf(f"nx{i}") for i in range(NX)]
            xsem = [P.new_dsem(f"n_xs{i}") for i in range(NX)]
            junk = k.sb("n_junk", [128, D], BF16, l1)
            b_junk = Buf("junk")
            stat = [k.sb(f"n_st{i}", [128, 4], F32, l1) for i in range(NX)]
            b_stat = [Buf(f"nst{i}") for i in range(NX)]
            pt = [k.ps(f"n_ps{i}", [128, 512], F32, l1) for i in range(4)]
            b_pt = [Buf(f"nps{i}") for i in range(4)]
            pi = 0
            P.op("dve", lambda e: e.tensor_scalar(out=k.ssq[:, 0:18], in0=k.ssq[:, 0:18], scalar1=1.0 / D, scalar2=EPS, op0=ALU.mult, op1=ALU.add),
                 reads=[k.b_ssq], writes=[k.b_ssq])
            P.op("act", lambda e: e.activation(out=k.ssq[:, 0:18], in_=k.ssq[:, 0:18], func=AF.Ln), reads=[k.b_ssq], writes=[k.b_ssq])
            P.op("act", lambda e: e.activation(out=k.ssq[:, 20:38], in_=k.ssq[:, 0:18], func=AF.Exp, scale=-0.5), reads=[k.b_ssq], writes=[k.b_ssq])
            for t in range(18):
                s = t % NX
                v = 0 if t < 16 else 1
                src = I["x"][t * 128:(t + 1) * 128, :] if t < 16 else I["ctx"][(t - 16) * 128:(t - 15) * 128, :]
                P.dma("sp", xt[s][:, 0:1024], src[:, 0:1024], writes=[b_x[s]], sem=xsem[s])
                P.dma("sp", xt[s][:, 1024:2048], src[:, 1024:2048], writes=[b_x[s]], sem=xsem[s])
                P.op("dve", lambda e, s=s, t=t: e.tensor_scalar(out=xt[s][:], in0=xt[s][:], scalar1=k.ssq[:, 20 + t:21 + t], scalar2=None,
                                                              op0=ALU.mult),
                     reads=[b_x[s], k.b_ssq], writes=[b_x[s]])
                for g4 in range(4):
                    p_, bp = pt[pi % 4], b_pt[pi % 4]
                    pi += 1
                    for j in range(4):
                        kc = g4 * 4 + j
                        P.op("pe", lambda e, s=s, kc=kc, j=j, p_=p_: e.transpose(out=p_[:, j * 128:(j + 1) * 128],
                                                                             in_=xt[s][:, kc * 128:(kc + 1) * 128],
                                                                             identity=k.ident[:]),
                             reads=[b_x[s], k.b_ident], writes=[bp])
                    for j in range(4):
                        kc = g4 * 4 + j
                        eng = "dve" if (g4 % 2 == 0) else "act"
                        if eng == "dve":
                            P.op("dve", lambda e, kc=kc, j=j, p_=p_, t=t, v=v: e.tensor_scalar(
                                out=hT[:, kc, t * 128:(t + 1) * 128], in0=p_[:, j * 128:(j + 1) * 128],
                                scalar1=Amod[:, kc, v:v + 1], scalar2=Smod[:, kc, v:v + 1], op0=ALU.mult, op1=ALU.add),
                                reads=[bp, b_A, b_S], writes=[b_hT[t][kc]])
                        else:
                            P.op("act", lambda e, kc=kc, j=j, p_=p_, t=t, v=v: e.activation(
                                out=hT[:, kc, t * 128:(t + 1) * 128], in_=p_[:, j * 128:(j + 1) * 128],
                                func=AF.Identity, scale=Amod[:, kc, v:v + 1], bias=Smod[:, kc, v:v + 1]),
                                reads=[bp, b_A, b_S], writes=[b_hT[t][kc]])
        if S["hT"] is not None:
            dsem = P.new_dsem("dbg")
            P.dma("sp", S["hT"], hT[:], reads=[b for row in b_hT for b in row], writes=[Buf("x")], sem=dsem)
        P.barrier()
        if debug is not None and debug.get("_upto", 99) < 1.5:
            return
        inproj(k, ls, hT, b_hT)


def inproj(k, ls, hT, b_hT):
    nc, P, I, S = k.nc, k.P, k.I, k.S
    cosT = k.sb("cosT", [128, L], F32, ls)
    sinS = k.sb("sinS", [128, L], F32, ls)
    perm = k.sb("perm", [128, 128], F32, ls)
    b_cos, b_sin, b_perm = Buf("cos"), Buf("sin"), Buf("perm")
    tsem = P.new_dsem("ip_t")
    P.dma("sp", perm[:], I["c_perm"], writes=[b_perm], sem=tsem)
    with ExitStack() as l0:
        pos = k.sb("pos", [128, L], F32, l0)
        yv = k.sb("yv", [128, L], F32, l0)
        inv = k.sb("inv", [128, 1], F32, l0)
        b_pos, b_y, b_inv = Buf("pos"), Buf("yv"), Buf("inv")
        P.dma("sp", pos[:], I["c_pos"], writes=[b_pos], sem=tsem)
        P.op("act", lambda e: e.activation(out=inv[:], in_=k.colc[:, 1:2], func=AF.Exp, scale=-math.log(10000.0)),
             reads=[k.b_colc], writes=[b_inv])
        P.op("dve", lambda e: e.tensor_scalar(out=yv[:], in0=pos[:], scalar1=inv[:, 0:1], scalar2=1.0 / TWO_PI,
                                              op0=ALU.mult, op1=ALU.mult), reads=[b_pos, b_inv], writes=[b_y])
        range_sin(k, l0, sinS[:], yv[:], [128, L], "rs1", [b_y], [b_sin])
        P.op("dve", lambda e: e.tensor_scalar(out=sinS[:], in0=sinS[:], scalar1=k.colc[:, 0:1], scalar2=None, op0=ALU.mult),
             reads=[b_sin, k.b_colc], writes=[b_sin])
        P.op("dve", lambda e: e.tensor_scalar(out=yv[:], in0=yv[:], scalar1=0.25, scalar2=None, op0=ALU.add),
             reads=[b_y], writes=[b_y])
        range_sin(k, l0, cosT[:], yv[:], [128, L], "rs2", [b_y], [b_cos])
        P.barrier()
    b_wbq = [[Buf(f"wbq{i}_{j}") for j in range(4)] for i in range(2)]
    wb = [k.sb(f"ip_wb{i}", [128, NKC, 512], BF16, ls) for i in range(2)]
    b_wb = [Buf(f"wb{i}") for i in range(2)]
    NOB = 4
    ob = [k.sb(f"ip_ob{i}", [128, 512], BF16, ls) for i in range(NOB)]
    b_ob = [Buf(f"ob{i}") for i in range(NOB)]
    osem = [P.new_dsem(f"ip_os{i}") for i in range(NOB)]
    NOF = 3
    of = [k.sb(f"ip_of{i}", [128, 512], F32, ls) for i in range(NOF)]
    b_of = [Buf(f"of{i}") for i in range(NOF)]
    fsem = [P.new_dsem(f"ip_fs{i}") for i in range(NOF)]
    t1 = [k.sb(f"ip_t1{i}", [128, 512], F32, ls) for i in range(2)]
    b_t1 = [Buf(f"t1{i}") for i in range(2)]
    t2 = [k.sb(f"ip_t2{i}", [128, 512], F32, ls) for i in range(2)]
    b_t2 = [Buf(f"t2{i}") for i in range(2)]
    pb = [k.ps(f"ip_ps{i}", [128, 512], F32, ls) for i in range(4)]
    b_pb = [Buf(f"ipps{i}") for i in range(4)]
    pr = [k.ps(f"ip_pr{i}", [128, 512], F32, ls) for i in range(2)]
    b_pr = [Buf(f"ippr{i}") for i in range(2)]
    wv = I["w_in"].rearrange("(kc p) c -> p kc c", p=128)
    cnt = {"pb": 0, "ob": 0, "of": 0, "r": 0, "ld": 0, "ev": 0}

    rope_pending = []

    def load_block(cb):
        s2 = cb % 2
        for q4 in range(4):
            P.dma("pool", wb[s2][:, q4 * 4:(q4 + 1) * 4, :], wv[:, q4 * 4:(q4 + 1) * 4, cb * 512:(cb + 1) * 512], writes=[b_wbq[s2][q4]])

    def next_ob():
        i = cnt["ob"] % NOB
        cnt["ob"] += 1
        return i

    def evac_eng():
        cnt["ev"] += 1
        return "act" if cnt["ev"] % 2 else "dve"

    def tiles_of(tok0, n, kc):
        return [b_hT[t][kc] for t in range(tok0 // 128, (tok0 + n) // 128)]

    def fm_unit(cb, fc, tok0, n, kind, row0, dst):
        s2 = cb % 2
        pi = cnt["pb"] % 4
        cnt["pb"] += 1
        pt, bp = pb[pi], b_pb[pi]
        for kc in range(NKC):
            P.op("pe", lambda e, kc=kc: e.matmul(pt[:, 0:n], lhsT=wb[s2][:, kc, fc * 128:(fc + 1) * 128],
                                                 rhs=hT[:, kc, tok0:tok0 + n], start=(kc == 0), stop=(kc == NKC - 1)),
                 reads=[b_wbq[s2][kc // 4]] + tiles_of(tok0, n, kc), writes=[bp])
        while rope_pending:
            rope_pending.pop(0)()
        oi = next_ob()
        if kind == "rope":
            ri = cnt["r"] % 2
            cnt["r"] += 1
            fi = cnt["of"] % NOF
            cnt["of"] += 1
            P.op("act", lambda e: e.activation(out=of[fi][:, 0:n], in_=pt[:, 0:n], func=AF.Copy), reads=[bp], writes=[b_of[fi]])
            P.op("dve", lambda e: e.tensor_tensor(out=t1[ri][:, 0:n], in0=of[fi][:, 0:n], in1=cosT[:, tok0:tok0 + n], op=ALU.mult),
                 reads=[b_of[fi], b_cos], writes=[b_t1[ri]])

            def fin():
                P.op("pe", lambda e: e.matmul(pr[ri][:, 0:n], lhsT=perm[:], rhs=of[fi][:, 0:n], start=True, stop=True),
                     reads=[b_perm, b_of[fi]], writes=[b_pr[ri]])
                P.op("dve", lambda e: e.tensor_tensor(out=t2[ri][:, 0:n], in0=pr[ri][:, 0:n], in1=sinS[:, tok0:tok0 + n], op=ALU.mult),
                     reads=[b_pr[ri], b_sin], writes=[b_t2[ri]])
                P.op("pool", lambda e: e.tensor_tensor(out=ob[oi][:, 0:n], in0=t1[ri][:, 0:n], in1=t2[ri][:, 0:n], op=ALU.add),
                     reads=[b_t1[ri], b_t2[ri]], writes=[b_ob[oi]])
                P.dma("sp", dst, ob[oi][:, 0:n], reads=[b_ob[oi]], sem=osem[oi])
            rope_pending.append(fin)
            return
        elif kind == "copy":
            eg = evac_eng()
            if eg == "act":
                P.op("act", lambda e: e.activation(out=ob[oi][:, 0:n], in_=pt[:, 0:n], func=AF.Copy), reads=[bp], writes=[b_ob[oi]])
            else:
                P.op("dve", lambda e: e.tensor_copy(out=ob[oi][:, 0:n], in_=pt[:, 0:n]), reads=[bp], writes=[b_ob[oi]])
        else:
            fn = AF.Silu if kind == "silu" else AF.Sigmoid
            P.op("act", lambda e: e.activation(out=ob[oi][:, 0:n], in_=pt[:, 0:n], func=fn), reads=[bp], writes=[b_ob[oi]])
        P.dma("sp", dst, ob[oi][:, 0:n], reads=[b_ob[oi]], sem=osem[oi])

    def tm_unit(cb, t, kind, dst):
        s2 = cb % 2
        pi = cnt["pb"] % 4
        cnt["pb"] += 1
        pt, bp = pb[pi], b_pb[pi]
        for kc in range(NKC):
            P.op("pe", lambda e, kc=kc: e.matmul(pt[:], lhsT=hT[:, kc, t * 128:(t + 1) * 128], rhs=wb[s2][:, kc, :],
                                                 start=(kc == 0), stop=(kc == NKC - 1)),
                 reads=[b_wbq[s2][kc // 4], b_hT[t][kc]], writes=[bp])
        while rope_pending:
            rope_pending.pop(0)()
        if kind == "f32":
            fi = cnt["of"] % NOF
            cnt["of"] += 1
            P.op("dve", lambda e: e.tensor_copy(out=of[fi][:], in_=pt[:]), reads=[bp], writes=[b_of[fi]])
            P.dma("sp", dst, of[fi][:], reads=[b_of[fi]], sem=fsem[fi])
            return
        oi = next_ob()
        if kind == "copy":
            eg = evac_eng()
            if eg == "act":
                P.op("act", lambda e: e.activation(out=ob[oi][:], in_=pt[:], func=AF.Copy), reads=[bp], writes=[b_ob[oi]])
            else:
                P.op("dve", lambda e: e.tensor_copy(out=ob[oi][:], in_=pt[:]), reads=[bp], writes=[b_ob[oi]])
        else:
            P.op("act", lambda e: e.activation(out=ob[oi][:], in_=pt[:], func=AF.Silu), reads=[bp], writes=[b_ob[oi]])
        P.dma("sp", dst, ob[oi][:], reads=[b_ob[oi]], sem=osem[oi])

    NCB = INW // 512
    load_block(0)
    for cb in range(NCB):
        if cb + 1 < NCB:
            load_block(cb + 1)
        c0 = cb * 512
        if cb < 2:
            for fc in range(4):
                h = cb * 4 + fc
                for tb in range(4):
                    fm_unit(cb, fc, tb * 512, 512, "rope", 0, S["qT"][h, :, tb * 512:(tb + 1) * 512])
        elif cb < 4:
            for fc in range(4):
                h = (cb - 2) * 4 + fc
                for tb in range(4):
                    fm_unit(cb, fc, tb * 512, 512, "rope", 0, S["kT"][h, :, tb * 512:(tb + 1) * 512])
                fm_unit(cb, fc, L, LC, "copy", 0, S["kT"][h, :, L:LT])
        elif cb < 6:
            for t in range(18):
                tm_unit(cb, t, "copy", S["v"][t * 128:(t + 1) * 128, (cb - 4) * 512:(cb - 3) * 512])
        elif cb < 8:
            for t in range(16):
                tm_unit(cb, t, "silu", S["sga"][t * 128:(t + 1) * 128, (cb - 6) * 512:(cb - 5) * 512])
        elif cb == 8:
            for t in range(18):
                tm_unit(cb, t, "f32", S["u"][t * 128:(t + 1) * 128, :])
        elif cb == 9:
            for fc in range(4):
                for tb in range(4):
                    fm_unit(cb, fc, tb * 512, 512, "silu", 0, S["sgsT"][fc * 128:(fc + 1) * 128, tb * 512:(tb + 1) * 512])
        else:
            for fc in range(4):
                r0 = (cb - 10) * 512 + fc * 128
                for tb in range(4):
                    fm_unit(cb, fc, tb * 512, 512, "sigm", 0, S["sgmT"][r0:r0 + 128, tb * 512:(tb + 1) * 512])


def phase_ssm(k):
    nc, P, I, S = k.nc, k.P, k.I, k.S
    MUL, ADD, SUB = ALU.mult, ALU.add, ALU.subtract
    with ExitStack() as ls:
        ToepT = k.sb("ss_toep", [128, 32, 128], BF16, ls)
        RCp = k.sb("ss_rcp", [128, 2, 2, 16, 256], BF16, ls)
        WT = k.sb("ss_wt", [128, 2, 16, 2, 128], BF16, ls)
        A8c = k.sb("ss_a8c", [128, 2, 16, 2], F32, ls)
        A8s = k.sb("ss_a8s", [128, 2, 16, 2], F32, ls)
        AKc = k.sb("ss_akc", [128, 2, 8, 16, 2], F32, ls)
        AKs = k.sb("ss_aks", [128, 2, 8, 16, 2], F32, ls)
        b_ak = Buf("ak")
        b_toep = [Buf(f"toep{g}") for g in range(32)]
        b_rcp = Buf("rcp")
        b_wtl = [[[Buf(f"wt{d}_{g2}_{ri}") for ri in range(2)] for g2 in range(16)] for d in range(2)]
        b_U = [Buf(f"U{g}") for g in range(32)]
        b_zbf = [Buf("zbf0"), Buf("zbf1")]
        b_ygT = Buf("ygT")
        b_a8 = Buf("a8")
        pbk = [k.ps(f"ss_ps{i}", [128, 512], F32, ls) for i in range(8)]
        b_pbk = [Buf(f"ssps{i}") for i in range(8)]
        pc = {"i": 0}

        def nb():
            i = pc["i"] % 8
            pc["i"] += 1
            return pbk[i], b_pbk[i]

        csem = P.new_dsem("ss_c")
        with ExitStack() as l0:
            lre = k.sb("ss_lre", [128, 32], F32, l0)
            lim = k.sb("ss_lim", [128, 32], F32, l0)
            dtt = k.sb("ss_dt", [128, 32], F32, l0)
            alog = k.sb("ss_alog", [128, 32], F32, l0)
            th = k.sb("ss_th", [128, 32], F32, l0)
            b_l, b_dt, b_al = Buf("lrelim"), Buf("dtt"), Buf("alogth")
            for gp in range(2):
                for d in range(2):
                    P.dma("sp", lre[gp * 64:(gp + 1) * 64, d * 16:(d + 1) * 16], I["ssm_lre"][d, gp * 16:(gp + 1) * 16, :].rearrange("g p -> p g"),
                          writes=[b_l], sem=csem, allow_slow_non_contiguous=True)
                    P.dma("sp", lim[gp * 64:(gp + 1) * 64, d * 16:(d + 1) * 16], I["ssm_lim"][d, gp * 16:(gp + 1) * 16, :].rearrange("g p -> p g"),
                          writes=[b_l], sem=csem, allow_slow_non_contiguous=True)
                    P.dma("sp", dtt[gp * 64:(gp + 1) * 64, d * 16:(d + 1) * 16], I["ssm_ls"][d:d + 1, gp * 16:(gp + 1) * 16].broadcast_to([64, 16]),
                          writes=[b_dt], sem=csem)
            P.op("act", lambda e: e.activation(out=dtt[:], in_=dtt[:], func=AF.Exp), reads=[b_dt], writes=[b_dt])
            P.op("dve", lambda e: e.tensor_tensor(out=alog[:], in0=lre[:], in1=dtt[:], op=MUL), reads=[b_l, b_dt], writes=[b_al])
            P.op("dve", lambda e: e.scalar_tensor_tensor(out=th[:], in0=lim[:], scalar=1.0 / TWO_PI, in1=dtt[:], op0=MUL, op1=MUL),
                 reads=[b_l, b_dt], writes=[b_al])
            tabs = {}
            for nm, n in (("A", 9), ("B", 8), ("C", 8)):
                tau = k.sb(f"ss_tau{nm}", [128, 32, n], F32, l0)
                ex = k.sb(f"ss_ex{nm}", [128, 32, n], F32, l0)
                yv = k.sb(f"ss_yv{nm}", [128, 32, n], F32, l0)
                sn = k.sb(f"ss_sn{nm}", [128, 32, n], F32, l0)
                cs = k.sb(f"ss_cs{nm}", [128, 32, n], F32, l0)
                b_tau, b_ex, b_yv, b_sn, b_cs = Buf("tau" + nm), Buf("ex" + nm), Buf("yv" + nm), Buf("sn" + nm), Buf("cs" + nm)
                P.dma("sp", tau[:], I["c_tau" + nm], writes=[b_tau], sem=csem)
                P.op("dve", lambda e, ex=ex, tau=tau, n=n: e.tensor_tensor(out=ex[:], in0=tau[:], in1=alog[:, :, None].broadcast_to([128, 32, n]), op=MUL),
                     reads=[b_tau, b_al], writes=[b_ex])
                P.op("act", lambda e, ex=ex: e.activation(out=ex[:], in_=ex[:], func=AF.Exp), reads=[b_ex], writes=[b_ex])
                P.op("dve", lambda e, yv=yv, tau=tau, n=n: e.tensor_tensor(out=yv[:], in0=tau[:], in1=th[:, :, None].broadcast_to([128, 32, n]), op=MUL),
                     reads=[b_tau, b_al], writes=[b_yv])
                fl = lambda t: t[:].rearrange("p a b -> p (a b)")
                range_sin(k, l0, fl(sn), fl(yv), [128, 32 * n], "ssr1" + nm, [b_yv], [b_sn])
                P.op("dve", lambda e, yv=yv: e.tensor_scalar(out=yv[:], in0=yv[:], scalar1=0.25, scalar2=None, op0=ADD), reads=[b_yv], writes=[b_yv])
                range_sin(k, l0, fl(cs), fl(yv), [128, 32 * n], "ssr2" + nm, [b_yv], [b_cs])
                P.op("dve", lambda e, cs=cs, ex=ex: e.tensor_tensor(out=cs[:], in0=cs[:], in1=ex[:], op=MUL), reads=[b_cs, b_ex], writes=[b_cs])
                P.op("dve", lambda e, sn=sn, ex=ex: e.tensor_tensor(out=sn[:], in0=sn[:], in1=ex[:], op=MUL), reads=[b_sn, b_ex], writes=[b_sn])
                tabs[nm] = (cs, sn, b_cs, b_sn)
            ARA, AIA, b_ARA, b_AIA = tabs["A"]
            ARB, AIB, b_ARB, b_AIB = tabs["B"]
            ARC, AIC, b_ARC, b_AIC = tabs["C"]
            for d in range(2):
                dsl = slice(d * 16, (d + 1) * 16)
                for ri in range(2):
                    P.op("dve", lambda e, d=d, ri=ri, dsl=dsl: e.tensor_copy(out=AKc[:, d, :, :, ri], in_=ARC[:, dsl, :].rearrange("p g k -> p k g")),
                         reads=[b_ARC], writes=[b_ak])
                P.op("dve", lambda e, d=d, dsl=dsl: e.tensor_scalar(out=AKs[:, d, :, :, 0], in0=AIC[:, dsl, :].rearrange("p g k -> p k g"),
                                                                   scalar1=-1.0, scalar2=None, op0=MUL), reads=[b_AIC], writes=[b_ak])
                P.op("dve", lambda e, d=d, dsl=dsl: e.tensor_copy(out=AKs[:, d, :, :, 1], in_=AIC[:, dsl, :].rearrange("p g k -> p k g")),
                     reads=[b_AIC], writes=[b_ak])
            a1 = k.sb("ss_a1", [128, 2, 32], F32, l0)
            b_a1 = Buf("a1")
            for d in range(2):
                i8 = 8 if d == 0 else 0
                i1 = 1 if d == 0 else 7
                dsl = slice(d * 16, (d + 1) * 16)
                for ri in range(2):
                    P.op("dve", lambda e, d=d, ri=ri, i8=i8, dsl=dsl: e.tensor_copy(out=A8c[:, d, :, ri], in_=ARA[:, dsl, i8]), reads=[b_ARA], writes=[b_a8])
                P.op("dve", lambda e, d=d, i8=i8, dsl=dsl: e.tensor_scalar(out=A8s[:, d, :, 0], in0=AIA[:, dsl, i8], scalar1=-1.0, scalar2=None, op0=MUL),
                     reads=[b_AIA], writes=[b_a8])
                P.op("dve", lambda e, d=d, i8=i8, dsl=dsl: e.tensor_copy(out=A8s[:, d, :, 1], in_=AIA[:, dsl, i8]), reads=[b_AIA], writes=[b_a8])
                P.op("dve", lambda e, d=d, i1=i1, dsl=dsl: e.tensor_copy(out=a1[:, 0, dsl], in_=ARA[:, dsl, i1]), reads=[b_ARA], writes=[b_a1])
                P.op("dve", lambda e, d=d, i1=i1, dsl=dsl: e.tensor_copy(out=a1[:, 1, dsl], in_=AIA[:, dsl, i1]), reads=[b_AIA], writes=[b_a1])
            fz = k.sb("ss_fz", [128, 6, 32], F32, l0)
            b_fz = Buf("fz")
            P.op("dve", lambda e: e.tensor_tensor(out=fz[:, 0, :], in0=lre[:], in1=lre[:], op=MUL), reads=[b_l], writes=[b_fz])
            P.op("dve", lambda e: e.tensor_tensor(out=fz[:, 1, :], in0=lim[:], in1=lim[:], op=MUL), reads=[b_l], writes=[b_fz])
            P.op("dve", lambda e: e.tensor_tensor(out=fz[:, 0, :], in0=fz[:, 0, :], in1=fz[:, 1, :], op=ADD), reads=[b_fz], writes=[b_fz])
            P.op("dve", lambda e: e.reciprocal(out=fz[:, 1, :], in_=fz[:, 0, :]), reads=[b_fz], writes=[b_fz])
            P.op("dve", lambda e: e.tensor_scalar(out=fz[:, 0, :], in0=a1[:, 0, :], scalar1=-1.0, scalar2=None, op0=ADD), reads=[b_a1], writes=[b_fz])
            P.op("dve", lambda e: e.tensor_tensor(out=fz[:, 2, :], in0=fz[:, 0, :], in1=lre[:], op=MUL), reads=[b_fz, b_l], writes=[b_fz])
            P.op("dve", lambda e: e.tensor_tensor(out=fz[:, 3, :], in0=a1[:, 1, :], in1=lim[:], op=MUL), reads=[b_a1, b_l], writes=[b_fz])
            P.op("dve", lambda e: e.tensor_tensor(out=fz[:, 2, :], in0=fz[:, 2, :], in1=fz[:, 3, :], op=ADD), reads=[b_fz], writes=[b_fz])
            P.op("dve", lambda e: e.tensor_tensor(out=fz[:, 2, :], in0=fz[:, 2, :], in1=fz[:, 1, :], op=MUL), reads=[b_fz], writes=[b_fz])
            P.op("dve", lambda e: e.tensor_tensor(out=fz[:, 4, :], in0=a1[:, 1, :], in1=lre[:], op=MUL), reads=[b_a1, b_l], writes=[b_fz])
            P.op("dve", lambda e: e.tensor_tensor(out=fz[:, 5, :], in0=fz[:, 0, :], in1=lim[:], op=MUL), reads=[b_fz, b_l], writes=[b_fz])
            P.op("dve", lambda e: e.tensor_tensor(out=fz[:, 4, :], in0=fz[:, 4, :], in1=fz[:, 5, :], op=SUB), reads=[b_fz], writes=[b_fz])
            P.op("dve", lambda e: e.tensor_tensor(out=fz[:, 4, :], in0=fz[:, 4, :], in1=fz[:, 1, :], op=MUL), reads=[b_fz], writes=[b_fz])
            BT = k.sb("ss_BT", [128, 2, 2, 16, 16], F32, l0)
            BB = k.sb("ss_BB", [128, 2, 2, 16, 16], F32, l0)
            CN = k.sb("ss_CN", [128, 2, 2, 2, 128], F32, l0)
            CT = k.sb("ss_CT", [128, 2, 2, 16, 16], F32, l0)
            tA = k.sb("ss_tA", [128, 16, 9, 16], F32, l0)
            tB = k.sb("ss_tB", [128, 16, 9, 16], F32, l0)
            b_BT, b_BB, b_CN, b_CT, b_tA, b_tB = Buf("BT"), Buf("BB"), Buf("CN"), Buf("CT"), Buf("tA"), Buf("tB")
            for d in range(2):
                for ri in range(2):
                    bsrc = I["ssm_bre"] if ri == 0 else I["ssm_bim"]
                    csrc = I["ssm_cre"] if ri == 0 else I["ssm_cim"]
                    for gp in range(2):
                        P.dma("sp", BT[gp * 64:(gp + 1) * 64, d, ri, :, :], bsrc[d, gp * 16:(gp + 1) * 16].rearrange("g p c -> p g c"),
                              writes=[b_BT], sem=csem)
                        for blk in range(2):
                            g0 = gp * 16 + blk * 8
                            P.dma("sp", CN[:, d, ri, blk, gp * 64:(gp + 1) * 64], csrc[d, g0:g0 + 8].rearrange("g c p -> (g c) p"),
                                  writes=[b_CN], sem=csem)
            for d in range(2):
                for ri in range(2):
                    for blk in range(2):
                        pt, bp = nb()
                        P.op("pe", lambda e, d=d, ri=ri, blk=blk, pt=pt: e.transpose(out=pt[:, 0:128], in_=CN[:, d, ri, blk, :], identity=k.ident[:]),
                             reads=[b_CN, k.b_ident], writes=[bp])
                        P.op("dve", lambda e, d=d, ri=ri, blk=blk, pt=pt: e.tensor_copy(
                            out=CT[:, d, ri, blk * 8:(blk + 1) * 8, :].rearrange("p a b -> p (a b)"), in_=pt[:, 0:128]), reads=[bp], writes=[b_CT])
            for d in range(2):
                dsl = slice(d * 16, (d + 1) * 16)
                frb = lambda d=d, dsl=dsl: fz[:, 2, dsl][:, :, None].broadcast_to([128, 16, 16])
                fib = lambda d=d, dsl=dsl: fz[:, 4, dsl][:, :, None].broadcast_to([128, 16, 16])
                t16a = tA[:, :, 0, :]
                t16b = tB[:, :, 0, :]
                P.op("dve", lambda e, d=d, frb=frb: e.tensor_tensor(out=t16a, in0=BT[:, d, 0], in1=frb(), op=MUL), reads=[b_BT, b_fz], writes=[b_tA])
                P.op("dve", lambda e, d=d, fib=fib: e.tensor_tensor(out=t16b, in0=BT[:, d, 1], in1=fib(), op=MUL), reads=[b_BT, b_fz], writes=[b_tB])
                P.op("dve", lambda e, d=d: e.tensor_tensor(out=BB[:, d, 0], in0=t16a, in1=t16b, op=SUB), reads=[b_tA, b_tB], writes=[b_BB])
                P.op("dve", lambda e, d=d, frb=frb: e.tensor_tensor(out=t16a, in0=BT[:, d, 1], in1=frb(), op=MUL), reads=[b_BT, b_fz], writes=[b_tA])
                P.op("dve", lambda e, d=d, fib=fib: e.tensor_tensor(out=t16b, in0=BT[:, d, 0], in1=fib(), op=MUL), reads=[b_BT, b_fz], writes=[b_tB])
                P.op("dve", lambda e, d=d: e.tensor_tensor(out=BB[:, d, 1], in0=t16a, in1=t16b, op=ADD), reads=[b_tA, b_tB], writes=[b_BB])
            P.op("pool", lambda e: e.memset(RCp[:].rearrange("p a b c d -> p (a b c d)"), 0.0), writes=[b_rcp])
            for d in range(2):
                dsl = slice(d * 16, (d + 1) * 16)
                off = 112 if d == 0 else 0
                bc_c = lambda ri, d=d: CT[:, d, ri][:, :, None, :].broadcast_to([128, 16, 9, 16])
                bc_ar = lambda dsl=dsl: ARA[:, dsl, :][:, :, :, None].broadcast_to([128, 16, 9, 16])
                bc_ai = lambda dsl=dsl: AIA[:, dsl, :][:, :, :, None].broadcast_to([128, 16, 9, 16])
                dst = lambda ri, d=d, off=off: RCp[:, d, ri, :, off:off + 144].rearrange("p g (t c) -> p g t c", c=16)
                P.op("dve", lambda e, bc_c=bc_c, bc_ar=bc_ar: e.tensor_tensor(out=tA[:], in0=bc_c(0), in1=bc_ar(), op=MUL), reads=[b_CT, b_ARA], writes=[b_tA])
                P.op("dve", lambda e, bc_c=bc_c, bc_ai=bc_ai: e.tensor_tensor(out=tB[:], in0=bc_c(1), in1=bc_ai(), op=MUL), reads=[b_CT, b_AIA], writes=[b_tB])
                P.op("dve", lambda e, dst=dst: e.tensor_tensor(out=dst(0), in0=tA[:], in1=tB[:], op=SUB), reads=[b_tA, b_tB], writes=[b_rcp])
                P.op("dve", lambda e, bc_c=bc_c, bc_ai=bc_ai: e.tensor_tensor(out=tA[:], in0=bc_c(0), in1=bc_ai(), op=MUL), reads=[b_CT, b_AIA], writes=[b_tA])
                P.op("dve", lambda e, bc_c=bc_c, bc_ar=bc_ar: e.tensor_tensor(out=tB[:], in0=bc_c(1), in1=bc_ar(), op=MUL), reads=[b_CT, b_ARA], writes=[b_tB])
                P.op("dve", lambda e: e.tensor_tensor(out=tA[:], in0=tA[:], in1=tB[:], op=ADD), reads=[b_tA, b_tB], writes=[b_tA])
                P.op("dve", lambda e, dst=dst: e.tensor_scalar(out=dst(1), in0=tA[:], scalar1=-1.0, scalar2=None, op0=MUL), reads=[b_tA], writes=[b_rcp])
            Lp = k.sb("ss_Lp", [128, 64, 240], BF16, l0)
            b_Lp = Buf("Lp")
            P.op("pool", lambda e: e.memset(Lp[:].rearrange("p a b -> p (a b)"), 0.0), writes=[b_Lp])
            P.op("pool", lambda e: e.tensor_copy(out=Lp[:, :, 112:128], in_=BB[:].rearrange("p a b c d -> p (a b c) d")), reads=[b_BB], writes=[b_Lp])
            BW = k.sb("ss_BW", [128, 2, 2, 16, 128], F32, l0)
            b_BW = Buf("BW")
            for d in range(2):
                dsl = slice(d * 16, (d + 1) * 16)
                bc_b = lambda ri, d=d: BB[:, d, ri][:, :, None, :].broadcast_to([128, 16, 8, 16])
                bc_ar = lambda dsl=dsl: ARB[:, dsl, :][:, :, :, None].broadcast_to([128, 16, 8, 16])
                bc_ai = lambda dsl=dsl: AIB[:, dsl, :][:, :, :, None].broadcast_to([128, 16, 8, 16])
                dst = lambda ri, d=d: BW[:, d, ri].rearrange("p g (t c) -> p g t c", c=16)
                ta8 = tA[:, :, 0:8, :]
                tb8 = tB[:, :, 0:8, :]
                P.op("dve", lambda e, bc_b=bc_b, bc_ar=bc_ar: e.tensor_tensor(out=ta8, in0=bc_b(0), in1=bc_ar(), op=MUL), reads=[b_BB, b_ARB], writes=[b_tA])
                P.op("dve", lambda e, bc_b=bc_b, bc_ai=bc_ai: e.tensor_tensor(out=tb8, in0=bc_b(1), in1=bc_ai(), op=MUL), reads=[b_BB, b_AIB], writes=[b_tB])
                P.op("dve", lambda e, dst=dst: e.tensor_tensor(out=dst(0), in0=ta8, in1=tb8, op=SUB), reads=[b_tA, b_tB], writes=[b_BW])
                P.op("dve", lambda e, bc_b=bc_b, bc_ai=bc_ai: e.tensor_tensor(out=ta8, in0=bc_b(0), in1=bc_ai(), op=MUL), reads=[b_BB, b_AIB], writes=[b_tA])
                P.op("dve", lambda e, bc_b=bc_b, bc_ar=bc_ar: e.tensor_tensor(out=tb8, in0=bc_b(1), in1=bc_ar(), op=MUL), reads=[b_BB, b_ARB], writes=[b_tB])
                P.op("dve", lambda e, dst=dst: e.tensor_tensor(out=dst(1), in0=ta8, in1=tb8, op=ADD), reads=[b_tA, b_tB], writes=[b_BW])
            for d in range(2):
                for g2 in range(16):
                    for ri in range(2):
                        pt, bp = nb()
                        P.op("pe", lambda e, d=d, g2=g2, ri=ri, pt=pt: e.transpose(out=pt[:, 0:128], in_=BW[:, d, ri, g2, :], identity=k.ident[:]),
                             reads=[b_BW, k.b_ident], writes=[bp])
                        eng = "act" if (g2 + ri) % 2 else "dve"
                        if eng == "act":
                            P.op("act", lambda e, d=d, g2=g2, ri=ri, pt=pt: e.activation(out=WT[:, d, g2, ri, :], in_=pt[:, 0:128], func=AF.Copy), reads=[bp], writes=[b_wtl[d][g2][ri]])
                        else:
                            P.op("dve", lambda e, d=d, g2=g2, ri=ri, pt=pt: e.tensor_copy(out=WT[:, d, g2, ri, :], in_=pt[:, 0:128]), reads=[bp], writes=[b_wtl[d][g2][ri]])
            for g2 in range(16):
                for gp in range(2):
                    g = gp * 16 + g2
                    pt, bp = nb()
                    psl = slice(gp * 64, (gp + 1) * 64)
                    n = 0
                    for d in range(2):
                        for ri in range(2):
                            for s_ in range(8):
                                w0 = (7 - s_) * 16 if d == 0 else (8 - s_) * 16
                                l0_ = (7 - s_) * 16
                                P.op("pe", lambda e, d=d, ri=ri, g2=g2, w0=w0, l0_=l0_, psl=psl, pt=pt, n=n: e.matmul(
                                    pt[:, 0:128], lhsT=Lp[psl, (d * 2 + ri) * 16 + g2, l0_:l0_ + 128], rhs=RCp[psl, d, ri, g2, w0:w0 + 128],
                                    start=(n == 0), stop=(n == 31)), reads=[b_Lp, b_rcp], writes=[bp])
                                n += 1
                    P.op("dve" if g % 2 else "act",
                         (lambda e, g=g, pt=pt: e.tensor_copy(out=ToepT[:, g, :], in_=pt[:, 0:128])) if g % 2 else
                         (lambda e, g=g, pt=pt: e.activation(out=ToepT[:, g, :], in_=pt[:, 0:128], func=AF.Copy)),
                         reads=[bp], writes=[b_toep[g]])
            P.barrier()
        Ubuf = k.sb("ss_ubuf", [128, 32, 320], BF16, ls)
        Zbf = k.sb("ss_zbf", [128, 2, 16, 2, 288], BF16, ls)
        if k.debug.get("_ssm_upto", 99) < 1:
            return
        with ExitStack() as l1:
            ucm = [k.sb(f"ss_ucm{i}", [128, 8, 512], F32, l1) for i in range(2)]
            b_ucm = [Buf(f"ucm{i}") for i in range(2)]
            usem = [P.new_dsem(f"ss_us{i}") for i in range(2)]
            ucg = k.sb("ss_ucg", [128, 32, 128], F32, l1)
            b_ucg = Buf("ucg")
            for jt in range(3):
                si = jt % 2
                nj = 128 if jt < 2 else 32
                r0 = jt * 1024
                P.dma("sp", ucm[si][0:nj], S["u"][r0:r0 + nj * 8, :].rearrange("(j s) c -> j s c", s=8), writes=[b_ucm[si]], sem=usem[si])
                P.op("dve", lambda e, si=si, nj=nj: e.tensor_copy(out=ucg[0:nj].rearrange("p g (s c) -> p g s c", c=16),
                                                                 in_=ucm[si][0:nj].rearrange("p s (g c) -> p g s c", c=16)),
                     reads=[b_ucm[si]], writes=[b_ucg])
                for g0 in range(0, 32, 4):
                    pt, bp = nb()
                    for gg in range(4):
                        g = g0 + gg
                        P.op("pe", lambda e, si=si, nj=nj, g=g, gg=gg, pt=pt: e.transpose(
                            out=pt[:, gg * 128:gg * 128 + nj], in_=ucg[0:nj, g, :], identity=k.ident[0:nj, 0:nj]),
                            reads=[b_ucg, k.b_ident], writes=[bp])
                    src = lambda pt=pt, nj=nj: pt[:].rearrange("p (a b) -> p a b", b=128)[:, :, 0:nj]
                    cols = [32 + jt * 128] if jt < 2 else [0, 288]
                    for ci, c0 in enumerate(cols):
                        eng = "act" if (g0 // 4 + ci) % 2 else "dve"
                        if eng == "act":
                            P.op("act", lambda e, g0=g0, c0=c0, nj=nj, src=src: e.activation(out=Ubuf[:, g0:g0 + 4, c0:c0 + nj], in_=src(), func=AF.Copy),
                                 reads=[bp], writes=[b_U[g0 + i] for i in range(4)])
                        else:
                            P.op("dve", lambda e, g0=g0, c0=c0, nj=nj, src=src: e.tensor_copy(out=Ubuf[:, g0:g0 + 4, c0:c0 + nj], in_=src()),
                                 reads=[bp], writes=[b_U[g0 + i] for i in range(4)])
            P.barrier()
        if k.debug.get("_ssm_upto", 99) < 2:
            return
        with ExitStack() as l2:
            Z = [k.sb(f"ss_Z{d}", [128, 16, 2, 288], F32, l2) for d in range(2)]
            b_Z = [Buf("Z0"), Buf("Z1")]
            b_Ze = [[Buf(f"Ze{d}_{i}") for i in range(32)] for d in range(2)]
            zjoin = {0: True, 1: True}
            for d in range(2):
                j0 = 0 if d == 0 else 32
                for g2 in range(16):
                    for ri in range(2):
                        pt, bp = nb()
                        for gp in range(2):
                            P.op("pe", lambda e, d=d, g2=g2, ri=ri, gp=gp, pt=pt, j0=j0: e.matmul(
                                pt[gp * 64:(gp + 1) * 64, 0:288], lhsT=WT[:, d, g2, ri, gp * 64:(gp + 1) * 64], rhs=Ubuf[:, gp * 16 + g2, j0:j0 + 288],
                                start=True, stop=True), reads=[b_wtl[d][g2][ri], b_U[gp * 16 + g2]], writes=[bp])
                        if (g2 + ri) % 2:
                            P.op("act", lambda e, d=d, g2=g2, ri=ri, pt=pt: e.activation(out=Z[d][:, g2, ri, :], in_=pt[:, 0:288], func=AF.Copy), reads=[bp], writes=[b_Ze[d][g2 * 2 + ri]])
                        else:
                            P.op("dve", lambda e, d=d, g2=g2, ri=ri, pt=pt: e.tensor_copy(out=Z[d][:, g2, ri, :], in_=pt[:, 0:288]), reads=[bp], writes=[b_Ze[d][g2 * 2 + ri]])
            k.dbg("V_dbg", [2, 128, 16 * 2 * 288], F32, lambda dd: (dd[0], Z[0][:].rearrange("p a b c -> p (a b c)")), b_Ze[0])
            k.dbg("V_dbg", [2, 128, 16 * 2 * 288], F32, lambda dd: (dd[1], Z[1][:].rearrange("p a b c -> p (a b c)")), [b_Z[1]])
            m1 = [k.sb(f"ss_m1{d}", [128, 16, 2, 36], F32, l2) for d in range(2)]
            m2 = [k.sb(f"ss_m2{d}", [128, 16, 2, 36], F32, l2) for d in range(2)]
            b_m1 = [Buf("m10"), Buf("m11")]
            b_m2 = [Buf("m20"), Buf("m21")]
            Zv = [Z[d][:].rearrange("p g r (K c) -> p g r K c", c=8) for d in range(2)]

            def cmul_add(eng, d, kidx, dst, src, src_sw, nK):
                if nK:
                    ac = AKc[:, d, kidx][:, :, :, None].broadcast_to([128, 16, 2, nK])
                    as_ = AKs[:, d, kidx][:, :, :, None].broadcast_to([128, 16, 2, nK])
                    t1_, t2_ = m1[d][:, :, :, 0:nK], m2[d][:, :, :, 0:nK]
                else:
                    ac, as_ = AKc[:, d, kidx], AKs[:, d, kidx]
                    t1_, t2_ = m1[d][:, :, :, 0], m2[d][:, :, :, 0]
                extra = []
                if zjoin[d]:
                    zjoin[d] = False
                    extra = list(b_Ze[d])
                P.op(eng, lambda e: e.tensor_tensor(out=t1_, in0=src, in1=ac, op=MUL), reads=[b_Z[d], b_ak] + extra, writes=[b_m1[d]])
                P.op(eng, lambda e: e.tensor_tensor(out=t2_, in0=src_sw, in1=as_, op=MUL), reads=[b_Z[d], b_ak], writes=[b_m2[d]])
                P.op(eng, lambda e: e.tensor_tensor(out=t1_, in0=t1_, in1=t2_, op=ADD), reads=[b_m1[d], b_m2[d]], writes=[b_m1[d]])
                P.op(eng, lambda e: e.tensor_tensor(out=dst, in0=dst, in1=t1_, op=ADD), reads=[b_Z[d], b_m1[d]], writes=[b_Z[d]])

            for i in range(1, 8):
                r = i
                cmul_add("dve", 0, 0, Zv[0][:, :, :, :, r], Zv[0][:, :, :, :, r - 1], Zv[0][:, :, ::-1, :, r - 1], 36)
                r = 7 - i
                cmul_add("dve", 1, 0, Zv[1][:, :, :, :, r], Zv[1][:, :, :, :, r + 1], Zv[1][:, :, ::-1, :, r + 1], 36)
            for i in range(1, 36):
                K = i
                cmul_add("dve", 0, 7, Zv[0][:, :, :, K, 7], Zv[0][:, :, :, K - 1, 7], Zv[0][:, :, ::-1, K - 1, 7], 0)
                K = 35 - i
                cmul_add("pool", 1, 7, Zv[1][:, :, :, K, 0], Zv[1][:, :, :, K + 1, 0], Zv[1][:, :, ::-1, K + 1, 0], 0)
            for i in range(7):
                r = i
                cmul_add("dve", 0, r, Zv[0][:, :, :, 1:36, r], Zv[0][:, :, :, 0:35, 7], Zv[0][:, :, ::-1, 0:35, 7], 35)
                r = 7 - i
                cmul_add("dve", 1, 7 - r, Zv[1][:, :, :, 0:35, r], Zv[1][:, :, :, 1:36, 0], Zv[1][:, :, ::-1, 1:36, 0], 35)
            for d in range(2):
                eng = "dve" if d == 0 else "pool"
                P.op(eng, lambda e, d=d: e.tensor_copy(out=Zbf[:, d].rearrange("p a b c -> p (a b c)"), in_=Z[d][:].rearrange("p a b c -> p (a b c)")),
                     reads=[b_Z[d]], writes=[b_zbf[d]])
            k.dbg("Z_dbg", [2, 128, 16 * 2 * 288], F32, lambda dd: (dd[0], Z[0][:].rearrange("p a b c -> p (a b c)")), [b_Z[0]])
            k.dbg("Z_dbg", [2, 128, 16 * 2 * 288], F32, lambda dd: (dd[1], Z[1][:].rearrange("p a b c -> p (a b c)")), [b_Z[1]])
            P.barrier()
        if k.debug.get("_ssm_upto", 99) < 3:
            return
        ygT = k.sb("ss_ygT", [128, 4, L], BF16, ls)
        with ExitStack() as l3:
            ycm = k.sb("ss_ycm", [128, 8, 512], F32, l3)
            b_ycm = [Buf(f"ycm{g}") for g in range(32)]
            ut = k.sb("ss_ut", [128, 8, 512], F32, l3)
            b_ut = Buf("ut")
            utsem = P.new_dsem("ss_uts")
            Dfull = k.sb("ss_D", [128, 512], F32, l3)
            b_D = Buf("Dfull")
            P.dma("sp", Dfull[:], I["ssm_d"][0:1, :].broadcast_to([128, 512]), writes=[b_D], sem=csem)
            sq = [k.sb(f"ss_sq{i}", [128, 512], F32, l3) for i in range(2)]
            b_sq = [Buf("sq0"), Buf("sq1")]
            GC = math.sqrt(2.0 / math.pi)
            for jt in range(2):
                P.dma("sp", ut[:], S["u"][jt * 1024:(jt + 1) * 1024, :].rearrange("(j s) c -> j s c", s=8), writes=[b_ut], sem=utsem)
                for g in range(32):
                    gp, g2 = g // 16, g % 16
                    psl = slice(gp * 64, (gp + 1) * 64)
                    pt, bp = nb()
                    c0 = 32 + jt * 128
                    P.op("pe", lambda e, g=g, c0=c0, pt=pt: e.matmul(pt[:, 0:128], lhsT=Ubuf[:, g, c0:c0 + 128], rhs=ToepT[:, g, :], start=True, stop=False),
                         reads=[b_U[g], b_toep[g]], writes=[bp])
                    for d in range(2):
                        jz = (31 + jt * 128) if d == 0 else (1 + jt * 128)
                        w0 = 128 if d == 0 else 0
                        for ri in range(2):
                            last = (d == 1 and ri == 1)
                            P.op("pe", lambda e, d=d, ri=ri, g2=g2, psl=psl, jz=jz, w0=w0, pt=pt, last=last: e.matmul(
                                pt[:, 0:128], lhsT=Zbf[psl, d, g2, ri, jz:jz + 128], rhs=RCp[psl, d, ri, g2, w0:w0 + 128], start=False, stop=last),
                                reads=[b_zbf[d], b_rcp], writes=[bp])
                    src = lambda pt=pt: pt[:, 0:128].rearrange("p (t c) -> p t c", c=16)
                    P.op("dve", lambda e, g=g, src=src: e.tensor_tensor(out=ycm[:, :, g * 16:(g + 1) * 16], in0=ut[:, :, g * 16:(g + 1) * 16],
                                                                       in1=Dfull[:, g * 16:(g + 1) * 16][:, None, :].broadcast_to([128, 8, 16]), op=MUL),
                         reads=[b_ut, b_D], writes=[b_ycm[g]])
                    P.op("dve", lambda e, g=g, src=src: e.tensor_tensor(out=ycm[:, :, g * 16:(g + 1) * 16], in0=ycm[:, :, g * 16:(g + 1) * 16], in1=src(), op=ADD),
                         reads=[bp, b_ycm[g]], writes=[b_ycm[g]])
                k.dbg("y_dbg", [L, 512], F32, lambda dd, jt=jt: (dd[jt * 1024:(jt + 1) * 1024, :].rearrange("(j s) c -> j s c", s=8), ycm[:]), b_ycm)
                for t in range(8):
                    i = t % 2
                    P.op("dve", lambda e, t=t, i=i: e.tensor_tensor(out=sq[i][:], in0=ycm[:, t, :], in1=ycm[:, t, :], op=MUL), reads=b_ycm, writes=[b_sq[i]])
                    P.op("dve", lambda e, t=t, i=i: e.tensor_scalar(out=sq[i][:], in0=sq[i][:], scalar1=0.044715, scalar2=1.0, op0=MUL, op1=ADD), reads=[b_sq[i]], writes=[b_sq[i]])
                    P.op("dve", lambda e, t=t, i=i: e.tensor_tensor(out=sq[i][:], in0=sq[i][:], in1=ycm[:, t, :], op=MUL), reads=[b_sq[i]] + b_ycm, writes=[b_sq[i]])
                    P.op("act", lambda e, t=t, i=i: e.activation(out=sq[i][:], in_=sq[i][:], func=AF.Sigmoid, scale=2.0 * GC), reads=[b_sq[i]], writes=[b_sq[i]])
                    P.op("dve", lambda e, t=t, i=i: e.tensor_tensor(out=sq[i][:], in0=sq[i][:], in1=ycm[:, t, :], op=MUL), reads=[b_sq[i]] + b_ycm, writes=[b_sq[i]])
                    pt, bp = nb()
                    for chb in range(4):
                        P.op("pe", lambda e, i=i, chb=chb, pt=pt: e.transpose(out=pt[:, chb * 128:(chb + 1) * 128], in_=sq[i][:, chb * 128:(chb + 1) * 128], identity=k.ident[:]),
                             reads=[b_sq[i], k.b_ident], writes=[bp])
                    tsl = slice(jt * 1024 + t, (jt + 1) * 1024, 8)
                    P.op("act", lambda e, pt=pt, tsl=tsl: e.activation(out=ygT[:, :, tsl], in_=pt[:].rearrange("p (a b) -> p a b", b=128), func=AF.Copy),
                         reads=[bp], writes=[b_ygT])
            P.barrier()
        if k.debug.get("_ssm_upto", 99) < 4:
            return
        with ExitStack() as l4:
            wg32 = k.sb("ss_wg32", [128, 4, 512], F32, l4)
            wg = k.sb("ss_wg", [128, 4, 512], BF16, l4)
            bg = k.sb("ss_bg", [128, 4], F32, l4)
            b_wg32, b_wg, b_bg = Buf("wg32"), Buf("wg"), Buf("bg")
            P.dma("sp", wg32[:], I["w_glu"].rearrange("(fc p) c -> p fc c", p=128), writes=[b_wg32], sem=csem)
            P.dma("sp", bg[:], I["b_glu"].rearrange("(fc p) -> p fc", p=128), writes=[b_bg], sem=csem, allow_slow_non_contiguous=True)
            P.op("dve", lambda e: e.tensor_copy(out=wg[:], in_=wg32[:]), reads=[b_wg32], writes=[b_wg])
            gst = [k.sb(f"ss_gst{i}", [128, 512], BF16, l4) for i in range(2)]
            b_gst = [Buf("gst0"), Buf("gst1")]
            gsem = [P.new_dsem(f"ss_gs{i}") for i in range(2)]
            sg = [k.sb(f"ss_sg{i}", [128, 512], F32, l4) for i in range(2)]
            b_sg = [Buf("sg0"), Buf("sg1")]
            so = [k.sb(f"ss_so{i}", [128, 512], BF16, l4) for i in range(2)]
            b_so = [Buf("so0"), Buf("so1")]
            sosem = [P.new_dsem(f"ss_sos{i}") for i in range(2)]
            ui = 0
            for fo in range(4):
                for tb in range(4):
                    i = ui % 2
                    ui += 1
                    tsl = slice(tb * 512, (tb + 1) * 512)
                    P.dma("sp", gst[i][:], S["sgsT"][fo * 128:(fo + 1) * 128, tsl], writes=[b_gst[i]], sem=gsem[i])
                    pt, bp = nb()
                    for fc in range(4):
                        P.op("pe", lambda e, fc=fc, fo=fo, tsl=tsl, pt=pt: e.matmul(pt[:], lhsT=wg[:, fc, fo * 128:(fo + 1) * 128], rhs=ygT[:, fc, tsl],
                                                                               start=(fc == 0), stop=(fc == 3)), reads=[b_wg, b_ygT], writes=[bp])
                    P.op("act", lambda e, i=i, fo=fo, pt=pt: e.activation(out=sg[i][:], in_=pt[:], func=AF.Sigmoid, bias=bg[:, fo:fo + 1]),
                         reads=[bp, b_bg], writes=[b_sg[i]])
                    P.op("dve", lambda e, i=i, fo=fo, tsl=tsl: e.tensor_tensor(out=sg[i][:], in0=sg[i][:], in1=ygT[:, fo, tsl], op=MUL),
                         reads=[b_sg[i], b_ygT], writes=[b_sg[i]])
                    P.op("dve", lambda e, i=i: e.tensor_tensor(out=so[i][:], in0=sg[i][:], in1=gst[i][:], op=MUL),
                         reads=[b_sg[i], b_gst[i]], writes=[b_so[i]])
                    P.dma("sp", S["sbrT"][fo * 128:(fo + 1) * 128, tsl], so[i][:], reads=[b_so[i]], sem=sosem[i])


def phase_attn(k):
    nc, P, I, S = k.nc, k.P, k.I, k.S
    with ExitStack() as ls:
        lamv = k.sb("at_lamv", [128, 4, 64], F32, ls)
        lw = k.sb("at_lw", [128, 8], F32, ls)
        G = k.sb("at_G", [128, 128], F32, ls)
        b_lamv, b_lw, b_G = Buf("lamv"), Buf("lw"), Buf("G")
        csem = P.new_dsem("at_c")
        P.dma("sp", lamv[:].rearrange("p a b -> p (a b)"), I["lam"].rearrange("a b -> (a b)").partition_broadcast(128),
              writes=[b_lamv], sem=csem)
        P.dma("sp", G[:], I["subln_g"][0:1, :].broadcast_to([128, 128]), writes=[b_G], sem=csem)
        P.op("dve", lambda e: e.tensor_scalar(out=G[:], in0=G[:], scalar1=(1.0 - LAM_INIT), scalar2=None, op0=ALU.mult),
             reads=[b_G], writes=[b_G])
        for i in range(2):
            P.op("dve", lambda e, i=i: e.tensor_tensor(out=lamv[:, 2 * i, :], in0=lamv[:, 2 * i, :], in1=lamv[:, 2 * i + 1, :], op=ALU.mult),
                 reads=[b_lamv], writes=[b_lamv])
            P.op("dve", lambda e, i=i: e.tensor_reduce(out=lw[:, i:i + 1], in_=lamv[:, 2 * i, :], axis=mybir.AxisListType.X, op=ALU.add),
                 reads=[b_lamv], writes=[b_lw])
        P.op("act", lambda e: e.activation(out=lw[:, 2:4], in_=lw[:, 0:2], func=AF.Exp), reads=[b_lw], writes=[b_lw])
        P.op("dve", lambda e: e.tensor_tensor(out=lw[:, 4:5], in0=lw[:, 3:4], in1=lw[:, 2:3], op=ALU.subtract), reads=[b_lw], writes=[b_lw])
        P.op("dve", lambda e: e.tensor_scalar(out=lw[:, 5:6], in0=lw[:, 4:5], scalar1=-LAM_INIT, scalar2=None, op0=ALU.add),
             reads=[b_lw], writes=[b_lw])
        neglam = lw[:, 5:6]
        qTs = [k.sb(f"at_q{i}", [128, L], BF16, ls) for i in range(2)]
        kTs = [k.sb(f"at_k{i}", [128, LT], BF16, ls) for i in range(2)]
        Vs = [k.sb(f"at_v{i}", [128, 18, 130], BF16, ls) for i in range(2)]
        gas = [k.sb(f"at_ga{i}", [128, 16, 128], BF16, ls) for i in range(2)]
        aTs = [k.sb(f"at_aT{i}", [128, L], BF16, ls) for i in range(2)]
        b_q = [Buf(f"atq{i}") for i in range(2)]
        b_k = [Buf(f"atk{i}") for i in range(2)]
        b_v = [Buf(f"atv{i}") for i in range(2)]
        b_ga = [Buf(f"atga{i}") for i in range(2)]
        b_aT = [Buf(f"ataT{i}") for i in range(2)]
        hsem = [P.new_dsem(f"at_h{i}") for i in range(2)]
        asem = [P.new_dsem(f"at_a{i}") for i in range(2)]
        for i in range(2):
            P.op("pool", lambda e, i=i: e.memset(Vs[i][:, :, 128:130], 1.0), writes=[b_v[i]])
        PT = [k.sb(f"at_pt{i}", [128, 2, 18, 256], BF16, ls) for i in range(2)]
        b_PT = [[[Buf(f"pt{i}_{c}_{kp}") for kp in range(9)] for c in range(2)] for i in range(2)]
        sbk = [k.ps(f"at_s{i}", [128, 512], F32, ls) for i in range(3)]
        b_sbk = [Buf(f"ats{i}") for i in range(3)]
        obk = [k.ps(f"at_o{i}", [128, 512], F32, ls) for i in range(4)]
        b_obk = [Buf(f"ato{i}") for i in range(4)]
        tbk = k.ps("at_t", [128, 512], F32, ls)
        b_tbk = Buf("att")
        sm = [k.sb(f"at_sm{i}", [128, 8], F32, ls) for i in range(2)]
        b_sm = [Buf(f"atsm{i}") for i in range(2)]
        tmp = [k.sb(f"at_tmp{i}", [128, 128], F32, ls) for i in range(2)]
        b_tmp = [Buf(f"attmp{i}") for i in range(2)]
        ov = [k.sb(f"at_ov{i}", [128, 128], F32, ls) for i in range(2)]
        b_ov = [Buf(f"atov{i}") for i in range(2)]
        junk = k.sb("at_junk", [128, 128], F32, ls)
        b_junk = Buf("atjunk")
        cnt = {"s": 0, "u": 0}

        def load_head(h):
            s = h % 2
            P.dma("sp", qTs[s][:], S["qT"][h], writes=[b_q[s]], sem=hsem[s])
            P.dma("sp", kTs[s][:], S["kT"][h], writes=[b_k[s]], sem=hsem[s])
            P.dma("sp", Vs[s][:, :, 0:128], S["v"][:, h * 128:(h + 1) * 128].rearrange("(t p) e -> p t e", p=128),
                  writes=[b_v[s]], sem=hsem[s])
            P.dma("sp", gas[s][:], S["sga"][:, h * 128:(h + 1) * 128].rearrange("(t p) e -> p t e", p=128),
                  writes=[b_ga[s]], sem=hsem[s])

        def A_steps(h, qb):
            s = h % 2
            ps_ = qb % 2
            steps = []
            for kp in range(9):
                def step(kp=kp):
                    for c in range(2):
                        si = cnt["s"] % 3
                        cnt["s"] += 1
                        for j in range(2):
                            kt = 2 * kp + j
                            P.op("pe", lambda e, kt=kt, j=j, c=c, si=si: e.matmul(
                                sbk[si][:, j * 256:(j + 1) * 256], lhsT=kTs[s][c * 64:(c + 1) * 64, kt * 128:(kt + 1) * 128],
                                rhs=qTs[s][c * 64:(c + 1) * 64, qb * 256:(qb + 1) * 256], start=True, stop=True),
                                reads=[b_k[s], b_q[s]], writes=[b_sbk[si]])
                        P.op("act", lambda e, c=c, kp=kp, si=si: e.activation(
                            out=PT[ps_][:, c, 2 * kp:2 * kp + 2, :].rearrange("p a b -> p (a b)"), in_=sbk[si][:], func=AF.Exp, scale=0.125),
                            reads=[b_sbk[si]], writes=[b_PT[ps_][c][kp]])
                steps.append(step)
            return steps

        def B_gen(h, qb):
            s = h % 2
            ps_ = qb % 2
            for qi_ in range(2):
                yield from unitB(h, qb, qi_, s, ps_)

        def unitB(h, qb, qi, s, ps_):
            if True:
                qt = qb * 2 + qi
                u = cnt["u"] % 2
                cnt["u"] += 1
                banks = [obk[u * 2], obk[u * 2 + 1]]
                bb = [b_obk[u * 2], b_obk[u * 2 + 1]]
                for c in range(2):
                    for kt in range(18):
                        P.op("pe", lambda e, c=c, kt=kt: e.matmul(
                            banks[c][:, 0:129], lhsT=PT[ps_][:, c, kt, qi * 128:(qi + 1) * 128], rhs=Vs[s][:, kt, 0:129],
                            start=(kt == 0), stop=(kt == 17)),
                            reads=[b_PT[ps_][c][kt // 2], b_v[s]], writes=[bb[c]])
                        yield
                flush_pending()
                smt, bsm = sm[u], b_sm[u]
                for c in range(2):
                    P.op("dve", lambda e, c=c: e.reciprocal(out=smt[:, c:c + 1], in_=banks[c][:, 128:129]), reads=[bb[c]], writes=[bsm])
                P.op("dve", lambda e: e.tensor_tensor(out=smt[:, 2:3], in0=smt[:, 1:2], in1=neglam, op=ALU.mult), reads=[bsm, b_lw], writes=[bsm])
                P.op("dve", lambda e: e.tensor_scalar(out=tmp[u][:], in0=banks[1][:, 0:128], scalar1=smt[:, 2:3], scalar2=None, op0=ALU.mult),
                     reads=[bb[1], bsm], writes=[b_tmp[u]])
                P.op("dve", lambda e: e.scalar_tensor_tensor(out=ov[u][:], in0=banks[0][:, 0:128], scalar=smt[:, 0:1], in1=tmp[u][:],
                                                            op0=ALU.mult, op1=ALU.add),
                     reads=[bb[0], bsm, b_tmp[u]], writes=[b_ov[u]])
                P.op("dve", lambda e: e.tensor_tensor(out=tmp[u][:], in0=ov[u][:], in1=ov[u][:], op=ALU.mult),
                     reads=[b_ov[u]], writes=[b_tmp[u]])
                P.op("dve", lambda e: e.tensor_reduce(out=smt[:, 3:4], in_=tmp[u][:], axis=mybir.AxisListType.X, op=ALU.add),
                     reads=[b_tmp[u]], writes=[bsm])
                P.op("dve", lambda e: e.tensor_scalar(out=smt[:, 4:5], in0=smt[:, 3:4], scalar1=1.0 / 128, scalar2=EPS, op0=ALU.mult, op1=ALU.add),
                     reads=[bsm], writes=[bsm])
                def fin():
                    P.op("act", lambda e: e.activation(out=smt[:, 5:6], in_=smt[:, 4:5], func=AF.Ln), reads=[bsm], writes=[bsm])
                    P.op("act", lambda e: e.activation(out=smt[:, 6:7], in_=smt[:, 5:6], func=AF.Exp, scale=-0.5), reads=[bsm], writes=[bsm])
                    P.op("dve", lambda e: e.scalar_tensor_tensor(out=ov[u][:], in0=ov[u][:], scalar=smt[:, 6:7], in1=G[:], op0=ALU.mult, op1=ALU.mult),
                         reads=[b_ov[u], bsm, b_G], writes=[b_ov[u]])
                    P.op("pool", lambda e: e.tensor_tensor(out=ov[u][:], in0=ov[u][:], in1=gas[s][:, qt, :], op=ALU.mult),
                         reads=[b_ov[u], b_ga[s]], writes=[b_ov[u]])

                    def fin2():
                        P.op("pe", lambda e: e.transpose(out=tbk[:, 0:128], in_=ov[u][:], identity=k.ident[:]), reads=[b_ov[u], k.b_ident], writes=[b_tbk])
                        P.op("dve", lambda e: e.tensor_copy(out=aTs[s][:, qt * 128:(qt + 1) * 128], in_=tbk[:, 0:128]),
                             reads=[b_tbk], writes=[b_aT[s]])
                    pending2.append(fin2)
                pending.append(fin)

        pending = []
        pending2 = []

        def flush_pending():
            while pending2:
                pending2.pop(0)()
            while pending:
                pending.pop(0)()

        def interleave(a_steps, bgen, per=8):
            for st_ in a_steps:
                st_()
                if bgen is not None:
                    for _ in range(per):
                        try:
                            next(bgen)
                        except StopIteration:
                            bgen = None
                            break
            if bgen is not None:
                for _ in bgen:
                    pass

        load_head(0)
        load_head(1)
        interleave(A_steps(0, 0), None)
        for h in range(HEADS):
            for qb in range(8):
                if qb + 1 < 8:
                    nxt = A_steps(h, qb + 1)
                elif h + 1 < HEADS:
                    nxt = A_steps(h + 1, 0)
                else:
                    nxt = []
                interleave(nxt, B_gen(h, qb))
            flush_pending()
            flush_pending()
            P.dma("pool", S["abrT"][h * 128:(h + 1) * 128, :], aTs[h % 2][:], reads=[b_aT[h % 2]], sem=asem[h % 2])
            if h + 2 < HEADS:
                load_head(h + 2)


def phase_merge(k):
    nc, P, I, S = k.nc, k.P, k.I, k.S
    with ExitStack() as ls:
        mT = k.sb("mg_mT", [128, NKC, L], BF16, ls)
        b_mT = [Buf(f"mT{tb}") for tb in range(4)]
        wout = k.sb("mg_wout", [128, NKC, D], BF16, ls)
        b_wout = Buf("wout")
        NXB = 3
        wov = I["w_out"].rearrange("(kc p) c -> p kc c", p=128)
        wo_state = {"kc": 0}
        b_woutc = [Buf(f"woutc{i}") for i in range(NKC)]

        def load_wout_chunk():
            kc = wo_state["kc"]
            if kc >= NKC:
                return
            wo_state["kc"] += 1
            P.dma("pool", wout[:, kc, :], wov[:, kc, :], writes=[b_woutc[kc]])
        with ExitStack() as l1:
            abrT = k.sb("mg_abrT", [128, 8, L], BF16, l1)
            sbrT = k.sb("mg_sbrT", [128, 4, L], BF16, l1)
            b_abrT, b_sbrT = Buf("abrT"), Buf("sbrT")
            lsem = P.new_dsem("mg_l")
            P.dma("sp", abrT[:], S["abrT"].rearrange("(fc p) t -> p fc t", p=128), writes=[b_abrT], sem=lsem)
            P.dma("sp", sbrT[:], S["sbrT"].rearrange("(fc p) t -> p fc t", p=128), writes=[b_sbrT], sem=lsem)
            NWS = 2
            wbf = [k.sb(f"mg_wbf{i}", [128, 12, 128], BF16, l1) for i in range(NWS)]
            b_wbf = [Buf(f"mgwbf{i}") for i in range(NWS)]
            NG = 2
            gt = [k.sb(f"mg_gt{i}", [128, 2, L], BF16, l1) for i in range(NG)]
            b_gt = [Buf(f"mggt{i}") for i in range(NG)]
            t1 = [k.sb(f"mg_t1{i}", [128, 512], F32, l1) for i in range(2)]
            t2 = [k.sb(f"mg_t2{i}", [128, 512], F32, l1) for i in range(2)]
            b_t1 = [Buf(f"mgt1{i}") for i in range(2)]
            b_t2 = [Buf(f"mgt2{i}") for i in range(2)]
            pa = [k.ps(f"mg_pa{i}", [128, 512], F32, l1) for i in range(2)]
            pp = [k.ps(f"mg_pp{i}", [128, 512], F32, l1) for i in range(2)]
            b_pa = [Buf(f"mgpa{i}") for i in range(2)]
            b_pp = [Buf(f"mgpp{i}") for i in range(2)]
            wpa_v = I["w_pa"].rearrange("(fc p) c -> p fc c", p=128)
            wps_v = I["w_ps"].rearrange("(fc p) c -> p fc c", p=128)
            ui = 0

            def load_w(fo):
                s = fo % NWS
                P.dma("pool", wbf[s][:, 0:8, :], wpa_v[:, :, fo * 128:(fo + 1) * 128], writes=[b_wbf[s]])
                P.dma("pool", wbf[s][:, 8:12, :], wps_v[:, :, fo * 128:(fo + 1) * 128], writes=[b_wbf[s]])
                gi = fo % NG
                P.dma("sp", gt[gi][:, 0, :], S["sgmT"][fo * 128:(fo + 1) * 128, :], writes=[b_gt[gi]])
                P.dma("sp", gt[gi][:, 1, :], S["sgmT"][D + fo * 128:D + (fo + 1) * 128, :], writes=[b_gt[gi]])

            load_w(0)
            for fo in range(NKC):
                if fo + 1 < NKC:
                    load_w(fo + 1)
                load_wout_chunk()
                s = fo % NWS
                gi = fo % NG
                for tb in range(4):
                    u2 = ui % 2
                    ui += 1
                    tsl = slice(tb * 512, (tb + 1) * 512)
                    for fc in range(8):
                        P.op("pe", lambda e, fc=fc, s=s, tsl=tsl, u2=u2: e.matmul(pa[u2][:], lhsT=wbf[s][:, fc, :], rhs=abrT[:, fc, tsl],
                                                                        start=(fc == 0), stop=(fc == 7)),
                             reads=[b_wbf[s], b_abrT], writes=[b_pa[u2]])
                    for fc in range(4):
                        P.op("pe", lambda e, fc=fc, s=s, tsl=tsl, u2=u2: e.matmul(pp[u2][:], lhsT=wbf[s][:, 8 + fc, :], rhs=sbrT[:, fc, tsl],
                                                                        start=(fc == 0), stop=(fc == 3)),
                             reads=[b_wbf[s], b_sbrT], writes=[b_pp[u2]])
                    P.op("dve", lambda e, gi=gi, u2=u2, tsl=tsl: e.tensor_tensor(out=t1[u2][:], in0=pa[u2][:], in1=gt[gi][:, 0, tsl], op=ALU.mult),
                         reads=[b_pa[u2], b_gt[gi]], writes=[b_t1[u2]])
                    P.op("dve", lambda e, gi=gi, u2=u2, tsl=tsl: e.tensor_tensor(out=t2[u2][:], in0=pp[u2][:], in1=gt[gi][:, 1, tsl], op=ALU.mult),
                         reads=[b_pp[u2], b_gt[gi]], writes=[b_t2[u2]])
                    P.op("pool", lambda e, fo=fo, tsl=tsl, u2=u2: e.tensor_tensor(out=mT[:, fo, tsl], in0=t1[u2][:], in1=t2[u2][:], op=ALU.add),
                         reads=[b_t1[u2], b_t2[u2]], writes=[b_mT[tb]])
            P.barrier()
        gateB = k.sb("mg_gateB", [128, D], F32, ls)
        fgB = k.sb("mg_fgB", [128, D], F32, ls)
        b_gateB, b_fgB = Buf("gateB"), Buf("fgB")
        c2 = P.new_dsem("mg_c2")
        P.dma("sp", gateB[:], S["modrow"][0:1, 2 * D:3 * D].broadcast_to([128, D]), writes=[b_gateB], sem=c2)
        P.dma("sp", fgB[:], I["final_g"][0:1, :].broadcast_to([128, D]), writes=[b_fgB], sem=c2)
        xb = [k.sb(f"mg_x{i}", [128, D], F32, ls) for i in range(NXB)]
        b_xb = [Buf(f"mgx{i}") for i in range(NXB)]
        xn = [k.sb(f"mg_xn{i}", [128, D], F32, ls) for i in range(NXB)]
        b_xn = [Buf(f"mgxn{i}") for i in range(NXB)]
        xsem = [P.new_dsem(f"mg_xs{i}") for i in range(NXB)]
        osem = [P.new_dsem(f"mg_os{i}") for i in range(NXB)]
        st2 = [k.sb(f"mg_st{i}", [128, 4], F32, ls) for i in range(NXB)]
        b_st2 = [Buf(f"mgst{i}") for i in range(NXB)]
        while wo_state["kc"] < NKC:
            load_wout_chunk()
        po = [k.ps(f"mg_po{i}", [128, 512], F32, ls) for i in range(3)]
        b_po = [Buf(f"mgpo{i}") for i in range(3)]
        pi = 0
        for t in range(16):
            s = t % NXB
            tb = t // 4
            P.dma("act", xb[s][:], I["x"][t * 128:(t + 1) * 128, :], writes=[b_xb[s]], sem=xsem[s])
            for cbk in range(4):
                p_ = pi % 3
                pi += 1
                for kc in range(NKC):
                    P.op("pe", lambda e, kc=kc, cbk=cbk, p_=p_, t=t: e.matmul(po[p_][:], lhsT=mT[:, kc, t * 128:(t + 1) * 128],
                                                                          rhs=wout[:, kc, cbk * 512:(cbk + 1) * 512],
                                                                          start=(kc == 0), stop=(kc == NKC - 1)),
                         reads=[b_mT[tb], b_woutc[kc]], writes=[b_po[p_]])
                P.op("dve", lambda e, cbk=cbk, p_=p_, s=s: e.tensor_tensor(out=xn[s][:, cbk * 512:(cbk + 1) * 512], in0=po[p_][:],
                                                                       in1=gateB[:, cbk * 512:(cbk + 1) * 512], op=ALU.mult),
                     reads=[b_po[p_], b_gateB], writes=[b_xn[s]])
            P.op("pool", lambda e, s=s: e.tensor_tensor(out=xn[s][:], in0=xn[s][:], in1=xb[s][:], op=ALU.add),
                 reads=[b_xn[s], b_xb[s]], writes=[b_xn[s]])
            P.op("pool", lambda e, s=s: e.tensor_tensor(out=xb[s][:], in0=xn[s][:], in1=xn[s][:], op=ALU.mult),
                 reads=[b_xn[s]], writes=[b_xb[s]])
            P.op("dve", lambda e, s=s: e.tensor_reduce(out=st2[s][:, 0:1], in_=xb[s][:], axis=mybir.AxisListType.X, op=ALU.add),
                 reads=[b_xb[s]], writes=[b_st2[s]])
            P.op("dve", lambda e, s=s: e.tensor_scalar(out=st2[s][:, 1:2], in0=st2[s][:, 0:1], scalar1=1.0 / D, scalar2=EPS, op0=ALU.mult, op1=ALU.add),
                 reads=[b_st2[s]], writes=[b_st2[s]])
            P.op("act", lambda e, s=s: e.activation(out=st2[s][:, 2:3], in_=st2[s][:, 1:2], func=AF.Ln), reads=[b_st2[s]], writes=[b_st2[s]])
            P.op("act", lambda e, s=s: e.activation(out=st2[s][:, 3:4], in_=st2[s][:, 2:3], func=AF.Exp, scale=-0.5), reads=[b_st2[s]], writes=[b_st2[s]])
            P.op("dve", lambda e, s=s: e.scalar_tensor_tensor(out=xn[s][:], in0=xn[s][:], scalar=st2[s][:, 3:4], in1=fgB[:], op0=ALU.mult, op1=ALU.mult),
                 reads=[b_xn[s], b_st2[s], b_fgB], writes=[b_xn[s]])
            P.dma("sp", k.out[t * 128:(t + 1) * 128, :], xn[s][:], reads=[b_xn[s]], sem=osem[s])


_CACHE = {}


def _prep_inputs(inputs, b):
    f = lambda a: np.ascontiguousarray(np.asarray(a, dtype=np.float32))
    m = {}
    m["x"] = f(inputs["x"][b])
    m["ctx"] = f(inputs["ctx"][b])
    m["cc"] = f(np.stack([np.asarray(inputs["c"])[b], np.asarray(inputs["c_ctx"])], axis=0))
    m["w_ada"] = f(inputs["w_ada"][0])
    m["b_ada"] = f(inputs["b_ada"][0]).reshape(1, -1)
    m["norm_g"] = f(inputs["norm_g"][0])
    m["w_in"] = f(inputs["w_in"][0])
    m["lam"] = f(np.stack([np.asarray(inputs["lambda_q1"])[0], np.asarray(inputs["lambda_k1"])[0],
                           np.asarray(inputs["lambda_q2"])[0], np.asarray(inputs["lambda_k2"])[0]], axis=0))
    m["subln_g"] = f(inputs["subln_g"][0]).reshape(1, 128)
    m["ssm_lre"] = f(inputs["ssm_lambda_re"][0])
    m["ssm_lim"] = f(inputs["ssm_lambda_im"][0])
    m["ssm_ls"] = f(inputs["ssm_log_step"][0])
    m["ssm_bre"] = f(inputs["ssm_b_re"][0])
    m["ssm_bim"] = f(inputs["ssm_b_im"][0])
    m["ssm_cre"] = f(inputs["ssm_c_re"][0])
    m["ssm_cim"] = f(inputs["ssm_c_im"][0])
    m["ssm_d"] = f(inputs["ssm_d"][0]).reshape(1, 512)
    m["w_glu"] = f(inputs["w_glu"][0])
    m["b_glu"] = f(inputs["b_glu"][0])
    m["w_pa"] = f(inputs["w_pa"][0])
    m["w_ps"] = f(inputs["w_ps"][0])
    m["w_out"] = f(inputs["w_out"][0])
    m["final_g"] = f(inputs["final_g"]).reshape(1, D)
    m.update(_consts())
    return m


def kernel(**inputs):
    if "nc" not in _CACHE:
        _CACHE["nc"] = build()[0]
    nc = _CACHE["nc"]
    shared = None
    in_maps = []
    for b in range(8):
        m = _prep_inputs(inputs, b)
        if shared is None:
            shared = m
        else:
            for key in m:
                if key not in ("x", "ctx", "cc"):
                    m[key] = shared[key]
        in_maps.append(m)
    res = run_bass_kernel_spmd(nc, in_maps, core_ids=list(range(8)))
    return np.stack([np.asarray(r["out"], dtype=np.float32) for r in res.results], axis=0)
```

```python
import math
import numpy as np
import ml_dtypes
from contextlib import ExitStack
import concourse.bass as bass
import concourse.mybir as mybir
from concourse.bass_utils import run_bass_kernel_spmd

F32 = mybir.dt.float32
BF16 = mybir.dt.bfloat16
I32 = mybir.dt.int32
AF = mybir.ActivationFunctionType
ALU = mybir.AluOpType

D = 2048
L = 2048
LC = 256
LT = L + LC
NKC = D // 128
INW = 9216
HEADS = 8
EPS = 1e-6
LAM_INIT = 0.8 - 0.6 * math.exp(-0.3 * 0)
TWO_PI = 2.0 * math.pi


class Buf:
    __slots__ = ("name", "w", "r")

    def __init__(self, name):
        self.name = name
        self.w = None
        self.r = {}


class Prog:
    ENG = ["pe", "act", "dve", "pool", "sp"]

    def __init__(self, nc, st):
        self.nc = nc
        self.st = st
        self.q = {e: [] for e in self.ENG}
        self.seen = {e: {} for e in self.ENG}
        self.psem = {e: st.enter_context(nc.semaphore("p_" + e)) for e in ["pe", "act", "dve", "pool"]}
        self.dsems = []
        self.bufsem = {}
        self.bufsem_keep = []
        self.free_dsems = []

    def new_dsem(self, name):
        return None

    def _auto_dsem(self, reads, writes):
        b = writes[0] if len(writes) else reads[0]
        key = id(b)
        d = self.bufsem.get(key)
        if d is None:
            if self.free_dsems:
                d = self.free_dsems.pop()
            else:
                h = self.st.enter_context(self.nc.semaphore(f"d{len(self.dsems)}"))
                d = {"h": h, "n": 0, "name": f"d{len(self.dsems)}"}
                self.dsems.append(d)
            self.bufsem[key] = d
            self.bufsem_keep.append(b)
        return d

    def _deps(self, eng, reads, writes):
        need = {}

        def add(t):
            if t[0] == "c":
                if t[1] == "pe" and eng == "pe":
                    return
                key = ("c", t[1])
                if need.get(key, (None, -1))[1] < t[2]:
                    need[key] = (t[1], t[2])
            else:
                key = ("d", id(t[1]))
                if need.get(key, (None, -1))[1] < t[2]:
                    need[key] = (t[1], t[2])

        for b in reads:
            if b.w is not None:
                add(b.w)
        for b in writes:
            if b.w is not None:
                add(b.w)
            for t in b.r.values():
                add(t)
        waits = []
        for key, (obj, v) in need.items():
            if self.seen[eng].get(key, -1) >= v:
                continue
            self.seen[eng][key] = v
            waits.append((key[0], obj, v))
        return waits

    def _record(self, tok, reads, writes):
        for b in reads:
            key = (tok[0], tok[1] if tok[0] == "c" else id(tok[1]))
            b.r[key] = tok
        for b in writes:
            b.w = tok
            b.r = {}

    def op(self, eng, fn, reads=(), writes=()):
        waits = self._deps(eng, reads, writes)
        idx = len(self.q[eng])
        self.q[eng].append({"fn": fn, "waits": waits, "awaited": False, "dma": None})
        tok = ("c", eng, idx)
        self._record(tok, reads, writes)
        return tok

    def dma(self, eng, out, in_, reads=(), writes=(), sem=None, **kw):
        reads, writes = list(reads), list(writes)
        sem = self._auto_dsem(reads, writes)
        waits = self._deps(eng, reads, writes)
        sem["n"] += 16
        tok = ("d", sem, sem["n"])
        self.q[eng].append({"fn": (lambda e, o=out, i=in_, k=kw: e.dma_start(out=o, in_=i, **k)),
                            "waits": waits, "awaited": False, "dma": sem})
        self._record(tok, reads, writes)
        return tok

    def barrier(self):
        for e in self.ENG:
            waits = []
            for e2 in ["pe", "act", "dve", "pool"]:
                n = len(self.q[e2])
                if e2 == e:
                    n -= 0
                idx = None
                for i in range(len(self.q[e2]) - 1, -1, -1):
                    if self.q[e2][i]["fn"] is not None and self.q[e2][i]["dma"] is None:
                        idx = i
                        break
                if idx is None:
                    continue
                key = ("c", e2)
                if self.seen[e].get(key, -1) >= idx:
                    continue
                self.seen[e][key] = idx
                waits.append(("c", e2, idx))
            for d in self.dsems:
                if d["n"] == 0:
                    continue
                key = ("d", id(d))
                if self.seen[e].get(key, -1) >= d["n"]:
                    continue
                self.seen[e][key] = d["n"]
                waits.append(("d", d, d["n"]))
            if waits:
                self.q[e].append({"fn": None, "waits": waits, "awaited": False, "dma": None})
        for d in self.bufsem.values():
            self.free_dsems.append(d)
        self.bufsem = {}
        self.bufsem_keep = []

    def emit(self):
        for e in self.ENG:
            for ent in self.q[e]:
                for w in ent["waits"]:
                    if w[0] == "c":
                        self.q[w[1]][w[2]]["awaited"] = True
        cnt = {}
        for e in ["pe", "act", "dve", "pool"]:
            c = 0
            arr = []
            for ent in self.q[e]:
                if ent["awaited"]:
                    c += 1
                arr.append(c)
            cnt[e] = arr
        psem = self.psem
        q = self.q

        def run(name, e):
            for ent in q[name]:
                for w in ent["waits"]:
                    if w[0] == "c":
                        e.wait_ge(psem[w[1]], cnt[w[1]][w[2]])
                    else:
                        e.wait_ge(w[1]["h"], w[2])
                if ent["fn"] is None:
                    continue
                inst = ent["fn"](e)
                if ent["dma"] is not None:
                    inst.then_inc(ent["dma"]["h"], 16)
                elif ent["awaited"]:
                    inst.then_inc(psem[name], 1)

        with self.nc.Block() as block:
            @block.sync
            def _(e):
                run("sp", e)

            @block.scalar
            def _(e):
                run("act", e)

            @block.vector
            def _(e):
                run("dve", e)

            @block.gpsimd
            def _(e):
                run("pool", e)

            @block.tensor
            def _(e):
                run("pe", e)


def _consts():
    ident = np.eye(128, dtype=np.float32)
    m = np.arange(128)
    partner = np.where((m % 32) < 16, m + 16, m - 16)
    perm = np.zeros((128, 128), np.float32)
    perm[partner, m] = 1.0
    sgn = np.where((m % 32) < 16, -1.0, 1.0).astype(np.float32)
    tok = np.arange(L)
    pos = np.where(((m % 64) < 32)[:, None], (tok // 64)[None, :], (tok % 64)[None, :]).astype(np.float32)
    fexp = ((m % 16) / 16.0).astype(np.float32)
    colc = np.zeros((128, 4), np.float32)
    colc[:, 0] = sgn
    colc[:, 1] = fexp
    colc[:, 2] = np.where(m < 64, 1.0, -1.0)
    sel = np.zeros((2, 128), np.float32)
    sel[0, :] = 1.0
    tauA = np.zeros((128, 32, 9), np.float32)
    tauA[:, 0:16, :] = np.arange(9)[None, None, :]
    tauA[:, 16:32, :] = (8 - np.arange(9))[None, None, :]
    tauB = np.zeros((128, 32, 8), np.float32)
    tauB[:, 0:16, :] = (7 - np.arange(8))[None, None, :]
    tauB[:, 16:32, :] = np.arange(8)[None, None, :]
    tauC = np.zeros((128, 32, 8), np.float32)
    tauC[:, :, :] = (8.0 * (np.arange(8) + 1))[None, None, :]
    return {"c_ident": ident, "c_perm": perm, "c_pos": pos, "c_col": colc, "c_sel": sel, "c_tauA": tauA, "c_tauB": tauB,
            "c_tauC": tauC}


class K:
    pass


def build(debug=None):
    nc = bass.Bass("TRN2", target_bir_lowering=False)
    st = ExitStack()
    P = Prog(nc, st)
    k = K()
    k.nc, k.P, k.st = nc, P, st
    k.debug = debug or {}

    def dram_in(name, shape, dt=F32):
        return nc.dram_tensor(name, list(shape), dt, kind="ExternalInput").ap()

    dbg_outs = []

    def dram_scr(name, shape, dt):
        kind = "Internal"
        if debug is not None and name in debug.get("_inject", ()):
            kind = "ExternalInput"
        elif debug is not None and name in debug:
            kind = "ExternalOutput"
            dbg_outs.append(name)
        return nc.dram_tensor(name, list(shape), dt, kind=kind).ap()

    I = {}
    I["x"] = dram_in("x", [L, D])
    I["ctx"] = dram_in("ctx", [LC, D])
    I["cc"] = dram_in("cc", [2, D])
    I["w_ada"] = dram_in("w_ada", [D, 3 * D])
    I["b_ada"] = dram_in("b_ada", [1, 3 * D])
    I["norm_g"] = dram_in("norm_g", [D])
    I["w_in"] = dram_in("w_in", [D, INW])
    I["lam"] = dram_in("lam", [4, 64])
    I["subln_g"] = dram_in("subln_g", [1, 128])
    I["ssm_lre"] = dram_in("ssm_lre", [2, 32, 64])
    I["ssm_lim"] = dram_in("ssm_lim", [2, 32, 64])
    I["ssm_ls"] = dram_in("ssm_ls", [2, 32])
    I["ssm_bre"] = dram_in("ssm_bre", [2, 32, 64, 16])
    I["ssm_bim"] = dram_in("ssm_bim", [2, 32, 64, 16])
    I["ssm_cre"] = dram_in("ssm_cre", [2, 32, 16, 64])
    I["ssm_cim"] = dram_in("ssm_cim", [2, 32, 16, 64])
    I["ssm_d"] = dram_in("ssm_d", [1, 512])
    I["w_glu"] = dram_in("w_glu", [512, 512])
    I["b_glu"] = dram_in("b_glu", [512])
    I["w_pa"] = dram_in("w_pa", [1024, D])
    I["w_ps"] = dram_in("w_ps", [512, D])
    I["w_out"] = dram_in("w_out", [D, D])
    I["final_g"] = dram_in("final_g", [1, D])
    for cn, arr in _consts().items():
        I[cn] = dram_in(cn, arr.shape)
    out = nc.dram_tensor("out", [L, D], F32, kind="ExternalOutput").ap()

    S = {}
    S["modrow"] = dram_scr("modrow", [2, 3 * D], F32)
    S["qT"] = dram_scr("qT", [HEADS, 128, L], BF16)
    S["kT"] = dram_scr("kT", [HEADS, 128, LT], BF16)
    S["v"] = dram_scr("v", [LT, 1024], BF16)
    S["sga"] = dram_scr("sga", [L, 1024], BF16)
    S["u"] = dram_scr("u", [LT, 512], F32)
    S["sgsT"] = dram_scr("sgsT", [512, L], BF16)
    S["sgmT"] = dram_scr("sgmT", [2 * D, L], BF16)
    S["abrT"] = dram_scr("abrT", [1024, L], BF16)
    S["sbrT"] = dram_scr("sbrT", [512, L], BF16)
    S["hT"] = dram_scr("hT_dbg", [128, NKC, LT], BF16) if (debug is not None and "hT_dbg" in debug) else None
    k.I, k.S, k.out = I, S, out
    k.dbg_sem = None

    def dbg(name, shape, dt, ap_fn, bufs):
        if debug is None or name not in debug:
            return
        if name not in S:
            S[name] = nc.dram_tensor(name, list(shape), dt, kind="ExternalOutput").ap()
            dbg_outs.append(name)
        if k.dbg_sem is None:
            k.dbg_sem = P.new_dsem("dbgsem")
        o, i = ap_fn(S[name])
        P.dma("sp", o, i, reads=bufs, sem=k.dbg_sem)
    k.dbg = dbg

    def sb(name, shape, dt, stack=st):
        return stack.enter_context(nc.sbuf_tensor(name, list(shape), dt))

    def ps(name, shape, dt, stack=st):
        return stack.enter_context(nc.psum_tensor(name, list(shape), dt))

    k.sb, k.ps = sb, ps
    ident = sb("ident", [128, 128], F32)
    colc = sb("colc", [128, 4], F32)
    b_ident, b_colc = Buf("ident"), Buf("colc")
    csem = P.new_dsem("csem")
    P.dma("sp", ident[:], I["c_ident"], writes=[b_ident], sem=csem)
    P.dma("sp", colc[:], I["c_col"], writes=[b_colc], sem=csem)
    k.ident, k.b_ident, k.colc, k.b_colc, k.csem = ident, b_ident, colc, b_colc, csem
    k.ssq = sb("ssq", [128, 40], F32)
    k.b_ssq = Buf("ssq")

    phase_adaln(k)
    P.barrier()
    if debug is None or debug.get("_upto", 99) >= 1:
        phase_norm_inproj(k, debug)
        P.barrier()
    if (debug is None or debug.get("_upto", 99) >= 2) and not (debug or {}).get("_skip_ssm"):
        phase_ssm(k)
        P.barrier()
    if debug is None or debug.get("_upto", 99) >= 3:
        phase_attn(k)
        P.barrier()
    if debug is None or debug.get("_upto", 99) >= 4:
        phase_merge(k)
        P.barrier()
    P.emit()
    st.close()
    return nc, dbg_outs


def range_sin(k, stack, out_ap, y_ap, shape, tag, rbufs, wbufs, eng="dve"):
    nc, P = k.nc, k.P
    ki = k.sb(tag + "_ki", shape, I32, stack)
    kf = k.sb(tag + "_kf", shape, F32, stack)
    g = k.sb(tag + "_g", shape, F32, stack)
    bki, bkf, bg = Buf(tag + "ki"), Buf(tag + "kf"), Buf(tag + "g")
    sl = tuple([slice(None)] * len(shape))
    P.op(eng, lambda e: e.tensor_copy(out=ki[sl], in_=y_ap), reads=rbufs, writes=[bki])
    P.op(eng, lambda e: e.tensor_copy(out=kf[sl], in_=ki[sl]), reads=[bki], writes=[bkf])
    P.op(eng, lambda e: e.tensor_tensor(out=kf[sl], in0=y_ap, in1=kf[sl], op=ALU.subtract), reads=rbufs + [bkf], writes=[bkf])
    P.op(eng, lambda e: e.tensor_single_scalar(out=g[sl], in_=kf[sl], scalar=0.5, op=ALU.is_gt), reads=[bkf], writes=[bg])
    P.op(eng, lambda e: e.tensor_tensor(out=kf[sl], in0=kf[sl], in1=g[sl], op=ALU.subtract), reads=[bkf, bg], writes=[bkf])
    P.op(eng, lambda e: e.tensor_single_scalar(out=g[sl], in_=kf[sl], scalar=-0.5, op=ALU.is_lt), reads=[bkf], writes=[bg])
    P.op(eng, lambda e: e.tensor_tensor(out=kf[sl], in0=kf[sl], in1=g[sl], op=ALU.add), reads=[bkf, bg], writes=[bkf])
    P.op("act", lambda e: e.activation(out=out_ap, in_=kf[sl], func=AF.Sin, scale=TWO_PI * (1.0 - 2e-7)), reads=[bkf], writes=wbufs)


def phase_adaln(k):
    nc, P, I, S = k.nc, k.P, k.I, k.S
    with ExitStack() as ls:
        sT = k.sb("ad_sT", [128, NKC, 2], F32, ls)
        b_sT = Buf("sT")
        sem_c = P.new_dsem("ad_c")
        for v in range(2):
            P.dma("sp", sT[:, :, v], I["cc"][v].rearrange("(kc p) -> p kc", p=128), writes=[b_sT], sem=sem_c,
                  allow_slow_non_contiguous=True)
        P.op("act", lambda e: e.activation(out=sT[:], in_=sT[:], func=AF.Silu), reads=[b_sT], writes=[b_sT])
        brow = k.sb("ad_brow", [2, 3 * D], F32, ls)
        b_brow = Buf("brow")
        for v in range(2):
            P.dma("sp", brow[v:v + 1, :], I["b_ada"], writes=[b_brow], sem=sem_c)
        modrow = k.sb("ad_modrow", [2, 3 * D], F32, ls)
        b_modrow = Buf("modrow")
        NS = 2
        wst = [k.sb(f"ad_w{i}", [128, NKC, 512], F32, ls) for i in range(NS)]
        b_w = [Buf(f"adw{i}") for i in range(NS)]
        wsem = [P.new_dsem(f"ad_ws{i}") for i in range(NS)]
        pst = [k.ps(f"ad_ps{i}", [128, 512], F32, ls) for i in range(2)]
        b_ps = [Buf(f"adps{i}") for i in range(2)]
        wv = I["w_ada"].rearrange("(kc p) c -> p kc c", p=128)
        xs1 = [k.sb(f"ad_x{i}", [128, D], F32, ls) for i in range(2)]
        b_xs1 = [Buf(f"adx{i}") for i in range(2)]
        junk1 = k.sb("ad_junk", [128, D], BF16, ls)
        b_junk1 = Buf("adjunk")
        tiles_done = 0

        def ss_tile(t):
            s1 = t % 2
            src = I["x"][t * 128:(t + 1) * 128, :] if t < 16 else I["ctx"][(t - 16) * 128:(t - 15) * 128, :]
            P.dma("pool", xs1[s1][:], src, writes=[b_xs1[s1]])
            P.op("act", lambda e, s1=s1, t=t: e.activation(out=junk1[:], in_=xs1[s1][:], func=AF.Square, accum_out=k.ssq[:, t:t + 1]),
                 reads=[b_xs1[s1]], writes=[b_junk1, k.b_ssq])

        for cb in range(12):
            for _ in range(2 if cb < 6 else 1):
                if tiles_done < 18:
                    ss_tile(tiles_done)
                    tiles_done += 1
            s = cb % NS
            P.dma("sp", wst[s][:, 0:8, :], wv[:, 0:8, cb * 512:(cb + 1) * 512], writes=[b_w[s]], sem=wsem[s])
            P.dma("act", wst[s][:, 8:16, :], wv[:, 8:16, cb * 512:(cb + 1) * 512], writes=[b_w[s]], sem=wsem[s])
            pt, bp = pst[cb % 2], b_ps[cb % 2]
            for kc in range(NKC):
                P.op("pe", lambda e, kc=kc, s=s, pt=pt: e.matmul(pt[0:2, :], lhsT=sT[:, kc, :], rhs=wst[s][:, kc, :],
                                                              start=(kc == 0), stop=(kc == NKC - 1)),
                     reads=[b_sT, b_w[s]], writes=[bp])
            P.op("dve", lambda e, cb=cb, pt=pt: e.tensor_tensor(out=modrow[:, cb * 512:(cb + 1) * 512], in0=pt[0:2, :],
                                                             in1=brow[:, cb * 512:(cb + 1) * 512], op=ALU.add),
                 reads=[bp, b_brow], writes=[b_modrow])
        b_mr = Buf("modrow_d")
        k.b_modrow_d = b_mr
        P.dma("sp", S["modrow"], modrow[:], reads=[b_modrow], writes=[b_mr], sem=sem_c)


def phase_norm_inproj(k, debug):
    nc, P, I, S = k.nc, k.P, k.I, k.S
    with ExitStack() as ls:
        hT = k.sb("hT", [128, NKC, LT], BF16, ls)
        b_hT = [[Buf(f"hT{t}_{kc}") for kc in range(NKC)] for t in range(18)]
        Amod = k.sb("Amod", [128, NKC, 2], F32, ls)
        Smod = k.sb("Smod", [128, NKC, 2], F32, ls)
        gcol = k.sb("gcol", [128, NKC], F32, ls)
        b_A, b_S, b_g = Buf("Amod"), Buf("Smod"), Buf("gcol")
        msem = P.new_dsem("n_m")
        for v in range(2):
            P.dma("sp", Smod[:, :, v], S["modrow"][v, 0:D].rearrange("(kc p) -> p kc", p=128),
                  reads=[k.b_modrow_d], writes=[b_S], sem=msem, allow_slow_non_contiguous=True)
            P.dma("sp", Amod[:, :, v], S["modrow"][v, D:2 * D].rearrange("(kc p) -> p kc", p=128),
                  reads=[k.b_modrow_d], writes=[b_A], sem=msem, allow_slow_non_contiguous=True)
        P.dma("sp", gcol[:], I["norm_g"].rearrange("(kc p) -> p kc", p=128), writes=[b_g], sem=msem,
              allow_slow_non_contiguous=True)
        for v in range(2):
            P.op("dve", lambda e, v=v: e.scalar_tensor_tensor(out=Amod[:, :, v], in0=Amod[:, :, v], scalar=1.0, in1=gcol[:],
                                                             op0=ALU.add, op1=ALU.mult),
                 reads=[b_A, b_g], writes=[b_A])
        with ExitStack() as l1:
            NX = 2
            xt = [k.sb(f"n_x{i}", [128, D], F32, l1) for i in range(NX)]
            b_x = [Buf(f"nx{i}") for i in range(NX)]
            xsem = [P.new_dsem(f"n_xs{i}") for i in range(NX)]
            junk = k.sb("n_junk", [128, D], BF16, l1)
            b_junk = Buf("junk")
            stat = [k.sb(f"n_st{i}", [128, 4], F32, l1) for i in range(NX)]
            b_stat = [Buf(f"nst{i}") for i in range(NX)]
            pt = [k.ps(f"n_ps{i}", [128, 512], F32, l1) for i in range(4)]
            b_pt = [Buf(f"nps{i}") for i in range(4)]
            pi = 0
            P.op("dve", lambda e: e.tensor_scalar(out=k.ssq[:, 0:18], in0=k.ssq[:, 0:18], scalar1=1.0 / D, scalar2=EPS, op0=ALU.mult, op1=ALU.add),
                 reads=[k.b_ssq], writes=[k.b_ssq])
            P.op("act", lambda e: e.activation(out=k.ssq[:, 0:18], in_=k.ssq[:, 0:18], func=AF.Ln), reads=[k.b_ssq], writes=[k.b_ssq])
            P.op("act", lambda e: e.activation(out=k.ssq[:, 20:38], in_=k.ssq[:, 0:18], func=AF.Exp, scale=-0.5), reads=[k.b_ssq], writes=[k.b_ssq])
            for t in range(18):
                s = t % NX
                v = 0 if t < 16 else 1
                src = I["x"][t * 128:(t + 1) * 128, :] if t < 16 else I["ctx"][(t - 16) * 128:(t - 15) * 128, :]
                P.dma("sp", xt[s][:, 0:1024], src[:, 0:1024], writes=[b_x[s]], sem=xsem[s])
                P.dma("sp", xt[s][:, 1024:2048], src[:, 1024:2048], writes=[b_x[s]], sem=xsem[s])
                P.op("dve", lambda e, s=s, t=t: e.tensor_scalar(out=xt[s][:], in0=xt[s][:], scalar1=k.ssq[:, 20 + t:21 + t], scalar2=None,
                                                              op0=ALU.mult),
                     reads=[b_x[s], k.b_ssq], writes=[b_x[s]])
                for g4 in range(4):
                    p_, bp = pt[pi % 4], b_pt[pi % 4]
                    pi += 1
                    for j in range(4):
                        kc = g4 * 4 + j
                        P.op("pe", lambda e, s=s, kc=kc, j=j, p_=p_: e.transpose(out=p_[:, j * 128:(j + 1) * 128],
                                                                             in_=xt[s][:, kc * 128:(kc + 1) * 128],
                                                                             identity=k.ident[:]),
                             reads=[b_x[s], k.b_ident], writes=[bp])
                    for j in range(4):
                        kc = g4 * 4 + j
                        eng = "dve" if (g4 % 2 == 0) else "act"
                        if eng == "dve":
                            P.op("dve", lambda e, kc=kc, j=j, p_=p_, t=t, v=v: e.tensor_scalar(
                                out=hT[:, kc, t * 128:(t + 1) * 128], in0=p_[:, j * 128:(j + 1) * 128],
                                scalar1=Amod[:, kc, v:v + 1], scalar2=Smod[:, kc, v:v + 1], op0=ALU.mult, op1=ALU.add),
                                reads=[bp, b_A, b_S], writes=[b_hT[t][kc]])
                        else:
                            P.op("act", lambda e, kc=kc, j=j, p_=p_, t=t, v=v: e.activation(
                                out=hT[:, kc, t * 128:(t + 1) * 128], in_=p_[:, j * 128:(j + 1) * 128],
                                func=AF.Identity, scale=Amod[:, kc, v:v + 1], bias=Smod[:, kc, v:v + 1]),
                                reads=[bp, b_A, b_S], writes=[b_hT[t][kc]])
        if S["hT"] is not None:
            dsem = P.new_dsem("dbg")
            P.dma("sp", S["hT"], hT[:], reads=[b for row in b_hT for b in row], writes=[Buf("x")], sem=dsem)
        P.barrier()
        if debug is not None and debug.get("_upto", 99) < 1.5:
            return
        inproj(k, ls, hT, b_hT)


def inproj(k, ls, hT, b_hT):
    nc, P, I, S = k.nc, k.P, k.I, k.S
    cosT = k.sb("cosT", [128, L], F32, ls)
    sinS = k.sb("sinS", [128, L], F32, ls)
    perm = k.sb("perm", [128, 128], F32, ls)
    b_cos, b_sin, b_perm = Buf("cos"), Buf("sin"), Buf("perm")
    tsem = P.new_dsem("ip_t")
    P.dma("sp", perm[:], I["c_perm"], writes=[b_perm], sem=tsem)
    with ExitStack() as l0:
        pos = k.sb("pos", [128, L], F32, l0)
        yv = k.sb("yv", [128, L], F32, l0)
        inv = k.sb("inv", [128, 1], F32, l0)
        b_pos, b_y, b_inv = Buf("pos"), Buf("yv"), Buf("inv")
        P.dma("sp", pos[:], I["c_pos"], writes=[b_pos], sem=tsem)
        P.op("act", lambda e: e.activation(out=inv[:], in_=k.colc[:, 1:2], func=AF.Exp, scale=-math.log(10000.0)),
             reads=[k.b_colc], writes=[b_inv])
        P.op("dve", lambda e: e.tensor_scalar(out=yv[:], in0=pos[:], scalar1=inv[:, 0:1], scalar2=1.0 / TWO_PI,
                                              op0=ALU.mult, op1=ALU.mult), reads=[b_pos, b_inv], writes=[b_y])
        range_sin(k, l0, sinS[:], yv[:], [128, L], "rs1", [b_y], [b_sin])
        P.op("dve", lambda e: e.tensor_scalar(out=sinS[:], in0=sinS[:], scalar1=k.colc[:, 0:1], scalar2=None, op0=ALU.mult),
             reads=[b_sin, k.b_colc], writes=[b_sin])
        P.op("dve", lambda e: e.tensor_scalar(out=yv[:], in0=yv[:], scalar1=0.25, scalar2=None, op0=ALU.add),
             reads=[b_y], writes=[b_y])
        range_sin(k, l0, cosT[:], yv[:], [128, L], "rs2", [b_y], [b_cos])
        P.barrier()
    b_wbq = [[Buf(f"wbq{i}_{j}") for j in range(4)] for i in range(2)]
    wb = [k.sb(f"ip_wb{i}", [128, NKC, 512], BF16, ls) for i in range(2)]
    b_wb = [Buf(f"wb{i}") for i in range(2)]
    NOB = 4
    ob = [k.sb(f"ip_ob{i}", [128, 512], BF16, ls) for i in range(NOB)]
    b_ob = [Buf(f"ob{i}") for i in range(NOB)]
    osem = [P.new_dsem(f"ip_os{i}") for i in range(NOB)]
    NOF = 3
    of = [k.sb(f"ip_of{i}", [128, 512], F32, ls) for i in range(NOF)]
    b_of = [Buf(f"of{i}") for i in range(NOF)]
    fsem = [P.new_dsem(f"ip_fs{i}") for i in range(NOF)]
    t1 = [k.sb(f"ip_t1{i}", [128, 512], F32, ls) for i in range(2)]
    b_t1 = [Buf(f"t1{i}") for i in range(2)]
    t2 = [k.sb(f"ip_t2{i}", [128, 512], F32, ls) for i in range(2)]
    b_t2 = [Buf(f"t2{i}") for i in range(2)]
    pb = [k.ps(f"ip_ps{i}", [128, 512], F32, ls) for i in range(4)]
    b_pb = [Buf(f"ipps{i}") for i in range(4)]
    pr = [k.ps(f"ip_pr{i}", [128, 512], F32, ls) for i in range(2)]
    b_pr = [Buf(f"ippr{i}") for i in range(2)]
    wv = I["w_in"].rearrange("(kc p) c -> p kc c", p=128)
    cnt = {"pb": 0, "ob": 0, "of": 0, "r": 0, "ld": 0, "ev": 0}

    rope_pending = []

    def load_block(cb):
        s2 = cb % 2
        for q4 in range(4):
            P.dma("pool", wb[s2][:, q4 * 4:(q4 + 1) * 4, :], wv[:, q4 * 4:(q4 + 1) * 4, cb * 512:(cb + 1) * 512], writes=[b_wbq[s2][q4]])

    def next_ob():
        i = cnt["ob"] % NOB
        cnt["ob"] += 1
        return i

    def evac_eng():
        cnt["ev"] += 1
        return "act" if cnt["ev"] % 2 else "dve"

    def tiles_of(tok0, n, kc):
        return [b_hT[t][kc] for t in range(tok0 // 128, (tok0 + n) // 128)]

    def fm_unit(cb, fc, tok0, n, kind, row0, dst):
        s2 = cb % 2
        pi = cnt["pb"] % 4
        cnt["pb"] += 1
        pt, bp = pb[pi], b_pb[pi]
        for kc in range(NKC):
            P.op("pe", lambda e, kc=kc: e.matmul(pt[:, 0:n], lhsT=wb[s2][:, kc, fc * 128:(fc + 1) * 128],
                                                 rhs=hT[:, kc, tok0:tok0 + n], start=(kc == 0), stop=(kc == NKC - 1)),
                 reads=[b_wbq[s2][kc // 4]] + tiles_of(tok0, n, kc), writes=[bp])
        while rope_pending:
            rope_pending.pop(0)()
        oi = next_ob()
        if kind == "rope":
            ri = cnt["r"] % 2
            cnt["r"] += 1
            fi = cnt["of"] % NOF
            cnt["of"] += 1
            P.op("act", lambda e: e.activation(out=of[fi][:, 0:n], in_=pt[:, 0:n], func=AF.Copy), reads=[bp], writes=[b_of[fi]])
            P.op("dve", lambda e: e.tensor_tensor(out=t1[ri][:, 0:n], in0=of[fi][:, 0:n], in1=cosT[:, tok0:tok0 + n], op=ALU.mult),
                 reads=[b_of[fi], b_cos], writes=[b_t1[ri]])

            def fin():
                P.op("pe", lambda e: e.matmul(pr[ri][:, 0:n], lhsT=perm[:], rhs=of[fi][:, 0:n], start=True, stop=True),
                     reads=[b_perm, b_of[fi]], writes=[b_pr[ri]])
                P.op("dve", lambda e: e.tensor_tensor(out=t2[ri][:, 0:n], in0=pr[ri][:, 0:n], in1=sinS[:, tok0:tok0 + n], op=ALU.mult),
                     reads=[b_pr[ri], b_sin], writes=[b_t2[ri]])
                P.op("pool", lambda e: e.tensor_tensor(out=ob[oi][:, 0:n], in0=t1[ri][:, 0:n], in1=t2[ri][:, 0:n], op=ALU.add),
                     reads=[b_t1[ri], b_t2[ri]], writes=[b_ob[oi]])
                P.dma("sp", dst, ob[oi][:, 0:n], reads=[b_ob[oi]], sem=osem[oi])
            rope_pending.append(fin)
            return
        elif kind == "copy":
            eg = evac_eng()
            if eg == "act":
                P.op("act", lambda e: e.activation(out=ob[oi][:, 0:n], in_=pt[:, 0:n], func=AF.Copy), reads=[bp], writes=[b_ob[oi]])
            else:
                P.op("dve", lambda e: e.tensor_copy(out=ob[oi][:, 0:n], in_=pt[:, 0:n]), reads=[bp], writes=[b_ob[oi]])
        else:
            fn = AF.Silu if kind == "silu" else AF.Sigmoid
            P.op("act", lambda e: e.activation(out=ob[oi][:, 0:n], in_=pt[:, 0:n], func=fn), reads=[bp], writes=[b_ob[oi]])
        P.dma("sp", dst, ob[oi][:, 0:n], reads=[b_ob[oi]], sem=osem[oi])

    def tm_unit(cb, t, kind, dst):
        s2 = cb % 2
        pi = cnt["pb"] % 4
        cnt["pb"] += 1
        pt, bp = pb[pi], b_pb[pi]
        for kc in range(NKC):
            P.op("pe", lambda e, kc=kc: e.matmul(pt[:], lhsT=hT[:, kc, t * 128:(t + 1) * 128], rhs=wb[s2][:, kc, :],
                                                 start=(kc == 0), stop=(kc == NKC - 1)),
                 reads=[b_wbq[s2][kc // 4], b_hT[t][kc]], writes=[bp])
        while rope_pending:
            rope_pending.pop(0)()
        if kind == "f32":
            fi = cnt["of"] % NOF
            cnt["of"] += 1
            P.op("dve", lambda e: e.tensor_copy(out=of[fi][:], in_=pt[:]), reads=[bp], writes=[b_of[fi]])
            P.dma("sp", dst, of[fi][:], reads=[b_of[fi]], sem=fsem[fi])
            return
        oi = next_ob()
        if kind == "copy":
            eg = evac_eng()
            if eg == "act":
                P.op("act", lambda e: e.activation(out=ob[oi][:], in_=pt[:], func=AF.Copy), reads=[bp], writes=[b_ob[oi]])
            else:
                P.op("dve", lambda e: e.tensor_copy(out=ob[oi][:], in_=pt[:]), reads=[bp], writes=[b_ob[oi]])
        else:
            P.op("act", lambda e: e.activation(out=ob[oi][:], in_=pt[:], func=AF.Silu), reads=[bp], writes=[b_ob[oi]])
        P.dma("sp", dst, ob[oi][:], reads=[b_ob[oi]], sem=osem[oi])

    NCB = INW // 512
    load_block(0)
    for cb in range(NCB):
        if cb + 1 < NCB:
            load_block(cb + 1)
        c0 = cb * 512
        if cb < 2:
            for fc in range(4):
                h = cb * 4 + fc
                for tb in range(4):
                    fm_unit(cb, fc, tb * 512, 512, "rope", 0, S["qT"][h, :, tb * 512:(tb + 1) * 512])
        elif cb < 4:
            for fc in range(4):
                h = (cb - 2) * 4 + fc
                for tb in range(4):
                    fm_unit(cb, fc, tb * 512, 512, "rope", 0, S["kT"][h, :, tb * 512:(tb + 1) * 512])
                fm_unit(cb, fc, L, LC, "copy", 0, S["kT"][h, :, L:LT])
        elif cb < 6:
            for t in range(18):
                tm_unit(cb, t, "copy", S["v"][t * 128:(t + 1) * 128, (cb - 4) * 512:(cb - 3) * 512])
        elif cb < 8:
            for t in range(16):
                tm_unit(cb, t, "silu", S["sga"][t * 128:(t + 1) * 128, (cb - 6) * 512:(cb - 5) * 512])
        elif cb == 8:
            for t in range(18):
                tm_unit(cb, t, "f32", S["u"][t * 128:(t + 1) * 128, :])
        elif cb == 9:
            for fc in range(4):
                for tb in range(4):
                    fm_unit(cb, fc, tb * 512, 512, "silu", 0, S["sgsT"][fc * 128:(fc + 1) * 128, tb * 512:(tb + 1) * 512])
        else:
            for fc in range(4):
                r0 = (cb - 10) * 512 + fc * 128
                for tb in range(4):
                    fm_unit(cb, fc, tb * 512, 512, "sigm", 0, S["sgmT"][r0:r0 + 128, tb * 512:(tb + 1) * 512])


def phase_ssm(k):
    nc, P, I, S = k.nc, k.P, k.I, k.S
    MUL, ADD, SUB = ALU.mult, ALU.add, ALU.subtract
    with ExitStack() as ls:
        ToepT = k.sb("ss_toep", [128, 32, 128], BF16, ls)
        RCp = k.sb("ss_rcp", [128, 2, 2, 16, 256], BF16, ls)
        WT = k.sb("ss_wt", [128, 2, 16, 2, 128], BF16, ls)
        A8c = k.sb("ss_a8c", [128, 2, 16, 2], F32, ls)
        A8s = k.sb("ss_a8s", [128, 2, 16, 2], F32, ls)
        AKc = k.sb("ss_akc", [128, 2, 8, 16, 2], F32, ls)
        AKs = k.sb("ss_aks", [128, 2, 8, 16, 2], F32, ls)
        b_ak = Buf("ak")
        b_toep = [Buf(f"toep{g}") for g in range(32)]
        b_rcp = Buf("rcp")
        b_wtl = [[[Buf(f"wt{d}_{g2}_{ri}") for ri in range(2)] for g2 in range(16)] for d in range(2)]
        b_U = [Buf(f"U{g}") for g in range(32)]
        b_zbf = [Buf("zbf0"), Buf("zbf1")]
        b_ygT = Buf("ygT")
        b_a8 = Buf("a8")
        pbk = [k.ps(f"ss_ps{i}", [128, 512], F32, ls) for i in range(8)]
        b_pbk = [Buf(f"ssps{i}") for i in range(8)]
        pc = {"i": 0}

        def nb():
            i = pc["i"] % 8
            pc["i"] += 1
            return pbk[i], b_pbk[i]

        csem = P.new_dsem("ss_c")
        with ExitStack() as l0:
            lre = k.sb("ss_lre", [128, 32], F32, l0)
            lim = k.sb("ss_lim", [128, 32], F32, l0)
            dtt = k.sb("ss_dt", [128, 32], F32, l0)
            alog = k.sb("ss_alog", [128, 32], F32, l0)
            th = k.sb("ss_th", [128, 32], F32, l0)
            b_l, b_dt, b_al = Buf("lrelim"), Buf("dtt"), Buf("alogth")
            for gp in range(2):
                for d in range(2):
                    P.dma("sp", lre[gp * 64:(gp + 1) * 64, d * 16:(d + 1) * 16], I["ssm_lre"][d, gp * 16:(gp + 1) * 16, :].rearrange("g p -> p g"),
                          writes=[b_l], sem=csem, allow_slow_non_contiguous=True)
                    P.dma("sp", lim[gp * 64:(gp + 1) * 64, d * 16:(d + 1) * 16], I["ssm_lim"][d, gp * 16:(gp + 1) * 16, :].rearrange("g p -> p g"),
                          writes=[b_l], sem=csem, allow_slow_non_contiguous=True)
                    P.dma("sp", dtt[gp * 64:(gp + 1) * 64, d * 16:(d + 1) * 16], I["ssm_ls"][d:d + 1, gp * 16:(gp + 1) * 16].broadcast_to([64, 16]),
                          writes=[b_dt], sem=csem)
            P.op("act", lambda e: e.activation(out=dtt[:], in_=dtt[:], func=AF.Exp), reads=[b_dt], writes=[b_dt])
            P.op("dve", lambda e: e.tensor_tensor(out=alog[:], in0=lre[:], in1=dtt[:], op=MUL), reads=[b_l, b_dt], writes=[b_al])
            P.op("dve", lambda e: e.scalar_tensor_tensor(out=th[:], in0=lim[:], scalar=1.0 / TWO_PI, in1=dtt[:], op0=MUL, op1=MUL),
                 reads=[b_l, b_dt], writes=[b_al])
            tabs = {}
            for nm, n in (("A", 9), ("B", 8), ("C", 8)):
                tau = k.sb(f"ss_tau{nm}", [128, 32, n], F32, l0)
                ex = k.sb(f"ss_ex{nm}", [128, 32, n], F32, l0)
                yv = k.sb(f"ss_yv{nm}", [128, 32, n], F32, l0)
                sn = k.sb(f"ss_sn{nm}", [128, 32, n], F32, l0)
                cs = k.sb(f"ss_cs{nm}", [128, 32, n], F32, l0)
                b_tau, b_ex, b_yv, b_sn, b_cs = Buf("tau" + nm), Buf("ex" + nm), Buf("yv" + nm), Buf("sn" + nm), Buf("cs" + nm)
                P.dma("sp", tau[:], I["c_tau" + nm], writes=[b_tau], sem=csem)
                P.op("dve", lambda e, ex=ex, tau=tau, n=n: e.tensor_tensor(out=ex[:], in0=tau[:], in1=alog[:, :, None].broadcast_to([128, 32, n]), op=MUL),
                     reads=[b_tau, b_al], writes=[b_ex])
                P.op("act", lambda e, ex=ex: e.activation(out=ex[:], in_=ex[:], func=AF.Exp), reads=[b_ex], writes=[b_ex])
                P.op("dve", lambda e, yv=yv, tau=tau, n=n: e.tensor_tensor(out=yv[:], in0=tau[:], in1=th[:, :, None].broadcast_to([128, 32, n]), op=MUL),
                     reads=[b_tau, b_al], writes=[b_yv])
                fl = lambda t: t[:].rearrange("p a b -> p (a b)")
                range_sin(k, l0, fl(sn), fl(yv), [128, 32 * n], "ssr1" + nm, [b_yv], [b_sn])
                P.op("dve", lambda e, yv=yv: e.tensor_scalar(out=yv[:], in0=yv[:], scalar1=0.25, scalar2=None, op0=ADD), reads=[b_yv], writes=[b_yv])
                range_sin(k, l0, fl(cs), fl(yv), [128, 32 * n], "ssr2" + nm, [b_yv], [b_cs])
                P.op("dve", lambda e, cs=cs, ex=ex: e.tensor_tensor(out=cs[:], in0=cs[:], in1=ex[:], op=MUL), reads=[b_cs, b_ex], writes=[b_cs])
                P.op("dve", lambda e, sn=sn, ex=ex: e.tensor_tensor(out=sn[:], in0=sn[:], in1=ex[:], op=MUL), reads=[b_sn, b_ex], writes=[b_sn])
                tabs[nm] = (cs, sn, b_cs, b_sn)
            ARA, AIA, b_ARA, b_AIA = tabs["A"]
            ARB, AIB, b_ARB, b_AIB = tabs["B"]
            ARC, AIC, b_ARC, b_AIC = tabs["C"]
            for d in range(2):
                dsl = slice(d * 16, (d + 1) * 16)
                for ri in range(2):
                    P.op("dve", lambda e, d=d, ri=ri, dsl=dsl: e.tensor_copy(out=AKc[:, d, :, :, ri], in_=ARC[:, dsl, :].rearrange("p g k -> p k g")),
                         reads=[b_ARC], writes=[b_ak])
                P.op("dve", lambda e, d=d, dsl=dsl: e.tensor_scalar(out=AKs[:, d, :, :, 0], in0=AIC[:, dsl, :].rearrange("p g k -> p k g"),
                                                                   scalar1=-1.0, scalar2=None, op0=MUL), reads=[b_AIC], writes=[b_ak])
                P.op("dve", lambda e, d=d, dsl=dsl: e.tensor_copy(out=AKs[:, d, :, :, 1], in_=AIC[:, dsl, :].rearrange("p g k -> p k g")),
                     reads=[b_AIC], writes=[b_ak])
            a1 = k.sb("ss_a1", [128, 2, 32], F32, l0)
            b_a1 = Buf("a1")
            for d in range(2):
                i8 = 8 if d == 0 else 0
                i1 = 1 if d == 0 else 7
                dsl = slice(d * 16, (d + 1) * 16)
                for ri in range(2):
                    P.op("dve", lambda e, d=d, ri=ri, i8=i8, dsl=dsl: e.tensor_copy(out=A8c[:, d, :, ri], in_=ARA[:, dsl, i8]), reads=[b_ARA], writes=[b_a8])
                P.op("dve", lambda e, d=d, i8=i8, dsl=dsl: e.tensor_scalar(out=A8s[:, d, :, 0], in0=AIA[:, dsl, i8], scalar1=-1.0, scalar2=None, op0=MUL),
                     reads=[b_AIA], writes=[b_a8])
                P.op("dve", lambda e, d=d, i8=i8, dsl=dsl: e.tensor_copy(out=A8s[:, d, :, 1], in_=AIA[:, dsl, i8]), reads=[b_AIA], writes=[b_a8])
                P.op("dve", lambda e, d=d, i1=i1, dsl=dsl: e.tensor_copy(out=a1[:, 0, dsl], in_=ARA[:, dsl, i1]), reads=[b_ARA], writes=[b_a1])
                P.op("dve", lambda e, d=d, i1=i1, dsl=dsl: e.tensor_copy(out=a1[:, 1, dsl], in_=AIA[:, dsl, i1]), reads=[b_AIA], writes=[b_a1])
            fz = k.sb("ss_fz", [128, 6, 32], F32, l0)
            b_fz = Buf("fz")
            P.op("dve", lambda e: e.tensor_tensor(out=fz[:, 0, :], in0=lre[:], in1=lre[:], op=MUL), reads=[b_l], writes=[b_fz])
            P.op("dve", lambda e: e.tensor_tensor(out=fz[:, 1, :], in0=lim[:], in1=lim[:], op=MUL), reads=[b_l], writes=[b_fz])
            P.op("dve", lambda e: e.tensor_tensor(out=fz[:, 0, :], in0=fz[:, 0, :], in1=fz[:, 1, :], op=ADD), reads=[b_fz], writes=[b_fz])
            P.op("dve", lambda e: e.reciprocal(out=fz[:, 1, :], in_=fz[:, 0, :]), reads=[b_fz], writes=[b_fz])
            P.op("dve", lambda e: e.tensor_scalar(out=fz[:, 0, :], in0=a1[:, 0, :], scalar1=-1.0, scalar2=None, op0=ADD), reads=[b_a1], writes=[b_fz])
            P.op("dve", lambda e: e.tensor_tensor(out=fz[:, 2, :], in0=fz[:, 0, :], in1=lre[:], op=MUL), reads=[b_fz, b_l], writes=[b_fz])
            P.op("dve", lambda e: e.tensor_tensor(out=fz[:, 3, :], in0=a1[:, 1, :], in1=lim[:], op=MUL), reads=[b_a1, b_l], writes=[b_fz])
            P.op("dve", lambda e: e.tensor_tensor(out=fz[:, 2, :], in0=fz[:, 2, :], in1=fz[:, 3, :], op=ADD), reads=[b_fz], writes=[b_fz])
            P.op("dve", lambda e: e.tensor_tensor(out=fz[:, 2, :], in0=fz[:, 2, :], in1=fz[:, 1, :], op=MUL), reads=[b_fz], writes=[b_fz])
            P.op("dve", lambda e: e.tensor_tensor(out=fz[:, 4, :], in0=a1[:, 1, :], in1=lre[:], op=MUL), reads=[b_a1, b_l], writes=[b_fz])
            P.op("dve", lambda e: e.tensor_tensor(out=fz[:, 5, :], in0=fz[:, 0, :], in1=lim[:], op=MUL), reads=[b_fz, b_l], writes=[b_fz])
            P.op("dve", lambda e: e.tensor_tensor(out=fz[:, 4, :], in0=fz[:, 4, :], in1=fz[:, 5, :], op=SUB), reads=[b_fz], writes=[b_fz])
            P.op("dve", lambda e: e.tensor_tensor(out=fz[:, 4, :], in0=fz[:, 4, :], in1=fz[:, 1, :], op=MUL), reads=[b_fz], writes=[b_fz])
            BT = k.sb("ss_BT", [128, 2, 2, 16, 16], F32, l0)
            BB = k.sb("ss_BB", [128, 2, 2, 16, 16], F32, l0)
            CN = k.sb("ss_CN", [128, 2, 2, 2, 128], F32, l0)
            CT = k.sb("ss_CT", [128, 2, 2, 16, 16], F32, l0)
            tA = k.sb("ss_tA", [128, 16, 9, 16], F32, l0)
            tB = k.sb("ss_tB", [128, 16, 9, 16], F32, l0)
            b_BT, b_BB, b_CN, b_CT, b_tA, b_tB = Buf("BT"), Buf("BB"), Buf("CN"), Buf("CT"), Buf("tA"), Buf("tB")
            for d in range(2):
                for ri in range(2):
                    bsrc = I["ssm_bre"] if ri == 0 else I["ssm_bim"]
                    csrc = I["ssm_cre"] if ri == 0 else I["ssm_cim"]
                    for gp in range(2):
                        P.dma("sp", BT[gp * 64:(gp + 1) * 64, d, ri, :, :], bsrc[d, gp * 16:(gp + 1) * 16].rearrange("g p c -> p g c"),
                              writes=[b_BT], sem=csem)
                        for blk in range(2):
                            g0 = gp * 16 + blk * 8
                            P.dma("sp", CN[:, d, ri, blk, gp * 64:(gp + 1) * 64], csrc[d, g0:g0 + 8].rearrange("g c p -> (g c) p"),
                                  writes=[b_CN], sem=csem)
            for d in range(2):
                for ri in range(2):
                    for blk in range(2):
                        pt, bp = nb()
                        P.op("pe", lambda e, d=d, ri=ri, blk=blk, pt=pt: e.transpose(out=pt[:, 0:128], in_=CN[:, d, ri, blk, :], identity=k.ident[:]),
                             reads=[b_CN, k.b_ident], writes=[bp])
                        P.op("dve", lambda e, d=d, ri=ri, blk=blk, pt=pt: e.tensor_copy(
                            out=CT[:, d, ri, blk * 8:(blk + 1) * 8, :].rearrange("p a b -> p (a b)"), in_=pt[:, 0:128]), reads=[bp], writes=[b_CT])
            for d in range(2):
                dsl = slice(d * 16, (d + 1) * 16)
                frb = lambda d=d, dsl=dsl: fz[:, 2, dsl][:, :, None].broadcast_to([128, 16, 16])
                fib = lambda d=d, dsl=dsl: fz[:, 4, dsl][:, :, None].broadcast_to([128, 16, 16])
                t16a = tA[:, :, 0, :]
                t16b = tB[:, :, 0, :]
                P.op("dve", lambda e, d=d, frb=frb: e.tensor_tensor(out=t16a, in0=BT[:, d, 0], in1=frb(), op=MUL), reads=[b_BT, b_fz], writes=[b_tA])
                P.op("dve", lambda e, d=d, fib=fib: e.tensor_tensor(out=t16b, in0=BT[:, d, 1], in1=fib(), op=MUL), reads=[b_BT, b_fz], writes=[b_tB])
                P.op("dve", lambda e, d=d: e.tensor_tensor(out=BB[:, d, 0], in0=t16a, in1=t16b, op=SUB), reads=[b_tA, b_tB], writes=[b_BB])
                P.op("dve", lambda e, d=d, frb=frb: e.tensor_tensor(out=t16a, in0=BT[:, d, 1], in1=frb(), op=MUL), reads=[b_BT, b_fz], writes=[b_tA])
                P.op("dve", lambda e, d=d, fib=fib: e.tensor_tensor(out=t16b, in0=BT[:, d, 0], in1=fib(), op=MUL), reads=[b_BT, b_fz], writes=[b_tB])
                P.op("dve", lambda e, d=d: e.tensor_tensor(out=BB[:, d, 1], in0=t16a, in1=t16b, op=ADD), reads=[b_tA, b_tB], writes=[b_BB])
            P.op("pool", lambda e: e.memset(RCp[:].rearrange("p a b c d -> p (a b c d)"), 0.0), writes=[b_rcp])
            for d in range(2):
                dsl = slice(d * 16, (d + 1) * 16)
                off = 112 if d == 0 else 0
                bc_c = lambda ri, d=d: CT[:, d, ri][:, :, None, :].broadcast_to([128, 16, 9, 16])
                bc_ar = lambda dsl=dsl: ARA[:, dsl, :][:, :, :, None].broadcast_to([128, 16, 9, 16])
                bc_ai = lambda dsl=dsl: AIA[:, dsl, :][:, :, :, None].broadcast_to([128, 16, 9, 16])
                dst = lambda ri, d=d, off=off: RCp[:, d, ri, :, off:off + 144].rearrange("p g (t c) -> p g t c", c=16)
                P.op("dve", lambda e, bc_c=bc_c, bc_ar=bc_ar: e.tensor_tensor(out=tA[:], in0=bc_c(0), in1=bc_ar(), op=MUL), reads=[b_CT, b_ARA], writes=[b_tA])
                P.op("dve", lambda e, bc_c=bc_c, bc_ai=bc_ai: e.tensor_tensor(out=tB[:], in0=bc_c(1), in1=bc_ai(), op=MUL), reads=[b_CT, b_AIA], writes=[b_tB])
                P.op("dve", lambda e, dst=dst: e.tensor_tensor(out=dst(0), in0=tA[:], in1=tB[:], op=SUB), reads=[b_tA, b_tB], writes=[b_rcp])
                P.op("dve", lambda e, bc_c=bc_c, bc_ai=bc_ai: e.tensor_tensor(out=tA[:], in0=bc_c(0), in1=bc_ai(), op=MUL), reads=[b_CT, b_AIA], writes=[b_tA])
                P.op("dve", lambda e, bc_c=bc_c, bc_ar=bc_ar: e.tensor_tensor(out=tB[:], in0=bc_c(1), in1=bc_ar(), op=MUL), reads=[b_CT, b_ARA], writes=[b_tB])
                P.op("dve", lambda e: e.tensor_tensor(out=tA[:], in0=tA[:], in1=tB[:], op=ADD), reads=[b_tA, b_tB], writes=[b_tA])
                P.op("dve", lambda e, dst=dst: e.tensor_scalar(out=dst(1), in0=tA[:], scalar1=-1.0, scalar2=None, op0=MUL), reads=[b_tA], writes=[b_rcp])
            Lp = k.sb("ss_Lp", [128, 64, 240], BF16, l0)
            b_Lp = Buf("Lp")
            P.op("pool", lambda e: e.memset(Lp[:].rearrange("p a b -> p (a b)"), 0.0), writes=[b_Lp])
            P.op("pool", lambda e: e.tensor_copy(out=Lp[:, :, 112:128], in_=BB[:].rearrange("p a b c d -> p (a b c) d")), reads=[b_BB], writes=[b_Lp])
            BW = k.sb("ss_BW", [128, 2, 2, 16, 128], F32, l0)
            b_BW = Buf("BW")
            for d in range(2):
                dsl = slice(d * 16, (d + 1) * 16)
                bc_b = lambda ri, d=d: BB[:, d, ri][:, :, None, :].broadcast_to([128, 16, 8, 16])
                bc_ar = lambda dsl=dsl: ARB[:, dsl, :][:, :, :, None].broadcast_to([128, 16, 8, 16])
                bc_ai = lambda dsl=dsl: AIB[:, dsl, :][:, :, :, None].broadcast_to([128, 16, 8, 16])
                dst = lambda ri, d=d: BW[:, d, ri].rearrange("p g (t c) -> p g t c", c=16)
                ta8 = tA[:, :, 0:8, :]
                tb8 = tB[:, :, 0:8, :]
                P.op("dve", lambda e, bc_b=bc_b, bc_ar=bc_ar: e.tensor_tensor(out=ta8, in0=bc_b(0), in1=bc_ar(), op=MUL), reads=[b_BB, b_ARB], writes=[b_tA])
                P.op("dve", lambda e, bc_b=bc_b, bc_ai=bc_ai: e.tensor_tensor(out=tb8, in0=bc_b(1), in1=bc_ai(), op=MUL), reads=[b_BB, b_AIB], writes=[b_tB])
                P.op("dve", lambda e, dst=dst: e.tensor_tensor(out=dst(0), in0=ta8, in1=tb8, op=SUB), reads=[b_tA, b_tB], writes=[b_BW])
                P.op("dve", lambda e, bc_b=bc_b, bc_ai=bc_ai: e.tensor_tensor(out=ta8, in0=bc_b(0), in1=bc_ai(), op=MUL), reads=[b_BB, b_AIB], writes=[b_tA])
                P.op("dve", lambda e, bc_b=bc_b, bc_ar=bc_ar: e.tensor_tensor(out=tb8, in0=bc_b(1), in1=bc_ar(), op=MUL), reads=[b_BB, b_ARB], writes=[b_tB])
                P.op("dve", lambda e, dst=dst: e.tensor_tensor(out=dst(1), in0=ta8, in1=tb8, op=ADD), reads=[b_tA, b_tB], writes=[b_BW])
            for d in range(2):
                for g2 in range(16):
                    for ri in range(2):
                        pt, bp = nb()
                        P.op("pe", lambda e, d=d, g2=g2, ri=ri, pt=pt: e.transpose(out=pt[:, 0:128], in_=BW[:, d, ri, g2, :], identity=k.ident[:]),
                             reads=[b_BW, k.b_ident], writes=[bp])
                        eng = "act" if (g2 + ri) % 2 else "dve"
                        if eng == "act":
                            P.op("act", lambda e, d=d, g2=g2, ri=ri, pt=pt: e.activation(out=WT[:, d, g2, ri, :], in_=pt[:, 0:128], func=AF.Copy), reads=[bp], writes=[b_wtl[d][g2][ri]])
                        else:
                            P.op("dve", lambda e, d=d, g2=g2, ri=ri, pt=pt: e.tensor_copy(out=WT[:, d, g2, ri, :], in_=pt[:, 0:128]), reads=[bp], writes=[b_wtl[d][g2][ri]])
            for g2 in range(16):
                for gp in range(2):
                    g = gp * 16 + g2
                    pt, bp = nb()
                    psl = slice(gp * 64, (gp + 1) * 64)
                    n = 0
                    for d in range(2):
                        for ri in range(2):
                            for s_ in range(8):
                                w0 = (7 - s_) * 16 if d == 0 else (8 - s_) * 16
                                l0_ = (7 - s_) * 16
                                P.op("pe", lambda e, d=d, ri=ri, g2=g2, w0=w0, l0_=l0_, psl=psl, pt=pt, n=n: e.matmul(
                                    pt[:, 0:128], lhsT=Lp[psl, (d * 2 + ri) * 16 + g2, l0_:l0_ + 128], rhs=RCp[psl, d, ri, g2, w0:w0 + 128],
                                    start=(n == 0), stop=(n == 31)), reads=[b_Lp, b_rcp], writes=[bp])
                                n += 1
                    P.op("dve" if g % 2 else "act",
                         (lambda e, g=g, pt=pt: e.tensor_copy(out=ToepT[:, g, :], in_=pt[:, 0:128])) if g % 2 else
                         (lambda e, g=g, pt=pt: e.activation(out=ToepT[:, g, :], in_=pt[:, 0:128], func=AF.Copy)),
                         reads=[bp], writes=[b_toep[g]])
            P.barrier()
        Ubuf = k.sb("ss_ubuf", [128, 32, 320], BF16, ls)
        Zbf = k.sb("ss_zbf", [128, 2, 16, 2, 288], BF16, ls)
        if k.debug.get("_ssm_upto", 99) < 1:
            return
        with ExitStack() as l1:
            ucm = [k.sb(f"ss_ucm{i}", [128, 8, 512], F32, l1) for i in range(2)]
            b_ucm = [Buf(f"ucm{i}") for i in range(2)]
            usem = [P.new_dsem(f"ss_us{i}") for i in range(2)]
            ucg = k.sb("ss_ucg", [128, 32, 128], F32, l1)
            b_ucg = Buf("ucg")
            for jt in range(3):
                si = jt % 2
                nj = 128 if jt < 2 else 32
                r0 = jt * 1024
                P.dma("sp", ucm[si][0:nj], S["u"][r0:r0 + nj * 8, :].rearrange("(j s) c -> j s c", s=8), writes=[b_ucm[si]], sem=usem[si])
                P.op("dve", lambda e, si=si, nj=nj: e.tensor_copy(out=ucg[0:nj].rearrange("p g (s c) -> p g s c", c=16),
                                                                 in_=ucm[si][0:nj].rearrange("p s (g c) -> p g s c", c=16)),
                     reads=[b_ucm[si]], writes=[b_ucg])
                for g0 in range(0, 32, 4):
                    pt, bp = nb()
                    for gg in range(4):
                        g = g0 + gg
                        P.op("pe", lambda e, si=si, nj=nj, g=g, gg=gg, pt=pt: e.transpose(
                            out=pt[:, gg * 128:gg * 128 + nj], in_=ucg[0:nj, g, :], identity=k.ident[0:nj, 0:nj]),
                            reads=[b_ucg, k.b_ident], writes=[bp])
                    src = lambda pt=pt, nj=nj: pt[:].rearrange("p (a b) -> p a b", b=128)[:, :, 0:nj]
                    cols = [32 + jt * 128] if jt < 2 else [0, 288]
                    for ci, c0 in enumerate(cols):
                        eng = "act" if (g0 // 4 + ci) % 2 else "dve"
                        if eng == "act":
                            P.op("act", lambda e, g0=g0, c0=c0, nj=nj, src=src: e.activation(out=Ubuf[:, g0:g0 + 4, c0:c0 + nj], in_=src(), func=AF.Copy),
                                 reads=[bp], writes=[b_U[g0 + i] for i in range(4)])
                        else:
                            P.op("dve", lambda e, g0=g0, c0=c0, nj=nj, src=src: e.tensor_copy(out=Ubuf[:, g0:g0 + 4, c0:c0 + nj], in_=src()),
                                 reads=[bp], writes=[b_U[g0 + i] for i in range(4)])
            P.barrier()
        if k.debug.get("_ssm_upto", 99) < 2:
            return
        with ExitStack() as l2:
            Z = [k.sb(f"ss_Z{d}", [128, 16, 2, 288], F32, l2) for d in range(2)]
            b_Z = [Buf("Z0"), Buf("Z1")]
            b_Ze = [[Buf(f"Ze{d}_{i}") for i in range(32)] for d in range(2)]
            zjoin = {0: True, 1: True}
            for d in range(2):
                j0 = 0 if d == 0 else 32
                for g2 in range(16):
                    for ri in range(2):
                        pt, bp = nb()
                        for gp in range(2):
                            P.op("pe", lambda e, d=d, g2=g2, ri=ri, gp=gp, pt=pt, j0=j0: e.matmul(
                                pt[gp * 64:(gp + 1) * 64, 0:288], lhsT=WT[:, d, g2, ri, gp * 64:(gp + 1) * 64], rhs=Ubuf[:, gp * 16 + g2, j0:j0 + 288],
                                start=True, stop=True), reads=[b_wtl[d][g2][ri], b_U[gp * 16 + g2]], writes=[bp])
                        if (g2 + ri) % 2:
                            P.op("act", lambda e, d=d, g2=g2, ri=ri, pt=pt: e.activation(out=Z[d][:, g2, ri, :], in_=pt[:, 0:288], func=AF.Copy), reads=[bp], writes=[b_Ze[d][g2 * 2 + ri]])
                        else:
                            P.op("dve", lambda e, d=d, g2=g2, ri=ri, pt=pt: e.tensor_copy(out=Z[d][:, g2, ri, :], in_=pt[:, 0:288]), reads=[bp], writes=[b_Ze[d][g2 * 2 + ri]])
            k.dbg("V_dbg", [2, 128, 16 * 2 * 288], F32, lambda dd: (dd[0], Z[0][:].rearrange("p a b c -> p (a b c)")), b_Ze[0])
            k.dbg("V_dbg", [2, 128, 16 * 2 * 288], F32, lambda dd: (dd[1], Z[1][:].rearrange("p a b c -> p (a b c)")), [b_Z[1]])
            m1 = [k.sb(f"ss_m1{d}", [128, 16, 2, 36], F32, l2) for d in range(2)]
            m2 = [k.sb(f"ss_m2{d}", [128, 16, 2, 36], F32, l2) for d in range(2)]
            b_m1 = [Buf("m10"), Buf("m11")]
            b_m2 = [Buf("m20"), Buf("m21")]
            Zv = [Z[d][:].rearrange("p g r (K c) -> p g r K c", c=8) for d in range(2)]

            def cmul_add(eng, d, kidx, dst, src, src_sw, nK):
                if nK:
                    ac = AKc[:, d, kidx][:, :, :, None].broadcast_to([128, 16, 2, nK])
                    as_ = AKs[:, d, kidx][:, :, :, None].broadcast_to([128, 16, 2, nK])
                    t1_, t2_ = m1[d][:, :, :, 0:nK], m2[d][:, :, :, 0:nK]
                else:
                    ac, as_ = AKc[:, d, kidx], AKs[:, d, kidx]
                    t1_, t2_ = m1[d][:, :, :, 0], m2[d][:, :, :, 0]
                extra = []
                if zjoin[d]:
                    zjoin[d] = False
                    extra = list(b_Ze[d])
                P.op(eng, lambda e: e.tensor_tensor(out=t1_, in0=src, in1=ac, op=MUL), reads=[b_Z[d], b_ak] + extra, writes=[b_m1[d]])
                P.op(eng, lambda e: e.tensor_tensor(out=t2_, in0=src_sw, in1=as_, op=MUL), reads=[b_Z[d], b_ak], writes=[b_m2[d]])
                P.op(eng, lambda e: e.tensor_tensor(out=t1_, in0=t1_, in1=t2_, op=ADD), reads=[b_m1[d], b_m2[d]], writes=[b_m1[d]])
                P.op(eng, lambda e: e.tensor_tensor(out=dst, in0=dst, in1=t1_, op=ADD), reads=[b_Z[d], b_m1[d]], writes=[b_Z[d]])

            for i in range(1, 8):
                r = i
                cmul_add("dve", 0, 0, Zv[0][:, :, :, :, r], Zv[0][:, :, :, :, r - 1], Zv[0][:, :, ::-1, :, r - 1], 36)
                r = 7 - i
                cmul_add("dve", 1, 0, Zv[1][:, :, :, :, r], Zv[1][:, :, :, :, r + 1], Zv[1][:, :, ::-1, :, r + 1], 36)
            for i in range(1, 36):
                K = i
                cmul_add("dve", 0, 7, Zv[0][:, :, :, K, 7], Zv[0][:, :, :, K - 1, 7], Zv[0][:, :, ::-1, K - 1, 7], 0)
                K = 35 - i
                cmul_add("pool", 1, 7, Zv[1][:, :, :, K, 0], Zv[1][:, :, :, K + 1, 0], Zv[1][:, :, ::-1, K + 1, 0], 0)
            for i in range(7):
                r = i
                cmul_add("dve", 0, r, Zv[0][:, :, :, 1:36, r], Zv[0][:, :, :, 0:35, 7], Zv[0][:, :, ::-1, 0:35, 7], 35)
                r = 7 - i
                cmul_add("dve", 1, 7 - r, Zv[1][:, :, :, 0:35, r], Zv[1][:, :, :, 1:36, 0], Zv[1][:, :, ::-1, 1:36, 0], 35)
            for d in range(2):
                eng = "dve" if d == 0 else "pool"
                P.op(eng, lambda e, d=d: e.tensor_copy(out=Zbf[:, d].rearrange("p a b c -> p (a b c)"), in_=Z[d][:].rearrange("p a b c -> p (a b c)")),
                     reads=[b_Z[d]], writes=[b_zbf[d]])
            k.dbg("Z_dbg", [2, 128, 16 * 2 * 288], F32, lambda dd: (dd[0], Z[0][:].rearrange("p a b c -> p (a b c)")), [b_Z[0]])
            k.dbg("Z_dbg", [2, 128, 16 * 2 * 288], F32, lambda dd: (dd[1], Z[1][:].rearrange("p a b c -> p (a b c)")), [b_Z[1]])
            P.barrier()
        if k.debug.get("_ssm_upto", 99) < 3:
            return
        ygT = k.sb("ss_ygT", [128, 4, L], BF16, ls)
        with ExitStack() as l3:
            ycm = k.sb("ss_ycm", [128, 8, 512], F32, l3)
            b_ycm = [Buf(f"ycm{g}") for g in range(32)]
            ut = k.sb("ss_ut", [128, 8, 512], F32, l3)
            b_ut = Buf("ut")
            utsem = P.new_dsem("ss_uts")
            Dfull = k.sb("ss_D", [128, 512], F32, l3)
            b_D = Buf("Dfull")
            P.dma("sp", Dfull[:], I["ssm_d"][0:1, :].broadcast_to([128, 512]), writes=[b_D], sem=csem)
            sq = [k.sb(f"ss_sq{i}", [128, 512], F32, l3) for i in range(2)]
            b_sq = [Buf("sq0"), Buf("sq1")]
            GC = math.sqrt(2.0 / math.pi)
            for jt in range(2):
                P.dma("sp", ut[:], S["u"][jt * 1024:(jt + 1) * 1024, :].rearrange("(j s) c -> j s c", s=8), writes=[b_ut], sem=utsem)
                for g in range(32):
                    gp, g2 = g // 16, g % 16
                    psl = slice(gp * 64, (gp + 1) * 64)
                    pt, bp = nb()
                    c0 = 32 + jt * 128
                    P.op("pe", lambda e, g=g, c0=c0, pt=pt: e.matmul(pt[:, 0:128], lhsT=Ubuf[:, g, c0:c0 + 128], rhs=ToepT[:, g, :], start=True, stop=False),
                         reads=[b_U[g], b_toep[g]], writes=[bp])
                    for d in range(2):
                        jz = (31 + jt * 128) if d == 0 else (1 + jt * 128)
                        w0 = 128 if d == 0 else 0
                        for ri in range(2):
                            last = (d == 1 and ri == 1)
                            P.op("pe", lambda e, d=d, ri=ri, g2=g2, psl=psl, jz=jz, w0=w0, pt=pt, last=last: e.matmul(
                                pt[:, 0:128], lhsT=Zbf[psl, d, g2, ri, jz:jz + 128], rhs=RCp[psl, d, ri, g2, w0:w0 + 128], start=False, stop=last),
                                reads=[b_zbf[d], b_rcp], writes=[bp])
                    src = lambda pt=pt: pt[:, 0:128].rearrange("p (t c) -> p t c", c=16)
                    P.op("dve", lambda e, g=g, src=src: e.tensor_tensor(out=ycm[:, :, g * 16:(g + 1) * 16], in0=ut[:, :, g * 16:(g + 1) * 16],
                                                                       in1=Dfull[:, g * 16:(g + 1) * 16][:, None, :].broadcast_to([128, 8, 16]), op=MUL),
                         reads=[b_ut, b_D], writes=[b_ycm[g]])
                    P.op("dve", lambda e, g=g, src=src: e.tensor_tensor(out=ycm[:, :, g * 16:(g + 1) * 16], in0=ycm[:, :, g * 16:(g + 1) * 16], in1=src(), op=ADD),
                         reads=[bp, b_ycm[g]], writes=[b_ycm[g]])
                k.dbg("y_dbg", [L, 512], F32, lambda dd, jt=jt: (dd[jt * 1024:(jt + 1) * 1024, :].rearrange("(j s) c -> j s c", s=8), ycm[:]), b_ycm)
                for t in range(8):
                    i = t % 2
                    P.op("dve", lambda e, t=t, i=i: e.tensor_tensor(out=sq[i][:], in0=ycm[:, t, :], in1=ycm[:, t, :], op=MUL), reads=b_ycm, writes=[b_sq[i]])
                    P.op("dve", lambda e, t=t, i=i: e.tensor_scalar(out=sq[i][:], in0=sq[i][:], scalar1=0.044715, scalar2=1.0, op0=MUL, op1=ADD), reads=[b_sq[i]], writes=[b_sq[i]])
                    P.op("dve", lambda e, t=t, i=i: e.tensor_tensor(out=sq[i][:], in0=sq[i][:], in1=ycm[:, t, :], op=MUL), reads=[b_sq[i]] + b_ycm, writes=[b_sq[i]])
                    P.op("act", lambda e, t=t, i=i: e.activation(out=sq[i][:], in_=sq[i][:], func=AF.Sigmoid, scale=2.0 * GC), reads=[b_sq[i]], writes=[b_sq[i]])
                    P.op("dve", lambda e, t=t, i=i: e.tensor_tensor(out=sq[i][:], in0=sq[i][:], in1=ycm[:, t, :], op=MUL), reads=[b_sq[i]] + b_ycm, writes=[b_sq[i]])
                    pt, bp = nb()
                    for chb in range(4):
                        P.op("pe", lambda e, i=i, chb=chb, pt=pt: e.transpose(out=pt[:, chb * 128:(chb + 1) * 128], in_=sq[i][:, chb * 128:(chb + 1) * 128], identity=k.ident[:]),
                             reads=[b_sq[i], k.b_ident], writes=[bp])
                    tsl = slice(jt * 1024 + t, (jt + 1) * 1024, 8)
                    P.op("act", lambda e, pt=pt, tsl=tsl: e.activation(out=ygT[:, :, tsl], in_=pt[:].rearrange("p (a b) -> p a b", b=128), func=AF.Copy),
                         reads=[bp], writes=[b_ygT])
            P.barrier()
        if k.debug.get("_ssm_upto", 99) < 4:
            return
        with ExitStack() as l4:
            wg32 = k.sb("ss_wg32", [128, 4, 512], F32, l4)
            wg = k.sb("ss_wg", [128, 4, 512], BF16, l4)
            bg = k.sb("ss_bg", [128, 4], F32, l4)
            b_wg32, b_wg, b_bg = Buf("wg32"), Buf("wg"), Buf("bg")
            P.dma("sp", wg32[:], I["w_glu"].rearrange("(fc p) c -> p fc c", p=128), writes=[b_wg32], sem=csem)
            P.dma("sp", bg[:], I["b_glu"].rearrange("(fc p) -> p fc", p=128), writes=[b_bg], sem=csem, allow_slow_non_contiguous=True)
            P.op("dve", lambda e: e.tensor_copy(out=wg[:], in_=wg32[:]), reads=[b_wg32], writes=[b_wg])
            gst = [k.sb(f"ss_gst{i}", [128, 512], BF16, l4) for i in range(2)]
            b_gst = [Buf("gst0"), Buf("gst1")]
            gsem = [P.new_dsem(f"ss_gs{i}") for i in range(2)]
            sg = [k.sb(f"ss_sg{i}", [128, 512], F32, l4) for i in range(2)]
            b_sg = [Buf("sg0"), Buf("sg1")]
            so = [k.sb(f"ss_so{i}", [128, 512], BF16, l4) for i in range(2)]
            b_so = [Buf("so0"), Buf("so1")]
            sosem = [P.new_dsem(f"ss_sos{i}") for i in range(2)]
            ui = 0
            for fo in range(4):
                for tb in range(4):
                    i = ui % 2
                    ui += 1
                    tsl = slice(tb * 512, (tb + 1) * 512)
                    P.dma("sp", gst[i][:], S["sgsT"][fo * 128:(fo + 1) * 128, tsl], writes=[b_gst[i]], sem=gsem[i])
                    pt, bp = nb()
                    for fc in range(4):
                        P.op("pe", lambda e, fc=fc, fo=fo, tsl=tsl, pt=pt: e.matmul(pt[:], lhsT=wg[:, fc, fo * 128:(fo + 1) * 128], rhs=ygT[:, fc, tsl],
                                                                               start=(fc == 0), stop=(fc == 3)), reads=[b_wg, b_ygT], writes=[bp])
                    P.op("act", lambda e, i=i, fo=fo, pt=pt: e.activation(out=sg[i][:], in_=pt[:], func=AF.Sigmoid, bias=bg[:, fo:fo + 1]),
                         reads=[bp, b_bg], writes=[b_sg[i]])
                    P.op("dve", lambda e, i=i, fo=fo, tsl=tsl: e.tensor_tensor(out=sg[i][:], in0=sg[i][:], in1=ygT[:, fo, tsl], op=MUL),
                         reads=[b_sg[i], b_ygT], writes=[b_sg[i]])
                    P.op("dve", lambda e, i=i: e.tensor_tensor(out=so[i][:], in0=sg[i][:], in1=gst[i][:], op=MUL),
                         reads=[b_sg[i], b_gst[i]], writes=[b_so[i]])
                    P.dma("sp", S["sbrT"][fo * 128:(fo + 1) * 128, tsl], so[i][:], reads=[b_so[i]], sem=sosem[i])


def phase_attn(k):
    nc, P, I, S = k.nc, k.P, k.I, k.S
    with ExitStack() as ls:
        lamv = k.sb("at_lamv", [128, 4, 64], F32, ls)
        lw = k.sb("at_lw", [128, 8], F32, ls)
        G = k.sb("at_G", [128, 128], F32, ls)
        b_lamv, b_lw, b_G = Buf("lamv"), Buf("lw"), Buf("G")
        csem = P.new_dsem("at_c")
        P.dma("sp", lamv[:].rearrange("p a b -> p (a b)"), I["lam"].rearrange("a b -> (a b)").partition_broadcast(128),
              writes=[b_lamv], sem=csem)
        P.dma("sp", G[:], I["subln_g"][0:1, :].broadcast_to([128, 128]), writes=[b_G], sem=csem)
        P.op("dve", lambda e: e.tensor_scalar(out=G[:], in0=G[:], scalar1=(1.0 - LAM_INIT), scalar2=None, op0=ALU.mult),
             reads=[b_G], writes=[b_G])
        for i in range(2):
            P.op("dve", lambda e, i=i: e.tensor_tensor(out=lamv[:, 2 * i, :], in0=lamv[:, 2 * i, :], in1=lamv[:, 2 * i + 1, :], op=ALU.mult),
                 reads=[b_lamv], writes=[b_lamv])
            P.op("dve", lambda e, i=i: e.tensor_reduce(out=lw[:, i:i + 1], in_=lamv[:, 2 * i, :], axis=mybir.AxisListType.X, op=ALU.add),
                 reads=[b_lamv], writes=[b_lw])
        P.op("act", lambda e: e.activation(out=lw[:, 2:4], in_=lw[:, 0:2], func=AF.Exp), reads=[b_lw], writes=[b_lw])
        P.op("dve", lambda e: e.tensor_tensor(out=lw[:, 4:5], in0=lw[:, 3:4], in1=lw[:, 2:3], op=ALU.subtract), reads=[b_lw], writes=[b_lw])
        P.op("dve", lambda e: e.tensor_scalar(out=lw[:, 5:6], in0=lw[:, 4:5], scalar1=-LAM_INIT, scalar2=None, op0=ALU.add),
             reads=[b_lw], writes=[b_lw])
        neglam = lw[:, 5:6]
        qTs = [k.sb(f"at_q{i}", [128, L], BF16, ls) for i in range(2)]
        kTs = [k.sb(f"at_k{i}", [128, LT], BF16, ls) for i in range(2)]
        Vs = [k.sb(f"at_v{i}", [128, 18, 130], BF16, ls) for i in range(2)]
        gas = [k.sb(f"at_ga{i}", [128, 16, 128], BF16, ls) for i in range(2)]
        aTs = [k.sb(f"at_aT{i}", [128, L], BF16, ls) for i in range(2)]
        b_q = [Buf(f"atq{i}") for i in range(2)]
        b_k = [Buf(f"atk{i}") for i in range(2)]
        b_v = [Buf(f"atv{i}") for i in range(2)]
        b_ga = [Buf(f"atga{i}") for i in range(2)]
        b_aT = [Buf(f"ataT{i}") for i in range(2)]
        hsem = [P.new_dsem(f"at_h{i}") for i in range(2)]
        asem = [P.new_dsem(f"at_a{i}") for i in range(2)]
        for i in range(2):
            P.op("pool", lambda e, i=i: e.memset(Vs[i][:, :, 128:130], 1.0), writes=[b_v[i]])
        PT = [k.sb(f"at_pt{i}", [128, 2, 18, 256], BF16, ls) for i in range(2)]
        b_PT = [[[Buf(f"pt{i}_{c}_{kp}") for kp in range(9)] for c in range(2)] for i in range(2)]
        sbk = [k.ps(f"at_s{i}", [128, 512], F32, ls) for i in range(3)]
        b_sbk = [Buf(f"ats{i}") for i in range(3)]
        obk = [k.ps(f"at_o{i}", [128, 512], F32, ls) for i in range(4)]
        b_obk = [Buf(f"ato{i}") for i in range(4)]
        tbk = k.ps("at_t", [128, 512], F32, ls)
        b_tbk = Buf("att")
        sm = [k.sb(f"at_sm{i}", [128, 8], F32, ls) for i in range(2)]
        b_sm = [Buf(f"atsm{i}") for i in range(2)]
        tmp = [k.sb(f"at_tmp{i}", [128, 128], F32, ls) for i in range(2)]
        b_tmp = [Buf(f"attmp{i}") for i in range(2)]
        ov = [k.sb(f"at_ov{i}", [128, 128], F32, ls) for i in range(2)]
        b_ov = [Buf(f"atov{i}") for i in range(2)]
        junk = k.sb("at_junk", [128, 128], F32, ls)
        b_junk = Buf("atjunk")
        cnt = {"s": 0, "u": 0}

        def load_head(h):
            s = h % 2
            P.dma("sp", qTs[s][:], S["qT"][h], writes=[b_q[s]], sem=hsem[s])
            P.dma("sp", kTs[s][:], S["kT"][h], writes=[b_k[s]], sem=hsem[s])
            P.dma("sp", Vs[s][:, :, 0:128], S["v"][:, h * 128:(h + 1) * 128].rearrange("(t p) e -> p t e", p=128),
                  writes=[b_v[s]], sem=hsem[s])
            P.dma("sp", gas[s][:], S["sga"][:, h * 128:(h + 1) * 128].rearrange("(t p) e -> p t e", p=128),
                  writes=[b_ga[s]], sem=hsem[s])

        def A_steps(h, qb):
            s = h % 2
            ps_ = qb % 2
            steps = []
            for kp in range(9):
                def step(kp=kp):
                    for c in range(2):
                        si = cnt["s"] % 3
                        cnt["s"] += 1
                        for j in range(2):
                            kt = 2 * kp + j
                            P.op("pe", lambda e, kt=kt, j=j, c=c, si=si: e.matmul(
                                sbk[si][:, j * 256:(j + 1) * 256], lhsT=kTs[s][c * 64:(c + 1) * 64, kt * 128:(kt + 1) * 128],
                                rhs=qTs[s][c * 64:(c + 1) * 64, qb * 256:(qb + 1) * 256], start=True, stop=True),
                                reads=[b_k[s], b_q[s]], writes=[b_sbk[si]])
                        P.op("act", lambda e, c=c, kp=kp, si=si: e.activation(
                            out=PT[ps_][:, c, 2 * kp:2 * kp + 2, :].rearrange("p a b -> p (a b)"), in_=sbk[si][:], func=AF.Exp, scale=0.125),
                            reads=[b_sbk[si]], writes=[b_PT[ps_][c][kp]])
                steps.append(step)
            return steps

        def B_gen(h, qb):
            s = h % 2
            ps_ = qb % 2
            for qi_ in range(2):
                yield from unitB(h, qb, qi_, s, ps_)

        def unitB(h, qb, qi, s, ps_):
            if True:
                qt = qb * 2 + qi
                u = cnt["u"] % 2
                cnt["u"] += 1
                banks = [obk[u * 2], obk[u * 2 + 1]]
                bb = [b_obk[u * 2], b_obk[u * 2 + 1]]
                for c in range(2):
                    for kt in range(18):
                        P.op("pe", lambda e, c=c, kt=kt: e.matmul(
                            banks[c][:, 0:129], lhsT=PT[ps_][:, c, kt, qi * 128:(qi + 1) * 128], rhs=Vs[s][:, kt, 0:129],
                            start=(kt == 0), stop=(kt == 17)),
                            reads=[b_PT[ps_][c][kt // 2], b_v[s]], writes=[bb[c]])
                        yield
                flush_pending()
                smt, bsm = sm[u], b_sm[u]
                for c in range(2):
                    P.op("dve", lambda e, c=c: e.reciprocal(out=smt[:, c:c + 1], in_=banks[c][:, 128:129]), reads=[bb[c]], writes=[bsm])
                P.op("dve", lambda e: e.tensor_tensor(out=smt[:, 2:3], in0=smt[:, 1:2], in1=neglam, op=ALU.mult), reads=[bsm, b_lw], writes=[bsm])
                P.op("dve", lambda e: e.tensor_scalar(out=tmp[u][:], in0=banks[1][:, 0:128], scalar1=smt[:, 2:3], scalar2=None, op0=ALU.mult),
                     reads=[bb[1], bsm], writes=[b_tmp[u]])
                P.op("dve", lambda e: e.scalar_tensor_tensor(out=ov[u][:], in0=banks[0][:, 0:128], scalar=smt[:, 0:1], in1=tmp[u][:],
                                                            op0=ALU.mult, op1=ALU.add),
                     reads=[bb[0], bsm, b_tmp[u]], writes=[b_ov[u]])
                P.op("dve", lambda e: e.tensor_tensor(out=tmp[u][:], in0=ov[u][:], in1=ov[u][:], op=ALU.mult),
                     reads=[b_ov[u]], writes=[b_tmp[u]])
                P.op("dve", lambda e: e.tensor_reduce(out=smt[:, 3:4], in_=tmp[u][:], axis=mybir.AxisListType.X, op=ALU.add),
                     reads=[b_tmp[u]], writes=[bsm])
                P.op("dve", lambda e: e.tensor_scalar(out=smt[:, 4:5], in0=smt[:, 3:4], scalar1=1.0 / 128, scalar2=EPS, op0=ALU.mult, op1=ALU.add),
                     reads=[bsm], writes=[bsm])
                def fin():
                    P.op("act", lambda e: e.activation(out=smt[:, 5:6], in_=smt[:, 4:5], func=AF.Ln), reads=[bsm], writes=[bsm])
                    P.op("act", lambda e: e.activation(out=smt[:, 6:7], in_=smt[:, 5:6], func=AF.Exp, scale=-0.5), reads=[bsm], writes=[bsm])
                    P.op("dve", lambda e: e.scalar_tensor_tensor(out=ov[u][:], in0=ov[u][:], scalar=smt[:, 6:7], in1=G[:], op0=ALU.mult, op1=ALU.mult),
                         reads=[b_ov[u], bsm, b_G], writes=[b_ov[u]])
                    P.op("pool", lambda e: e.tensor_tensor(out=ov[u][:], in0=ov[u][:], in1=gas[s][:, qt, :], op=ALU.mult),
                         reads=[b_ov[u], b_ga[s]], writes=[b_ov[u]])

                    def fin2():
                        P.op("pe", lambda e: e.transpose(out=tbk[:, 0:128], in_=ov[u][:], identity=k.ident[:]), reads=[b_ov[u], k.b_ident], writes=[b_tbk])
                        P.op("dve", lambda e: e.tensor_copy(out=aTs[s][:, qt * 128:(qt + 1) * 128], in_=tbk[:, 0:128]),
                             reads=[b_tbk], writes=[b_aT[s]])
                    pending2.append(fin2)
                pending.append(fin)

        pending = []
        pending2 = []

        def flush_pending():
            while pending2:
                pending2.pop(0)()
            while pending:
                pending.pop(0)()

        def interleave(a_steps, bgen, per=8):
            for st_ in a_steps:
                st_()
                if bgen is not None:
                    for _ in range(per):
                        try:
                            next(bgen)
                        except StopIteration:
                            bgen = None
                            break
            if bgen is not None:
                for _ in bgen:
                    pass

        load_head(0)
        load_head(1)
        interleave(A_steps(0, 0), None)
        for h in range(HEADS):
            for qb in range(8):
                if qb + 1 < 8:
                    nxt = A_steps(h, qb + 1)
                elif h + 1 < HEADS:
                    nxt = A_steps(h + 1, 0)
                else:
                    nxt = []
                interleave(nxt, B_gen(h, qb))
            flush_pending()
            flush_pending()
            P.dma("pool", S["abrT"][h * 128:(h + 1) * 128, :], aTs[h % 2][:], reads=[b_aT[h % 2]], sem=asem[h % 2])
            if h + 2 < HEADS:
                load_head(h + 2)


def phase_merge(k):
    nc, P, I, S = k.nc, k.P, k.I, k.S
    with ExitStack() as ls:
        mT = k.sb("mg_mT", [128, NKC, L], BF16, ls)
        b_mT = [Buf(f"mT{tb}") for tb in range(4)]
        wout = k.sb("mg_wout", [128, NKC, D], BF16, ls)
        b_wout = Buf("wout")
        NXB = 3
        wov = I["w_out"].rearrange("(kc p) c -> p kc c", p=128)
        wo_state = {"kc": 0}
        b_woutc = [Buf(f"woutc{i}") for i in range(NKC)]

        def load_wout_chunk():
            kc = wo_state["kc"]
            if kc >= NKC:
                return
            wo_state["kc"] += 1
            P.dma("pool", wout[:, kc, :], wov[:, kc, :], writes=[b_woutc[kc]])
        with ExitStack() as l1:
            abrT = k.sb("mg_abrT", [128, 8, L], BF16, l1)
            sbrT = k.sb("mg_sbrT", [128, 4, L], BF16, l1)
            b_abrT, b_sbrT = Buf("abrT"), Buf("sbrT")
            lsem = P.new_dsem("mg_l")
            P.dma("sp", abrT[:], S["abrT"].rearrange("(fc p) t -> p fc t", p=128), writes=[b_abrT], sem=lsem)
            P.dma("sp", sbrT[:], S["sbrT"].rearrange("(fc p) t -> p fc t", p=128), writes=[b_sbrT], sem=lsem)
            NWS = 2
            wbf = [k.sb(f"mg_wbf{i}", [128, 12, 128], BF16, l1) for i in range(NWS)]
            b_wbf = [Buf(f"mgwbf{i}") for i in range(NWS)]
            NG = 2
            gt = [k.sb(f"mg_gt{i}", [128, 2, L], BF16, l1) for i in range(NG)]
            b_gt = [Buf(f"mggt{i}") for i in range(NG)]
            t1 = [k.sb(f"mg_t1{i}", [128, 512], F32, l1) for i in range(2)]
            t2 = [k.sb(f"mg_t2{i}", [128, 512], F32, l1) for i in range(2)]
            b_t1 = [Buf(f"mgt1{i}") for i in range(2)]
            b_t2 = [Buf(f"mgt2{i}") for i in range(2)]
            pa = [k.ps(f"mg_pa{i}", [128, 512], F32, l1) for i in range(2)]
            pp = [k.ps(f"mg_pp{i}", [128, 512], F32, l1) for i in range(2)]
            b_pa = [Buf(f"mgpa{i}") for i in range(2)]
            b_pp = [Buf(f"mgpp{i}") for i in range(2)]
            wpa_v = I["w_pa"].rearrange("(fc p) c -> p fc c", p=128)
            wps_v = I["w_ps"].rearrange("(fc p) c -> p fc c", p=128)
            ui = 0

            def load_w(fo):
                s = fo % NWS
                P.dma("pool", wbf[s][:, 0:8, :], wpa_v[:, :, fo * 128:(fo + 1) * 128], writes=[b_wbf[s]])
                P.dma("pool", wbf[s][:, 8:12, :], wps_v[:, :, fo * 128:(fo + 1) * 128], writes=[b_wbf[s]])
                gi = fo % NG
                P.dma("sp", gt[gi][:, 0, :], S["sgmT"][fo * 128:(fo + 1) * 128, :], writes=[b_gt[gi]])
                P.dma("sp", gt[gi][:, 1, :], S["sgmT"][D + fo * 128:D + (fo + 1) * 128, :], writes=[b_gt[gi]])

            load_w(0)
            for fo in range(NKC):
                if fo + 1 < NKC:
                    load_w(fo + 1)
                load_wout_chunk()
                s = fo % NWS
                gi = fo % NG
                for tb in range(4):
                    u2 = ui % 2
                    ui += 1
                    tsl = slice(tb * 512, (tb + 1) * 512)
                    for fc in range(8):
                        P.op("pe", lambda e, fc=fc, s=s, tsl=tsl, u2=u2: e.matmul(pa[u2][:], lhsT=wbf[s][:, fc, :], rhs=abrT[:, fc, tsl],
                                                                        start=(fc == 0), stop=(fc == 7)),
                             reads=[b_wbf[s], b_abrT], writes=[b_pa[u2]])
                    for fc in range(4):
                        P.op("pe", lambda e, fc=fc, s=s, tsl=tsl, u2=u2: e.matmul(pp[u2][:], lhsT=wbf[s][:, 8 + fc, :], rhs=sbrT[:, fc, tsl],
                                                                        start=(fc == 0), stop=(fc == 3)),
                             reads=[b_wbf[s], b_sbrT], writes=[b_pp[u2]])
                    P.op("dve", lambda e, gi=gi, u2=u2, tsl=tsl: e.tensor_tensor(out=t1[u2][:], in0=pa[u2][:], in1=gt[gi][:, 0, tsl], op=ALU.mult),
                         reads=[b_pa[u2], b_gt[gi]], writes=[b_t1[u2]])
                    P.op("dve", lambda e, gi=gi, u2=u2, tsl=tsl: e.tensor_tensor(out=t2[u2][:], in0=pp[u2][:], in1=gt[gi][:, 1, tsl], op=ALU.mult),
                         reads=[b_pp[u2], b_gt[gi]], writes=[b_t2[u2]])
                    P.op("pool", lambda e, fo=fo, tsl=tsl, u2=u2: e.tensor_tensor(out=mT[:, fo, tsl], in0=t1[u2][:], in1=t2[u2][:], op=ALU.add),
                         reads=[b_t1[u2], b_t2[u2]], writes=[b_mT[tb]])
            P.barrier()
        gateB = k.sb("mg_gateB", [128, D], F32, ls)
        fgB = k.sb("mg_fgB", [128, D], F32, ls)
        b_gateB, b_fgB = Buf("gateB"), Buf("fgB")
        c2 = P.new_dsem("mg_c2")
        P.dma("sp", gateB[:], S["modrow"][0:1, 2 * D:3 * D].broadcast_to([128, D]), writes=[b_gateB], sem=c2)
        P.dma("sp", fgB[:], I["final_g"][0:1, :].broadcast_to([128, D]), writes=[b_fgB], sem=c2)
        xb = [k.sb(f"mg_x{i}", [128, D], F32, ls) for i in range(NXB)]
        b_xb = [Buf(f"mgx{i}") for i in range(NXB)]
        xn = [k.sb(f"mg_xn{i}", [128, D], F32, ls) for i in range(NXB)]
        b_xn = [Buf(f"mgxn{i}") for i in range(NXB)]
        xsem = [P.new_dsem(f"mg_xs{i}") for i in range(NXB)]
        osem = [P.new_dsem(f"mg_os{i}") for i in range(NXB)]
        st2 = [k.sb(f"mg_st{i}", [128, 4], F32, ls) for i in range(NXB)]
        b_st2 = [Buf(f"mgst{i}") for i in range(NXB)]
        while wo_state["kc"] < NKC:
            load_wout_chunk()
        po = [k.ps(f"mg_po{i}", [128, 512], F32, ls) for i in range(3)]
        b_po = [Buf(f"mgpo{i}") for i in range(3)]
        pi = 0

        def load_x(t):
            if t < 16:
                s_ = t % NXB
                P.dma("act", xb[s_][:], I["x"][t * 128:(t + 1) * 128, :], writes=[b_xb[s_]], sem=xsem[s_])

        def fin1(t):
            s = t % NXB
            P.op("pool", lambda e: e.tensor_tensor(out=xn[s][:], in0=xn[s][:], in1=xb[s][:], op=ALU.add),
                 reads=[b_xn[s], b_xb[s]], writes=[b_xn[s]])
            P.op("pool", lambda e: e.tensor_tensor(out=xb[s][:], in0=xn[s][:], in1=xn[s][:], op=ALU.mult),
                 reads=[b_xn[s]], writes=[b_xb[s]])

        def fin2(t):
            s = t % NXB
            P.op("dve", lambda e: e.tensor_reduce(out=st2[s][:, 0:1], in_=xb[s][:], axis=mybir.AxisListType.X, op=ALU.add),
                 reads=[b_xb[s]], writes=[b_st2[s]])
            P.op("dve", lambda e: e.tensor_scalar(out=st2[s][:, 1:2], in0=st2[s][:, 0:1], scalar1=1.0 / D, scalar2=EPS, op0=ALU.mult, op1=ALU.add),
                 reads=[b_st2[s]], writes=[b_st2[s]])
            P.op("act", lambda e: e.activation(out=st2[s][:, 2:3], in_=st2[s][:, 1:2], func=AF.Ln), reads=[b_st2[s]], writes=[b_st2[s]])
            P.op("act", lambda e: e.activation(out=st2[s][:, 3:4], in_=st2[s][:, 2:3], func=AF.Exp, scale=-0.5), reads=[b_st2[s]], writes=[b_st2[s]])
            P.op("dve", lambda e: e.scalar_tensor_tensor(out=xn[s][:], in0=xn[s][:], scalar=st2[s][:, 3:4], in1=fgB[:], op0=ALU.mult, op1=ALU.mult),
                 reads=[b_xn[s], b_st2[s], b_fgB], writes=[b_xn[s]])
            P.dma("sp", k.out[t * 128:(t + 1) * 128, :], xn[s][:], reads=[b_xn[s]], sem=osem[s])
            load_x(t + NXB)

        for t0 in range(NXB):
            load_x(t0)
        for t in range(16):
            s = t % NXB
            tb = t // 4
            for cbk in range(4):
                p_ = pi % 3
                pi += 1
                for kc in range(NKC):
                    P.op("pe", lambda e, kc=kc, cbk=cbk, p_=p_, t=t: e.matmul(po[p_][:], lhsT=mT[:, kc, t * 128:(t + 1) * 128],
                                                                          rhs=wout[:, kc, cbk * 512:(cbk + 1) * 512],
                                                                          start=(kc == 0), stop=(kc == NKC - 1)),
                         reads=[b_mT[tb], b_woutc[kc]], writes=[b_po[p_]])
                P.op("dve", lambda e, cbk=cbk, p_=p_, s=s: e.tensor_tensor(out=xn[s][:, cbk * 512:(cbk + 1) * 512], in0=po[p_][:],
                                                                       in1=gateB[:, cbk * 512:(cbk + 1) * 512], op=ALU.mult),
                     reads=[b_po[p_], b_gateB], writes=[b_xn[s]])
            if t >= 2:
                fin2(t - 2)
            if t >= 1:
                fin1(t - 1)
        fin2(14)
        fin1(15)
        fin2(15)


_CACHE = {}


def _prep_inputs(inputs, b):
    f = lambda a: np.ascontiguousarray(np.asarray(a, dtype=np.float32))
    m = {}
    m["x"] = f(inputs["x"][b])
    m["ctx"] = f(inputs["ctx"][b])
    m["cc"] = f(np.stack([np.asarray(inputs["c"])[b], np.asarray(inputs["c_ctx"])], axis=0))
    m["w_ada"] = f(inputs["w_ada"][0])
    m["b_ada"] = f(inputs["b_ada"][0]).reshape(1, -1)
    m["norm_g"] = f(inputs["norm_g"][0])
    m["w_in"] = f(inputs["w_in"][0])
    m["lam"] = f(np.stack([np.asarray(inputs["lambda_q1"])[0], np.asarray(inputs["lambda_k1"])[0],
                           np.asarray(inputs["lambda_q2"])[0], np.asarray(inputs["lambda_k2"])[0]], axis=0))
    m["subln_g"] = f(inputs["subln_g"][0]).reshape(1, 128)
    m["ssm_lre"] = f(inputs["ssm_lambda_re"][0])
    m["ssm_lim"] = f(inputs["ssm_lambda_im"][0])
    m["ssm_ls"] = f(inputs["ssm_log_step"][0])
    m["ssm_bre"] = f(inputs["ssm_b_re"][0])
    m["ssm_bim"] = f(inputs["ssm_b_im"][0])
    m["ssm_cre"] = f(inputs["ssm_c_re"][0])
    m["ssm_cim"] = f(inputs["ssm_c_im"][0])
    m["ssm_d"] = f(inputs["ssm_d"][0]).reshape(1, 512)
    m["w_glu"] = f(inputs["w_glu"][0])
    m["b_glu"] = f(inputs["b_glu"][0])
    m["w_pa"] = f(inputs["w_pa"][0])
    m["w_ps"] = f(inputs["w_ps"][0])
    m["w_out"] = f(inputs["w_out"][0])
    m["final_g"] = f(inputs["final_g"]).reshape(1, D)
    m.update(_consts())
    return m


def kernel(**inputs):
    if "nc" not in _CACHE:
        _CACHE["nc"] = build()[0]
    nc = _CACHE["nc"]
    shared = None
    in_maps = []
    for b in range(8):
        m = _prep_inputs(inputs, b)
        if shared is None:
            shared = m
        else:
            for key in m:
                if key not in ("x", "ctx", "cc"):
                    m[key] = shared[key]
        in_maps.append(m)
    res = run_bass_kernel_spmd(nc, in_maps, core_ids=list(range(8)))
    return np.stack([np.asarray(r["out"], dtype=np.float32) for r in res.results], axis=0)
```

```python
import math
import numpy as np
import ml_dtypes
from contextlib import ExitStack
import concourse.bass as bass
import concourse.mybir as mybir
from concourse.bass_utils import run_bass_kernel_spmd

F32 = mybir.dt.float32
BF16 = mybir.dt.bfloat16
I32 = mybir.dt.int32
AF = mybir.ActivationFunctionType
ALU = mybir.AluOpType

D = 2048
L = 2048
LC = 256
LT = L + LC
NKC = D // 128
INW = 9216
HEADS = 8
EPS = 1e-6
LAM_INIT = 0.8 - 0.6 * math.exp(-0.3 * 0)
TWO_PI = 2.0 * math.pi


class Buf:
    __slots__ = ("name", "w", "r")

    def __init__(self, name):
        self.name = name
        self.w = None
        self.r = {}


class Prog:
    ENG = ["pe", "act", "dve", "pool", "sp"]

    def __init__(self, nc, st):
        self.nc = nc
        self.st = st
        self.q = {e: [] for e in self.ENG}
        self.seen = {e: {} for e in self.ENG}
        self.psem = {e: st.enter_context(nc.semaphore("p_" + e)) for e in ["pe", "act", "dve", "pool"]}
        self.dsems = []
        self.bufsem = {}
        self.bufsem_keep = []
        self.free_dsems = []

    def new_dsem(self, name):
        return None

    def _auto_dsem(self, reads, writes):
        b = writes[0] if len(writes) else reads[0]
        key = id(b)
        d = self.bufsem.get(key)
        if d is None:
            if self.free_dsems:
                d = self.free_dsems.pop()
            else:
                h = self.st.enter_context(self.nc.semaphore(f"d{len(self.dsems)}"))
                d = {"h": h, "n": 0, "name": f"d{len(self.dsems)}"}
                self.dsems.append(d)
            self.bufsem[key] = d
            self.bufsem_keep.append(b)
        return d

    def _deps(self, eng, reads, writes):
        need = {}

        def add(t):
            if t[0] == "c":
                if t[1] == "pe" and eng == "pe":
                    return
                key = ("c", t[1])
                if need.get(key, (None, -1))[1] < t[2]:
                    need[key] = (t[1], t[2])
            else:
                key = ("d", id(t[1]))
                if need.get(key, (None, -1))[1] < t[2]:
                    need[key] = (t[1], t[2])

        for b in reads:
            if b.w is not None:
                add(b.w)
        for b in writes:
            if b.w is not None:
                add(b.w)
            for t in b.r.values():
                add(t)
        waits = []
        for key, (obj, v) in need.items():
            if self.seen[eng].get(key, -1) >= v:
                continue
            self.seen[eng][key] = v
            waits.append((key[0], obj, v))
        return waits

    def _record(self, tok, reads, writes):
        for b in reads:
            key = (tok[0], tok[1] if tok[0] == "c" else id(tok[1]))
            b.r[key] = tok
        for b in writes:
            b.w = tok
            b.r = {}

    def op(self, eng, fn, reads=(), writes=()):
        waits = self._deps(eng, reads, writes)
        idx = len(self.q[eng])
        self.q[eng].append({"fn": fn, "waits": waits, "awaited": False, "dma": None})
        tok = ("c", eng, idx)
        self._record(tok, reads, writes)
        return tok

    def dma(self, eng, out, in_, reads=(), writes=(), sem=None, **kw):
        reads, writes = list(reads), list(writes)
        sem = self._auto_dsem(reads, writes)
        waits = self._deps(eng, reads, writes)
        sem["n"] += 16
        tok = ("d", sem, sem["n"])
        self.q[eng].append({"fn": (lambda e, o=out, i=in_, k=kw: e.dma_start(out=o, in_=i, **k)),
                            "waits": waits, "awaited": False, "dma": sem})
        self._record(tok, reads, writes)
        return tok

    def barrier(self):
        for e in self.ENG:
            waits = []
            for e2 in ["pe", "act", "dve", "pool"]:
                n = len(self.q[e2])
                if e2 == e:
                    n -= 0
                idx = None
                for i in range(len(self.q[e2]) - 1, -1, -1):
                    if self.q[e2][i]["fn"] is not None and self.q[e2][i]["dma"] is None:
                        idx = i
                        break
                if idx is None:
                    continue
                key = ("c", e2)
                if self.seen[e].get(key, -1) >= idx:
                    continue
                self.seen[e][key] = idx
                waits.append(("c", e2, idx))
            for d in self.dsems:
                if d["n"] == 0:
                    continue
                key = ("d", id(d))
                if self.seen[e].get(key, -1) >= d["n"]:
                    continue
                self.seen[e][key] = d["n"]
                waits.append(("d", d, d["n"]))
            if waits:
                self.q[e].append({"fn": None, "waits": waits, "awaited": False, "dma": None})
        for d in self.bufsem.values():
            self.free_dsems.append(d)
        self.bufsem = {}
        self.bufsem_keep = []

    def emit(self):
        for e in self.ENG:
            for ent in self.q[e]:
                for w in ent["waits"]:
                    if w[0] == "c":
                        self.q[w[1]][w[2]]["awaited"] = True
        cnt = {}
        for e in ["pe", "act", "dve", "pool"]:
            c = 0
            arr = []
            for ent in self.q[e]:
                if ent["awaited"]:
                    c += 1
                arr.append(c)
            cnt[e] = arr
        psem = self.psem
        q = self.q

        def run(name, e):
            for ent in q[name]:
                for w in ent["waits"]:
                    if w[0] == "c":
                        e.wait_ge(psem[w[1]], cnt[w[1]][w[2]])
                    else:
                        e.wait_ge(w[1]["h"], w[2])
                if ent["fn"] is None:
                    continue
                inst = ent["fn"](e)
                if ent["dma"] is not None:
                    inst.then_inc(ent["dma"]["h"], 16)
                elif ent["awaited"]:
                    inst.then_inc(psem[name], 1)

        with self.nc.Block() as block:
            @block.sync
            def _(e):
                run("sp", e)

            @block.scalar
            def _(e):
                run("act", e)

            @block.vector
            def _(e):
                run("dve", e)

            @block.gpsimd
            def _(e):
                run("pool", e)

            @block.tensor
            def _(e):
                run("pe", e)


def _consts():
    ident = np.eye(128, dtype=np.float32)
    m = np.arange(128)
    partner = np.where((m % 32) < 16, m + 16, m - 16)
    perm = np.zeros((128, 128), np.float32)
    perm[partner, m] = 1.0
    sgn = np.where((m % 32) < 16, -1.0, 1.0).astype(np.float32)
    tok = np.arange(L)
    pos = np.where(((m % 64) < 32)[:, None], (tok // 64)[None, :], (tok % 64)[None, :]).astype(np.float32)
    fexp = ((m % 16) / 16.0).astype(np.float32)
    colc = np.zeros((128, 4), np.float32)
    colc[:, 0] = sgn
    colc[:, 1] = fexp
    colc[:, 2] = np.where(m < 64, 1.0, -1.0)
    sel = np.zeros((2, 128), np.float32)
    sel[0, :] = 1.0
    tauA = np.zeros((128, 32, 9), np.float32)
    tauA[:, 0:16, :] = np.arange(9)[None, None, :]
    tauA[:, 16:32, :] = (8 - np.arange(9))[None, None, :]
    tauB = np.zeros((128, 32, 8), np.float32)
    tauB[:, 0:16, :] = (7 - np.arange(8))[None, None, :]
    tauB[:, 16:32, :] = np.arange(8)[None, None, :]
    tauC = np.zeros((128, 32, 8), np.float32)
    tauC[:, :, :] = (8.0 * (np.arange(8) + 1))[None, None, :]
    return {"c_ident": ident, "c_perm": perm, "c_pos": pos, "c_col": colc, "c_sel": sel, "c_tauA": tauA, "c_tauB": tauB,
            "c_tauC": tauC}


class K:
    pass


def build(debug=None):
    nc = bass.Bass("TRN2", target_bir_lowering=False)
    st = ExitStack()
    P = Prog(nc, st)
    k = K()
    k.nc, k.P, k.st = nc, P, st
    k.debug = debug or {}

    def dram_in(name, shape, dt=F32):
        return nc.dram_tensor(name, list(shape), dt, kind="ExternalInput").ap()

    dbg_outs = []

    def dram_scr(name, shape, dt):
        kind = "Internal"
        if debug is not None and name in debug.get("_inject", ()):
            kind = "ExternalInput"
        elif debug is not None and name in debug:
            kind = "ExternalOutput"
            dbg_outs.append(name)
        return nc.dram_tensor(name, list(shape), dt, kind=kind).ap()

    I = {}
    I["x"] = dram_in("x", [L, D])
    I["ctx"] = dram_in("ctx", [LC, D])
    I["cc"] = dram_in("cc", [2, D])
    I["w_ada"] = dram_in("w_ada", [D, 3 * D])
    I["b_ada"] = dram_in("b_ada", [1, 3 * D])
    I["norm_g"] = dram_in("norm_g", [D])
    I["w_in"] = dram_in("w_in", [D, INW])
    I["lam"] = dram_in("lam", [4, 64])
    I["subln_g"] = dram_in("subln_g", [1, 128])
    I["ssm_lre"] = dram_in("ssm_lre", [2, 32, 64])
    I["ssm_lim"] = dram_in("ssm_lim", [2, 32, 64])
    I["ssm_ls"] = dram_in("ssm_ls", [2, 32])
    I["ssm_bre"] = dram_in("ssm_bre", [2, 32, 64, 16])
    I["ssm_bim"] = dram_in("ssm_bim", [2, 32, 64, 16])
    I["ssm_cre"] = dram_in("ssm_cre", [2, 32, 16, 64])
    I["ssm_cim"] = dram_in("ssm_cim", [2, 32, 16, 64])
    I["ssm_d"] = dram_in("ssm_d", [1, 512])
    I["w_glu"] = dram_in("w_glu", [512, 512])
    I["b_glu"] = dram_in("b_glu", [512])
    I["w_pa"] = dram_in("w_pa", [1024, D])
    I["w_ps"] = dram_in("w_ps", [512, D])
    I["w_out"] = dram_in("w_out", [D, D])
    I["final_g"] = dram_in("final_g", [1, D])
    for cn, arr in _consts().items():
        I[cn] = dram_in(cn, arr.shape)
    out = nc.dram_tensor("out", [L, D], F32, kind="ExternalOutput").ap()

    S = {}
    S["modrow"] = dram_scr("modrow", [2, 3 * D], F32)
    S["qT"] = dram_scr("qT", [HEADS, 128, L], BF16)
    S["kT"] = dram_scr("kT", [HEADS, 128, LT], BF16)
    S["v"] = dram_scr("v", [LT, 1024], BF16)
    S["sga"] = dram_scr("sga", [L, 1024], BF16)
    S["u"] = dram_scr("u", [LT, 512], F32)
    S["sgsT"] = dram_scr("sgsT", [512, L], BF16)
    S["sgmT"] = dram_scr("sgmT", [2 * D, L], BF16)
    S["abrT"] = dram_scr("abrT", [1024, L], BF16)
    S["sbrT"] = dram_scr("sbrT", [512, L], BF16)
    S["hT"] = dram_scr("hT_dbg", [128, NKC, LT], BF16) if (debug is not None and "hT_dbg" in debug) else None
    k.I, k.S, k.out = I, S, out
    k.dbg_sem = None

    def dbg(name, shape, dt, ap_fn, bufs):
        if debug is None or name not in debug:
            return
        if name not in S:
            S[name] = nc.dram_tensor(name, list(shape), dt, kind="ExternalOutput").ap()
            dbg_outs.append(name)
        if k.dbg_sem is None:
            k.dbg_sem = P.new_dsem("dbgsem")
        o, i = ap_fn(S[name])
        P.dma("sp", o, i, reads=bufs, sem=k.dbg_sem)
    k.dbg = dbg

    def sb(name, shape, dt, stack=st):
        return stack.enter_context(nc.sbuf_tensor(name, list(shape), dt))

    def ps(name, shape, dt, stack=st):
        return stack.enter_context(nc.psum_tensor(name, list(shape), dt))

    k.sb, k.ps = sb, ps
    ident = sb("ident", [128, 128], F32)
    colc = sb("colc", [128, 4], F32)
    b_ident, b_colc = Buf("ident"), Buf("colc")
    csem = P.new_dsem("csem")
    P.dma("sp", ident[:], I["c_ident"], writes=[b_ident], sem=csem)
    P.dma("sp", colc[:], I["c_col"], writes=[b_colc], sem=csem)
    k.ident, k.b_ident, k.colc, k.b_colc, k.csem = ident, b_ident, colc, b_colc, csem
    k.ssq = sb("ssq", [128, 40], F32)
    k.b_ssq = Buf("ssq")

    phase_adaln(k)
    P.barrier()
    if debug is None or debug.get("_upto", 99) >= 1:
        phase_norm_inproj(k, debug)
        P.barrier()
    if (debug is None or debug.get("_upto", 99) >= 2) and not (debug or {}).get("_skip_ssm"):
        phase_ssm(k)
        P.barrier()
    if debug is None or debug.get("_upto", 99) >= 3:
        phase_attn(k)
        P.barrier()
    if debug is None or debug.get("_upto", 99) >= 4:
        phase_merge(k)
        P.barrier()
    P.emit()
    st.close()
    return nc, dbg_outs


def range_sin(k, stack, out_ap, y_ap, shape, tag, rbufs, wbufs, eng="dve"):
    nc, P = k.nc, k.P
    ki = k.sb(tag + "_ki", shape, I32, stack)
    kf = k.sb(tag + "_kf", shape, F32, stack)
    g = k.sb(tag + "_g", shape, F32, stack)
    bki, bkf, bg = Buf(tag + "ki"), Buf(tag + "kf"), Buf(tag + "g")
    sl = tuple([slice(None)] * len(shape))
    P.op(eng, lambda e: e.tensor_copy(out=ki[sl], in_=y_ap), reads=rbufs, writes=[bki])
    P.op(eng, lambda e: e.tensor_copy(out=kf[sl], in_=ki[sl]), reads=[bki], writes=[bkf])
    P.op(eng, lambda e: e.tensor_tensor(out=kf[sl], in0=y_ap, in1=kf[sl], op=ALU.subtract), reads=rbufs + [bkf], writes=[bkf])
    P.op(eng, lambda e: e.tensor_single_scalar(out=g[sl], in_=kf[sl], scalar=0.5, op=ALU.is_gt), reads=[bkf], writes=[bg])
    P.op(eng, lambda e: e.tensor_tensor(out=kf[sl], in0=kf[sl], in1=g[sl], op=ALU.subtract), reads=[bkf, bg], writes=[bkf])
    P.op(eng, lambda e: e.tensor_single_scalar(out=g[sl], in_=kf[sl], scalar=-0.5, op=ALU.is_lt), reads=[bkf], writes=[bg])
    P.op(eng, lambda e: e.tensor_tensor(out=kf[sl], in0=kf[sl], in1=g[sl], op=ALU.add), reads=[bkf, bg], writes=[bkf])
    P.op("act", lambda e: e.activation(out=out_ap, in_=kf[sl], func=AF.Sin, scale=TWO_PI * (1.0 - 2e-7)), reads=[bkf], writes=wbufs)


def phase_adaln(k):
    nc, P, I, S = k.nc, k.P, k.I, k.S
    with ExitStack() as ls:
        sT = k.sb("ad_sT", [128, NKC, 2], F32, ls)
        b_sT = Buf("sT")
        sem_c = P.new_dsem("ad_c")
        for v in range(2):
            P.dma("sp", sT[:, :, v], I["cc"][v].rearrange("(kc p) -> p kc", p=128), writes=[b_sT], sem=sem_c,
                  allow_slow_non_contiguous=True)
        P.op("act", lambda e: e.activation(out=sT[:], in_=sT[:], func=AF.Silu), reads=[b_sT], writes=[b_sT])
        brow = k.sb("ad_brow", [2, 3 * D], F32, ls)
        b_brow = Buf("brow")
        for v in range(2):
            P.dma("sp", brow[v:v + 1, :], I["b_ada"], writes=[b_brow], sem=sem_c)
        modrow = k.sb("ad_modrow", [2, 3 * D], F32, ls)
        b_modrow = Buf("modrow")
        NS = 2
        wst = [k.sb(f"ad_w{i}", [128, NKC, 512], F32, ls) for i in range(NS)]
        b_w = [Buf(f"adw{i}") for i in range(NS)]
        wsem = [P.new_dsem(f"ad_ws{i}") for i in range(NS)]
        pst = [k.ps(f"ad_ps{i}", [128, 512], F32, ls) for i in range(2)]
        b_ps = [Buf(f"adps{i}") for i in range(2)]
        wv = I["w_ada"].rearrange("(kc p) c -> p kc c", p=128)
        xs1 = [k.sb(f"ad_x{i}", [128, D], F32, ls) for i in range(2)]
        b_xs1 = [Buf(f"adx{i}") for i in range(2)]
        junk1 = k.sb("ad_junk", [128, D], BF16, ls)
        b_junk1 = Buf("adjunk")
        tiles_done = 0

        def ss_tile(t):
            s1 = t % 2
            src = I["x"][t * 128:(t + 1) * 128, :] if t < 16 else I["ctx"][(t - 16) * 128:(t - 15) * 128, :]
            P.dma("pool", xs1[s1][:], src, writes=[b_xs1[s1]])
            P.op("act", lambda e, s1=s1, t=t: e.activation(out=junk1[:], in_=xs1[s1][:], func=AF.Square, accum_out=k.ssq[:, t:t + 1]),
                 reads=[b_xs1[s1]], writes=[b_junk1, k.b_ssq])

        for cb in range(12):
            for _ in range(2 if cb < 6 else 1):
                if tiles_done < 18:
                    ss_tile(tiles_done)
                    tiles_done += 1
            s = cb % NS
            P.dma("sp", wst[s][:, 0:8, :], wv[:, 0:8, cb * 512:(cb + 1) * 512], writes=[b_w[s]], sem=wsem[s])
            P.dma("act", wst[s][:, 8:16, :], wv[:, 8:16, cb * 512:(cb + 1) * 512], writes=[b_w[s]], sem=wsem[s])
            pt, bp = pst[cb % 2], b_ps[cb % 2]
            for kc in range(NKC):
                P.op("pe", lambda e, kc=kc, s=s, pt=pt: e.matmul(pt[0:2, :], lhsT=sT[:, kc, :], rhs=wst[s][:, kc, :],
                                                              start=(kc == 0), stop=(kc == NKC - 1)),
                     reads=[b_sT, b_w[s]], writes=[bp])
            P.op("dve", lambda e, cb=cb, pt=pt: e.tensor_tensor(out=modrow[:, cb * 512:(cb + 1) * 512], in0=pt[0:2, :],
                                                             in1=brow[:, cb * 512:(cb + 1) * 512], op=ALU.add),
                 reads=[bp, b_brow], writes=[b_modrow])
        b_mr = Buf("modrow_d")
        k.b_modrow_d = b_mr
        P.dma("sp", S["modrow"], modrow[:], reads=[b_modrow], writes=[b_mr], sem=sem_c)


def phase_norm_inproj(k, debug):
    nc, P, I, S = k.nc, k.P, k.I, k.S
    with ExitStack() as ls:
        hT = k.sb("hT", [128, NKC, LT], BF16, ls)
        b_hT = [[Buf(f"hT{t}_{kc}") for kc in range(NKC)] for t in range(18)]
        Amod = k.sb("Amod", [128, NKC, 2], F32, ls)
        Smod = k.sb("Smod", [128, NKC, 2], F32, ls)
        gcol = k.sb("gcol", [128, NKC], F32, ls)
        b_A, b_S, b_g = Buf("Amod"), Buf("Smod"), Buf("gcol")
        msem = P.new_dsem("n_m")
        for v in range(2):
            P.dma("sp", Smod[:, :, v], S["modrow"][v, 0:D].rearrange("(kc p) -> p kc", p=128),
                  reads=[k.b_modrow_d], writes=[b_S], sem=msem, allow_slow_non_contiguous=True)
            P.dma("sp", Amod[:, :, v], S["modrow"][v, D:2 * D].rearrange("(kc p) -> p kc", p=128),
                  reads=[k.b_modrow_d], writes=[b_A], sem=msem, allow_slow_non_contiguous=True)
        P.dma("sp", gcol[:], I["norm_g"].rearrange("(kc p) -> p kc", p=128), writes=[b_g], sem=msem,
              allow_slow_non_contiguous=True)
        for v in range(2):
            P.op("dve", lambda e, v=v: e.scalar_tensor_tensor(out=Amod[:, :, v], in0=Amod[:, :, v], scalar=1.0, in1=gcol[:],
                                                             op0=ALU.add, op1=ALU.mult),
                 reads=[b_A, b_g], writes=[b_A])
        with ExitStack() as l1:
            NX = 2
            xt = [k.sb(f"n_x{i}", [128, D], F32, l1) for i in range(NX)]
            b_x = [Buf(f"nx{i}") for i in range(NX)]
            xsem = [P.new_dsem(f"n_xs{i}") for i in range(NX)]
            junk = k.sb("n_junk", [128, D], BF16, l1)
            b_junk = Buf("junk")
            stat = [k.sb(f"n_st{i}", [128, 4], F32, l1) for i in range(NX)]
            b_stat = [Buf(f"nst{i}") for i in range(NX)]
            pt = [k.ps(f"n_ps{i}", [128, 512], F32, l1) for i in range(4)]
            b_pt = [Buf(f"nps{i}") for i in range(4)]
            pi = 0
            P.op("dve", lambda e: e.tensor_scalar(out=k.ssq[:, 0:18], in0=k.ssq[:, 0:18], scalar1=1.0 / D, scalar2=EPS, op0=ALU.mult, op1=ALU.add),
                 reads=[k.b_ssq], writes=[k.b_ssq])
            P.op("act", lambda e: e.activation(out=k.ssq[:, 0:18], in_=k.ssq[:, 0:18], func=AF.Ln), reads=[k.b_ssq], writes=[k.b_ssq])
            P.op("act", lambda e: e.activation(out=k.ssq[:, 20:38], in_=k.ssq[:, 0:18], func=AF.Exp, scale=-0.5), reads=[k.b_ssq], writes=[k.b_ssq])
            for t in range(18):
                s = t % NX
                v = 0 if t < 16 else 1
                src = I["x"][t * 128:(t + 1) * 128, :] if t < 16 else I["ctx"][(t - 16) * 128:(t - 15) * 128, :]
                P.dma("sp", xt[s][:, 0:1024], src[:, 0:1024], writes=[b_x[s]], sem=xsem[s])
                P.dma("sp", xt[s][:, 1024:2048], src[:, 1024:2048], writes=[b_x[s]], sem=xsem[s])
                P.op("dve", lambda e, s=s, t=t: e.tensor_scalar(out=xt[s][:], in0=xt[s][:], scalar1=k.ssq[:, 20 + t:21 + t], scalar2=None,
                                                              op0=ALU.mult),
                     reads=[b_x[s], k.b_ssq], writes=[b_x[s]])
                for g4 in range(4):
                    p_, bp = pt[pi % 4], b_pt[pi % 4]
                    pi += 1
                    for j in range(4):
                        kc = g4 * 4 + j
                        P.op("pe", lambda e, s=s, kc=kc, j=j, p_=p_: e.transpose(out=p_[:, j * 128:(j + 1) * 128],
                                                                             in_=xt[s][:, kc * 128:(kc + 1) * 128],
                                                                             identity=k.ident[:]),
                             reads=[b_x[s], k.b_ident], writes=[bp])
                    for j in range(4):
                        kc = g4 * 4 + j
                        eng = "dve" if (g4 % 2 == 0) else "act"
                        if eng == "dve":
                            P.op("dve", lambda e, kc=kc, j=j, p_=p_, t=t, v=v: e.tensor_scalar(
                                out=hT[:, kc, t * 128:(t + 1) * 128], in0=p_[:, j * 128:(j + 1) * 128],
                                scalar1=Amod[:, kc, v:v + 1], scalar2=Smod[:, kc, v:v + 1], op0=ALU.mult, op1=ALU.add),
                                reads=[bp, b_A, b_S], writes=[b_hT[t][kc]])
                        else:
                            P.op("act", lambda e, kc=kc, j=j, p_=p_, t=t, v=v: e.activation(
                                out=hT[:, kc, t * 128:(t + 1) * 128], in_=p_[:, j * 128:(j + 1) * 128],
                                func=AF.Identity, scale=Amod[:, kc, v:v + 1], bias=Smod[:, kc, v:v + 1]),
                                reads=[bp, b_A, b_S], writes=[b_hT[t][kc]])
        if S["hT"] is not None:
            dsem = P.new_dsem("dbg")
            P.dma("sp", S["hT"], hT[:], reads=[b for row in b_hT for b in row], writes=[Buf("x")], sem=dsem)
        P.barrier()
        if debug is not None and debug.get("_upto", 99) < 1.5:
            return
        inproj(k, ls, hT, b_hT)


def inproj(k, ls, hT, b_hT):
    nc, P, I, S = k.nc, k.P, k.I, k.S
    cosT = k.sb("cosT", [128, L], F32, ls)
    sinS = k.sb("sinS", [128, L], F32, ls)
    perm = k.sb("perm", [128, 128], F32, ls)
    b_cos, b_sin, b_perm = Buf("cos"), Buf("sin"), Buf("perm")
    tsem = P.new_dsem("ip_t")
    P.dma("sp", perm[:], I["c_perm"], writes=[b_perm], sem=tsem)
    with ExitStack() as l0:
        pos = k.sb("pos", [128, L], F32, l0)
        yv = k.sb("yv", [128, L], F32, l0)
        inv = k.sb("inv", [128, 1], F32, l0)
        b_pos, b_y, b_inv = Buf("pos"), Buf("yv"), Buf("inv")
        P.dma("sp", pos[:], I["c_pos"], writes=[b_pos], sem=tsem)
        P.op("act", lambda e: e.activation(out=inv[:], in_=k.colc[:, 1:2], func=AF.Exp, scale=-math.log(10000.0)),
             reads=[k.b_colc], writes=[b_inv])
        P.op("dve", lambda e: e.tensor_scalar(out=yv[:], in0=pos[:], scalar1=inv[:, 0:1], scalar2=1.0 / TWO_PI,
                                              op0=ALU.mult, op1=ALU.mult), reads=[b_pos, b_inv], writes=[b_y])
        range_sin(k, l0, sinS[:], yv[:], [128, L], "rs1", [b_y], [b_sin])
        P.op("dve", lambda e: e.tensor_scalar(out=sinS[:], in0=sinS[:], scalar1=k.colc[:, 0:1], scalar2=None, op0=ALU.mult),
             reads=[b_sin, k.b_colc], writes=[b_sin])
        P.op("dve", lambda e: e.tensor_scalar(out=yv[:], in0=yv[:], scalar1=0.25, scalar2=None, op0=ALU.add),
             reads=[b_y], writes=[b_y])
        range_sin(k, l0, cosT[:], yv[:], [128, L], "rs2", [b_y], [b_cos])
        P.barrier()
    b_wbq = [[Buf(f"wbq{i}_{j}") for j in range(4)] for i in range(2)]
    wb = [k.sb(f"ip_wb{i}", [128, NKC, 512], BF16, ls) for i in range(2)]
    b_wb = [Buf(f"wb{i}") for i in range(2)]
    NOB = 4
    ob = [k.sb(f"ip_ob{i}", [128, 512], BF16, ls) for i in range(NOB)]
    b_ob = [Buf(f"ob{i}") for i in range(NOB)]
    osem = [P.new_dsem(f"ip_os{i}") for i in range(NOB)]
    NOF = 3
    of = [k.sb(f"ip_of{i}", [128, 512], F32, ls) for i in range(NOF)]
    b_of = [Buf(f"of{i}") for i in range(NOF)]
    fsem = [P.new_dsem(f"ip_fs{i}") for i in range(NOF)]
    t1 = [k.sb(f"ip_t1{i}", [128, 512], F32, ls) for i in range(2)]
    b_t1 = [Buf(f"t1{i}") for i in range(2)]
    t2 = [k.sb(f"ip_t2{i}", [128, 512], F32, ls) for i in range(2)]
    b_t2 = [Buf(f"t2{i}") for i in range(2)]
    pb = [k.ps(f"ip_ps{i}", [128, 512], F32, ls) for i in range(4)]
    b_pb = [Buf(f"ipps{i}") for i in range(4)]
    pr = [k.ps(f"ip_pr{i}", [128, 512], F32, ls) for i in range(2)]
    b_pr = [Buf(f"ippr{i}") for i in range(2)]
    wv = I["w_in"].rearrange("(kc p) c -> p kc c", p=128)
    cnt = {"pb": 0, "ob": 0, "of": 0, "r": 0, "ld": 0, "ev": 0}

    rope_pending = []

    def load_block(cb):
        s2 = cb % 2
        for q4 in range(4):
            P.dma("pool", wb[s2][:, q4 * 4:(q4 + 1) * 4, :], wv[:, q4 * 4:(q4 + 1) * 4, cb * 512:(cb + 1) * 512], writes=[b_wbq[s2][q4]])

    def next_ob():
        i = cnt["ob"] % NOB
        cnt["ob"] += 1
        return i

    def evac_eng():
        cnt["ev"] += 1
        return "act" if cnt["ev"] % 2 else "dve"

    def tiles_of(tok0, n, kc):
        return [b_hT[t][kc] for t in range(tok0 // 128, (tok0 + n) // 128)]

    def fm_unit(cb, fc, tok0, n, kind, row0, dst):
        s2 = cb % 2
        pi = cnt["pb"] % 4
        cnt["pb"] += 1
        pt, bp = pb[pi], b_pb[pi]
        for kc in range(NKC):
            P.op("pe", lambda e, kc=kc: e.matmul(pt[:, 0:n], lhsT=wb[s2][:, kc, fc * 128:(fc + 1) * 128],
                                                 rhs=hT[:, kc, tok0:tok0 + n], start=(kc == 0), stop=(kc == NKC - 1)),
                 reads=[b_wbq[s2][kc // 4]] + tiles_of(tok0, n, kc), writes=[bp])
        while rope_pending:
            rope_pending.pop(0)()
        oi = next_ob()
        if kind == "rope":
            ri = cnt["r"] % 2
            cnt["r"] += 1
            fi = cnt["of"] % NOF
            cnt["of"] += 1
            P.op("act", lambda e: e.activation(out=of[fi][:, 0:n], in_=pt[:, 0:n], func=AF.Copy), reads=[bp], writes=[b_of[fi]])
            P.op("dve", lambda e: e.tensor_tensor(out=t1[ri][:, 0:n], in0=of[fi][:, 0:n], in1=cosT[:, tok0:tok0 + n], op=ALU.mult),
                 reads=[b_of[fi], b_cos], writes=[b_t1[ri]])

            def fin():
                P.op("pe", lambda e: e.matmul(pr[ri][:, 0:n], lhsT=perm[:], rhs=of[fi][:, 0:n], start=True, stop=True),
                     reads=[b_perm, b_of[fi]], writes=[b_pr[ri]])
                P.op("dve", lambda e: e.tensor_tensor(out=t2[ri][:, 0:n], in0=pr[ri][:, 0:n], in1=sinS[:, tok0:tok0 + n], op=ALU.mult),
                     reads=[b_pr[ri], b_sin], writes=[b_t2[ri]])
                P.op("pool", lambda e: e.tensor_tensor(out=ob[oi][:, 0:n], in0=t1[ri][:, 0:n], in1=t2[ri][:, 0:n], op=ALU.add),
                     reads=[b_t1[ri], b_t2[ri]], writes=[b_ob[oi]])
                P.dma("sp", dst, ob[oi][:, 0:n], reads=[b_ob[oi]], sem=osem[oi])
            rope_pending.append(fin)
            return
        elif kind == "copy":
            eg = evac_eng()
            if eg == "act":
                P.op("act", lambda e: e.activation(out=ob[oi][:, 0:n], in_=pt[:, 0:n], func=AF.Copy), reads=[bp], writes=[b_ob[oi]])
            else:
                P.op("dve", lambda e: e.tensor_copy(out=ob[oi][:, 0:n], in_=pt[:, 0:n]), reads=[bp], writes=[b_ob[oi]])
        else:
            fn = AF.Silu if kind == "silu" else AF.Sigmoid
            P.op("act", lambda e: e.activation(out=ob[oi][:, 0:n], in_=pt[:, 0:n], func=fn), reads=[bp], writes=[b_ob[oi]])
        P.dma("sp", dst, ob[oi][:, 0:n], reads=[b_ob[oi]], sem=osem[oi])

    def tm_unit(cb, t, kind, dst):
        s2 = cb % 2
        pi = cnt["pb"] % 4
        cnt["pb"] += 1
        pt, bp = pb[pi], b_pb[pi]
        for kc in range(NKC):
            P.op("pe", lambda e, kc=kc: e.matmul(pt[:], lhsT=hT[:, kc, t * 128:(t + 1) * 128], rhs=wb[s2][:, kc, :],
                                                 start=(kc == 0), stop=(kc == NKC - 1)),
                 reads=[b_wbq[s2][kc // 4], b_hT[t][kc]], writes=[bp])
        while rope_pending:
            rope_pending.pop(0)()
        if kind == "f32":
            fi = cnt["of"] % NOF
            cnt["of"] += 1
            P.op("dve", lambda e: e.tensor_copy(out=of[fi][:], in_=pt[:]), reads=[bp], writes=[b_of[fi]])
            P.dma("sp", dst, of[fi][:], reads=[b_of[fi]], sem=fsem[fi])
            return
        oi = next_ob()
        if kind == "copy":
            eg = evac_eng()
            if eg == "act":
                P.op("act", lambda e: e.activation(out=ob[oi][:], in_=pt[:], func=AF.Copy), reads=[bp], writes=[b_ob[oi]])
            else:
                P.op("dve", lambda e: e.tensor_copy(out=ob[oi][:], in_=pt[:]), reads=[bp], writes=[b_ob[oi]])
        else:
            P.op("act", lambda e: e.activation(out=ob[oi][:], in_=pt[:], func=AF.Silu), reads=[bp], writes=[b_ob[oi]])
        P.dma("sp", dst, ob[oi][:], reads=[b_ob[oi]], sem=osem[oi])

    NCB = INW // 512
    load_block(0)
    for cb in range(NCB):
        if cb + 1 < NCB:
            load_block(cb + 1)
        c0 = cb * 512
        if cb < 2:
            for fc in range(4):
                h = cb * 4 + fc
                for tb in range(4):
                    fm_unit(cb, fc, tb * 512, 512, "rope", 0, S["qT"][h, :, tb * 512:(tb + 1) * 512])
        elif cb < 4:
            for fc in range(4):
                h = (cb - 2) * 4 + fc
                for tb in range(4):
                    fm_unit(cb, fc, tb * 512, 512, "rope", 0, S["kT"][h, :, tb * 512:(tb + 1) * 512])
                fm_unit(cb, fc, L, LC, "copy", 0, S["kT"][h, :, L:LT])
        elif cb < 6:
            for t in range(18):
                tm_unit(cb, t, "copy", S["v"][t * 128:(t + 1) * 128, (cb - 4) * 512:(cb - 3) * 512])
        elif cb < 8:
            for t in range(16):
                tm_unit(cb, t, "silu", S["sga"][t * 128:(t + 1) * 128, (cb - 6) * 512:(cb - 5) * 512])
        elif cb == 8:
            for t in range(18):
                tm_unit(cb, t, "f32", S["u"][t * 128:(t + 1) * 128, :])
        elif cb == 9:
            for fc in range(4):
                for tb in range(4):
                    fm_unit(cb, fc, tb * 512, 512, "silu", 0, S["sgsT"][fc * 128:(fc + 1) * 128, tb * 512:(tb + 1) * 512])
        else:
            for fc in range(4):
                r0 = (cb - 10) * 512 + fc * 128
                for tb in range(4):
                    fm_unit(cb, fc, tb * 512, 512, "sigm", 0, S["sgmT"][r0:r0 + 128, tb * 512:(tb + 1) * 512])


def phase_ssm(k):
    nc, P, I, S = k.nc, k.P, k.I, k.S
    MUL, ADD, SUB = ALU.mult, ALU.add, ALU.subtract
    with ExitStack() as ls:
        ToepT = k.sb("ss_toep", [128, 32, 128], BF16, ls)
        RCp = k.sb("ss_rcp", [128, 2, 2, 16, 256], BF16, ls)
        WT = k.sb("ss_wt", [128, 2, 16, 2, 128], BF16, ls)
        A8c = k.sb("ss_a8c", [128, 2, 16, 2], F32, ls)
        A8s = k.sb("ss_a8s", [128, 2, 16, 2], F32, ls)
        AKc = k.sb("ss_akc", [128, 2, 8, 16, 2], F32, ls)
        AKs = k.sb("ss_aks", [128, 2, 8, 16, 2], F32, ls)
        b_ak = Buf("ak")
        b_toep = [Buf(f"toep{g}") for g in range(32)]
        b_rcp = Buf("rcp")
        b_wtl = [[[Buf(f"wt{d}_{g2}_{ri}") for ri in range(2)] for g2 in range(16)] for d in range(2)]
        b_U = [Buf(f"U{g}") for g in range(32)]
        b_zbf = [Buf("zbf0"), Buf("zbf1")]
        b_ygT = Buf("ygT")
        b_a8 = Buf("a8")
        pbk = [k.ps(f"ss_ps{i}", [128, 512], F32, ls) for i in range(8)]
        b_pbk = [Buf(f"ssps{i}") for i in range(8)]
        pc = {"i": 0}

        def nb():
            i = pc["i"] % 8
            pc["i"] += 1
            return pbk[i], b_pbk[i]

        csem = P.new_dsem("ss_c")
        with ExitStack() as l0:
            lre = k.sb("ss_lre", [128, 32], F32, l0)
            lim = k.sb("ss_lim", [128, 32], F32, l0)
            dtt = k.sb("ss_dt", [128, 32], F32, l0)
            alog = k.sb("ss_alog", [128, 32], F32, l0)
            th = k.sb("ss_th", [128, 32], F32, l0)
            b_l, b_dt, b_al = Buf("lrelim"), Buf("dtt"), Buf("alogth")
            for gp in range(2):
                for d in range(2):
                    P.dma("sp", lre[gp * 64:(gp + 1) * 64, d * 16:(d + 1) * 16], I["ssm_lre"][d, gp * 16:(gp + 1) * 16, :].rearrange("g p -> p g"),
                          writes=[b_l], sem=csem, allow_slow_non_contiguous=True)
                    P.dma("sp", lim[gp * 64:(gp + 1) * 64, d * 16:(d + 1) * 16], I["ssm_lim"][d, gp * 16:(gp + 1) * 16, :].rearrange("g p -> p g"),
                          writes=[b_l], sem=csem, allow_slow_non_contiguous=True)
                    P.dma("sp", dtt[gp * 64:(gp + 1) * 64, d * 16:(d + 1) * 16], I["ssm_ls"][d:d + 1, gp * 16:(gp + 1) * 16].broadcast_to([64, 16]),
                          writes=[b_dt], sem=csem)
            P.op("act", lambda e: e.activation(out=dtt[:], in_=dtt[:], func=AF.Exp), reads=[b_dt], writes=[b_dt])
            P.op("dve", lambda e: e.tensor_tensor(out=alog[:], in0=lre[:], in1=dtt[:], op=MUL), reads=[b_l, b_dt], writes=[b_al])
            P.op("dve", lambda e: e.scalar_tensor_tensor(out=th[:], in0=lim[:], scalar=1.0 / TWO_PI, in1=dtt[:], op0=MUL, op1=MUL),
                 reads=[b_l, b_dt], writes=[b_al])
            tabs = {}
            for nm, n in (("A", 9), ("B", 8), ("C", 8)):
                tau = k.sb(f"ss_tau{nm}", [128, 32, n], F32, l0)
                ex = k.sb(f"ss_ex{nm}", [128, 32, n], F32, l0)
                yv = k.sb(f"ss_yv{nm}", [128, 32, n], F32, l0)
                sn = k.sb(f"ss_sn{nm}", [128, 32, n], F32, l0)
                cs = k.sb(f"ss_cs{nm}", [128, 32, n], F32, l0)
                b_tau, b_ex, b_yv, b_sn, b_cs = Buf("tau" + nm), Buf("ex" + nm), Buf("yv" + nm), Buf("sn" + nm), Buf("cs" + nm)
                P.dma("sp", tau[:], I["c_tau" + nm], writes=[b_tau], sem=csem)
                P.op("dve", lambda e, ex=ex, tau=tau, n=n: e.tensor_tensor(out=ex[:], in0=tau[:], in1=alog[:, :, None].broadcast_to([128, 32, n]), op=MUL),
                     reads=[b_tau, b_al], writes=[b_ex])
                P.op("act", lambda e, ex=ex: e.activation(out=ex[:], in_=ex[:], func=AF.Exp), reads=[b_ex], writes=[b_ex])
                P.op("dve", lambda e, yv=yv, tau=tau, n=n: e.tensor_tensor(out=yv[:], in0=tau[:], in1=th[:, :, None].broadcast_to([128, 32, n]), op=MUL),
                     reads=[b_tau, b_al], writes=[b_yv])
                fl = lambda t: t[:].rearrange("p a b -> p (a b)")
                range_sin(k, l0, fl(sn), fl(yv), [128, 32 * n], "ssr1" + nm, [b_yv], [b_sn])
                P.op("dve", lambda e, yv=yv: e.tensor_scalar(out=yv[:], in0=yv[:], scalar1=0.25, scalar2=None, op0=ADD), reads=[b_yv], writes=[b_yv])
                range_sin(k, l0, fl(cs), fl(yv), [128, 32 * n], "ssr2" + nm, [b_yv], [b_cs])
                P.op("dve", lambda e, cs=cs, ex=ex: e.tensor_tensor(out=cs[:], in0=cs[:], in1=ex[:], op=MUL), reads=[b_cs, b_ex], writes=[b_cs])
                P.op("dve", lambda e, sn=sn, ex=ex: e.tensor_tensor(out=sn[:], in0=sn[:], in1=ex[:], op=MUL), reads=[b_sn, b_ex], writes=[b_sn])
                tabs[nm] = (cs, sn, b_cs, b_sn)
            ARA, AIA, b_ARA, b_AIA = tabs["A"]
            ARB, AIB, b_ARB, b_AIB = tabs["B"]
            ARC, AIC, b_ARC, b_AIC = tabs["C"]
            for d in range(2):
                dsl = slice(d * 16, (d + 1) * 16)
                for ri in range(2):
                    P.op("dve", lambda e, d=d, ri=ri, dsl=dsl: e.tensor_copy(out=AKc[:, d, :, :, ri], in_=ARC[:, dsl, :].rearrange("p g k -> p k g")),
                         reads=[b_ARC], writes=[b_ak])
                P.op("dve", lambda e, d=d, dsl=dsl: e.tensor_scalar(out=AKs[:, d, :, :, 0], in0=AIC[:, dsl, :].rearrange("p g k -> p k g"),
                                                                   scalar1=-1.0, scalar2=None, op0=MUL), reads=[b_AIC], writes=[b_ak])
                P.op("dve", lambda e, d=d, dsl=dsl: e.tensor_copy(out=AKs[:, d, :, :, 1], in_=AIC[:, dsl, :].rearrange("p g k -> p k g")),
                     reads=[b_AIC], writes=[b_ak])
            a1 = k.sb("ss_a1", [128, 2, 32], F32, l0)
            b_a1 = Buf("a1")
            for d in range(2):
                i8 = 8 if d == 0 else 0
                i1 = 1 if d == 0 else 7
                dsl = slice(d * 16, (d + 1) * 16)
                for ri in range(2):
                    P.op("dve", lambda e, d=d, ri=ri, i8=i8, dsl=dsl: e.tensor_copy(out=A8c[:, d, :, ri], in_=ARA[:, dsl, i8]), reads=[b_ARA], writes=[b_a8])
                P.op("dve", lambda e, d=d, i8=i8, dsl=dsl: e.tensor_scalar(out=A8s[:, d, :, 0], in0=AIA[:, dsl, i8], scalar1=-1.0, scalar2=None, op0=MUL),
                     reads=[b_AIA], writes=[b_a8])
                P.op("dve", lambda e, d=d, i8=i8, dsl=dsl: e.tensor_copy(out=A8s[:, d, :, 1], in_=AIA[:, dsl, i8]), reads=[b_AIA], writes=[b_a8])
                P.op("dve", lambda e, d=d, i1=i1, dsl=dsl: e.tensor_copy(out=a1[:, 0, dsl], in_=ARA[:, dsl, i1]), reads=[b_ARA], writes=[b_a1])
                P.op("dve", lambda e, d=d, i1=i1, dsl=dsl: e.tensor_copy(out=a1[:, 1, dsl], in_=AIA[:, dsl, i1]), reads=[b_AIA], writes=[b_a1])
            fz = k.sb("ss_fz", [128, 6, 32], F32, l0)
            b_fz = Buf("fz")
            P.op("dve", lambda e: e.tensor_tensor(out=fz[:, 0, :], in0=lre[:], in1=lre[:], op=MUL), reads=[b_l], writes=[b_fz])
            P.op("dve", lambda e: e.tensor_tensor(out=fz[:, 1, :], in0=lim[:], in1=lim[:], op=MUL), reads=[b_l], writes=[b_fz])
            P.op("dve", lambda e: e.tensor_tensor(out=fz[:, 0, :], in0=fz[:, 0, :], in1=fz[:, 1, :], op=ADD), reads=[b_fz], writes=[b_fz])
            P.op("dve", lambda e: e.reciprocal(out=fz[:, 1, :], in_=fz[:, 0, :]), reads=[b_fz], writes=[b_fz])
            P.op("dve", lambda e: e.tensor_scalar(out=fz[:, 0, :], in0=a1[:, 0, :], scalar1=-1.0, scalar2=None, op0=ADD), reads=[b_a1], writes=[b_fz])
            P.op("dve", lambda e: e.tensor_tensor(out=fz[:, 2, :], in0=fz[:, 0, :], in1=lre[:], op=MUL), reads=[b_fz, b_l], writes=[b_fz])
            P.op("dve", lambda e: e.tensor_tensor(out=fz[:, 3, :], in0=a1[:, 1, :], in1=lim[:], op=MUL), reads=[b_a1, b_l], writes=[b_fz])
            P.op("dve", lambda e: e.tensor_tensor(out=fz[:, 2, :], in0=fz[:, 2, :], in1=fz[:, 3, :], op=ADD), reads=[b_fz], writes=[b_fz])
            P.op("dve", lambda e: e.tensor_tensor(out=fz[:, 2, :], in0=fz[:, 2, :], in1=fz[:, 1, :], op=MUL), reads=[b_fz], writes=[b_fz])
            P.op("dve", lambda e: e.tensor_tensor(out=fz[:, 4, :], in0=a1[:, 1, :], in1=lre[:], op=MUL), reads=[b_a1, b_l], writes=[b_fz])
            P.op("dve", lambda e: e.tensor_tensor(out=fz[:, 5, :], in0=fz[:, 0, :], in1=lim[:], op=MUL), reads=[b_fz, b_l], writes=[b_fz])
            P.op("dve", lambda e: e.tensor_tensor(out=fz[:, 4, :], in0=fz[:, 4, :], in1=fz[:, 5, :], op=SUB), reads=[b_fz], writes=[b_fz])
            P.op("dve", lambda e: e.tensor_tensor(out=fz[:, 4, :], in0=fz[:, 4, :], in1=fz[:, 1, :], op=MUL), reads=[b_fz], writes=[b_fz])
            BT = k.sb("ss_BT", [128, 2, 2, 16, 16], F32, l0)
            BB = k.sb("ss_BB", [128, 2, 2, 16, 16], F32, l0)
            CN = k.sb("ss_CN", [128, 2, 2, 2, 128], F32, l0)
            CT = k.sb("ss_CT", [128, 2, 2, 16, 16], F32, l0)
            tA = k.sb("ss_tA", [128, 16, 9, 16], F32, l0)
            tB = k.sb("ss_tB", [128, 16, 9, 16], F32, l0)
            b_BT, b_BB, b_CN, b_CT, b_tA, b_tB = Buf("BT"), Buf("BB"), Buf("CN"), Buf("CT"), Buf("tA"), Buf("tB")
            for d in range(2):
                for ri in range(2):
                    bsrc = I["ssm_bre"] if ri == 0 else I["ssm_bim"]
                    csrc = I["ssm_cre"] if ri == 0 else I["ssm_cim"]
                    for gp in range(2):
                        P.dma("sp", BT[gp * 64:(gp + 1) * 64, d, ri, :, :], bsrc[d, gp * 16:(gp + 1) * 16].rearrange("g p c -> p g c"),
                              writes=[b_BT], sem=csem)
                        for blk in range(2):
                            g0 = gp * 16 + blk * 8
                            P.dma("sp", CN[:, d, ri, blk, gp * 64:(gp + 1) * 64], csrc[d, g0:g0 + 8].rearrange("g c p -> (g c) p"),
                                  writes=[b_CN], sem=csem)
            for d in range(2):
                for ri in range(2):
                    for blk in range(2):
                        pt, bp = nb()
                        P.op("pe", lambda e, d=d, ri=ri, blk=blk, pt=pt: e.transpose(out=pt[:, 0:128], in_=CN[:, d, ri, blk, :], identity=k.ident[:]),
                             reads=[b_CN, k.b_ident], writes=[bp])
                        P.op("dve", lambda e, d=d, ri=ri, blk=blk, pt=pt: e.tensor_copy(
                            out=CT[:, d, ri, blk * 8:(blk + 1) * 8, :].rearrange("p a b -> p (a b)"), in_=pt[:, 0:128]), reads=[bp], writes=[b_CT])
            for d in range(2):
                dsl = slice(d * 16, (d + 1) * 16)
                frb = lambda d=d, dsl=dsl: fz[:, 2, dsl][:, :, None].broadcast_to([128, 16, 16])
                fib = lambda d=d, dsl=dsl: fz[:, 4, dsl][:, :, None].broadcast_to([128, 16, 16])
                t16a = tA[:, :, 0, :]
                t16b = tB[:, :, 0, :]
                P.op("dve", lambda e, d=d, frb=frb: e.tensor_tensor(out=t16a, in0=BT[:, d, 0], in1=frb(), op=MUL), reads=[b_BT, b_fz], writes=[b_tA])
                P.op("dve", lambda e, d=d, fib=fib: e.tensor_tensor(out=t16b, in0=BT[:, d, 1], in1=fib(), op=MUL), reads=[b_BT, b_fz], writes=[b_tB])
                P.op("dve", lambda e, d=d: e.tensor_tensor(out=BB[:, d, 0], in0=t16a, in1=t16b, op=SUB), reads=[b_tA, b_tB], writes=[b_BB])
                P.op("dve", lambda e, d=d, frb=frb: e.tensor_tensor(out=t16a, in0=BT[:, d, 1], in1=frb(), op=MUL), reads=[b_BT, b_fz], writes=[b_tA])
                P.op("dve", lambda e, d=d, fib=fib: e.tensor_tensor(out=t16b, in0=BT[:, d, 0], in1=fib(), op=MUL), reads=[b_BT, b_fz], writes=[b_tB])
                P.op("dve", lambda e, d=d: e.tensor_tensor(out=BB[:, d, 1], in0=t16a, in1=t16b, op=ADD), reads=[b_tA, b_tB], writes=[b_BB])
            P.op("pool", lambda e: e.memset(RCp[:].rearrange("p a b c d -> p (a b c d)"), 0.0), writes=[b_rcp])
            for d in range(2):
                dsl = slice(d * 16, (d + 1) * 16)
                off = 112 if d == 0 else 0
                bc_c = lambda ri, d=d: CT[:, d, ri][:, :, None, :].broadcast_to([128, 16, 9, 16])
                bc_ar = lambda dsl=dsl: ARA[:, dsl, :][:, :, :, None].broadcast_to([128, 16, 9, 16])
                bc_ai = lambda dsl=dsl: AIA[:, dsl, :][:, :, :, None].broadcast_to([128, 16, 9, 16])
                dst = lambda ri, d=d, off=off: RCp[:, d, ri, :, off:off + 144].rearrange("p g (t c) -> p g t c", c=16)
                P.op("dve", lambda e, bc_c=bc_c, bc_ar=bc_ar: e.tensor_tensor(out=tA[:], in0=bc_c(0), in1=bc_ar(), op=MUL), reads=[b_CT, b_ARA], writes=[b_tA])
                P.op("dve", lambda e, bc_c=bc_c, bc_ai=bc_ai: e.tensor_tensor(out=tB[:], in0=bc_c(1), in1=bc_ai(), op=MUL), reads=[b_CT, b_AIA], writes=[b_tB])
                P.op("dve", lambda e, dst=dst: e.tensor_tensor(out=dst(0), in0=tA[:], in1=tB[:], op=SUB), reads=[b_tA, b_tB], writes=[b_rcp])
                P.op("dve", lambda e, bc_c=bc_c, bc_ai=bc_ai: e.tensor_tensor(out=tA[:], in0=bc_c(0), in1=bc_ai(), op=MUL), reads=[b_CT, b_AIA], writes=[b_tA])
                P.op("dve", lambda e, bc_c=bc_c, bc_ar=bc_ar: e.tensor_tensor(out=tB[:], in0=bc_c(1), in1=bc_ar(), op=MUL), reads=[b_CT, b_ARA], writes=[b_tB])
                P.op("dve", lambda e: e.tensor_tensor(out=tA[:], in0=tA[:], in1=tB[:], op=ADD), reads=[b_tA, b_tB], writes=[b_tA])
                P.op("dve", lambda e, dst=dst: e.tensor_scalar(out=dst(1), in0=tA[:], scalar1=-1.0, scalar2=None, op0=MUL), reads=[b_tA], writes=[b_rcp])
            Lp = k.sb("ss_Lp", [128, 64, 240], BF16, l0)
            b_Lp = Buf("Lp")
            P.op("pool", lambda e: e.memset(Lp[:].rearrange("p a b -> p (a b)"), 0.0), writes=[b_Lp])
            P.op("pool", lambda e: e.tensor_copy(out=Lp[:, :, 112:128], in_=BB[:].rearrange("p a b c d -> p (a b c) d")), reads=[b_BB], writes=[b_Lp])
            BW = k.sb("ss_BW", [128, 2, 2, 16, 128], F32, l0)
            b_BW = Buf("BW")
            for d in range(2):
                dsl = slice(d * 16, (d + 1) * 16)
                bc_b = lambda ri, d=d: BB[:, d, ri][:, :, None, :].broadcast_to([128, 16, 8, 16])
                bc_ar = lambda dsl=dsl: ARB[:, dsl, :][:, :, :, None].broadcast_to([128, 16, 8, 16])
                bc_ai = lambda dsl=dsl: AIB[:, dsl, :][:, :, :, None].broadcast_to([128, 16, 8, 16])
                dst = lambda ri, d=d: BW[:, d, ri].rearrange("p g (t c) -> p g t c", c=16)
                ta8 = tA[:, :, 0:8, :]
                tb8 = tB[:, :, 0:8, :]
                P.op("dve", lambda e, bc_b=bc_b, bc_ar=bc_ar: e.tensor_tensor(out=ta8, in0=bc_b(0), in1=bc_ar(), op=MUL), reads=[b_BB, b_ARB], writes=[b_tA])
                P.op("dve", lambda e, bc_b=bc_b, bc_ai=bc_ai: e.tensor_tensor(out=tb8, in0=bc_b(1), in1=bc_ai(), op=MUL), reads=[b_BB, b_AIB], writes=[b_tB])
                P.op("dve", lambda e, dst=dst: e.tensor_tensor(out=dst(0), in0=ta8, in1=tb8, op=SUB), reads=[b_tA, b_tB], writes=[b_BW])
                P.op("dve", lambda e, bc_b=bc_b, bc_ai=bc_ai: e.tensor_tensor(out=ta8, in0=bc_b(0), in1=bc_ai(), op=MUL), reads=[b_BB, b_AIB], writes=[b_tA])
                P.op("dve", lambda e, bc_b=bc_b, bc_ar=bc_ar: e.tensor_tensor(out=tb8, in0=bc_b(1), in1=bc_ar(), op=MUL), reads=[b_BB, b_ARB], writes=[b_tB])
                P.op("dve", lambda e, dst=dst: e.tensor_tensor(out=dst(1), in0=ta8, in1=tb8, op=ADD), reads=[b_tA, b_tB], writes=[b_BW])
            for d in range(2):
                for g2 in range(16):
                    for ri in range(2):
                        pt, bp = nb()
                        P.op("pe", lambda e, d=d, g2=g2, ri=ri, pt=pt: e.transpose(out=pt[:, 0:128], in_=BW[:, d, ri, g2, :], identity=k.ident[:]),
                             reads=[b_BW, k.b_ident], writes=[bp])
                        eng = "act" if (g2 + ri) % 2 else "dve"
                        if eng == "act":
                            P.op("act", lambda e, d=d, g2=g2, ri=ri, pt=pt: e.activation(out=WT[:, d, g2, ri, :], in_=pt[:, 0:128], func=AF.Copy), reads=[bp], writes=[b_wtl[d][g2][ri]])
                        else:
                            P.op("dve", lambda e, d=d, g2=g2, ri=ri, pt=pt: e.tensor_copy(out=WT[:, d, g2, ri, :], in_=pt[:, 0:128]), reads=[bp], writes=[b_wtl[d][g2][ri]])
            for g2 in range(16):
                for gp in range(2):
                    g = gp * 16 + g2
                    pt, bp = nb()
                    psl = slice(gp * 64, (gp + 1) * 64)
                    n = 0
                    for d in range(2):
                        for ri in range(2):
                            for s_ in range(8):
                                w0 = (7 - s_) * 16 if d == 0 else (8 - s_) * 16
                                l0_ = (7 - s_) * 16
                                P.op("pe", lambda e, d=d, ri=ri, g2=g2, w0=w0, l0_=l0_, psl=psl, pt=pt, n=n: e.matmul(
                                    pt[:, 0:128], lhsT=Lp[psl, (d * 2 + ri) * 16 + g2, l0_:l0_ + 128], rhs=RCp[psl, d, ri, g2, w0:w0 + 128],
                                    start=(n == 0), stop=(n == 31)), reads=[b_Lp, b_rcp], writes=[bp])
                                n += 1
                    P.op("dve" if g % 2 else "act",
                         (lambda e, g=g, pt=pt: e.tensor_copy(out=ToepT[:, g, :], in_=pt[:, 0:128])) if g % 2 else
                         (lambda e, g=g, pt=pt: e.activation(out=ToepT[:, g, :], in_=pt[:, 0:128], func=AF.Copy)),
                         reads=[bp], writes=[b_toep[g]])
            P.barrier()
        Ubuf = k.sb("ss_ubuf", [128, 32, 320], BF16, ls)
        Zbf = k.sb("ss_zbf", [128, 2, 16, 2, 288], BF16, ls)
        if k.debug.get("_ssm_upto", 99) < 1:
            return
        with ExitStack() as l1:
            ucm = [k.sb(f"ss_ucm{i}", [128, 8, 512], F32, l1) for i in range(2)]
            b_ucm = [Buf(f"ucm{i}") for i in range(2)]
            usem = [P.new_dsem(f"ss_us{i}") for i in range(2)]
            ucg = k.sb("ss_ucg", [128, 32, 128], F32, l1)
            b_ucg = Buf("ucg")
            for jt in range(3):
                si = jt % 2
                nj = 128 if jt < 2 else 32
                r0 = jt * 1024
                P.dma("sp", ucm[si][0:nj], S["u"][r0:r0 + nj * 8, :].rearrange("(j s) c -> j s c", s=8), writes=[b_ucm[si]], sem=usem[si])
                P.op("dve", lambda e, si=si, nj=nj: e.tensor_copy(out=ucg[0:nj].rearrange("p g (s c) -> p g s c", c=16),
                                                                 in_=ucm[si][0:nj].rearrange("p s (g c) -> p g s c", c=16)),
                     reads=[b_ucm[si]], writes=[b_ucg])
                for g0 in range(0, 32, 4):
                    pt, bp = nb()
                    for gg in range(4):
                        g = g0 + gg
                        P.op("pe", lambda e, si=si, nj=nj, g=g, gg=gg, pt=pt: e.transpose(
                            out=pt[:, gg * 128:gg * 128 + nj], in_=ucg[0:nj, g, :], identity=k.ident[0:nj, 0:nj]),
                            reads=[b_ucg, k.b_ident], writes=[bp])
                    src = lambda pt=pt, nj=nj: pt[:].rearrange("p (a b) -> p a b", b=128)[:, :, 0:nj]
                    cols = [32 + jt * 128] if jt < 2 else [0, 288]
                    for ci, c0 in enumerate(cols):
                        eng = "act" if (g0 // 4 + ci) % 2 else "dve"
                        if eng == "act":
                            P.op("act", lambda e, g0=g0, c0=c0, nj=nj, src=src: e.activation(out=Ubuf[:, g0:g0 + 4, c0:c0 + nj], in_=src(), func=AF.Copy),
                                 reads=[bp], writes=[b_U[g0 + i] for i in range(4)])
                        else:
                            P.op("dve", lambda e, g0=g0, c0=c0, nj=nj, src=src: e.tensor_copy(out=Ubuf[:, g0:g0 + 4, c0:c0 + nj], in_=src()),
                                 reads=[bp], writes=[b_U[g0 + i] for i in range(4)])
            P.barrier()
        if k.debug.get("_ssm_upto", 99) < 2:
            return
        with ExitStack() as l2:
            Z = [k.sb(f"ss_Z{d}", [128, 16, 2, 288], F32, l2) for d in range(2)]
            b_Z = [Buf("Z0"), Buf("Z1")]
            b_Ze = [[Buf(f"Ze{d}_{i}") for i in range(32)] for d in range(2)]
            zjoin = {0: True, 1: True}
            for d in range(2):
                j0 = 0 if d == 0 else 32
                for g2 in range(16):
                    for ri in range(2):
                        pt, bp = nb()
                        for gp in range(2):
                            P.op("pe", lambda e, d=d, g2=g2, ri=ri, gp=gp, pt=pt, j0=j0: e.matmul(
                                pt[gp * 64:(gp + 1) * 64, 0:288], lhsT=WT[:, d, g2, ri, gp * 64:(gp + 1) * 64], rhs=Ubuf[:, gp * 16 + g2, j0:j0 + 288],
                                start=True, stop=True), reads=[b_wtl[d][g2][ri], b_U[gp * 16 + g2]], writes=[bp])
                        if (g2 + ri) % 2:
                            P.op("act", lambda e, d=d, g2=g2, ri=ri, pt=pt: e.activation(out=Z[d][:, g2, ri, :], in_=pt[:, 0:288], func=AF.Copy), reads=[bp], writes=[b_Ze[d][g2 * 2 + ri]])
                        else:
                            P.op("dve", lambda e, d=d, g2=g2, ri=ri, pt=pt: e.tensor_copy(out=Z[d][:, g2, ri, :], in_=pt[:, 0:288]), reads=[bp], writes=[b_Ze[d][g2 * 2 + ri]])
            k.dbg("V_dbg", [2, 128, 16 * 2 * 288], F32, lambda dd: (dd[0], Z[0][:].rearrange("p a b c -> p (a b c)")), b_Ze[0])
            k.dbg("V_dbg", [2, 128, 16 * 2 * 288], F32, lambda dd: (dd[1], Z[1][:].rearrange("p a b c -> p (a b c)")), [b_Z[1]])
            m1 = [k.sb(f"ss_m1{d}", [128, 16, 2, 36], F32, l2) for d in range(2)]
            m2 = [k.sb(f"ss_m2{d}", [128, 16, 2, 36], F32, l2) for d in range(2)]
            b_m1 = [Buf("m10"), Buf("m11")]
            b_m2 = [Buf("m20"), Buf("m21")]
            Zv = [Z[d][:].rearrange("p g r (K c) -> p g r K c", c=8) for d in range(2)]

            def cmul_add(eng, d, kidx, dst, src, src_sw, nK):
                if nK:
                    ac = AKc[:, d, kidx][:, :, :, None].broadcast_to([128, 16, 2, nK])
                    as_ = AKs[:, d, kidx][:, :, :, None].broadcast_to([128, 16, 2, nK])
                    t1_, t2_ = m1[d][:, :, :, 0:nK], m2[d][:, :, :, 0:nK]
                else:
                    ac, as_ = AKc[:, d, kidx], AKs[:, d, kidx]
                    t1_, t2_ = m1[d][:, :, :, 0], m2[d][:, :, :, 0]
                extra = []
                if zjoin[d]:
                    zjoin[d] = False
                    extra = list(b_Ze[d])
                P.op(eng, lambda e: e.tensor_tensor(out=t1_, in0=src, in1=ac, op=MUL), reads=[b_Z[d], b_ak] + extra, writes=[b_m1[d]])
                P.op(eng, lambda e: e.tensor_tensor(out=t2_, in0=src_sw, in1=as_, op=MUL), reads=[b_Z[d], b_ak], writes=[b_m2[d]])
                P.op(eng, lambda e: e.tensor_tensor(out=t1_, in0=t1_, in1=t2_, op=ADD), reads=[b_m1[d], b_m2[d]], writes=[b_m1[d]])
                P.op(eng, lambda e: e.tensor_tensor(out=dst, in0=dst, in1=t1_, op=ADD), reads=[b_Z[d], b_m1[d]], writes=[b_Z[d]])

            for i in range(1, 8):
                r = i
                cmul_add("dve", 0, 0, Zv[0][:, :, :, :, r], Zv[0][:, :, :, :, r - 1], Zv[0][:, :, ::-1, :, r - 1], 36)
                r = 7 - i
                cmul_add("dve", 1, 0, Zv[1][:, :, :, :, r], Zv[1][:, :, :, :, r + 1], Zv[1][:, :, ::-1, :, r + 1], 36)
            for i in range(1, 36):
                K = i
                cmul_add("dve", 0, 7, Zv[0][:, :, :, K, 7], Zv[0][:, :, :, K - 1, 7], Zv[0][:, :, ::-1, K - 1, 7], 0)
                K = 35 - i
                cmul_add("dve", 1, 7, Zv[1][:, :, :, K, 0], Zv[1][:, :, :, K + 1, 0], Zv[1][:, :, ::-1, K + 1, 0], 0)
            for i in range(7):
                r = i
                cmul_add("dve", 0, r, Zv[0][:, :, :, 1:36, r], Zv[0][:, :, :, 0:35, 7], Zv[0][:, :, ::-1, 0:35, 7], 35)
                r = 7 - i
                cmul_add("dve", 1, 7 - r, Zv[1][:, :, :, 0:35, r], Zv[1][:, :, :, 1:36, 0], Zv[1][:, :, ::-1, 1:36, 0], 35)
            for d in range(2):
                eng = "dve" if d == 0 else "pool"
                P.op(eng, lambda e, d=d: e.tensor_copy(out=Zbf[:, d].rearrange("p a b c -> p (a b c)"), in_=Z[d][:].rearrange("p a b c -> p (a b c)")),
                     reads=[b_Z[d]], writes=[b_zbf[d]])
            k.dbg("Z_dbg", [2, 128, 16 * 2 * 288], F32, lambda dd: (dd[0], Z[0][:].rearrange("p a b c -> p (a b c)")), [b_Z[0]])
            k.dbg("Z_dbg", [2, 128, 16 * 2 * 288], F32, lambda dd: (dd[1], Z[1][:].rearrange("p a b c -> p (a b c)")), [b_Z[1]])
            P.barrier()
        if k.debug.get("_ssm_upto", 99) < 3:
            return
        ygT = k.sb("ss_ygT", [128, 4, L], BF16, ls)
        with ExitStack() as l3:
            ycm = k.sb("ss_ycm", [128, 8, 512], F32, l3)
            b_ycm = [Buf(f"ycm{g}") for g in range(32)]
            ut = k.sb("ss_ut", [128, 8, 512], F32, l3)
            b_ut = Buf("ut")
            utsem = P.new_dsem("ss_uts")
            Dfull = k.sb("ss_D", [128, 512], F32, l3)
            b_D = Buf("Dfull")
            P.dma("sp", Dfull[:], I["ssm_d"][0:1, :].broadcast_to([128, 512]), writes=[b_D], sem=csem)
            sq = [k.sb(f"ss_sq{i}", [128, 512], F32, l3) for i in range(8)]
            b_sq = [Buf(f"sq{i}") for i in range(8)]
            GC = math.sqrt(2.0 / math.pi)
            for jt in range(2):
                P.dma("sp", ut[:], S["u"][jt * 1024:(jt + 1) * 1024, :].rearrange("(j s) c -> j s c", s=8), writes=[b_ut], sem=utsem)
                for g in range(32):
                    gp, g2 = g // 16, g % 16
                    psl = slice(gp * 64, (gp + 1) * 64)
                    pt, bp = nb()
                    c0 = 32 + jt * 128
                    P.op("pe", lambda e, g=g, c0=c0, pt=pt: e.matmul(pt[:, 0:128], lhsT=Ubuf[:, g, c0:c0 + 128], rhs=ToepT[:, g, :], start=True, stop=False),
                         reads=[b_U[g], b_toep[g]], writes=[bp])
                    for d in range(2):
                        jz = (31 + jt * 128) if d == 0 else (1 + jt * 128)
                        w0 = 128 if d == 0 else 0
                        for ri in range(2):
                            last = (d == 1 and ri == 1)
                            P.op("pe", lambda e, d=d, ri=ri, g2=g2, psl=psl, jz=jz, w0=w0, pt=pt, last=last: e.matmul(
                                pt[:, 0:128], lhsT=Zbf[psl, d, g2, ri, jz:jz + 128], rhs=RCp[psl, d, ri, g2, w0:w0 + 128], start=False, stop=last),
                                reads=[b_zbf[d], b_rcp], writes=[bp])
                    src = lambda pt=pt: pt[:, 0:128].rearrange("p (t c) -> p t c", c=16)
                    P.op("dve", lambda e, g=g, src=src: e.tensor_tensor(out=ycm[:, :, g * 16:(g + 1) * 16], in0=ut[:, :, g * 16:(g + 1) * 16],
                                                                       in1=Dfull[:, g * 16:(g + 1) * 16][:, None, :].broadcast_to([128, 8, 16]), op=MUL),
                         reads=[b_ut, b_D], writes=[b_ycm[g]])
                    P.op("dve", lambda e, g=g, src=src: e.tensor_tensor(out=ycm[:, :, g * 16:(g + 1) * 16], in0=ycm[:, :, g * 16:(g + 1) * 16], in1=src(), op=ADD),
                         reads=[bp, b_ycm[g]], writes=[b_ycm[g]])
                k.dbg("y_dbg", [L, 512], F32, lambda dd, jt=jt: (dd[jt * 1024:(jt + 1) * 1024, :].rearrange("(j s) c -> j s c", s=8), ycm[:]), b_ycm)
                for t in range(8):
                    P.op("dve", lambda e, t=t: e.tensor_tensor(out=sq[t][:], in0=ycm[:, t, :], in1=ycm[:, t, :], op=MUL), reads=b_ycm, writes=[b_sq[t]])
                    P.op("dve", lambda e, t=t: e.tensor_scalar(out=sq[t][:], in0=sq[t][:], scalar1=0.044715, scalar2=1.0, op0=MUL, op1=ADD), reads=[b_sq[t]], writes=[b_sq[t]])
                    P.op("dve", lambda e, t=t: e.tensor_tensor(out=sq[t][:], in0=sq[t][:], in1=ycm[:, t, :], op=MUL), reads=[b_sq[t]] + b_ycm, writes=[b_sq[t]])
                for t in range(8):
                    P.op("act", lambda e, t=t: e.activation(out=sq[t][:], in_=sq[t][:], func=AF.Sigmoid, scale=2.0 * GC), reads=[b_sq[t]], writes=[b_sq[t]])
                for t in range(8):
                    P.op("dve", lambda e, t=t: e.tensor_tensor(out=sq[t][:], in0=sq[t][:], in1=ycm[:, t, :], op=MUL), reads=[b_sq[t]] + b_ycm, writes=[b_sq[t]])
                gbanks = []
                for t in range(8):
                    pt, bp = nb()
                    gbanks.append((pt, bp))
                    for chb in range(4):
                        P.op("pe", lambda e, t=t, chb=chb, pt=pt: e.transpose(out=pt[:, chb * 128:(chb + 1) * 128], in_=sq[t][:, chb * 128:(chb + 1) * 128], identity=k.ident[:]),
                             reads=[b_sq[t], k.b_ident], writes=[bp])
                for t in range(8):
                    pt, bp = gbanks[t]
                    tsl = slice(jt * 1024 + t, (jt + 1) * 1024, 8)
                    P.op("act", lambda e, pt=pt, tsl=tsl: e.activation(out=ygT[:, :, tsl], in_=pt[:].rearrange("p (a b) -> p a b", b=128), func=AF.Copy),
                         reads=[bp], writes=[b_ygT])
            P.barrier()
        if k.debug.get("_ssm_upto", 99) < 4:
            return
        with ExitStack() as l4:
            wg32 = k.sb("ss_wg32", [128, 4, 512], F32, l4)
            wg = k.sb("ss_wg", [128, 4, 512], BF16, l4)
            bg = k.sb("ss_bg", [128, 4], F32, l4)
            b_wg32, b_wg, b_bg = Buf("wg32"), Buf("wg"), Buf("bg")
            P.dma("sp", wg32[:], I["w_glu"].rearrange("(fc p) c -> p fc c", p=128), writes=[b_wg32], sem=csem)
            P.dma("sp", bg[:], I["b_glu"].rearrange("(fc p) -> p fc", p=128), writes=[b_bg], sem=csem, allow_slow_non_contiguous=True)
            P.op("dve", lambda e: e.tensor_copy(out=wg[:], in_=wg32[:]), reads=[b_wg32], writes=[b_wg])
            gst = [k.sb(f"ss_gst{i}", [128, 512], BF16, l4) for i in range(2)]
            b_gst = [Buf("gst0"), Buf("gst1")]
            gsem = [P.new_dsem(f"ss_gs{i}") for i in range(2)]
            sg = [k.sb(f"ss_sg{i}", [128, 512], F32, l4) for i in range(2)]
            b_sg = [Buf("sg0"), Buf("sg1")]
            so = [k.sb(f"ss_so{i}", [128, 512], BF16, l4) for i in range(2)]
            b_so = [Buf("so0"), Buf("so1")]
            sosem = [P.new_dsem(f"ss_sos{i}") for i in range(2)]
            ui = 0
            for fo in range(4):
                for tb in range(4):
                    i = ui % 2
                    ui += 1
                    tsl = slice(tb * 512, (tb + 1) * 512)
                    P.dma("sp", gst[i][:], S["sgsT"][fo * 128:(fo + 1) * 128, tsl], writes=[b_gst[i]], sem=gsem[i])
                    pt, bp = nb()
                    for fc in range(4):
                        P.op("pe", lambda e, fc=fc, fo=fo, tsl=tsl, pt=pt: e.matmul(pt[:], lhsT=wg[:, fc, fo * 128:(fo + 1) * 128], rhs=ygT[:, fc, tsl],
                                                                               start=(fc == 0), stop=(fc == 3)), reads=[b_wg, b_ygT], writes=[bp])
                    P.op("act", lambda e, i=i, fo=fo, pt=pt: e.activation(out=sg[i][:], in_=pt[:], func=AF.Sigmoid, bias=bg[:, fo:fo + 1]),
                         reads=[bp, b_bg], writes=[b_sg[i]])
                    P.op("dve", lambda e, i=i, fo=fo, tsl=tsl: e.tensor_tensor(out=sg[i][:], in0=sg[i][:], in1=ygT[:, fo, tsl], op=MUL),
                         reads=[b_sg[i], b_ygT], writes=[b_sg[i]])
                    P.op("dve", lambda e, i=i: e.tensor_tensor(out=so[i][:], in0=sg[i][:], in1=gst[i][:], op=MUL),
                         reads=[b_sg[i], b_gst[i]], writes=[b_so[i]])
                    P.dma("sp", S["sbrT"][fo * 128:(fo + 1) * 128, tsl], so[i][:], reads=[b_so[i]], sem=sosem[i])


def phase_attn(k):
    nc, P, I, S = k.nc, k.P, k.I, k.S
    with ExitStack() as ls:
        lamv = k.sb("at_lamv", [128, 4, 64], F32, ls)
        lw = k.sb("at_lw", [128, 8], F32, ls)
        G = k.sb("at_G", [128, 128], F32, ls)
        b_lamv, b_lw, b_G = Buf("lamv"), Buf("lw"), Buf("G")
        csem = P.new_dsem("at_c")
        P.dma("sp", lamv[:].rearrange("p a b -> p (a b)"), I["lam"].rearrange("a b -> (a b)").partition_broadcast(128),
              writes=[b_lamv], sem=csem)
        P.dma("sp", G[:], I["subln_g"][0:1, :].broadcast_to([128, 128]), writes=[b_G], sem=csem)
        P.op("dve", lambda e: e.tensor_scalar(out=G[:], in0=G[:], scalar1=(1.0 - LAM_INIT), scalar2=None, op0=ALU.mult),
             reads=[b_G], writes=[b_G])
        for i in range(2):
            P.op("dve", lambda e, i=i: e.tensor_tensor(out=lamv[:, 2 * i, :], in0=lamv[:, 2 * i, :], in1=lamv[:, 2 * i + 1, :], op=ALU.mult),
                 reads=[b_lamv], writes=[b_lamv])
            P.op("dve", lambda e, i=i: e.tensor_reduce(out=lw[:, i:i + 1], in_=lamv[:, 2 * i, :], axis=mybir.AxisListType.X, op=ALU.add),
                 reads=[b_lamv], writes=[b_lw])
        P.op("act", lambda e: e.activation(out=lw[:, 2:4], in_=lw[:, 0:2], func=AF.Exp), reads=[b_lw], writes=[b_lw])
        P.op("dve", lambda e: e.tensor_tensor(out=lw[:, 4:5], in0=lw[:, 3:4], in1=lw[:, 2:3], op=ALU.subtract), reads=[b_lw], writes=[b_lw])
        P.op("dve", lambda e: e.tensor_scalar(out=lw[:, 5:6], in0=lw[:, 4:5], scalar1=-LAM_INIT, scalar2=None, op0=ALU.add),
             reads=[b_lw], writes=[b_lw])
        neglam = lw[:, 5:6]
        qTs = [k.sb(f"at_q{i}", [128, L], BF16, ls) for i in range(2)]
        kTs = [k.sb(f"at_k{i}", [128, LT], BF16, ls) for i in range(2)]
        Vs = [k.sb(f"at_v{i}", [128, 18, 130], BF16, ls) for i in range(2)]
        gas = [k.sb(f"at_ga{i}", [128, 16, 128], BF16, ls) for i in range(2)]
        aTs = [k.sb(f"at_aT{i}", [128, L], BF16, ls) for i in range(2)]
        b_q = [Buf(f"atq{i}") for i in range(2)]
        b_k = [Buf(f"atk{i}") for i in range(2)]
        b_v = [Buf(f"atv{i}") for i in range(2)]
        b_ga = [Buf(f"atga{i}") for i in range(2)]
        b_aT = [Buf(f"ataT{i}") for i in range(2)]
        hsem = [P.new_dsem(f"at_h{i}") for i in range(2)]
        asem = [P.new_dsem(f"at_a{i}") for i in range(2)]
        for i in range(2):
            P.op("pool", lambda e, i=i: e.memset(Vs[i][:, :, 128:130], 1.0), writes=[b_v[i]])
        PT = [k.sb(f"at_pt{i}", [128, 2, 18, 256], BF16, ls) for i in range(2)]
        b_PT = [[[Buf(f"pt{i}_{c}_{kp}") for kp in range(9)] for c in range(2)] for i in range(2)]
        sbk = [k.ps(f"at_s{i}", [128, 512], F32, ls) for i in range(3)]
        b_sbk = [Buf(f"ats{i}") for i in range(3)]
        obk = [k.ps(f"at_o{i}", [128, 512], F32, ls) for i in range(4)]
        b_obk = [Buf(f"ato{i}") for i in range(4)]
        tbk = k.ps("at_t", [128, 512], F32, ls)
        b_tbk = Buf("att")
        sm = [k.sb(f"at_sm{i}", [128, 8], F32, ls) for i in range(2)]
        b_sm = [Buf(f"atsm{i}") for i in range(2)]
        tmp = [k.sb(f"at_tmp{i}", [128, 128], F32, ls) for i in range(2)]
        b_tmp = [Buf(f"attmp{i}") for i in range(2)]
        ov = [k.sb(f"at_ov{i}", [128, 128], F32, ls) for i in range(2)]
        b_ov = [Buf(f"atov{i}") for i in range(2)]
        junk = k.sb("at_junk", [128, 128], F32, ls)
        b_junk = Buf("atjunk")
        cnt = {"s": 0, "u": 0}

        def load_head(h):
            s = h % 2
            P.dma("sp", qTs[s][:], S["qT"][h], writes=[b_q[s]], sem=hsem[s])
            P.dma("sp", kTs[s][:], S["kT"][h], writes=[b_k[s]], sem=hsem[s])
            P.dma("sp", Vs[s][:, :, 0:128], S["v"][:, h * 128:(h + 1) * 128].rearrange("(t p) e -> p t e", p=128),
                  writes=[b_v[s]], sem=hsem[s])
            P.dma("sp", gas[s][:], S["sga"][:, h * 128:(h + 1) * 128].rearrange("(t p) e -> p t e", p=128),
                  writes=[b_ga[s]], sem=hsem[s])

        def A_steps(h, qb):
            s = h % 2
            ps_ = qb % 2
            steps = []
            for kp in range(9):
                def step(kp=kp):
                    for c in range(2):
                        si = cnt["s"] % 3
                        cnt["s"] += 1
                        for j in range(2):
                            kt = 2 * kp + j
                            P.op("pe", lambda e, kt=kt, j=j, c=c, si=si: e.matmul(
                                sbk[si][:, j * 256:(j + 1) * 256], lhsT=kTs[s][c * 64:(c + 1) * 64, kt * 128:(kt + 1) * 128],
                                rhs=qTs[s][c * 64:(c + 1) * 64, qb * 256:(qb + 1) * 256], start=True, stop=True),
                                reads=[b_k[s], b_q[s]], writes=[b_sbk[si]])
                        P.op("act", lambda e, c=c, kp=kp, si=si: e.activation(
                            out=PT[ps_][:, c, 2 * kp:2 * kp + 2, :].rearrange("p a b -> p (a b)"), in_=sbk[si][:], func=AF.Exp, scale=0.125),
                            reads=[b_sbk[si]], writes=[b_PT[ps_][c][kp]])
                steps.append(step)
            return steps

        def B_gen(h, qb):
            s = h % 2
            ps_ = qb % 2
            for qi_ in range(2):
                yield from unitB(h, qb, qi_, s, ps_)

        def unitB(h, qb, qi, s, ps_):
            if True:
                qt = qb * 2 + qi
                u = cnt["u"] % 2
                cnt["u"] += 1
                banks = [obk[u * 2], obk[u * 2 + 1]]
                bb = [b_obk[u * 2], b_obk[u * 2 + 1]]
                for c in range(2):
                    for kt in range(18):
                        P.op("pe", lambda e, c=c, kt=kt: e.matmul(
                            banks[c][:, 0:129], lhsT=PT[ps_][:, c, kt, qi * 128:(qi + 1) * 128], rhs=Vs[s][:, kt, 0:129],
                            start=(kt == 0), stop=(kt == 17)),
                            reads=[b_PT[ps_][c][kt // 2], b_v[s]], writes=[bb[c]])
                        yield
                flush_pending()
                smt, bsm = sm[u], b_sm[u]
                for c in range(2):
                    P.op("dve", lambda e, c=c: e.reciprocal(out=smt[:, c:c + 1], in_=banks[c][:, 128:129]), reads=[bb[c]], writes=[bsm])
                P.op("dve", lambda e: e.tensor_tensor(out=smt[:, 2:3], in0=smt[:, 1:2], in1=neglam, op=ALU.mult), reads=[bsm, b_lw], writes=[bsm])
                P.op("dve", lambda e: e.tensor_scalar(out=tmp[u][:], in0=banks[1][:, 0:128], scalar1=smt[:, 2:3], scalar2=None, op0=ALU.mult),
                     reads=[bb[1], bsm], writes=[b_tmp[u]])
                P.op("dve", lambda e: e.scalar_tensor_tensor(out=ov[u][:], in0=banks[0][:, 0:128], scalar=smt[:, 0:1], in1=tmp[u][:],
                                                            op0=ALU.mult, op1=ALU.add),
                     reads=[bb[0], bsm, b_tmp[u]], writes=[b_ov[u]])
                P.op("dve", lambda e: e.tensor_tensor(out=tmp[u][:], in0=ov[u][:], in1=ov[u][:], op=ALU.mult),
                     reads=[b_ov[u]], writes=[b_tmp[u]])
                P.op("dve", lambda e: e.tensor_reduce(out=smt[:, 3:4], in_=tmp[u][:], axis=mybir.AxisListType.X, op=ALU.add),
                     reads=[b_tmp[u]], writes=[bsm])
                P.op("dve", lambda e: e.tensor_scalar(out=smt[:, 4:5], in0=smt[:, 3:4], scalar1=1.0 / 128, scalar2=EPS, op0=ALU.mult, op1=ALU.add),
                     reads=[bsm], writes=[bsm])
                def fin():
                    P.op("act", lambda e: e.activation(out=smt[:, 5:6], in_=smt[:, 4:5], func=AF.Ln), reads=[bsm], writes=[bsm])
                    P.op("act", lambda e: e.activation(out=smt[:, 6:7], in_=smt[:, 5:6], func=AF.Exp, scale=-0.5), reads=[bsm], writes=[bsm])
                    P.op("dve", lambda e: e.scalar_tensor_tensor(out=ov[u][:], in0=ov[u][:], scalar=smt[:, 6:7], in1=G[:], op0=ALU.mult, op1=ALU.mult),
                         reads=[b_ov[u], bsm, b_G], writes=[b_ov[u]])
                    P.op("pool", lambda e: e.tensor_tensor(out=ov[u][:], in0=ov[u][:], in1=gas[s][:, qt, :], op=ALU.mult),
                         reads=[b_ov[u], b_ga[s]], writes=[b_ov[u]])

                    def fin2():
                        P.op("pe", lambda e: e.transpose(out=tbk[:, 0:128], in_=ov[u][:], identity=k.ident[:]), reads=[b_ov[u], k.b_ident], writes=[b_tbk])
                        P.op("dve", lambda e: e.tensor_copy(out=aTs[s][:, qt * 128:(qt + 1) * 128], in_=tbk[:, 0:128]),
                             reads=[b_tbk], writes=[b_aT[s]])
                    pending2.append(fin2)
                pending.append(fin)

        pending = []
        pending2 = []

        def flush_pending():
            while pending2:
                pending2.pop(0)()
            while pending:
                pending.pop(0)()

        def interleave(a_steps, bgen, per=8):
            for st_ in a_steps:
                st_()
                if bgen is not None:
                    for _ in range(per):
                        try:
                            next(bgen)
                        except StopIteration:
                            bgen = None
                            break
            if bgen is not None:
                for _ in bgen:
                    pass

        load_head(0)
        load_head(1)
        interleave(A_steps(0, 0), None)
        for h in range(HEADS):
            for qb in range(8):
                if qb + 1 < 8:
                    nxt = A_steps(h, qb + 1)
                elif h + 1 < HEADS:
                    nxt = A_steps(h + 1, 0)
                else:
                    nxt = []
                interleave(nxt, B_gen(h, qb))
            flush_pending()
            flush_pending()
            P.dma("pool", S["abrT"][h * 128:(h + 1) * 128, :], aTs[h % 2][:], reads=[b_aT[h % 2]], sem=asem[h % 2])
            if h + 2 < HEADS:
                load_head(h + 2)


def phase_merge(k):
    nc, P, I, S = k.nc, k.P, k.I, k.S
    with ExitStack() as ls:
        mT = k.sb("mg_mT", [128, NKC, L], BF16, ls)
        b_mT = [Buf(f"mT{tb}") for tb in range(4)]
        wout = k.sb("mg_wout", [128, NKC, D], BF16, ls)
        b_wout = Buf("wout")
        NXB = 3
        wov = I["w_out"].rearrange("(kc p) c -> p kc c", p=128)
        wo_state = {"kc": 0}
        b_woutc = [Buf(f"woutc{i}") for i in range(NKC)]

        def load_wout_chunk():
            kc = wo_state["kc"]
            if kc >= NKC:
                return
            wo_state["kc"] += 1
            P.dma("pool", wout[:, kc, :], wov[:, kc, :], writes=[b_woutc[kc]])
        with ExitStack() as l1:
            abrT = k.sb("mg_abrT", [128, 8, L], BF16, l1)
            sbrT = k.sb("mg_sbrT", [128, 4, L], BF16, l1)
            b_abrT, b_sbrT = Buf("abrT"), Buf("sbrT")
            lsem = P.new_dsem("mg_l")
            P.dma("sp", abrT[:], S["abrT"].rearrange("(fc p) t -> p fc t", p=128), writes=[b_abrT], sem=lsem)
            P.dma("sp", sbrT[:], S["sbrT"].rearrange("(fc p) t -> p fc t", p=128), writes=[b_sbrT], sem=lsem)
            NWS = 2
            wbf = [k.sb(f"mg_wbf{i}", [128, 12, 128], BF16, l1) for i in range(NWS)]
            b_wbf = [Buf(f"mgwbf{i}") for i in range(NWS)]
            NG = 2
            gt = [k.sb(f"mg_gt{i}", [128, 2, L], BF16, l1) for i in range(NG)]
            b_gt = [Buf(f"mggt{i}") for i in range(NG)]
            t1 = [k.sb(f"mg_t1{i}", [128, 512], F32, l1) for i in range(2)]
            t2 = [k.sb(f"mg_t2{i}", [128, 512], F32, l1) for i in range(2)]
            b_t1 = [Buf(f"mgt1{i}") for i in range(2)]
            b_t2 = [Buf(f"mgt2{i}") for i in range(2)]
            pa = [k.ps(f"mg_pa{i}", [128, 512], F32, l1) for i in range(2)]
            pp = [k.ps(f"mg_pp{i}", [128, 512], F32, l1) for i in range(2)]
            b_pa = [Buf(f"mgpa{i}") for i in range(2)]
            b_pp = [Buf(f"mgpp{i}") for i in range(2)]
            wpa_v = I["w_pa"].rearrange("(fc p) c -> p fc c", p=128)
            wps_v = I["w_ps"].rearrange("(fc p) c -> p fc c", p=128)
            ui = 0

            def load_w(fo):
                s = fo % NWS
                P.dma("pool", wbf[s][:, 0:8, :], wpa_v[:, :, fo * 128:(fo + 1) * 128], writes=[b_wbf[s]])
                P.dma("pool", wbf[s][:, 8:12, :], wps_v[:, :, fo * 128:(fo + 1) * 128], writes=[b_wbf[s]])
                gi = fo % NG
                P.dma("sp", gt[gi][:, 0, :], S["sgmT"][fo * 128:(fo + 1) * 128, :], writes=[b_gt[gi]])
                P.dma("sp", gt[gi][:, 1, :], S["sgmT"][D + fo * 128:D + (fo + 1) * 128, :], writes=[b_gt[gi]])

            load_w(0)
            for fo in range(NKC):
                if fo + 1 < NKC:
                    load_w(fo + 1)
                load_wout_chunk()
                s = fo % NWS
                gi = fo % NG
                for tb in range(4):
                    u2 = ui % 2
                    ui += 1
                    tsl = slice(tb * 512, (tb + 1) * 512)
                    for fc in range(8):
                        P.op("pe", lambda e, fc=fc, s=s, tsl=tsl, u2=u2: e.matmul(pa[u2][:], lhsT=wbf[s][:, fc, :], rhs=abrT[:, fc, tsl],
                                                                        start=(fc == 0), stop=(fc == 7)),
                             reads=[b_wbf[s], b_abrT], writes=[b_pa[u2]])
                    for fc in range(4):
                        P.op("pe", lambda e, fc=fc, s=s, tsl=tsl, u2=u2: e.matmul(pp[u2][:], lhsT=wbf[s][:, 8 + fc, :], rhs=sbrT[:, fc, tsl],
                                                                        start=(fc == 0), stop=(fc == 3)),
                             reads=[b_wbf[s], b_sbrT], writes=[b_pp[u2]])
                    P.op("dve", lambda e, gi=gi, u2=u2, tsl=tsl: e.tensor_tensor(out=t1[u2][:], in0=pa[u2][:], in1=gt[gi][:, 0, tsl], op=ALU.mult),
                         reads=[b_pa[u2], b_gt[gi]], writes=[b_t1[u2]])
                    P.op("dve", lambda e, gi=gi, u2=u2, tsl=tsl: e.tensor_tensor(out=t2[u2][:], in0=pp[u2][:], in1=gt[gi][:, 1, tsl], op=ALU.mult),
                         reads=[b_pp[u2], b_gt[gi]], writes=[b_t2[u2]])
                    P.op("pool", lambda e, fo=fo, tsl=tsl, u2=u2: e.tensor_tensor(out=mT[:, fo, tsl], in0=t1[u2][:], in1=t2[u2][:], op=ALU.add),
                         reads=[b_t1[u2], b_t2[u2]], writes=[b_mT[tb]])
            P.barrier()
        gateB = k.sb("mg_gateB", [128, D], F32, ls)
        fgB = k.sb("mg_fgB", [128, D], F32, ls)
        b_gateB, b_fgB = Buf("gateB"), Buf("fgB")
        c2 = P.new_dsem("mg_c2")
        P.dma("sp", gateB[:], S["modrow"][0:1, 2 * D:3 * D].broadcast_to([128, D]), writes=[b_gateB], sem=c2)
        P.dma("sp", fgB[:], I["final_g"][0:1, :].broadcast_to([128, D]), writes=[b_fgB], sem=c2)
        xb = [k.sb(f"mg_x{i}", [128, D], F32, ls) for i in range(NXB)]
        b_xb = [Buf(f"mgx{i}") for i in range(NXB)]
        xn = [k.sb(f"mg_xn{i}", [128, D], F32, ls) for i in range(NXB)]
        b_xn = [Buf(f"mgxn{i}") for i in range(NXB)]
        xsem = [P.new_dsem(f"mg_xs{i}") for i in range(NXB)]
        osem = [P.new_dsem(f"mg_os{i}") for i in range(NXB)]
        st2 = [k.sb(f"mg_st{i}", [128, 4], F32, ls) for i in range(NXB)]
        b_st2 = [Buf(f"mgst{i}") for i in range(NXB)]
        while wo_state["kc"] < NKC:
            load_wout_chunk()
        po = [k.ps(f"mg_po{i}", [128, 512], F32, ls) for i in range(3)]
        b_po = [Buf(f"mgpo{i}") for i in range(3)]
        pi = 0

        def load_x(t):
            if t < 16:
                s_ = t % NXB
                P.dma("act", xb[s_][:], I["x"][t * 128:(t + 1) * 128, :], writes=[b_xb[s_]], sem=xsem[s_])

        def fin1(t):
            s = t % NXB
            P.op("pool", lambda e: e.tensor_tensor(out=xn[s][:], in0=xn[s][:], in1=xb[s][:], op=ALU.add),
                 reads=[b_xn[s], b_xb[s]], writes=[b_xn[s]])
            P.op("pool", lambda e: e.tensor_tensor(out=xb[s][:], in0=xn[s][:], in1=xn[s][:], op=ALU.mult),
                 reads=[b_xn[s]], writes=[b_xb[s]])

        def fin2(t):
            s = t % NXB
            P.op("dve", lambda e: e.tensor_reduce(out=st2[s][:, 0:1], in_=xb[s][:], axis=mybir.AxisListType.X, op=ALU.add),
                 reads=[b_xb[s]], writes=[b_st2[s]])
            P.op("dve", lambda e: e.tensor_scalar(out=st2[s][:, 1:2], in0=st2[s][:, 0:1], scalar1=1.0 / D, scalar2=EPS, op0=ALU.mult, op1=ALU.add),
                 reads=[b_st2[s]], writes=[b_st2[s]])
            P.op("act", lambda e: e.activation(out=st2[s][:, 2:3], in_=st2[s][:, 1:2], func=AF.Ln), reads=[b_st2[s]], writes=[b_st2[s]])
            P.op("act", lambda e: e.activation(out=st2[s][:, 3:4], in_=st2[s][:, 2:3], func=AF.Exp, scale=-0.5), reads=[b_st2[s]], writes=[b_st2[s]])
            P.op("dve", lambda e: e.scalar_tensor_tensor(out=xn[s][:], in0=xn[s][:], scalar=st2[s][:, 3:4], in1=fgB[:], op0=ALU.mult, op1=ALU.mult),
                 reads=[b_xn[s], b_st2[s], b_fgB], writes=[b_xn[s]])
            P.dma("sp", k.out[t * 128:(t + 1) * 128, :], xn[s][:], reads=[b_xn[s]], sem=osem[s])
            load_x(t + NXB)

        for t0 in range(NXB):
            load_x(t0)
        for t in range(16):
            s = t % NXB
            tb = t // 4
            for cbk in range(4):
                p_ = pi % 3
                pi += 1
                for kc in range(NKC):
                    P.op("pe", lambda e, kc=kc, cbk=cbk, p_=p_, t=t: e.matmul(po[p_][:], lhsT=mT[:, kc, t * 128:(t + 1) * 128],
                                                                          rhs=wout[:, kc, cbk * 512:(cbk + 1) * 512],
                                                                          start=(kc == 0), stop=(kc == NKC - 1)),
                         reads=[b_mT[tb], b_woutc[kc]], writes=[b_po[p_]])
                P.op("dve", lambda e, cbk=cbk, p_=p_, s=s: e.tensor_tensor(out=xn[s][:, cbk * 512:(cbk + 1) * 512], in0=po[p_][:],
                                                                       in1=gateB[:, cbk * 512:(cbk + 1) * 512], op=ALU.mult),
                     reads=[b_po[p_], b_gateB], writes=[b_xn[s]])
            if t >= 2:
                fin2(t - 2)
            if t >= 1:
                fin1(t - 1)
        fin2(14)
        fin1(15)
        fin2(15)


_CACHE = {}


def _prep_inputs(inputs, b):
    f = lambda a: np.ascontiguousarray(np.asarray(a, dtype=np.float32))
    m = {}
    m["x"] = f(inputs["x"][b])
    m["ctx"] = f(inputs["ctx"][b])
    m["cc"] = f(np.stack([np.asarray(inputs["c"])[b], np.asarray(inputs["c_ctx"])], axis=0))
    m["w_ada"] = f(inputs["w_ada"][0])
    m["b_ada"] = f(inputs["b_ada"][0]).reshape(1, -1)
    m["norm_g"] = f(inputs["norm_g"][0])
    m["w_in"] = f(inputs["w_in"][0])
    m["lam"] = f(np.stack([np.asarray(inputs["lambda_q1"])[0], np.asarray(inputs["lambda_k1"])[0],
                           np.asarray(inputs["lambda_q2"])[0], np.asarray(inputs["lambda_k2"])[0]], axis=0))
    m["subln_g"] = f(inputs["subln_g"][0]).reshape(1, 128)
    m["ssm_lre"] = f(inputs["ssm_lambda_re"][0])
    m["ssm_lim"] = f(inputs["ssm_lambda_im"][0])
    m["ssm_ls"] = f(inputs["ssm_log_step"][0])
    m["ssm_bre"] = f(inputs["ssm_b_re"][0])
    m["ssm_bim"] = f(inputs["ssm_b_im"][0])
    m["ssm_cre"] = f(inputs["ssm_c_re"][0])
    m["ssm_cim"] = f(inputs["ssm_c_im"][0])
    m["ssm_d"] = f(inputs["ssm_d"][0]).reshape(1, 512)
    m["w_glu"] = f(inputs["w_glu"][0])
    m["b_glu"] = f(inputs["b_glu"][0])
    m["w_pa"] = f(inputs["w_pa"][0])
    m["w_ps"] = f(inputs["w_ps"][0])
    m["w_out"] = f(inputs["w_out"][0])
    m["final_g"] = f(inputs["final_g"]).reshape(1, D)
    m.update(_consts())
    return m


def kernel(**inputs):
    if "nc" not in _CACHE:
        _CACHE["nc"] = build()[0]
    nc = _CACHE["nc"]
    shared = None
    in_maps = []
    for b in range(8):
        m = _prep_inputs(inputs, b)
        if shared is None:
            shared = m
        else:
            for key in m:
                if key not in ("x", "ctx", "cc"):
                    m[key] = shared[key]
        in_maps.append(m)
    res = run_bass_kernel_spmd(nc, in_maps, core_ids=list(range(8)))
    return np.stack([np.asarray(r["out"], dtype=np.float32) for r in res.results], axis=0)
```

```python
import math
import numpy as np
import ml_dtypes
from contextlib import ExitStack
import concourse.bass as bass
import concourse.mybir as mybir
from concourse.bass_utils import run_bass_kernel_spmd

F32 = mybir.dt.float32
BF16 = mybir.dt.bfloat16
I32 = mybir.dt.int32
AF = mybir.ActivationFunctionType
ALU = mybir.AluOpType

D = 2048
L = 2048
LC = 256
LT = L + LC
NKC = D // 128
INW = 9216
HEADS = 8
EPS = 1e-6
LAM_INIT = 0.8 - 0.6 * math.exp(-0.3 * 0)
TWO_PI = 2.0 * math.pi


class Buf:
    __slots__ = ("name", "w", "r")

    def __init__(self, name):
        self.name = name
        self.w = None
        self.r = {}


class Prog:
    ENG = ["pe", "act", "dve", "pool", "sp"]

    def __init__(self, nc, st):
        self.nc = nc
        self.st = st
        self.q = {e: [] for e in self.ENG}
        self.seen = {e: {} for e in self.ENG}
        self.psem = {e: st.enter_context(nc.semaphore("p_" + e)) for e in ["pe", "act", "dve", "pool"]}
        self.dsems = []
        self.bufsem = {}
        self.bufsem_keep = []
        self.free_dsems = []

    def new_dsem(self, name):
        return None

    def _auto_dsem(self, reads, writes):
        b = writes[0] if len(writes) else reads[0]
        key = id(b)
        d = self.bufsem.get(key)
        if d is None:
            if self.free_dsems:
                d = self.free_dsems.pop()
            else:
                h = self.st.enter_context(self.nc.semaphore(f"d{len(self.dsems)}"))
                d = {"h": h, "n": 0, "name": f"d{len(self.dsems)}"}
                self.dsems.append(d)
            self.bufsem[key] = d
            self.bufsem_keep.append(b)
        return d

    def _deps(self, eng, reads, writes):
        need = {}

        def add(t):
            if t[0] == "c":
                if t[1] == "pe" and eng == "pe":
                    return
                key = ("c", t[1])
                if need.get(key, (None, -1))[1] < t[2]:
                    need[key] = (t[1], t[2])
            else:
                key = ("d", id(t[1]))
                if need.get(key, (None, -1))[1] < t[2]:
                    need[key] = (t[1], t[2])

        for b in reads:
            if b.w is not None:
                add(b.w)
        for b in writes:
            if b.w is not None:
                add(b.w)
            for t in b.r.values():
                add(t)
        waits = []
        for key, (obj, v) in need.items():
            if self.seen[eng].get(key, -1) >= v:
                continue
            self.seen[eng][key] = v
            waits.append((key[0], obj, v))
        return waits

    def _record(self, tok, reads, writes):
        for b in reads:
            key = (tok[0], tok[1] if tok[0] == "c" else id(tok[1]))
            b.r[key] = tok
        for b in writes:
            b.w = tok
            b.r = {}

    def op(self, eng, fn, reads=(), writes=()):
        waits = self._deps(eng, reads, writes)
        idx = len(self.q[eng])
        self.q[eng].append({"fn": fn, "waits": waits, "awaited": False, "dma": None})
        tok = ("c", eng, idx)
        self._record(tok, reads, writes)
        return tok

    def dma(self, eng, out, in_, reads=(), writes=(), sem=None, **kw):
        reads, writes = list(reads), list(writes)
        sem = self._auto_dsem(reads, writes)
        waits = self._deps(eng, reads, writes)
        sem["n"] += 16
        tok = ("d", sem, sem["n"])
        self.q[eng].append({"fn": (lambda e, o=out, i=in_, k=kw: e.dma_start(out=o, in_=i, **k)),
                            "waits": waits, "awaited": False, "dma": sem})
        self._record(tok, reads, writes)
        return tok

    def barrier(self):
        for e in self.ENG:
            waits = []
            for e2 in ["pe", "act", "dve", "pool"]:
                n = len(self.q[e2])
                if e2 == e:
                    n -= 0
                idx = None
                for i in range(len(self.q[e2]) - 1, -1, -1):
                    if self.q[e2][i]["fn"] is not None and self.q[e2][i]["dma"] is None:
                        idx = i
                        break
                if idx is None:
                    continue
                key = ("c", e2)
                if self.seen[e].get(key, -1) >= idx:
                    continue
                self.seen[e][key] = idx
                waits.append(("c", e2, idx))
            for d in self.dsems:
                if d["n"] == 0:
                    continue
                key = ("d", id(d))
                if self.seen[e].get(key, -1) >= d["n"]:
                    continue
                self.seen[e][key] = d["n"]
                waits.append(("d", d, d["n"]))
            if waits:
                self.q[e].append({"fn": None, "waits": waits, "awaited": False, "dma": None})
        for d in self.bufsem.values():
            self.free_dsems.append(d)
        self.bufsem = {}
        self.bufsem_keep = []

    def emit(self):
        for e in self.ENG:
            for ent in self.q[e]:
                for w in ent["waits"]:
                    if w[0] == "c":
                        self.q[w[1]][w[2]]["awaited"] = True
        cnt = {}
        for e in ["pe", "act", "dve", "pool"]:
            c = 0
            arr = []
            for ent in self.q[e]:
                if ent["awaited"]:
                    c += 1
                arr.append(c)
            cnt[e] = arr
        psem = self.psem
        q = self.q

        def run(name, e):
            for ent in q[name]:
                for w in ent["waits"]:
                    if w[0] == "c":
                        e.wait_ge(psem[w[1]], cnt[w[1]][w[2]])
                    else:
                        e.wait_ge(w[1]["h"], w[2])
                if ent["fn"] is None:
                    continue
                inst = ent["fn"](e)
                if ent["dma"] is not None:
                    inst.then_inc(ent["dma"]["h"], 16)
                elif ent["awaited"]:
                    inst.then_inc(psem[name], 1)

        with self.nc.Block() as block:
            @block.sync
            def _(e):
                run("sp", e)

            @block.scalar
            def _(e):
                run("act", e)

            @block.vector
            def _(e):
                run("dve", e)

            @block.gpsimd
            def _(e):
                run("pool", e)

            @block.tensor
            def _(e):
                run("pe", e)


def _consts():
    ident = np.eye(128, dtype=np.float32)
    m = np.arange(128)
    partner = np.where((m % 32) < 16, m + 16, m - 16)
    perm = np.zeros((128, 128), np.float32)
    perm[partner, m] = 1.0
    sgn = np.where((m % 32) < 16, -1.0, 1.0).astype(np.float32)
    tok = np.arange(L)
    pos = np.where(((m % 64) < 32)[:, None], (tok // 64)[None, :], (tok % 64)[None, :]).astype(np.float32)
    fexp = ((m % 16) / 16.0).astype(np.float32)
    colc = np.zeros((128, 4), np.float32)
    colc[:, 0] = sgn
    colc[:, 1] = fexp
    colc[:, 2] = np.where(m < 64, 1.0, -1.0)
    sel = np.zeros((2, 128), np.float32)
    sel[0, :] = 1.0
    tauA = np.zeros((128, 32, 9), np.float32)
    tauA[:, 0:16, :] = np.arange(9)[None, None, :]
    tauA[:, 16:32, :] = (8 - np.arange(9))[None, None, :]
    tauB = np.zeros((128, 32, 8), np.float32)
    tauB[:, 0:16, :] = (7 - np.arange(8))[None, None, :]
    tauB[:, 16:32, :] = np.arange(8)[None, None, :]
    tauC = np.zeros((128, 32, 8), np.float32)
    tauC[:, :, :] = (8.0 * (np.arange(8) + 1))[None, None, :]
    return {"c_ident": ident, "c_perm": perm, "c_pos": pos, "c_col": colc, "c_sel": sel, "c_tauA": tauA, "c_tauB": tauB,
            "c_tauC": tauC}


class K:
    pass


def build(debug=None):
    nc = bass.Bass("TRN2", target_bir_lowering=False)
    st = ExitStack()
    P = Prog(nc, st)
    k = K()
    k.nc, k.P, k.st = nc, P, st
    k.debug = debug or {}

    def dram_in(name, shape, dt=F32):
        return nc.dram_tensor(name, list(shape), dt, kind="ExternalInput").ap()

    dbg_outs = []

    def dram_scr(name, shape, dt):
        kind = "Internal"
        if debug is not None and name in debug.get("_inject", ()):
            kind = "ExternalInput"
        elif debug is not None and name in debug:
            kind = "ExternalOutput"
            dbg_outs.append(name)
        return nc.dram_tensor(name, list(shape), dt, kind=kind).ap()

    I = {}
    I["x"] = dram_in("x", [L, D])
    I["ctx"] = dram_in("ctx", [LC, D])
    I["cc"] = dram_in("cc", [2, D])
    I["w_ada"] = dram_in("w_ada", [D, 3 * D])
    I["b_ada"] = dram_in("b_ada", [1, 3 * D])
    I["norm_g"] = dram_in("norm_g", [D])
    I["w_in"] = dram_in("w_in", [D, INW])
    I["lam"] = dram_in("lam", [4, 64])
    I["subln_g"] = dram_in("subln_g", [1, 128])
    I["ssm_lre"] = dram_in("ssm_lre", [2, 32, 64])
    I["ssm_lim"] = dram_in("ssm_lim", [2, 32, 64])
    I["ssm_ls"] = dram_in("ssm_ls", [2, 32])
    I["ssm_bre"] = dram_in("ssm_bre", [2, 32, 64, 16])
    I["ssm_bim"] = dram_in("ssm_bim", [2, 32, 64, 16])
    I["ssm_cre"] = dram_in("ssm_cre", [2, 32, 16, 64])
    I["ssm_cim"] = dram_in("ssm_cim", [2, 32, 16, 64])
    I["ssm_d"] = dram_in("ssm_d", [1, 512])
    I["w_glu"] = dram_in("w_glu", [512, 512])
    I["b_glu"] = dram_in("b_glu", [512])
    I["w_pa"] = dram_in("w_pa", [1024, D])
    I["w_ps"] = dram_in("w_ps", [512, D])
    I["w_out"] = dram_in("w_out", [D, D])
    I["final_g"] = dram_in("final_g", [1, D])
    for cn, arr in _consts().items():
        I[cn] = dram_in(cn, arr.shape)
    out = nc.dram_tensor("out", [L, D], F32, kind="ExternalOutput").ap()

    S = {}
    S["modrow"] = dram_scr("modrow", [2, 3 * D], F32)
    S["qT"] = dram_scr("qT", [HEADS, 128, L], BF16)
    S["kT"] = dram_scr("kT", [HEADS, 128, LT], BF16)
    S["v"] = dram_scr("v", [LT, 1024], BF16)
    S["sga"] = dram_scr("sga", [L, 1024], BF16)
    S["u"] = dram_scr("u", [LT, 512], F32)
    S["sgsT"] = dram_scr("sgsT", [512, L], BF16)
    S["sgmT"] = dram_scr("sgmT", [2 * D, L], BF16)
    S["abrT"] = dram_scr("abrT", [1024, L], BF16)
    S["sbrT"] = dram_scr("sbrT", [512, L], BF16)
    S["hT"] = dram_scr("hT_dbg", [128, NKC, LT], BF16) if (debug is not None and "hT_dbg" in debug) else None
    k.I, k.S, k.out = I, S, out
    k.dbg_sem = None

    def dbg(name, shape, dt, ap_fn, bufs):
        if debug is None or name not in debug:
            return
        if name not in S:
            S[name] = nc.dram_tensor(name, list(shape), dt, kind="ExternalOutput").ap()
            dbg_outs.append(name)
        if k.dbg_sem is None:
            k.dbg_sem = P.new_dsem("dbgsem")
        o, i = ap_fn(S[name])
        P.dma("sp", o, i, reads=bufs, sem=k.dbg_sem)
    k.dbg = dbg

    def sb(name, shape, dt, stack=st):
        return stack.enter_context(nc.sbuf_tensor(name, list(shape), dt))

    def ps(name, shape, dt, stack=st):
        return stack.enter_context(nc.psum_tensor(name, list(shape), dt))

    k.sb, k.ps = sb, ps
    ident = sb("ident", [128, 128], F32)
    colc = sb("colc", [128, 4], F32)
    b_ident, b_colc = Buf("ident"), Buf("colc")
    csem = P.new_dsem("csem")
    P.dma("sp", ident[:], I["c_ident"], writes=[b_ident], sem=csem)
    P.dma("sp", colc[:], I["c_col"], writes=[b_colc], sem=csem)
    k.ident, k.b_ident, k.colc, k.b_colc, k.csem = ident, b_ident, colc, b_colc, csem
    k.ssq = sb("ssq", [128, 40], F32)
    k.b_ssq = Buf("ssq")

    phase_adaln(k)
    P.barrier()
    if debug is None or debug.get("_upto", 99) >= 1:
        phase_norm_inproj(k, debug)
        P.barrier()
    if (debug is None or debug.get("_upto", 99) >= 2) and not (debug or {}).get("_skip_ssm"):
        phase_ssm(k)
        P.barrier()
    if debug is None or debug.get("_upto", 99) >= 3:
        phase_attn(k)
        P.barrier()
    if debug is None or debug.get("_upto", 99) >= 4:
        phase_merge(k)
        P.barrier()
    P.emit()
    st.close()
    return nc, dbg_outs


def range_sin(k, stack, out_ap, y_ap, shape, tag, rbufs, wbufs, eng="dve"):
    nc, P = k.nc, k.P
    ki = k.sb(tag + "_ki", shape, I32, stack)
    kf = k.sb(tag + "_kf", shape, F32, stack)
    g = k.sb(tag + "_g", shape, F32, stack)
    bki, bkf, bg = Buf(tag + "ki"), Buf(tag + "kf"), Buf(tag + "g")
    sl = tuple([slice(None)] * len(shape))
    P.op(eng, lambda e: e.tensor_copy(out=ki[sl], in_=y_ap), reads=rbufs, writes=[bki])
    P.op(eng, lambda e: e.tensor_copy(out=kf[sl], in_=ki[sl]), reads=[bki], writes=[bkf])
    P.op(eng, lambda e: e.tensor_tensor(out=kf[sl], in0=y_ap, in1=kf[sl], op=ALU.subtract), reads=rbufs + [bkf], writes=[bkf])
    P.op(eng, lambda e: e.tensor_single_scalar(out=g[sl], in_=kf[sl], scalar=0.5, op=ALU.is_gt), reads=[bkf], writes=[bg])
    P.op(eng, lambda e: e.tensor_tensor(out=kf[sl], in0=kf[sl], in1=g[sl], op=ALU.subtract), reads=[bkf, bg], writes=[bkf])
    P.op(eng, lambda e: e.tensor_single_scalar(out=g[sl], in_=kf[sl], scalar=-0.5, op=ALU.is_lt), reads=[bkf], writes=[bg])
    P.op(eng, lambda e: e.tensor_tensor(out=kf[sl], in0=kf[sl], in1=g[sl], op=ALU.add), reads=[bkf, bg], writes=[bkf])
    P.op("act", lambda e: e.activation(out=out_ap, in_=kf[sl], func=AF.Sin, scale=TWO_PI * (1.0 - 2e-7)), reads=[bkf], writes=wbufs)


def phase_adaln(k):
    nc, P, I, S = k.nc, k.P, k.I, k.S
    with ExitStack() as ls:
        sT = k.sb("ad_sT", [128, NKC, 2], F32, ls)
        b_sT = Buf("sT")
        sem_c = P.new_dsem("ad_c")
        for v in range(2):
            P.dma("sp", sT[:, :, v], I["cc"][v].rearrange("(kc p) -> p kc", p=128), writes=[b_sT], sem=sem_c,
                  allow_slow_non_contiguous=True)
        P.op("act", lambda e: e.activation(out=sT[:], in_=sT[:], func=AF.Silu), reads=[b_sT], writes=[b_sT])
        brow = k.sb("ad_brow", [2, 3 * D], F32, ls)
        b_brow = Buf("brow")
        for v in range(2):
            P.dma("sp", brow[v:v + 1, :], I["b_ada"], writes=[b_brow], sem=sem_c)
        modrow = k.sb("ad_modrow", [2, 3 * D], F32, ls)
        b_modrow = Buf("modrow")
        NS = 2
        wst = [k.sb(f"ad_w{i}", [128, NKC, 512], F32, ls) for i in range(NS)]
        b_w = [Buf(f"adw{i}") for i in range(NS)]
        wsem = [P.new_dsem(f"ad_ws{i}") for i in range(NS)]
        pst = [k.ps(f"ad_ps{i}", [128, 512], F32, ls) for i in range(2)]
        b_ps = [Buf(f"adps{i}") for i in range(2)]
        wv = I["w_ada"].rearrange("(kc p) c -> p kc c", p=128)
        xs1 = [k.sb(f"ad_x{i}", [128, D], F32, ls) for i in range(2)]
        b_xs1 = [Buf(f"adx{i}") for i in range(2)]
        junk1 = k.sb("ad_junk", [128, D], BF16, ls)
        b_junk1 = Buf("adjunk")
        tiles_done = 0

        def ss_tile(t):
            s1 = t % 2
            src = I["x"][t * 128:(t + 1) * 128, :] if t < 16 else I["ctx"][(t - 16) * 128:(t - 15) * 128, :]
            P.dma("pool", xs1[s1][:], src, writes=[b_xs1[s1]])
            P.op("act", lambda e, s1=s1, t=t: e.activation(out=junk1[:], in_=xs1[s1][:], func=AF.Square, accum_out=k.ssq[:, t:t + 1]),
                 reads=[b_xs1[s1]], writes=[b_junk1, k.b_ssq])

        for cb in range(12):
            for _ in range(2 if cb < 6 else 1):
                if tiles_done < 18:
                    ss_tile(tiles_done)
                    tiles_done += 1
            s = cb % NS
            P.dma("sp", wst[s][:, 0:8, :], wv[:, 0:8, cb * 512:(cb + 1) * 512], writes=[b_w[s]], sem=wsem[s])
            P.dma("act", wst[s][:, 8:16, :], wv[:, 8:16, cb * 512:(cb + 1) * 512], writes=[b_w[s]], sem=wsem[s])
            pt, bp = pst[cb % 2], b_ps[cb % 2]
            for kc in range(NKC):
                P.op("pe", lambda e, kc=kc, s=s, pt=pt: e.matmul(pt[0:2, :], lhsT=sT[:, kc, :], rhs=wst[s][:, kc, :],
                                                              start=(kc == 0), stop=(kc == NKC - 1)),
                     reads=[b_sT, b_w[s]], writes=[bp])
            P.op("dve", lambda e, cb=cb, pt=pt: e.tensor_tensor(out=modrow[:, cb * 512:(cb + 1) * 512], in0=pt[0:2, :],
                                                             in1=brow[:, cb * 512:(cb + 1) * 512], op=ALU.add),
                 reads=[bp, b_brow], writes=[b_modrow])
        b_mr = Buf("modrow_d")
        k.b_modrow_d = b_mr
        P.dma("sp", S["modrow"], modrow[:], reads=[b_modrow], writes=[b_mr], sem=sem_c)


def phase_norm_inproj(k, debug):
    nc, P, I, S = k.nc, k.P, k.I, k.S
    with ExitStack() as ls:
        hT = k.sb("hT", [128, NKC, LT], BF16, ls)
        b_hT = [[Buf(f"hT{t}_{kc}") for kc in range(NKC)] for t in range(18)]
        Amod = k.sb("Amod", [128, NKC, 2], F32, ls)
        Smod = k.sb("Smod", [128, NKC, 2], F32, ls)
        gcol = k.sb("gcol", [128, NKC], F32, ls)
        b_A, b_S, b_g = Buf("Amod"), Buf("Smod"), Buf("gcol")
        msem = P.new_dsem("n_m")
        for v in range(2):
            P.dma("sp", Smod[:, :, v], S["modrow"][v, 0:D].rearrange("(kc p) -> p kc", p=128),
                  reads=[k.b_modrow_d], writes=[b_S], sem=msem, allow_slow_non_contiguous=True)
            P.dma("sp", Amod[:, :, v], S["modrow"][v, D:2 * D].rearrange("(kc p) -> p kc", p=128),
                  reads=[k.b_modrow_d], writes=[b_A], sem=msem, allow_slow_non_contiguous=True)
        P.dma("sp", gcol[:], I["norm_g"].rearrange("(kc p) -> p kc", p=128), writes=[b_g], sem=msem,
              allow_slow_non_contiguous=True)
        for v in range(2):
            P.op("dve", lambda e, v=v: e.scalar_tensor_tensor(out=Amod[:, :, v], in0=Amod[:, :, v], scalar=1.0, in1=gcol[:],
                                                             op0=ALU.add, op1=ALU.mult),
                 reads=[b_A, b_g], writes=[b_A])
        with ExitStack() as l1:
            NX = 2
            xt = [k.sb(f"n_x{i}", [128, D], F32, l1) for i in range(NX)]
            b_x = [Buf(f"nx{i}") for i in range(NX)]
            xsem = [P.new_dsem(f"n_xs{i}") for i in range(NX)]
            junk = k.sb("n_junk", [128, D], BF16, l1)
            b_junk = Buf("junk")
            stat = [k.sb(f"n_st{i}", [128, 4], F32, l1) for i in range(NX)]
            b_stat = [Buf(f"nst{i}") for i in range(NX)]
            pt = [k.ps(f"n_ps{i}", [128, 512], F32, l1) for i in range(4)]
            b_pt = [Buf(f"nps{i}") for i in range(4)]
            pi = 0
            P.op("dve", lambda e: e.tensor_scalar(out=k.ssq[:, 0:18], in0=k.ssq[:, 0:18], scalar1=1.0 / D, scalar2=EPS, op0=ALU.mult, op1=ALU.add),
                 reads=[k.b_ssq], writes=[k.b_ssq])
            P.op("act", lambda e: e.activation(out=k.ssq[:, 0:18], in_=k.ssq[:, 0:18], func=AF.Ln), reads=[k.b_ssq], writes=[k.b_ssq])
            P.op("act", lambda e: e.activation(out=k.ssq[:, 20:38], in_=k.ssq[:, 0:18], func=AF.Exp, scale=-0.5), reads=[k.b_ssq], writes=[k.b_ssq])
            for t in range(18):
                s = t % NX
                v = 0 if t < 16 else 1
                src = I["x"][t * 128:(t + 1) * 128, :] if t < 16 else I["ctx"][(t - 16) * 128:(t - 15) * 128, :]
                P.dma("sp", xt[s][:, 0:1024], src[:, 0:1024], writes=[b_x[s]], sem=xsem[s])
                P.dma("sp", xt[s][:, 1024:2048], src[:, 1024:2048], writes=[b_x[s]], sem=xsem[s])
                P.op("dve", lambda e, s=s, t=t: e.tensor_scalar(out=xt[s][:], in0=xt[s][:], scalar1=k.ssq[:, 20 + t:21 + t], scalar2=None,
                                                              op0=ALU.mult),
                     reads=[b_x[s], k.b_ssq], writes=[b_x[s]])
                for g4 in range(4):
                    p_, bp = pt[pi % 4], b_pt[pi % 4]
                    pi += 1
                    for j in range(4):
                        kc = g4 * 4 + j
                        P.op("pe", lambda e, s=s, kc=kc, j=j, p_=p_: e.transpose(out=p_[:, j * 128:(j + 1) * 128],
                                                                             in_=xt[s][:, kc * 128:(kc + 1) * 128],
                                                                             identity=k.ident[:]),
                             reads=[b_x[s], k.b_ident], writes=[bp])
                    for j in range(4):
                        kc = g4 * 4 + j
                        eng = "dve" if (g4 % 2 == 0) else "act"
                        if eng == "dve":
                            P.op("dve", lambda e, kc=kc, j=j, p_=p_, t=t, v=v: e.tensor_scalar(
                                out=hT[:, kc, t * 128:(t + 1) * 128], in0=p_[:, j * 128:(j + 1) * 128],
                                scalar1=Amod[:, kc, v:v + 1], scalar2=Smod[:, kc, v:v + 1], op0=ALU.mult, op1=ALU.add),
                                reads=[bp, b_A, b_S], writes=[b_hT[t][kc]])
                        else:
                            P.op("act", lambda e, kc=kc, j=j, p_=p_, t=t, v=v: e.activation(
                                out=hT[:, kc, t * 128:(t + 1) * 128], in_=p_[:, j * 128:(j + 1) * 128],
                                func=AF.Identity, scale=Amod[:, kc, v:v + 1], bias=Smod[:, kc, v:v + 1]),
                                reads=[bp, b_A, b_S], writes=[b_hT[t][kc]])
        if S["hT"] is not None:
            dsem = P.new_dsem("dbg")
            P.dma("sp", S["hT"], hT[:], reads=[b for row in b_hT for b in row], writes=[Buf("x")], sem=dsem)
        P.barrier()
        if debug is not None and debug.get("_upto", 99) < 1.5:
            return
        inproj(k, ls, hT, b_hT)


def inproj(k, ls, hT, b_hT):
    nc, P, I, S = k.nc, k.P, k.I, k.S
    cosT = k.sb("cosT", [128, L], F32, ls)
    sinS = k.sb("sinS", [128, L], F32, ls)
    perm = k.sb("perm", [128, 128], F32, ls)
    b_cos, b_sin, b_perm = Buf("cos"), Buf("sin"), Buf("perm")
    tsem = P.new_dsem("ip_t")
    P.dma("sp", perm[:], I["c_perm"], writes=[b_perm], sem=tsem)
    with ExitStack() as l0:
        pos = k.sb("pos", [128, L], F32, l0)
        yv = k.sb("yv", [128, L], F32, l0)
        inv = k.sb("inv", [128, 1], F32, l0)
        b_pos, b_y, b_inv = Buf("pos"), Buf("yv"), Buf("inv")
        P.dma("sp", pos[:], I["c_pos"], writes=[b_pos], sem=tsem)
        P.op("act", lambda e: e.activation(out=inv[:], in_=k.colc[:, 1:2], func=AF.Exp, scale=-math.log(10000.0)),
             reads=[k.b_colc], writes=[b_inv])
        P.op("dve", lambda e: e.tensor_scalar(out=yv[:], in0=pos[:], scalar1=inv[:, 0:1], scalar2=1.0 / TWO_PI,
                                              op0=ALU.mult, op1=ALU.mult), reads=[b_pos, b_inv], writes=[b_y])
        range_sin(k, l0, sinS[:], yv[:], [128, L], "rs1", [b_y], [b_sin])
        P.op("dve", lambda e: e.tensor_scalar(out=sinS[:], in0=sinS[:], scalar1=k.colc[:, 0:1], scalar2=None, op0=ALU.mult),
             reads=[b_sin, k.b_colc], writes=[b_sin])
        P.op("dve", lambda e: e.tensor_scalar(out=yv[:], in0=yv[:], scalar1=0.25, scalar2=None, op0=ALU.add),
             reads=[b_y], writes=[b_y])
        range_sin(k, l0, cosT[:], yv[:], [128, L], "rs2", [b_y], [b_cos])
        P.barrier()
    b_wbq = [[Buf(f"wbq{i}_{j}") for j in range(4)] for i in range(2)]
    wb = [k.sb(f"ip_wb{i}", [128, NKC, 512], BF16, ls) for i in range(2)]
    b_wb = [Buf(f"wb{i}") for i in range(2)]
    NOB = 4
    ob = [k.sb(f"ip_ob{i}", [128, 512], BF16, ls) for i in range(NOB)]
    b_ob = [Buf(f"ob{i}") for i in range(NOB)]
    osem = [P.new_dsem(f"ip_os{i}") for i in range(NOB)]
    NOF = 3
    of = [k.sb(f"ip_of{i}", [128, 512], F32, ls) for i in range(NOF)]
    b_of = [Buf(f"of{i}") for i in range(NOF)]
    fsem = [P.new_dsem(f"ip_fs{i}") for i in range(NOF)]
    t1 = [k.sb(f"ip_t1{i}", [128, 512], F32, ls) for i in range(2)]
    b_t1 = [Buf(f"t1{i}") for i in range(2)]
    t2 = [k.sb(f"ip_t2{i}", [128, 512], F32, ls) for i in range(2)]
    b_t2 = [Buf(f"t2{i}") for i in range(2)]
    pb = [k.ps(f"ip_ps{i}", [128, 512], F32, ls) for i in range(4)]
    b_pb = [Buf(f"ipps{i}") for i in range(4)]
    pr = [k.ps(f"ip_pr{i}", [128, 512], F32, ls) for i in range(2)]
    b_pr = [Buf(f"ippr{i}") for i in range(2)]
    wv = I["w_in"].rearrange("(kc p) c -> p kc c", p=128)
    cnt = {"pb": 0, "ob": 0, "of": 0, "r": 0, "ld": 0, "ev": 0}

    rope_pending = []

    def load_block(cb):
        s2 = cb % 2
        for q4 in range(4):
            P.dma("pool", wb[s2][:, q4 * 4:(q4 + 1) * 4, :], wv[:, q4 * 4:(q4 + 1) * 4, cb * 512:(cb + 1) * 512], writes=[b_wbq[s2][q4]])

    def next_ob():
        i = cnt["ob"] % NOB
        cnt["ob"] += 1
        return i

    def evac_eng():
        cnt["ev"] += 1
        return "act" if cnt["ev"] % 2 else "dve"

    def tiles_of(tok0, n, kc):
        return [b_hT[t][kc] for t in range(tok0 // 128, (tok0 + n) // 128)]

    def fm_unit(cb, fc, tok0, n, kind, row0, dst):
        s2 = cb % 2
        pi = cnt["pb"] % 4
        cnt["pb"] += 1
        pt, bp = pb[pi], b_pb[pi]
        for kc in range(NKC):
            P.op("pe", lambda e, kc=kc: e.matmul(pt[:, 0:n], lhsT=wb[s2][:, kc, fc * 128:(fc + 1) * 128],
                                                 rhs=hT[:, kc, tok0:tok0 + n], start=(kc == 0), stop=(kc == NKC - 1)),
                 reads=[b_wbq[s2][kc // 4]] + tiles_of(tok0, n, kc), writes=[bp])
        while rope_pending:
            rope_pending.pop(0)()
        oi = next_ob()
        if kind == "rope":
            ri = cnt["r"] % 2
            cnt["r"] += 1
            fi = cnt["of"] % NOF
            cnt["of"] += 1
            P.op("act", lambda e: e.activation(out=of[fi][:, 0:n], in_=pt[:, 0:n], func=AF.Copy), reads=[bp], writes=[b_of[fi]])
            P.op("dve", lambda e: e.tensor_tensor(out=t1[ri][:, 0:n], in0=of[fi][:, 0:n], in1=cosT[:, tok0:tok0 + n], op=ALU.mult),
                 reads=[b_of[fi], b_cos], writes=[b_t1[ri]])

            def fin():
                P.op("pe", lambda e: e.matmul(pr[ri][:, 0:n], lhsT=perm[:], rhs=of[fi][:, 0:n], start=True, stop=True),
                     reads=[b_perm, b_of[fi]], writes=[b_pr[ri]])
                P.op("dve", lambda e: e.tensor_tensor(out=t2[ri][:, 0:n], in0=pr[ri][:, 0:n], in1=sinS[:, tok0:tok0 + n], op=ALU.mult),
                     reads=[b_pr[ri], b_sin], writes=[b_t2[ri]])
                P.op("pool", lambda e: e.tensor_tensor(out=ob[oi][:, 0:n], in0=t1[ri][:, 0:n], in1=t2[ri][:, 0:n], op=ALU.add),
                     reads=[b_t1[ri], b_t2[ri]], writes=[b_ob[oi]])
                P.dma("sp", dst, ob[oi][:, 0:n], reads=[b_ob[oi]], sem=osem[oi])
            rope_pending.append(fin)
            return
        elif kind == "copy":
            eg = evac_eng()
            if eg == "act":
                P.op("act", lambda e: e.activation(out=ob[oi][:, 0:n], in_=pt[:, 0:n], func=AF.Copy), reads=[bp], writes=[b_ob[oi]])
            else:
                P.op("dve", lambda e: e.tensor_copy(out=ob[oi][:, 0:n], in_=pt[:, 0:n]), reads=[bp], writes=[b_ob[oi]])
        else:
            fn = AF.Silu if kind == "silu" else AF.Sigmoid
            P.op("act", lambda e: e.activation(out=ob[oi][:, 0:n], in_=pt[:, 0:n], func=fn), reads=[bp], writes=[b_ob[oi]])
        P.dma("sp", dst, ob[oi][:, 0:n], reads=[b_ob[oi]], sem=osem[oi])

    def tm_unit(cb, t, kind, dst):
        s2 = cb % 2
        pi = cnt["pb"] % 4
        cnt["pb"] += 1
        pt, bp = pb[pi], b_pb[pi]
        for kc in range(NKC):
            P.op("pe", lambda e, kc=kc: e.matmul(pt[:], lhsT=hT[:, kc, t * 128:(t + 1) * 128], rhs=wb[s2][:, kc, :],
                                                 start=(kc == 0), stop=(kc == NKC - 1)),
                 reads=[b_wbq[s2][kc // 4], b_hT[t][kc]], writes=[bp])
        while rope_pending:
            rope_pending.pop(0)()
        if kind == "f32":
            fi = cnt["of"] % NOF
            cnt["of"] += 1
            P.op("dve", lambda e: e.tensor_copy(out=of[fi][:], in_=pt[:]), reads=[bp], writes=[b_of[fi]])
            P.dma("sp", dst, of[fi][:], reads=[b_of[fi]], sem=fsem[fi])
            return
        oi = next_ob()
        if kind == "copy":
            eg = evac_eng()
            if eg == "act":
                P.op("act", lambda e: e.activation(out=ob[oi][:], in_=pt[:], func=AF.Copy), reads=[bp], writes=[b_ob[oi]])
            else:
                P.op("dve", lambda e: e.tensor_copy(out=ob[oi][:], in_=pt[:]), reads=[bp], writes=[b_ob[oi]])
        else:
            P.op("act", lambda e: e.activation(out=ob[oi][:], in_=pt[:], func=AF.Silu), reads=[bp], writes=[b_ob[oi]])
        P.dma("sp", dst, ob[oi][:], reads=[b_ob[oi]], sem=osem[oi])

    NCB = INW // 512
    load_block(0)
    for cb in range(NCB):
        if cb + 1 < NCB:
            load_block(cb + 1)
        c0 = cb * 512
        if cb < 2:
            for fc in range(4):
                h = cb * 4 + fc
                for tb in range(4):
                    fm_unit(cb, fc, tb * 512, 512, "rope", 0, S["qT"][h, :, tb * 512:(tb + 1) * 512])
        elif cb < 4:
            for fc in range(4):
                h = (cb - 2) * 4 + fc
                for tb in range(4):
                    fm_unit(cb, fc, tb * 512, 512, "rope", 0, S["kT"][h, :, tb * 512:(tb + 1) * 512])
                fm_unit(cb, fc, L, LC, "copy", 0, S["kT"][h, :, L:LT])
        elif cb < 6:
            for t in range(18):
                tm_unit(cb, t, "copy", S["v"][t * 128:(t + 1) * 128, (cb - 4) * 512:(cb - 3) * 512])
        elif cb < 8:
            for t in range(16):
                tm_unit(cb, t, "silu", S["sga"][t * 128:(t + 1) * 128, (cb - 6) * 512:(cb - 5) * 512])
        elif cb == 8:
            for t in range(18):
                tm_unit(cb, t, "f32", S["u"][t * 128:(t + 1) * 128, :])
        elif cb == 9:
            for fc in range(4):
                for tb in range(4):
                    fm_unit(cb, fc, tb * 512, 512, "silu", 0, S["sgsT"][fc * 128:(fc + 1) * 128, tb * 512:(tb + 1) * 512])
        else:
            for fc in range(4):
                r0 = (cb - 10) * 512 + fc * 128
                for tb in range(4):
                    fm_unit(cb, fc, tb * 512, 512, "sigm", 0, S["sgmT"][r0:r0 + 128, tb * 512:(tb + 1) * 512])


def phase_ssm(k):
    nc, P, I, S = k.nc, k.P, k.I, k.S
    MUL, ADD, SUB = ALU.mult, ALU.add, ALU.subtract
    with ExitStack() as ls:
        ToepT = k.sb("ss_toep", [128, 32, 128], BF16, ls)
        RCp = k.sb("ss_rcp", [128, 2, 2, 16, 256], BF16, ls)
        WT = k.sb("ss_wt", [128, 2, 16, 2, 128], BF16, ls)
        A8c = k.sb("ss_a8c", [128, 2, 16, 2], F32, ls)
        A8s = k.sb("ss_a8s", [128, 2, 16, 2], F32, ls)
        AKc = k.sb("ss_akc", [128, 2, 8, 16, 2], F32, ls)
        AKs = k.sb("ss_aks", [128, 2, 8, 16, 2], F32, ls)
        b_ak = Buf("ak")
        b_toep = [Buf(f"toep{g}") for g in range(32)]
        b_rcp = Buf("rcp")
        b_wtl = [[[Buf(f"wt{d}_{g2}_{ri}") for ri in range(2)] for g2 in range(16)] for d in range(2)]
        b_U = [Buf(f"U{g}") for g in range(32)]
        b_zbf = [Buf("zbf0"), Buf("zbf1")]
        b_ygT = Buf("ygT")
        b_a8 = Buf("a8")
        pbk = [k.ps(f"ss_ps{i}", [128, 512], F32, ls) for i in range(8)]
        b_pbk = [Buf(f"ssps{i}") for i in range(8)]
        pc = {"i": 0}

        def nb():
            i = pc["i"] % 8
            pc["i"] += 1
            return pbk[i], b_pbk[i]

        csem = P.new_dsem("ss_c")
        with ExitStack() as l0:
            lre = k.sb("ss_lre", [128, 32], F32, l0)
            lim = k.sb("ss_lim", [128, 32], F32, l0)
            dtt = k.sb("ss_dt", [128, 32], F32, l0)
            alog = k.sb("ss_alog", [128, 32], F32, l0)
            th = k.sb("ss_th", [128, 32], F32, l0)
            b_l, b_dt, b_al = Buf("lrelim"), Buf("dtt"), Buf("alogth")
            for gp in range(2):
                for d in range(2):
                    P.dma("sp", lre[gp * 64:(gp + 1) * 64, d * 16:(d + 1) * 16], I["ssm_lre"][d, gp * 16:(gp + 1) * 16, :].rearrange("g p -> p g"),
                          writes=[b_l], sem=csem, allow_slow_non_contiguous=True)
                    P.dma("sp", lim[gp * 64:(gp + 1) * 64, d * 16:(d + 1) * 16], I["ssm_lim"][d, gp * 16:(gp + 1) * 16, :].rearrange("g p -> p g"),
                          writes=[b_l], sem=csem, allow_slow_non_contiguous=True)
                    P.dma("sp", dtt[gp * 64:(gp + 1) * 64, d * 16:(d + 1) * 16], I["ssm_ls"][d:d + 1, gp * 16:(gp + 1) * 16].broadcast_to([64, 16]),
                          writes=[b_dt], sem=csem)
            P.op("act", lambda e: e.activation(out=dtt[:], in_=dtt[:], func=AF.Exp), reads=[b_dt], writes=[b_dt])
            P.op("dve", lambda e: e.tensor_tensor(out=alog[:], in0=lre[:], in1=dtt[:], op=MUL), reads=[b_l, b_dt], writes=[b_al])
            P.op("dve", lambda e: e.scalar_tensor_tensor(out=th[:], in0=lim[:], scalar=1.0 / TWO_PI, in1=dtt[:], op0=MUL, op1=MUL),
                 reads=[b_l, b_dt], writes=[b_al])
            tabs = {}
            for nm, n in (("A", 9), ("B", 8), ("C", 8)):
                tau = k.sb(f"ss_tau{nm}", [128, 32, n], F32, l0)
                ex = k.sb(f"ss_ex{nm}", [128, 32, n], F32, l0)
                yv = k.sb(f"ss_yv{nm}", [128, 32, n], F32, l0)
                sn = k.sb(f"ss_sn{nm}", [128, 32, n], F32, l0)
                cs = k.sb(f"ss_cs{nm}", [128, 32, n], F32, l0)
                b_tau, b_ex, b_yv, b_sn, b_cs = Buf("tau" + nm), Buf("ex" + nm), Buf("yv" + nm), Buf("sn" + nm), Buf("cs" + nm)
                P.dma("sp", tau[:], I["c_tau" + nm], writes=[b_tau], sem=csem)
                P.op("dve", lambda e, ex=ex, tau=tau, n=n: e.tensor_tensor(out=ex[:], in0=tau[:], in1=alog[:, :, None].broadcast_to([128, 32, n]), op=MUL),
                     reads=[b_tau, b_al], writes=[b_ex])
                P.op("act", lambda e, ex=ex: e.activation(out=ex[:], in_=ex[:], func=AF.Exp), reads=[b_ex], writes=[b_ex])
                P.op("dve", lambda e, yv=yv, tau=tau, n=n: e.tensor_tensor(out=yv[:], in0=tau[:], in1=th[:, :, None].broadcast_to([128, 32, n]), op=MUL),
                     reads=[b_tau, b_al], writes=[b_yv])
                fl = lambda t: t[:].rearrange("p a b -> p (a b)")
                range_sin(k, l0, fl(sn), fl(yv), [128, 32 * n], "ssr1" + nm, [b_yv], [b_sn])
                P.op("dve", lambda e, yv=yv: e.tensor_scalar(out=yv[:], in0=yv[:], scalar1=0.25, scalar2=None, op0=ADD), reads=[b_yv], writes=[b_yv])
                range_sin(k, l0, fl(cs), fl(yv), [128, 32 * n], "ssr2" + nm, [b_yv], [b_cs])
                P.op("dve", lambda e, cs=cs, ex=ex: e.tensor_tensor(out=cs[:], in0=cs[:], in1=ex[:], op=MUL), reads=[b_cs, b_ex], writes=[b_cs])
                P.op("dve", lambda e, sn=sn, ex=ex: e.tensor_tensor(out=sn[:], in0=sn[:], in1=ex[:], op=MUL), reads=[b_sn, b_ex], writes=[b_sn])
                tabs[nm] = (cs, sn, b_cs, b_sn)
            ARA, AIA, b_ARA, b_AIA = tabs["A"]
            ARB, AIB, b_ARB, b_AIB = tabs["B"]
            ARC, AIC, b_ARC, b_AIC = tabs["C"]
            for d in range(2):
                dsl = slice(d * 16, (d + 1) * 16)
                for ri in range(2):
                    P.op("dve", lambda e, d=d, ri=ri, dsl=dsl: e.tensor_copy(out=AKc[:, d, :, :, ri], in_=ARC[:, dsl, :].rearrange("p g k -> p k g")),
                         reads=[b_ARC], writes=[b_ak])
                P.op("dve", lambda e, d=d, dsl=dsl: e.tensor_scalar(out=AKs[:, d, :, :, 0], in0=AIC[:, dsl, :].rearrange("p g k -> p k g"),
                                                                   scalar1=-1.0, scalar2=None, op0=MUL), reads=[b_AIC], writes=[b_ak])
                P.op("dve", lambda e, d=d, dsl=dsl: e.tensor_copy(out=AKs[:, d, :, :, 1], in_=AIC[:, dsl, :].rearrange("p g k -> p k g")),
                     reads=[b_AIC], writes=[b_ak])
            a1 = k.sb("ss_a1", [128, 2, 32], F32, l0)
            b_a1 = Buf("a1")
            for d in range(2):
                i8 = 8 if d == 0 else 0
                i1 = 1 if d == 0 else 7
                dsl = slice(d * 16, (d + 1) * 16)
                for ri in range(2):
                    P.op("dve", lambda e, d=d, ri=ri, i8=i8, dsl=dsl: e.tensor_copy(out=A8c[:, d, :, ri], in_=ARA[:, dsl, i8]), reads=[b_ARA], writes=[b_a8])
                P.op("dve", lambda e, d=d, i8=i8, dsl=dsl: e.tensor_scalar(out=A8s[:, d, :, 0], in0=AIA[:, dsl, i8], scalar1=-1.0, scalar2=None, op0=MUL),
                     reads=[b_AIA], writes=[b_a8])
                P.op("dve", lambda e, d=d, i8=i8, dsl=dsl: e.tensor_copy(out=A8s[:, d, :, 1], in_=AIA[:, dsl, i8]), reads=[b_AIA], writes=[b_a8])
                P.op("dve", lambda e, d=d, i1=i1, dsl=dsl: e.tensor_copy(out=a1[:, 0, dsl], in_=ARA[:, dsl, i1]), reads=[b_ARA], writes=[b_a1])
                P.op("dve", lambda e, d=d, i1=i1, dsl=dsl: e.tensor_copy(out=a1[:, 1, dsl], in_=AIA[:, dsl, i1]), reads=[b_AIA], writes=[b_a1])
            fz = k.sb("ss_fz", [128, 6, 32], F32, l0)
            b_fz = Buf("fz")
            P.op("dve", lambda e: e.tensor_tensor(out=fz[:, 0, :], in0=lre[:], in1=lre[:], op=MUL), reads=[b_l], writes=[b_fz])
            P.op("dve", lambda e: e.tensor_tensor(out=fz[:, 1, :], in0=lim[:], in1=lim[:], op=MUL), reads=[b_l], writes=[b_fz])
            P.op("dve", lambda e: e.tensor_tensor(out=fz[:, 0, :], in0=fz[:, 0, :], in1=fz[:, 1, :], op=ADD), reads=[b_fz], writes=[b_fz])
            P.op("dve", lambda e: e.reciprocal(out=fz[:, 1, :], in_=fz[:, 0, :]), reads=[b_fz], writes=[b_fz])
            P.op("dve", lambda e: e.tensor_scalar(out=fz[:, 0, :], in0=a1[:, 0, :], scalar1=-1.0, scalar2=None, op0=ADD), reads=[b_a1], writes=[b_fz])
            P.op("dve", lambda e: e.tensor_tensor(out=fz[:, 2, :], in0=fz[:, 0, :], in1=lre[:], op=MUL), reads=[b_fz, b_l], writes=[b_fz])
            P.op("dve", lambda e: e.tensor_tensor(out=fz[:, 3, :], in0=a1[:, 1, :], in1=lim[:], op=MUL), reads=[b_a1, b_l], writes=[b_fz])
            P.op("dve", lambda e: e.tensor_tensor(out=fz[:, 2, :], in0=fz[:, 2, :], in1=fz[:, 3, :], op=ADD), reads=[b_fz], writes=[b_fz])
            P.op("dve", lambda e: e.tensor_tensor(out=fz[:, 2, :], in0=fz[:, 2, :], in1=fz[:, 1, :], op=MUL), reads=[b_fz], writes=[b_fz])
            P.op("dve", lambda e: e.tensor_tensor(out=fz[:, 4, :], in0=a1[:, 1, :], in1=lre[:], op=MUL), reads=[b_a1, b_l], writes=[b_fz])
            P.op("dve", lambda e: e.tensor_tensor(out=fz[:, 5, :], in0=fz[:, 0, :], in1=lim[:], op=MUL), reads=[b_fz, b_l], writes=[b_fz])
            P.op("dve", lambda e: e.tensor_tensor(out=fz[:, 4, :], in0=fz[:, 4, :], in1=fz[:, 5, :], op=SUB), reads=[b_fz], writes=[b_fz])
            P.op("dve", lambda e: e.tensor_tensor(out=fz[:, 4, :], in0=fz[:, 4, :], in1=fz[:, 1, :], op=MUL), reads=[b_fz], writes=[b_fz])
            BT = k.sb("ss_BT", [128, 2, 2, 16, 16], F32, l0)
            BB = k.sb("ss_BB", [128, 2, 2, 16, 16], F32, l0)
            CN = k.sb("ss_CN", [128, 2, 2, 2, 128], F32, l0)
            CT = k.sb("ss_CT", [128, 2, 2, 16, 16], F32, l0)
            tA = k.sb("ss_tA", [128, 16, 9, 16], F32, l0)
            tB = k.sb("ss_tB", [128, 16, 9, 16], F32, l0)
            b_BT, b_BB, b_CN, b_CT, b_tA, b_tB = Buf("BT"), Buf("BB"), Buf("CN"), Buf("CT"), Buf("tA"), Buf("tB")
            for d in range(2):
                for ri in range(2):
                    bsrc = I["ssm_bre"] if ri == 0 else I["ssm_bim"]
                    csrc = I["ssm_cre"] if ri == 0 else I["ssm_cim"]
                    for gp in range(2):
                        P.dma("sp", BT[gp * 64:(gp + 1) * 64, d, ri, :, :], bsrc[d, gp * 16:(gp + 1) * 16].rearrange("g p c -> p g c"),
                              writes=[b_BT], sem=csem)
                        for blk in range(2):
                            g0 = gp * 16 + blk * 8
                            P.dma("sp", CN[:, d, ri, blk, gp * 64:(gp + 1) * 64], csrc[d, g0:g0 + 8].rearrange("g c p -> (g c) p"),
                                  writes=[b_CN], sem=csem)
            for d in range(2):
                for ri in range(2):
                    for blk in range(2):
                        pt, bp = nb()
                        P.op("pe", lambda e, d=d, ri=ri, blk=blk, pt=pt: e.transpose(out=pt[:, 0:128], in_=CN[:, d, ri, blk, :], identity=k.ident[:]),
                             reads=[b_CN, k.b_ident], writes=[bp])
                        P.op("dve", lambda e, d=d, ri=ri, blk=blk, pt=pt: e.tensor_copy(
                            out=CT[:, d, ri, blk * 8:(blk + 1) * 8, :].rearrange("p a b -> p (a b)"), in_=pt[:, 0:128]), reads=[bp], writes=[b_CT])
            for d in range(2):
                dsl = slice(d * 16, (d + 1) * 16)
                frb = lambda d=d, dsl=dsl: fz[:, 2, dsl][:, :, None].broadcast_to([128, 16, 16])
                fib = lambda d=d, dsl=dsl: fz[:, 4, dsl][:, :, None].broadcast_to([128, 16, 16])
                t16a = tA[:, :, 0, :]
                t16b = tB[:, :, 0, :]
                P.op("dve", lambda e, d=d, frb=frb: e.tensor_tensor(out=t16a, in0=BT[:, d, 0], in1=frb(), op=MUL), reads=[b_BT, b_fz], writes=[b_tA])
                P.op("dve", lambda e, d=d, fib=fib: e.tensor_tensor(out=t16b, in0=BT[:, d, 1], in1=fib(), op=MUL), reads=[b_BT, b_fz], writes=[b_tB])
                P.op("dve", lambda e, d=d: e.tensor_tensor(out=BB[:, d, 0], in0=t16a, in1=t16b, op=SUB), reads=[b_tA, b_tB], writes=[b_BB])
                P.op("dve", lambda e, d=d, frb=frb: e.tensor_tensor(out=t16a, in0=BT[:, d, 1], in1=frb(), op=MUL), reads=[b_BT, b_fz], writes=[b_tA])
                P.op("dve", lambda e, d=d, fib=fib: e.tensor_tensor(out=t16b, in0=BT[:, d, 0], in1=fib(), op=MUL), reads=[b_BT, b_fz], writes=[b_tB])
                P.op("dve", lambda e, d=d: e.tensor_tensor(out=BB[:, d, 1], in0=t16a, in1=t16b, op=ADD), reads=[b_tA, b_tB], writes=[b_BB])
            P.op("pool", lambda e: e.memset(RCp[:].rearrange("p a b c d -> p (a b c d)"), 0.0), writes=[b_rcp])
            for d in range(2):
                dsl = slice(d * 16, (d + 1) * 16)
                off = 112 if d == 0 else 0
                bc_c = lambda ri, d=d: CT[:, d, ri][:, :, None, :].broadcast_to([128, 16, 9, 16])
                bc_ar = lambda dsl=dsl: ARA[:, dsl, :][:, :, :, None].broadcast_to([128, 16, 9, 16])
                bc_ai = lambda dsl=dsl: AIA[:, dsl, :][:, :, :, None].broadcast_to([128, 16, 9, 16])
                dst = lambda ri, d=d, off=off: RCp[:, d, ri, :, off:off + 144].rearrange("p g (t c) -> p g t c", c=16)
                P.op("dve", lambda e, bc_c=bc_c, bc_ar=bc_ar: e.tensor_tensor(out=tA[:], in0=bc_c(0), in1=bc_ar(), op=MUL), reads=[b_CT, b_ARA], writes=[b_tA])
                P.op("dve", lambda e, bc_c=bc_c, bc_ai=bc_ai: e.tensor_tensor(out=tB[:], in0=bc_c(1), in1=bc_ai(), op=MUL), reads=[b_CT, b_AIA], writes=[b_tB])
                P.op("dve", lambda e, dst=dst: e.tensor_tensor(out=dst(0), in0=tA[:], in1=tB[:], op=SUB), reads=[b_tA, b_tB], writes=[b_rcp])
                P.op("dve", lambda e, bc_c=bc_c, bc_ai=bc_ai: e.tensor_tensor(out=tA[:], in0=bc_c(0), in1=bc_ai(), op=MUL), reads=[b_CT, b_AIA], writes=[b_tA])
                P.op("dve", lambda e, bc_c=bc_c, bc_ar=bc_ar: e.tensor_tensor(out=tB[:], in0=bc_c(1), in1=bc_ar(), op=MUL), reads=[b_CT, b_ARA], writes=[b_tB])
                P.op("dve", lambda e: e.tensor_tensor(out=tA[:], in0=tA[:], in1=tB[:], op=ADD), reads=[b_tA, b_tB], writes=[b_tA])
                P.op("dve", lambda e, dst=dst: e.tensor_scalar(out=dst(1), in0=tA[:], scalar1=-1.0, scalar2=None, op0=MUL), reads=[b_tA], writes=[b_rcp])
            Lp = k.sb("ss_Lp", [128, 64, 240], BF16, l0)
            b_Lp = Buf("Lp")
            P.op("pool", lambda e: e.memset(Lp[:].rearrange("p a b -> p (a b)"), 0.0), writes=[b_Lp])
            P.op("pool", lambda e: e.tensor_copy(out=Lp[:, :, 112:128], in_=BB[:].rearrange("p a b c d -> p (a b c) d")), reads=[b_BB], writes=[b_Lp])
            BW = k.sb("ss_BW", [128, 2, 2, 16, 128], F32, l0)
            b_BW = Buf("BW")
            for d in range(2):
                dsl = slice(d * 16, (d + 1) * 16)
                bc_b = lambda ri, d=d: BB[:, d, ri][:, :, None, :].broadcast_to([128, 16, 8, 16])
                bc_ar = lambda dsl=dsl: ARB[:, dsl, :][:, :, :, None].broadcast_to([128, 16, 8, 16])
                bc_ai = lambda dsl=dsl: AIB[:, dsl, :][:, :, :, None].broadcast_to([128, 16, 8, 16])
                dst = lambda ri, d=d: BW[:, d, ri].rearrange("p g (t c) -> p g t c", c=16)
                ta8 = tA[:, :, 0:8, :]
                tb8 = tB[:, :, 0:8, :]
                P.op("dve", lambda e, bc_b=bc_b, bc_ar=bc_ar: e.tensor_tensor(out=ta8, in0=bc_b(0), in1=bc_ar(), op=MUL), reads=[b_BB, b_ARB], writes=[b_tA])
                P.op("dve", lambda e, bc_b=bc_b, bc_ai=bc_ai: e.tensor_tensor(out=tb8, in0=bc_b(1), in1=bc_ai(), op=MUL), reads=[b_BB, b_AIB], writes=[b_tB])
                P.op("dve", lambda e, dst=dst: e.tensor_tensor(out=dst(0), in0=ta8, in1=tb8, op=SUB), reads=[b_tA, b_tB], writes=[b_BW])
                P.op("dve", lambda e, bc_b=bc_b, bc_ai=bc_ai: e.tensor_tensor(out=ta8, in0=bc_b(0), in1=bc_ai(), op=MUL), reads=[b_BB, b_AIB], writes=[b_tA])
                P.op("dve", lambda e, bc_b=bc_b, bc_ar=bc_ar: e.tensor_tensor(out=tb8, in0=bc_b(1), in1=bc_ar(), op=MUL), reads=[b_BB, b_ARB], writes=[b_tB])
                P.op("dve", lambda e, dst=dst: e.tensor_tensor(out=dst(1), in0=ta8, in1=tb8, op=ADD), reads=[b_tA, b_tB], writes=[b_BW])
            for d in range(2):
                for g2 in range(16):
                    for ri in range(2):
                        pt, bp = nb()
                        P.op("pe", lambda e, d=d, g2=g2, ri=ri, pt=pt: e.transpose(out=pt[:, 0:128], in_=BW[:, d, ri, g2, :], identity=k.ident[:]),
                             reads=[b_BW, k.b_ident], writes=[bp])
                        eng = "act" if (g2 + ri) % 2 else "dve"
                        if eng == "act":
                            P.op("act", lambda e, d=d, g2=g2, ri=ri, pt=pt: e.activation(out=WT[:, d, g2, ri, :], in_=pt[:, 0:128], func=AF.Copy), reads=[bp], writes=[b_wtl[d][g2][ri]])
                        else:
                            P.op("dve", lambda e, d=d, g2=g2, ri=ri, pt=pt: e.tensor_copy(out=WT[:, d, g2, ri, :], in_=pt[:, 0:128]), reads=[bp], writes=[b_wtl[d][g2][ri]])
            for g2 in range(16):
                for gp in range(2):
                    g = gp * 16 + g2
                    pt, bp = nb()
                    psl = slice(gp * 64, (gp + 1) * 64)
                    n = 0
                    for d in range(2):
                        for ri in range(2):
                            for s_ in range(8):
                                w0 = (7 - s_) * 16 if d == 0 else (8 - s_) * 16
                                l0_ = (7 - s_) * 16
                                P.op("pe", lambda e, d=d, ri=ri, g2=g2, w0=w0, l0_=l0_, psl=psl, pt=pt, n=n: e.matmul(
                                    pt[:, 0:128], lhsT=Lp[psl, (d * 2 + ri) * 16 + g2, l0_:l0_ + 128], rhs=RCp[psl, d, ri, g2, w0:w0 + 128],
                                    start=(n == 0), stop=(n == 31)), reads=[b_Lp, b_rcp], writes=[bp])
                                n += 1
                    P.op("dve" if g % 2 else "act",
                         (lambda e, g=g, pt=pt: e.tensor_copy(out=ToepT[:, g, :], in_=pt[:, 0:128])) if g % 2 else
                         (lambda e, g=g, pt=pt: e.activation(out=ToepT[:, g, :], in_=pt[:, 0:128], func=AF.Copy)),
                         reads=[bp], writes=[b_toep[g]])
            P.barrier()
        Ubuf = k.sb("ss_ubuf", [128, 32, 320], BF16, ls)
        Zbf = k.sb("ss_zbf", [128, 2, 16, 2, 288], BF16, ls)
        if k.debug.get("_ssm_upto", 99) < 1:
            return
        with ExitStack() as l1:
            ucm = [k.sb(f"ss_ucm{i}", [128, 8, 512], F32, l1) for i in range(2)]
            b_ucm = [Buf(f"ucm{i}") for i in range(2)]
            usem = [P.new_dsem(f"ss_us{i}") for i in range(2)]
            ucg = k.sb("ss_ucg", [128, 32, 128], F32, l1)
            b_ucg = Buf("ucg")
            for jt in range(3):
                si = jt % 2
                nj = 128 if jt < 2 else 32
                r0 = jt * 1024
                P.dma("sp", ucm[si][0:nj], S["u"][r0:r0 + nj * 8, :].rearrange("(j s) c -> j s c", s=8), writes=[b_ucm[si]], sem=usem[si])
                P.op("dve", lambda e, si=si, nj=nj: e.tensor_copy(out=ucg[0:nj].rearrange("p g (s c) -> p g s c", c=16),
                                                                 in_=ucm[si][0:nj].rearrange("p s (g c) -> p g s c", c=16)),
                     reads=[b_ucm[si]], writes=[b_ucg])
                for g0 in range(0, 32, 4):
                    pt, bp = nb()
                    for gg in range(4):
                        g = g0 + gg
                        P.op("pe", lambda e, si=si, nj=nj, g=g, gg=gg, pt=pt: e.transpose(
                            out=pt[:, gg * 128:gg * 128 + nj], in_=ucg[0:nj, g, :], identity=k.ident[0:nj, 0:nj]),
                            reads=[b_ucg, k.b_ident], writes=[bp])
                    src = lambda pt=pt, nj=nj: pt[:].rearrange("p (a b) -> p a b", b=128)[:, :, 0:nj]
                    cols = [32 + jt * 128] if jt < 2 else [0, 288]
                    for ci, c0 in enumerate(cols):
                        eng = "act" if (g0 // 4 + ci) % 2 else "dve"
                        if eng == "act":
                            P.op("act", lambda e, g0=g0, c0=c0, nj=nj, src=src: e.activation(out=Ubuf[:, g0:g0 + 4, c0:c0 + nj], in_=src(), func=AF.Copy),
                                 reads=[bp], writes=[b_U[g0 + i] for i in range(4)])
                        else:
                            P.op("dve", lambda e, g0=g0, c0=c0, nj=nj, src=src: e.tensor_copy(out=Ubuf[:, g0:g0 + 4, c0:c0 + nj], in_=src()),
                                 reads=[bp], writes=[b_U[g0 + i] for i in range(4)])
            P.barrier()
        if k.debug.get("_ssm_upto", 99) < 2:
            return
        with ExitStack() as l2:
            Z = [k.sb(f"ss_Z{d}", [128, 16, 2, 288], F32, l2) for d in range(2)]
            b_Z = [Buf("Z0"), Buf("Z1")]
            b_Ze = [[Buf(f"Ze{d}_{i}") for i in range(32)] for d in range(2)]
            zjoin = {0: True, 1: True}
            for d in range(2):
                j0 = 0 if d == 0 else 32
                for g2 in range(16):
                    for ri in range(2):
                        pt, bp = nb()
                        for gp in range(2):
                            P.op("pe", lambda e, d=d, g2=g2, ri=ri, gp=gp, pt=pt, j0=j0: e.matmul(
                                pt[gp * 64:(gp + 1) * 64, 0:288], lhsT=WT[:, d, g2, ri, gp * 64:(gp + 1) * 64], rhs=Ubuf[:, gp * 16 + g2, j0:j0 + 288],
                                start=True, stop=True), reads=[b_wtl[d][g2][ri], b_U[gp * 16 + g2]], writes=[bp])
                        if (g2 + ri) % 2:
                            P.op("act", lambda e, d=d, g2=g2, ri=ri, pt=pt: e.activation(out=Z[d][:, g2, ri, :], in_=pt[:, 0:288], func=AF.Copy), reads=[bp], writes=[b_Ze[d][g2 * 2 + ri]])
                        else:
                            P.op("dve", lambda e, d=d, g2=g2, ri=ri, pt=pt: e.tensor_copy(out=Z[d][:, g2, ri, :], in_=pt[:, 0:288]), reads=[bp], writes=[b_Ze[d][g2 * 2 + ri]])
            k.dbg("V_dbg", [2, 128, 16 * 2 * 288], F32, lambda dd: (dd[0], Z[0][:].rearrange("p a b c -> p (a b c)")), b_Ze[0])
            k.dbg("V_dbg", [2, 128, 16 * 2 * 288], F32, lambda dd: (dd[1], Z[1][:].rearrange("p a b c -> p (a b c)")), [b_Z[1]])
            m1 = [k.sb(f"ss_m1{d}", [128, 16, 2, 36], F32, l2) for d in range(2)]
            m2 = [k.sb(f"ss_m2{d}", [128, 16, 2, 36], F32, l2) for d in range(2)]
            b_m1 = [Buf("m10"), Buf("m11")]
            b_m2 = [Buf("m20"), Buf("m21")]
            Zv = [Z[d][:].rearrange("p g r (K c) -> p g r K c", c=8) for d in range(2)]

            def cmul_add(eng, d, kidx, dst, src, src_sw, nK):
                if nK:
                    ac = AKc[:, d, kidx][:, :, :, None].broadcast_to([128, 16, 2, nK])
                    as_ = AKs[:, d, kidx][:, :, :, None].broadcast_to([128, 16, 2, nK])
                    t1_, t2_ = m1[d][:, :, :, 0:nK], m2[d][:, :, :, 0:nK]
                else:
                    ac, as_ = AKc[:, d, kidx], AKs[:, d, kidx]
                    t1_, t2_ = m1[d][:, :, :, 0], m2[d][:, :, :, 0]
                extra = []
                if zjoin[d]:
                    zjoin[d] = False
                    extra = list(b_Ze[d])
                P.op(eng, lambda e: e.tensor_tensor(out=t1_, in0=src, in1=ac, op=MUL), reads=[b_Z[d], b_ak] + extra, writes=[b_m1[d]])
                P.op(eng, lambda e: e.tensor_tensor(out=t2_, in0=src_sw, in1=as_, op=MUL), reads=[b_Z[d], b_ak], writes=[b_m2[d]])
                P.op(eng, lambda e: e.tensor_tensor(out=t1_, in0=t1_, in1=t2_, op=ADD), reads=[b_m1[d], b_m2[d]], writes=[b_m1[d]])
                P.op(eng, lambda e: e.tensor_tensor(out=dst, in0=dst, in1=t1_, op=ADD), reads=[b_Z[d], b_m1[d]], writes=[b_Z[d]])

            for i in range(1, 8):
                r = i
                cmul_add("dve", 0, 0, Zv[0][:, :, :, :, r], Zv[0][:, :, :, :, r - 1], Zv[0][:, :, ::-1, :, r - 1], 36)
                r = 7 - i
                cmul_add("dve", 1, 0, Zv[1][:, :, :, :, r], Zv[1][:, :, :, :, r + 1], Zv[1][:, :, ::-1, :, r + 1], 36)
            for i in range(1, 36):
                K = i
                cmul_add("dve", 0, 7, Zv[0][:, :, :, K, 7], Zv[0][:, :, :, K - 1, 7], Zv[0][:, :, ::-1, K - 1, 7], 0)
                K = 35 - i
                cmul_add("dve", 1, 7, Zv[1][:, :, :, K, 0], Zv[1][:, :, :, K + 1, 0], Zv[1][:, :, ::-1, K + 1, 0], 0)
            for i in range(7):
                r = i
                cmul_add("dve", 0, r, Zv[0][:, :, :, 1:36, r], Zv[0][:, :, :, 0:35, 7], Zv[0][:, :, ::-1, 0:35, 7], 35)
                r = 7 - i
                cmul_add("dve", 1, 7 - r, Zv[1][:, :, :, 0:35, r], Zv[1][:, :, :, 1:36, 0], Zv[1][:, :, ::-1, 1:36, 0], 35)
            for d in range(2):
                eng = "dve" if d == 0 else "pool"
                P.op(eng, lambda e, d=d: e.tensor_copy(out=Zbf[:, d].rearrange("p a b c -> p (a b c)"), in_=Z[d][:].rearrange("p a b c -> p (a b c)")),
                     reads=[b_Z[d]], writes=[b_zbf[d]])
            k.dbg("Z_dbg", [2, 128, 16 * 2 * 288], F32, lambda dd: (dd[0], Z[0][:].rearrange("p a b c -> p (a b c)")), [b_Z[0]])
            k.dbg("Z_dbg", [2, 128, 16 * 2 * 288], F32, lambda dd: (dd[1], Z[1][:].rearrange("p a b c -> p (a b c)")), [b_Z[1]])
            P.barrier()
        if k.debug.get("_ssm_upto", 99) < 3:
            return
        ygT = k.sb("ss_ygT", [128, 4, L], BF16, ls)
        with ExitStack() as l3:
            ycm = k.sb("ss_ycm", [128, 8, 512], F32, l3)
            b_ycm = [Buf(f"ycm{g}") for g in range(32)]
            ut = k.sb("ss_ut", [128, 8, 512], F32, l3)
            b_ut = Buf("ut")
            utsem = P.new_dsem("ss_uts")
            Dfull = k.sb("ss_D", [128, 512], F32, l3)
            b_D = Buf("Dfull")
            P.dma("sp", Dfull[:], I["ssm_d"][0:1, :].broadcast_to([128, 512]), writes=[b_D], sem=csem)
            sq = [k.sb(f"ss_sq{i}", [128, 512], F32, l3) for i in range(8)]
            b_sq = [Buf(f"sq{i}") for i in range(8)]
            GC = math.sqrt(2.0 / math.pi)
            for jt in range(2):
                P.dma("sp", ut[:], S["u"][jt * 1024:(jt + 1) * 1024, :].rearrange("(j s) c -> j s c", s=8), writes=[b_ut], sem=utsem)
                for g in range(32):
                    gp, g2 = g // 16, g % 16
                    psl = slice(gp * 64, (gp + 1) * 64)
                    pt, bp = nb()
                    c0 = 32 + jt * 128
                    P.op("pe", lambda e, g=g, c0=c0, pt=pt: e.matmul(pt[:, 0:128], lhsT=Ubuf[:, g, c0:c0 + 128], rhs=ToepT[:, g, :], start=True, stop=False),
                         reads=[b_U[g], b_toep[g]], writes=[bp])
                    for d in range(2):
                        jz = (31 + jt * 128) if d == 0 else (1 + jt * 128)
                        w0 = 128 if d == 0 else 0
                        for ri in range(2):
                            last = (d == 1 and ri == 1)
                            P.op("pe", lambda e, d=d, ri=ri, g2=g2, psl=psl, jz=jz, w0=w0, pt=pt, last=last: e.matmul(
                                pt[:, 0:128], lhsT=Zbf[psl, d, g2, ri, jz:jz + 128], rhs=RCp[psl, d, ri, g2, w0:w0 + 128], start=False, stop=last),
                                reads=[b_zbf[d], b_rcp], writes=[bp])
                    src = lambda pt=pt: pt[:, 0:128].rearrange("p (t c) -> p t c", c=16)
                    P.op("dve", lambda e, g=g, src=src: e.tensor_tensor(out=ycm[:, :, g * 16:(g + 1) * 16], in0=ut[:, :, g * 16:(g + 1) * 16],
                                                                       in1=Dfull[:, g * 16:(g + 1) * 16][:, None, :].broadcast_to([128, 8, 16]), op=MUL),
                         reads=[b_ut, b_D], writes=[b_ycm[g]])
                    P.op("dve", lambda e, g=g, src=src: e.tensor_tensor(out=ycm[:, :, g * 16:(g + 1) * 16], in0=ycm[:, :, g * 16:(g + 1) * 16], in1=src(), op=ADD),
                         reads=[bp, b_ycm[g]], writes=[b_ycm[g]])
                k.dbg("y_dbg", [L, 512], F32, lambda dd, jt=jt: (dd[jt * 1024:(jt + 1) * 1024, :].rearrange("(j s) c -> j s c", s=8), ycm[:]), b_ycm)
                for t in range(8):
                    P.op("dve", lambda e, t=t: e.tensor_tensor(out=sq[t][:], in0=ycm[:, t, :], in1=ycm[:, t, :], op=MUL), reads=b_ycm, writes=[b_sq[t]])
                    P.op("dve", lambda e, t=t: e.tensor_scalar(out=sq[t][:], in0=sq[t][:], scalar1=0.044715, scalar2=1.0, op0=MUL, op1=ADD), reads=[b_sq[t]], writes=[b_sq[t]])
                    P.op("dve", lambda e, t=t: e.tensor_tensor(out=sq[t][:], in0=sq[t][:], in1=ycm[:, t, :], op=MUL), reads=[b_sq[t]] + b_ycm, writes=[b_sq[t]])
                for t in range(8):
                    P.op("act", lambda e, t=t: e.activation(out=sq[t][:], in_=sq[t][:], func=AF.Sigmoid, scale=2.0 * GC), reads=[b_sq[t]], writes=[b_sq[t]])
                for t in range(8):
                    P.op("dve", lambda e, t=t: e.tensor_tensor(out=sq[t][:], in0=sq[t][:], in1=ycm[:, t, :], op=MUL), reads=[b_sq[t]] + b_ycm, writes=[b_sq[t]])
                gbanks = []
                for t in range(8):
                    pt, bp = nb()
                    gbanks.append((pt, bp))
                    for chb in range(4):
                        P.op("pe", lambda e, t=t, chb=chb, pt=pt: e.transpose(out=pt[:, chb * 128:(chb + 1) * 128], in_=sq[t][:, chb * 128:(chb + 1) * 128], identity=k.ident[:]),
                             reads=[b_sq[t], k.b_ident], writes=[bp])
                for t in range(8):
                    pt, bp = gbanks[t]
                    tsl = slice(jt * 1024 + t, (jt + 1) * 1024, 8)
                    P.op("act", lambda e, pt=pt, tsl=tsl: e.activation(out=ygT[:, :, tsl], in_=pt[:].rearrange("p (a b) -> p a b", b=128), func=AF.Copy),
                         reads=[bp], writes=[b_ygT])
            P.barrier()
        if k.debug.get("_ssm_upto", 99) < 4:
            return
        with ExitStack() as l4:
            wg32 = k.sb("ss_wg32", [128, 4, 512], F32, l4)
            wg = k.sb("ss_wg", [128, 4, 512], BF16, l4)
            bg = k.sb("ss_bg", [128, 4], F32, l4)
            b_wg32, b_wg, b_bg = Buf("wg32"), Buf("wg"), Buf("bg")
            P.dma("sp", wg32[:], I["w_glu"].rearrange("(fc p) c -> p fc c", p=128), writes=[b_wg32], sem=csem)
            P.dma("sp", bg[:], I["b_glu"].rearrange("(fc p) -> p fc", p=128), writes=[b_bg], sem=csem, allow_slow_non_contiguous=True)
            P.op("dve", lambda e: e.tensor_copy(out=wg[:], in_=wg32[:]), reads=[b_wg32], writes=[b_wg])
            gst = [k.sb(f"ss_gst{i}", [128, 512], BF16, l4) for i in range(2)]
            b_gst = [Buf("gst0"), Buf("gst1")]
            gsem = [P.new_dsem(f"ss_gs{i}") for i in range(2)]
            sg = [k.sb(f"ss_sg{i}", [128, 512], F32, l4) for i in range(2)]
            b_sg = [Buf("sg0"), Buf("sg1")]
            so = [k.sb(f"ss_so{i}", [128, 512], BF16, l4) for i in range(2)]
            b_so = [Buf("so0"), Buf("so1")]
            sosem = [P.new_dsem(f"ss_sos{i}") for i in range(2)]
            ui = 0
            for fo in range(4):
                for tb in range(4):
                    i = ui % 2
                    ui += 1
                    tsl = slice(tb * 512, (tb + 1) * 512)
                    P.dma("sp", gst[i][:], S["sgsT"][fo * 128:(fo + 1) * 128, tsl], writes=[b_gst[i]], sem=gsem[i])
                    pt, bp = nb()
                    for fc in range(4):
                        P.op("pe", lambda e, fc=fc, fo=fo, tsl=tsl, pt=pt: e.matmul(pt[:], lhsT=wg[:, fc, fo * 128:(fo + 1) * 128], rhs=ygT[:, fc, tsl],
                                                                               start=(fc == 0), stop=(fc == 3)), reads=[b_wg, b_ygT], writes=[bp])
                    P.op("act", lambda e, i=i, fo=fo, pt=pt: e.activation(out=sg[i][:], in_=pt[:], func=AF.Sigmoid, bias=bg[:, fo:fo + 1]),
                         reads=[bp, b_bg], writes=[b_sg[i]])
                    P.op("dve", lambda e, i=i, fo=fo, tsl=tsl: e.tensor_tensor(out=sg[i][:], in0=sg[i][:], in1=ygT[:, fo, tsl], op=MUL),
                         reads=[b_sg[i], b_ygT], writes=[b_sg[i]])
                    P.op("dve", lambda e, i=i: e.tensor_tensor(out=so[i][:], in0=sg[i][:], in1=gst[i][:], op=MUL),
                         reads=[b_sg[i], b_gst[i]], writes=[b_so[i]])
                    P.dma("sp", S["sbrT"][fo * 128:(fo + 1) * 128, tsl], so[i][:], reads=[b_so[i]], sem=sosem[i])


def phase_attn(k):
    nc, P, I, S = k.nc, k.P, k.I, k.S
    with ExitStack() as ls:
        lamv = k.sb("at_lamv", [128, 4, 64], F32, ls)
        lw = k.sb("at_lw", [128, 8], F32, ls)
        G = k.sb("at_G", [128, 128], F32, ls)
        b_lamv, b_lw, b_G = Buf("lamv"), Buf("lw"), Buf("G")
        csem = P.new_dsem("at_c")
        P.dma("sp", lamv[:].rearrange("p a b -> p (a b)"), I["lam"].rearrange("a b -> (a b)").partition_broadcast(128),
              writes=[b_lamv], sem=csem)
        P.dma("sp", G[:], I["subln_g"][0:1, :].broadcast_to([128, 128]), writes=[b_G], sem=csem)
        P.op("dve", lambda e: e.tensor_scalar(out=G[:], in0=G[:], scalar1=(1.0 - LAM_INIT), scalar2=None, op0=ALU.mult),
             reads=[b_G], writes=[b_G])
        for i in range(2):
            P.op("dve", lambda e, i=i: e.tensor_tensor(out=lamv[:, 2 * i, :], in0=lamv[:, 2 * i, :], in1=lamv[:, 2 * i + 1, :], op=ALU.mult),
                 reads=[b_lamv], writes=[b_lamv])
            P.op("dve", lambda e, i=i: e.tensor_reduce(out=lw[:, i:i + 1], in_=lamv[:, 2 * i, :], axis=mybir.AxisListType.X, op=ALU.add),
                 reads=[b_lamv], writes=[b_lw])
        P.op("act", lambda e: e.activation(out=lw[:, 2:4], in_=lw[:, 0:2], func=AF.Exp), reads=[b_lw], writes=[b_lw])
        P.op("dve", lambda e: e.tensor_tensor(out=lw[:, 4:5], in0=lw[:, 3:4], in1=lw[:, 2:3], op=ALU.subtract), reads=[b_lw], writes=[b_lw])
        P.op("dve", lambda e: e.tensor_scalar(out=lw[:, 5:6], in0=lw[:, 4:5], scalar1=-LAM_INIT, scalar2=None, op0=ALU.add),
             reads=[b_lw], writes=[b_lw])
        neglam = lw[:, 5:6]
        qTs = [k.sb(f"at_q{i}", [128, L], BF16, ls) for i in range(2)]
        kTs = [k.sb(f"at_k{i}", [128, LT], BF16, ls) for i in range(2)]
        Vs = [k.sb(f"at_v{i}", [128, 18, 130], BF16, ls) for i in range(2)]
        gas = [k.sb(f"at_ga{i}", [128, 16, 128], BF16, ls) for i in range(2)]
        aTs = [k.sb(f"at_aT{i}", [128, L], BF16, ls) for i in range(2)]
        b_q = [Buf(f"atq{i}") for i in range(2)]
        b_k = [Buf(f"atk{i}") for i in range(2)]
        b_v = [Buf(f"atv{i}") for i in range(2)]
        b_ga = [Buf(f"atga{i}") for i in range(2)]
        b_aT = [Buf(f"ataT{i}") for i in range(2)]
        hsem = [P.new_dsem(f"at_h{i}") for i in range(2)]
        asem = [P.new_dsem(f"at_a{i}") for i in range(2)]
        for i in range(2):
            P.op("pool", lambda e, i=i: e.memset(Vs[i][:, :, 128:130], 1.0), writes=[b_v[i]])
        PT = [k.sb(f"at_pt{i}", [128, 2, 18, 256], BF16, ls) for i in range(2)]
        b_PT = [[[Buf(f"pt{i}_{c}_{kp}") for kp in range(9)] for c in range(2)] for i in range(2)]
        sbk = [k.ps(f"at_s{i}", [128, 512], F32, ls) for i in range(3)]
        b_sbk = [Buf(f"ats{i}") for i in range(3)]
        obk = [k.ps(f"at_o{i}", [128, 512], F32, ls) for i in range(4)]
        b_obk = [Buf(f"ato{i}") for i in range(4)]
        tbk = k.ps("at_t", [128, 512], F32, ls)
        b_tbk = Buf("att")
        sm = [k.sb(f"at_sm{i}", [128, 8], F32, ls) for i in range(2)]
        b_sm = [Buf(f"atsm{i}") for i in range(2)]
        tmp = [k.sb(f"at_tmp{i}", [128, 128], F32, ls) for i in range(2)]
        b_tmp = [Buf(f"attmp{i}") for i in range(2)]
        ov = [k.sb(f"at_ov{i}", [128, 128], F32, ls) for i in range(2)]
        b_ov = [Buf(f"atov{i}") for i in range(2)]
        junk = k.sb("at_junk", [128, 128], F32, ls)
        b_junk = Buf("atjunk")
        cnt = {"s": 0, "u": 0}

        def load_head(h):
            s = h % 2
            P.dma("sp", qTs[s][:], S["qT"][h], writes=[b_q[s]], sem=hsem[s])
            P.dma("sp", kTs[s][:], S["kT"][h], writes=[b_k[s]], sem=hsem[s])
            P.dma("sp", Vs[s][:, :, 0:128], S["v"][:, h * 128:(h + 1) * 128].rearrange("(t p) e -> p t e", p=128),
                  writes=[b_v[s]], sem=hsem[s])
            P.dma("sp", gas[s][:], S["sga"][:, h * 128:(h + 1) * 128].rearrange("(t p) e -> p t e", p=128),
                  writes=[b_ga[s]], sem=hsem[s])

        def A_steps(h, qb):
            s = h % 2
            ps_ = qb % 2
            steps = []
            for kp in range(9):
                def step(kp=kp):
                    for c in range(2):
                        si = cnt["s"] % 3
                        cnt["s"] += 1
                        for j in range(2):
                            kt = 2 * kp + j
                            P.op("pe", lambda e, kt=kt, j=j, c=c, si=si: e.matmul(
                                sbk[si][:, j * 256:(j + 1) * 256], lhsT=kTs[s][c * 64:(c + 1) * 64, kt * 128:(kt + 1) * 128],
                                rhs=qTs[s][c * 64:(c + 1) * 64, qb * 256:(qb + 1) * 256], start=True, stop=True),
                                reads=[b_k[s], b_q[s]], writes=[b_sbk[si]])
                        P.op("act", lambda e, c=c, kp=kp, si=si: e.activation(
                            out=PT[ps_][:, c, 2 * kp:2 * kp + 2, :].rearrange("p a b -> p (a b)"), in_=sbk[si][:], func=AF.Exp, scale=0.125),
                            reads=[b_sbk[si]], writes=[b_PT[ps_][c][kp]])
                steps.append(step)
            return steps

        def B_gen(h, qb):
            s = h % 2
            ps_ = qb % 2
            for qi_ in range(2):
                yield from unitB(h, qb, qi_, s, ps_)

        def unitB(h, qb, qi, s, ps_):
            if True:
                qt = qb * 2 + qi
                u = cnt["u"] % 2
                cnt["u"] += 1
                banks = [obk[u * 2], obk[u * 2 + 1]]
                bb = [b_obk[u * 2], b_obk[u * 2 + 1]]
                for c in range(2):
                    for kt in range(18):
                        P.op("pe", lambda e, c=c, kt=kt: e.matmul(
                            banks[c][:, 0:129], lhsT=PT[ps_][:, c, kt, qi * 128:(qi + 1) * 128], rhs=Vs[s][:, kt, 0:129],
                            start=(kt == 0), stop=(kt == 17)),
                            reads=[b_PT[ps_][c][kt // 2], b_v[s]], writes=[bb[c]])
                        yield
                flush_pending()
                smt, bsm = sm[u], b_sm[u]
                for c in range(2):
                    P.op("dve", lambda e, c=c: e.reciprocal(out=smt[:, c:c + 1], in_=banks[c][:, 128:129]), reads=[bb[c]], writes=[bsm])
                P.op("dve", lambda e: e.tensor_tensor(out=smt[:, 2:3], in0=smt[:, 1:2], in1=neglam, op=ALU.mult), reads=[bsm, b_lw], writes=[bsm])
                P.op("dve", lambda e: e.tensor_scalar(out=tmp[u][:], in0=banks[1][:, 0:128], scalar1=smt[:, 2:3], scalar2=None, op0=ALU.mult),
                     reads=[bb[1], bsm], writes=[b_tmp[u]])
                P.op("dve", lambda e: e.scalar_tensor_tensor(out=ov[u][:], in0=banks[0][:, 0:128], scalar=smt[:, 0:1], in1=tmp[u][:],
                                                            op0=ALU.mult, op1=ALU.add),
                     reads=[bb[0], bsm, b_tmp[u]], writes=[b_ov[u]])
                P.op("dve", lambda e: e.tensor_tensor(out=tmp[u][:], in0=ov[u][:], in1=ov[u][:], op=ALU.mult),
                     reads=[b_ov[u]], writes=[b_tmp[u]])
                P.op("dve", lambda e: e.tensor_reduce(out=smt[:, 3:4], in_=tmp[u][:], axis=mybir.AxisListType.X, op=ALU.add),
                     reads=[b_tmp[u]], writes=[bsm])
                P.op("dve", lambda e: e.tensor_scalar(out=smt[:, 4:5], in0=smt[:, 3:4], scalar1=1.0 / 128, scalar2=EPS, op0=ALU.mult, op1=ALU.add),
                     reads=[bsm], writes=[bsm])
                def fin():
                    P.op("act", lambda e: e.activation(out=smt[:, 5:6], in_=smt[:, 4:5], func=AF.Ln), reads=[bsm], writes=[bsm])
                    P.op("act", lambda e: e.activation(out=smt[:, 6:7], in_=smt[:, 5:6], func=AF.Exp, scale=-0.5), reads=[bsm], writes=[bsm])
                    P.op("dve", lambda e: e.scalar_tensor_tensor(out=ov[u][:], in0=ov[u][:], scalar=smt[:, 6:7], in1=G[:], op0=ALU.mult, op1=ALU.mult),
                         reads=[b_ov[u], bsm, b_G], writes=[b_ov[u]])
                    P.op("pool", lambda e: e.tensor_tensor(out=ov[u][:], in0=ov[u][:], in1=gas[s][:, qt, :], op=ALU.mult),
                         reads=[b_ov[u], b_ga[s]], writes=[b_ov[u]])

                    def fin2():
                        P.op("pe", lambda e: e.transpose(out=tbk[:, 0:128], in_=ov[u][:], identity=k.ident[:]), reads=[b_ov[u], k.b_ident], writes=[b_tbk])
                        P.op("dve", lambda e: e.tensor_copy(out=aTs[s][:, qt * 128:(qt + 1) * 128], in_=tbk[:, 0:128]),
                             reads=[b_tbk], writes=[b_aT[s]])
                    pending2.append(fin2)
                pending.append(fin)

        pending = []
        pending2 = []

        def flush_pending():
            while pending2:
                pending2.pop(0)()
            while pending:
                pending.pop(0)()

        def interleave(a_steps, bgen, per=8):
            for st_ in a_steps:
                st_()
                if bgen is not None:
                    for _ in range(per):
                        try:
                            next(bgen)
                        except StopIteration:
                            bgen = None
                            break
            if bgen is not None:
                for _ in bgen:
                    pass

        load_head(0)
        load_head(1)
        interleave(A_steps(0, 0), None)
        for h in range(HEADS):
            for qb in range(8):
                if qb + 1 < 8:
                    nxt = A_steps(h, qb + 1)
                elif h + 1 < HEADS:
                    nxt = A_steps(h + 1, 0)
                else:
                    nxt = []
                interleave(nxt, B_gen(h, qb))
            flush_pending()
            flush_pending()
            P.dma("pool", S["abrT"][h * 128:(h + 1) * 128, :], aTs[h % 2][:], reads=[b_aT[h % 2]], sem=asem[h % 2])
            if h + 2 < HEADS:
                load_head(h + 2)


def phase_merge(k):
    nc, P, I, S = k.nc, k.P, k.I, k.S
    with ExitStack() as ls:
        mT = k.sb("mg_mT", [128, NKC, L], BF16, ls)
        b_mT = [Buf(f"mT{tb}") for tb in range(4)]
        wout = k.sb("mg_wout", [128, NKC, D], BF16, ls)
        b_wout = Buf("wout")
        NXB = 3
        wov = I["w_out"].rearrange("(kc p) c -> p kc c", p=128)
        wo_state = {"kc": 0}
        b_woutc = [Buf(f"woutc{i}") for i in range(NKC)]

        def load_wout_chunk():
            kc = wo_state["kc"]
            if kc >= NKC:
                return
            wo_state["kc"] += 1
            P.dma("pool", wout[:, kc, :], wov[:, kc, :], writes=[b_woutc[kc]])
        with ExitStack() as l1:
            abrT = k.sb("mg_abrT", [128, 8, L], BF16, l1)
            sbrT = k.sb("mg_sbrT", [128, 4, L], BF16, l1)
            b_abrT, b_sbrT = Buf("abrT"), Buf("sbrT")
            lsem = P.new_dsem("mg_l")
            P.dma("sp", abrT[:], S["abrT"].rearrange("(fc p) t -> p fc t", p=128), writes=[b_abrT], sem=lsem)
            P.dma("sp", sbrT[:], S["sbrT"].rearrange("(fc p) t -> p fc t", p=128), writes=[b_sbrT], sem=lsem)
            NWS = 2
            wbf = [k.sb(f"mg_wbf{i}", [128, 12, 128], BF16, l1) for i in range(NWS)]
            b_wbf = [Buf(f"mgwbf{i}") for i in range(NWS)]
            NG = 2
            gt = [k.sb(f"mg_gt{i}", [128, 2, L], BF16, l1) for i in range(NG)]
            b_gt = [Buf(f"mggt{i}") for i in range(NG)]
            t1 = [k.sb(f"mg_t1{i}", [128, 512], F32, l1) for i in range(2)]
            t2 = [k.sb(f"mg_t2{i}", [128, 512], F32, l1) for i in range(2)]
            b_t1 = [Buf(f"mgt1{i}") for i in range(2)]
            b_t2 = [Buf(f"mgt2{i}") for i in range(2)]
            pa = [k.ps(f"mg_pa{i}", [128, 512], F32, l1) for i in range(2)]
            pp = [k.ps(f"mg_pp{i}", [128, 512], F32, l1) for i in range(2)]
            b_pa = [Buf(f"mgpa{i}") for i in range(2)]
            b_pp = [Buf(f"mgpp{i}") for i in range(2)]
            wpa_v = I["w_pa"].rearrange("(fc p) c -> p fc c", p=128)
            wps_v = I["w_ps"].rearrange("(fc p) c -> p fc c", p=128)
            ui = 0

            def load_w(fo):
                s = fo % NWS
                P.dma("pool", wbf[s][:, 0:8, :], wpa_v[:, :, fo * 128:(fo + 1) * 128], writes=[b_wbf[s]])
                P.dma("pool", wbf[s][:, 8:12, :], wps_v[:, :, fo * 128:(fo + 1) * 128], writes=[b_wbf[s]])
                gi = fo % NG
                P.dma("sp", gt[gi][:, 0, :], S["sgmT"][fo * 128:(fo + 1) * 128, :], writes=[b_gt[gi]])
                P.dma("sp", gt[gi][:, 1, :], S["sgmT"][D + fo * 128:D + (fo + 1) * 128, :], writes=[b_gt[gi]])

            load_w(0)
            for fo in range(NKC):
                if fo + 1 < NKC:
                    load_w(fo + 1)
                load_wout_chunk()
                s = fo % NWS
                gi = fo % NG
                for tb in range(4):
                    u2 = ui % 2
                    ui += 1
                    tsl = slice(tb * 512, (tb + 1) * 512)
                    for fc in range(8):
                        P.op("pe", lambda e, fc=fc, s=s, tsl=tsl, u2=u2: e.matmul(pa[u2][:], lhsT=wbf[s][:, fc, :], rhs=abrT[:, fc, tsl],
                                                                        start=(fc == 0), stop=(fc == 7)),
                             reads=[b_wbf[s], b_abrT], writes=[b_pa[u2]])
                    for fc in range(4):
                        P.op("pe", lambda e, fc=fc, s=s, tsl=tsl, u2=u2: e.matmul(pp[u2][:], lhsT=wbf[s][:, 8 + fc, :], rhs=sbrT[:, fc, tsl],
                                                                        start=(fc == 0), stop=(fc == 3)),
                             reads=[b_wbf[s], b_sbrT], writes=[b_pp[u2]])
                    P.op("dve", lambda e, gi=gi, u2=u2, tsl=tsl: e.tensor_tensor(out=t1[u2][:], in0=pa[u2][:], in1=gt[gi][:, 0, tsl], op=ALU.mult),
                         reads=[b_pa[u2], b_gt[gi]], writes=[b_t1[u2]])
                    P.op("dve", lambda e, gi=gi, u2=u2, tsl=tsl: e.tensor_tensor(out=t2[u2][:], in0=pp[u2][:], in1=gt[gi][:, 1, tsl], op=ALU.mult),
                         reads=[b_pp[u2], b_gt[gi]], writes=[b_t2[u2]])
                    P.op("pool", lambda e, fo=fo, tsl=tsl, u2=u2: e.tensor_tensor(out=mT[:, fo, tsl], in0=t1[u2][:], in1=t2[u2][:], op=ALU.add),
                         reads=[b_t1[u2], b_t2[u2]], writes=[b_mT[tb]])
            P.barrier()
        gateB = k.sb("mg_gateB", [128, D], F32, ls)
        fgB = k.sb("mg_fgB", [128, D], F32, ls)
        b_gateB, b_fgB = Buf("gateB"), Buf("fgB")
        c2 = P.new_dsem("mg_c2")
        P.dma("sp", gateB[:], S["modrow"][0:1, 2 * D:3 * D].broadcast_to([128, D]), writes=[b_gateB], sem=c2)
        P.dma("sp", fgB[:], I["final_g"][0:1, :].broadcast_to([128, D]), writes=[b_fgB], sem=c2)
        xb = [k.sb(f"mg_x{i}", [128, D], F32, ls) for i in range(NXB)]
        b_xb = [Buf(f"mgx{i}") for i in range(NXB)]
        xn = [k.sb(f"mg_xn{i}", [128, D], F32, ls) for i in range(NXB)]
        b_xn = [Buf(f"mgxn{i}") for i in range(NXB)]
        xsem = [P.new_dsem(f"mg_xs{i}") for i in range(NXB)]
        osem = [P.new_dsem(f"mg_os{i}") for i in range(NXB)]
        st2 = [k.sb(f"mg_st{i}", [128, 4], F32, ls) for i in range(NXB)]
        b_st2 = [Buf(f"mgst{i}") for i in range(NXB)]
        while wo_state["kc"] < NKC:
            load_wout_chunk()
        po = [k.ps(f"mg_po{i}", [128, 512], F32, ls) for i in range(6)]
        b_po = [Buf(f"mgpo{i}") for i in range(6)]
        pi = 0

        def load_x(t):
            if t < 16:
                s_ = t % NXB
                P.dma("act", xb[s_][:], I["x"][t * 128:(t + 1) * 128, :], writes=[b_xb[s_]], sem=xsem[s_])

        def fin1(t):
            s = t % NXB
            P.op("pool", lambda e: e.tensor_tensor(out=xn[s][:], in0=xn[s][:], in1=xb[s][:], op=ALU.add),
                 reads=[b_xn[s], b_xb[s]], writes=[b_xn[s]])
            P.op("pool", lambda e: e.tensor_tensor(out=xb[s][:], in0=xn[s][:], in1=xn[s][:], op=ALU.mult),
                 reads=[b_xn[s]], writes=[b_xb[s]])

        def fin2(t):
            s = t % NXB
            P.op("dve", lambda e: e.tensor_reduce(out=st2[s][:, 0:1], in_=xb[s][:], axis=mybir.AxisListType.X, op=ALU.add),
                 reads=[b_xb[s]], writes=[b_st2[s]])
            P.op("dve", lambda e: e.tensor_scalar(out=st2[s][:, 1:2], in0=st2[s][:, 0:1], scalar1=1.0 / D, scalar2=EPS, op0=ALU.mult, op1=ALU.add),
                 reads=[b_st2[s]], writes=[b_st2[s]])
            P.op("act", lambda e: e.activation(out=st2[s][:, 2:3], in_=st2[s][:, 1:2], func=AF.Ln), reads=[b_st2[s]], writes=[b_st2[s]])
            P.op("act", lambda e: e.activation(out=st2[s][:, 3:4], in_=st2[s][:, 2:3], func=AF.Exp, scale=-0.5), reads=[b_st2[s]], writes=[b_st2[s]])
            P.op("dve", lambda e: e.scalar_tensor_tensor(out=xn[s][:], in0=xn[s][:], scalar=st2[s][:, 3:4], in1=fgB[:], op0=ALU.mult, op1=ALU.mult),
                 reads=[b_xn[s], b_st2[s], b_fgB], writes=[b_xn[s]])
            P.dma("sp", k.out[t * 128:(t + 1) * 128, :], xn[s][:], reads=[b_xn[s]], sem=osem[s])
            load_x(t + NXB)

        for t0 in range(NXB):
            load_x(t0)
        for t in range(16):
            s = t % NXB
            tb = t // 4
            for cbk in range(4):
                p_ = pi % 6
                pi += 1
                for kc in range(NKC):
                    P.op("pe", lambda e, kc=kc, cbk=cbk, p_=p_, t=t: e.matmul(po[p_][:], lhsT=mT[:, kc, t * 128:(t + 1) * 128],
                                                                          rhs=wout[:, kc, cbk * 512:(cbk + 1) * 512],
                                                                          start=(kc == 0), stop=(kc == NKC - 1)),
                         reads=[b_mT[tb], b_woutc[kc]], writes=[b_po[p_]])
                P.op("dve", lambda e, cbk=cbk, p_=p_, s=s: e.tensor_tensor(out=xn[s][:, cbk * 512:(cbk + 1) * 512], in0=po[p_][:],
                                                                       in1=gateB[:, cbk * 512:(cbk + 1) * 512], op=ALU.mult),
                     reads=[b_po[p_], b_gateB], writes=[b_xn[s]])
            if t >= 2:
                fin2(t - 2)
            if t >= 1:
                fin1(t - 1)
        fin2(14)
        fin1(15)
        fin2(15)


_CACHE = {}


def _prep_inputs(inputs, b):
    f = lambda a: np.ascontiguousarray(np.asarray(a, dtype=np.float32))
    m = {}
    m["x"] = f(inputs["x"][b])
    m["ctx"] = f(inputs["ctx"][b])
    m["cc"] = f(np.stack([np.asarray(inputs["c"])[b], np.asarray(inputs["c_ctx"])], axis=0))
    m["w_ada"] = f(inputs["w_ada"][0])
    m["b_ada"] = f(inputs["b_ada"][0]).reshape(1, -1)
    m["norm_g"] = f(inputs["norm_g"][0])
    m["w_in"] = f(inputs["w_in"][0])
    m["lam"] = f(np.stack([np.asarray(inputs["lambda_q1"])[0], np.asarray(inputs["lambda_k1"])[0],
                           np.asarray(inputs["lambda_q2"])[0], np.asarray(inputs["lambda_k2"])[0]], axis=0))
    m["subln_g"] = f(inputs["subln_g"][0]).reshape(1, 128)
    m["ssm_lre"] = f(inputs["ssm_lambda_re"][0])
    m["ssm_lim"] = f(inputs["ssm_lambda_im"][0])
    m["ssm_ls"] = f(inputs["ssm_log_step"][0])
    m["ssm_bre"] = f(inputs["ssm_b_re"][0])
    m["ssm_bim"] = f(inputs["ssm_b_im"][0])
    m["ssm_cre"] = f(inputs["ssm_c_re"][0])
    m["ssm_cim"] = f(inputs["ssm_c_im"][0])
    m["ssm_d"] = f(inputs["ssm_d"][0]).reshape(1, 512)
    m["w_glu"] = f(inputs["w_glu"][0])
    m["b_glu"] = f(inputs["b_glu"][0])
    m["w_pa"] = f(inputs["w_pa"][0])
    m["w_ps"] = f(inputs["w_ps"][0])
    m["w_out"] = f(inputs["w_out"][0])
    m["final_g"] = f(inputs["final_g"]).reshape(1, D)
    m.update(_consts())
    return m


def kernel(**inputs):
    if "nc" not in _CACHE:
        _CACHE["nc"] = build()[0]
    nc = _CACHE["nc"]
    shared = None
    in_maps = []
    for b in range(8):
        m = _prep_inputs(inputs, b)
        if shared is None:
            shared = m
        else:
            for key in m:
                if key not in ("x", "ctx", "cc"):
                    m[key] = shared[key]
        in_maps.append(m)
    res = run_bass_kernel_spmd(nc, in_maps, core_ids=list(range(8)))
    return np.stack([np.asarray(r["out"], dtype=np.float32) for r in res.results], axis=0)
```
